# Optimizing a Trainium2 kernel written in Bass

```python
import jax, jax.numpy as jnp
from jax import lax
import numpy as np

D_MODEL = 1024
BATCH = 4
SEQ = 4096
DEPTH = 1
DEC_BATCH = 32
DEC_SEQ = 32
PAST_LEN = 2048

CHUNK = 64
BAND_CHUNKS = 8
WINDOW_A = BAND_CHUNKS * CHUNK
H_A = 8
HD_A = 64
MAX_REL = 128
H_B = 4
DK_B = 64
DV_B = 128
GK_RANK = 16
GK_NORM = 16.0
GLA_BLOCK = 16
D_FF = 2816
CONV_W = 3
EPS = 1e-6
NEG_INF = -1e30

W_A = H_A * HD_A
W_BK = H_B * DK_B
W_BV = H_B * DV_B
IN_SIZES = (W_A, W_A, W_A, W_BK, W_BK, W_BV, W_BV, GK_RANK, D_MODEL, D_MODEL)
D_IN = 3 * W_A + 2 * W_BK + 2 * W_BV + GK_RANK + 2 * D_MODEL

kernel_name = 'hybrid_chunkband_gla_convffn_step'


def rmsnorm(x, g):
    xf = x.astype(jnp.float32)
    r = lax.rsqrt(jnp.mean(xf * xf, axis=-1, keepdims=True) + EPS)
    return (xf * r).astype(x.dtype) * g


def adaln_modulation(c, w_ada, b_ada):
    m = jax.nn.silu(c) @ w_ada + b_ada
    return jnp.split(m[:, None, :], 6, axis=-1)


def rel_bias_lookup(rel_bias, rel):
    return rel_bias[:, jnp.clip(rel, -MAX_REL, MAX_REL) + MAX_REL]


def band_attention_prompt(q, k, v, rel_bias):
    b, t, h, d = q.shape
    nc = t // CHUNK
    nband = BAND_CHUNKS + 1
    qc = q.reshape(b, nc, CHUNK, h, d)
    pad_k = jnp.zeros((b, BAND_CHUNKS, CHUNK, h, d), k.dtype)
    pad_v = jnp.zeros((b, BAND_CHUNKS, CHUNK, h, d), v.dtype)
    kc = jnp.concatenate([pad_k, k.reshape(b, nc, CHUNK, h, d)], axis=1)
    vc = jnp.concatenate([pad_v, v.reshape(b, nc, CHUNK, h, d)], axis=1)
    idx = jnp.arange(nc)[:, None] + jnp.arange(nband)[None, :]
    kb = kc[:, idx].reshape(b, nc, nband * CHUNK, h, d)
    vb = vc[:, idx].reshape(b, nc, nband * CHUNK, h, d)
    valid = jnp.repeat(idx >= BAND_CHUNKS, CHUNK, axis=1)
    qi = jnp.arange(CHUNK)
    kr = jnp.arange(nband * CHUNK) - BAND_CHUNKS * CHUNK
    bias = rel_bias_lookup(rel_bias, qi[:, None] - kr[None, :]).astype(jnp.float32)
    s = jnp.einsum('bcqhd,bckhd->bchqk', qc, kb).astype(jnp.float32) * (d ** -0.5) + bias
    s = jnp.where(valid[None, :, None, None, :], s, NEG_INF)
    p = jax.nn.softmax(s, axis=-1).astype(v.dtype)
    o = jnp.einsum('bchqk,bckhd->bcqhd', p, vb)
    return o.reshape(b, t, h * d)


def band_attention_sample(q, k_new, v_new, k_cache, v_cache, rel_bias):
    b, s_len, h, d = q.shape
    w = k_cache.shape[1]
    kk = jnp.concatenate([k_cache, k_new], axis=1)
    vv = jnp.concatenate([v_cache, v_new], axis=1)
    qpos = PAST_LEN + jnp.arange(s_len)
    kpos = jnp.concatenate([PAST_LEN - w + jnp.arange(w), qpos])
    bias = rel_bias_lookup(rel_bias, qpos[:, None] - kpos[None, :]).astype(jnp.float32)
    s = jnp.einsum('bqhd,bkhd->bhqk', q, kk).astype(jnp.float32) * (d ** -0.5) + bias
    p = jax.nn.softmax(s, axis=-1).astype(vv.dtype)
    o = jnp.einsum('bhqk,bkhd->bqhd', p, vv)
    return o.reshape(b, s_len, h * d)


def gla(q, k, v, log_a, s0):
    b, t, h, dk = q.shape
    dv = v.shape[-1]
    pad = (-t) % GLA_BLOCK
    if pad:
        cfg = ((0, 0), (0, pad), (0, 0), (0, 0))
        q, k, v, log_a = (jnp.pad(a, cfg) for a in (q, k, v, log_a))
    nb = (t + pad) // GLA_BLOCK
    rs = lambda a: a.reshape(b, nb, GLA_BLOCK, h, a.shape[-1]).astype(jnp.float32)
    qb, kb, vb, ab = rs(q), rs(k), rs(v), rs(log_a)
    cum = jnp.cumsum(ab, axis=2)
    cum_end = cum[:, :, -1:]
    q_t = qb * jnp.exp(cum) * (dk ** -0.5)
    k_t = kb * jnp.exp(-cum)
    k_end = kb * jnp.exp(cum_end - cum)
    mask = jnp.tril(jnp.ones((GLA_BLOCK, GLA_BLOCK), jnp.float32))
    att = jnp.einsum('bnthk,bnshk->bnhts', q_t, k_t) * mask
    o_intra = jnp.einsum('bnhts,bnshv->bnthv', att, vb)
    upd = jnp.einsum('bnshk,bnshv->bnhkv', k_end, vb)
    dec = jnp.exp(cum_end[:, :, 0])

    def step(state, xs):
        u, dcy = xs
        return dcy[..., None] * state + u, state

    s_fin, s_prev = lax.scan(step, s0.astype(jnp.float32),
                             (jnp.swapaxes(upd, 0, 1), jnp.swapaxes(dec, 0, 1)))
    s_prev = jnp.swapaxes(s_prev, 0, 1)
    o_inter = jnp.einsum('bnthk,bnhkv->bnthv', q_t, s_prev)
    o = (o_intra + o_inter).reshape(b, nb * GLA_BLOCK, h, dv)[:, :t]
    return o.astype(v.dtype), s_fin.astype(s0.dtype)


def conv_ffn(h, conv_state, w_up, w_dw, b_dw, w_down):
    t = h.shape[1]
    u = h @ w_up
    uu = jnp.concatenate([conv_state.astype(u.dtype), u], axis=1)
    y = b_dw
    for j in range(CONV_W):
        y = y + w_dw[j] * uu[:, j:j + t]
    a, g = jnp.split(y, 2, axis=-1)
    out = (jax.nn.gelu(a, approximate=True) * g) @ w_down
    return out, uu[:, -(CONV_W - 1):]


def trunk_layer(x, c, k_cache, v_cache, s_gla, s_conv, p, first_chunk):
    b, t, _ = x.shape
    shift_m, scale_m, gate_m, shift_f, scale_f, gate_f = adaln_modulation(c, p['w_ada'], p['b_ada'])

    h = rmsnorm(x, p['g_pre_mix']) * (1.0 + scale_m) + shift_m
    z = h @ p['w_in']
    qa, ka, va, qb, kb, vb, gb, gk_low, gate_a, gate_b = jnp.split(
        z, np.cumsum(IN_SIZES)[:-1].tolist(), axis=-1)
    qa = qa.reshape(b, t, H_A, HD_A)
    ka = ka.reshape(b, t, H_A, HD_A)
    va = va.reshape(b, t, H_A, HD_A)
    if first_chunk:
        ya = band_attention_prompt(qa, ka, va, p['rel_bias'])
        rows = min(WINDOW_A, t)
        k_keep, v_keep = ka[:, t - rows:], va[:, t - rows:]
        s_gla = jnp.zeros((b, H_B, DK_B, DV_B), x.dtype)
        s_conv = jnp.zeros((b, CONV_W - 1, 2 * D_FF), x.dtype)
    else:
        ya = band_attention_sample(qa, ka, va, k_cache, v_cache, p['rel_bias'])
        k_keep, v_keep = ka, va
    log_a = jax.nn.log_sigmoid((gk_low @ p['w_gk2'] + p['b_gk']).astype(jnp.float32)) / GK_NORM
    yb, s_gla_new = gla(qb.reshape(b, t, H_B, DK_B), kb.reshape(b, t, H_B, DK_B),
                        vb.reshape(b, t, H_B, DV_B), log_a.reshape(b, t, H_B, DK_B), s_gla)
    yb = rmsnorm(yb, p['g_gla']) * jax.nn.silu(gb.reshape(b, t, H_B, DV_B))
    yb = yb.reshape(b, t, W_BV)
    merged = jax.nn.sigmoid(gate_a) * (ya @ p['w_br_a']) + jax.nn.sigmoid(gate_b) * (yb @ p['w_br_b'])
    x = x + gate_m * rmsnorm(merged @ p['w_out'], p['g_post_mix'])

    h = rmsnorm(x, p['g_pre_ffn']) * (1.0 + scale_f) + shift_f
    yf, conv_new = conv_ffn(h, s_conv, p['w_up'], p['w_dw'], p['b_dw'], p['w_down'])
    x = x + gate_f * rmsnorm(yf, p['g_post_ffn'])
    return x, k_keep, v_keep, s_gla_new, conv_new


def setup_inputs(seed: int = 0) -> dict:
    key = jax.random.key(seed)
    ks = jax.random.split(key, 26)
    kv_rows = min(WINDOW_A, PAST_LEN)

    def nrm(k, shape, scale):
        return jax.random.normal(k, shape, jnp.float32) * scale

    return {
        'x_prompt': nrm(ks[0], (BATCH, SEQ, D_MODEL), 1.0),
        'x_sample': nrm(ks[1], (DEC_BATCH, DEC_SEQ, D_MODEL), 1.0),
        'cache_k_a': nrm(ks[2], (DEPTH, DEC_BATCH, kv_rows, H_A, HD_A), 1.0),
        'cache_v_a': nrm(ks[3], (DEPTH, DEC_BATCH, kv_rows, H_A, HD_A), 1.0),
        'state_gla': nrm(ks[4], (DEPTH, DEC_BATCH, H_B, DK_B, DV_B), 1.0),
        'state_conv': nrm(ks[5], (DEPTH, DEC_BATCH, CONV_W - 1, 2 * D_FF), 1.0),
        'c_prompt': nrm(ks[6], (BATCH, D_MODEL), 1.0),
        'c_sample': nrm(ks[7], (DEC_BATCH, D_MODEL), 1.0),
        'w_ada': nrm(ks[8], (DEPTH, D_MODEL, 6 * D_MODEL), 0.5 * D_MODEL ** -0.5),
        'b_ada': nrm(ks[9], (DEPTH, 6 * D_MODEL), 0.01),
        'g_pre_mix': 1.0 + nrm(ks[10], (DEPTH, D_MODEL), 0.05),
        'g_post_mix': 1.0 + nrm(ks[11], (DEPTH, D_MODEL), 0.05),
        'g_pre_ffn': 1.0 + nrm(ks[12], (DEPTH, D_MODEL), 0.05),
        'g_post_ffn': 1.0 + nrm(ks[13], (DEPTH, D_MODEL), 0.05),
        'w_in': nrm(ks[14], (DEPTH, D_MODEL, D_IN), D_MODEL ** -0.5),
        'w_gk2': nrm(ks[15], (DEPTH, GK_RANK, W_BK), GK_RANK ** -0.5),
        'b_gk': nrm(ks[16], (DEPTH, W_BK), 0.1),
        'rel_bias': nrm(ks[17], (DEPTH, H_A, 2 * MAX_REL + 1), 0.5),
        'g_gla': 1.0 + nrm(ks[18], (DEPTH, DV_B), 0.05),
        'w_br_a': nrm(ks[19], (DEPTH, W_A, D_MODEL), W_A ** -0.5),
        'w_br_b': nrm(ks[20], (DEPTH, W_BV, D_MODEL), W_BV ** -0.5),
        'w_out': nrm(ks[21], (DEPTH, D_MODEL, D_MODEL), D_MODEL ** -0.5),
        'w_up': nrm(ks[22], (DEPTH, D_MODEL, 2 * D_FF), D_MODEL ** -0.5),
        'w_dw': nrm(ks[23], (DEPTH, CONV_W, 2 * D_FF), CONV_W ** -0.5),
        'b_dw': nrm(ks[24], (DEPTH, 2 * D_FF), 0.01),
        'w_down': nrm(ks[25], (DEPTH, D_FF, D_MODEL), D_FF ** -0.5),
    }


def reference(x_prompt, x_sample, cache_k_a, cache_v_a, state_gla, state_conv, c_prompt, c_sample,
              w_ada, b_ada, g_pre_mix, g_post_mix, g_pre_ffn, g_post_ffn, w_in, w_gk2, b_gk,
              rel_bias, g_gla, w_br_a, w_br_b, w_out, w_up, w_dw, b_dw, w_down):
    yp, ys = x_prompt, x_sample
    kp_l, vp_l, gp_l, cp_l, ks_l, vs_l, gs_l, cs_l = [], [], [], [], [], [], [], []
    for l in range(DEPTH):
        p = {'w_ada': w_ada[l], 'b_ada': b_ada[l], 'g_pre_mix': g_pre_mix[l],
             'g_post_mix': g_post_mix[l], 'g_pre_ffn': g_pre_ffn[l], 'g_post_ffn': g_post_ffn[l],
             'w_in': w_in[l], 'w_gk2': w_gk2[l], 'b_gk': b_gk[l], 'rel_bias': rel_bias[l],
             'g_gla': g_gla[l], 'w_br_a': w_br_a[l], 'w_br_b': w_br_b[l], 'w_out': w_out[l],
             'w_up': w_up[l], 'w_dw': w_dw[l], 'b_dw': b_dw[l], 'w_down': w_down[l]}
        yp, kp, vp, gp, cp = trunk_layer(yp, c_prompt, None, None, None, None, p, True)
        ys, kn, vn, gn, cn = trunk_layer(ys, c_sample, cache_k_a[l], cache_v_a[l],
                                         state_gla[l], state_conv[l], p, False)
        kp_l.append(kp); vp_l.append(vp); gp_l.append(gp); cp_l.append(cp)
        ks_l.append(kn); vs_l.append(vn); gs_l.append(gn); cs_l.append(cn)
    k_a_prompt = jnp.stack(kp_l)
    v_a_prompt = jnp.stack(vp_l)
    gla_prompt = jnp.stack(gp_l)
    conv_prompt = jnp.stack(cp_l)
    k_a_sample = jnp.stack(ks_l)
    v_a_sample = jnp.stack(vs_l)
    gla_sample = jnp.stack(gs_l)
    conv_sample = jnp.stack(cs_l)
    return (yp, ys, k_a_prompt, v_a_prompt, gla_prompt, conv_prompt,
            k_a_sample, v_a_sample, gla_sample, conv_sample)
```

```python
from contextlib import ExitStack
import os

import numpy as np
import concourse.bass as bass
import concourse.mybir as mybir
from concourse.bass_utils import run_bass_kernel_spmd

F32 = mybir.dt.float32
BF16 = mybir.dt.bfloat16
AF = mybir.ActivationFunctionType
ALU = mybir.AluOpType

D = 1024
KC = 8
DFF = 2816
NFF = 22
DIN = 5136
O_QA, O_KA, O_VA, O_QB, O_KB, O_VB, O_GB, O_GK, O_GA, O_GBR = 0, 512, 1024, 1536, 1792, 2048, 2560, 3072, 3088, 4112
EPS = 1e-6
NEG = -30000.0
NPRE = 15
NMAIN = 16
NB = 2


class Op:
    __slots__ = ("eng", "fn", "reads", "writes", "dsem", "signal", "sigval", "deps")

    def __init__(self, eng, fn, reads, writes, dsem):
        self.eng, self.fn, self.reads, self.writes, self.dsem = eng, fn, reads, writes, dsem
        self.signal = False
        self.sigval = 0
        self.deps = ()


class Sched:
    DMA = ("sp", "pool_dma")

    def __init__(self, nc, stack):
        self.nc = nc
        self.stack = stack
        self.ops = []
        self.eng_obj = {"pe": nc.tensor, "act": nc.scalar, "dve": nc.vector, "pool": nc.gpsimd,
                        "sp": nc.sync, "pool_dma": nc.gpsimd}
        self.wuses = []
        self.wdepth = 2

    def add(self, eng, fn, reads=(), writes=(), dsem=None):
        op = Op(eng, fn, tuple(reads), tuple(writes), dsem)
        self.ops.append(op)
        return op

    def queue_of(self, op):
        return "pool" if op.eng == "pool_dma" else op.eng

    def finalize(self):
        nc = self.nc
        inserts = {}
        lastrd = {}
        for idx, op in enumerate(self.ops):
            for k in op.reads:
                if k and k[0] == "wuse":
                    lastrd[k[1]] = idx
        prev = 0
        for i, (pos, op) in enumerate(self.wuses):
            tgt = self.wuses[max(0, i - self.wdepth)][0]
            if i >= 3 and (i - 3) in lastrd:
                tgt = max(tgt, lastrd[i - 3] + 1)
            tgt = max(tgt, prev)
            prev = tgt
            assert tgt <= pos, (i, tgt, pos)
            inserts.setdefault(tgt, []).append(op)
        ops = []
        for i, op in enumerate(self.ops):
            if i in inserts:
                ops.extend(inserts[i])
            ops.append(op)
        self.ops = ops
        last_w = {}
        readers = {}
        for i, op in enumerate(ops):
            deps = set()
            q = self.queue_of(op)
            isdma = op.dsem is not None
            for k in op.reads:
                j = last_w.get(k)
                if j is not None:
                    deps.add(j)
            for k in op.writes:
                j = last_w.get(k)
                if j is not None:
                    oj = ops[j]
                    if isdma or oj.dsem is not None or self.queue_of(oj) != q:
                        deps.add(j)
                for j in readers.get(k, ()):
                    oj = ops[j]
                    if isdma or oj.dsem is not None or self.queue_of(oj) != q:
                        deps.add(j)
            deps.discard(i)
            op.deps = tuple(sorted(deps))
            for j in op.deps:
                ops[j].signal = True
            for k in op.reads:
                readers.setdefault(k, []).append(i)
            for k in op.writes:
                last_w[k] = i
                readers[k] = []
        esem = {}
        for e in ("pe", "act", "dve", "pool"):
            esem[e] = self.stack.enter_context(nc.semaphore("sem_" + e))
        dsems = {}
        cnt = {}
        for op in ops:
            if op.dsem is not None:
                if op.dsem not in dsems:
                    dsems[op.dsem] = self.stack.enter_context(nc.semaphore("dsem_%d" % len(dsems)))
                    cnt[op.dsem] = 0
                cnt[op.dsem] += 1
                op.sigval = 16 * cnt[op.dsem]
            elif op.signal:
                q = self.queue_of(op)
                cnt[q] = cnt.get(q, 0) + 1
                op.sigval = cnt[q]
        known = {q: {} for q in ("pe", "act", "dve", "pool", "sp")}
        for op in ops:
            q = self.queue_of(op)
            eng = self.eng_obj[op.eng]
            need = {}
            for j in op.deps:
                oj = ops[j]
                s = dsems[oj.dsem] if oj.dsem is not None else esem[self.queue_of(oj)]
                key = id(s)
                if key not in need or need[key][1] < oj.sigval:
                    need[key] = (s, oj.sigval)
            for key, (s, v) in need.items():
                if known[q].get(key, 0) >= v:
                    continue
                eng.wait_ge(s, v)
                known[q][key] = v
            ins = op.fn(eng)
            if op.dsem is not None:
                ins.then_inc(dsems[op.dsem], 16)
            elif op.signal:
                ins.then_inc(esem[q], 1)
        self.counts = dict((str(k), v) for k, v in cnt.items())
        for k, s in dsems.items():
            nc.sync.wait_ge(s, 16 * cnt[k])
        return len(ops)


class Builder:
    def __init__(self):
        self.stack = ExitStack()
        self.nc = bass.Bass("TRN2", target_bir_lowering=False)
        self.S = Sched(self.nc, self.stack)
        self.nbank = 0
        self.wuse_n = 0
        self.uid = 0

    def din(self, name, shape, dt=F32):
        return self.nc.dram_tensor(name, list(shape), dt, kind="ExternalInput").ap()

    def dout(self, name, shape):
        return self.nc.dram_tensor(name, list(shape), F32, kind="ExternalOutput").ap()

    def dscr(self, name, shape, dt=BF16):
        return self.nc.dram_tensor(name, list(shape), dt, kind="Internal").ap()

    def sb(self, name, shape, dt=F32):
        return self.stack.enter_context(self.nc.sbuf_tensor(name, list(shape), dt))

    def ps(self, name, shape, dt=F32):
        return self.stack.enter_context(self.nc.psum_tensor(name, list(shape), dt))

    def bank(self):
        i = self.nbank % len(self.banks)
        self.nbank += 1
        return self.banks[i], ("ps", i)

    def key(self, name):
        self.uid += 1
        return (name, self.uid)

    def mm(self, out, lhsT, rhs, start, stop, reads, writes):
        self.S.add("pe", lambda e: e.matmul(out, lhsT, rhs, start=start, stop=stop), reads, writes)

    def tr(self, out, in_, ident, reads, writes):
        self.S.add("pe", lambda e: e.transpose(out, in_, ident), reads, writes)

    def act(self, out, in_, func, reads, writes, bias=None, scale=None, accum_out=None):
        kw = {}
        if bias is not None:
            kw["bias"] = bias
        if scale is not None:
            kw["scale"] = scale
        if accum_out is not None:
            kw["accum_out"] = accum_out
        self.S.add("act", lambda e: e.activation(out, in_, func, **kw), reads, writes)

    def tt(self, eng, out, in0, in1, op, reads, writes):
        self.S.add(eng, lambda e: e.tensor_tensor(out, in0, in1, op), reads, writes)

    def ts(self, eng, out, in0, s1, s2, op0, op1, reads, writes):
        if s2 is None:
            self.S.add(eng, lambda e: e.tensor_scalar(out, in0, s1, None, op0), reads, writes)
        else:
            self.S.add(eng, lambda e: e.tensor_scalar(out, in0, s1, s2, op0, op1), reads, writes)

    def stt(self, eng, out, in0, scalar, in1, op0, op1, reads, writes):
        self.S.add(eng, lambda e: e.scalar_tensor_tensor(out, in0, scalar, in1, op0=op0, op1=op1), reads, writes)

    def cp(self, eng, out, in_, reads, writes):
        if eng == "act":
            self.S.add("act", lambda e: e.copy(out, in_), reads, writes)
        else:
            self.S.add(eng, lambda e: e.tensor_copy(out, in_), reads, writes)

    def memset(self, eng, ap, val, writes):
        self.S.add(eng, lambda e: e.memset(ap, val), (), writes)

    def dma(self, q, out, in_, reads, writes, dsem, slow=False):
        if slow:
            self.S.add(q, lambda e: e.dma_start(out=out, in_=in_, allow_slow_non_contiguous=True), reads, writes, dsem)
        else:
            self.S.add(q, lambda e: e.dma_start(out=out, in_=in_), reads, writes, dsem)

    def wload(self, wd, wkey, kcn, c0, ncols):
        i = self.wuse_n
        self.wuse_n += 1
        slot = i % 3
        buf = self.wbufs[slot]
        key = ("wuse", i)
        src = wd.rearrange("(kc p) n -> p kc n", p=128)[:, :, c0:c0 + ncols]
        dst = buf[:, 0:kcn * ncols].rearrange("p (kc n) -> p kc n", kc=kcn)
        op = Op("sp", lambda e: e.dma_start(out=dst, in_=src), tuple(wkey), (key, ("wbuf", slot)) + ((("wuse", i - 3),) if i >= 3 else ()), ("wstream", slot))
        self.S.wuses.append((len(self.S.ops), op))
        return (lambda kc, a=0, b=ncols: buf[:, kc * ncols + a: kc * ncols + b]), key

    def build(self):
        nc = self.nc
        S = self.S
        sb, ps = self.sb, self.ps
        NBT = NB * 128
        K = lambda *a: tuple(a)
        xpre = self.din("xpre", [NPRE * 128, D])
        xov = self.din("xov", [128, D])
        xmain = self.din("xmain", [NMAIN * 128, D])
        xsam = self.din("xsam", [128, D])
        crow = self.din("crow", [40, 128])
        flag_d = self.din("flag", [128, 1])
        ck = self.din("ck", [4, 512, 512])
        cv = self.din("cv", [4, 512, 512])
        sgla = self.din("sgla", [4, 4, 64, 128])
        sconv = self.din("sconv", [4, 88, 128])
        w_ada = self.din("w_ada", [D, 6 * D])
        b_ada = self.din("b_ada", [48, 128])
        gvec = self.din("gvec", [32, 128])
        w_in = self.din("w_in", [D, DIN])
        w_gk2 = self.din("w_gk2", [16, 256])
        b_gk = self.din("b_gk", [1, 256])
        btab = self.din("btab", [2, 128, 8, 128])
        cvec = self.din("cvec", [128, 8])
        ggla = self.din("ggla", [128, 512])
        w_br_a = self.din("w_br_a", [512, D])
        w_br_b = self.din("w_br_b", [512, D])
        w_out = self.din("w_out", [D, D])
        w_up = self.din("w_up", [D, 2 * DFF])
        w_dw = self.din("w_dw", [3 * 44, 128])
        b_dw = self.din("b_dw", [44, 128])
        w_down = self.din("w_down", [DFF, D])

        ymain = self.dout("ymain", [NMAIN * 128, D])
        ysam = self.dout("ysam", [128, D])
        yov = self.dout("yov", [128, D])
        kp_o = self.dout("kp", [512, 512])
        vp_o = self.dout("vp", [512, 512])
        glap_o = self.dout("glap", [4, 64, 128])
        convp_o = self.dout("convp", [88, 128])
        ks_o = self.dout("ks", [128, 512])
        vs_o = self.dout("vs", [128, 512])
        glas_o = self.dout("glas", [4, 4, 64, 128])
        convs_o = self.dout("convs", [4, 88, 128])

        win_b = self.dscr("win_b", [D, DIN])
        wbra_b = self.dscr("wbra_b", [512, D])
        wbrb_b = self.dscr("wbrb_b", [512, D])
        wout_b = self.dscr("wout_b", [D, D])
        wup_b = self.dscr("wup_b", [D, 2 * DFF])
        wdown_b = self.dscr("wdown_b", [DFF, D])

        self.banks = [ps("bank%d" % i, [128, 512]) for i in range(5)]
        obank = [ps("obank%d" % i, [128, 512]) for i in range(2)]
        pbf = ps("pbf", [128, 1024], BF16)

        self.wbufs = [sb("wbuf%d" % i, [128, 4096], BF16) for i in range(3)]
        ident_f = sb("ident_f", [128, 128])
        ident_b = sb("ident_b", [128, 128], BF16)
        ones_b = sb("ones_b", [128, 128], BF16)
        triu_f = sb("triu_f", [128, 128])
        trisl_f = sb("trisl_f", [128, 128])
        ones_f = sb("ones_f", [128, 128])
        epsb = sb("epsb", [128, 1])
        mod = sb("mod", [128, 6, KC, 5])
        Am = sb("Am", [128, KC, 5]); Bm = sb("Bm", [128, KC, 5]); Gm = sb("Gm", [128, KC, 5])
        Af = sb("Af", [128, KC, 5]); Bf = sb("Bf", [128, KC, 5]); Gf = sb("Gf", [128, KC, 5])
        AmP = sb("AmP", [128, KC]); BmP = sb("BmP", [128, KC])
        AfP = sb("AfP", [128, KC]); BfP = sb("BfP", [128, KC])
        gT = sb("gT", [128, 32])
        badaT = sb("badaT", [128, 48])
        cT = sb("cT", [128, 40])
        siluT = sb("siluT", [128, 40], BF16)
        flag = sb("flag_s", [128, 1])
        wdwT = sb("wdwT", [128, 3, 44])
        bdwT = sb("bdwT", [128, 44])
        wgk_f = sb("wgk_f", [17, 256])
        wgk_b = sb("wgk_b", [17, 256], BF16)
        tab_prev = sb("tab_prev", [128, 8, 128], BF16)
        tab_own = sb("tab_own", [128, 8, 128], BF16)
        tab_mask = sb("tab_mask", [128, 128], BF16)
        tabf = sb("tabf", [128, 8, 128])
        cvec_s = sb("cvec_s", [128, 8])
        ggla_s = sb("ggla_s", [128, 512])
        stage = sb("stage", [128, 128])
        hist = sb("hist", [128, 44, 2])
        hist_s = sb("hist_s", [128, 4, 44, 2])
        h88 = sb("h88", [128, 2, 44])
        convst = sb("convst", [128, 128])
        xin = [sb("xin%d" % i, [128, D]) for i in range(2)]
        xT = sb("xT", [128, KC, NBT])
        hT = sb("hT", [128, KC, NBT], BF16)
        sq = sb("sq", [128, 2, NBT], BF16)
        rbc = sb("rbc", [128, NBT])
        tmpf = sb("tmpf", [128, NBT])
        tmpf2 = sb("tmpf2", [128, NBT])
        qaT = sb("qaT", [64, 8, NBT], BF16)
        kaT = [sb("kaT%d" % i, [64, 8, 128], BF16) for i in range(8)]
        vaug = [sb("vaug%d" % i, [128, 8, 65], BF16) for i in range(8)]
        kvout = sb("kvout", [128, 2, 512])
        qbT = sb("qbT", [64, 4, NBT])
        kbT = sb("kbT", [64, 4, NBT])
        gkT = sb("gkT", [32, NBT], BF16)
        pT = sb("pT", [128, 5, 8, 128], BF16)
        ya_tok = sb("ya_tok", [128, 512], BF16)
        rden = sb("rden", [128, 8])
        yaT = sb("yaT", [128, 4, NBT], BF16)
        ybT = sb("ybT", [128, 4, NBT], BF16)
        kb_tok = sb("kb_tok", [128, 256])
        vb_tok = sb("vb_tok", [128, 4, 128], BF16)
        gb2 = sb("gb2", [128, 512])
        gtanh = sb("gtanh", [128, 512])
        Lsp = sb("Lsp", [128, 256])
        e_sb = sb("e_sb", [128, 256])
        e1 = sb("e1", [64, 4, 128]); e2 = sb("e2", [64, 4, 128])
        qtT = sb("qtT", [64, 4, 128], BF16); ktT = sb("ktT", [64, 4, 128], BF16)
        kend = sb("kend", [128, 256], BF16)
        dec = sb("dec", [64, 4])
        attT = sb("attT", [128, 4, 128], BF16)
        Sst = sb("Sst", [64, 4, 128])
        Sbf = sb("Sbf", [64, 4, 128], BF16)
        ssq = sb("ssq", [128, 4]); rgl = sb("rgl", [128, 4])
        junk = sb("junk", [128, 128])
        yb_tok = sb("yb_tok", [128, 512], BF16)
        sgA = sb("sgA", [128, NBT]); sgB = sb("sgB", [128, NBT])
        mergedT = sb("mergedT", [128, KC, NBT], BF16)
        m2 = sb("m2", [128, KC, NBT])
        ua = [sb("ua%d" % i, [128, NBT + 8]) for i in range(2)]
        y0 = [sb("y0_%d" % i, [128, NBT]) for i in range(2)]
        y1 = [sb("y1_%d" % i, [128, NBT]) for i in range(2)]
        gel = sb("gel", [128, NBT])
        actT = sb("actT", [128, NFF, NBT], BF16)
        kcT = sb("kcT", [64, 8, 512], BF16)
        vcaug = sb("vcaug", [128, 4, 8, 65], BF16)
        vown = sb("vown", [32, 8, 65], BF16)
        Ss = sb("Ss", [64, 4, 128]); Ssb = sb("Ssb", [64, 4, 128], BF16)

        def cast_w(src, dst, rows, name):
            keys = []
            for r in range(0, rows, 128):
                self.dma("pool_dma", dst[r:r + 128, :], src[r:r + 128, :], (), (K("scr", name, r),), ("cast", name))
                keys.append(K("scr", name, r))
            return tuple(keys)

        def const_mask(t, keyname, pattern, cm, base, cmp_op):
            self.memset("pool", t[:], 1.0, (K(keyname),))
            S.add("pool", lambda e: e.affine_select(t[:], t[:], pattern=pattern, compare_op=cmp_op, fill=0.0,
                                                    base=base, channel_multiplier=cm), (K(keyname),), (K(keyname),))
        self.memset("pool", ident_f[:], 0.0, (K("ident_f"),))
        S.add("pool", lambda e: e.affine_select(ident_f[:], ident_f[:], pattern=[[-1, 128]], compare_op=ALU.not_equal,
                                                fill=1.0, base=0, channel_multiplier=1), (K("ident_f"),), (K("ident_f"),))
        const_mask(triu_f, "triu_f", [[1, 128]], -1, 0, ALU.is_ge)
        const_mask(trisl_f, "trisl_f", [[-1, 128]], 1, -1, ALU.is_ge)
        self.cp("dve", ident_b[:], ident_f[:], (K("ident_f"),), (K("ident_b"),))
        self.memset("dve", ones_b[:], 1.0, (K("ones_b"),))
        self.memset("dve", ones_f[:], 1.0, (K("ones_f"),))
        self.memset("dve", epsb[:], EPS, (K("epsb"),))
        self.memset("dve", gkT[:], 1.0, (K("gkT_ones"),))
        self.memset("dve", hist[:], 0.0, (K("hist"),))
        self.memset("dve", Sst[:], 0.0, (K("Sst"),))
        self.memset("dve", Sbf[:], 0.0, (K("Sst", "b"),))

        def load_T(src2d, rows, dst, kname, view=None):
            self.dma("pool_dma", stage[0:rows, :], src2d, (), (K("stage"),), ("stage",))
            bk, bkey = self.bank()
            self.tr(bk[:, 0:rows], stage[0:rows, :], ident_f[0:rows, 0:rows], (K("stage"), K("ident_f")), (bkey,))
            src = bk[:, 0:rows] if view is None else view(bk[:, 0:rows])
            self.cp("dve", dst, src, (bkey,), (K(kname),))

        load_T(gvec, 32, gT[:], "gT")
        load_T(b_ada, 48, badaT[:], "badaT")
        load_T(crow, 40, cT[:], "cT")
        for j in range(3):
            load_T(w_dw[j * 44:(j + 1) * 44, :], 44, wdwT[:, j, :], "wdwT%d" % j)
        load_T(b_dw, 44, bdwT[:], "bdwT")
        self.dma("pool_dma", flag[:], flag_d, (), (K("flag"),), ("misc", 0))
        self.dma("pool_dma", wgk_f[0:16, :], w_gk2, (), (K("wgk_f0"),), ("misc", 1))
        self.dma("pool_dma", wgk_f[16:17, :], b_gk, (), (K("wgk_f1"),), ("misc", 2))
        self.cp("dve", wgk_b[:], wgk_f[:], (K("wgk_f0"), K("wgk_f1")), (K("wgk_b"),))
        self.dma("pool_dma", cvec_s[:], cvec, (), (K("cvec"),), ("misc", 3))
        self.dma("pool_dma", ggla_s[:], ggla, (), (K("ggla"),), ("misc", 4))
        self.ts("dve", ggla_s[:], ggla_s[:], 0.5, None, ALU.mult, None, (K("ggla"),), (K("ggla"),))
        for ti, tdst in ((0, tab_prev), (1, tab_own)):
            self.dma("pool_dma", tabf[:], btab[ti], (), (K("tabf"),), ("misc", 5))
            for h in range(8):
                self.ts("dve", tdst[:, h, :], tabf[:, h, :], cvec_s[:, h:h + 1], None, ALU.subtract, None,
                        (K("tabf"), K("cvec")), (K("tab", ti, h),))
        tabkeys = tuple(K("tab", ti, h) for ti in range(2) for h in range(8)) + (K("tab_mask"), K("tab_m2"))
        self.memset("dve", tab_own[64:128, :, 0:64], NEG, (K("tab_m2"),))
        self.memset("dve", tab_mask[:], 0.0, (K("tab_mask"),))
        self.memset("dve", tab_mask[0:64, 64:128], NEG, (K("tab_mask"),))

        kwin = cast_w(w_in, win_b, D, "win")

        self.act(tmpf[:, 0:40], cT[:], AF.Tanh, (K("cT"),), (K("tmpf"),), scale=0.5)
        self.ts("dve", tmpf[:, 0:40], tmpf[:, 0:40], 0.5, 0.5, ALU.mult, ALU.add, (K("tmpf"),), (K("tmpf"),))
        self.tt("dve", siluT[:], tmpf[:, 0:40], cT[:], ALU.mult, (K("tmpf"), K("cT")), (K("siluT"),))
        siluv = siluT[:].rearrange("p (s k) -> p k s", k=8)
        for g in range(12):
            slot = g % 3
            buf = self.wbufs[slot]
            self.dma("pool_dma", buf[:].rearrange("p (kc n) -> p kc n", kc=8),
                     w_ada.rearrange("(kc p) n -> p kc n", p=128)[:, :, g * 512:(g + 1) * 512],
                     (), (K("wbuf", slot),), ("ada", slot))
            bk, bkey = self.bank()
            for oc in range(4):
                for kc in range(8):
                    self.mm(bk[:, oc * 8:oc * 8 + 5], buf[:, kc * 512 + oc * 128: kc * 512 + (oc + 1) * 128], siluv[:, kc, :],
                            kc == 0, kc == 7, (K("wbuf", slot), K("siluT")), (bkey,))
            for oc in range(4):
                ch = g * 4 + oc
                self.ts("dve", mod[:, ch // 8, ch % 8, :], bk[:, oc * 8:oc * 8 + 5], badaT[:, ch:ch + 1], None, ALU.add, None,
                        (bkey, K("badaT")), (K("mod", ch),))
        modk = tuple(K("mod", ch) for ch in range(48))
        for kc in range(8):
            rw = modk + (K("gT"),)
            self.ts("dve", Am[:, kc, :], mod[:, 1, kc, :], 1.0, gT[:, kc:kc + 1], ALU.add, ALU.mult, rw, (K("Am", kc),))
            self.ts("dve", Af[:, kc, :], mod[:, 4, kc, :], 1.0, gT[:, 16 + kc:17 + kc], ALU.add, ALU.mult, rw, (K("Af", kc),))
            self.ts("dve", Gm[:, kc, :], mod[:, 2, kc, :], gT[:, 8 + kc:9 + kc], None, ALU.mult, None, rw, (K("Gm", kc),))
            self.ts("dve", Gf[:, kc, :], mod[:, 5, kc, :], gT[:, 24 + kc:25 + kc], None, ALU.mult, None, rw, (K("Gf", kc),))
        self.cp("dve", Bm[:], mod[:, 0, :, :], modk, (K("Bm"),))
        self.cp("dve", Bf[:], mod[:, 3, :, :], modk, (K("Bf"),))
        k8 = lambda n: tuple(K(n, kc) for kc in range(8))
        self.ts("dve", AmP[:], Am[:, :, 0], flag[:, 0:1], None, ALU.mult, None, k8("Am") + (K("flag"),), (K("AmP"),))
        self.ts("dve", AfP[:], Af[:, :, 0], flag[:, 0:1], None, ALU.mult, None, k8("Af") + (K("flag"),), (K("AfP"),))
        self.ts("dve", BmP[:], Bm[:, :, 0], flag[:, 0:1], None, ALU.mult, None, (K("Bm"), K("flag")), (K("BmP"),))
        self.ts("dve", BfP[:], Bf[:, :, 0], flag[:, 0:1], None, ALU.mult, None, (K("Bf"), K("flag")), (K("BfP"),))
        modkeys = {"Am": k8("Am"), "Af": k8("Af"), "Gm": k8("Gm"), "Gf": k8("Gf"), "Bm": (K("Bm"),), "Bf": (K("Bf"),),
                   "AmP": (K("AmP"),), "AfP": (K("AfP"),), "BmP": (K("BmP"),), "BfP": (K("BfP"),)}

        kwbra = cast_w(w_br_a, wbra_b, 512, "wbra")
        kwbrb = cast_w(w_br_b, wbrb_b, 512, "wbrb")
        kwout = cast_w(w_out, wout_b, D, "wout")
        kwup = cast_w(w_up, wup_b, D, "wup")
        kwdown = cast_w(w_down, wdown_b, DFF, "wdown")

        def load_xT(src_rows, ntile):
            for t in range(ntile):
                xb = xin[t % 2]
                kx = K("xin", t % 2)
                self.dma("sp", xb[:], src_rows[t * 128:(t + 1) * 128, :], (), (kx,), ("xin", t % 2))
                for half in range(2):
                    bk, bkey = self.bank()
                    for j in range(4):
                        kc = half * 4 + j
                        self.tr(bk[:, j * 128:(j + 1) * 128], xb[:, kc * 128:(kc + 1) * 128], ident_f[:], (kx, K("ident_f")), (bkey,))
                    self.cp("act", xT[:, half * 4:half * 4 + 4, t * 128:(t + 1) * 128],
                            bk[:].rearrange("p (j n) -> p j n", j=4), (bkey,), tuple(K("xT", half * 4 + j, t) for j in range(4)))

        def rms_bcast(srcT, n, src_keys_fn):
            bk, bkey = self.bank()
            for kc in range(8):
                self.act(sq[:, kc % 2, 0:n], srcT[:, kc, 0:n], AF.Square, src_keys_fn(kc), (K("sq", kc % 2),))
                self.mm(bk[:, 0:n], ones_b[:], sq[:, kc % 2, 0:n], kc == 0, kc == 7, (K("ones_b"), K("sq", kc % 2)), (bkey,))
            self.act(tmpf[:, 0:n], bk[:, 0:n], AF.Ln, (bkey, K("epsb")), (K("tmpf"),), bias=epsb[:, 0:1], scale=1.0 / D)
            self.act(rbc[:, 0:n], tmpf[:, 0:n], AF.Exp, (K("tmpf"),), (K("rbc"),), scale=-0.5)

        def modulate(n, ntile, Asc, Bsc, segs):
            for kc in range(8):
                xk = tuple(K("xT", kc, t) for t in range(ntile))
                self.tt("dve", tmpf2[:, 0:n], xT[:, kc, 0:n], rbc[:, 0:n], ALU.mult, xk + (K("rbc"),), (K("tmpf2"),))
                for (c0, c1, Afn, Bfn) in segs:
                    self.act(hT[:, kc, c0:c1], tmpf2[:, c0:c1], AF.Identity, (K("tmpf2"),) + modkeys[Asc] + modkeys[Bsc], (K("hT", kc),),
                             bias=Bfn(kc), scale=Afn(kc))

        xkeys = lambda ntile: (lambda kc: tuple(K("xT", kc, t) for t in range(ntile)))

        def proj_fm(wd, wkeys, c0, n, evac):
            wg, wk = self.wload(wd, wkeys, 8, c0, 128)
            bk, bkey = self.bank()
            for kc in range(8):
                self.mm(bk[:, 0:n], wg(kc), hT[:, kc, 0:n], kc == 0, kc == 7, (wk, K("hT", kc)), (bkey,))
            evac(bk, bkey)

        def proj_fm_group(c0, nch, n, evac):
            wg, wk = self.wload(win_b, kwin, 8, c0, 128 * nch)
            for j in range(nch):
                bk, bkey = self.bank()
                for kc in range(8):
                    self.mm(bk[:, 0:n], wg(kc, j * 128, (j + 1) * 128), hT[:, kc, 0:n], kc == 0, kc == 7, (wk, K("hT", kc)), (bkey,))
                evac(j, bk, bkey)

        def proj_fm_heads(c0, nheads, n, evac):
            wg, wk = self.wload(win_b, kwin, 8, c0, 64 * nheads)
            for h in range(nheads):
                bk, bkey = self.bank()
                for kc in range(8):
                    self.mm(bk[0:64, 0:n], wg(kc, h * 64, (h + 1) * 64), hT[:, kc, 0:n], kc == 0, kc == 7, (wk, K("hT", kc)), (bkey,))
                evac(h, bk, bkey)

        def proj_tm(c0, ncols, units, evac):
            wg, wk = self.wload(win_b, kwin, 8, c0, ncols)
            for ui, (col0, C) in enumerate(units):
                bk, bkey = self.bank()
                for kc in range(8):
                    self.mm(bk[0:C, 0:ncols], hT[:, kc, col0:col0 + C], wg(kc), kc == 0, kc == 7, (wk, K("hT", kc)), (bkey,))
                evac(ui, bk, bkey)

        def gla_block(C, col0, S_t, S_b, Skey, out_cb, prefix_only):
            bk, bkey = self.bank()
            self.mm(bk[0:C, 0:256], gkT[0:17, col0:col0 + C], wgk_b[0:17, :], True, True, (K("gkT"), K("gkT_ones"), K("wgk_b")), (bkey,))
            self.act(e_sb[0:C, :], bk[0:C, 0:256], AF.Exp, (bkey,), (K("e_sb"),), scale=-1.0)
            self.act(Lsp[0:C, :], e_sb[0:C, :], AF.Ln, (K("e_sb"),), (K("Lsp"),), bias=1.0)
            bk2, bkey2 = self.bank()
            self.mm(bk2[0:C, 0:256], trisl_f[0:C, 0:C], Lsp[0:C, :], True, True, (K("trisl_f"), K("Lsp")), (bkey2,))
            self.act(e_sb[0:C, :], bk2[0:C, 0:256], AF.Exp, (bkey2,), (K("e_sb"),), scale=-1.0 / 16)
            self.tt("dve", kend[0:C, :], kb_tok[0:C, :], e_sb[0:C, :], ALU.mult, (K("kb_tok"), K("e_sb")), (K("kend"),))
            bk3, bkey3 = self.bank()
            for h in range(4):
                self.mm(bk3[0:64, h * 128:h * 128 + C], Lsp[0:C, h * 64:(h + 1) * 64], triu_f[0:C, 0:C], True, True,
                        (K("Lsp"), K("triu_f")), (bkey3,))
            b3 = bk3[0:64, :].rearrange("p (c t) -> p c t", c=4)
            self.act(dec[:, :], b3[:, :, C - 1], AF.Exp, (bkey3,), (K("dec"),), scale=-1.0 / 16)
            if not prefix_only:
                self.act(e1[:, :, 0:C], b3[:, :, 0:C], AF.Exp, (bkey3,), (K("e1"),), scale=-1.0 / 16)
                self.act(e2[:, :, 0:C], b3[:, :, 0:C], AF.Exp, (bkey3,), (K("e2"),), scale=1.0 / 16)
                self.stt("dve", qtT[:, :, 0:C], qbT[:, :, col0:col0 + C], 0.125, e1[:, :, 0:C], ALU.mult, ALU.mult,
                         tuple(K("qbT", j) for j in range(4)) + (K("e1"),), (K("qtT"),))
                self.tt("dve", ktT[:, :, 0:C], kbT[:, :, col0:col0 + C], e2[:, :, 0:C], ALU.mult, tuple(K("kbT", j) for j in range(4)) + (K("e2"),), (K("ktT"),))
                KG = int(os.environ.get("KG", "99"))
                if KG <= 1:
                    return
                bk4, bkey4 = self.bank()
                for h in range(4):
                    self.mm(bk4[0:C, h * 128:h * 128 + C], ktT[:, h, 0:C], qtT[:, h, 0:C], True, True,
                            (K("ktT"), K("qtT")), (bkey4,))
                for h in range(4):
                    self.tt("dve", attT[0:C, h, 0:C], bk4[0:C, h * 128:h * 128 + C], triu_f[0:C, 0:C], ALU.mult,
                            (bkey4, K("triu_f")), (K("attT", h),))
                if KG <= 2:
                    return
                bk5, bkey5 = self.bank()
                for h in range(4):
                    self.mm(bk5[0:C, h * 128:(h + 1) * 128], attT[0:C, h, 0:C], vb_tok[0:C, h, :], True, False, (K("attT", h), K("vb_tok")), (bkey5,))
                    self.mm(bk5[0:C, h * 128:(h + 1) * 128], qtT[:, h, 0:C], S_b[:, h, :], False, True,
                            (K("qtT"), Skey + ("b",)), (bkey5,))
                if KG <= 3:
                    return
                out_cb(bk5, bkey5)
            bk6, bkey6 = self.bank()
            for h in range(4):
                self.mm(bk6[0:64, h * 128:(h + 1) * 128], kend[0:C, h * 64:(h + 1) * 64], vb_tok[0:C, h, :], True, True, (K("kend"), K("vb_tok")), (bkey6,))
            for h in range(4):
                self.stt("dve", S_t[:, h, :], S_t[:, h, :], dec[:, h:h + 1], bk6[0:64, h * 128:(h + 1) * 128],
                         ALU.mult, ALU.add, (Skey, K("dec"), bkey6), (Skey,))
            self.cp("act", S_b[:], S_t[:], (Skey,), (Skey + ("b",),))

        def gla_out(C, obk, obkey):
            self.memset("dve", ssq[0:C, :], 0.0, (K("ssq"),))
            for h in range(4):
                self.act(junk[0:C, :], obk[0:C, h * 128:(h + 1) * 128], AF.Square, (obkey,), (K("junk"), K("ssq")), accum_out=ssq[0:C, h:h + 1])
            self.act(rgl[0:C, :], ssq[0:C, :], AF.Ln, (K("ssq"), K("epsb")), (K("rgl"),), bias=epsb[0:C, 0:1], scale=1.0 / 128)
            self.act(rgl[0:C, :], rgl[0:C, :], AF.Exp, (K("rgl"),), (K("rgl"),), scale=-0.5)
            self.act(gtanh[0:C, :], gb2[0:C, :], AF.Tanh, (K("gb2"),), (K("gtanh"),), scale=0.5)
            self.stt("dve", gtanh[0:C, :], gtanh[0:C, :], 1.0, gb2[0:C, :], ALU.add, ALU.mult, (K("gtanh"), K("gb2")), (K("gtanh"),))
            self.tt("dve", gtanh[0:C, :], gtanh[0:C, :], ggla_s[0:C, :], ALU.mult, (K("gtanh"), K("ggla")), (K("gtanh"),))
            for h in range(4):
                self.stt("dve", yb_tok[0:C, h * 128:(h + 1) * 128], obk[0:C, h * 128:(h + 1) * 128], rgl[0:C, h:h + 1],
                         gtanh[0:C, h * 128:(h + 1) * 128], ALU.mult, ALU.mult, (obkey, K("rgl"), K("gtanh")), (K("yb_tok"),))

        def gla_units(units, pre, S_of):
            wgk_, wkk = self.wload(win_b, kwin, 8, O_KB, 256)
            wgv_, wkv = self.wload(win_b, kwin, 8, O_VB, 512)
            if not pre:
                wgg_, wkg = self.wload(win_b, kwin, 8, O_GB, 512)
            for ui, (col0, C) in enumerate(units):
                bk, bkey = self.bank()
                for kc in range(8):
                    self.mm(bk[0:C, 0:256], hT[:, kc, col0:col0 + C], wgk_(kc), kc == 0, kc == 7, (wkk, K("hT", kc)), (bkey,))
                self.cp("act", kb_tok[0:C, :], bk[0:C, 0:256], (bkey,), (K("kb_tok"),))
                bk, bkey = self.bank()
                for kc in range(8):
                    self.mm(bk[0:C, :], hT[:, kc, col0:col0 + C], wgv_(kc), kc == 0, kc == 7, (wkv, K("hT", kc)), (bkey,))
                self.cp("act", vb_tok[0:C, :, :].rearrange("p h e -> p (h e)"), bk[0:C, :], (bkey,), (K("vb_tok"),))
                if not pre:
                    bk, bkey = self.bank()
                    for kc in range(8):
                        self.mm(bk[0:C, :], hT[:, kc, col0:col0 + C], wgg_(kc), kc == 0, kc == 7, (wkg, K("hT", kc)), (bkey,))
                    self.cp("act", gb2[0:C, :], bk[0:C, :], (bkey,), (K("gb2"),))
                S_t, S_b, Skey, before_fn, after_fn = S_of(ui)
                if before_fn is not None:
                    before_fn()

                def out_cb(obk, obkey, col0=col0, C=C, ui=ui):
                    gla_out(C, obk, obkey)
                    if int(os.environ.get("KG", "99")) <= 4:
                        return
                    for c in range(4):
                        self.tr(pbf[:, 512 + c * 128:512 + c * 128 + C], yb_tok[0:C, c * 128:(c + 1) * 128], ident_b[0:C, 0:C],
                                (K("yb_tok"), K("ident_b")), (K("pbf2"),))
                    self.cp("act", ybT[:, :, col0:col0 + C], pbf[:, 512:1024].rearrange("p (c n) -> p c n", c=4)[:, :, 0:C],
                            (K("pbf2"),), (K("ybT", ui),))
                gla_block(C, col0, S_t, S_b, Skey, out_cb, pre)
                if after_fn is not None:
                    after_fn()

        def attn_tile(tglob, tl):
            for kb in range(5):
                slot = (tglob - 4 + kb) % 8
                for hg in range(2):
                    bk, bkey = self.bank()
                    for hh in range(4):
                        h = hg * 4 + hh
                        tab = {0: tab_mask[:, :], 3: tab_prev[:, h, :], 4: tab_own[:, h, :]}.get(kb)
                        self.mm(bk[:, hh * 128:(hh + 1) * 128], kaT[slot][:, h, :], qaT[:, h, tl * 128:(tl + 1) * 128],
                                True, tab is None, (K("kaT", slot, h), K("qaT", h)), (bkey,))
                        if tab is not None:
                            self.mm(bk[:, hh * 128:(hh + 1) * 128], ident_b[:], tab, False, True, (K("ident_b"),) + tabkeys, (bkey,))
                    self.act(pT[:, kb, hg * 4:(hg + 1) * 4, :], bk[:].rearrange("p (h q) -> p h q", h=4), AF.Exp, (bkey,), (K("pT", kb, hg),))
            for hg in range(2):
                ob, obkey = obank[hg], K("obank", hg)
                for hh in range(4):
                    h = hg * 4 + hh
                    for kb in range(5):
                        slot = (tglob - 4 + kb) % 8
                        self.mm(ob[:, hh * 65:(hh + 1) * 65], pT[:, kb, h, :], vaug[slot][:, h, :], kb == 0, kb == 4,
                                (K("pT", kb, hg), K("vaug", slot), K("vaug1", slot)), (obkey,))
            for hg in range(2):
                ob, obkey = obank[hg], K("obank", hg)
                ov = ob[:, 0:260].rearrange("p (h e) -> p h e", h=4)
                self.ts("dve", rden[:, hg * 4:(hg + 1) * 4], ov[:, :, 64], 1e-30, None, ALU.add, None, (obkey,), (K("rden", hg),))
                S.add("dve", lambda e, hg=hg: e.reciprocal(rden[:, hg * 4:(hg + 1) * 4], rden[:, hg * 4:(hg + 1) * 4]), (K("rden", hg),), (K("rden", hg),))
                for hh in range(4):
                    h = hg * 4 + hh
                    self.ts("dve", ya_tok[:, h * 64:(h + 1) * 64], ov[:, hh, 0:64], rden[:, h:h + 1], None, ALU.mult, None,
                            (obkey, K("rden", hg)), (K("ya_tok", h),))
            yk = tuple(K("ya_tok", h) for h in range(8))
            for c in range(4):
                self.tr(pbf[:, c * 128:(c + 1) * 128], ya_tok[:, c * 128:(c + 1) * 128], ident_b[:], yk + (K("ident_b"),), (K("pbf1"),))
            self.cp("act", yaT[:, :, tl * 128:(tl + 1) * 128], pbf[:, 0:512].rearrange("p (c n) -> p c n", c=4), (K("pbf1"),), (K("yaT", tl),))

        def attn_sample(s, slot):
            for rb in range(4):
                xb = xin[rb % 2]
                kx = K("xin", rb % 2)
                self.dma("sp", xb[:, 0:512], ck[s, rb * 128:(rb + 1) * 128, :], (), (kx,), ("xin", rb % 2))
                for hg in range(2):
                    bk, bkey = self.bank()
                    for hh in range(4):
                        h = hg * 4 + hh
                        self.tr(bk[0:64, hh * 128:(hh + 1) * 128], xb[:, h * 64:(h + 1) * 64], ident_f[:], (kx, K("ident_f")), (bkey,))
                    self.cp("act", kcT[:, hg * 4:(hg + 1) * 4, rb * 128:(rb + 1) * 128], bk[0:64, :].rearrange("p (c n) -> p c n", c=4),
                            (bkey,), (K("kcT", rb, hg),))
                self.dma("sp", xb[:, 512:1024], cv[s, rb * 128:(rb + 1) * 128, :], (), (K("xinv", rb % 2),), ("xinv", rb % 2))
                self.cp("dve", vcaug[:, rb, :, 0:64], xb[:, 512:1024].rearrange("p (h e) -> p h e", h=8), (K("xinv", rb % 2),), (K("vcaug", rb),))
                self.cp("dve", vcaug[:, rb, :, 64], ones_f[:, 0:8], (K("ones_f"),), (K("vcaug1", rb),))
            q0 = s * 32
            for kb in range(5):
                kn = 128 if kb < 4 else 32
                bk, bkey = self.bank()
                for h in range(8):
                    if kb < 4:
                        lhs = kcT[:, h, kb * 128:(kb + 1) * 128]
                        rk = (K("kcT", kb, 0), K("kcT", kb, 1))
                    else:
                        lhs = kaT[slot][:, h, q0:q0 + 32]
                        rk = (K("kaT", slot, h),)
                    tab = {3: tab_prev[:, h, 0:32], 4: tab_own[0:32, h, 0:32]}.get(kb)
                    self.mm(bk[0:kn, h * 32:(h + 1) * 32], lhs, qaT[:, h, q0:q0 + 32], True, tab is None, rk + (K("qaT", h),), (bkey,))
                    if tab is not None:
                        self.mm(bk[0:kn, h * 32:(h + 1) * 32], ident_b[0:kn, 0:kn], tab, False, True, (K("ident_b"),) + tabkeys, (bkey,))
                self.act(pT[0:kn, kb, :, 0:32], bk[0:kn, 0:256].rearrange("p (h q) -> p h q", h=8), AF.Exp, (bkey,), (K("pT", kb, 0), K("pT", kb, 1)))
            for h in range(8):
                hg, hh = h // 4, h % 4
                for kb in range(5):
                    kn = 128 if kb < 4 else 32
                    rhs = vcaug[:, kb, h, :] if kb < 4 else vown[0:32, h, :]
                    rk = (K("vcaug", kb), K("vcaug1", kb)) if kb < 4 else (K("vown"), K("vown1"))
                    self.mm(obank[hg][0:32, hh * 65:(hh + 1) * 65], pT[0:kn, kb, h, 0:32], rhs, kb == 0, kb == 4,
                            (K("pT", kb, hg),) + rk, (K("obank", hg),))
            for hg in range(2):
                ob, obkey = obank[hg], K("obank", hg)
                ov = ob[0:32, 0:260].rearrange("p (h e) -> p h e", h=4)
                S.add("dve", lambda e, ov=ov, hg=hg: e.reciprocal(rden[0:32, hg * 4:(hg + 1) * 4], ov[:, :, 64]), (obkey,), (K("rden", hg),))
                for hh in range(4):
                    h = hg * 4 + hh
                    self.ts("dve", ya_tok[0:32, h * 64:(h + 1) * 64], ov[:, hh, 0:64], rden[0:32, h:h + 1], None, ALU.mult, None,
                            (obkey, K("rden", hg)), (K("ya_tok", h),))
            yk = tuple(K("ya_tok", h) for h in range(8))
            for c in range(4):
                self.tr(pbf[:, c * 128:c * 128 + 32], ya_tok[0:32, c * 128:(c + 1) * 128], ident_b[0:32, 0:32], yk + (K("ident_b"),), (K("pbf1"),))
            self.cp("act", yaT[:, :, q0:q0 + 32], pbf[:, 0:512].rearrange("p (c n) -> p c n", c=4)[:, :, 0:32], (K("pbf1"),), (K("yaT", s),))

        def resid(n, ntile, segsG):
            for kc in range(8):
                self.tt("dve", tmpf2[:, 0:n], m2[:, kc, 0:n], rbc[:, 0:n], ALU.mult, (K("m2", kc), K("rbc")), (K("tmpf2"),))
                xk = tuple(K("xT", kc, t) for t in range(ntile))
                for (c0, c1, Gfn, gname) in segsG:
                    self.stt("dve", xT[:, kc, c0:c1], tmpf2[:, c0:c1], Gfn(kc), xT[:, kc, c0:c1], ALU.mult, ALU.add,
                             (K("tmpf2"),) + modkeys[gname] + xk, xk)

        def merge_and_out(n, ntile, nunit, segsG):
            yak = tuple(K("yaT", t) for t in range(nunit))
            ybk = tuple(K("ybT", t) for t in range(nunit))
            for c in range(8):
                def ev_gate(dst, dkey):
                    def f(bk, bkey):
                        self.act(dst[:, 0:n], bk[:, 0:n], AF.Tanh, (bkey,), (dkey,), scale=0.5)
                        self.ts("dve", dst[:, 0:n], dst[:, 0:n], 0.5, 0.5, ALU.mult, ALU.add, (dkey,), (dkey,))
                    return f
                proj_fm(win_b, kwin, O_GA + c * 128, n, ev_gate(sgA, K("sgA")))
                proj_fm(win_b, kwin, O_GBR + c * 128, n, ev_gate(sgB, K("sgB")))
                wg, wk = self.wload(wbra_b, kwbra, 4, c * 128, 128)
                bk, bkey = self.bank()
                for kc in range(4):
                    self.mm(bk[:, 0:n], wg(kc), yaT[:, kc, 0:n], kc == 0, kc == 3, (wk,) + yak, (bkey,))
                self.tt("dve", sgA[:, 0:n], sgA[:, 0:n], bk[:, 0:n], ALU.mult, (K("sgA"), bkey), (K("sgA"),))
                wg, wk = self.wload(wbrb_b, kwbrb, 4, c * 128, 128)
                bk, bkey = self.bank()
                for kc in range(4):
                    self.mm(bk[:, 0:n], wg(kc), ybT[:, kc, 0:n], kc == 0, kc == 3, (wk,) + ybk, (bkey,))
                self.tt("dve", sgB[:, 0:n], sgB[:, 0:n], bk[:, 0:n], ALU.mult, (K("sgB"), bkey), (K("sgB"),))
                self.tt("dve", mergedT[:, c, 0:n], sgA[:, 0:n], sgB[:, 0:n], ALU.add, (K("sgA"), K("sgB")), (K("mergedT", c),))
            for c in range(8):
                wg, wk = self.wload(wout_b, kwout, 8, c * 128, 128)
                bk, bkey = self.bank()
                for kc in range(8):
                    self.mm(bk[:, 0:n], wg(kc), mergedT[:, kc, 0:n], kc == 0, kc == 7, (wk, K("mergedT", kc)), (bkey,))
                self.cp("act", m2[:, c, 0:n], bk[:, 0:n], (bkey,), (K("m2", c),))
            rms_bcast(m2, n, lambda kc: (K("m2", kc),))
            resid(n, ntile, segsG)

        def ffn(n, segs_hist):
            for j in range(NFF):
                for half in range(2):
                    jj = half * NFF + j
                    wg, wk = self.wload(wup_b, kwup, 8, jj * 128, 128)
                    bk, bkey = self.bank()
                    for kc in range(8):
                        self.mm(bk[:, 0:n], wg(kc), hT[:, kc, 0:n], kc == 0, kc == 7, (wk, K("hT", kc)), (bkey,))
                    u = ua[half]
                    uk = K("ua", half)
                    uh = K("uah", half)
                    yk0 = K("y0", half)
                    yk1 = K("y1", half)
                    for si, (c0, c1, hfn, hkey) in enumerate(segs_hist):
                        w = c1 - c0
                        s0 = si * (w + 2)
                        self.cp("pool", u[:, s0:s0 + 2], hfn(jj), (hkey,), (uh,))
                        self.cp("act", u[:, s0 + 2:s0 + 2 + w], bk[:, c0:c1], (bkey,), (uk,))
                    self.act(y0[half][:, 0:n], bk[:, 0:n], AF.Identity, (bkey, K("wdwT2"), K("bdwT")), (yk0,),
                             bias=bdwT[:, jj:jj + 1], scale=wdwT[:, 2, jj:jj + 1])
                    for si, (c0, c1, hfn, hkey) in enumerate(segs_hist):
                        w = c1 - c0
                        s0 = si * (w + 2)
                        self.stt("dve", y1[half][:, c0:c1], u[:, s0 + 1:s0 + 1 + w], wdwT[:, 1, jj:jj + 1], y0[half][:, c0:c1],
                                 ALU.mult, ALU.add, (uk, uh, yk0, K("wdwT1")), (yk1,))
                        self.stt("dve", y0[half][:, c0:c1], u[:, s0:s0 + w], wdwT[:, 0, jj:jj + 1], y1[half][:, c0:c1],
                                 ALU.mult, ALU.add, (uk, uh, yk1, K("wdwT0")), (yk0,))
                        self.cp("pool", hfn(jj), u[:, s0 + w:s0 + w + 2], (uk,), (hkey,))
                self.act(gel[:, 0:n], y0[0][:, 0:n], AF.Gelu_apprx_tanh, (K("y0", 0),), (K("gel"),))
                self.tt("dve", actT[:, j, 0:n], gel[:, 0:n], y0[1][:, 0:n], ALU.mult, (K("gel"), K("y0", 1)), (K("actT", j),))
            for c in range(8):
                wg, wk = self.wload(wdown_b, kwdown, NFF, c * 128, 128)
                bk, bkey = self.bank()
                for j in range(NFF):
                    self.mm(bk[:, 0:n], wg(j), actT[:, j, 0:n], j == 0, j == NFF - 1, (wk, K("actT", j)), (bkey,))
                self.cp("act", m2[:, c, 0:n], bk[:, 0:n], (bkey,), (K("m2", c),))
            rms_bcast(m2, n, lambda kc: (K("m2", kc),))

        def store_y(dst_rows, ntile):
            for t in range(ntile):
                yb_ = xin[t % 2]
                ky = K("xin", t % 2)
                for half in range(2):
                    bk, bkey = self.bank()
                    for j in range(4):
                        kc = half * 4 + j
                        self.tr(bk[:, j * 128:(j + 1) * 128], xT[:, kc, t * 128:(t + 1) * 128], ident_f[:], (K("xT", kc, t), K("ident_f")), (bkey,))
                    self.cp("act", yb_[:, half * 512:(half + 1) * 512], bk[:], (bkey,), (ky, K("xinv", t % 2)) if half else (ky,))
                self.dma("sp", dst_rows[t * 128:(t + 1) * 128, :], yb_[:], (ky, K("xinv", t % 2)), (), ("xin", t % 2))

        def store_hist(hsrc, hkey, dst):
            self.cp("dve", h88[:], hsrc.rearrange("p j t -> p t j"), (hkey,), (K("h88"),))
            bk, bkey = self.bank()
            self.tr(bk[0:88, 0:128], h88[:].rearrange("p t j -> p (t j)"), ident_f[:], (K("h88"), K("ident_f")), (bkey,))
            self.cp("dve", convst[0:88, :], bk[0:88, 0:128], (bkey,), (K("convst"),))
            self.dma("sp", dst, convst[0:88, :], (K("convst"),), (), ("convst",))

        LAST_KV0 = 16 + NMAIN - 4 + (100 if os.environ.get('KNOKV') else 0)

        def prompt_block(src_rows, ntile, tglob0, mode, dst_rows, kv_from):
            n = ntile * 128
            pre = mode == "prefix"
            flagged = mode != "full"
            units = [(t * 128, 128) for t in range(ntile)]
            load_xT(src_rows, ntile)
            rms_bcast(xT, n, xkeys(ntile))
            if flagged:
                modulate(n, ntile, "AmP", "BmP", [(0, n, lambda kc: AmP[:, kc:kc + 1], lambda kc: BmP[:, kc:kc + 1])])
            else:
                modulate(n, ntile, "Am", "Bm", [(0, n, lambda kc: Am[:, kc, 0:1], lambda kc: Bm[:, kc, 0:1])])
            if not pre:
                def ev_q(j, bk, bkey):
                    self.act(qaT[:, j, 0:n], bk[0:64, 0:n], AF.Identity, (bkey,), (K("qaT", j),), scale=0.125)
                proj_fm_heads(O_QA, 8, n, ev_q)
            if kv_from < ntile:
                def ev_k(j, bk, bkey):
                    for t in range(kv_from, ntile):
                        slot = (tglob0 + t) % 8
                        self.cp("act", kaT[slot][:, j, :], bk[0:64, t * 128:(t + 1) * 128], (bkey,), (K("kaT", slot, j),))
                proj_fm_heads(O_KA, 8, n, ev_k)
            if not pre:
                def ev_qb(j, bk, bkey):
                    self.cp("act", qbT[:, j, 0:n], bk[0:64, 0:n], (bkey,), (K("qbT", j),))
                def ev_kb(j, bk, bkey):
                    self.cp("act", kbT[:, j, 0:n], bk[0:64, 0:n], (bkey,), (K("kbT", j),))
                proj_fm_heads(O_QB, 4, n, ev_qb)
                proj_fm_heads(O_KB, 4, n, ev_kb)
            wg, wk = self.wload(win_b, kwin, 8, O_GK, 16)
            bk, bkey = self.bank()
            for kc in range(8):
                self.mm(bk[0:16, 0:n], wg(kc), hT[:, kc, 0:n], kc == 0, kc == 7, (wk, K("hT", kc)), (bkey,))
            self.cp("act", gkT[0:16, 0:n], bk[0:16, 0:n], (bkey,), (K("gkT"),))
            if kv_from < ntile:
                kvunits = units[kv_from:]
                def ev_v(ui, bk, bkey):
                    t = kv_from + ui
                    slot = (tglob0 + t) % 8
                    self.cp("act", vaug[slot][:, :, 0:64], bk[:].rearrange("p (h e) -> p h e", h=8), (bkey,), (K("vaug", slot),))
                    if flagged:
                        self.ts("dve", vaug[slot][:, :, 64], ones_f[:, 0:8], flag[:, 0:1], None, ALU.mult, None,
                                (K("ones_f"), K("flag")), (K("vaug1", slot),))
                    else:
                        self.cp("dve", vaug[slot][:, :, 64], ones_f[:, 0:8], (K("ones_f"),), (K("vaug1", slot),))
                    if mode == "full" and tglob0 + t >= LAST_KV0 and os.environ.get("KVSEL", "kv").find("v") >= 0:
                        r0 = (tglob0 + t - LAST_KV0) * 128
                        self.cp("act", kvout[:, 1, :], bk[:], (bkey,), (K("kvout", 1),))
                        self.dma("sp", vp_o[r0:r0 + 128, :], kvout[:, 1, :], (K("kvout", 1),), (), ("kvout", 1))
                proj_tm(O_VA, 512, kvunits, ev_v)
                if mode == "full" and tglob0 + ntile > LAST_KV0 and os.environ.get("KVSEL", "kv").find("k") >= 0:
                    tfirst = max(0, LAST_KV0 - tglob0)
                    def ev_ktok(ui, bk, bkey):
                        r0 = (tglob0 + tfirst + ui - LAST_KV0) * 128
                        self.cp("act", kvout[:, 0, :], bk[:], (bkey,), (K("kvout", 0),))
                        self.dma("sp", kp_o[r0:r0 + 128, :], kvout[:, 0, :], (K("kvout", 0),), (), ("kvout", 0))
                    proj_tm(O_KA, 512, units[tfirst:], ev_ktok)
            SUB = int(os.environ.get("KSUB", "99"))
            if SUB <= 0:
                return
            gla_units(units, pre, lambda ui: (Sst, Sbf, K("Sst"), None, None))
            if pre or SUB <= 1:
                return
            for t in range(ntile):
                attn_tile(tglob0 + t, t)
            if SUB <= 2:
                return
            merge_and_out(n, ntile, ntile, [(0, n, lambda kc: Gm[:, kc, 0:1], "Gm")])
            if SUB <= 3:
                return
            rms_bcast(xT, n, xkeys(ntile))
            if flagged:
                modulate(n, ntile, "AfP", "BfP", [(0, n, lambda kc: AfP[:, kc:kc + 1], lambda kc: BfP[:, kc:kc + 1])])
            else:
                modulate(n, ntile, "Af", "Bf", [(0, n, lambda kc: Af[:, kc, 0:1], lambda kc: Bf[:, kc, 0:1])])
            ffn(n, [(0, n, lambda jj: hist[:, jj, :], K("hist"))])
            if SUB <= 4:
                return
            resid(n, ntile, [(0, n, lambda kc: Gf[:, kc, 0:1], "Gf")])
            store_y(dst_rows, ntile)

        def sample_block():
            n = 128
            slot = 0
            units = [(s * 32, 32) for s in range(4)]
            load_xT(xsam, 1)
            rms_bcast(xT, n, xkeys(1))
            modulate(n, 1, "Am", "Bm", [(s * 32, s * 32 + 32, (lambda kc, s=s: Am[:, kc, s + 1:s + 2]), (lambda kc, s=s: Bm[:, kc, s + 1:s + 2]))
                                          for s in range(4)])
            def ev_q(j, bk, bkey):
                self.act(qaT[:, j, 0:n], bk[0:64, 0:n], AF.Identity, (bkey,), (K("qaT", j),), scale=0.125)
            proj_fm_heads(O_QA, 8, n, ev_q)
            def ev_k(j, bk, bkey):
                self.cp("act", kaT[slot][:, j, :], bk[0:64, 0:128], (bkey,), (K("kaT", slot, j),))
            proj_fm_heads(O_KA, 8, n, ev_k)
            def ev_qb(j, bk, bkey):
                self.cp("act", qbT[:, j, 0:n], bk[0:64, 0:n], (bkey,), (K("qbT", j),))
            def ev_kb(j, bk, bkey):
                self.cp("act", kbT[:, j, 0:n], bk[0:64, 0:n], (bkey,), (K("kbT", j),))
            proj_fm_heads(O_QB, 4, n, ev_qb)
            proj_fm_heads(O_KB, 4, n, ev_kb)
            wg, wk = self.wload(win_b, kwin, 8, O_GK, 16)
            bk, bkey = self.bank()
            for kc in range(8):
                self.mm(bk[0:16, 0:n], wg(kc), hT[:, kc, 0:n], kc == 0, kc == 7, (wk, K("hT", kc)), (bkey,))
            self.cp("act", gkT[0:16, 0:n], bk[0:16, 0:n], (bkey,), (K("gkT"),))
            def ev_vout(ui, bk, bkey):
                self.cp("act", kvout[:, 1, :], bk[:], (bkey,), (K("kvout", 1),))
                self.dma("sp", vs_o, kvout[:, 1, :], (K("kvout", 1),), (), ("kvout", 1))
            proj_tm(O_VA, 512, [(0, 128)], ev_vout)
            def ev_kout(ui, bk, bkey):
                self.cp("act", kvout[:, 0, :], bk[:], (bkey,), (K("kvout", 0),))
                self.dma("sp", ks_o, kvout[:, 0, :], (K("kvout", 0),), (), ("kvout", 0))
            proj_tm(O_KA, 512, [(0, 128)], ev_kout)
            def S_of(ui):
                def before():
                    self.dma("sp", Ss[:], sgla[ui].rearrange("h k v -> k h v"), (), (K("Ss"),), ("Ss",))
                    self.cp("act", Ssb[:], Ss[:], (K("Ss"),), (K("Ss", "b"),))
                def after():
                    self.dma("sp", glas_o[ui].rearrange("h k v -> k h v"), Ss[:], (K("Ss"),), (), ("Ss",))
                return (Ss, Ssb, K("Ss"), before, after)
            gla_units(units, False, S_of)
            wgv, wkv = self.wload(win_b, kwin, 8, O_VA, 512)
            for s in range(4):
                bk, bkey = self.bank()
                for kc in range(8):
                    self.mm(bk[0:32, :], hT[:, kc, s * 32:(s + 1) * 32], wgv(kc), kc == 0, kc == 7, (wkv, K("hT", kc)), (bkey,))
                self.cp("act", vown[:, :, 0:64], bk[0:32, :].rearrange("p (h e) -> p h e", h=8), (bkey,), (K("vown"),))
                self.cp("dve", vown[:, :, 64], ones_f[0:32, 0:8], (K("ones_f"),), (K("vown1"),))
                attn_sample(s, slot)
            segG = lambda Gt, name: [(s * 32, s * 32 + 32, (lambda kc, s=s: Gt[:, kc, s + 1:s + 2]), name) for s in range(4)]
            merge_and_out(n, 1, 4, segG(Gm, "Gm"))
            rms_bcast(xT, n, xkeys(1))
            modulate(n, 1, "Af", "Bf", [(s * 32, s * 32 + 32, (lambda kc, s=s: Af[:, kc, s + 1:s + 2]), (lambda kc, s=s: Bf[:, kc, s + 1:s + 2]))
                                          for s in range(4)])
            for s in range(4):
                load_T(sconv[s], 88, hist_s[:, s, :, :].rearrange("p j t -> p t j"), "hist_s%d" % s,
                       view=lambda a: a.rearrange("p (t j) -> p t j", t=2))
            ffn(n, [(s * 32, s * 32 + 32, (lambda jj, s=s: hist_s[:, s, jj, :]), K("hist_s%d" % s)) for s in range(4)])
            resid(n, 1, segG(Gf, "Gf"))
            store_y(ysam, 1)
            for s in range(4):
                store_hist(hist_s[:, s, :, :], K("hist_s%d" % s), convs_o[s])

        STG = int(os.environ.get("KSTAGE", "99"))
        if STG <= 0:
            return S.finalize()
        t0 = 0
        while t0 < 11:
            nt = min(NB, 11 - t0)
            prompt_block(xpre[t0 * 128:(t0 + nt) * 128, :], nt, t0, "prefix", None, nt)
            t0 += nt
            if STG <= 1:
                return S.finalize()
        while t0 < 15:
            nt = min(NB, 15 - t0)
            prompt_block(xpre[t0 * 128:(t0 + nt) * 128, :], nt, t0, "prefix", None, 0)
            t0 += nt
        if STG <= 2:
            return S.finalize()
        prompt_block(xov, 1, 15, "overlap", yov, 0)
        self.ts("dve", hist[:].rearrange("p j t -> p (j t)"), hist[:].rearrange("p j t -> p (j t)"), flag[:, 0:1], None, ALU.mult, None,
                (K("hist"), K("flag")), (K("hist"),))
        if STG <= 3:
            return S.finalize()
        for b in range(min(NMAIN // NB, int(os.environ.get('KMAIN', '99')))):
            prompt_block(xmain[b * NBT:(b + 1) * NBT, :], NB, 16 + b * NB, "full", ymain[b * NBT:(b + 1) * NBT, :], 0)
        self.dma("sp", glap_o.rearrange("h k v -> k h v"), Sst[:], (K("Sst"),), (), ("glap",))
        store_hist(hist[:], K("hist"), convp_o)
        if STG <= 4:
            return S.finalize()
        sample_block()
        return S.finalize()


_CACHE = {}


def _program():
    if "nc" not in _CACHE:
        b = Builder()
        b.build()
        _CACHE["nc"] = b.nc
    return _CACHE["nc"]


def kernel(x_prompt, x_sample, cache_k_a, cache_v_a, state_gla, state_conv, c_prompt, c_sample,
           w_ada, b_ada, g_pre_mix, g_post_mix, g_pre_ffn, g_post_ffn, w_in, w_gk2, b_gk,
           rel_bias, g_gla, w_br_a, w_br_b, w_out, w_up, w_dw, b_dw, w_down):
    f = lambda a: np.ascontiguousarray(np.asarray(a, dtype=np.float32))
    x_prompt, x_sample = f(x_prompt), f(x_sample)
    rb = f(rel_bias)[0]
    kk = np.arange(128)[:, None]
    qq = np.arange(128)[None, :]
    idx_prev = np.clip(qq + 128 - kk, -128, 128) + 128
    idx_own = np.clip(qq - kk, -128, 128) + 128
    btab = np.stack([rb[:, idx_prev].transpose(1, 0, 2), rb[:, idx_own].transpose(1, 0, 2)])
    cvec = np.broadcast_to(rb[:, 256][None, :], (128, 8))
    shared = {
        "w_ada": f(w_ada)[0], "b_ada": f(b_ada)[0].reshape(48, 128),
        "gvec": np.concatenate([f(g_pre_mix)[0], f(g_post_mix)[0], f(g_pre_ffn)[0], f(g_post_ffn)[0]]).reshape(32, 128),
        "w_in": f(w_in)[0], "w_gk2": f(w_gk2)[0], "b_gk": f(b_gk)[0].reshape(1, 256),
        "btab": f(btab), "cvec": f(cvec), "ggla": f(np.broadcast_to(np.tile(f(g_gla)[0], 4)[None, :], (128, 512))),
        "w_br_a": f(w_br_a)[0], "w_br_b": f(w_br_b)[0], "w_out": f(w_out)[0], "w_up": f(w_up)[0],
        "w_dw": f(w_dw)[0].reshape(3 * 44, 128), "b_dw": f(b_dw)[0].reshape(44, 128), "w_down": f(w_down)[0],
    }
    in_maps = []
    for c in range(8):
        b, hf = c // 2, c % 2
        m = dict(shared)
        if hf == 1:
            m["xpre"] = x_prompt[b, 0:NPRE * 128]
            m["xov"] = x_prompt[b, NPRE * 128:2048]
        else:
            m["xpre"] = np.zeros((NPRE * 128, D), np.float32)
            m["xov"] = np.zeros((128, D), np.float32)
        m["xmain"] = x_prompt[b, hf * 2048:(hf + 1) * 2048]
        m["xsam"] = x_sample[4 * c:4 * c + 4].reshape(128, D)
        m["crow"] = f(np.concatenate([f(c_prompt)[b:b + 1], f(c_sample)[4 * c:4 * c + 4]], 0).reshape(40, 128))
        m["flag"] = np.full((128, 1), float(hf), np.float32)
        m["ck"] = f(cache_k_a)[0, 4 * c:4 * c + 4].reshape(4, 512, 512)
        m["cv"] = f(cache_v_a)[0, 4 * c:4 * c + 4].reshape(4, 512, 512)
        m["sgla"] = f(state_gla)[0, 4 * c:4 * c + 4]
        m["sconv"] = f(state_conv)[0, 4 * c:4 * c + 4].reshape(4, 88, 128)
        in_maps.append({k: np.ascontiguousarray(v) for k, v in m.items()})
    nc = _program()
    cores = [int(t) for t in os.environ.get("KCORES", "0,1,2,3,4,5,6,7").split(",")]
    res = run_bass_kernel_spmd(nc, [in_maps[c] for c in cores], core_ids=list(range(len(cores))))
    R = {c: res.results[i] for i, c in enumerate(cores)}
    y_prompt = np.zeros((4, 4096, D), np.float32)
    y_sample = np.zeros((32, 32, D), np.float32)
    k_p = np.zeros((1, 4, 512, 8, 64), np.float32); v_p = np.zeros_like(k_p)
    gla_p = np.zeros((1, 4, 4, 64, 128), np.float32)
    conv_p = np.zeros((1, 4, 2, 2 * DFF), np.float32)
    k_s = np.zeros((1, 32, 32, 8, 64), np.float32); v_s = np.zeros_like(k_s)
    gla_s = np.zeros((1, 32, 4, 64, 128), np.float32)
    conv_s = np.zeros((1, 32, 2, 2 * DFF), np.float32)
    for c in cores:
        b, hf = c // 2, c % 2
        r = R[c]
        y_prompt[b, hf * 2048:(hf + 1) * 2048] = r["ymain"]
        y_sample[4 * c:4 * c + 4] = r["ysam"].reshape(4, 32, D)
        if hf == 1:
            k_p[0, b] = r["kp"].reshape(512, 8, 64)
            v_p[0, b] = r["vp"].reshape(512, 8, 64)
            gla_p[0, b] = r["glap"]
            conv_p[0, b] = r["convp"].reshape(2, 2 * DFF)
        k_s[0, 4 * c:4 * c + 4] = r["ks"].reshape(4, 32, 8, 64)
        v_s[0, 4 * c:4 * c + 4] = r["vs"].reshape(4, 32, 8, 64)
        gla_s[0, 4 * c:4 * c + 4] = r["glas"]
        conv_s[0, 4 * c:4 * c + 4] = r["convs"].reshape(4, 2, 2 * DFF)
    return (y_prompt, y_sample, k_p, v_p, gla_p, conv_p, k_s, v_s, gla_s, conv_s)
```

```python
from contextlib import ExitStack
import os

import numpy as np
import concourse.bass as bass
import concourse.mybir as mybir
from concourse.bass_utils import run_bass_kernel_spmd

F32 = mybir.dt.float32
BF16 = mybir.dt.bfloat16
AF = mybir.ActivationFunctionType
ALU = mybir.AluOpType

D = 1024
KC = 8
DFF = 2816
NFF = 22
DIN = 5136
G_QA, G_KA, G_VA, G_QKB, G_VB, G_GB, G_GA, G_GBR = 0, 1, 2, 3, 4, 5, 6, 8
EPS = 1e-6
NEG = -30000.0
NPRE = 15
NMAIN = 16
NB = 4


class Op:
    __slots__ = ("eng", "fn", "reads", "writes", "dsem", "signal", "sigval", "deps")

    def __init__(self, eng, fn, reads, writes, dsem):
        self.eng, self.fn, self.reads, self.writes, self.dsem = eng, fn, reads, writes, dsem
        self.signal = False
        self.sigval = 0
        self.deps = ()


class Sched:
    DMA = ("sp", "pool_dma")

    def __init__(self, nc, stack):
        self.nc = nc
        self.stack = stack
        self.ops = []
        self.eng_obj = {"pe": nc.tensor, "act": nc.scalar, "dve": nc.vector, "pool": nc.gpsimd,
                        "sp": nc.sync, "pool_dma": nc.gpsimd}
        self.wuses = []
        self.wdepth = 2

    def add(self, eng, fn, reads=(), writes=(), dsem=None):
        op = Op(eng, fn, tuple(reads), tuple(writes), dsem)
        self.ops.append(op)
        return op

    def queue_of(self, op):
        return "pool" if op.eng == "pool_dma" else op.eng

    def finalize(self):
        nc = self.nc
        inserts = {}
        lastrd = {}
        for idx, op in enumerate(self.ops):
            for k in op.reads:
                if k and k[0] == "wuse":
                    lastrd[k[1]] = idx
        prev = 0
        for i, (pos, op) in enumerate(self.wuses):
            tgt = self.wuses[max(0, i - self.wdepth)][0]
            if i >= 3 and (i - 3) in lastrd:
                tgt = max(tgt, lastrd[i - 3] + 1)
            tgt = max(tgt, prev)
            prev = tgt
            assert tgt <= pos, (i, tgt, pos)
            inserts.setdefault(tgt, []).append(op)
        ops = []
        for i, op in enumerate(self.ops):
            if i in inserts:
                ops.extend(inserts[i])
            ops.append(op)
        self.ops = ops
        last_w = {}
        readers = {}
        for i, op in enumerate(ops):
            deps = set()
            q = self.queue_of(op)
            isdma = op.dsem is not None
            for k in op.reads:
                j = last_w.get(k)
                if j is not None:
                    deps.add(j)
            for k in op.writes:
                j = last_w.get(k)
                if j is not None:
                    oj = ops[j]
                    if isdma or oj.dsem is not None or self.queue_of(oj) != q or q != "pe":
                        deps.add(j)
                for j in readers.get(k, ()):
                    oj = ops[j]
                    if isdma or oj.dsem is not None or self.queue_of(oj) != q:
                        deps.add(j)
            deps.discard(i)
            op.deps = tuple(sorted(deps))
            for j in op.deps:
                ops[j].signal = True
            for k in op.reads:
                readers.setdefault(k, []).append(i)
            for k in op.writes:
                last_w[k] = i
                readers[k] = []
        esem = {}
        for e in ("pe", "act", "dve", "pool"):
            esem[e] = self.stack.enter_context(nc.semaphore("sem_" + e))
        dsems = {}
        cnt = {}
        for op in ops:
            if op.dsem is not None:
                if op.dsem not in dsems:
                    dsems[op.dsem] = self.stack.enter_context(nc.semaphore("dsem_%d" % len(dsems)))
                    cnt[op.dsem] = 0
                cnt[op.dsem] += 1
                op.sigval = 16 * cnt[op.dsem]
            elif op.signal:
                q = self.queue_of(op)
                cnt[q] = cnt.get(q, 0) + 1
                op.sigval = cnt[q]
        known = {q: {} for q in ("pe", "act", "dve", "pool", "sp")}
        for op in ops:
            q = self.queue_of(op)
            eng = self.eng_obj[op.eng]
            need = {}
            for j in op.deps:
                oj = ops[j]
                s = dsems[oj.dsem] if oj.dsem is not None else esem[self.queue_of(oj)]
                key = id(s)
                if key not in need or need[key][1] < oj.sigval:
                    need[key] = (s, oj.sigval)
            for key, (s, v) in need.items():
                if known[q].get(key, 0) >= v:
                    continue
                eng.wait_ge(s, v)
                known[q][key] = v
            ins = op.fn(eng)
            if op.dsem is not None:
                ins.then_inc(dsems[op.dsem], 16)
            elif op.signal:
                ins.then_inc(esem[q], 1)
        self.counts = dict((str(k), v) for k, v in cnt.items())
        for k, s in dsems.items():
            nc.sync.wait_ge(s, 16 * cnt[k])
        return len(ops)


class Builder:
    def __init__(self):
        self.stack = ExitStack()
        self.nc = bass.Bass("TRN2", target_bir_lowering=False)
        self.S = Sched(self.nc, self.stack)
        self.nbank = 0
        self.wuse_n = 0
        self.marks = []
        self.uid = 0

    def din(self, name, shape, dt=F32):
        return self.nc.dram_tensor(name, list(shape), dt, kind="ExternalInput").ap()

    def dout(self, name, shape):
        return self.nc.dram_tensor(name, list(shape), F32, kind="ExternalOutput").ap()

    def dscr(self, name, shape, dt=BF16):
        return self.nc.dram_tensor(name, list(shape), dt, kind="Internal").ap()

    def sb(self, name, shape, dt=F32):
        return self.stack.enter_context(self.nc.sbuf_tensor(name, list(shape), dt))

    def ps(self, name, shape, dt=F32):
        return self.stack.enter_context(self.nc.psum_tensor(name, list(shape), dt))

    def bank(self):
        i = self.nbank % len(self.banks)
        self.nbank += 1
        return self.banks[i], ("ps", i)

    def key(self, name):
        self.uid += 1
        return (name, self.uid)

    def mm(self, out, lhsT, rhs, start, stop, reads, writes):
        self.S.add("pe", lambda e: e.matmul(out, lhsT, rhs, start=start, stop=stop), reads, writes)

    def tr(self, out, in_, ident, reads, writes):
        self.S.add("pe", lambda e: e.transpose(out, in_, ident), reads, writes)

    def act(self, out, in_, func, reads, writes, bias=None, scale=None, accum_out=None):
        kw = {}
        if bias is not None:
            kw["bias"] = bias
        if scale is not None:
            kw["scale"] = scale
        if accum_out is not None:
            kw["accum_out"] = accum_out
        self.S.add("act", lambda e: e.activation(out, in_, func, **kw), reads, writes)

    def tt(self, eng, out, in0, in1, op, reads, writes):
        self.S.add(eng, lambda e: e.tensor_tensor(out, in0, in1, op), reads, writes)

    def ts(self, eng, out, in0, s1, s2, op0, op1, reads, writes):
        if s2 is None:
            self.S.add(eng, lambda e: e.tensor_scalar(out, in0, s1, None, op0), reads, writes)
        else:
            self.S.add(eng, lambda e: e.tensor_scalar(out, in0, s1, s2, op0, op1), reads, writes)

    def stt(self, eng, out, in0, scalar, in1, op0, op1, reads, writes):
        self.S.add(eng, lambda e: e.scalar_tensor_tensor(out, in0, scalar, in1, op0=op0, op1=op1), reads, writes)

    def cp(self, eng, out, in_, reads, writes):
        if eng == "act":
            self.S.add("act", lambda e: e.copy(out, in_), reads, writes)
        else:
            self.S.add(eng, lambda e: e.tensor_copy(out, in_), reads, writes)

    def memset(self, eng, ap, val, writes):
        self.S.add(eng, lambda e: e.memset(ap, val), (), writes)

    def dma(self, q, out, in_, reads, writes, dsem, slow=False):
        if slow:
            self.S.add(q, lambda e: e.dma_start(out=out, in_=in_, allow_slow_non_contiguous=True), reads, writes, dsem)
        else:
            self.S.add(q, lambda e: e.dma_start(out=out, in_=in_), reads, writes, dsem)

    def wload(self, scr_g, wkeys, kcn, width):
        i = self.wuse_n
        self.wuse_n += 1
        slot = i % 3
        buf = self.wbufs[slot]
        key = ("wuse", i)
        dst = buf[:, 0:kcn * width].rearrange("p (kc n) -> p kc n", kc=kcn)
        op = Op("sp", lambda e: e.dma_start(out=dst, in_=scr_g), tuple(wkeys),
                (key, ("wbuf", slot)) + ((("wuse", i - 3),) if i >= 3 else ()), ("wstream", slot))
        self.S.wuses.append((len(self.S.ops), op))
        return (lambda kc, a=0, b=width: buf[:, kc * width + a: kc * width + b]), key

    def build(self):
        nc = self.nc
        S = self.S
        sb, ps = self.sb, self.ps
        NBT = NB * 128
        K = lambda *a: tuple(a)
        xpre = self.din("xpre", [NPRE * 128, D])
        xov = self.din("xov", [128, D])
        xmain = self.din("xmain", [NMAIN * 128, D])
        xsam = self.din("xsam", [128, D])
        crow = self.din("crow", [40, 128])
        flag_d = self.din("flag", [128, 1])
        ck = self.din("ck", [4, 512, 512])
        cv = self.din("cv", [4, 512, 512])
        sgla = self.din("sgla", [4, 4, 64, 128])
        sconv = self.din("sconv", [4, 88, 128])
        w_ada = self.din("w_ada", [D, 6 * D])
        b_ada = self.din("b_ada", [48, 128])
        gvec = self.din("gvec", [32, 128])
        w_in = self.din("w_in", [D, DIN])
        w_gk2 = self.din("w_gk2", [16, 256])
        b_gk = self.din("b_gk", [1, 256])
        btab = self.din("btab", [2, 128, 8, 128])
        cvec = self.din("cvec", [128, 8])
        ggla = self.din("ggla", [128, 512])
        w_br_a = self.din("w_br_a", [512, D])
        w_br_b = self.din("w_br_b", [512, D])
        w_out = self.din("w_out", [D, D])
        w_up = self.din("w_up", [D, 2 * DFF])
        w_dw = self.din("w_dw", [3 * 44, 128])
        b_dw = self.din("b_dw", [44, 128])
        w_down = self.din("w_down", [DFF, D])

        ymain = self.dout("ymain", [NMAIN * 128, D])
        ysam = self.dout("ysam", [128, D])
        yov = self.dout("yov", [128, D])
        kp_o = self.dout("kp", [512, 512])
        vp_o = self.dout("vp", [512, 512])
        glap_o = self.dout("glap", [4, 64, 128])
        convp_o = self.dout("convp", [88, 128])
        ks_o = self.dout("ks", [128, 512])
        vs_o = self.dout("vs", [128, 512])
        glas_o = self.dout("glas", [4, 4, 64, 128])
        convs_o = self.dout("convs", [4, 88, 128])

        win_g = self.dscr("win_g", [10, 128, 8, 512])
        wgk_g = self.dscr("wgk_g", [128, 8, 16])
        wbra_g = self.dscr("wbra_g", [2, 128, 4, 512])
        wbrb_g = self.dscr("wbrb_g", [2, 128, 4, 512])
        wout_g = self.dscr("wout_g", [2, 128, 8, 512])
        wup_g = self.dscr("wup_g", [11, 128, 8, 512])
        wdown_g = self.dscr("wdown_g", [8, 128, NFF, 128])

        self.banks = [ps("bank%d" % i, [128, 512]) for i in range(5)]
        obank = [ps("obank%d" % i, [128, 512]) for i in range(2)]
        pbf = ps("pbf", [128, 1024], BF16)

        self.wbufs = [sb("wbuf%d" % i, [128, 4096], BF16) for i in range(3)]
        ident_f = sb("ident_f", [128, 128])
        ident_b = sb("ident_b", [128, 128], BF16)
        ones_b = sb("ones_b", [128, 128], BF16)
        triu_f = sb("triu_f", [128, 128])
        trisl_f = sb("trisl_f", [128, 128])
        ones_f = sb("ones_f", [128, 8])
        epsb = sb("epsb", [128, 1])
        mod = sb("mod", [128, 6, KC, 5])
        Am = sb("Am", [128, KC, 5]); Bm = sb("Bm", [128, KC, 5]); Gm = sb("Gm", [128, KC, 5])
        Af = sb("Af", [128, KC, 5]); Bf = sb("Bf", [128, KC, 5]); Gf = sb("Gf", [128, KC, 5])
        AmP = sb("AmP", [128, KC]); BmP = sb("BmP", [128, KC])
        AfP = sb("AfP", [128, KC]); BfP = sb("BfP", [128, KC])
        gT = sb("gT", [128, 32])
        badaT = sb("badaT", [128, 48])
        cT = sb("cT", [128, 40])
        siluT = sb("siluT", [128, 40], BF16)
        flag = sb("flag_s", [128, 1])
        wdwT = sb("wdwT", [128, 3, 44])
        bdwT = sb("bdwT", [128, 44])
        wgk_f = sb("wgk_f", [17, 256])
        wgk_b = sb("wgk_b", [17, 256], BF16)
        tab_prev = sb("tab_prev", [128, 8, 128], BF16)
        tab_own = sb("tab_own", [128, 8, 128], BF16)
        tab_mask = sb("tab_mask", [128, 128], BF16)
        cvec_s = sb("cvec_s", [128, 8])
        ggla_s = sb("ggla_s", [128, 512])
        stage = sb("stage", [128, 128])
        hist = sb("hist", [128, 44, 2])
        hist_s = sb("hist_s", [128, 4, 44, 2])
        h88 = sb("h88", [128, 2, 44])
        xin = [sb("xin%d" % i, [128, D]) for i in range(2)]
        xT = sb("xT", [128, KC, NBT])
        hT = sb("hT", [128, KC, NBT], BF16)
        sq = sb("sq", [128, 2, NBT], BF16)
        rbc = sb("rbc", [128, NBT])
        tmpf = sb("tmpf", [128, NBT])
        tmpf2 = sb("tmpf2", [128, NBT])
        qaT = sb("qaT", [64, 8, NBT], BF16)
        kaT = [sb("kaT%d" % i, [64, 8, 128], BF16) for i in range(8)]
        vaug = [sb("vaug%d" % i, [128, 8, 65], BF16) for i in range(8)]
        qbT = sb("qbT", [64, 4, NBT], BF16)
        kbT = sb("kbT", [64, 4, NBT], BF16)
        gkT = sb("gkT", [32, NBT], BF16)
        pT = sb("pT", [128, 5, 8, 128], BF16)
        ya_tok = sb("ya_tok", [128, 512], BF16)
        rden = sb("rden", [128, 8])
        kb_tok = sb("kb_tok", [128, 256])
        vb_tok = sb("vb_tok", [128, 4, 128], BF16)
        gb2 = sb("gb2", [128, 512])
        gtanh = sb("gtanh", [128, 512])
        Lsp = sb("Lsp", [128, 256])
        e_sb = sb("e_sb", [128, 256])
        e1 = sb("e1", [64, 4, 128]); e2 = sb("e2", [64, 4, 128])
        qtT = sb("qtT", [64, 4, 128], BF16); ktT = sb("ktT", [64, 4, 128], BF16)
        kend = sb("kend", [128, 256], BF16)
        dec = sb("dec", [64, 4])
        attT = sb("attT", [128, 4, 128], BF16)
        Sst = sb("Sst", [64, 4, 128])
        Sbf = sb("Sbf", [64, 4, 128], BF16)
        ssq = sb("ssq", [128, 4]); rgl = sb("rgl", [128, 4])
        yb_tok = sb("yb_tok", [128, 512], BF16)
        sgA = sb("sgA", [128, NBT]); sgB = sb("sgB", [128, NBT])
        m2 = sb("m2", [128, KC, NBT])
        ua = [sb("ua%d" % i, [128, NBT + 8]) for i in range(2)]
        y0 = [sb("y0_%d" % i, [128, NBT]) for i in range(2)]
        actT = sb("actT", [128, NFF, NBT], BF16)
        mergedT = lambda c: actT[:, c, :]
        MK = lambda c: K("actT", c)
        yaT = lambda c: actT[:, 8 + c, :]
        YAK = tuple(K("actT", 8 + c) for c in range(4))
        ybT = lambda c: actT[:, 12 + c, :]
        YBK = tuple(K("actT", 12 + c) for c in range(4))
        m2b = m2.bitcast(BF16)
        kcT = lambda h: m2b[0:64, h, 0:512]
        vcaug = sb("vcaug", [128, 4, 8, 65], BF16)
        vown = sb("vown", [32, 8, 65], BF16)
        Ss, Ssb = Sst, Sbf

        def const_mask(t, keyname, pattern, cm, base, cmp_op):
            self.memset("pool", t[:], 1.0, (K(keyname),))
            S.add("pool", lambda e: e.affine_select(t[:], t[:], pattern=pattern, compare_op=cmp_op, fill=0.0,
                                                    base=base, channel_multiplier=cm), (K(keyname),), (K(keyname),))
        self.memset("pool", ident_f[:], 0.0, (K("ident_f"),))
        S.add("pool", lambda e: e.affine_select(ident_f[:], ident_f[:], pattern=[[-1, 128]], compare_op=ALU.not_equal,
                                                fill=1.0, base=0, channel_multiplier=1), (K("ident_f"),), (K("ident_f"),))
        const_mask(triu_f, "triu_f", [[1, 128]], -1, 0, ALU.is_ge)
        const_mask(trisl_f, "trisl_f", [[-1, 128]], 1, -1, ALU.is_ge)
        self.cp("dve", ident_b[:], ident_f[:], (K("ident_f"),), (K("ident_b"),))
        self.memset("dve", ones_b[:], 1.0, (K("ones_b"),))
        self.memset("dve", ones_f[:], 1.0, (K("ones_f"),))
        self.memset("dve", epsb[:], EPS, (K("epsb"),))
        self.memset("dve", gkT[:], 1.0, (K("gkT_ones"),))
        self.memset("dve", hist[:], 0.0, tuple(K("hist", jj) for jj in range(44)))
        self.memset("dve", Sst[:], 0.0, tuple(K("Sst", h) for h in range(4)))
        self.memset("dve", Sbf[:], 0.0, (K("Sst", "b"),))

        def load_T(src2d, rows, dst, kname, view=None):
            self.dma("pool_dma", stage[0:rows, :], src2d, (), (K("stage"),), ("stage",))
            bk, bkey = self.bank()
            self.tr(bk[:, 0:rows], stage[0:rows, :], ident_f[0:rows, 0:rows], (K("stage"), K("ident_f")), (bkey,))
            src = bk[:, 0:rows] if view is None else view(bk[:, 0:rows])
            self.cp("dve", dst, src, (bkey,), (K(kname),))

        load_T(gvec, 32, gT[:], "gT")
        load_T(b_ada, 48, badaT[:], "badaT")
        load_T(crow, 40, cT[:], "cT")
        for j in range(3):
            load_T(w_dw[j * 44:(j + 1) * 44, :], 44, wdwT[:, j, :], "wdwT%d" % j)
        load_T(b_dw, 44, bdwT[:], "bdwT")
        self.dma("pool_dma", flag[:], flag_d, (), (K("flag"),), ("misc", 0))
        self.dma("pool_dma", wgk_f[0:16, :], w_gk2, (), (K("wgk_f0"),), ("misc", 1))
        self.dma("pool_dma", wgk_f[16:17, :], b_gk, (), (K("wgk_f1"),), ("misc", 2))
        self.cp("dve", wgk_b[:], wgk_f[:], (K("wgk_f0"), K("wgk_f1")), (K("wgk_b"),))
        self.dma("pool_dma", cvec_s[:], cvec, (), (K("cvec"),), ("misc", 3))
        self.dma("pool_dma", ggla_s[:], ggla, (), (K("ggla"),), ("misc", 4))
        self.ts("dve", ggla_s[:], ggla_s[:], 0.5, None, ALU.mult, None, (K("ggla"),), (K("ggla"),))
        tabf = xin[0][:, :].rearrange("p (h q) -> p h q", h=8)
        for ti, tdst in ((0, tab_prev), (1, tab_own)):
            self.dma("pool_dma", tabf, btab[ti], (), (K("xin", 0), K("xinv", 0)), ("misc", 5))
            for h in range(8):
                self.ts("dve", tdst[:, h, :], tabf[:, h, :], cvec_s[:, h:h + 1], None, ALU.subtract, None,
                        (K("xin", 0), K("xinv", 0), K("cvec")), (K("tab", ti, h),))
        self.memset("dve", tab_own[64:128, :, 0:64], NEG, tuple(K("tab", 1, h) for h in range(8)))
        self.memset("dve", tab_mask[:], 0.0, (K("tab_mask"),))
        self.memset("dve", tab_mask[0:64, 64:128], NEG, (K("tab_mask"),))
        tabkeys = tuple(K("tab", ti, h) for ti in range(2) for h in range(8)) + (K("tab_mask"),)

        def cast_groups(src, dst, kcn, ngroup, width, name):
            keys = []
            for kc in range(kcn):
                self.dma("pool_dma", dst[:, :, kc, :].rearrange("g p n -> p g n"),
                         src[kc * 128:(kc + 1) * 128, 0:ngroup * width].rearrange("p (g n) -> p g n", g=ngroup),
                         (), (K("scr", name, kc),), ("cast", name))
                keys.append(K("scr", name, kc))
            return tuple(keys)

        kwin = cast_groups(w_in, win_g, 8, 10, 512, "win")
        self.dma("pool_dma", wgk_g, w_in.rearrange("(kc p) n -> p kc n", p=128)[:, :, 5120:5136], (), (K("scr", "wgk"),), ("cast", "wgk"))
        kwgk = (K("scr", "wgk"),)

        self.act(tmpf[:, 0:40], cT[:], AF.Tanh, (K("cT"),), (K("tmpf"),), scale=0.5)
        self.ts("dve", tmpf[:, 0:40], tmpf[:, 0:40], 0.5, 0.5, ALU.mult, ALU.add, (K("tmpf"),), (K("tmpf"),))
        self.tt("dve", siluT[:], tmpf[:, 0:40], cT[:], ALU.mult, (K("tmpf"), K("cT")), (K("siluT"),))
        siluv = siluT[:].rearrange("p (s k) -> p k s", k=8)
        for g in range(12):
            slot = g % 3
            buf = self.wbufs[slot]
            self.dma("pool_dma", buf[:].rearrange("p (kc n) -> p kc n", kc=8),
                     w_ada.rearrange("(kc p) n -> p kc n", p=128)[:, :, g * 512:(g + 1) * 512],
                     (), (K("wbuf", slot),), ("ada", slot))
            bk, bkey = self.bank()
            for oc in range(4):
                for kc in range(8):
                    self.mm(bk[:, oc * 8:oc * 8 + 5], buf[:, kc * 512 + oc * 128: kc * 512 + (oc + 1) * 128], siluv[:, kc, :],
                            kc == 0, kc == 7, (K("wbuf", slot), K("siluT")), (bkey,))
            for oc in range(4):
                ch = g * 4 + oc
                self.ts("dve", mod[:, ch // 8, ch % 8, :], bk[:, oc * 8:oc * 8 + 5], badaT[:, ch:ch + 1], None, ALU.add, None,
                        (bkey, K("badaT")), (K("mod", ch),))
        modk = tuple(K("mod", ch) for ch in range(48))
        for kc in range(8):
            rw = modk + (K("gT"),)
            self.ts("dve", Am[:, kc, :], mod[:, 1, kc, :], 1.0, gT[:, kc:kc + 1], ALU.add, ALU.mult, rw, (K("Am", kc),))
            self.ts("dve", Af[:, kc, :], mod[:, 4, kc, :], 1.0, gT[:, 16 + kc:17 + kc], ALU.add, ALU.mult, rw, (K("Af", kc),))
            self.ts("dve", Gm[:, kc, :], mod[:, 2, kc, :], gT[:, 8 + kc:9 + kc], None, ALU.mult, None, rw, (K("Gm", kc),))
            self.ts("dve", Gf[:, kc, :], mod[:, 5, kc, :], gT[:, 24 + kc:25 + kc], None, ALU.mult, None, rw, (K("Gf", kc),))
        self.cp("dve", Bm[:], mod[:, 0, :, :], modk, (K("Bm"),))
        self.cp("dve", Bf[:], mod[:, 3, :, :], modk, (K("Bf"),))
        k8 = lambda n: tuple(K(n, kc) for kc in range(8))
        self.ts("dve", AmP[:], Am[:, :, 0], flag[:, 0:1], None, ALU.mult, None, k8("Am") + (K("flag"),), (K("AmP"),))
        self.ts("dve", AfP[:], Af[:, :, 0], flag[:, 0:1], None, ALU.mult, None, k8("Af") + (K("flag"),), (K("AfP"),))
        self.ts("dve", BmP[:], Bm[:, :, 0], flag[:, 0:1], None, ALU.mult, None, (K("Bm"), K("flag")), (K("BmP"),))
        self.ts("dve", BfP[:], Bf[:, :, 0], flag[:, 0:1], None, ALU.mult, None, (K("Bf"), K("flag")), (K("BfP"),))
        modkeys = {"Am": k8("Am"), "Af": k8("Af"), "Gm": k8("Gm"), "Gf": k8("Gf"), "Bm": (K("Bm"),), "Bf": (K("Bf"),),
                   "AmP": (K("AmP"),), "AfP": (K("AfP"),), "BmP": (K("BmP"),), "BfP": (K("BfP"),)}

        kwbra = cast_groups(w_br_a, wbra_g, 4, 2, 512, "wbra")
        kwbrb = cast_groups(w_br_b, wbrb_g, 4, 2, 512, "wbrb")
        kwout = cast_groups(w_out, wout_g, 8, 2, 512, "wout")
        kwup = cast_groups(w_up, wup_g, 8, 11, 512, "wup")
        kwdown = cast_groups(w_down, wdown_g, NFF, 8, 128, "wdown")

        def load_xT(src_rows, ntile):
            for t in range(ntile):
                xb = xin[t % 2]
                kx = K("xin", t % 2)
                self.dma("sp", xb[:], src_rows[t * 128:(t + 1) * 128, :], (), (kx, K("xinv", t % 2)), ("xin", t % 2))
                for half in range(2):
                    bk, bkey = self.bank()
                    for j in range(4):
                        kc = half * 4 + j
                        self.tr(bk[:, j * 128:(j + 1) * 128], xb[:, kc * 128:(kc + 1) * 128], ident_f[:],
                                (kx if half == 0 else K("xinv", t % 2), K("ident_f")), (bkey,))
                    self.cp("act", xT[:, half * 4:half * 4 + 4, t * 128:(t + 1) * 128],
                            bk[:].rearrange("p (j n) -> p j n", j=4), (bkey,), tuple(K("xT", half * 4 + j, t) for j in range(4)))

        def rms_bcast(srcT, n, src_keys_fn):
            bk, bkey = self.bank()
            for kc in range(8):
                self.act(sq[:, kc % 2, 0:n], srcT[:, kc, 0:n], AF.Square, src_keys_fn(kc), (K("sq", kc % 2),))
                self.mm(bk[:, 0:n], ones_b[:], sq[:, kc % 2, 0:n], kc == 0, kc == 7, (K("ones_b"), K("sq", kc % 2)), (bkey,))
            self.act(tmpf[:, 0:n], bk[:, 0:n], AF.Ln, (bkey, K("epsb")), (K("tmpf"),), bias=epsb[:, 0:1], scale=1.0 / D)
            self.act(rbc[:, 0:n], tmpf[:, 0:n], AF.Exp, (K("tmpf"),), (K("rbc"),), scale=-0.5)

        def modulate(n, ntile, Asc, Bsc, Afn, Bfn):
            for kc in range(8):
                xk = tuple(K("xT", kc, t) for t in range(ntile))
                self.tt("dve", tmpf2[:, 0:n], xT[:, kc, 0:n], rbc[:, 0:n], ALU.mult, xk + (K("rbc"),), (K("tmpf2"),))
                self.act(hT[:, kc, 0:n], tmpf2[:, 0:n], AF.Identity, (K("tmpf2"),) + modkeys[Asc] + modkeys[Bsc], (K("hT", kc),),
                         bias=Bfn(kc), scale=Afn(kc))

        def modulate_seq(At, Bt, Asc, Bsc):
            for kc in range(8):
                xk = (K("xT", kc, 0),)
                v3 = lambda ap: ap.rearrange("p (s w) -> p s w", s=4)
                self.tt("dve", tmpf2[:, 0:128], xT[:, kc, 0:128], rbc[:, 0:128], ALU.mult, xk + (K("rbc"),), (K("tmpf2"),))
                self.tt("dve", v3(tmpf2[:, 0:128]), v3(tmpf2[:, 0:128]), At[:, kc, 1:5].to_broadcast([128, 4, 32]), ALU.mult,
                        (K("tmpf2"),) + modkeys[Asc], (K("tmpf2"),))
                self.tt("dve", v3(hT[:, kc, 0:128]), v3(tmpf2[:, 0:128]), Bt[:, kc, 1:5].to_broadcast([128, 4, 32]), ALU.add,
                        (K("tmpf2"),) + modkeys[Bsc], (K("hT", kc),))

        xkeys = lambda ntile: (lambda kc: tuple(K("xT", kc, t) for t in range(ntile)))

        def heads_fm(wg, wk, col0, nheads, n, evac):
            for h in range(nheads):
                bk, bkey = self.bank()
                for kc in range(8):
                    self.mm(bk[0:64, 0:n], wg(kc, col0 + h * 64, col0 + (h + 1) * 64), hT[:, kc, 0:n], kc == 0, kc == 7, (wk, K("hT", kc)), (bkey,))
                evac(h, bk, bkey)

        def units_tm(wg, wk, col0, ncols, units, evac):
            for ui, (c0, C) in enumerate(units):
                bk, bkey = self.bank()
                for kc in range(8):
                    self.mm(bk[0:C, 0:ncols], hT[:, kc, c0:c0 + C], wg(kc, col0, col0 + ncols), kc == 0, kc == 7, (wk, K("hT", kc)), (bkey,))
                evac(ui, bk, bkey)

        def gla_block(C, col0, S_t, S_b, Skey, out_cb, prefix_only):
            bk, bkey = self.bank()
            self.mm(bk[0:C, 0:256], gkT[0:17, col0:col0 + C], wgk_b[0:17, :], True, True, (K("gkT"), K("gkT_ones"), K("wgk_b")), (bkey,))
            self.act(e_sb[0:C, :], bk[0:C, 0:256], AF.Exp, (bkey,), (K("e_sb"),), scale=-1.0)
            self.act(Lsp[0:C, :], e_sb[0:C, :], AF.Ln, (K("e_sb"),), (K("Lsp"),), bias=1.0)
            bk2, bkey2 = self.bank()
            self.mm(bk2[0:C, 0:256], trisl_f[0:C, 0:C], Lsp[0:C, :], True, True, (K("trisl_f"), K("Lsp")), (bkey2,))
            self.act(e_sb[0:C, :], bk2[0:C, 0:256], AF.Exp, (bkey2,), (K("e_sb"),), scale=-1.0 / 16)
            self.tt("dve", kend[0:C, :], kb_tok[0:C, :], e_sb[0:C, :], ALU.mult, (K("kb_tok"), K("e_sb")), (K("kend"),))
            bk3, bkey3 = self.bank()
            for h in range(4):
                self.mm(bk3[0:64, h * 128:h * 128 + C], Lsp[0:C, h * 64:(h + 1) * 64], triu_f[0:C, 0:C], True, True,
                        (K("Lsp"), K("triu_f")), (bkey3,))
            b3 = bk3[0:64, :].rearrange("p (c t) -> p c t", c=4)
            self.act(dec[:, :], b3[:, :, C - 1], AF.Exp, (bkey3,), (K("dec"),), scale=-1.0 / 16)
            if not prefix_only:
                self.act(e1[:, :, 0:C], b3[:, :, 0:C], AF.Exp, (bkey3,), (K("e1"),), scale=-1.0 / 16)
                self.act(e2[:, :, 0:C], b3[:, :, 0:C], AF.Exp, (bkey3,), (K("e2"),), scale=1.0 / 16)
                self.stt("dve", qtT[:, :, 0:C], qbT[:, :, col0:col0 + C], 0.125, e1[:, :, 0:C], ALU.mult, ALU.mult,
                         tuple(K("qbT", j) for j in range(4)) + (K("e1"),), (K("qtT"),))
                self.tt("dve", ktT[:, :, 0:C], kbT[:, :, col0:col0 + C], e2[:, :, 0:C], ALU.mult, tuple(K("kbT", j) for j in range(4)) + (K("e2"),), (K("ktT"),))
                bk4, bkey4 = self.bank()
                for h in range(4):
                    self.mm(bk4[0:C, h * 128:h * 128 + C], ktT[:, h, 0:C], qtT[:, h, 0:C], True, True,
                            (K("ktT"), K("qtT")), (bkey4,))
                for h in range(4):
                    self.tt("dve", attT[0:C, h, 0:C], bk4[0:C, h * 128:h * 128 + C], triu_f[0:C, 0:C], ALU.mult,
                            (bkey4, K("triu_f")), (K("attT", h),))
                bk5, bkey5 = self.bank()
                for h in range(4):
                    self.mm(bk5[0:C, h * 128:(h + 1) * 128], attT[0:C, h, 0:C], vb_tok[0:C, h, :], True, False, (K("attT", h), K("vb_tok")), (bkey5,))
                    self.mm(bk5[0:C, h * 128:(h + 1) * 128], qtT[:, h, 0:C], S_b[:, h, :], False, True,
                            (K("qtT"), Skey + ("b",)), (bkey5,))
                out_cb(bk5, bkey5)
            bk6, bkey6 = self.bank()
            for h in range(4):
                self.mm(bk6[0:64, h * 128:(h + 1) * 128], kend[0:C, h * 64:(h + 1) * 64], vb_tok[0:C, h, :], True, True, (K("kend"), K("vb_tok")), (bkey6,))
            for h in range(4):
                self.stt("dve", S_t[:, h, :], S_t[:, h, :], dec[:, h:h + 1], bk6[0:64, h * 128:(h + 1) * 128],
                         ALU.mult, ALU.add, (Skey + (h,), K("dec"), bkey6), (Skey + (h,),))
            self.cp("act", S_b[:], S_t[:], tuple(Skey + (h,) for h in range(4)), (Skey + ("b",),))

        def gla_out(C, obk, obkey):
            self.memset("dve", ssq[0:C, :], 0.0, tuple(K("ssq", h) for h in range(4)))
            for h in range(4):
                self.act(attT[0:C, h, :], obk[0:C, h * 128:(h + 1) * 128], AF.Square, (obkey,), (K("attT", h), K("ssq", h)), accum_out=ssq[0:C, h:h + 1])
            self.act(rgl[0:C, :], ssq[0:C, :], AF.Ln, tuple(K("ssq", h) for h in range(4)) + (K("epsb"),), (K("rgl"),), bias=epsb[0:C, 0:1], scale=1.0 / 128)
            self.act(rgl[0:C, :], rgl[0:C, :], AF.Exp, (K("rgl"),), (K("rgl"),), scale=-0.5)
            self.act(gtanh[0:C, :], gb2[0:C, :], AF.Tanh, (K("gb2"),), (K("gtanh"),), scale=0.5)
            self.stt("dve", gtanh[0:C, :], gtanh[0:C, :], 1.0, gb2[0:C, :], ALU.add, ALU.mult, (K("gtanh"), K("gb2")), (K("gtanh"),))
            self.tt("dve", gtanh[0:C, :], gtanh[0:C, :], ggla_s[0:C, :], ALU.mult, (K("gtanh"), K("ggla")), (K("gtanh"),))
            for h in range(4):
                self.stt("dve", yb_tok[0:C, h * 128:(h + 1) * 128], obk[0:C, h * 128:(h + 1) * 128], rgl[0:C, h:h + 1],
                         gtanh[0:C, h * 128:(h + 1) * 128], ALU.mult, ALU.mult, (obkey, K("rgl"), K("gtanh")), (K("yb_tok", h),))

        def gla_units(units, pre, S_of):
            wgk_, wkk = self.wload(win_g[G_QKB], kwin, 8, 512)
            wgv_, wkv = self.wload(win_g[G_VB], kwin, 8, 512)
            if not pre:
                wgg_, wkg = self.wload(win_g[G_GB], kwin, 8, 512)
            for ui, (col0, C) in enumerate(units):
                bk, bkey = self.bank()
                for kc in range(8):
                    self.mm(bk[0:C, 0:256], hT[:, kc, col0:col0 + C], wgk_(kc, 256, 512), kc == 0, kc == 7, (wkk, K("hT", kc)), (bkey,))
                self.cp("act", kb_tok[0:C, :], bk[0:C, 0:256], (bkey,), (K("kb_tok"),))
                bk, bkey = self.bank()
                for kc in range(8):
                    self.mm(bk[0:C, :], hT[:, kc, col0:col0 + C], wgv_(kc), kc == 0, kc == 7, (wkv, K("hT", kc)), (bkey,))
                self.cp("act", vb_tok[0:C, :, :].rearrange("p h e -> p (h e)"), bk[0:C, :], (bkey,), (K("vb_tok"),))
                if not pre:
                    bk, bkey = self.bank()
                    for kc in range(8):
                        self.mm(bk[0:C, :], hT[:, kc, col0:col0 + C], wgg_(kc), kc == 0, kc == 7, (wkg, K("hT", kc)), (bkey,))
                    self.cp("act", gb2[0:C, :], bk[0:C, :], (bkey,), (K("gb2"),))
                S_t, S_b, Skey, before_fn, after_fn = S_of(ui)
                if before_fn is not None:
                    before_fn()

                def out_cb(obk, obkey, col0=col0, C=C, ui=ui):
                    gla_out(C, obk, obkey)
                    for c in range(4):
                        self.tr(pbf[:, 512 + c * 128:512 + c * 128 + C], yb_tok[0:C, c * 128:(c + 1) * 128], ident_b[0:C, 0:C],
                                (K("yb_tok", c), K("ident_b")), (K("pbf2"),))
                    for c in range(4):
                        self.cp("act", ybT(c)[:, col0:col0 + C], pbf[:, 512 + c * 128:512 + c * 128 + C], (K("pbf2"),), (YBK[c],))
                gla_block(C, col0, S_t, S_b, Skey, out_cb, pre)
                if after_fn is not None:
                    after_fn()

        def attn_tile(tglob, tl):
            for kb in range(5):
                slot = (tglob - 4 + kb) % 8
                for hg in range(2):
                    bk, bkey = self.bank()
                    for hh in range(4):
                        h = hg * 4 + hh
                        tab = {0: tab_mask[:, :], 3: tab_prev[:, h, :], 4: tab_own[:, h, :]}.get(kb)
                        self.mm(bk[:, hh * 128:(hh + 1) * 128], kaT[slot][:, h, :], qaT[:, h, tl * 128:(tl + 1) * 128],
                                True, tab is None, (K("kaT", slot, h), K("qaT", h)), (bkey,))
                        if tab is not None:
                            self.mm(bk[:, hh * 128:(hh + 1) * 128], ident_b[:], tab, False, True, (K("ident_b"),) + tabkeys, (bkey,))
                    self.act(pT[:, kb, hg * 4:(hg + 1) * 4, :], bk[:].rearrange("p (h q) -> p h q", h=4), AF.Exp, (bkey,), (K("pT", kb, hg),))
            for hg in range(2):
                ob, obkey = obank[hg], K("obank", hg)
                for hh in range(4):
                    h = hg * 4 + hh
                    for kb in range(5):
                        slot = (tglob - 4 + kb) % 8
                        self.mm(ob[:, hh * 65:(hh + 1) * 65], pT[:, kb, h, :], vaug[slot][:, h, :], kb == 0, kb == 4,
                                (K("pT", kb, hg), K("vaug", slot), K("vaug1", slot)), (obkey,))
            for hg in range(2):
                ob, obkey = obank[hg], K("obank", hg)
                ov = ob[:, 0:260].rearrange("p (h e) -> p h e", h=4)
                self.ts("dve", rden[:, hg * 4:(hg + 1) * 4], ov[:, :, 64], 1e-30, None, ALU.add, None, (obkey,), (K("rden", hg),))
                S.add("dve", lambda e, hg=hg: e.reciprocal(rden[:, hg * 4:(hg + 1) * 4], rden[:, hg * 4:(hg + 1) * 4]), (K("rden", hg),), (K("rden", hg),))
                for hh in range(4):
                    h = hg * 4 + hh
                    self.ts("dve", ya_tok[:, h * 64:(h + 1) * 64], ov[:, hh, 0:64], rden[:, h:h + 1], None, ALU.mult, None,
                            (obkey, K("rden", hg)), (K("ya_tok", h),))
            yk = tuple(K("ya_tok", h) for h in range(8))
            for c in range(4):
                self.tr(pbf[:, c * 128:(c + 1) * 128], ya_tok[:, c * 128:(c + 1) * 128], ident_b[:], yk + (K("ident_b"),), (K("pbf1"),))
            for c in range(4):
                self.cp("act", yaT(c)[:, tl * 128:(tl + 1) * 128], pbf[:, c * 128:(c + 1) * 128], (K("pbf1"),), (YAK[c],))

        def attn_sample(s, slot):
            for rb in range(4):
                xb = xin[rb % 2]
                kx = K("xin", rb % 2)
                self.dma("sp", xb[:, 0:512], ck[s, rb * 128:(rb + 1) * 128, :], (), (kx,), ("xin", rb % 2))
                for hg in range(2):
                    bk, bkey = self.bank()
                    for hh in range(4):
                        h = hg * 4 + hh
                        self.tr(bk[0:64, hh * 128:(hh + 1) * 128], xb[:, h * 64:(h + 1) * 64], ident_f[:], (kx, K("ident_f")), (bkey,))
                    for hh in range(4):
                        h = hg * 4 + hh
                        self.cp("act", kcT(h)[:, rb * 128:(rb + 1) * 128], bk[0:64, hh * 128:(hh + 1) * 128], (bkey,), (K("m2", h),))
                self.dma("sp", xb[:, 512:1024], cv[s, rb * 128:(rb + 1) * 128, :], (), (K("xinv", rb % 2),), ("xinv", rb % 2))
                self.cp("dve", vcaug[:, rb, :, 0:64], xb[:, 512:1024].rearrange("p (h e) -> p h e", h=8), (K("xinv", rb % 2),), (K("vcaug", rb),))
                self.cp("dve", vcaug[:, rb, :, 64], ones_f[:, 0:8], (K("ones_f"),), (K("vcaug1", rb),))
            q0 = s * 32
            for kb in range(5):
                kn = 128 if kb < 4 else 32
                bk, bkey = self.bank()
                for h in range(8):
                    if kb < 4:
                        lhs = kcT(h)[:, kb * 128:(kb + 1) * 128]
                        rk = (K("m2", h),)
                    else:
                        lhs = kaT[slot][:, h, q0:q0 + 32]
                        rk = (K("kaT", slot, h),)
                    tab = {3: tab_prev[:, h, 0:32], 4: tab_own[0:32, h, 0:32]}.get(kb)
                    self.mm(bk[0:kn, h * 32:(h + 1) * 32], lhs, qaT[:, h, q0:q0 + 32], True, tab is None, rk + (K("qaT", h),), (bkey,))
                    if tab is not None:
                        self.mm(bk[0:kn, h * 32:(h + 1) * 32], ident_b[0:kn, 0:kn], tab, False, True, (K("ident_b"),) + tabkeys, (bkey,))
                self.act(pT[0:kn, kb, :, 0:32], bk[0:kn, 0:256].rearrange("p (h q) -> p h q", h=8), AF.Exp, (bkey,), (K("pT", kb, 0), K("pT", kb, 1)))
            for h in range(8):
                hg, hh = h // 4, h % 4
                for kb in range(5):
                    kn = 128 if kb < 4 else 32
                    rhs = vcaug[:, kb, h, :] if kb < 4 else vown[0:32, h, :]
                    rk = (K("vcaug", kb), K("vcaug1", kb)) if kb < 4 else (K("vown"), K("vown1"))
                    self.mm(obank[hg][0:32, hh * 65:(hh + 1) * 65], pT[0:kn, kb, h, 0:32], rhs, kb == 0, kb == 4,
                            (K("pT", kb, hg),) + rk, (K("obank", hg),))
            for hg in range(2):
                ob, obkey = obank[hg], K("obank", hg)
                ov = ob[0:32, 0:260].rearrange("p (h e) -> p h e", h=4)
                S.add("dve", lambda e, ov=ov, hg=hg: e.reciprocal(rden[0:32, hg * 4:(hg + 1) * 4], ov[:, :, 64]), (obkey,), (K("rden", hg),))
                for hh in range(4):
                    h = hg * 4 + hh
                    self.ts("dve", ya_tok[0:32, h * 64:(h + 1) * 64], ov[:, hh, 0:64], rden[0:32, h:h + 1], None, ALU.mult, None,
                            (obkey, K("rden", hg)), (K("ya_tok", h),))
            yk = tuple(K("ya_tok", h) for h in range(8))
            for c in range(4):
                self.tr(pbf[:, c * 128:c * 128 + 32], ya_tok[0:32, c * 128:(c + 1) * 128], ident_b[0:32, 0:32], yk + (K("ident_b"),), (K("pbf1"),))
            for c in range(4):
                self.cp("act", yaT(c)[:, q0:q0 + 32], pbf[:, c * 128:c * 128 + 32], (K("pbf1"),), (YAK[c],))

        def resid(n, ntile, Gfn, gname):
            for kc in range(8):
                self.tt("dve", tmpf2[:, 0:n], m2[:, kc, 0:n], rbc[:, 0:n], ALU.mult, (K("m2", kc), K("rbc")), (K("tmpf2"),))
                xk = tuple(K("xT", kc, t) for t in range(ntile))
                self.stt("dve", xT[:, kc, 0:n], tmpf2[:, 0:n], Gfn(kc), xT[:, kc, 0:n], ALU.mult, ALU.add,
                         (K("tmpf2"),) + modkeys[gname] + xk, xk)

        def resid_seq(Gt, gname):
            v3 = lambda ap: ap.rearrange("p (s w) -> p s w", s=4)
            for kc in range(8):
                xk = (K("xT", kc, 0),)
                self.tt("dve", tmpf2[:, 0:128], m2[:, kc, 0:128], rbc[:, 0:128], ALU.mult, (K("m2", kc), K("rbc")), (K("tmpf2"),))
                self.tt("dve", v3(tmpf2[:, 0:128]), v3(tmpf2[:, 0:128]), Gt[:, kc, 1:5].to_broadcast([128, 4, 32]), ALU.mult,
                        (K("tmpf2"),) + modkeys[gname], (K("tmpf2"),))
                self.tt("dve", xT[:, kc, 0:128], xT[:, kc, 0:128], tmpf2[:, 0:128], ALU.add, (K("tmpf2"),) + xk, xk)

        def merge_and_out(n):
            def ev_gate(dst, dkey):
                def f(bk, bkey):
                    self.act(dst[:, 0:n], bk[:, 0:n], AF.Tanh, (bkey,), (dkey,), scale=0.5)
                    self.ts("dve", dst[:, 0:n], dst[:, 0:n], 0.5, 0.5, ALU.mult, ALU.add, (dkey,), (dkey,))
                return f
            for G in range(2):
                wga, wka = self.wload(win_g[G_GA + G], kwin, 8, 512)
                wba, wkba = self.wload(wbra_g[G], kwbra, 4, 512)
                for j in range(4):
                    c = G * 4 + j
                    bk, bkey = self.bank()
                    for kc in range(8):
                        self.mm(bk[:, 0:n], wga(kc, j * 128, (j + 1) * 128), hT[:, kc, 0:n], kc == 0, kc == 7, (wka, K("hT", kc)), (bkey,))
                    ev_gate(sgA, K("sgA"))(bk, bkey)
                    bk, bkey = self.bank()
                    for kc in range(4):
                        self.mm(bk[:, 0:n], wba(kc, j * 128, (j + 1) * 128), yaT(kc)[:, 0:n], kc == 0, kc == 3, (wkba, YAK[kc]), (bkey,))
                    self.tt("dve", m2[:, c, 0:n], sgA[:, 0:n], bk[:, 0:n], ALU.mult, (K("sgA"), bkey), (K("m2", c),))
                wgb, wkb = self.wload(win_g[G_GBR + G], kwin, 8, 512)
                wbb, wkbb = self.wload(wbrb_g[G], kwbrb, 4, 512)
                for j in range(4):
                    c = G * 4 + j
                    bk, bkey = self.bank()
                    for kc in range(8):
                        self.mm(bk[:, 0:n], wgb(kc, j * 128, (j + 1) * 128), hT[:, kc, 0:n], kc == 0, kc == 7, (wkb, K("hT", kc)), (bkey,))
                    ev_gate(sgB, K("sgB"))(bk, bkey)
                    bk, bkey = self.bank()
                    for kc in range(4):
                        self.mm(bk[:, 0:n], wbb(kc, j * 128, (j + 1) * 128), ybT(kc)[:, 0:n], kc == 0, kc == 3, (wkbb, YBK[kc]), (bkey,))
                    self.tt("dve", sgB[:, 0:n], sgB[:, 0:n], bk[:, 0:n], ALU.mult, (K("sgB"), bkey), (K("sgB"),))
                    self.tt("dve", mergedT(c)[:, 0:n], sgB[:, 0:n], m2[:, c, 0:n], ALU.add, (K("sgB"), K("m2", c)), (MK(c),))
            for G in range(2):
                wg, wk = self.wload(wout_g[G], kwout, 8, 512)
                for j in range(4):
                    c = G * 4 + j
                    bk, bkey = self.bank()
                    for kc in range(8):
                        self.mm(bk[:, 0:n], wg(kc, j * 128, (j + 1) * 128), mergedT(kc)[:, 0:n], kc == 0, kc == 7, (wk, MK(kc)), (bkey,))
                    self.cp("act", m2[:, c, 0:n], bk[:, 0:n], (bkey,), (K("m2", c),))
            rms_bcast(m2, n, lambda kc: (K("m2", kc),))

        def ffn(n, nseg, hview, hkeyf, extra=()):
            w = n // nseg
            v3 = lambda ap: ap.rearrange("p (s w) -> p s w", s=nseg)
            for g in range(11):
                wg, wk = self.wload(wup_g[g], kwup, 8, 512)
                for pair in range(2):
                    for half in range(2):
                        jc = half * 2 + pair
                        jj = half * NFF + 2 * g + pair
                        bk, bkey = self.bank()
                        for kc in range(8):
                            self.mm(bk[:, 0:n], wg(kc, jc * 128, (jc + 1) * 128), hT[:, kc, 0:n], kc == 0, kc == 7, (wk, K("hT", kc)), (bkey,))
                        u3 = ua[half][:, 0:nseg * (w + 2)].rearrange("p (s w) -> p s w", s=nseg)
                        uk, uh, yk = K("ua", half), K("uah", half), K("y0", half)
                        yv = v3(y0[half][:, 0:n])
                        self.cp("pool", u3[:, :, 0:2], hview(jj), (hkeyf(jj),) + extra, (uh,))
                        self.cp("act", u3[:, :, 2:2 + w], v3(bk[:, 0:n]), (bkey,), (uk,))
                        self.act(y0[half][:, 0:n], bk[:, 0:n], AF.Identity, (bkey, K("wdwT2"), K("bdwT")), (yk,),
                                 bias=bdwT[:, jj:jj + 1], scale=wdwT[:, 2, jj:jj + 1])
                        self.stt("dve", yv, u3[:, :, 1:1 + w], wdwT[:, 1, jj:jj + 1], yv, ALU.mult, ALU.add, (uk, uh, yk, K("wdwT1")), (yk,))
                        self.stt("dve", yv, u3[:, :, 0:w], wdwT[:, 0, jj:jj + 1], yv, ALU.mult, ALU.add, (uk, uh, yk, K("wdwT0")), (yk,))
                        self.cp("pool", hview(jj), u3[:, :, w:w + 2], (uk,), (hkeyf(jj),))
                    j = 2 * g + pair
                    self.act(y0[0][:, 0:n], y0[0][:, 0:n], AF.Gelu_apprx_tanh, (K("y0", 0),), (K("y0", 0),))
                    self.tt("dve", actT[:, j, 0:n], y0[0][:, 0:n], y0[1][:, 0:n], ALU.mult, (K("y0", 0), K("y0", 1)), (K("actT", j),))
            for c in range(8):
                wg, wk = self.wload(wdown_g[c], kwdown, NFF, 128)
                bk, bkey = self.bank()
                for j in range(NFF):
                    self.mm(bk[:, 0:n], wg(j), actT[:, j, 0:n], j == 0, j == NFF - 1, (wk, K("actT", j)), (bkey,))
                self.cp("act", m2[:, c, 0:n], bk[:, 0:n], (bkey,), (K("m2", c),))
            rms_bcast(m2, n, lambda kc: (K("m2", kc),))

        def store_y(dst_rows, ntile):
            for t in range(ntile):
                yb_ = xin[t % 2]
                ky = K("xin", t % 2)
                for half in range(2):
                    bk, bkey = self.bank()
                    for j in range(4):
                        kc = half * 4 + j
                        self.tr(bk[:, j * 128:(j + 1) * 128], xT[:, kc, t * 128:(t + 1) * 128], ident_f[:], (K("xT", kc, t), K("ident_f")), (bkey,))
                    self.cp("act", yb_[:, half * 512:(half + 1) * 512], bk[:], (bkey,), (K("xinv", t % 2),) if half else (ky,))
                self.dma("sp", dst_rows[t * 128:(t + 1) * 128, :], yb_[:], (ky, K("xinv", t % 2)), (), ("xin", t % 2))

        def store_hist(hsrc, hkeys, dst):
            self.cp("dve", h88[:], hsrc.rearrange("p j t -> p t j"), hkeys, (K("h88"),))
            bk, bkey = self.bank()
            self.tr(bk[0:88, 0:128], h88[:].rearrange("p t j -> p (t j)"), ident_f[:], (K("h88"), K("ident_f")), (bkey,))
            self.cp("act", stage[0:88, :], bk[0:88, 0:128], (bkey,), (K("stage"),))
            self.dma("sp", dst, stage[0:88, :], (K("stage"),), (), ("stage_o",))

        def kv_stage(bk, bkey, which, dst):
            st = xin[1][:, which * 512:(which + 1) * 512]
            sk = K("xin", 1) if which == 0 else K("xinv", 1)
            self.cp("act", st, bk[:], (bkey,), (sk,))
            self.dma("sp", dst, st, (sk,), (), ("xin", 1) if which == 0 else ("xinv", 1))

        LAST_KV0 = 16 + NMAIN - 4

        def common_proj(n, units, tglob0, kv_from, pre, flagged, kv_out):
            if not pre:
                wg, wk = self.wload(win_g[G_QA], kwin, 8, 512)
                def ev_q(j, bk, bkey):
                    self.act(qaT[:, j, 0:n], bk[0:64, 0:n], AF.Identity, (bkey,), (K("qaT", j),), scale=0.125)
                heads_fm(wg, wk, 0, 8, n, ev_q)
            if kv_from < len(units):
                wg, wk = self.wload(win_g[G_KA], kwin, 8, 512)
                def ev_k(j, bk, bkey):
                    for ui in range(kv_from, len(units)):
                        c0, C = units[ui]
                        if C == 128:
                            slot = (tglob0 + ui) % 8
                            self.cp("act", kaT[slot][:, j, :], bk[0:64, c0:c0 + 128], (bkey,), (K("kaT", slot, j),))
                    if units[0][1] != 128:
                        self.cp("act", kaT[tglob0 % 8][:, j, :], bk[0:64, 0:128], (bkey,), (K("kaT", tglob0 % 8, j),))
                heads_fm(wg, wk, 0, 8, n, ev_k)
                kunits = [(ui, units[ui]) for ui in range(kv_from, len(units)) if kv_out(ui, 0) is not None] if units[0][1] == 128 else []
                if kunits:
                    units_tm(wg, wk, 0, 512, [u for _, u in kunits], lambda i, bk, bkey: kv_stage(bk, bkey, 0, kv_out(kunits[i][0], 0)))
                if units[0][1] != 128:
                    units_tm(wg, wk, 0, 512, [(0, 128)], lambda i, bk, bkey: kv_stage(bk, bkey, 0, kv_out(0, 0)))
                wg, wk = self.wload(win_g[G_VA], kwin, 8, 512)
                if units[0][1] == 128:
                    def ev_v(i, bk, bkey):
                        ui = kv_from + i
                        slot = (tglob0 + ui) % 8
                        self.cp("act", vaug[slot][:, :, 0:64], bk[:].rearrange("p (h e) -> p h e", h=8), (bkey,), (K("vaug", slot),))
                        if flagged:
                            self.ts("dve", vaug[slot][:, :, 64], ones_f[:, 0:8], flag[:, 0:1], None, ALU.mult, None,
                                    (K("ones_f"), K("flag")), (K("vaug1", slot),))
                        else:
                            self.cp("dve", vaug[slot][:, :, 64], ones_f[:, 0:8], (K("ones_f"),), (K("vaug1", slot),))
                        if kv_out(ui, 1) is not None:
                            kv_stage(bk, bkey, 1, kv_out(ui, 1))
                    units_tm(wg, wk, 0, 512, units[kv_from:], ev_v)
                else:
                    units_tm(wg, wk, 0, 512, [(0, 128)], lambda i, bk, bkey: kv_stage(bk, bkey, 1, kv_out(0, 1)))
            if not pre:
                wg, wk = self.wload(win_g[G_QKB], kwin, 8, 512)
                def ev_qb(j, bk, bkey):
                    self.cp("act", qbT[:, j, 0:n], bk[0:64, 0:n], (bkey,), (K("qbT", j),))
                def ev_kb(j, bk, bkey):
                    self.cp("act", kbT[:, j, 0:n], bk[0:64, 0:n], (bkey,), (K("kbT", j),))
                heads_fm(wg, wk, 0, 4, n, ev_qb)
                heads_fm(wg, wk, 256, 4, n, ev_kb)
            wg, wk = self.wload(wgk_g, kwgk, 8, 16)
            bk, bkey = self.bank()
            for kc in range(8):
                self.mm(bk[0:16, 0:n], wg(kc), hT[:, kc, 0:n], kc == 0, kc == 7, (wk, K("hT", kc)), (bkey,))
            self.cp("act", gkT[0:16, 0:n], bk[0:16, 0:n], (bkey,), (K("gkT"),))

        SK = K("Sst")
        def prompt_block(src_rows, ntile, tglob0, mode, dst_rows, kv_from):
            n = ntile * 128
            pre = mode == "prefix"
            flagged = mode != "full"
            units = [(t * 128, 128) for t in range(ntile)]
            mark = lambda nm: self.marks.append((mode, tglob0, nm, len(S.ops)))
            mark("start")
            load_xT(src_rows, ntile)
            rms_bcast(xT, n, xkeys(ntile))
            if flagged:
                modulate(n, ntile, "AmP", "BmP", lambda kc: AmP[:, kc:kc + 1], lambda kc: BmP[:, kc:kc + 1])
            else:
                modulate(n, ntile, "Am", "Bm", lambda kc: Am[:, kc, 0:1], lambda kc: Bm[:, kc, 0:1])
            mark("norm1")

            def kv_out(ui, which):
                tg = tglob0 + ui
                if mode != "full" or tg < LAST_KV0:
                    return None
                r0 = (tg - LAST_KV0) * 128
                return (kp_o if which == 0 else vp_o)[r0:r0 + 128, :]
            common_proj(n, units, tglob0, kv_from, pre, flagged, kv_out)
            mark("proj")
            gla_units(units, pre, lambda ui: (Sst, Sbf, SK, None, None))
            mark("gla")
            if pre:
                return
            for t in range(ntile):
                attn_tile(tglob0 + t, t)
            mark("attn")
            merge_and_out(n)
            resid(n, ntile, lambda kc: Gm[:, kc, 0:1], "Gm")
            mark("merge")
            rms_bcast(xT, n, xkeys(ntile))
            if flagged:
                modulate(n, ntile, "AfP", "BfP", lambda kc: AfP[:, kc:kc + 1], lambda kc: BfP[:, kc:kc + 1])
            else:
                modulate(n, ntile, "Af", "Bf", lambda kc: Af[:, kc, 0:1], lambda kc: Bf[:, kc, 0:1])
            ffn(n, 1, lambda jj: hist[:, jj:jj + 1, :], lambda jj: K("hist", jj))
            mark("ffn")
            resid(n, ntile, lambda kc: Gf[:, kc, 0:1], "Gf")
            store_y(dst_rows, ntile)
            mark("end")

        def sample_block():
            n = 128
            slot = 0
            units = [(s * 32, 32) for s in range(4)]
            self.marks.append(("sample", 0, "start", len(S.ops)))
            load_xT(xsam, 1)
            rms_bcast(xT, n, xkeys(1))
            modulate_seq(Am, Bm, "Am", "Bm")
            common_proj(n, units, slot, 0, False, False, lambda ui, which: (ks_o if which == 0 else vs_o))
            SSK = K("Sst")
            def S_of(ui):
                def before():
                    self.dma("sp", Ss[:], sgla[ui].rearrange("h k v -> k h v"), (), tuple(SSK + (h,) for h in range(4)), ("Ss",))
                    self.cp("act", Ssb[:], Ss[:], tuple(SSK + (h,) for h in range(4)), (SSK + ("b",),))
                def after():
                    self.dma("sp", glas_o[ui].rearrange("h k v -> k h v"), Ss[:], tuple(SSK + (h,) for h in range(4)), (), ("Ss",))
                return (Ss, Ssb, SSK, before, after)
            gla_units(units, False, S_of)
            wgv, wkv = self.wload(win_g[G_VA], kwin, 8, 512)
            for s in range(4):
                bk, bkey = self.bank()
                for kc in range(8):
                    self.mm(bk[0:32, :], hT[:, kc, s * 32:(s + 1) * 32], wgv(kc), kc == 0, kc == 7, (wkv, K("hT", kc)), (bkey,))
                self.cp("act", vown[:, :, 0:64], bk[0:32, :].rearrange("p (h e) -> p h e", h=8), (bkey,), (K("vown"),))
                self.cp("dve", vown[:, :, 64], ones_f[0:32, 0:8], (K("ones_f"),), (K("vown1"),))
                attn_sample(s, slot)
            merge_and_out(n)
            resid_seq(Gm, "Gm")
            rms_bcast(xT, n, xkeys(1))
            modulate_seq(Af, Bf, "Af", "Bf")
            for s in range(4):
                load_T(sconv[s], 88, hist_s[:, s, :, :].rearrange("p j t -> p t j"), "hist_s_in%d" % s,
                       view=lambda a: a.rearrange("p (t j) -> p t j", t=2))
            hs_in = tuple(K("hist_s_in%d" % s) for s in range(4))
            ffn(n, 4, lambda jj: hist_s[:, :, jj, :], lambda jj: K("hist_s", jj), hs_in)
            resid_seq(Gf, "Gf")
            store_y(ysam, 1)
            for s in range(4):
                store_hist(hist_s[:, s, :, :], tuple(K("hist_s", jj) for jj in range(44)), convs_o[s])

        STG = int(os.environ.get("KSTAGE", "99"))
        if STG <= 0:
            return S.finalize()
        t0 = 0
        while t0 < 11:
            nt = min(NB, 11 - t0)
            prompt_block(xpre[t0 * 128:(t0 + nt) * 128, :], nt, t0, "prefix", None, nt)
            t0 += nt
        while t0 < 15:
            nt = min(NB, 15 - t0)
            prompt_block(xpre[t0 * 128:(t0 + nt) * 128, :], nt, t0, "prefix", None, 0)
            t0 += nt
        if STG <= 2:
            return S.finalize()
        prompt_block(xov, 1, 15, "overlap", yov, 0)
        for jj in range(44):
            pass
        self.ts("dve", hist[:].rearrange("p j t -> p (j t)"), hist[:].rearrange("p j t -> p (j t)"), flag[:, 0:1], None, ALU.mult, None,
                tuple(K("hist", jj) for jj in range(44)) + (K("flag"),), tuple(K("hist", jj) for jj in range(44)))
        if STG <= 3:
            return S.finalize()
        for b in range(min(NMAIN // NB, int(os.environ.get('KMAIN', '99')))):
            prompt_block(xmain[b * NBT:(b + 1) * NBT, :], NB, 16 + b * NB, "full", ymain[b * NBT:(b + 1) * NBT, :], 0)
        self.dma("sp", glap_o.rearrange("h k v -> k h v"), Sst[:], tuple(SK + (h,) for h in range(4)), (), ("glap",))
        store_hist(hist[:], tuple(K("hist", jj) for jj in range(44)), convp_o)
        if STG <= 4:
            return S.finalize()
        sample_block()
        return S.finalize()


_CACHE = {}


def _program():
    if "nc" not in _CACHE:
        b = Builder()
        b.build()
        _CACHE["nc"] = b.nc
    return _CACHE["nc"]


def kernel(x_prompt, x_sample, cache_k_a, cache_v_a, state_gla, state_conv, c_prompt, c_sample,
           w_ada, b_ada, g_pre_mix, g_post_mix, g_pre_ffn, g_post_ffn, w_in, w_gk2, b_gk,
           rel_bias, g_gla, w_br_a, w_br_b, w_out, w_up, w_dw, b_dw, w_down):
    f = lambda a: np.ascontiguousarray(np.asarray(a, dtype=np.float32))
    x_prompt, x_sample = f(x_prompt), f(x_sample)
    rb = f(rel_bias)[0]
    kk = np.arange(128)[:, None]
    qq = np.arange(128)[None, :]
    idx_prev = np.clip(qq + 128 - kk, -128, 128) + 128
    idx_own = np.clip(qq - kk, -128, 128) + 128
    btab = np.stack([rb[:, idx_prev].transpose(1, 0, 2), rb[:, idx_own].transpose(1, 0, 2)])
    cvec = np.broadcast_to(rb[:, 256][None, :], (128, 8))
    wi = f(w_in)[0]
    wi = np.concatenate([wi[:, :3072], wi[:, 3088:], wi[:, 3072:3088]], axis=1)
    wu = f(w_up)[0]
    cols = []
    for g in range(11):
        cols += [wu[:, 2 * g * 128:(2 * g + 2) * 128], wu[:, DFF + 2 * g * 128:DFF + (2 * g + 2) * 128]]
    wu = np.concatenate(cols, axis=1)
    shared = {
        "w_ada": f(w_ada)[0], "b_ada": f(b_ada)[0].reshape(48, 128),
        "gvec": np.concatenate([f(g_pre_mix)[0], f(g_post_mix)[0], f(g_pre_ffn)[0], f(g_post_ffn)[0]]).reshape(32, 128),
        "w_in": f(wi), "w_gk2": f(w_gk2)[0], "b_gk": f(b_gk)[0].reshape(1, 256),
        "btab": f(btab), "cvec": f(cvec), "ggla": f(np.broadcast_to(np.tile(f(g_gla)[0], 4)[None, :], (128, 512))),
        "w_br_a": f(w_br_a)[0], "w_br_b": f(w_br_b)[0], "w_out": f(w_out)[0], "w_up": f(wu),
        "w_dw": f(w_dw)[0].reshape(3 * 44, 128), "b_dw": f(b_dw)[0].reshape(44, 128), "w_down": f(w_down)[0],
    }
    in_maps = []
    for c in range(8):
        b, hf = c // 2, c % 2
        m = dict(shared)
        if hf == 1:
            m["xpre"] = x_prompt[b, 0:NPRE * 128]
            m["xov"] = x_prompt[b, NPRE * 128:2048]
        else:
            m["xpre"] = np.zeros((NPRE * 128, D), np.float32)
            m["xov"] = np.zeros((128, D), np.float32)
        m["xmain"] = x_prompt[b, hf * 2048:(hf + 1) * 2048]
        m["xsam"] = x_sample[4 * c:4 * c + 4].reshape(128, D)
        m["crow"] = f(np.concatenate([f(c_prompt)[b:b + 1], f(c_sample)[4 * c:4 * c + 4]], 0).reshape(40, 128))
        m["flag"] = np.full((128, 1), float(hf), np.float32)
        m["ck"] = f(cache_k_a)[0, 4 * c:4 * c + 4].reshape(4, 512, 512)
        m["cv"] = f(cache_v_a)[0, 4 * c:4 * c + 4].reshape(4, 512, 512)
        m["sgla"] = f(state_gla)[0, 4 * c:4 * c + 4]
        m["sconv"] = f(state_conv)[0, 4 * c:4 * c + 4].reshape(4, 88, 128)
        in_maps.append({k: np.ascontiguousarray(v) for k, v in m.items()})
    nc = _program()
    cores = [int(t) for t in os.environ.get("KCORES", "0,1,2,3,4,5,6,7").split(",")]
    if os.environ.get("KTRACE"):
        res = run_bass_kernel_spmd(nc, [in_maps[c] for c in cores], core_ids=list(range(len(cores))), trace=True)
        print("EXEC_TIME_NS", res.exec_time_ns)
    else:
        res = run_bass_kernel_spmd(nc, [in_maps[c] for c in cores], core_ids=list(range(len(cores))))
    R = {c: res.results[i] for i, c in enumerate(cores)}
    y_prompt = np.zeros((4, 4096, D), np.float32)
    y_sample = np.zeros((32, 32, D), np.float32)
    k_p = np.zeros((1, 4, 512, 8, 64), np.float32); v_p = np.zeros_like(k_p)
    gla_p = np.zeros((1, 4, 4, 64, 128), np.float32)
    conv_p = np.zeros((1, 4, 2, 2 * DFF), np.float32)
    k_s = np.zeros((1, 32, 32, 8, 64), np.float32); v_s = np.zeros_like(k_s)
    gla_s = np.zeros((1, 32, 4, 64, 128), np.float32)
    conv_s = np.zeros((1, 32, 2, 2 * DFF), np.float32)
    for c in cores:
        b, hf = c // 2, c % 2
        r = R[c]
        y_prompt[b, hf * 2048:(hf + 1) * 2048] = r["ymain"]
        y_sample[4 * c:4 * c + 4] = r["ysam"].reshape(4, 32, D)
        if hf == 1:
            k_p[0, b] = r["kp"].reshape(512, 8, 64)
            v_p[0, b] = r["vp"].reshape(512, 8, 64)
            gla_p[0, b] = r["glap"]
            conv_p[0, b] = r["convp"].reshape(2, 2 * DFF)
        k_s[0, 4 * c:4 * c + 4] = r["ks"].reshape(4, 32, 8, 64)
        v_s[0, 4 * c:4 * c + 4] = r["vs"].reshape(4, 32, 8, 64)
        gla_s[0, 4 * c:4 * c + 4] = r["glas"]
        conv_s[0, 4 * c:4 * c + 4] = r["convs"].reshape(4, 2, 2 * DFF)
    return (y_prompt, y_sample, k_p, v_p, gla_p, conv_p, k_s, v_s, gla_s, conv_s)
```

```python
from contextlib import ExitStack
import os

import numpy as np
import concourse.bass as bass
import concourse.mybir as mybir
from concourse.bass_utils import run_bass_kernel_spmd

F32 = mybir.dt.float32
BF16 = mybir.dt.bfloat16
AF = mybir.ActivationFunctionType
ALU = mybir.AluOpType

D = 1024
KC = 8
DFF = 2816
NFF = 22
DIN = 5136
G_QA, G_KA, G_VA, G_QKB, G_VB, G_GB, G_GA, G_GBR = 0, 1, 2, 3, 4, 5, 6, 8
EPS = 1e-6
NEG = -30000.0
NPRE = 15
NMAIN = 16
NB = 4


class Op:
    __slots__ = ("eng", "fn", "reads", "writes", "dsem", "signal", "sigval", "deps")

    def __init__(self, eng, fn, reads, writes, dsem):
        self.eng, self.fn, self.reads, self.writes, self.dsem = eng, fn, reads, writes, dsem
        self.signal = False
        self.sigval = 0
        self.deps = ()


class Sched:
    DMA = ("sp", "pool_dma")

    def __init__(self, nc, stack):
        self.nc = nc
        self.stack = stack
        self.ops = []
        self.eng_obj = {"pe": nc.tensor, "act": nc.scalar, "dve": nc.vector, "pool": nc.gpsimd,
                        "sp": nc.sync, "pool_dma": nc.gpsimd}
        self.wuses = []
        self.wdepth = 2

    def add(self, eng, fn, reads=(), writes=(), dsem=None):
        op = Op(eng, fn, tuple(reads), tuple(writes), dsem)
        self.ops.append(op)
        return op

    def queue_of(self, op):
        return "pool" if op.eng == "pool_dma" else op.eng

    def finalize(self):
        nc = self.nc
        inserts = {}
        lastrd = {}
        for idx, op in enumerate(self.ops):
            for k in op.reads:
                if k and k[0] == "wuse":
                    lastrd[k[1]] = idx
        prev = 0
        for i, (pos, op) in enumerate(self.wuses):
            tgt = self.wuses[max(0, i - self.wdepth)][0]
            if i >= 3 and (i - 3) in lastrd:
                tgt = max(tgt, lastrd[i - 3] + 1)
            tgt = max(tgt, prev)
            prev = tgt
            assert tgt <= pos, (i, tgt, pos)
            inserts.setdefault(tgt, []).append(op)
        ops = []
        for i, op in enumerate(self.ops):
            if i in inserts:
                ops.extend(inserts[i])
            ops.append(op)
        self.ops = ops
        last_w = {}
        readers = {}
        for i, op in enumerate(ops):
            deps = set()
            q = self.queue_of(op)
            isdma = op.dsem is not None
            for k in op.reads:
                j = last_w.get(k)
                if j is not None:
                    deps.add(j)
            for k in op.writes:
                j = last_w.get(k)
                if j is not None:
                    oj = ops[j]
                    if isdma or oj.dsem is not None or self.queue_of(oj) != q or q != "pe":
                        deps.add(j)
                for j in readers.get(k, ()):
                    oj = ops[j]
                    if isdma or oj.dsem is not None or self.queue_of(oj) != q:
                        deps.add(j)
            deps.discard(i)
            op.deps = tuple(sorted(deps))
            for j in op.deps:
                ops[j].signal = True
            for k in op.reads:
                lst = readers.setdefault(k, [])
                if op.dsem is None:
                    lst[:] = [j for j in lst if ops[j].dsem is not None or self.queue_of(ops[j]) != q]
                lst.append(i)
            for k in op.writes:
                last_w[k] = i
                readers[k] = []
        esem = {}
        for e in ("pe", "act", "dve", "pool"):
            esem[e] = self.stack.enter_context(nc.semaphore("sem_" + e))
        dsems = {}
        cnt = {}
        for op in ops:
            if op.dsem is not None:
                if op.dsem not in dsems:
                    dsems[op.dsem] = self.stack.enter_context(nc.semaphore("dsem_%d" % len(dsems)))
                    cnt[op.dsem] = 0
                cnt[op.dsem] += 1
                op.sigval = 16 * cnt[op.dsem]
            elif op.signal:
                q = self.queue_of(op)
                cnt[q] = cnt.get(q, 0) + 1
                op.sigval = cnt[q]
        known = {q: {} for q in ("pe", "act", "dve", "pool", "sp")}
        for op in ops:
            q = self.queue_of(op)
            eng = self.eng_obj[op.eng]
            need = {}
            for j in op.deps:
                oj = ops[j]
                s = dsems[oj.dsem] if oj.dsem is not None else esem[self.queue_of(oj)]
                key = id(s)
                if key not in need or need[key][1] < oj.sigval:
                    need[key] = (s, oj.sigval)
            for key, (s, v) in need.items():
                if known[q].get(key, 0) >= v:
                    continue
                eng.wait_ge(s, v)
                known[q][key] = v
            ins = op.fn(eng)
            if op.dsem is not None:
                ins.then_inc(dsems[op.dsem], 16)
            elif op.signal:
                ins.then_inc(esem[q], 1)
        self.counts = dict((str(k), v) for k, v in cnt.items())
        for k, s in dsems.items():
            nc.sync.wait_ge(s, 16 * cnt[k])
        return len(ops)


class Builder:
    def __init__(self):
        self.stack = ExitStack()
        self.nc = bass.Bass("TRN2", target_bir_lowering=False)
        self.S = Sched(self.nc, self.stack)
        self.nbank = 0
        self.wuse_n = 0
        self.bank_set = [0, 1, 2, 3, 4]
        self.marks = []
        self.uid = 0

    def din(self, name, shape, dt=F32):
        return self.nc.dram_tensor(name, list(shape), dt, kind="ExternalInput").ap()

    def dout(self, name, shape):
        return self.nc.dram_tensor(name, list(shape), F32, kind="ExternalOutput").ap()

    def dscr(self, name, shape, dt=BF16):
        return self.nc.dram_tensor(name, list(shape), dt, kind="Internal").ap()

    def sb(self, name, shape, dt=F32):
        return self.stack.enter_context(self.nc.sbuf_tensor(name, list(shape), dt))

    def ps(self, name, shape, dt=F32):
        return self.stack.enter_context(self.nc.psum_tensor(name, list(shape), dt))

    def bank(self):
        bs = self.bank_set
        i = bs[self.nbank % len(bs)]
        self.nbank += 1
        return self.banks[i], ("ps", i)

    def key(self, name):
        self.uid += 1
        return (name, self.uid)

    def mm(self, out, lhsT, rhs, start, stop, reads, writes):
        self.S.add("pe", lambda e: e.matmul(out, lhsT, rhs, start=start, stop=stop), reads, writes)

    def tr(self, out, in_, ident, reads, writes):
        self.S.add("pe", lambda e: e.transpose(out, in_, ident), reads, writes)

    def act(self, out, in_, func, reads, writes, bias=None, scale=None, accum_out=None):
        kw = {}
        if bias is not None:
            kw["bias"] = bias
        if scale is not None:
            kw["scale"] = scale
        if accum_out is not None:
            kw["accum_out"] = accum_out
        self.S.add("act", lambda e: e.activation(out, in_, func, **kw), reads, writes)

    def tt(self, eng, out, in0, in1, op, reads, writes):
        self.S.add(eng, lambda e: e.tensor_tensor(out, in0, in1, op), reads, writes)

    def ts(self, eng, out, in0, s1, s2, op0, op1, reads, writes):
        if s2 is None:
            self.S.add(eng, lambda e: e.tensor_scalar(out, in0, s1, None, op0), reads, writes)
        else:
            self.S.add(eng, lambda e: e.tensor_scalar(out, in0, s1, s2, op0, op1), reads, writes)

    def stt(self, eng, out, in0, scalar, in1, op0, op1, reads, writes):
        self.S.add(eng, lambda e: e.scalar_tensor_tensor(out, in0, scalar, in1, op0=op0, op1=op1), reads, writes)

    def cp(self, eng, out, in_, reads, writes):
        if eng == "act":
            self.S.add("act", lambda e: e.copy(out, in_), reads, writes)
        else:
            self.S.add(eng, lambda e: e.tensor_copy(out, in_), reads, writes)

    def memset(self, eng, ap, val, writes):
        self.S.add(eng, lambda e: e.memset(ap, val), (), writes)

    def dma(self, q, out, in_, reads, writes, dsem, slow=False):
        if slow:
            self.S.add(q, lambda e: e.dma_start(out=out, in_=in_, allow_slow_non_contiguous=True), reads, writes, dsem)
        else:
            self.S.add(q, lambda e: e.dma_start(out=out, in_=in_), reads, writes, dsem)

    def interleave(self, fns_banks):
        main = self.S.ops
        streams = []
        for fn, banks in fns_banks:
            self.S.ops = []
            self.bank_set = banks
            fn()
            streams.append(self.S.ops)
        self.S.ops = main
        self.bank_set = [0, 1, 2, 3, 4]
        idx = [0] * len(streams)
        total = sum(len(st) for st in streams)
        for _ in range(total):
            best, bf = None, None
            for si, st in enumerate(streams):
                if idx[si] < len(st):
                    frac = idx[si] / len(st)
                    if bf is None or frac < bf:
                        best, bf = si, frac
            main.append(streams[best][idx[best]])
            idx[best] += 1

    def wload(self, scr_g, wkeys, kcn, width):
        i = self.wuse_n
        self.wuse_n += 1
        slot = i % 3
        buf = self.wbufs[slot]
        key = ("wuse", i)
        dst = buf[:, 0:kcn * width].rearrange("p (kc n) -> p kc n", kc=kcn)
        op = Op("sp", lambda e: e.dma_start(out=dst, in_=scr_g), tuple(wkeys),
                (key, ("wbuf", slot)) + ((("wuse", i - 3),) if i >= 3 else ()), ("wstream", slot))
        self.S.wuses.append((len(self.S.ops), op))
        return (lambda kc, a=0, b=width: buf[:, kc * width + a: kc * width + b]), key

    def build(self):
        nc = self.nc
        S = self.S
        sb, ps = self.sb, self.ps
        NBT = NB * 128
        K = lambda *a: tuple(a)
        xpre = self.din("xpre", [NPRE * 128, D])
        xov = self.din("xov", [128, D])
        xmain = self.din("xmain", [NMAIN * 128, D])
        xsam = self.din("xsam", [128, D])
        crow = self.din("crow", [40, 128])
        flag_d = self.din("flag", [128, 1])
        ck = self.din("ck", [4, 512, 512])
        cv = self.din("cv", [4, 512, 512])
        sgla = self.din("sgla", [4, 4, 64, 128])
        sconv = self.din("sconv", [4, 88, 128])
        w_ada = self.din("w_ada", [D, 6 * D])
        b_ada = self.din("b_ada", [48, 128])
        gvec = self.din("gvec", [32, 128])
        w_in = self.din("w_in", [D, DIN])
        w_gk2 = self.din("w_gk2", [16, 256])
        b_gk = self.din("b_gk", [1, 256])
        btab = self.din("btab", [2, 128, 8, 128])
        cvec = self.din("cvec", [128, 8])
        ggla = self.din("ggla", [128, 512])
        w_br_a = self.din("w_br_a", [512, D])
        w_br_b = self.din("w_br_b", [512, D])
        w_out = self.din("w_out", [D, D])
        w_up = self.din("w_up", [D, 2 * DFF])
        w_dw = self.din("w_dw", [3 * 44, 128])
        b_dw = self.din("b_dw", [44, 128])
        w_down = self.din("w_down", [DFF, D])

        ymain = self.dout("ymain", [NMAIN * 128, D])
        ysam = self.dout("ysam", [128, D])
        yov = self.dout("yov", [128, D])
        kp_o = self.dout("kp", [512, 512])
        vp_o = self.dout("vp", [512, 512])
        glap_o = self.dout("glap", [4, 64, 128])
        convp_o = self.dout("convp", [88, 128])
        ks_o = self.dout("ks", [128, 512])
        vs_o = self.dout("vs", [128, 512])
        glas_o = self.dout("glas", [4, 4, 64, 128])
        convs_o = self.dout("convs", [4, 88, 128])

        win_g = self.dscr("win_g", [10, 128, 8, 512])
        wgk_g = self.dscr("wgk_g", [128, 8, 16])
        wbra_g = self.dscr("wbra_g", [2, 128, 4, 512])
        wbrb_g = self.dscr("wbrb_g", [2, 128, 4, 512])
        wout_g = self.dscr("wout_g", [2, 128, 8, 512])
        wup_g = self.dscr("wup_g", [11, 128, 8, 512])
        wdown_g = self.dscr("wdown_g", [8, 128, NFF, 128])

        self.banks = [ps("bank%d" % i, [128, 512]) for i in range(5)]
        obank = [ps("obank%d" % i, [128, 512]) for i in range(2)]
        pbf = ps("pbf", [128, 1024], BF16)

        self.wbufs = [sb("wbuf%d" % i, [128, 4096], BF16) for i in range(3)]
        ident_f = sb("ident_f", [128, 128])
        ident_b = sb("ident_b", [128, 128], BF16)
        ones_b = sb("ones_b", [128, 128], BF16)
        triu_f = sb("triu_f", [128, 128])
        trisl_f = sb("trisl_f", [128, 128])
        ones_f = sb("ones_f", [128, 8])
        epsb = sb("epsb", [128, 1])
        mod = sb("mod", [128, 6, KC, 5])
        Am = sb("Am", [128, KC, 5]); Bm = sb("Bm", [128, KC, 5]); Gm = sb("Gm", [128, KC, 5])
        Af = sb("Af", [128, KC, 5]); Bf = sb("Bf", [128, KC, 5]); Gf = sb("Gf", [128, KC, 5])
        AmP = sb("AmP", [128, KC]); BmP = sb("BmP", [128, KC])
        AfP = sb("AfP", [128, KC]); BfP = sb("BfP", [128, KC])
        gT = sb("gT", [128, 32])
        badaT = sb("badaT", [128, 48])
        cT = sb("cT", [128, 40])
        siluT = sb("siluT", [128, 40], BF16)
        flag = sb("flag_s", [128, 1])
        wdwT = sb("wdwT", [128, 3, 44])
        bdwT = sb("bdwT", [128, 44])
        wgk_f = sb("wgk_f", [17, 256])
        wgk_b = sb("wgk_b", [17, 256], BF16)
        tab_prev = sb("tab_prev", [128, 8, 128], BF16)
        tab_own = sb("tab_own", [128, 8, 128], BF16)
        tab_mask = sb("tab_mask", [128, 128], BF16)
        cvec_s = sb("cvec_s", [128, 8])
        ggla_s = sb("ggla_s", [128, 512])
        stage = sb("stage", [128, 128])
        hist = sb("hist", [128, 44, 2])
        hist_s = sb("hist_s", [128, 4, 44, 2])
        h88 = sb("h88", [128, 2, 44])
        xin = [sb("xin%d" % i, [128, D]) for i in range(2)]
        xT = sb("xT", [128, KC, NBT])
        hT = sb("hT", [128, KC, NBT], BF16)
        sq = sb("sq", [128, 2, NBT], BF16)
        rbc = sb("rbc", [128, NBT])
        tmpf = sb("tmpf", [128, NBT])
        tmpf2 = sb("tmpf2", [128, NBT])
        qaT = sb("qaT", [64, 8, NBT], BF16)
        kaT = [sb("kaT%d" % i, [64, 8, 128], BF16) for i in range(8)]
        vaug = [sb("vaug%d" % i, [128, 8, 65], BF16) for i in range(8)]
        qbT = sb("qbT", [64, 4, NBT], BF16)
        kbT = sb("kbT", [64, 4, NBT], BF16)
        gkT = sb("gkT", [32, NBT], BF16)
        pT_raw = sb("pT_raw", [128, 2560])
        pT = pT_raw.bitcast(BF16)[:, :].rearrange("p (k h q) -> p k h q", k=5, h=8)
        ya_tok = sb("ya_tok", [128, 512], BF16)
        rden = sb("rden", [128, 8])
        kb_tok = sb("kb_tok", [128, 256])
        vb_tok = sb("vb_tok", [128, 4, 128], BF16)
        gb2 = sb("gb2", [128, 512])
        gtanh = sb("gtanh", [128, 512])
        Lsp = sb("Lsp", [128, 256])
        e_sb = sb("e_sb", [128, 256])
        e1 = sb("e1", [64, 4, 128]); e2 = sb("e2", [64, 4, 128])
        qtT = sb("qtT", [64, 4, 128], BF16); ktT = sb("ktT", [64, 4, 128], BF16)
        kend = sb("kend", [128, 256], BF16)
        dec = sb("dec", [64, 4])
        attT = sb("attT", [128, 4, 128], BF16)
        Sst = sb("Sst", [64, 4, 128])
        Sbf = sb("Sbf", [64, 4, 128], BF16)
        ssq = sb("ssq", [128, 4]); rgl = sb("rgl", [128, 4])
        yb_tok = sb("yb_tok", [128, 512], BF16)
        sgA = sb("sgA", [128, NBT]); sgB = sb("sgB", [128, NBT])
        m2 = sb("m2", [128, KC, NBT])
        ua = [sb("ua%d" % i, [128, NBT + 8]) for i in range(2)]
        y0 = [sb("y0_%d" % i, [128, NBT]) for i in range(2)]
        ua_sets = [ua, [pT_raw[:, 0:NBT + 8], pT_raw[:, NBT + 8:2 * NBT + 16]]]
        y0_sets = [y0, [pT_raw[:, 2 * NBT + 16:3 * NBT + 16], pT_raw[:, 3 * NBT + 16:4 * NBT + 16]]]
        actT = sb("actT", [128, NFF, NBT], BF16)
        mergedT = lambda c: actT[:, c, :]
        MK = lambda c: K("actT", c)
        yaT = lambda c: actT[:, 8 + c, :]
        YAK = tuple(K("actT", 8 + c) for c in range(4))
        ybT = lambda c: actT[:, 12 + c, :]
        YBK = tuple(K("actT", 12 + c) for c in range(4))
        m2b = m2.bitcast(BF16)
        kcT = lambda h: m2b[0:64, h, 0:512]
        vcaug = sb("vcaug", [128, 4, 8, 65], BF16)
        vown = sb("vown", [32, 8, 65], BF16)
        Ss, Ssb = Sst, Sbf

        def const_mask(t, keyname, pattern, cm, base, cmp_op):
            self.memset("pool", t[:], 1.0, (K(keyname),))
            S.add("pool", lambda e: e.affine_select(t[:], t[:], pattern=pattern, compare_op=cmp_op, fill=0.0,
                                                    base=base, channel_multiplier=cm), (K(keyname),), (K(keyname),))
        self.memset("pool", ident_f[:], 0.0, (K("ident_f"),))
        S.add("pool", lambda e: e.affine_select(ident_f[:], ident_f[:], pattern=[[-1, 128]], compare_op=ALU.not_equal,
                                                fill=1.0, base=0, channel_multiplier=1), (K("ident_f"),), (K("ident_f"),))
        const_mask(triu_f, "triu_f", [[1, 128]], -1, 0, ALU.is_ge)
        const_mask(trisl_f, "trisl_f", [[-1, 128]], 1, -1, ALU.is_ge)
        self.cp("dve", ident_b[:], ident_f[:], (K("ident_f"),), (K("ident_b"),))
        self.memset("dve", ones_b[:], 1.0, (K("ones_b"),))
        self.memset("dve", ones_f[:], 1.0, (K("ones_f"),))
        self.memset("dve", epsb[:], EPS, (K("epsb"),))
        self.memset("dve", gkT[:], 1.0, (K("gkT_ones"),))
        self.memset("dve", hist[:], 0.0, tuple(K("hist", jj) for jj in range(44)))
        self.memset("dve", Sst[:], 0.0, tuple(K("Sst", h) for h in range(4)))
        self.memset("dve", Sbf[:], 0.0, (K("Sst", "b"),))

        def load_T(src2d, rows, dst, kname, view=None):
            self.dma("pool_dma", stage[0:rows, :], src2d, (), (K("stage"),), ("stage",))
            bk, bkey = self.bank()
            self.tr(bk[:, 0:rows], stage[0:rows, :], ident_f[0:rows, 0:rows], (K("stage"), K("ident_f")), (bkey,))
            src = bk[:, 0:rows] if view is None else view(bk[:, 0:rows])
            self.cp("dve", dst, src, (bkey,), (K(kname),))

        load_T(gvec, 32, gT[:], "gT")
        load_T(b_ada, 48, badaT[:], "badaT")
        load_T(crow, 40, cT[:], "cT")
        for j in range(3):
            load_T(w_dw[j * 44:(j + 1) * 44, :], 44, wdwT[:, j, :], "wdwT%d" % j)
        load_T(b_dw, 44, bdwT[:], "bdwT")
        self.dma("pool_dma", flag[:], flag_d, (), (K("flag"),), ("misc", 0))
        self.dma("pool_dma", wgk_f[0:16, :], w_gk2, (), (K("wgk_f0"),), ("misc", 1))
        self.dma("pool_dma", wgk_f[16:17, :], b_gk, (), (K("wgk_f1"),), ("misc", 2))
        self.cp("dve", wgk_b[:], wgk_f[:], (K("wgk_f0"), K("wgk_f1")), (K("wgk_b"),))
        self.dma("pool_dma", cvec_s[:], cvec, (), (K("cvec"),), ("misc", 3))
        self.dma("pool_dma", ggla_s[:], ggla, (), (K("ggla"),), ("misc", 4))
        self.ts("dve", ggla_s[:], ggla_s[:], 0.5, None, ALU.mult, None, (K("ggla"),), (K("ggla"),))
        tabf = xin[0][:, :].rearrange("p (h q) -> p h q", h=8)
        for ti, tdst in ((0, tab_prev), (1, tab_own)):
            self.dma("pool_dma", tabf, btab[ti], (), (K("xin", 0), K("xinv", 0)), ("misc", 5))
            for h in range(8):
                self.ts("dve", tdst[:, h, :], tabf[:, h, :], cvec_s[:, h:h + 1], None, ALU.subtract, None,
                        (K("xin", 0), K("xinv", 0), K("cvec")), (K("tab", ti, h),))
        self.memset("dve", tab_own[64:128, :, 0:64], NEG, tuple(K("tab", 1, h) for h in range(8)))
        self.memset("dve", tab_mask[:], 0.0, (K("tab_mask"),))
        self.memset("dve", tab_mask[0:64, 64:128], NEG, (K("tab_mask"),))
        tabkeys = tuple(K("tab", ti, h) for ti in range(2) for h in range(8)) + (K("tab_mask"),)

        def cast_group(src_cols, dst_g, name, g):
            self.dma("pool_dma", dst_g, src_cols.rearrange("(kc p) n -> p kc n", p=128), (), (K("scr", name, g),), ("cast", name))
            return (K("scr", name, g),)

        self.act(tmpf[:, 0:40], cT[:], AF.Tanh, (K("cT"),), (K("tmpf"),), scale=0.5)
        self.ts("dve", tmpf[:, 0:40], tmpf[:, 0:40], 0.5, 0.5, ALU.mult, ALU.add, (K("tmpf"),), (K("tmpf"),))
        self.tt("dve", siluT[:], tmpf[:, 0:40], cT[:], ALU.mult, (K("tmpf"), K("cT")), (K("siluT"),))
        siluv = siluT[:].rearrange("p (s k) -> p k s", k=8)
        modkeys = {}
        k8 = lambda n: tuple(K(n, kc) for kc in range(8))

        def ada_groups(glist):
            for g in glist:
                sl = g % 2
                buf = actT[:, sl * 8:(sl + 1) * 8, :].rearrange("p c n -> p (c n)")
                bkeys = tuple(K("actT", sl * 8 + c) for c in range(8))
                self.dma("pool_dma", buf.rearrange("p (kc n) -> p kc n", kc=8),
                         w_ada.rearrange("(kc p) n -> p kc n", p=128)[:, :, g * 512:(g + 1) * 512], (), bkeys, ("ada", sl))
                bk, bkey = self.bank()
                for oc in range(4):
                    for kc in range(8):
                        self.mm(bk[:, oc * 8:oc * 8 + 5], buf[:, kc * 512 + oc * 128: kc * 512 + (oc + 1) * 128], siluv[:, kc, :],
                                kc == 0, kc == 7, bkeys + (K("siluT"),), (bkey,))
                for oc in range(4):
                    ch = g * 4 + oc
                    self.ts("dve", mod[:, ch // 8, ch % 8, :], bk[:, oc * 8:oc * 8 + 5], badaT[:, ch:ch + 1], None, ALU.add, None,
                            (bkey, K("badaT")), (K("mod", ch),))

        ada_groups(range(0, 4))
        mk1 = tuple(K("mod", ch) for ch in range(16))
        for kc in range(8):
            self.ts("dve", Am[:, kc, :], mod[:, 1, kc, :], 1.0, gT[:, kc:kc + 1], ALU.add, ALU.mult, mk1 + (K("gT"),), (K("Am", kc),))
        self.cp("dve", Bm[:], mod[:, 0, :, :], mk1, (K("Bm"),))
        self.ts("dve", AmP[:], Am[:, :, 0], flag[:, 0:1], None, ALU.mult, None, k8("Am") + (K("flag"),), (K("AmP"),))
        self.ts("dve", BmP[:], Bm[:, :, 0], flag[:, 0:1], None, ALU.mult, None, (K("Bm"), K("flag")), (K("BmP"),))
        modkeys.update({"Am": k8("Am"), "Bm": (K("Bm"),), "AmP": (K("AmP"),), "BmP": (K("BmP"),)})

        def ada_part2():
            ada_groups(range(4, 12))
            mk2 = tuple(K("mod", ch) for ch in range(16, 48))
            for kc in range(8):
                rw = mk2 + (K("gT"),)
                self.ts("dve", Af[:, kc, :], mod[:, 4, kc, :], 1.0, gT[:, 16 + kc:17 + kc], ALU.add, ALU.mult, rw, (K("Af", kc),))
                self.ts("dve", Gm[:, kc, :], mod[:, 2, kc, :], gT[:, 8 + kc:9 + kc], None, ALU.mult, None, rw, (K("Gm", kc),))
                self.ts("dve", Gf[:, kc, :], mod[:, 5, kc, :], gT[:, 24 + kc:25 + kc], None, ALU.mult, None, rw, (K("Gf", kc),))
            self.cp("dve", Bf[:], mod[:, 3, :, :], mk2, (K("Bf"),))
            self.ts("dve", AfP[:], Af[:, :, 0], flag[:, 0:1], None, ALU.mult, None, k8("Af") + (K("flag"),), (K("AfP"),))
            self.ts("dve", BfP[:], Bf[:, :, 0], flag[:, 0:1], None, ALU.mult, None, (K("Bf"), K("flag")), (K("BfP"),))
        modkeys.update({"Af": k8("Af"), "Gm": k8("Gm"), "Gf": k8("Gf"), "Bf": (K("Bf"),), "AfP": (K("AfP"),), "BfP": (K("BfP"),)})

        kwin = {}
        for g in (3, 4, 1, 2):
            kwin[g] = cast_group(w_in[:, g * 512:(g + 1) * 512], win_g[g], "win", g)
        self.dma("pool_dma", wgk_g, w_in.rearrange("(kc p) n -> p kc n", p=128)[:, :, 5120:5136], (), (K("scr", "wgk"),), ("cast", "wgk"))
        kwgk = (K("scr", "wgk"),)
        for g in (0, 5, 6, 7, 8, 9):
            kwin[g] = cast_group(w_in[:, g * 512:(g + 1) * 512], win_g[g], "win", g)
        kwbra = [cast_group(w_br_a[:, g * 512:(g + 1) * 512], wbra_g[g], "wbra", g) for g in range(2)]
        kwbrb = [cast_group(w_br_b[:, g * 512:(g + 1) * 512], wbrb_g[g], "wbrb", g) for g in range(2)]
        kwout = [cast_group(w_out[:, g * 512:(g + 1) * 512], wout_g[g], "wout", g) for g in range(2)]
        kwup = [cast_group(w_up[:, g * 512:(g + 1) * 512], wup_g[g], "wup", g) for g in range(11)]
        kwdown = [cast_group(w_down[:, c * 128:(c + 1) * 128], wdown_g[c], "wdown", c) for c in range(8)]

        def load_xT(src_rows, ntile):
            for t in range(ntile):
                xb = xin[t % 2]
                kx = K("xin", t % 2)
                self.dma("sp", xb[:], src_rows[t * 128:(t + 1) * 128, :], (), (kx, K("xinv", t % 2)), ("xin", t % 2))
                for half in range(2):
                    bk, bkey = self.bank()
                    for j in range(4):
                        kc = half * 4 + j
                        self.tr(bk[:, j * 128:(j + 1) * 128], xb[:, kc * 128:(kc + 1) * 128], ident_f[:],
                                (kx if half == 0 else K("xinv", t % 2), K("ident_f")), (bkey,))
                    self.cp("act", xT[:, half * 4:half * 4 + 4, t * 128:(t + 1) * 128],
                            bk[:].rearrange("p (j n) -> p j n", j=4), (bkey,), tuple(K("xT", half * 4 + j, t) for j in range(4)))

        def rms_bcast(srcT, n, src_keys_fn):
            bk, bkey = self.bank()
            for kc in range(8):
                self.act(sq[:, kc % 2, 0:n], srcT[:, kc, 0:n], AF.Square, src_keys_fn(kc), (K("sq", kc % 2),))
                self.mm(bk[:, 0:n], ones_b[:], sq[:, kc % 2, 0:n], kc == 0, kc == 7, (K("ones_b"), K("sq", kc % 2)), (bkey,))
            self.act(tmpf[:, 0:n], bk[:, 0:n], AF.Ln, (bkey, K("epsb")), (K("tmpf"),), bias=epsb[:, 0:1], scale=1.0 / D)
            self.act(rbc[:, 0:n], tmpf[:, 0:n], AF.Exp, (K("tmpf"),), (K("rbc"),), scale=-0.5)

        def modulate(n, ntile, Asc, Bsc, Afn, Bfn):
            for kc in range(8):
                xk = tuple(K("xT", kc, t) for t in range(ntile))
                self.tt("dve", tmpf2[:, 0:n], xT[:, kc, 0:n], rbc[:, 0:n], ALU.mult, xk + (K("rbc"),), (K("tmpf2"),))
                self.act(hT[:, kc, 0:n], tmpf2[:, 0:n], AF.Identity, (K("tmpf2"),) + modkeys[Asc] + modkeys[Bsc], (K("hT", kc),),
                         bias=Bfn(kc), scale=Afn(kc))

        def modulate_seq(At, Bt, Asc, Bsc):
            for kc in range(8):
                xk = (K("xT", kc, 0),)
                v3 = lambda ap: ap.rearrange("p (s w) -> p s w", s=4)
                self.tt("dve", tmpf2[:, 0:128], xT[:, kc, 0:128], rbc[:, 0:128], ALU.mult, xk + (K("rbc"),), (K("tmpf2"),))
                self.tt("dve", v3(tmpf2[:, 0:128]), v3(tmpf2[:, 0:128]), At[:, kc, 1:5].to_broadcast([128, 4, 32]), ALU.mult,
                        (K("tmpf2"),) + modkeys[Asc], (K("tmpf2"),))
                self.tt("dve", v3(hT[:, kc, 0:128]), v3(tmpf2[:, 0:128]), Bt[:, kc, 1:5].to_broadcast([128, 4, 32]), ALU.add,
                        (K("tmpf2"),) + modkeys[Bsc], (K("hT", kc),))

        xkeys = lambda ntile: (lambda kc: tuple(K("xT", kc, t) for t in range(ntile)))

        def heads_fm(wg, wk, col0, nheads, n, evac):
            for h in range(nheads):
                bk, bkey = self.bank()
                for kc in range(8):
                    self.mm(bk[0:64, 0:n], wg(kc, col0 + h * 64, col0 + (h + 1) * 64), hT[:, kc, 0:n], kc == 0, kc == 7, (wk, K("hT", kc)), (bkey,))
                evac(h, bk, bkey)

        def units_tm(wg, wk, col0, ncols, units, evac):
            for ui, (c0, C) in enumerate(units):
                bk, bkey = self.bank()
                for kc in range(8):
                    self.mm(bk[0:C, 0:ncols], hT[:, kc, c0:c0 + C], wg(kc, col0, col0 + ncols), kc == 0, kc == 7, (wk, K("hT", kc)), (bkey,))
                evac(ui, bk, bkey)

        def gla_block(C, col0, S_t, S_b, Skey, out_cb, prefix_only):
            bk, bkey = self.bank()
            self.mm(bk[0:C, 0:256], gkT[0:17, col0:col0 + C], wgk_b[0:17, :], True, True, (K("gkT"), K("gkT_ones"), K("wgk_b")), (bkey,))
            self.act(e_sb[0:C, :], bk[0:C, 0:256], AF.Exp, (bkey,), (K("e_sb"),), scale=-1.0)
            self.act(Lsp[0:C, :], e_sb[0:C, :], AF.Ln, (K("e_sb"),), (K("Lsp"),), bias=1.0)
            bk2, bkey2 = self.bank()
            self.mm(bk2[0:C, 0:256], trisl_f[0:C, 0:C], Lsp[0:C, :], True, True, (K("trisl_f"), K("Lsp")), (bkey2,))
            self.act(e_sb[0:C, :], bk2[0:C, 0:256], AF.Exp, (bkey2,), (K("e_sb"),), scale=-1.0 / 16)
            self.tt("dve", kend[0:C, :], kb_tok[0:C, :], e_sb[0:C, :], ALU.mult, (K("kb_tok"), K("e_sb")), (K("kend"),))
            bk3, bkey3 = self.bank()
            for h in range(4):
                self.mm(bk3[0:64, h * 128:h * 128 + C], Lsp[0:C, h * 64:(h + 1) * 64], triu_f[0:C, 0:C], True, True,
                        (K("Lsp"), K("triu_f")), (bkey3,))
            b3 = bk3[0:64, :].rearrange("p (c t) -> p c t", c=4)
            self.act(dec[:, :], b3[:, :, C - 1], AF.Exp, (bkey3,), (K("dec"),), scale=-1.0 / 16)
            if not prefix_only:
                self.act(e1[:, :, 0:C], b3[:, :, 0:C], AF.Exp, (bkey3,), (K("e1"),), scale=-1.0 / 16)
                self.act(e2[:, :, 0:C], b3[:, :, 0:C], AF.Exp, (bkey3,), (K("e2"),), scale=1.0 / 16)
                self.stt("dve", qtT[:, :, 0:C], qbT[:, :, col0:col0 + C], 0.125, e1[:, :, 0:C], ALU.mult, ALU.mult,
                         tuple(K("qbT", j) for j in range(4)) + (K("e1"),), (K("qtT"),))
                self.tt("dve", ktT[:, :, 0:C], kbT[:, :, col0:col0 + C], e2[:, :, 0:C], ALU.mult, tuple(K("kbT", j) for j in range(4)) + (K("e2"),), (K("ktT"),))
                bk4, bkey4 = self.bank()
                for h in range(4):
                    self.mm(bk4[0:C, h * 128:h * 128 + C], ktT[:, h, 0:C], qtT[:, h, 0:C], True, True,
                            (K("ktT"), K("qtT")), (bkey4,))
                for h in range(4):
                    self.tt("dve", attT[0:C, h, 0:C], bk4[0:C, h * 128:h * 128 + C], triu_f[0:C, 0:C], ALU.mult,
                            (bkey4, K("triu_f")), (K("attT", h),))
                bk5, bkey5 = self.bank()
                for h in range(4):
                    self.mm(bk5[0:C, h * 128:(h + 1) * 128], attT[0:C, h, 0:C], vb_tok[0:C, h, :], True, False, (K("attT", h), K("vb_tok")), (bkey5,))
                    self.mm(bk5[0:C, h * 128:(h + 1) * 128], qtT[:, h, 0:C], S_b[:, h, :], False, True,
                            (K("qtT"), Skey + ("b",)), (bkey5,))
                out_cb(bk5, bkey5)
            bk6, bkey6 = self.bank()
            for h in range(4):
                self.mm(bk6[0:64, h * 128:(h + 1) * 128], kend[0:C, h * 64:(h + 1) * 64], vb_tok[0:C, h, :], True, True, (K("kend"), K("vb_tok")), (bkey6,))
            for h in range(4):
                self.stt("dve", S_t[:, h, :], S_t[:, h, :], dec[:, h:h + 1], bk6[0:64, h * 128:(h + 1) * 128],
                         ALU.mult, ALU.add, (Skey + (h,), K("dec"), bkey6), (Skey + (h,),))
            self.cp("act", S_b[:], S_t[:], tuple(Skey + (h,) for h in range(4)), (Skey + ("b",),))

        def gla_out(C, obk, obkey):
            self.memset("dve", ssq[0:C, :], 0.0, tuple(K("ssq", h) for h in range(4)))
            for h in range(4):
                self.act(attT[0:C, h, :], obk[0:C, h * 128:(h + 1) * 128], AF.Square, (obkey,), (K("attT", h), K("ssq", h)), accum_out=ssq[0:C, h:h + 1])
            self.act(rgl[0:C, :], ssq[0:C, :], AF.Ln, tuple(K("ssq", h) for h in range(4)) + (K("epsb"),), (K("rgl"),), bias=epsb[0:C, 0:1], scale=1.0 / 128)
            self.act(rgl[0:C, :], rgl[0:C, :], AF.Exp, (K("rgl"),), (K("rgl"),), scale=-0.5)
            self.act(gtanh[0:C, :], gb2[0:C, :], AF.Tanh, (K("gb2"),), (K("gtanh"),), scale=0.5)
            self.stt("dve", gtanh[0:C, :], gtanh[0:C, :], 1.0, gb2[0:C, :], ALU.add, ALU.mult, (K("gtanh"), K("gb2")), (K("gtanh"),))
            self.tt("dve", gtanh[0:C, :], gtanh[0:C, :], ggla_s[0:C, :], ALU.mult, (K("gtanh"), K("ggla")), (K("gtanh"),))
            for h in range(4):
                self.stt("dve", yb_tok[0:C, h * 128:(h + 1) * 128], obk[0:C, h * 128:(h + 1) * 128], rgl[0:C, h:h + 1],
                         gtanh[0:C, h * 128:(h + 1) * 128], ALU.mult, ALU.mult, (obkey, K("rgl"), K("gtanh")), (K("yb_tok", h),))

        def gla_units(units, pre, S_of, with_fn=None):
            wgk_, wkk = self.wload(win_g[G_QKB], kwin[G_QKB], 8, 512)
            wgv_, wkv = self.wload(win_g[G_VB], kwin[G_VB], 8, 512)
            wgg_, wkg = (None, None) if pre else self.wload(win_g[G_GB], kwin[G_GB], 8, 512)
            body = lambda: gla_body(units, pre, S_of, wgk_, wkk, wgv_, wkv, wgg_, wkg)
            if with_fn is None:
                body()
            else:
                self.interleave([(body, [0, 1, 2]), (with_fn, [3, 4])])

        def gla_body(units, pre, S_of, wgk_, wkk, wgv_, wkv, wgg_, wkg):
            for ui, (col0, C) in enumerate(units):
                bk, bkey = self.bank()
                for kc in range(8):
                    self.mm(bk[0:C, 0:256], hT[:, kc, col0:col0 + C], wgk_(kc, 256, 512), kc == 0, kc == 7, (wkk, K("hT", kc)), (bkey,))
                self.cp("act", kb_tok[0:C, :], bk[0:C, 0:256], (bkey,), (K("kb_tok"),))
                bk, bkey = self.bank()
                for kc in range(8):
                    self.mm(bk[0:C, :], hT[:, kc, col0:col0 + C], wgv_(kc), kc == 0, kc == 7, (wkv, K("hT", kc)), (bkey,))
                self.cp("act", vb_tok[0:C, :, :].rearrange("p h e -> p (h e)"), bk[0:C, :], (bkey,), (K("vb_tok"),))
                if not pre:
                    bk, bkey = self.bank()
                    for kc in range(8):
                        self.mm(bk[0:C, :], hT[:, kc, col0:col0 + C], wgg_(kc), kc == 0, kc == 7, (wkg, K("hT", kc)), (bkey,))
                    self.cp("act", gb2[0:C, :], bk[0:C, :], (bkey,), (K("gb2"),))
                S_t, S_b, Skey, before_fn, after_fn = S_of(ui)
                if before_fn is not None:
                    before_fn()

                def out_cb(obk, obkey, col0=col0, C=C, ui=ui):
                    gla_out(C, obk, obkey)
                    for c in range(4):
                        self.tr(pbf[:, 512 + c * 128:512 + c * 128 + C], yb_tok[0:C, c * 128:(c + 1) * 128], ident_b[0:C, 0:C],
                                (K("yb_tok", c), K("ident_b")), (K("pbf2"),))
                    for c in range(4):
                        self.cp("act", ybT(c)[:, col0:col0 + C], pbf[:, 512 + c * 128:512 + c * 128 + C], (K("pbf2"),), (YBK[c],))
                gla_block(C, col0, S_t, S_b, Skey, out_cb, pre)
                if after_fn is not None:
                    after_fn()

        def attn_tile(tglob, tl):
            for kb in range(5):
                slot = (tglob - 4 + kb) % 8
                for hg in range(2):
                    bk, bkey = self.bank()
                    for hh in range(4):
                        h = hg * 4 + hh
                        tab = {0: tab_mask[:, :], 3: tab_prev[:, h, :], 4: tab_own[:, h, :]}.get(kb)
                        self.mm(bk[:, hh * 128:(hh + 1) * 128], kaT[slot][:, h, :], qaT[:, h, tl * 128:(tl + 1) * 128],
                                True, tab is None, (K("kaT", slot, h), K("qaT", h)), (bkey,))
                        if tab is not None:
                            self.mm(bk[:, hh * 128:(hh + 1) * 128], ident_b[:], tab, False, True, (K("ident_b"),) + tabkeys, (bkey,))
                    self.act(pT[:, kb, hg * 4:(hg + 1) * 4, :], bk[:].rearrange("p (h q) -> p h q", h=4), AF.Exp, (bkey,), (K("pT", kb, hg),))
            for hg in range(2):
                ob, obkey = obank[hg], K("obank", hg)
                for hh in range(4):
                    h = hg * 4 + hh
                    for kb in range(5):
                        slot = (tglob - 4 + kb) % 8
                        self.mm(ob[:, hh * 65:(hh + 1) * 65], pT[:, kb, h, :], vaug[slot][:, h, :], kb == 0, kb == 4,
                                (K("pT", kb, hg), K("vaug", slot), K("vaug1", slot)), (obkey,))
            for hg in range(2):
                ob, obkey = obank[hg], K("obank", hg)
                ov = ob[:, 0:260].rearrange("p (h e) -> p h e", h=4)
                self.ts("dve", rden[:, hg * 4:(hg + 1) * 4], ov[:, :, 64], 1e-30, None, ALU.add, None, (obkey,), (K("rden", hg),))
                S.add("dve", lambda e, hg=hg: e.reciprocal(rden[:, hg * 4:(hg + 1) * 4], rden[:, hg * 4:(hg + 1) * 4]), (K("rden", hg),), (K("rden", hg),))
                for hh in range(4):
                    h = hg * 4 + hh
                    self.ts("dve", ya_tok[:, h * 64:(h + 1) * 64], ov[:, hh, 0:64], rden[:, h:h + 1], None, ALU.mult, None,
                            (obkey, K("rden", hg)), (K("ya_tok", h),))
            yk = tuple(K("ya_tok", h) for h in range(8))
            for c in range(4):
                self.tr(pbf[:, c * 128:(c + 1) * 128], ya_tok[:, c * 128:(c + 1) * 128], ident_b[:], yk + (K("ident_b"),), (K("pbf1"),))
            for c in range(4):
                self.cp("act", yaT(c)[:, tl * 128:(tl + 1) * 128], pbf[:, c * 128:(c + 1) * 128], (K("pbf1"),), (YAK[c],))

        def attn_sample(s, slot):
            for rb in range(4):
                xb = xin[rb % 2]
                kx = K("xin", rb % 2)
                self.dma("sp", xb[:, 0:512], ck[s, rb * 128:(rb + 1) * 128, :], (), (kx,), ("xin", rb % 2))
                for hg in range(2):
                    bk, bkey = self.bank()
                    for hh in range(4):
                        h = hg * 4 + hh
                        self.tr(bk[0:64, hh * 128:(hh + 1) * 128], xb[:, h * 64:(h + 1) * 64], ident_f[:], (kx, K("ident_f")), (bkey,))
                    for hh in range(4):
                        h = hg * 4 + hh
                        self.cp("act", kcT(h)[:, rb * 128:(rb + 1) * 128], bk[0:64, hh * 128:(hh + 1) * 128], (bkey,), (K("m2", h),))
                self.dma("sp", xb[:, 512:1024], cv[s, rb * 128:(rb + 1) * 128, :], (), (K("xinv", rb % 2),), ("xinv", rb % 2))
                self.cp("dve", vcaug[:, rb, :, 0:64], xb[:, 512:1024].rearrange("p (h e) -> p h e", h=8), (K("xinv", rb % 2),), (K("vcaug", rb),))
                self.cp("dve", vcaug[:, rb, :, 64], ones_f[:, 0:8], (K("ones_f"),), (K("vcaug1", rb),))
            q0 = s * 32
            for kb in range(5):
                kn = 128 if kb < 4 else 32
                bk, bkey = self.bank()
                for h in range(8):
                    if kb < 4:
                        lhs = kcT(h)[:, kb * 128:(kb + 1) * 128]
                        rk = (K("m2", h),)
                    else:
                        lhs = kaT[slot][:, h, q0:q0 + 32]
                        rk = (K("kaT", slot, h),)
                    tab = {3: tab_prev[:, h, 0:32], 4: tab_own[0:32, h, 0:32]}.get(kb)
                    self.mm(bk[0:kn, h * 32:(h + 1) * 32], lhs, qaT[:, h, q0:q0 + 32], True, tab is None, rk + (K("qaT", h),), (bkey,))
                    if tab is not None:
                        self.mm(bk[0:kn, h * 32:(h + 1) * 32], ident_b[0:kn, 0:kn], tab, False, True, (K("ident_b"),) + tabkeys, (bkey,))
                self.act(pT[0:kn, kb, :, 0:32], bk[0:kn, 0:256].rearrange("p (h q) -> p h q", h=8), AF.Exp, (bkey,), (K("pT", kb, 0), K("pT", kb, 1)))
            for h in range(8):
                hg, hh = h // 4, h % 4
                for kb in range(5):
                    kn = 128 if kb < 4 else 32
                    rhs = vcaug[:, kb, h, :] if kb < 4 else vown[0:32, h, :]
                    rk = (K("vcaug", kb), K("vcaug1", kb)) if kb < 4 else (K("vown"), K("vown1"))
                    self.mm(obank[hg][0:32, hh * 65:(hh + 1) * 65], pT[0:kn, kb, h, 0:32], rhs, kb == 0, kb == 4,
                            (K("pT", kb, hg),) + rk, (K("obank", hg),))
            for hg in range(2):
                ob, obkey = obank[hg], K("obank", hg)
                ov = ob[0:32, 0:260].rearrange("p (h e) -> p h e", h=4)
                S.add("dve", lambda e, ov=ov, hg=hg: e.reciprocal(rden[0:32, hg * 4:(hg + 1) * 4], ov[:, :, 64]), (obkey,), (K("rden", hg),))
                for hh in range(4):
                    h = hg * 4 + hh
                    self.ts("dve", ya_tok[0:32, h * 64:(h + 1) * 64], ov[:, hh, 0:64], rden[0:32, h:h + 1], None, ALU.mult, None,
                            (obkey, K("rden", hg)), (K("ya_tok", h),))
            yk = tuple(K("ya_tok", h) for h in range(8))
            for c in range(4):
                self.tr(pbf[:, c * 128:c * 128 + 32], ya_tok[0:32, c * 128:(c + 1) * 128], ident_b[0:32, 0:32], yk + (K("ident_b"),), (K("pbf1"),))
            for c in range(4):
                self.cp("act", yaT(c)[:, q0:q0 + 32], pbf[:, c * 128:c * 128 + 32], (K("pbf1"),), (YAK[c],))

        def resid(n, ntile, Gfn, gname):
            for kc in range(8):
                self.tt("dve", tmpf2[:, 0:n], m2[:, kc, 0:n], rbc[:, 0:n], ALU.mult, (K("m2", kc), K("rbc")), (K("tmpf2"),))
                xk = tuple(K("xT", kc, t) for t in range(ntile))
                self.stt("dve", xT[:, kc, 0:n], tmpf2[:, 0:n], Gfn(kc), xT[:, kc, 0:n], ALU.mult, ALU.add,
                         (K("tmpf2"),) + modkeys[gname] + xk, xk)

        def resid_seq(Gt, gname):
            v3 = lambda ap: ap.rearrange("p (s w) -> p s w", s=4)
            for kc in range(8):
                xk = (K("xT", kc, 0),)
                self.tt("dve", tmpf2[:, 0:128], m2[:, kc, 0:128], rbc[:, 0:128], ALU.mult, (K("m2", kc), K("rbc")), (K("tmpf2"),))
                self.tt("dve", v3(tmpf2[:, 0:128]), v3(tmpf2[:, 0:128]), Gt[:, kc, 1:5].to_broadcast([128, 4, 32]), ALU.mult,
                        (K("tmpf2"),) + modkeys[gname], (K("tmpf2"),))
                self.tt("dve", xT[:, kc, 0:128], xT[:, kc, 0:128], tmpf2[:, 0:128], ALU.add, (K("tmpf2"),) + xk, xk)

        def merge_and_out(n):
            def ev_gate(dst, dkey):
                def f(bk, bkey):
                    self.act(dst[:, 0:n], bk[:, 0:n], AF.Tanh, (bkey,), (dkey,), scale=0.5)
                    self.ts("dve", dst[:, 0:n], dst[:, 0:n], 0.5, 0.5, ALU.mult, ALU.add, (dkey,), (dkey,))
                return f
            for G in range(2):
                wga, wka = self.wload(win_g[G_GA + G], kwin[G_GA + G], 8, 512)
                wba, wkba = self.wload(wbra_g[G], kwbra[G], 4, 512)
                for j in range(4):
                    c = G * 4 + j
                    bk, bkey = self.bank()
                    for kc in range(8):
                        self.mm(bk[:, 0:n], wga(kc, j * 128, (j + 1) * 128), hT[:, kc, 0:n], kc == 0, kc == 7, (wka, K("hT", kc)), (bkey,))
                    ev_gate(sgA, K("sgA"))(bk, bkey)
                    bk, bkey = self.bank()
                    for kc in range(4):
                        self.mm(bk[:, 0:n], wba(kc, j * 128, (j + 1) * 128), yaT(kc)[:, 0:n], kc == 0, kc == 3, (wkba, YAK[kc]), (bkey,))
                    self.tt("dve", m2[:, c, 0:n], sgA[:, 0:n], bk[:, 0:n], ALU.mult, (K("sgA"), bkey), (K("m2", c),))
                wgb, wkb = self.wload(win_g[G_GBR + G], kwin[G_GBR + G], 8, 512)
                wbb, wkbb = self.wload(wbrb_g[G], kwbrb[G], 4, 512)
                for j in range(4):
                    c = G * 4 + j
                    bk, bkey = self.bank()
                    for kc in range(8):
                        self.mm(bk[:, 0:n], wgb(kc, j * 128, (j + 1) * 128), hT[:, kc, 0:n], kc == 0, kc == 7, (wkb, K("hT", kc)), (bkey,))
                    ev_gate(sgB, K("sgB"))(bk, bkey)
                    bk, bkey = self.bank()
                    for kc in range(4):
                        self.mm(bk[:, 0:n], wbb(kc, j * 128, (j + 1) * 128), ybT(kc)[:, 0:n], kc == 0, kc == 3, (wkbb, YBK[kc]), (bkey,))
                    self.tt("dve", sgB[:, 0:n], sgB[:, 0:n], bk[:, 0:n], ALU.mult, (K("sgB"), bkey), (K("sgB"),))
                    self.tt("dve", mergedT(c)[:, 0:n], sgB[:, 0:n], m2[:, c, 0:n], ALU.add, (K("sgB"), K("m2", c)), (MK(c),))
            for G in range(2):
                wg, wk = self.wload(wout_g[G], kwout[G], 8, 512)
                for j in range(4):
                    c = G * 4 + j
                    bk, bkey = self.bank()
                    for kc in range(8):
                        self.mm(bk[:, 0:n], wg(kc, j * 128, (j + 1) * 128), mergedT(kc)[:, 0:n], kc == 0, kc == 7, (wk, MK(kc)), (bkey,))
                    self.cp("act", m2[:, c, 0:n], bk[:, 0:n], (bkey,), (K("m2", c),))
            rms_bcast(m2, n, lambda kc: (K("m2", kc),))

        def ffn(n, nseg, hview, hkeyf, extra=()):
            w = n // nseg
            v3 = lambda ap: ap.rearrange("p (s w) -> p s w", s=nseg)
            for g in range(11):
                wg, wk = self.wload(wup_g[g], kwup[g], 8, 512)
                for pair in range(2):
                    ua_, y0_ = ua_sets[pair], y0_sets[pair]
                    for half in range(2):
                        jc = half * 2 + pair
                        jj = half * NFF + 2 * g + pair
                        bk, bkey = self.bank()
                        for kc in range(8):
                            self.mm(bk[:, 0:n], wg(kc, jc * 128, (jc + 1) * 128), hT[:, kc, 0:n], kc == 0, kc == 7, (wk, K("hT", kc)), (bkey,))
                        u3 = ua_[half][:, 0:nseg * (w + 2)].rearrange("p (s w) -> p s w", s=nseg)
                        uk, uh, yk = K("ua", pair, half), K("uah", pair, half), K("y0", pair, half)
                        yv = v3(y0_[half][:, 0:n])
                        self.cp("pool", u3[:, :, 0:2], hview(jj), (hkeyf(jj),) + extra, (uh,))
                        self.cp("act", u3[:, :, 2:2 + w], v3(bk[:, 0:n]), (bkey,), (uk,))
                        self.act(y0_[half][:, 0:n], bk[:, 0:n], AF.Identity, (bkey, K("wdwT2"), K("bdwT")), (yk,),
                                 bias=bdwT[:, jj:jj + 1], scale=wdwT[:, 2, jj:jj + 1])
                        self.stt("dve", yv, u3[:, :, 1:1 + w], wdwT[:, 1, jj:jj + 1], yv, ALU.mult, ALU.add, (uk, uh, yk, K("wdwT1")), (yk,))
                        self.stt("dve", yv, u3[:, :, 0:w], wdwT[:, 0, jj:jj + 1], yv, ALU.mult, ALU.add, (uk, uh, yk, K("wdwT0")), (yk,))
                        self.cp("pool", hview(jj), u3[:, :, w:w + 2], (uk,), (hkeyf(jj),))
                    j = 2 * g + pair
                    self.act(y0_[0][:, 0:n], y0_[0][:, 0:n], AF.Gelu_apprx_tanh, (K("y0", pair, 0),), (K("y0", pair, 0),))
                    self.tt("dve", actT[:, j, 0:n], y0_[0][:, 0:n], y0_[1][:, 0:n], ALU.mult, (K("y0", pair, 0), K("y0", pair, 1)), (K("actT", j),))
            for c in range(8):
                wg, wk = self.wload(wdown_g[c], kwdown[c], NFF, 128)
                bk, bkey = self.bank()
                for j in range(NFF):
                    self.mm(bk[:, 0:n], wg(j), actT[:, j, 0:n], j == 0, j == NFF - 1, (wk, K("actT", j)), (bkey,))
                self.cp("act", m2[:, c, 0:n], bk[:, 0:n], (bkey,), (K("m2", c),))
            rms_bcast(m2, n, lambda kc: (K("m2", kc),))

        def store_y(dst_rows, ntile):
            for t in range(ntile):
                yb_ = xin[t % 2]
                ky = K("xin", t % 2)
                for half in range(2):
                    bk, bkey = self.bank()
                    for j in range(4):
                        kc = half * 4 + j
                        self.tr(bk[:, j * 128:(j + 1) * 128], xT[:, kc, t * 128:(t + 1) * 128], ident_f[:], (K("xT", kc, t), K("ident_f")), (bkey,))
                    self.cp("act", yb_[:, half * 512:(half + 1) * 512], bk[:], (bkey,), (K("xinv", t % 2),) if half else (ky,))
                self.dma("sp", dst_rows[t * 128:(t + 1) * 128, :], yb_[:], (ky, K("xinv", t % 2)), (), ("xin", t % 2))

        def store_hist(hsrc, hkeys, dst):
            self.cp("dve", h88[:], hsrc.rearrange("p j t -> p t j"), hkeys, (K("h88"),))
            bk, bkey = self.bank()
            self.tr(bk[0:88, 0:128], h88[:].rearrange("p t j -> p (t j)"), ident_f[:], (K("h88"), K("ident_f")), (bkey,))
            self.cp("act", stage[0:88, :], bk[0:88, 0:128], (bkey,), (K("stage"),))
            self.dma("sp", dst, stage[0:88, :], (K("stage"),), (), ("stage_o",))

        def kv_stage(bk, bkey, which, dst):
            st = xin[1][:, which * 512:(which + 1) * 512]
            sk = K("xin", 1) if which == 0 else K("xinv", 1)
            self.cp("act", st, bk[:], (bkey,), (sk,))
            self.dma("sp", dst, st, (sk,), (), ("xin", 1) if which == 0 else ("xinv", 1))

        LAST_KV0 = 16 + NMAIN - 4

        def common_proj(n, units, tglob0, kv_from, pre, flagged, kv_out):
            if not pre:
                wg, wk = self.wload(win_g[G_QA], kwin[G_QA], 8, 512)
                def ev_q(j, bk, bkey):
                    self.act(qaT[:, j, 0:n], bk[0:64, 0:n], AF.Identity, (bkey,), (K("qaT", j),), scale=0.125)
                heads_fm(wg, wk, 0, 8, n, ev_q)
            if kv_from < len(units):
                wg, wk = self.wload(win_g[G_KA], kwin[G_KA], 8, 512)
                def ev_k(j, bk, bkey):
                    for ui in range(kv_from, len(units)):
                        c0, C = units[ui]
                        if C == 128:
                            slot = (tglob0 + ui) % 8
                            self.cp("act", kaT[slot][:, j, :], bk[0:64, c0:c0 + 128], (bkey,), (K("kaT", slot, j),))
                    if units[0][1] != 128:
                        self.cp("act", kaT[tglob0 % 8][:, j, :], bk[0:64, 0:128], (bkey,), (K("kaT", tglob0 % 8, j),))
                heads_fm(wg, wk, 0, 8, n, ev_k)
                kunits = [(ui, units[ui]) for ui in range(kv_from, len(units)) if kv_out(ui, 0) is not None] if units[0][1] == 128 else []
                if kunits:
                    units_tm(wg, wk, 0, 512, [u for _, u in kunits], lambda i, bk, bkey: kv_stage(bk, bkey, 0, kv_out(kunits[i][0], 0)))
                if units[0][1] != 128:
                    units_tm(wg, wk, 0, 512, [(0, 128)], lambda i, bk, bkey: kv_stage(bk, bkey, 0, kv_out(0, 0)))
                wg, wk = self.wload(win_g[G_VA], kwin[G_VA], 8, 512)
                if units[0][1] == 128:
                    def ev_v(i, bk, bkey):
                        ui = kv_from + i
                        slot = (tglob0 + ui) % 8
                        self.cp("act", vaug[slot][:, :, 0:64], bk[:].rearrange("p (h e) -> p h e", h=8), (bkey,), (K("vaug", slot),))
                        if flagged:
                            self.ts("dve", vaug[slot][:, :, 64], ones_f[:, 0:8], flag[:, 0:1], None, ALU.mult, None,
                                    (K("ones_f"), K("flag")), (K("vaug1", slot),))
                        else:
                            self.cp("dve", vaug[slot][:, :, 64], ones_f[:, 0:8], (K("ones_f"),), (K("vaug1", slot),))
                        if kv_out(ui, 1) is not None:
                            kv_stage(bk, bkey, 1, kv_out(ui, 1))
                    units_tm(wg, wk, 0, 512, units[kv_from:], ev_v)
                else:
                    units_tm(wg, wk, 0, 512, [(0, 128)], lambda i, bk, bkey: kv_stage(bk, bkey, 1, kv_out(0, 1)))
            if not pre:
                wg, wk = self.wload(win_g[G_QKB], kwin[G_QKB], 8, 512)
                def ev_qb(j, bk, bkey):
                    self.cp("act", qbT[:, j, 0:n], bk[0:64, 0:n], (bkey,), (K("qbT", j),))
                def ev_kb(j, bk, bkey):
                    self.cp("act", kbT[:, j, 0:n], bk[0:64, 0:n], (bkey,), (K("kbT", j),))
                heads_fm(wg, wk, 0, 4, n, ev_qb)
                heads_fm(wg, wk, 256, 4, n, ev_kb)
            wg, wk = self.wload(wgk_g, kwgk, 8, 16)
            bk, bkey = self.bank()
            for kc in range(8):
                self.mm(bk[0:16, 0:n], wg(kc), hT[:, kc, 0:n], kc == 0, kc == 7, (wk, K("hT", kc)), (bkey,))
            self.cp("act", gkT[0:16, 0:n], bk[0:16, 0:n], (bkey,), (K("gkT"),))

        SK = K("Sst")
        def prompt_block(src_rows, ntile, tglob0, mode, dst_rows, kv_from):
            n = ntile * 128
            pre = mode == "prefix"
            flagged = mode != "full"
            units = [(t * 128, 128) for t in range(ntile)]
            mark = lambda nm: self.marks.append((mode, tglob0, nm, len(S.ops)))
            mark("start")
            load_xT(src_rows, ntile)
            rms_bcast(xT, n, xkeys(ntile))
            if flagged:
                modulate(n, ntile, "AmP", "BmP", lambda kc: AmP[:, kc:kc + 1], lambda kc: BmP[:, kc:kc + 1])
            else:
                modulate(n, ntile, "Am", "Bm", lambda kc: Am[:, kc, 0:1], lambda kc: Bm[:, kc, 0:1])
            mark("norm1")

            def kv_out(ui, which):
                tg = tglob0 + ui
                if mode != "full" or tg < LAST_KV0:
                    return None
                r0 = (tg - LAST_KV0) * 128
                return (kp_o if which == 0 else vp_o)[r0:r0 + 128, :]
            common_proj(n, units, tglob0, kv_from, pre, flagged, kv_out)
            mark("proj")
            if pre:
                gla_units(units, pre, lambda ui: (Sst, Sbf, SK, None, None))
                mark("gla")
                return
            def attn_all():
                for t in range(ntile):
                    attn_tile(tglob0 + t, t)
            gla_units(units, pre, lambda ui: (Sst, Sbf, SK, None, None), attn_all)
            mark("attn")
            merge_and_out(n)
            resid(n, ntile, lambda kc: Gm[:, kc, 0:1], "Gm")
            mark("merge")
            rms_bcast(xT, n, xkeys(ntile))
            if flagged:
                modulate(n, ntile, "AfP", "BfP", lambda kc: AfP[:, kc:kc + 1], lambda kc: BfP[:, kc:kc + 1])
            else:
                modulate(n, ntile, "Af", "Bf", lambda kc: Af[:, kc, 0:1], lambda kc: Bf[:, kc, 0:1])
            ffn(n, 1, lambda jj: hist[:, jj:jj + 1, :], lambda jj: K("hist", jj))
            mark("ffn")
            resid(n, ntile, lambda kc: Gf[:, kc, 0:1], "Gf")
            store_y(dst_rows, ntile)
            mark("end")

        def sample_block():
            n = 128
            slot = 0
            units = [(s * 32, 32) for s in range(4)]
            self.marks.append(("sample", 0, "start", len(S.ops)))
            load_xT(xsam, 1)
            rms_bcast(xT, n, xkeys(1))
            modulate_seq(Am, Bm, "Am", "Bm")
            common_proj(n, units, slot, 0, False, False, lambda ui, which: (ks_o if which == 0 else vs_o))
            SSK = K("Sst")
            def S_of(ui):
                def before():
                    self.dma("sp", Ss[:], sgla[ui].rearrange("h k v -> k h v"), (), tuple(SSK + (h,) for h in range(4)), ("Ss",))
                    self.cp("act", Ssb[:], Ss[:], tuple(SSK + (h,) for h in range(4)), (SSK + ("b",),))
                def after():
                    self.dma("sp", glas_o[ui].rearrange("h k v -> k h v"), Ss[:], tuple(SSK + (h,) for h in range(4)), (), ("Ss",))
                return (Ss, Ssb, SSK, before, after)
            gla_units(units, False, S_of)
            wgv, wkv = self.wload(win_g[G_VA], kwin[G_VA], 8, 512)
            for s in range(4):
                bk, bkey = self.bank()
                for kc in range(8):
                    self.mm(bk[0:32, :], hT[:, kc, s * 32:(s + 1) * 32], wgv(kc), kc == 0, kc == 7, (wkv, K("hT", kc)), (bkey,))
                self.cp("act", vown[:, :, 0:64], bk[0:32, :].rearrange("p (h e) -> p h e", h=8), (bkey,), (K("vown"),))
                self.cp("dve", vown[:, :, 64], ones_f[0:32, 0:8], (K("ones_f"),), (K("vown1"),))
                attn_sample(s, slot)
            merge_and_out(n)
            resid_seq(Gm, "Gm")
            rms_bcast(xT, n, xkeys(1))
            modulate_seq(Af, Bf, "Af", "Bf")
            for s in range(4):
                load_T(sconv[s], 88, hist_s[:, s, :, :].rearrange("p j t -> p t j"), "hist_s_in%d" % s,
                       view=lambda a: a.rearrange("p (t j) -> p t j", t=2))
            hs_in = tuple(K("hist_s_in%d" % s) for s in range(4))
            ffn(n, 4, lambda jj: hist_s[:, :, jj, :], lambda jj: K("hist_s", jj), hs_in)
            resid_seq(Gf, "Gf")
            store_y(ysam, 1)
            for s in range(4):
                store_hist(hist_s[:, s, :, :], tuple(K("hist_s", jj) for jj in range(44)), convs_o[s])

        STG = int(os.environ.get("KSTAGE", "99"))
        if STG <= 0:
            return S.finalize()
        t0 = 0
        while t0 < 11:
            nt = min(NB, 11 - t0)
            prompt_block(xpre[t0 * 128:(t0 + nt) * 128, :], nt, t0, "prefix", None, nt)
            t0 += nt
        while t0 < 15:
            nt = min(NB, 15 - t0)
            prompt_block(xpre[t0 * 128:(t0 + nt) * 128, :], nt, t0, "prefix", None, 0)
            t0 += nt
        ada_part2()
        if STG <= 2:
            return S.finalize()
        prompt_block(xov, 1, 15, "overlap", yov, 0)
        for jj in range(44):
            pass
        self.ts("dve", hist[:].rearrange("p j t -> p (j t)"), hist[:].rearrange("p j t -> p (j t)"), flag[:, 0:1], None, ALU.mult, None,
                tuple(K("hist", jj) for jj in range(44)) + (K("flag"),), tuple(K("hist", jj) for jj in range(44)))
        if STG <= 3:
            return S.finalize()
        for b in range(min(NMAIN // NB, int(os.environ.get('KMAIN', '99')))):
            prompt_block(xmain[b * NBT:(b + 1) * NBT, :], NB, 16 + b * NB, "full", ymain[b * NBT:(b + 1) * NBT, :], 0)
        self.dma("sp", glap_o.rearrange("h k v -> k h v"), Sst[:], tuple(SK + (h,) for h in range(4)), (), ("glap",))
        store_hist(hist[:], tuple(K("hist", jj) for jj in range(44)), convp_o)
        if STG <= 4:
            return S.finalize()
        sample_block()
        return S.finalize()


_CACHE = {}


def _program():
    if "nc" not in _CACHE:
        b = Builder()
        b.build()
        _CACHE["nc"] = b.nc
    return _CACHE["nc"]


def kernel(x_prompt, x_sample, cache_k_a, cache_v_a, state_gla, state_conv, c_prompt, c_sample,
           w_ada, b_ada, g_pre_mix, g_post_mix, g_pre_ffn, g_post_ffn, w_in, w_gk2, b_gk,
           rel_bias, g_gla, w_br_a, w_br_b, w_out, w_up, w_dw, b_dw, w_down):
    f = lambda a: np.ascontiguousarray(np.asarray(a, dtype=np.float32))
    x_prompt, x_sample = f(x_prompt), f(x_sample)
    rb = f(rel_bias)[0]
    kk = np.arange(128)[:, None]
    qq = np.arange(128)[None, :]
    idx_prev = np.clip(qq + 128 - kk, -128, 128) + 128
    idx_own = np.clip(qq - kk, -128, 128) + 128
    btab = np.stack([rb[:, idx_prev].transpose(1, 0, 2), rb[:, idx_own].transpose(1, 0, 2)])
    cvec = np.broadcast_to(rb[:, 256][None, :], (128, 8))
    wi = f(w_in)[0]
    wi = np.concatenate([wi[:, :3072], wi[:, 3088:], wi[:, 3072:3088]], axis=1)
    wu = f(w_up)[0]
    cols = []
    for g in range(11):
        cols += [wu[:, 2 * g * 128:(2 * g + 2) * 128], wu[:, DFF + 2 * g * 128:DFF + (2 * g + 2) * 128]]
    wu = np.concatenate(cols, axis=1)
    shared = {
        "w_ada": f(w_ada)[0], "b_ada": f(b_ada)[0].reshape(48, 128),
        "gvec": np.concatenate([f(g_pre_mix)[0], f(g_post_mix)[0], f(g_pre_ffn)[0], f(g_post_ffn)[0]]).reshape(32, 128),
        "w_in": f(wi), "w_gk2": f(w_gk2)[0], "b_gk": f(b_gk)[0].reshape(1, 256),
        "btab": f(btab), "cvec": f(cvec), "ggla": f(np.broadcast_to(np.tile(f(g_gla)[0], 4)[None, :], (128, 512))),
        "w_br_a": f(w_br_a)[0], "w_br_b": f(w_br_b)[0], "w_out": f(w_out)[0], "w_up": f(wu),
        "w_dw": f(w_dw)[0].reshape(3 * 44, 128), "b_dw": f(b_dw)[0].reshape(44, 128), "w_down": f(w_down)[0],
    }
    in_maps = []
    for c in range(8):
        b, hf = c // 2, c % 2
        m = dict(shared)
        if hf == 1:
            m["xpre"] = x_prompt[b, 0:NPRE * 128]
            m["xov"] = x_prompt[b, NPRE * 128:2048]
        else:
            m["xpre"] = np.zeros((NPRE * 128, D), np.float32)
            m["xov"] = np.zeros((128, D), np.float32)
        m["xmain"] = x_prompt[b, hf * 2048:(hf + 1) * 2048]
        m["xsam"] = x_sample[4 * c:4 * c + 4].reshape(128, D)
        m["crow"] = f(np.concatenate([f(c_prompt)[b:b + 1], f(c_sample)[4 * c:4 * c + 4]], 0).reshape(40, 128))
        m["flag"] = np.full((128, 1), float(hf), np.float32)
        m["ck"] = f(cache_k_a)[0, 4 * c:4 * c + 4].reshape(4, 512, 512)
        m["cv"] = f(cache_v_a)[0, 4 * c:4 * c + 4].reshape(4, 512, 512)
        m["sgla"] = f(state_gla)[0, 4 * c:4 * c + 4]
        m["sconv"] = f(state_conv)[0, 4 * c:4 * c + 4].reshape(4, 88, 128)
        in_maps.append({k: np.ascontiguousarray(v) for k, v in m.items()})
    nc = _program()
    cores = [int(t) for t in os.environ.get("KCORES", "0,1,2,3,4,5,6,7").split(",")]
    if os.environ.get("KTRACE"):
        res = run_bass_kernel_spmd(nc, [in_maps[c] for c in cores], core_ids=list(range(len(cores))), trace=True)
        print("EXEC_TIME_NS", res.exec_time_ns)
    else:
        res = run_bass_kernel_spmd(nc, [in_maps[c] for c in cores], core_ids=list(range(len(cores))))
    R = {c: res.results[i] for i, c in enumerate(cores)}
    y_prompt = np.zeros((4, 4096, D), np.float32)
    y_sample = np.zeros((32, 32, D), np.float32)
    k_p = np.zeros((1, 4, 512, 8, 64), np.float32); v_p = np.zeros_like(k_p)
    gla_p = np.zeros((1, 4, 4, 64, 128), np.float32)
    conv_p = np.zeros((1, 4, 2, 2 * DFF), np.float32)
    k_s = np.zeros((1, 32, 32, 8, 64), np.float32); v_s = np.zeros_like(k_s)
    gla_s = np.zeros((1, 32, 4, 64, 128), np.float32)
    conv_s = np.zeros((1, 32, 2, 2 * DFF), np.float32)
    for c in cores:
        b, hf = c // 2, c % 2
        r = R[c]
        y_prompt[b, hf * 2048:(hf + 1) * 2048] = r["ymain"]
        y_sample[4 * c:4 * c + 4] = r["ysam"].reshape(4, 32, D)
        if hf == 1:
            k_p[0, b] = r["kp"].reshape(512, 8, 64)
            v_p[0, b] = r["vp"].reshape(512, 8, 64)
            gla_p[0, b] = r["glap"]
            conv_p[0, b] = r["convp"].reshape(2, 2 * DFF)
        k_s[0, 4 * c:4 * c + 4] = r["ks"].reshape(4, 32, 8, 64)
        v_s[0, 4 * c:4 * c + 4] = r["vs"].reshape(4, 32, 8, 64)
        gla_s[0, 4 * c:4 * c + 4] = r["glas"]
        conv_s[0, 4 * c:4 * c + 4] = r["convs"].reshape(4, 2, 2 * DFF)
    return (y_prompt, y_sample, k_p, v_p, gla_p, conv_p, k_s, v_s, gla_s, conv_s)
```

```python
from contextlib import ExitStack
import os

import numpy as np
import concourse.bass as bass
import concourse.mybir as mybir
from concourse.bass_utils import run_bass_kernel_spmd

F32 = mybir.dt.float32
BF16 = mybir.dt.bfloat16
AF = mybir.ActivationFunctionType
ALU = mybir.AluOpType

D = 1024
KC = 8
DFF = 2816
NFF = 22
DIN = 5136
G_QA, G_KA, G_VA, G_QKB, G_VB, G_GB, G_GA, G_GBR = 0, 1, 2, 3, 4, 5, 6, 8
EPS = 1e-6
NEG = -30000.0
NPRE = 15
NMAIN = 16
NB = 4


class Op:
    __slots__ = ("eng", "fn", "reads", "writes", "dsem", "signal", "sigval", "deps")

    def __init__(self, eng, fn, reads, writes, dsem):
        self.eng, self.fn, self.reads, self.writes, self.dsem = eng, fn, reads, writes, dsem
        self.signal = False
        self.sigval = 0
        self.deps = ()


class Sched:
    DMA = ("sp", "pool_dma")

    def __init__(self, nc, stack):
        self.nc = nc
        self.stack = stack
        self.ops = []
        self.eng_obj = {"pe": nc.tensor, "act": nc.scalar, "dve": nc.vector, "pool": nc.gpsimd,
                        "sp": nc.sync, "pool_dma": nc.gpsimd}
        self.wuses = []
        self.wdepth = 2

    def add(self, eng, fn, reads=(), writes=(), dsem=None):
        op = Op(eng, fn, tuple(reads), tuple(writes), dsem)
        self.ops.append(op)
        return op

    def queue_of(self, op):
        return "pool" if op.eng == "pool_dma" else op.eng

    def finalize(self):
        nc = self.nc
        inserts = {}
        lastrd = {}
        for idx, op in enumerate(self.ops):
            for k in op.reads:
                if k and k[0] == "wuse":
                    lastrd[k[1]] = idx
        prev = 0
        for i, (pos, op) in enumerate(self.wuses):
            tgt = self.wuses[max(0, i - self.wdepth)][0]
            if i >= 3 and (i - 3) in lastrd:
                tgt = max(tgt, lastrd[i - 3] + 1)
            tgt = max(tgt, prev)
            prev = tgt
            assert tgt <= pos, (i, tgt, pos)
            inserts.setdefault(tgt, []).append(op)
        ops = []
        for i, op in enumerate(self.ops):
            if i in inserts:
                ops.extend(inserts[i])
            ops.append(op)
        self.ops = ops
        last_w = {}
        readers = {}
        for i, op in enumerate(ops):
            deps = set()
            q = self.queue_of(op)
            isdma = op.dsem is not None
            for k in op.reads:
                j = last_w.get(k)
                if j is not None:
                    deps.add(j)
            for k in op.writes:
                j = last_w.get(k)
                if j is not None:
                    oj = ops[j]
                    if isdma or oj.dsem is not None or self.queue_of(oj) != q or q != "pe":
                        deps.add(j)
                for j in readers.get(k, ()):
                    oj = ops[j]
                    if isdma or oj.dsem is not None or self.queue_of(oj) != q or (q != "pe" and os.environ.get("KWAR", "1") == "1"):
                        deps.add(j)
            deps.discard(i)
            op.deps = tuple(sorted(deps))
            for j in op.deps:
                ops[j].signal = True
            for k in op.reads:
                lst = readers.setdefault(k, [])
                if op.dsem is None:
                    lst[:] = [j for j in lst if ops[j].dsem is not None or self.queue_of(ops[j]) != q]
                lst.append(i)
            for k in op.writes:
                last_w[k] = i
                readers[k] = []
        esem = {}
        for e in ("pe", "act", "dve", "pool"):
            esem[e] = self.stack.enter_context(nc.semaphore("sem_" + e))
        dsems = {}
        cnt = {}
        for op in ops:
            if op.dsem is not None:
                if op.dsem not in dsems:
                    dsems[op.dsem] = self.stack.enter_context(nc.semaphore("dsem_%d" % len(dsems)))
                    cnt[op.dsem] = 0
                cnt[op.dsem] += 1
                op.sigval = 16 * cnt[op.dsem]
            elif op.signal:
                q = self.queue_of(op)
                cnt[q] = cnt.get(q, 0) + 1
                op.sigval = cnt[q]
        known = {q: {} for q in ("pe", "act", "dve", "pool", "sp")}
        for op in ops:
            q = self.queue_of(op)
            eng = self.eng_obj[op.eng]
            need = {}
            for j in op.deps:
                oj = ops[j]
                s = dsems[oj.dsem] if oj.dsem is not None else esem[self.queue_of(oj)]
                key = id(s)
                if key not in need or need[key][1] < oj.sigval:
                    need[key] = (s, oj.sigval)
            for key, (s, v) in need.items():
                if known[q].get(key, 0) >= v:
                    continue
                eng.wait_ge(s, v)
                known[q][key] = v
            ins = op.fn(eng)
            if op.dsem is not None:
                ins.then_inc(dsems[op.dsem], 16)
            elif op.signal:
                ins.then_inc(esem[q], 1)
        self.counts = dict((str(k), v) for k, v in cnt.items())
        for k, s in dsems.items():
            nc.sync.wait_ge(s, 16 * cnt[k])
        return len(ops)


class Builder:
    def __init__(self):
        self.stack = ExitStack()
        self.nc = bass.Bass("TRN2", target_bir_lowering=False)
        self.S = Sched(self.nc, self.stack)
        self.nbank = 0
        self.wuse_n = 0
        self.bank_set = [0, 1, 2, 3, 4]
        self.marks = []
        self.uid = 0

    def din(self, name, shape, dt=F32):
        return self.nc.dram_tensor(name, list(shape), dt, kind="ExternalInput").ap()

    def dout(self, name, shape):
        return self.nc.dram_tensor(name, list(shape), F32, kind="ExternalOutput").ap()

    def dscr(self, name, shape, dt=BF16):
        return self.nc.dram_tensor(name, list(shape), dt, kind="Internal").ap()

    def sb(self, name, shape, dt=F32):
        return self.stack.enter_context(self.nc.sbuf_tensor(name, list(shape), dt))

    def ps(self, name, shape, dt=F32):
        return self.stack.enter_context(self.nc.psum_tensor(name, list(shape), dt))

    def bank(self):
        bs = self.bank_set
        i = bs[self.nbank % len(bs)]
        self.nbank += 1
        return self.banks[i], ("ps", i)

    def key(self, name):
        self.uid += 1
        return (name, self.uid)

    def mm(self, out, lhsT, rhs, start, stop, reads, writes):
        self.S.add("pe", lambda e: e.matmul(out, lhsT, rhs, start=start, stop=stop), reads, writes)

    def tr(self, out, in_, ident, reads, writes):
        self.S.add("pe", lambda e: e.transpose(out, in_, ident), reads, writes)

    def act(self, out, in_, func, reads, writes, bias=None, scale=None, accum_out=None):
        kw = {}
        if bias is not None:
            kw["bias"] = bias
        if scale is not None:
            kw["scale"] = scale
        if accum_out is not None:
            kw["accum_out"] = accum_out
        self.S.add("act", lambda e: e.activation(out, in_, func, **kw), reads, writes)

    def tt(self, eng, out, in0, in1, op, reads, writes):
        self.S.add(eng, lambda e: e.tensor_tensor(out, in0, in1, op), reads, writes)

    def ts(self, eng, out, in0, s1, s2, op0, op1, reads, writes):
        if s2 is None:
            self.S.add(eng, lambda e: e.tensor_scalar(out, in0, s1, None, op0), reads, writes)
        else:
            self.S.add(eng, lambda e: e.tensor_scalar(out, in0, s1, s2, op0, op1), reads, writes)

    def stt(self, eng, out, in0, scalar, in1, op0, op1, reads, writes):
        self.S.add(eng, lambda e: e.scalar_tensor_tensor(out, in0, scalar, in1, op0=op0, op1=op1), reads, writes)

    def cp(self, eng, out, in_, reads, writes):
        if eng == "act":
            self.S.add("act", lambda e: e.copy(out, in_), reads, writes)
        else:
            self.S.add(eng, lambda e: e.tensor_copy(out, in_), reads, writes)

    def memset(self, eng, ap, val, writes):
        self.S.add(eng, lambda e: e.memset(ap, val), (), writes)

    def dma(self, q, out, in_, reads, writes, dsem, slow=False):
        if slow:
            self.S.add(q, lambda e: e.dma_start(out=out, in_=in_, allow_slow_non_contiguous=True), reads, writes, dsem)
        else:
            self.S.add(q, lambda e: e.dma_start(out=out, in_=in_), reads, writes, dsem)

    def interleave(self, fns_banks):
        main = self.S.ops
        streams = []
        for fn, banks in fns_banks:
            self.S.ops = []
            self.bank_set = banks
            fn()
            streams.append(self.S.ops)
        self.S.ops = main
        self.bank_set = [0, 1, 2, 3, 4]
        idx = [0] * len(streams)
        total = sum(len(st) for st in streams)
        for _ in range(total):
            best, bf = None, None
            for si, st in enumerate(streams):
                if idx[si] < len(st):
                    frac = idx[si] / len(st)
                    if bf is None or frac < bf:
                        best, bf = si, frac
            main.append(streams[best][idx[best]])
            idx[best] += 1

    def wload(self, scr_g, wkeys, kcn, width):
        i = self.wuse_n
        self.wuse_n += 1
        slot = i % 3
        buf = self.wbufs[slot]
        key = ("wuse", i)
        dst = buf[:, 0:kcn * width].rearrange("p (kc n) -> p kc n", kc=kcn)
        op = Op("sp", lambda e: e.dma_start(out=dst, in_=scr_g), tuple(wkeys),
                (key, ("wbuf", slot)) + ((("wuse", i - 3),) if i >= 3 else ()), ("wstream", slot))
        self.S.wuses.append((len(self.S.ops), op))
        return (lambda kc, a=0, b=width: buf[:, kc * width + a: kc * width + b]), key

    def build(self):
        nc = self.nc
        S = self.S
        sb, ps = self.sb, self.ps
        NBT = NB * 128
        K = lambda *a: tuple(a)
        xpre = self.din("xpre", [NPRE * 128, D])
        xov = self.din("xov", [128, D])
        xmain = self.din("xmain", [NMAIN * 128, D])
        xsam = self.din("xsam", [128, D])
        crow = self.din("crow", [40, 128])
        flag_d = self.din("flag", [128, 1])
        ck = self.din("ck", [4, 512, 512])
        cv = self.din("cv", [4, 512, 512])
        sgla = self.din("sgla", [4, 4, 64, 128])
        sconv = self.din("sconv", [4, 88, 128])
        w_ada = self.din("w_ada", [D, 6 * D])
        b_ada = self.din("b_ada", [48, 128])
        gvec = self.din("gvec", [32, 128])
        w_in = self.din("w_in", [D, DIN])
        w_gk2 = self.din("w_gk2", [16, 256])
        b_gk = self.din("b_gk", [1, 256])
        btab = self.din("btab", [2, 128, 8, 128])
        cvec = self.din("cvec", [128, 8])
        ggla = self.din("ggla", [128, 512])
        w_br_a = self.din("w_br_a", [512, D])
        w_br_b = self.din("w_br_b", [512, D])
        w_out = self.din("w_out", [D, D])
        w_up = self.din("w_up", [D, 2 * DFF])
        w_dw = self.din("w_dw", [3 * 44, 128])
        b_dw = self.din("b_dw", [44, 128])
        w_down = self.din("w_down", [DFF, D])

        ymain = self.dout("ymain", [NMAIN * 128, D])
        ysam = self.dout("ysam", [128, D])
        yov = self.dout("yov", [128, D])
        kp_o = self.dout("kp", [512, 512])
        vp_o = self.dout("vp", [512, 512])
        glap_o = self.dout("glap", [4, 64, 128])
        convp_o = self.dout("convp", [88, 128])
        ks_o = self.dout("ks", [128, 512])
        vs_o = self.dout("vs", [128, 512])
        glas_o = self.dout("glas", [4, 4, 64, 128])
        convs_o = self.dout("convs", [4, 88, 128])

        win_g = self.dscr("win_g", [10, 128, 8, 512])
        wgk_g = self.dscr("wgk_g", [128, 8, 16])
        wbra_g = self.dscr("wbra_g", [2, 128, 4, 512])
        wbrb_g = self.dscr("wbrb_g", [2, 128, 4, 512])
        wout_g = self.dscr("wout_g", [2, 128, 8, 512])
        wup_g = self.dscr("wup_g", [11, 128, 8, 512])
        wdown_g = self.dscr("wdown_g", [8, 128, NFF, 128])

        self.banks = [ps("bank%d" % i, [128, 512]) for i in range(5)]
        obank = [ps("obank%d" % i, [128, 512]) for i in range(2)]
        pbf = ps("pbf", [128, 1024], BF16)

        self.wbufs = [sb("wbuf%d" % i, [128, 4096], BF16) for i in range(3)]
        ident_f = sb("ident_f", [128, 128])
        ident_b = sb("ident_b", [128, 128], BF16)
        ones_b = sb("ones_b", [128, 128], BF16)
        triu_f = sb("triu_f", [128, 128])
        trisl_f = sb("trisl_f", [128, 128])
        ones_f = sb("ones_f", [128, 8])
        epsb = sb("epsb", [128, 1])
        mod = sb("mod", [128, 6, KC, 5])
        Am = sb("Am", [128, KC, 5]); Bm = sb("Bm", [128, KC, 5]); Gm = sb("Gm", [128, KC, 5])
        Af = sb("Af", [128, KC, 5]); Bf = sb("Bf", [128, KC, 5]); Gf = sb("Gf", [128, KC, 5])
        AmP = sb("AmP", [128, KC]); BmP = sb("BmP", [128, KC])
        AfP = sb("AfP", [128, KC]); BfP = sb("BfP", [128, KC])
        gT = sb("gT", [128, 32])
        badaT = sb("badaT", [128, 48])
        cT = sb("cT", [128, 40])
        siluT = sb("siluT", [128, 40], BF16)
        flag = sb("flag_s", [128, 1])
        wdwT = sb("wdwT", [128, 3, 44])
        bdwT = sb("bdwT", [128, 44])
        wgk_f = sb("wgk_f", [17, 256])
        wgk_b = sb("wgk_b", [17, 256], BF16)
        tab_prev = sb("tab_prev", [128, 8, 128], BF16)
        tab_own = sb("tab_own", [128, 8, 128], BF16)
        tab_mask = sb("tab_mask", [128, 128], BF16)
        cvec_s = sb("cvec_s", [128, 8])
        ggla_s = sb("ggla_s", [128, 512])
        stage = sb("stage", [128, 128])
        hist = sb("hist", [128, 44, 2])
        hist_s = sb("hist_s", [128, 4, 44, 2])
        h88 = sb("h88", [128, 2, 44])
        xin = [sb("xin%d" % i, [128, D]) for i in range(2)]
        xT = sb("xT", [128, KC, NBT])
        hT = sb("hT", [128, KC, NBT], BF16)
        sq = sb("sq", [128, 2, NBT], BF16)
        rbc = sb("rbc", [128, NBT])
        tmpf = sb("tmpf", [128, NBT])
        tmpf2 = sb("tmpf2", [128, NBT])
        qaT = sb("qaT", [128, 4, NBT], BF16)
        kaT = [sb("kaT%d" % i, [128, 4, 128], BF16) for i in range(8)]
        vaug = [sb("vaug%d" % i, [128, 8, 65], BF16) for i in range(8)]
        qbT = sb("qbT", [64, 4, NBT], BF16)
        kbT = sb("kbT", [64, 4, NBT], BF16)
        gkT = sb("gkT", [32, NBT], BF16)
        pT_raw = sb("pT_raw", [128, 2560])
        pT = pT_raw.bitcast(BF16)[:, :].rearrange("p (k h q) -> p k h q", k=5, h=8)
        ya_tok = sb("ya_tok", [128, 512], BF16)
        rden = sb("rden", [128, 8])
        kb_tok = sb("kb_tok", [128, 256])
        vb_tok = sb("vb_tok", [128, 4, 128], BF16)
        gb2 = sb("gb2", [128, 512])
        gtanh = sb("gtanh", [128, 512])
        Lsp = sb("Lsp", [128, 256])
        e_sb = sb("e_sb", [128, 256])
        e1 = sb("e1", [64, 4, 128]); e2 = sb("e2", [64, 4, 128])
        qtT = sb("qtT", [64, 4, 128], BF16); ktT = sb("ktT", [64, 4, 128], BF16)
        kend = sb("kend", [128, 256], BF16)
        dec = sb("dec", [64, 4])
        attT = sb("attT", [128, 4, 128], BF16)
        Sst = sb("Sst", [64, 4, 128])
        Sbf = sb("Sbf", [64, 4, 128], BF16)
        ssq = sb("ssq", [128, 4]); rgl = sb("rgl", [128, 4])
        yb_tok = sb("yb_tok", [128, 512], BF16)
        sgA = sb("sgA", [128, NBT]); sgB = sb("sgB", [128, NBT])
        m2 = sb("m2", [128, KC, NBT])
        ua = [sb("ua%d" % i, [128, NBT + 8]) for i in range(2)]
        y0 = [sb("y0_%d" % i, [128, NBT]) for i in range(2)]
        ua_sets = [ua, [pT_raw[:, 0:NBT + 8], pT_raw[:, NBT + 8:2 * NBT + 16]]]
        y0_sets = [y0, [pT_raw[:, 2 * NBT + 16:3 * NBT + 16], pT_raw[:, 3 * NBT + 16:4 * NBT + 16]]]
        actT = sb("actT", [128, NFF, NBT], BF16)
        mergedT = lambda c: actT[:, c, :]
        MK = lambda c: K("actT", c)
        yaT = lambda c: actT[:, 8 + c, :]
        YAK = tuple(K("actT", 8 + c) for c in range(4))
        ybT = lambda c: actT[:, 12 + c, :]
        YBK = tuple(K("actT", 12 + c) for c in range(4))
        m2b = m2.bitcast(BF16)
        kcT = lambda c: m2b[:, c, 0:512]
        vcaug = sb("vcaug", [128, 4, 8, 65], BF16)
        vown = sb("vown", [32, 4, 8, 65], BF16)
        Ss, Ssb = Sst, Sbf

        def const_mask(t, keyname, pattern, cm, base, cmp_op):
            self.memset("pool", t[:], 1.0, (K(keyname),))
            S.add("pool", lambda e: e.affine_select(t[:], t[:], pattern=pattern, compare_op=cmp_op, fill=0.0,
                                                    base=base, channel_multiplier=cm), (K(keyname),), (K(keyname),))
        self.memset("pool", ident_f[:], 0.0, (K("ident_f"),))
        S.add("pool", lambda e: e.affine_select(ident_f[:], ident_f[:], pattern=[[-1, 128]], compare_op=ALU.not_equal,
                                                fill=1.0, base=0, channel_multiplier=1), (K("ident_f"),), (K("ident_f"),))
        const_mask(triu_f, "triu_f", [[1, 128]], -1, 0, ALU.is_ge)
        const_mask(trisl_f, "trisl_f", [[-1, 128]], 1, -1, ALU.is_ge)
        self.cp("dve", ident_b[:], ident_f[:], (K("ident_f"),), (K("ident_b"),))
        self.memset("dve", ones_b[:], 1.0, (K("ones_b"),))
        self.memset("dve", ones_f[:], 1.0, (K("ones_f"),))
        self.memset("dve", epsb[:], EPS, (K("epsb"),))
        self.memset("dve", gkT[:], 1.0, (K("gkT_ones"),))
        self.memset("dve", hist[:], 0.0, tuple(K("hist", jj) for jj in range(44)))
        self.memset("dve", Sst[:], 0.0, tuple(K("Sst", h) for h in range(4)))
        self.memset("dve", Sbf[:], 0.0, (K("Sst", "b"),))

        def load_T(src2d, rows, dst, kname, view=None):
            self.dma("pool_dma", stage[0:rows, :], src2d, (), (K("stage"),), ("stage",))
            bk, bkey = self.bank()
            self.tr(bk[:, 0:rows], stage[0:rows, :], ident_f[0:rows, 0:rows], (K("stage"), K("ident_f")), (bkey,))
            src = bk[:, 0:rows] if view is None else view(bk[:, 0:rows])
            self.cp("dve", dst, src, (bkey,), (K(kname),))

        load_T(gvec, 32, gT[:], "gT")
        load_T(b_ada, 48, badaT[:], "badaT")
        load_T(crow, 40, cT[:], "cT")
        for j in range(3):
            load_T(w_dw[j * 44:(j + 1) * 44, :], 44, wdwT[:, j, :], "wdwT%d" % j)
        load_T(b_dw, 44, bdwT[:], "bdwT")
        self.dma("pool_dma", flag[:], flag_d, (), (K("flag"),), ("misc", 0))
        self.dma("pool_dma", wgk_f[0:16, :], w_gk2, (), (K("wgk_f0"),), ("misc", 1))
        self.dma("pool_dma", wgk_f[16:17, :], b_gk, (), (K("wgk_f1"),), ("misc", 2))
        self.cp("dve", wgk_b[:], wgk_f[:], (K("wgk_f0"), K("wgk_f1")), (K("wgk_b"),))
        self.dma("pool_dma", cvec_s[:], cvec, (), (K("cvec"),), ("misc", 3))
        self.dma("pool_dma", ggla_s[:], ggla, (), (K("ggla"),), ("misc", 4))
        self.ts("dve", ggla_s[:], ggla_s[:], 0.5, None, ALU.mult, None, (K("ggla"),), (K("ggla"),))
        tabf = xin[0][:, :].rearrange("p (h q) -> p h q", h=8)
        for ti, tdst in ((0, tab_prev), (1, tab_own)):
            self.dma("pool_dma", tabf, btab[ti], (), (K("xin", 0), K("xinv", 0)), ("misc", 5))
            for h in range(8):
                self.ts("dve", tdst[:, h, :], tabf[:, h, :], cvec_s[:, h:h + 1], None, ALU.subtract, None,
                        (K("xin", 0), K("xinv", 0), K("cvec")), (K("tab", ti, h),))
        self.memset("dve", tab_own[64:128, :, 0:64], NEG, tuple(K("tab", 1, h) for h in range(8)))
        for ti, tdst in ((0, tab_prev), (1, tab_own)):
            self.act(tdst[:], tdst[:], AF.Exp, tuple(K("tab", ti, h) for h in range(8)), (K("etab", ti),))
        self.memset("dve", tab_mask[:], 1.0, (K("tab_mask"),))
        self.memset("dve", tab_mask[0:64, 64:128], 0.0, (K("tab_mask"),))

        def cast_group(src_cols, dst_g, name, g):
            self.dma("pool_dma", dst_g, src_cols.rearrange("(kc p) n -> p kc n", p=128), (), (K("scr", name, g),), ("cast", name, g))
            return (K("scr", name, g),)

        self.act(tmpf[:, 0:40], cT[:], AF.Tanh, (K("cT"),), (K("tmpf"),), scale=0.5)
        self.ts("dve", tmpf[:, 0:40], tmpf[:, 0:40], 0.5, 0.5, ALU.mult, ALU.add, (K("tmpf"),), (K("tmpf"),))
        self.tt("dve", siluT[:], tmpf[:, 0:40], cT[:], ALU.mult, (K("tmpf"), K("cT")), (K("siluT"),))
        siluv = siluT[:].rearrange("p (s k) -> p k s", k=8)
        modkeys = {}
        k8 = lambda n: tuple(K(n, kc) for kc in range(8))

        def ada_groups(glist):
            for g in glist:
                sl = g % 2
                buf = actT[:, sl * 8:(sl + 1) * 8, :].rearrange("p c n -> p (c n)")
                bkeys = tuple(K("actT", sl * 8 + c) for c in range(8))
                self.dma("pool_dma", buf.rearrange("p (kc n) -> p kc n", kc=8),
                         w_ada.rearrange("(kc p) n -> p kc n", p=128)[:, :, g * 512:(g + 1) * 512], (), bkeys, ("ada", sl))
                bk, bkey = self.bank()
                for oc in range(4):
                    for kc in range(8):
                        self.mm(bk[:, oc * 8:oc * 8 + 5], buf[:, kc * 512 + oc * 128: kc * 512 + (oc + 1) * 128], siluv[:, kc, :],
                                kc == 0, kc == 7, bkeys + (K("siluT"),), (bkey,))
                for oc in range(4):
                    ch = g * 4 + oc
                    self.ts("dve", mod[:, ch // 8, ch % 8, :], bk[:, oc * 8:oc * 8 + 5], badaT[:, ch:ch + 1], None, ALU.add, None,
                            (bkey, K("badaT")), (K("mod", ch),))

        ada_groups(range(0, 4))
        mk1 = tuple(K("mod", ch) for ch in range(16))
        for kc in range(8):
            self.ts("dve", Am[:, kc, :], mod[:, 1, kc, :], 1.0, gT[:, kc:kc + 1], ALU.add, ALU.mult, mk1 + (K("gT"),), (K("Am", kc),))
        self.cp("dve", Bm[:], mod[:, 0, :, :], mk1, (K("Bm"),))
        self.ts("dve", AmP[:], Am[:, :, 0], flag[:, 0:1], None, ALU.mult, None, k8("Am") + (K("flag"),), (K("AmP"),))
        self.ts("dve", BmP[:], Bm[:, :, 0], flag[:, 0:1], None, ALU.mult, None, (K("Bm"), K("flag")), (K("BmP"),))
        modkeys.update({"Am": k8("Am"), "Bm": (K("Bm"),), "AmP": (K("AmP"),), "BmP": (K("BmP"),)})

        def ada_part2():
            ada_groups(range(4, 12))
            mk2 = tuple(K("mod", ch) for ch in range(16, 48))
            for kc in range(8):
                rw = mk2 + (K("gT"),)
                self.ts("dve", Af[:, kc, :], mod[:, 4, kc, :], 1.0, gT[:, 16 + kc:17 + kc], ALU.add, ALU.mult, rw, (K("Af", kc),))
                self.ts("dve", Gm[:, kc, :], mod[:, 2, kc, :], gT[:, 8 + kc:9 + kc], None, ALU.mult, None, rw, (K("Gm", kc),))
                self.ts("dve", Gf[:, kc, :], mod[:, 5, kc, :], gT[:, 24 + kc:25 + kc], None, ALU.mult, None, rw, (K("Gf", kc),))
            self.cp("dve", Bf[:], mod[:, 3, :, :], mk2, (K("Bf"),))
            self.ts("dve", AfP[:], Af[:, :, 0], flag[:, 0:1], None, ALU.mult, None, k8("Af") + (K("flag"),), (K("AfP"),))
            self.ts("dve", BfP[:], Bf[:, :, 0], flag[:, 0:1], None, ALU.mult, None, (K("Bf"), K("flag")), (K("BfP"),))
        modkeys.update({"Af": k8("Af"), "Gm": k8("Gm"), "Gf": k8("Gf"), "Bf": (K("Bf"),), "AfP": (K("AfP"),), "BfP": (K("BfP"),)})

        kwin = {}
        for g in (3, 4, 1, 2):
            kwin[g] = cast_group(w_in[:, g * 512:(g + 1) * 512], win_g[g], "win", g)
        self.dma("pool_dma", wgk_g, w_in.rearrange("(kc p) n -> p kc n", p=128)[:, :, 5120:5136], (), (K("scr", "wgk"),), ("cast", "wgk"))
        kwgk = (K("scr", "wgk"),)
        for g in (0, 5, 6, 7, 8, 9):
            kwin[g] = cast_group(w_in[:, g * 512:(g + 1) * 512], win_g[g], "win", g)
        kwbra = [cast_group(w_br_a[:, g * 512:(g + 1) * 512], wbra_g[g], "wbra", g) for g in range(2)]
        kwbrb = [cast_group(w_br_b[:, g * 512:(g + 1) * 512], wbrb_g[g], "wbrb", g) for g in range(2)]
        kwout = [cast_group(w_out[:, g * 512:(g + 1) * 512], wout_g[g], "wout", g) for g in range(2)]
        kwup = [cast_group(w_up[:, g * 512:(g + 1) * 512], wup_g[g], "wup", g) for g in range(11)]
        kwdown = [cast_group(w_down[:, c * 128:(c + 1) * 128], wdown_g[c], "wdown", c) for c in range(8)]

        def load_xT(src_rows, ntile):
            for t in range(ntile):
                xb = xin[t % 2]
                kx = K("xin", t % 2)
                self.dma("sp", xb[:], src_rows[t * 128:(t + 1) * 128, :], (), (kx, K("xinv", t % 2)), ("xin", t % 2))
                for half in range(2):
                    bk, bkey = self.bank()
                    for j in range(4):
                        kc = half * 4 + j
                        self.tr(bk[:, j * 128:(j + 1) * 128], xb[:, kc * 128:(kc + 1) * 128], ident_f[:],
                                (kx if half == 0 else K("xinv", t % 2), K("ident_f")), (bkey,))
                    self.cp("act", xT[:, half * 4:half * 4 + 4, t * 128:(t + 1) * 128],
                            bk[:].rearrange("p (j n) -> p j n", j=4), (bkey,), tuple(K("xT", half * 4 + j, t) for j in range(4)))

        def rms_bcast(srcT, n, src_keys_fn):
            bk, bkey = self.bank()
            for kc in range(8):
                self.act(sq[:, kc % 2, 0:n], srcT[:, kc, 0:n], AF.Square, src_keys_fn(kc), (K("sq", kc % 2),))
                self.mm(bk[:, 0:n], ones_b[:], sq[:, kc % 2, 0:n], kc == 0, kc == 7, (K("ones_b"), K("sq", kc % 2)), (bkey,))
            self.act(tmpf[:, 0:n], bk[:, 0:n], AF.Ln, (bkey, K("epsb")), (K("tmpf"),), bias=epsb[:, 0:1], scale=1.0 / D)
            self.act(rbc[:, 0:n], tmpf[:, 0:n], AF.Exp, (K("tmpf"),), (K("rbc"),), scale=-0.5)

        def modulate(n, ntile, Asc, Bsc, Afn, Bfn):
            for kc in range(8):
                xk = tuple(K("xT", kc, t) for t in range(ntile))
                self.tt("dve", tmpf2[:, 0:n], xT[:, kc, 0:n], rbc[:, 0:n], ALU.mult, xk + (K("rbc"),), (K("tmpf2"),))
                self.act(hT[:, kc, 0:n], tmpf2[:, 0:n], AF.Identity, (K("tmpf2"),) + modkeys[Asc] + modkeys[Bsc], (K("hT", kc),),
                         bias=Bfn(kc), scale=Afn(kc))

        def modulate_seq(At, Bt, Asc, Bsc):
            for kc in range(8):
                xk = (K("xT", kc, 0),)
                v3 = lambda ap: ap.rearrange("p (s w) -> p s w", s=4)
                self.tt("dve", tmpf2[:, 0:128], xT[:, kc, 0:128], rbc[:, 0:128], ALU.mult, xk + (K("rbc"),), (K("tmpf2"),))
                self.tt("dve", v3(tmpf2[:, 0:128]), v3(tmpf2[:, 0:128]), At[:, kc, 1:5].to_broadcast([128, 4, 32]), ALU.mult,
                        (K("tmpf2"),) + modkeys[Asc], (K("tmpf2"),))
                self.tt("dve", v3(hT[:, kc, 0:128]), v3(tmpf2[:, 0:128]), Bt[:, kc, 1:5].to_broadcast([128, 4, 32]), ALU.add,
                        (K("tmpf2"),) + modkeys[Bsc], (K("hT", kc),))

        xkeys = lambda ntile: (lambda kc: tuple(K("xT", kc, t) for t in range(ntile)))

        def heads_fm(wg, wk, col0, nheads, n, evac):
            for h in range(nheads):
                bk, bkey = self.bank()
                for kc in range(8):
                    self.mm(bk[0:64, 0:n], wg(kc, col0 + h * 64, col0 + (h + 1) * 64), hT[:, kc, 0:n], kc == 0, kc == 7, (wk, K("hT", kc)), (bkey,))
                evac(h, bk, bkey)

        def pairs_fm(wg, wk, n, evac):
            for c in range(4):
                bk, bkey = self.bank()
                for kc in range(8):
                    self.mm(bk[:, 0:n], wg(kc, c * 128, (c + 1) * 128), hT[:, kc, 0:n], kc == 0, kc == 7, (wk, K("hT", kc)), (bkey,))
                evac(c, bk, bkey)

        def units_tm(wg, wk, col0, ncols, units, evac):
            for ui, (c0, C) in enumerate(units):
                bk, bkey = self.bank()
                for kc in range(8):
                    self.mm(bk[0:C, 0:ncols], hT[:, kc, c0:c0 + C], wg(kc, col0, col0 + ncols), kc == 0, kc == 7, (wk, K("hT", kc)), (bkey,))
                evac(ui, bk, bkey)

        def gla_block(C, col0, S_t, S_b, Skey, out_cb, prefix_only):
            bk, bkey = self.bank()
            self.mm(bk[0:C, 0:256], gkT[0:17, col0:col0 + C], wgk_b[0:17, :], True, True, (K("gkT"), K("gkT_ones"), K("wgk_b")), (bkey,))
            self.act(e_sb[0:C, :], bk[0:C, 0:256], AF.Exp, (bkey,), (K("e_sb"),), scale=-1.0)
            self.act(Lsp[0:C, :], e_sb[0:C, :], AF.Ln, (K("e_sb"),), (K("Lsp"),), bias=1.0)
            bk2, bkey2 = self.bank()
            self.mm(bk2[0:C, 0:256], trisl_f[0:C, 0:C], Lsp[0:C, :], True, True, (K("trisl_f"), K("Lsp")), (bkey2,))
            self.act(e_sb[0:C, :], bk2[0:C, 0:256], AF.Exp, (bkey2,), (K("e_sb"),), scale=-1.0 / 16)
            self.tt("dve", kend[0:C, :], kb_tok[0:C, :], e_sb[0:C, :], ALU.mult, (K("kb_tok"), K("e_sb")), (K("kend"),))
            bk3, bkey3 = self.bank()
            for h in range(4):
                self.mm(bk3[0:64, h * 128:h * 128 + C], Lsp[0:C, h * 64:(h + 1) * 64], triu_f[0:C, 0:C], True, True,
                        (K("Lsp"), K("triu_f")), (bkey3,))
            b3 = bk3[0:64, :].rearrange("p (c t) -> p c t", c=4)
            self.act(dec[:, :], b3[:, :, C - 1], AF.Exp, (bkey3,), (K("dec"),), scale=-1.0 / 16)
            if not prefix_only:
                self.act(e1[:, :, 0:C], b3[:, :, 0:C], AF.Exp, (bkey3,), (K("e1"),), scale=-1.0 / 16)
                self.act(e2[:, :, 0:C], b3[:, :, 0:C], AF.Exp, (bkey3,), (K("e2"),), scale=1.0 / 16)
                self.stt("dve", qtT[:, :, 0:C], qbT[:, :, col0:col0 + C], 0.125, e1[:, :, 0:C], ALU.mult, ALU.mult,
                         tuple(K("qbT", j) for j in range(4)) + (K("e1"),), (K("qtT"),))
                self.tt("dve", ktT[:, :, 0:C], kbT[:, :, col0:col0 + C], e2[:, :, 0:C], ALU.mult, tuple(K("kbT", j) for j in range(4)) + (K("e2"),), (K("ktT"),))
                bk4, bkey4 = self.bank()
                for h in range(4):
                    self.mm(bk4[0:C, h * 128:h * 128 + C], ktT[:, h, 0:C], qtT[:, h, 0:C], True, True,
                            (K("ktT"), K("qtT")), (bkey4,))
                for h in range(4):
                    self.tt("dve", attT[0:C, h, 0:C], bk4[0:C, h * 128:h * 128 + C], triu_f[0:C, 0:C], ALU.mult,
                            (bkey4, K("triu_f")), (K("attT", h),))
                bk5, bkey5 = self.bank()
                for h in range(4):
                    self.mm(bk5[0:C, h * 128:(h + 1) * 128], attT[0:C, h, 0:C], vb_tok[0:C, h, :], True, False, (K("attT", h), K("vb_tok")), (bkey5,))
                    self.mm(bk5[0:C, h * 128:(h + 1) * 128], qtT[:, h, 0:C], S_b[:, h, :], False, True,
                            (K("qtT"), Skey + ("b",)), (bkey5,))
                out_cb(bk5, bkey5)
            bk6, bkey6 = self.bank()
            for h in range(4):
                self.mm(bk6[0:64, h * 128:(h + 1) * 128], kend[0:C, h * 64:(h + 1) * 64], vb_tok[0:C, h, :], True, True, (K("kend"), K("vb_tok")), (bkey6,))
            for h in range(4):
                self.stt("dve", S_t[:, h, :], S_t[:, h, :], dec[:, h:h + 1], bk6[0:64, h * 128:(h + 1) * 128],
                         ALU.mult, ALU.add, (Skey + (h,), K("dec"), bkey6), (Skey + (h,),))
            self.cp("act", S_b[:], S_t[:], tuple(Skey + (h,) for h in range(4)), (Skey + ("b",),))

        def gla_out(C, obk, obkey):
            self.memset("dve", ssq[0:C, :], 0.0, tuple(K("ssq", h) for h in range(4)))
            for h in range(4):
                self.act(attT[0:C, h, :], obk[0:C, h * 128:(h + 1) * 128], AF.Square, (obkey,), (K("attT", h), K("ssq", h)), accum_out=ssq[0:C, h:h + 1])
            self.act(rgl[0:C, :], ssq[0:C, :], AF.Ln, tuple(K("ssq", h) for h in range(4)) + (K("epsb"),), (K("rgl"),), bias=epsb[0:C, 0:1], scale=1.0 / 128)
            self.act(rgl[0:C, :], rgl[0:C, :], AF.Exp, (K("rgl"),), (K("rgl"),), scale=-0.5)
            self.act(gtanh[0:C, :], gb2[0:C, :], AF.Tanh, (K("gb2"),), (K("gtanh"),), scale=0.5)
            self.stt("dve", gtanh[0:C, :], gtanh[0:C, :], 1.0, gb2[0:C, :], ALU.add, ALU.mult, (K("gtanh"), K("gb2")), (K("gtanh"),))
            self.tt("dve", gtanh[0:C, :], gtanh[0:C, :], ggla_s[0:C, :], ALU.mult, (K("gtanh"), K("ggla")), (K("gtanh"),))
            for h in range(4):
                self.stt("dve", yb_tok[0:C, h * 128:(h + 1) * 128], obk[0:C, h * 128:(h + 1) * 128], rgl[0:C, h:h + 1],
                         gtanh[0:C, h * 128:(h + 1) * 128], ALU.mult, ALU.mult, (obkey, K("rgl"), K("gtanh")), (K("yb_tok", h),))

        def gla_units(units, pre, S_of, with_fn=None):
            wgk_, wkk = self.wload(win_g[G_QKB], kwin[G_QKB], 8, 512)
            wgv_, wkv = self.wload(win_g[G_VB], kwin[G_VB], 8, 512)
            wgg_, wkg = (None, None) if pre else self.wload(win_g[G_GB], kwin[G_GB], 8, 512)
            body = lambda: gla_body(units, pre, S_of, wgk_, wkk, wgv_, wkv, wgg_, wkg)
            if with_fn is None:
                body()
            else:
                self.interleave([(body, [0, 1, 2]), (with_fn, [3, 4])])

        def gla_body(units, pre, S_of, wgk_, wkk, wgv_, wkv, wgg_, wkg):
            for ui, (col0, C) in enumerate(units):
                bk, bkey = self.bank()
                for kc in range(8):
                    self.mm(bk[0:C, 0:256], hT[:, kc, col0:col0 + C], wgk_(kc, 256, 512), kc == 0, kc == 7, (wkk, K("hT", kc)), (bkey,))
                self.cp("act", kb_tok[0:C, :], bk[0:C, 0:256], (bkey,), (K("kb_tok"),))
                bk, bkey = self.bank()
                for kc in range(8):
                    self.mm(bk[0:C, :], hT[:, kc, col0:col0 + C], wgv_(kc), kc == 0, kc == 7, (wkv, K("hT", kc)), (bkey,))
                self.cp("act", vb_tok[0:C, :, :].rearrange("p h e -> p (h e)"), bk[0:C, :], (bkey,), (K("vb_tok"),))
                if not pre:
                    bk, bkey = self.bank()
                    for kc in range(8):
                        self.mm(bk[0:C, :], hT[:, kc, col0:col0 + C], wgg_(kc), kc == 0, kc == 7, (wkg, K("hT", kc)), (bkey,))
                    self.cp("act", gb2[0:C, :], bk[0:C, :], (bkey,), (K("gb2"),))
                S_t, S_b, Skey, before_fn, after_fn = S_of(ui)
                if before_fn is not None:
                    before_fn()

                def out_cb(obk, obkey, col0=col0, C=C, ui=ui):
                    gla_out(C, obk, obkey)
                    for c in range(4):
                        self.tr(pbf[:, 512 + c * 128:512 + c * 128 + C], yb_tok[0:C, c * 128:(c + 1) * 128], ident_b[0:C, 0:C],
                                (K("yb_tok", c), K("ident_b")), (K("pbf2"),))
                    for c in range(4):
                        self.cp("act", ybT(c)[:, col0:col0 + C], pbf[:, 512 + c * 128:512 + c * 128 + C], (K("pbf2"),), (YBK[c],))
                gla_block(C, col0, S_t, S_b, Skey, out_cb, pre)
                if after_fn is not None:
                    after_fn()

        def attn_tile(tglob, tl):
            pTv = lambda kb, par: pT[:, kb, :, :].rearrange("p (c two) q -> p two c q", two=2)[:, par]
            for kb in range(5):
                slot = (tglob - 4 + kb) % 8
                banks = [self.bank(), self.bank()]
                for c in range(4):
                    for par in range(2):
                        pb = par * 64
                        bk, bkey = banks[par]
                        self.mm(bk[:, c * 128:(c + 1) * 128], kaT[slot][pb:pb + 64, c, :], qaT[pb:pb + 64, c, tl * 128:(tl + 1) * 128],
                                True, True, (K("kaT", slot, c), K("qaT", c)), (bkey,))
                for par in range(2):
                    bk, bkey = banks[par]
                    self.act(pTv(kb, par), bk[:].rearrange("p (c q) -> p c q", c=4), AF.Exp, (bkey,), (K("pT", kb, par),))
                tab = {0: (tab_mask[:, :].to_broadcast([128, 8, 128]) if False else None), 3: tab_prev, 4: tab_own}.get(kb)
                pk = (K("pT", kb, 0), K("pT", kb, 1))
                if kb == 0:
                    for h0 in range(0, 8, 4):
                        pass
                    self.tt("dve", pT[:, 0, :, :], pT[:, 0, :, :], bass.AP(tab_mask, 0, [[128, 128], [0, 8], [1, 128]]), ALU.mult,
                            pk + (K("tab_mask"),), pk)
                elif tab is not None:
                    self.tt("dve", pT[:, kb, :, :], pT[:, kb, :, :], tab[:, :, :], ALU.mult, pk + (K("etab", kb - 3),), pk)
            for hg in range(2):
                ob, obkey = obank[hg], K("obank", hg)
                for hh in range(4):
                    h = hg * 4 + hh
                    for kb in range(5):
                        slot = (tglob - 4 + kb) % 8
                        self.mm(ob[:, hh * 65:(hh + 1) * 65], pT[:, kb, h, :], vaug[slot][:, h, :], kb == 0, kb == 4,
                                (K("pT", kb, h % 2), K("vaug", slot), K("vaug1", slot)), (obkey,))
            for hg in range(2):
                ob, obkey = obank[hg], K("obank", hg)
                ov = ob[:, 0:260].rearrange("p (h e) -> p h e", h=4)
                self.ts("dve", rden[:, hg * 4:(hg + 1) * 4], ov[:, :, 64], 1e-30, None, ALU.add, None, (obkey,), (K("rden", hg),))
                S.add("dve", lambda e, hg=hg: e.reciprocal(rden[:, hg * 4:(hg + 1) * 4], rden[:, hg * 4:(hg + 1) * 4]), (K("rden", hg),), (K("rden", hg),))
                for hh in range(4):
                    h = hg * 4 + hh
                    self.ts("dve", ya_tok[:, h * 64:(h + 1) * 64], ov[:, hh, 0:64], rden[:, h:h + 1], None, ALU.mult, None,
                            (obkey, K("rden", hg)), (K("ya_tok", h),))
            yk = tuple(K("ya_tok", h) for h in range(8))
            for c in range(4):
                self.tr(pbf[:, c * 128:(c + 1) * 128], ya_tok[:, c * 128:(c + 1) * 128], ident_b[:], yk + (K("ident_b"),), (K("pbf1"),))
            for c in range(4):
                self.cp("act", yaT(c)[:, tl * 128:(tl + 1) * 128], pbf[:, c * 128:(c + 1) * 128], (K("pbf1"),), (YAK[c],))

        def attn_sample(s, slot):
            for rb in range(4):
                xb = xin[rb % 2]
                kx = K("xin", rb % 2)
                self.dma("sp", xb[:, 0:512], ck[s, rb * 128:(rb + 1) * 128, :], (), (kx,), ("xin", rb % 2))
                bk, bkey = self.bank()
                for c in range(4):
                    self.tr(bk[:, c * 128:(c + 1) * 128], xb[:, c * 128:(c + 1) * 128], ident_f[:], (kx, K("ident_f")), (bkey,))
                for c in range(4):
                    self.cp("act", kcT(c)[:, rb * 128:(rb + 1) * 128], bk[:, c * 128:(c + 1) * 128], (bkey,), (K("m2", c),))
                self.dma("sp", xb[:, 512:1024], cv[s, rb * 128:(rb + 1) * 128, :], (), (K("xinv", rb % 2),), ("xinv", rb % 2))
                self.cp("dve", vcaug[:, rb, :, 0:64], xb[:, 512:1024].rearrange("p (h e) -> p h e", h=8), (K("xinv", rb % 2),), (K("vcaug", rb),))
                self.cp("dve", vcaug[:, rb, :, 64], ones_f[:, 0:8], (K("ones_f"),), (K("vcaug1", rb),))
            q0 = s * 32
            for kb in range(5):
                kn = 128 if kb < 4 else 32
                banks = [self.bank(), self.bank()]
                for c in range(4):
                    for par in range(2):
                        pb = par * 64
                        bk, bkey = banks[par]
                        if kb < 4:
                            lhs = kcT(c)[pb:pb + 64, kb * 128:(kb + 1) * 128]
                            rk = (K("m2", c),)
                        else:
                            lhs = kaT[slot][pb:pb + 64, c, q0:q0 + 32]
                            rk = (K("kaT", slot, c),)
                        self.mm(bk[0:kn, c * 32:(c + 1) * 32], lhs, qaT[pb:pb + 64, c, q0:q0 + 32], True, True, rk + (K("qaT", c),), (bkey,))
                for par in range(2):
                    bk, bkey = banks[par]
                    dstv = pT[0:kn, kb, :, 0:32].rearrange("p (c two) q -> p two c q", two=2)[:, par]
                    self.act(dstv, bk[0:kn, 0:128].rearrange("p (c q) -> p c q", c=4), AF.Exp, (bkey,), (K("pT", kb, par),))
                pk = (K("pT", kb, 0), K("pT", kb, 1))
                if kb == 3:
                    self.tt("dve", pT[:, 3, :, 0:32], pT[:, 3, :, 0:32], tab_prev[:, :, 0:32], ALU.mult, pk + (K("etab", 0),), pk)
                elif kb == 4:
                    self.tt("dve", pT[0:32, 4, :, 0:32], pT[0:32, 4, :, 0:32], tab_own[0:32, :, 0:32], ALU.mult, pk + (K("etab", 1),), pk)
            for h in range(8):
                hg, hh = h // 4, h % 4
                for kb in range(5):
                    kn = 128 if kb < 4 else 32
                    rhs = vcaug[:, kb, h, :] if kb < 4 else vown[0:32, s, h, :]
                    rk = (K("vcaug", kb), K("vcaug1", kb)) if kb < 4 else (K("vown", s), K("vown1", s))
                    self.mm(obank[hg][0:32, hh * 65:(hh + 1) * 65], pT[0:kn, kb, h, 0:32], rhs, kb == 0, kb == 4,
                            (K("pT", kb, h % 2),) + rk, (K("obank", hg),))
            for hg in range(2):
                ob, obkey = obank[hg], K("obank", hg)
                ov = ob[0:32, 0:260].rearrange("p (h e) -> p h e", h=4)
                S.add("dve", lambda e, ov=ov, hg=hg: e.reciprocal(rden[0:32, hg * 4:(hg + 1) * 4], ov[:, :, 64]), (obkey,), (K("rden", hg),))
                for hh in range(4):
                    h = hg * 4 + hh
                    self.ts("dve", ya_tok[0:32, h * 64:(h + 1) * 64], ov[:, hh, 0:64], rden[0:32, h:h + 1], None, ALU.mult, None,
                            (obkey, K("rden", hg)), (K("ya_tok", h),))
            yk = tuple(K("ya_tok", h) for h in range(8))
            for c in range(4):
                self.tr(pbf[:, c * 128:c * 128 + 32], ya_tok[0:32, c * 128:(c + 1) * 128], ident_b[0:32, 0:32], yk + (K("ident_b"),), (K("pbf1"),))
            for c in range(4):
                self.cp("act", yaT(c)[:, q0:q0 + 32], pbf[:, c * 128:c * 128 + 32], (K("pbf1"),), (YAK[c],))

        def resid(n, ntile, Gfn, gname):
            for kc in range(8):
                self.tt("dve", tmpf2[:, 0:n], m2[:, kc, 0:n], rbc[:, 0:n], ALU.mult, (K("m2", kc), K("rbc")), (K("tmpf2"),))
                xk = tuple(K("xT", kc, t) for t in range(ntile))
                self.stt("dve", xT[:, kc, 0:n], tmpf2[:, 0:n], Gfn(kc), xT[:, kc, 0:n], ALU.mult, ALU.add,
                         (K("tmpf2"),) + modkeys[gname] + xk, xk)

        def resid_seq(Gt, gname):
            v3 = lambda ap: ap.rearrange("p (s w) -> p s w", s=4)
            for kc in range(8):
                xk = (K("xT", kc, 0),)
                self.tt("dve", tmpf2[:, 0:128], m2[:, kc, 0:128], rbc[:, 0:128], ALU.mult, (K("m2", kc), K("rbc")), (K("tmpf2"),))
                self.tt("dve", v3(tmpf2[:, 0:128]), v3(tmpf2[:, 0:128]), Gt[:, kc, 1:5].to_broadcast([128, 4, 32]), ALU.mult,
                        (K("tmpf2"),) + modkeys[gname], (K("tmpf2"),))
                self.tt("dve", xT[:, kc, 0:128], xT[:, kc, 0:128], tmpf2[:, 0:128], ALU.add, (K("tmpf2"),) + xk, xk)

        def merge_and_out(n):
            def ev_gate(dst, dkey):
                def f(bk, bkey):
                    self.act(dst[:, 0:n], bk[:, 0:n], AF.Tanh, (bkey,), (dkey,), scale=0.5)
                    self.ts("dve", dst[:, 0:n], dst[:, 0:n], 0.5, 0.5, ALU.mult, ALU.add, (dkey,), (dkey,))
                return f
            for G in range(2):
                wga, wka = self.wload(win_g[G_GA + G], kwin[G_GA + G], 8, 512)
                wba, wkba = self.wload(wbra_g[G], kwbra[G], 4, 512)
                for j in range(4):
                    c = G * 4 + j
                    bk, bkey = self.bank()
                    for kc in range(8):
                        self.mm(bk[:, 0:n], wga(kc, j * 128, (j + 1) * 128), hT[:, kc, 0:n], kc == 0, kc == 7, (wka, K("hT", kc)), (bkey,))
                    ev_gate(sgA, K("sgA"))(bk, bkey)
                    bk, bkey = self.bank()
                    for kc in range(4):
                        self.mm(bk[:, 0:n], wba(kc, j * 128, (j + 1) * 128), yaT(kc)[:, 0:n], kc == 0, kc == 3, (wkba, YAK[kc]), (bkey,))
                    self.tt("dve", m2[:, c, 0:n], sgA[:, 0:n], bk[:, 0:n], ALU.mult, (K("sgA"), bkey), (K("m2", c),))
                wgb, wkb = self.wload(win_g[G_GBR + G], kwin[G_GBR + G], 8, 512)
                wbb, wkbb = self.wload(wbrb_g[G], kwbrb[G], 4, 512)
                for j in range(4):
                    c = G * 4 + j
                    bk, bkey = self.bank()
                    for kc in range(8):
                        self.mm(bk[:, 0:n], wgb(kc, j * 128, (j + 1) * 128), hT[:, kc, 0:n], kc == 0, kc == 7, (wkb, K("hT", kc)), (bkey,))
                    ev_gate(sgB, K("sgB"))(bk, bkey)
                    bk, bkey = self.bank()
                    for kc in range(4):
                        self.mm(bk[:, 0:n], wbb(kc, j * 128, (j + 1) * 128), ybT(kc)[:, 0:n], kc == 0, kc == 3, (wkbb, YBK[kc]), (bkey,))
                    self.tt("dve", sgB[:, 0:n], sgB[:, 0:n], bk[:, 0:n], ALU.mult, (K("sgB"), bkey), (K("sgB"),))
                    self.tt("dve", mergedT(c)[:, 0:n], sgB[:, 0:n], m2[:, c, 0:n], ALU.add, (K("sgB"), K("m2", c)), (MK(c),))
            for G in range(2):
                wg, wk = self.wload(wout_g[G], kwout[G], 8, 512)
                for j in range(4):
                    c = G * 4 + j
                    bk, bkey = self.bank()
                    for kc in range(8):
                        self.mm(bk[:, 0:n], wg(kc, j * 128, (j + 1) * 128), mergedT(kc)[:, 0:n], kc == 0, kc == 7, (wk, MK(kc)), (bkey,))
                    self.cp("act", m2[:, c, 0:n], bk[:, 0:n], (bkey,), (K("m2", c),))
            rms_bcast(m2, n, lambda kc: (K("m2", kc),))

        def ffn(n, nseg, hview, hkeyf, extra=()):
            w = n // nseg
            v3 = lambda ap: ap.rearrange("p (s w) -> p s w", s=nseg)
            for g in range(11):
                wg, wk = self.wload(wup_g[g], kwup[g], 8, 512)
                for pair in range(2):
                    ua_, y0_ = ua_sets[pair], y0_sets[pair]
                    for half in range(2):
                        jc = half * 2 + pair
                        jj = half * NFF + 2 * g + pair
                        bk, bkey = self.bank()
                        for kc in range(8):
                            self.mm(bk[:, 0:n], wg(kc, jc * 128, (jc + 1) * 128), hT[:, kc, 0:n], kc == 0, kc == 7, (wk, K("hT", kc)), (bkey,))
                        u3 = ua_[half][:, 0:nseg * (w + 2)].rearrange("p (s w) -> p s w", s=nseg)
                        uk, uh, yk = K("ua", pair, half), K("uah", pair, half), K("y0", pair, half)
                        yv = v3(y0_[half][:, 0:n])
                        self.cp("pool", u3[:, :, 0:2], hview(jj), (hkeyf(jj),) + extra, (uh,))
                        self.cp("act", u3[:, :, 2:2 + w], v3(bk[:, 0:n]), (bkey,), (uk,))
                        self.act(y0_[half][:, 0:n], bk[:, 0:n], AF.Identity, (bkey, K("wdwT2"), K("bdwT")), (yk,),
                                 bias=bdwT[:, jj:jj + 1], scale=wdwT[:, 2, jj:jj + 1])
                        self.stt("dve", yv, u3[:, :, 1:1 + w], wdwT[:, 1, jj:jj + 1], yv, ALU.mult, ALU.add, (uk, uh, yk, K("wdwT1")), (yk,))
                        self.stt("dve", yv, u3[:, :, 0:w], wdwT[:, 0, jj:jj + 1], yv, ALU.mult, ALU.add, (uk, uh, yk, K("wdwT0")), (yk,))
                        self.cp("pool", hview(jj), u3[:, :, w:w + 2], (uk,), (hkeyf(jj),))
                    j = 2 * g + pair
                    self.act(y0_[0][:, 0:n], y0_[0][:, 0:n], AF.Gelu_apprx_tanh, (K("y0", pair, 0),), (K("y0", pair, 0),))
                    self.tt("dve", actT[:, j, 0:n], y0_[0][:, 0:n], y0_[1][:, 0:n], ALU.mult, (K("y0", pair, 0), K("y0", pair, 1)), (K("actT", j),))
            for c in range(8):
                wg, wk = self.wload(wdown_g[c], kwdown[c], NFF, 128)
                bk, bkey = self.bank()
                for j in range(NFF):
                    self.mm(bk[:, 0:n], wg(j), actT[:, j, 0:n], j == 0, j == NFF - 1, (wk, K("actT", j)), (bkey,))
                self.cp("act", m2[:, c, 0:n], bk[:, 0:n], (bkey,), (K("m2", c),))
            rms_bcast(m2, n, lambda kc: (K("m2", kc),))

        def store_y(dst_rows, ntile):
            for t in range(ntile):
                yb_ = xin[t % 2]
                ky = K("xin", t % 2)
                for half in range(2):
                    bk, bkey = self.bank()
                    for j in range(4):
                        kc = half * 4 + j
                        self.tr(bk[:, j * 128:(j + 1) * 128], xT[:, kc, t * 128:(t + 1) * 128], ident_f[:], (K("xT", kc, t), K("ident_f")), (bkey,))
                    self.cp("act", yb_[:, half * 512:(half + 1) * 512], bk[:], (bkey,), (K("xinv", t % 2),) if half else (ky,))
                self.dma("sp", dst_rows[t * 128:(t + 1) * 128, :], yb_[:], (ky, K("xinv", t % 2)), (), ("xin", t % 2))

        def store_hist(hsrc, hkeys, dst):
            self.cp("dve", h88[:], hsrc.rearrange("p j t -> p t j"), hkeys, (K("h88"),))
            bk, bkey = self.bank()
            self.tr(bk[0:88, 0:128], h88[:].rearrange("p t j -> p (t j)"), ident_f[:], (K("h88"), K("ident_f")), (bkey,))
            self.cp("act", stage[0:88, :], bk[0:88, 0:128], (bkey,), (K("stage"),))
            self.dma("sp", dst, stage[0:88, :], (K("stage"),), (), ("stage_o",))

        def kv_stage(bk, bkey, which, dst):
            st = xin[1][:, which * 512:(which + 1) * 512]
            sk = K("xin", 1) if which == 0 else K("xinv", 1)
            self.cp("act", st, bk[:], (bkey,), (sk,))
            self.dma("sp", dst, st, (sk,), (), ("xin", 1) if which == 0 else ("xinv", 1))

        LAST_KV0 = 16 + NMAIN - 4

        def common_proj(n, units, tglob0, kv_from, pre, flagged, kv_out):
            if not pre:
                wg, wk = self.wload(win_g[G_QA], kwin[G_QA], 8, 512)
                def ev_q(j, bk, bkey):
                    self.act(qaT[:, j, 0:n], bk[:, 0:n], AF.Identity, (bkey,), (K("qaT", j),), scale=0.125)
                pairs_fm(wg, wk, n, ev_q)
            if kv_from < len(units):
                wg, wk = self.wload(win_g[G_KA], kwin[G_KA], 8, 512)
                def ev_k(j, bk, bkey):
                    for ui in range(kv_from, len(units)):
                        c0, C = units[ui]
                        if C == 128:
                            slot = (tglob0 + ui) % 8
                            self.cp("act", kaT[slot][:, j, :], bk[:, c0:c0 + 128], (bkey,), (K("kaT", slot, j),))
                    if units[0][1] != 128:
                        self.cp("act", kaT[tglob0 % 8][:, j, :], bk[:, 0:128], (bkey,), (K("kaT", tglob0 % 8, j),))
                pairs_fm(wg, wk, n, ev_k)
                kunits = [(ui, units[ui]) for ui in range(kv_from, len(units)) if kv_out(ui, 0) is not None] if units[0][1] == 128 else []
                if kunits:
                    units_tm(wg, wk, 0, 512, [u for _, u in kunits], lambda i, bk, bkey: kv_stage(bk, bkey, 0, kv_out(kunits[i][0], 0)))
                if units[0][1] != 128:
                    units_tm(wg, wk, 0, 512, [(0, 128)], lambda i, bk, bkey: kv_stage(bk, bkey, 0, kv_out(0, 0)))
                wg, wk = self.wload(win_g[G_VA], kwin[G_VA], 8, 512)
                if units[0][1] == 128:
                    def ev_v(i, bk, bkey):
                        ui = kv_from + i
                        slot = (tglob0 + ui) % 8
                        self.cp("act", vaug[slot][:, :, 0:64], bk[:].rearrange("p (h e) -> p h e", h=8), (bkey,), (K("vaug", slot),))
                        if flagged:
                            self.ts("dve", vaug[slot][:, :, 64], ones_f[:, 0:8], flag[:, 0:1], None, ALU.mult, None,
                                    (K("ones_f"), K("flag")), (K("vaug1", slot),))
                        else:
                            self.cp("dve", vaug[slot][:, :, 64], ones_f[:, 0:8], (K("ones_f"),), (K("vaug1", slot),))
                        if kv_out(ui, 1) is not None:
                            kv_stage(bk, bkey, 1, kv_out(ui, 1))
                    units_tm(wg, wk, 0, 512, units[kv_from:], ev_v)
                else:
                    units_tm(wg, wk, 0, 512, [(0, 128)], lambda i, bk, bkey: kv_stage(bk, bkey, 1, kv_out(0, 1)))
            if not pre:
                wg, wk = self.wload(win_g[G_QKB], kwin[G_QKB], 8, 512)
                def ev_qb(j, bk, bkey):
                    self.cp("act", qbT[:, j, 0:n], bk[0:64, 0:n], (bkey,), (K("qbT", j),))
                def ev_kb(j, bk, bkey):
                    self.cp("act", kbT[:, j, 0:n], bk[0:64, 0:n], (bkey,), (K("kbT", j),))
                heads_fm(wg, wk, 0, 4, n, ev_qb)
                heads_fm(wg, wk, 256, 4, n, ev_kb)
            wg, wk = self.wload(wgk_g, kwgk, 8, 16)
            bk, bkey = self.bank()
            for kc in range(8):
                self.mm(bk[0:16, 0:n], wg(kc), hT[:, kc, 0:n], kc == 0, kc == 7, (wk, K("hT", kc)), (bkey,))
            self.cp("act", gkT[0:16, 0:n], bk[0:16, 0:n], (bkey,), (K("gkT"),))

        SK = K("Sst")
        def prompt_block(src_rows, ntile, tglob0, mode, dst_rows, kv_from):
            n = ntile * 128
            pre = mode == "prefix"
            flagged = mode != "full"
            units = [(t * 128, 128) for t in range(ntile)]
            mark = lambda nm: self.marks.append((mode, tglob0, nm, len(S.ops)))
            mark("start")
            load_xT(src_rows, ntile)
            rms_bcast(xT, n, xkeys(ntile))
            if flagged:
                modulate(n, ntile, "AmP", "BmP", lambda kc: AmP[:, kc:kc + 1], lambda kc: BmP[:, kc:kc + 1])
            else:
                modulate(n, ntile, "Am", "Bm", lambda kc: Am[:, kc, 0:1], lambda kc: Bm[:, kc, 0:1])
            mark("norm1")

            def kv_out(ui, which):
                tg = tglob0 + ui
                if mode != "full" or tg < LAST_KV0:
                    return None
                r0 = (tg - LAST_KV0) * 128
                return (kp_o if which == 0 else vp_o)[r0:r0 + 128, :]
            common_proj(n, units, tglob0, kv_from, pre, flagged, kv_out)
            mark("proj")
            if pre:
                gla_units(units, pre, lambda ui: (Sst, Sbf, SK, None, None))
                mark("gla")
                return
            def attn_all():
                for t in range(ntile):
                    attn_tile(tglob0 + t, t)
            gla_units(units, pre, lambda ui: (Sst, Sbf, SK, None, None), attn_all)
            mark("attn")
            merge_and_out(n)
            resid(n, ntile, lambda kc: Gm[:, kc, 0:1], "Gm")
            mark("merge")
            rms_bcast(xT, n, xkeys(ntile))
            if flagged:
                modulate(n, ntile, "AfP", "BfP", lambda kc: AfP[:, kc:kc + 1], lambda kc: BfP[:, kc:kc + 1])
            else:
                modulate(n, ntile, "Af", "Bf", lambda kc: Af[:, kc, 0:1], lambda kc: Bf[:, kc, 0:1])
            ffn(n, 1, lambda jj: hist[:, jj:jj + 1, :], lambda jj: K("hist", jj))
            mark("ffn")
            resid(n, ntile, lambda kc: Gf[:, kc, 0:1], "Gf")
            store_y(dst_rows, ntile)
            mark("end")

        def sample_block():
            n = 128
            slot = 0
            units = [(s * 32, 32) for s in range(4)]
            self.marks.append(("sample", 0, "start", len(S.ops)))
            load_xT(xsam, 1)
            rms_bcast(xT, n, xkeys(1))
            modulate_seq(Am, Bm, "Am", "Bm")
            common_proj(n, units, slot, 0, False, False, lambda ui, which: (ks_o if which == 0 else vs_o))
            SSK = K("Sst")
            def S_of(ui):
                def before():
                    self.dma("sp", Ss[:], sgla[ui].rearrange("h k v -> k h v"), (), tuple(SSK + (h,) for h in range(4)), ("Ss",))
                    self.cp("act", Ssb[:], Ss[:], tuple(SSK + (h,) for h in range(4)), (SSK + ("b",),))
                def after():
                    self.dma("sp", glas_o[ui].rearrange("h k v -> k h v"), Ss[:], tuple(SSK + (h,) for h in range(4)), (), ("Ss",))
                return (Ss, Ssb, SSK, before, after)
            wgv, wkv = self.wload(win_g[G_VA], kwin[G_VA], 8, 512)
            for s in range(4):
                bk, bkey = self.bank()
                for kc in range(8):
                    self.mm(bk[0:32, :], hT[:, kc, s * 32:(s + 1) * 32], wgv(kc), kc == 0, kc == 7, (wkv, K("hT", kc)), (bkey,))
                self.cp("act", vown[:, s, :, 0:64], bk[0:32, :].rearrange("p (h e) -> p h e", h=8), (bkey,), (K("vown", s),))
                self.cp("dve", vown[:, s, :, 64], ones_f[0:32, 0:8], (K("ones_f"),), (K("vown1", s),))
            def attn_all():
                for s in range(4):
                    attn_sample(s, slot)
            gla_units(units, False, S_of, attn_all)
            merge_and_out(n)
            resid_seq(Gm, "Gm")
            rms_bcast(xT, n, xkeys(1))
            modulate_seq(Af, Bf, "Af", "Bf")
            for s in range(4):
                load_T(sconv[s], 88, hist_s[:, s, :, :].rearrange("p j t -> p t j"), "hist_s_in%d" % s,
                       view=lambda a: a.rearrange("p (t j) -> p t j", t=2))
            hs_in = tuple(K("hist_s_in%d" % s) for s in range(4))
            ffn(n, 4, lambda jj: hist_s[:, :, jj, :], lambda jj: K("hist_s", jj), hs_in)
            resid_seq(Gf, "Gf")
            store_y(ysam, 1)
            for s in range(4):
                store_hist(hist_s[:, s, :, :], tuple(K("hist_s", jj) for jj in range(44)), convs_o[s])

        STG = int(os.environ.get("KSTAGE", "99"))
        if STG <= 0:
            return S.finalize()
        t0 = 0
        while t0 < 11:
            nt = min(NB, 11 - t0)
            prompt_block(xpre[t0 * 128:(t0 + nt) * 128, :], nt, t0, "prefix", None, nt)
            t0 += nt
        while t0 < 15:
            nt = min(NB, 15 - t0)
            prompt_block(xpre[t0 * 128:(t0 + nt) * 128, :], nt, t0, "prefix", None, 0)
            t0 += nt
        ada_part2()
        if STG <= 2:
            return S.finalize()
        prompt_block(xov, 1, 15, "overlap", yov, 0)
        for jj in range(44):
            pass
        self.ts("dve", hist[:].rearrange("p j t -> p (j t)"), hist[:].rearrange("p j t -> p (j t)"), flag[:, 0:1], None, ALU.mult, None,
                tuple(K("hist", jj) for jj in range(44)) + (K("flag"),), tuple(K("hist", jj) for jj in range(44)))
        if STG <= 3:
            return S.finalize()
        for b in range(min(NMAIN // NB, int(os.environ.get('KMAIN', '99')))):
            prompt_block(xmain[b * NBT:(b + 1) * NBT, :], NB, 16 + b * NB, "full", ymain[b * NBT:(b + 1) * NBT, :], 0)
        self.dma("sp", glap_o.rearrange("h k v -> k h v"), Sst[:], tuple(SK + (h,) for h in range(4)), (), ("glap",))
        store_hist(hist[:], tuple(K("hist", jj) for jj in range(44)), convp_o)
        if STG <= 4:
            return S.finalize()
        sample_block()
        return S.finalize()


_CACHE = {}


def _program():
    if "nc" not in _CACHE:
        b = Builder()
        b.build()
        _CACHE["nc"] = b.nc
    return _CACHE["nc"]


def kernel(x_prompt, x_sample, cache_k_a, cache_v_a, state_gla, state_conv, c_prompt, c_sample,
           w_ada, b_ada, g_pre_mix, g_post_mix, g_pre_ffn, g_post_ffn, w_in, w_gk2, b_gk,
           rel_bias, g_gla, w_br_a, w_br_b, w_out, w_up, w_dw, b_dw, w_down):
    f = lambda a: np.ascontiguousarray(np.asarray(a, dtype=np.float32))
    x_prompt, x_sample = f(x_prompt), f(x_sample)
    rb = f(rel_bias)[0]
    kk = np.arange(128)[:, None]
    qq = np.arange(128)[None, :]
    idx_prev = np.clip(qq + 128 - kk, -128, 128) + 128
    idx_own = np.clip(qq - kk, -128, 128) + 128
    btab = np.stack([rb[:, idx_prev].transpose(1, 0, 2), rb[:, idx_own].transpose(1, 0, 2)])
    cvec = np.broadcast_to(rb[:, 256][None, :], (128, 8))
    wi = f(w_in)[0]
    wi = np.concatenate([wi[:, :3072], wi[:, 3088:], wi[:, 3072:3088]], axis=1)
    wu = f(w_up)[0]
    cols = []
    for g in range(11):
        cols += [wu[:, 2 * g * 128:(2 * g + 2) * 128], wu[:, DFF + 2 * g * 128:DFF + (2 * g + 2) * 128]]
    wu = np.concatenate(cols, axis=1)
    shared = {
        "w_ada": f(w_ada)[0], "b_ada": f(b_ada)[0].reshape(48, 128),
        "gvec": np.concatenate([f(g_pre_mix)[0], f(g_post_mix)[0], f(g_pre_ffn)[0], f(g_post_ffn)[0]]).reshape(32, 128),
        "w_in": f(wi), "w_gk2": f(w_gk2)[0], "b_gk": f(b_gk)[0].reshape(1, 256),
        "btab": f(btab), "cvec": f(cvec), "ggla": f(np.broadcast_to(np.tile(f(g_gla)[0], 4)[None, :], (128, 512))),
        "w_br_a": f(w_br_a)[0], "w_br_b": f(w_br_b)[0], "w_out": f(w_out)[0], "w_up": f(wu),
        "w_dw": f(w_dw)[0].reshape(3 * 44, 128), "b_dw": f(b_dw)[0].reshape(44, 128), "w_down": f(w_down)[0],
    }
    in_maps = []
    for c in range(8):
        b, hf = c // 2, c % 2
        m = dict(shared)
        if hf == 1:
            m["xpre"] = x_prompt[b, 0:NPRE * 128]
            m["xov"] = x_prompt[b, NPRE * 128:2048]
        else:
            m["xpre"] = np.zeros((NPRE * 128, D), np.float32)
            m["xov"] = np.zeros((128, D), np.float32)
        m["xmain"] = x_prompt[b, hf * 2048:(hf + 1) * 2048]
        m["xsam"] = x_sample[4 * c:4 * c + 4].reshape(128, D)
        m["crow"] = f(np.concatenate([f(c_prompt)[b:b + 1], f(c_sample)[4 * c:4 * c + 4]], 0).reshape(40, 128))
        m["flag"] = np.full((128, 1), float(hf), np.float32)
        m["ck"] = f(cache_k_a)[0, 4 * c:4 * c + 4].reshape(4, 512, 512)
        m["cv"] = f(cache_v_a)[0, 4 * c:4 * c + 4].reshape(4, 512, 512)
        m["sgla"] = f(state_gla)[0, 4 * c:4 * c + 4]
        m["sconv"] = f(state_conv)[0, 4 * c:4 * c + 4].reshape(4, 88, 128)
        in_maps.append({k: np.ascontiguousarray(v) for k, v in m.items()})
    nc = _program()
    cores = [int(t) for t in os.environ.get("KCORES", "0,1,2,3,4,5,6,7").split(",")]
    if os.environ.get("KTRACE"):
        res = run_bass_kernel_spmd(nc, [in_maps[c] for c in cores], core_ids=list(range(len(cores))), trace=True)
        print("EXEC_TIME_NS", res.exec_time_ns)
    else:
        res = run_bass_kernel_spmd(nc, [in_maps[c] for c in cores], core_ids=list(range(len(cores))))
    R = {c: res.results[i] for i, c in enumerate(cores)}
    y_prompt = np.zeros((4, 4096, D), np.float32)
    y_sample = np.zeros((32, 32, D), np.float32)
    k_p = np.zeros((1, 4, 512, 8, 64), np.float32); v_p = np.zeros_like(k_p)
    gla_p = np.zeros((1, 4, 4, 64, 128), np.float32)
    conv_p = np.zeros((1, 4, 2, 2 * DFF), np.float32)
    k_s = np.zeros((1, 32, 32, 8, 64), np.float32); v_s = np.zeros_like(k_s)
    gla_s = np.zeros((1, 32, 4, 64, 128), np.float32)
    conv_s = np.zeros((1, 32, 2, 2 * DFF), np.float32)
    for c in cores:
        b, hf = c // 2, c % 2
        r = R[c]
        y_prompt[b, hf * 2048:(hf + 1) * 2048] = r["ymain"]
        y_sample[4 * c:4 * c + 4] = r["ysam"].reshape(4, 32, D)
        if hf == 1:
            k_p[0, b] = r["kp"].reshape(512, 8, 64)
            v_p[0, b] = r["vp"].reshape(512, 8, 64)
            gla_p[0, b] = r["glap"]
            conv_p[0, b] = r["convp"].reshape(2, 2 * DFF)
        k_s[0, 4 * c:4 * c + 4] = r["ks"].reshape(4, 32, 8, 64)
        v_s[0, 4 * c:4 * c + 4] = r["vs"].reshape(4, 32, 8, 64)
        gla_s[0, 4 * c:4 * c + 4] = r["glas"]
        conv_s[0, 4 * c:4 * c + 4] = r["convs"].reshape(4, 2, 2 * DFF)
    return (y_prompt, y_sample, k_p, v_p, gla_p, conv_p, k_s, v_s, gla_s, conv_s)
```

```python
from contextlib import ExitStack
import os

import numpy as np
import concourse.bass as bass
import concourse.mybir as mybir
from concourse.bass_utils import run_bass_kernel_spmd

F32 = mybir.dt.float32
BF16 = mybir.dt.bfloat16
AF = mybir.ActivationFunctionType
ALU = mybir.AluOpType

D = 1024
KC = 8
DFF = 2816
NFF = 22
DIN = 5136
G_QA, G_KA, G_VA, G_QKB, G_VB, G_GB, G_GA, G_GBR = 0, 1, 2, 3, 4, 5, 6, 8
EPS = 1e-6
NEG = -30000.0
NPRE = 15
NMAIN = 16
NB = 4


class Op:
    __slots__ = ("eng", "fn", "reads", "writes", "dsem", "signal", "sigval", "deps")

    def __init__(self, eng, fn, reads, writes, dsem):
        self.eng, self.fn, self.reads, self.writes, self.dsem = eng, fn, reads, writes, dsem
        self.signal = False
        self.sigval = 0
        self.deps = ()


class Sched:
    DMA = ("sp", "pool_dma")

    def __init__(self, nc, stack):
        self.nc = nc
        self.stack = stack
        self.ops = []
        self.eng_obj = {"pe": nc.tensor, "act": nc.scalar, "dve": nc.vector, "pool": nc.gpsimd,
                        "sp": nc.sync, "pool_dma": nc.gpsimd}
        self.wuses = []
        self.wdepth = 2

    def add(self, eng, fn, reads=(), writes=(), dsem=None):
        op = Op(eng, fn, tuple(reads), tuple(writes), dsem)
        self.ops.append(op)
        return op

    def queue_of(self, op):
        return "pool" if op.eng == "pool_dma" else op.eng

    def finalize(self):
        nc = self.nc
        inserts = {}
        lastrd = {}
        for idx, op in enumerate(self.ops):
            for k in op.reads:
                if k and k[0] == "wuse":
                    lastrd[k[1]] = idx
        prev = 0
        for i, (pos, op) in enumerate(self.wuses):
            tgt = self.wuses[max(0, i - self.wdepth)][0]
            if i >= 3 and (i - 3) in lastrd:
                tgt = max(tgt, lastrd[i - 3] + 1)
            tgt = max(tgt, prev)
            prev = tgt
            assert tgt <= pos, (i, tgt, pos)
            inserts.setdefault(tgt, []).append(op)
        ops = []
        for i, op in enumerate(self.ops):
            if i in inserts:
                ops.extend(inserts[i])
            ops.append(op)
        self.ops = ops
        last_w = {}
        readers = {}
        for i, op in enumerate(ops):
            deps = set()
            q = self.queue_of(op)
            isdma = op.dsem is not None
            for k in op.reads:
                j = last_w.get(k)
                if j is not None:
                    deps.add(j)
            for k in op.writes:
                j = last_w.get(k)
                if j is not None:
                    oj = ops[j]
                    if isdma or oj.dsem is not None or self.queue_of(oj) != q or q != "pe":
                        deps.add(j)
                for j in readers.get(k, ()):
                    oj = ops[j]
                    if isdma or oj.dsem is not None or self.queue_of(oj) != q or (q != "pe" and os.environ.get("KWAR", "1") == "1"):
                        deps.add(j)
            deps.discard(i)
            op.deps = tuple(sorted(deps))
            for j in op.deps:
                ops[j].signal = True
            for k in op.reads:
                lst = readers.setdefault(k, [])
                if op.dsem is None:
                    lst[:] = [j for j in lst if ops[j].dsem is not None or self.queue_of(ops[j]) != q]
                lst.append(i)
            for k in op.writes:
                last_w[k] = i
                readers[k] = []
        esem = {}
        for e in ("pe", "act", "dve", "pool"):
            esem[e] = self.stack.enter_context(nc.semaphore("sem_" + e))
        dsems = {}
        cnt = {}
        for op in ops:
            if op.dsem is not None:
                if op.dsem not in dsems:
                    dsems[op.dsem] = self.stack.enter_context(nc.semaphore("dsem_%d" % len(dsems)))
                    cnt[op.dsem] = 0
                cnt[op.dsem] += 1
                op.sigval = 16 * cnt[op.dsem]
            elif op.signal:
                q = self.queue_of(op)
                cnt[q] = cnt.get(q, 0) + 1
                op.sigval = cnt[q]
        known = {q: {} for q in ("pe", "act", "dve", "pool", "sp")}
        for op in ops:
            q = self.queue_of(op)
            eng = self.eng_obj[op.eng]
            need = {}
            for j in op.deps:
                oj = ops[j]
                s = dsems[oj.dsem] if oj.dsem is not None else esem[self.queue_of(oj)]
                key = id(s)
                if key not in need or need[key][1] < oj.sigval:
                    need[key] = (s, oj.sigval)
            for key, (s, v) in need.items():
                if known[q].get(key, 0) >= v:
                    continue
                eng.wait_ge(s, v)
                known[q][key] = v
            ins = op.fn(eng)
            if op.dsem is not None:
                ins.then_inc(dsems[op.dsem], 16)
            elif op.signal:
                ins.then_inc(esem[q], 1)
        self.counts = dict((str(k), v) for k, v in cnt.items())
        for k, s in dsems.items():
            nc.sync.wait_ge(s, 16 * cnt[k])
        return len(ops)


class Builder:
    def __init__(self):
        self.stack = ExitStack()
        self.nc = bass.Bass("TRN2", target_bir_lowering=False)
        self.S = Sched(self.nc, self.stack)
        self.nbank = 0
        self.wuse_n = 0
        self.prefetched = False
        self.bank_set = [0, 1, 2, 3, 4]
        self.marks = []
        self.uid = 0

    def din(self, name, shape, dt=F32):
        return self.nc.dram_tensor(name, list(shape), dt, kind="ExternalInput").ap()

    def dout(self, name, shape):
        return self.nc.dram_tensor(name, list(shape), F32, kind="ExternalOutput").ap()

    def dscr(self, name, shape, dt=BF16):
        return self.nc.dram_tensor(name, list(shape), dt, kind="Internal").ap()

    def sb(self, name, shape, dt=F32):
        return self.stack.enter_context(self.nc.sbuf_tensor(name, list(shape), dt))

    def ps(self, name, shape, dt=F32):
        return self.stack.enter_context(self.nc.psum_tensor(name, list(shape), dt))

    def bank(self):
        bs = self.bank_set
        i = bs[self.nbank % len(bs)]
        self.nbank += 1
        return self.banks[i], ("ps", i)

    def key(self, name):
        self.uid += 1
        return (name, self.uid)

    def mm(self, out, lhsT, rhs, start, stop, reads, writes):
        self.S.add("pe", lambda e: e.matmul(out, lhsT, rhs, start=start, stop=stop), reads, writes)

    def tr(self, out, in_, ident, reads, writes):
        self.S.add("pe", lambda e: e.transpose(out, in_, ident), reads, writes)

    def act(self, out, in_, func, reads, writes, bias=None, scale=None, accum_out=None):
        kw = {}
        if bias is not None:
            kw["bias"] = bias
        if scale is not None:
            kw["scale"] = scale
        if accum_out is not None:
            kw["accum_out"] = accum_out
        self.S.add("act", lambda e: e.activation(out, in_, func, **kw), reads, writes)

    def tt(self, eng, out, in0, in1, op, reads, writes):
        self.S.add(eng, lambda e: e.tensor_tensor(out, in0, in1, op), reads, writes)

    def ts(self, eng, out, in0, s1, s2, op0, op1, reads, writes):
        if s2 is None:
            self.S.add(eng, lambda e: e.tensor_scalar(out, in0, s1, None, op0), reads, writes)
        else:
            self.S.add(eng, lambda e: e.tensor_scalar(out, in0, s1, s2, op0, op1), reads, writes)

    def stt(self, eng, out, in0, scalar, in1, op0, op1, reads, writes):
        self.S.add(eng, lambda e: e.scalar_tensor_tensor(out, in0, scalar, in1, op0=op0, op1=op1), reads, writes)

    def cp(self, eng, out, in_, reads, writes):
        if eng == "act":
            self.S.add("act", lambda e: e.copy(out, in_), reads, writes)
        else:
            self.S.add(eng, lambda e: e.tensor_copy(out, in_), reads, writes)

    def memset(self, eng, ap, val, writes):
        self.S.add(eng, lambda e: e.memset(ap, val), (), writes)

    def dma(self, q, out, in_, reads, writes, dsem, slow=False):
        if slow:
            self.S.add(q, lambda e: e.dma_start(out=out, in_=in_, allow_slow_non_contiguous=True), reads, writes, dsem)
        else:
            self.S.add(q, lambda e: e.dma_start(out=out, in_=in_), reads, writes, dsem)

    def interleave(self, fns_banks):
        main = self.S.ops
        streams = []
        for fn, banks in fns_banks:
            self.S.ops = []
            self.bank_set = banks
            fn()
            streams.append(self.S.ops)
        self.S.ops = main
        self.bank_set = [0, 1, 2, 3, 4]
        idx = [0] * len(streams)
        total = sum(len(st) for st in streams)
        for _ in range(total):
            best, bf = None, None
            for si, st in enumerate(streams):
                if idx[si] < len(st):
                    frac = idx[si] / len(st)
                    if bf is None or frac < bf:
                        best, bf = si, frac
            main.append(streams[best][idx[best]])
            idx[best] += 1

    def wload(self, scr_g, wkeys, kcn, width):
        i = self.wuse_n
        self.wuse_n += 1
        slot = i % 3
        buf = self.wbufs[slot]
        key = ("wuse", i)
        dst = buf[:, 0:kcn * width].rearrange("p (kc n) -> p kc n", kc=kcn)
        op = Op("sp", lambda e: e.dma_start(out=dst, in_=scr_g), tuple(wkeys),
                (key, ("wbuf", slot)) + ((("wuse", i - 3),) if i >= 3 else ()), ("wstream", slot))
        self.S.wuses.append((len(self.S.ops), op))
        return (lambda kc, a=0, b=width: buf[:, kc * width + a: kc * width + b]), key

    def build(self):
        nc = self.nc
        S = self.S
        sb, ps = self.sb, self.ps
        NBT = NB * 128
        K = lambda *a: tuple(a)
        xpre = self.din("xpre", [NPRE * 128, D])
        xov = self.din("xov", [128, D])
        xmain = self.din("xmain", [NMAIN * 128, D])
        xsam = self.din("xsam", [128, D])
        crow = self.din("crow", [40, 128])
        flag_d = self.din("flag", [128, 1])
        ck = self.din("ck", [4, 512, 512])
        cv = self.din("cv", [4, 512, 512])
        sgla = self.din("sgla", [4, 4, 64, 128])
        sconv = self.din("sconv", [4, 88, 128])
        w_ada = self.din("w_ada", [D, 6 * D])
        b_ada = self.din("b_ada", [48, 128])
        gvec = self.din("gvec", [32, 128])
        w_in = self.din("w_in", [D, DIN])
        w_gk2 = self.din("w_gk2", [16, 256])
        b_gk = self.din("b_gk", [1, 256])
        btab = self.din("btab", [2, 128, 8, 128])
        cvec = self.din("cvec", [128, 8])
        ggla = self.din("ggla", [128, 512])
        w_br_a = self.din("w_br_a", [512, D])
        w_br_b = self.din("w_br_b", [512, D])
        w_out = self.din("w_out", [D, D])
        w_up = self.din("w_up", [D, 2 * DFF])
        w_dw = self.din("w_dw", [3 * 44, 128])
        b_dw = self.din("b_dw", [44, 128])
        w_down = self.din("w_down", [DFF, D])

        ymain = self.dout("ymain", [NMAIN * 128, D])
        ysam = self.dout("ysam", [128, D])
        yov = self.dout("yov", [128, D])
        kp_o = self.dout("kp", [512, 512])
        vp_o = self.dout("vp", [512, 512])
        glap_o = self.dout("glap", [4, 64, 128])
        convp_o = self.dout("convp", [88, 128])
        ks_o = self.dout("ks", [128, 512])
        vs_o = self.dout("vs", [128, 512])
        glas_o = self.dout("glas", [4, 4, 64, 128])
        convs_o = self.dout("convs", [4, 88, 128])

        win_g = self.dscr("win_g", [10, 128, 8, 512])
        wgk_g = self.dscr("wgk_g", [128, 8, 16])
        wbra_g = self.dscr("wbra_g", [2, 128, 4, 512])
        wbrb_g = self.dscr("wbrb_g", [2, 128, 4, 512])
        wout_g = self.dscr("wout_g", [2, 128, 8, 512])
        wup_g = self.dscr("wup_g", [11, 128, 8, 512])
        wdown_g = self.dscr("wdown_g", [8, 128, NFF, 128])

        self.banks = [ps("bank%d" % i, [128, 512]) for i in range(5)]
        obank = [ps("obank%d" % i, [128, 512]) for i in range(2)]
        pbf = ps("pbf", [128, 1024], BF16)

        self.wbufs = [sb("wbuf%d" % i, [128, 4096], BF16) for i in range(3)]
        ident_f = sb("ident_f", [128, 128])
        ident_b = sb("ident_b", [128, 128], BF16)
        ones_b = sb("ones_b", [128, 128], BF16)
        triu_f = sb("triu_f", [128, 128])
        trisl_f = sb("trisl_f", [128, 128])
        ones_f = sb("ones_f", [128, 8])
        epsb = sb("epsb", [128, 1])
        mod = sb("mod", [128, 6, KC, 5])
        Am = sb("Am", [128, KC, 5]); Bm = sb("Bm", [128, KC, 5]); Gm = sb("Gm", [128, KC, 5])
        Af = sb("Af", [128, KC, 5]); Bf = sb("Bf", [128, KC, 5]); Gf = sb("Gf", [128, KC, 5])
        AmP = sb("AmP", [128, KC]); BmP = sb("BmP", [128, KC])
        AfP = sb("AfP", [128, KC]); BfP = sb("BfP", [128, KC])
        gT = sb("gT", [128, 32])
        badaT = sb("badaT", [128, 48])
        cT = sb("cT", [128, 40])
        siluT = sb("siluT", [128, 40], BF16)
        flag = sb("flag_s", [128, 1])
        wdwT = sb("wdwT", [128, 3, 44])
        bdwT = sb("bdwT", [128, 44])
        wgk_f = sb("wgk_f", [17, 256])
        wgk_b = sb("wgk_b", [17, 256], BF16)
        tab_prev = sb("tab_prev", [128, 8, 128], BF16)
        tab_own = sb("tab_own", [128, 8, 128], BF16)
        tab_mask = sb("tab_mask", [128, 128], BF16)
        cvec_s = sb("cvec_s", [128, 8])
        ggla_s = sb("ggla_s", [128, 512])
        stage = sb("stage", [128, 128])
        hist = sb("hist", [128, 44, 2])
        hist_s = sb("hist_s", [128, 4, 44, 2])
        h88 = sb("h88", [128, 2, 44])
        xin = [sb("xin%d" % i, [128, D]) for i in range(4)]
        xT = sb("xT", [128, KC, NBT])
        hT = sb("hT", [128, KC, NBT], BF16)
        sq = sb("sq", [128, 2, NBT], BF16)
        rbc = sb("rbc", [128, NBT])
        tmpf = sb("tmpf", [128, NBT])
        tmpf2 = sb("tmpf2", [128, NBT])
        qaT = sb("qaT", [128, 4, NBT], BF16)
        kaT = [sb("kaT%d" % i, [128, 4, 128], BF16) for i in range(8)]
        vaug = [sb("vaug%d" % i, [128, 8, 65], BF16) for i in range(8)]
        qbT = sb("qbT", [64, 4, NBT], BF16)
        kbT = sb("kbT", [64, 4, NBT], BF16)
        gkT = sb("gkT", [32, NBT], BF16)
        pT_raw = sb("pT_raw", [128, 2560])
        pT = pT_raw.bitcast(BF16)[:, :].rearrange("p (k h q) -> p k h q", k=5, h=8)
        ya_tok = sb("ya_tok", [128, 512], BF16)
        rden = sb("rden", [128, 8])
        kb_tok = sb("kb_tok", [128, 256])
        vb_tok = sb("vb_tok", [128, 4, 128], BF16)
        gb2 = sb("gb2", [128, 512])
        gtanh = sb("gtanh", [128, 512])
        Lsp = sb("Lsp", [128, 256])
        e_sb = sb("e_sb", [128, 256])
        e1 = sb("e1", [64, 4, 128]); e2 = sb("e2", [64, 4, 128])
        qtT = sb("qtT", [64, 4, 128], BF16); ktT = sb("ktT", [64, 4, 128], BF16)
        kend = sb("kend", [128, 256], BF16)
        dec = sb("dec", [64, 4])
        attT = sb("attT", [128, 4, 128], BF16)
        Sst = sb("Sst", [64, 4, 128])
        Sbf = sb("Sbf", [64, 4, 128], BF16)
        ssq = sb("ssq", [128, 4]); rgl = sb("rgl", [128, 4])
        yb_tok = sb("yb_tok", [128, 512], BF16)
        sgA = sb("sgA", [128, NBT]); sgB = sb("sgB", [128, NBT])
        m2 = sb("m2", [128, KC, NBT])
        ua = [sb("ua%d" % i, [128, NBT + 8]) for i in range(2)]
        y0 = [sb("y0_%d" % i, [128, NBT]) for i in range(2)]
        ua_sets = [ua, [pT_raw[:, 0:NBT + 8], pT_raw[:, NBT + 8:2 * NBT + 16]]]
        y0_sets = [y0, [pT_raw[:, 2 * NBT + 16:3 * NBT + 16], pT_raw[:, 3 * NBT + 16:4 * NBT + 16]]]
        actT = sb("actT", [128, NFF, NBT], BF16)
        mergedT = lambda c: actT[:, c, :]
        MK = lambda c: K("actT", c)
        yaT = lambda c: actT[:, 8 + c, :]
        YAK = tuple(K("actT", 8 + c) for c in range(4))
        ybT = lambda c: actT[:, 12 + c, :]
        YBK = tuple(K("actT", 12 + c) for c in range(4))
        m2b = m2.bitcast(BF16)
        kcT = lambda c: m2b[:, c, 0:512]
        vcaug = sb("vcaug", [128, 4, 8, 65], BF16)
        vown = sb("vown", [32, 4, 8, 65], BF16)
        Ss, Ssb = Sst, Sbf

        def const_mask(t, keyname, pattern, cm, base, cmp_op):
            self.memset("pool", t[:], 1.0, (K(keyname),))
            S.add("pool", lambda e: e.affine_select(t[:], t[:], pattern=pattern, compare_op=cmp_op, fill=0.0,
                                                    base=base, channel_multiplier=cm), (K(keyname),), (K(keyname),))
        self.memset("pool", ident_f[:], 0.0, (K("ident_f"),))
        S.add("pool", lambda e: e.affine_select(ident_f[:], ident_f[:], pattern=[[-1, 128]], compare_op=ALU.not_equal,
                                                fill=1.0, base=0, channel_multiplier=1), (K("ident_f"),), (K("ident_f"),))
        const_mask(triu_f, "triu_f", [[1, 128]], -1, 0, ALU.is_ge)
        const_mask(trisl_f, "trisl_f", [[-1, 128]], 1, -1, ALU.is_ge)
        self.cp("dve", ident_b[:], ident_f[:], (K("ident_f"),), (K("ident_b"),))
        self.memset("dve", ones_b[:], 1.0, (K("ones_b"),))
        self.memset("dve", ones_f[:], 1.0, (K("ones_f"),))
        self.memset("dve", epsb[:], EPS, (K("epsb"),))
        self.memset("dve", gkT[:], 1.0, (K("gkT_ones"),))
        self.memset("dve", hist[:], 0.0, tuple(K("hist", jj) for jj in range(44)))
        self.memset("dve", Sst[:], 0.0, tuple(K("Sst", h) for h in range(4)))
        self.memset("dve", Sbf[:], 0.0, (K("Sst", "b"),))

        def load_T(src2d, rows, dst, kname, view=None):
            self.dma("pool_dma", stage[0:rows, :], src2d, (), (K("stage"),), ("stage",))
            bk, bkey = self.bank()
            self.tr(bk[:, 0:rows], stage[0:rows, :], ident_f[0:rows, 0:rows], (K("stage"), K("ident_f")), (bkey,))
            src = bk[:, 0:rows] if view is None else view(bk[:, 0:rows])
            self.cp("dve", dst, src, (bkey,), (K(kname),))

        load_T(gvec, 32, gT[:], "gT")
        load_T(b_ada, 48, badaT[:], "badaT")
        load_T(crow, 40, cT[:], "cT")
        for j in range(3):
            load_T(w_dw[j * 44:(j + 1) * 44, :], 44, wdwT[:, j, :], "wdwT%d" % j)
        load_T(b_dw, 44, bdwT[:], "bdwT")
        self.dma("pool_dma", flag[:], flag_d, (), (K("flag"),), ("misc", 0))
        self.dma("pool_dma", wgk_f[0:16, :], w_gk2, (), (K("wgk_f0"),), ("misc", 1))
        self.dma("pool_dma", wgk_f[16:17, :], b_gk, (), (K("wgk_f1"),), ("misc", 2))
        self.cp("dve", wgk_b[:], wgk_f[:], (K("wgk_f0"), K("wgk_f1")), (K("wgk_b"),))
        self.dma("pool_dma", cvec_s[:], cvec, (), (K("cvec"),), ("misc", 3))
        self.dma("pool_dma", ggla_s[:], ggla, (), (K("ggla"),), ("misc", 4))
        self.ts("dve", ggla_s[:], ggla_s[:], 0.5, None, ALU.mult, None, (K("ggla"),), (K("ggla"),))
        tabf = xin[0][:, :].rearrange("p (h q) -> p h q", h=8)
        for ti, tdst in ((0, tab_prev), (1, tab_own)):
            self.dma("pool_dma", tabf, btab[ti], (), (K("xin", 0), K("xinv", 0)), ("misc", 5))
            for h in range(8):
                self.ts("dve", tdst[:, h, :], tabf[:, h, :], cvec_s[:, h:h + 1], None, ALU.subtract, None,
                        (K("xin", 0), K("xinv", 0), K("cvec")), (K("tab", ti, h),))
        self.memset("dve", tab_own[64:128, :, 0:64], NEG, tuple(K("tab", 1, h) for h in range(8)))
        for ti, tdst in ((0, tab_prev), (1, tab_own)):
            self.act(tdst[:], tdst[:], AF.Exp, tuple(K("tab", ti, h) for h in range(8)), (K("etab", ti),))
        self.memset("dve", tab_mask[:], 1.0, (K("tab_mask"),))
        self.memset("dve", tab_mask[0:64, 64:128], 0.0, (K("tab_mask"),))

        def cast_group(src_cols, dst_g, name, g):
            self.dma("pool_dma", dst_g, src_cols.rearrange("(kc p) n -> p kc n", p=128), (), (K("scr", name, g),), ("cast", name, g))
            return (K("scr", name, g),)

        self.act(tmpf[:, 0:40], cT[:], AF.Tanh, (K("cT"),), (K("tmpf"),), scale=0.5)
        self.ts("dve", tmpf[:, 0:40], tmpf[:, 0:40], 0.5, 0.5, ALU.mult, ALU.add, (K("tmpf"),), (K("tmpf"),))
        self.tt("dve", siluT[:], tmpf[:, 0:40], cT[:], ALU.mult, (K("tmpf"), K("cT")), (K("siluT"),))
        siluv = siluT[:].rearrange("p (s k) -> p k s", k=8)
        modkeys = {}
        k8 = lambda n: tuple(K(n, kc) for kc in range(8))

        def ada_groups(glist):
            for g in glist:
                sl = g % 2
                buf = actT[:, sl * 8:(sl + 1) * 8, :].rearrange("p c n -> p (c n)")
                bkeys = tuple(K("actT", sl * 8 + c) for c in range(8))
                self.dma("pool_dma", buf.rearrange("p (kc n) -> p kc n", kc=8),
                         w_ada.rearrange("(kc p) n -> p kc n", p=128)[:, :, g * 512:(g + 1) * 512], (), bkeys, ("ada", sl))
                bk, bkey = self.bank()
                for oc in range(4):
                    for kc in range(8):
                        self.mm(bk[:, oc * 8:oc * 8 + 5], buf[:, kc * 512 + oc * 128: kc * 512 + (oc + 1) * 128], siluv[:, kc, :],
                                kc == 0, kc == 7, bkeys + (K("siluT"),), (bkey,))
                for oc in range(4):
                    ch = g * 4 + oc
                    self.ts("dve", mod[:, ch // 8, ch % 8, :], bk[:, oc * 8:oc * 8 + 5], badaT[:, ch:ch + 1], None, ALU.add, None,
                            (bkey, K("badaT")), (K("mod", ch),))

        ada_groups(range(0, 4))
        mk1 = tuple(K("mod", ch) for ch in range(16))
        for kc in range(8):
            self.ts("dve", Am[:, kc, :], mod[:, 1, kc, :], 1.0, gT[:, kc:kc + 1], ALU.add, ALU.mult, mk1 + (K("gT"),), (K("Am", kc),))
        self.cp("dve", Bm[:], mod[:, 0, :, :], mk1, (K("Bm"),))
        self.ts("dve", AmP[:], Am[:, :, 0], flag[:, 0:1], None, ALU.mult, None, k8("Am") + (K("flag"),), (K("AmP"),))
        self.ts("dve", BmP[:], Bm[:, :, 0], flag[:, 0:1], None, ALU.mult, None, (K("Bm"), K("flag")), (K("BmP"),))
        modkeys.update({"Am": k8("Am"), "Bm": (K("Bm"),), "AmP": (K("AmP"),), "BmP": (K("BmP"),)})

        def ada_part2():
            ada_groups(range(4, 12))
            mk2 = tuple(K("mod", ch) for ch in range(16, 48))
            for kc in range(8):
                rw = mk2 + (K("gT"),)
                self.ts("dve", Af[:, kc, :], mod[:, 4, kc, :], 1.0, gT[:, 16 + kc:17 + kc], ALU.add, ALU.mult, rw, (K("Af", kc),))
                self.ts("dve", Gm[:, kc, :], mod[:, 2, kc, :], gT[:, 8 + kc:9 + kc], None, ALU.mult, None, rw, (K("Gm", kc),))
                self.ts("dve", Gf[:, kc, :], mod[:, 5, kc, :], gT[:, 24 + kc:25 + kc], None, ALU.mult, None, rw, (K("Gf", kc),))
            self.cp("dve", Bf[:], mod[:, 3, :, :], mk2, (K("Bf"),))
            self.ts("dve", AfP[:], Af[:, :, 0], flag[:, 0:1], None, ALU.mult, None, k8("Af") + (K("flag"),), (K("AfP"),))
            self.ts("dve", BfP[:], Bf[:, :, 0], flag[:, 0:1], None, ALU.mult, None, (K("Bf"), K("flag")), (K("BfP"),))
        modkeys.update({"Af": k8("Af"), "Gm": k8("Gm"), "Gf": k8("Gf"), "Bf": (K("Bf"),), "AfP": (K("AfP"),), "BfP": (K("BfP"),)})

        kwin = {}
        for g in (3, 4, 1, 2):
            kwin[g] = cast_group(w_in[:, g * 512:(g + 1) * 512], win_g[g], "win", g)
        self.dma("pool_dma", wgk_g, w_in.rearrange("(kc p) n -> p kc n", p=128)[:, :, 5120:5136], (), (K("scr", "wgk"),), ("cast", "wgk"))
        kwgk = (K("scr", "wgk"),)
        for g in (0, 5, 6, 7, 8, 9):
            kwin[g] = cast_group(w_in[:, g * 512:(g + 1) * 512], win_g[g], "win", g)
        kwbra = [cast_group(w_br_a[:, g * 512:(g + 1) * 512], wbra_g[g], "wbra", g) for g in range(2)]
        kwbrb = [cast_group(w_br_b[:, g * 512:(g + 1) * 512], wbrb_g[g], "wbrb", g) for g in range(2)]
        kwout = [cast_group(w_out[:, g * 512:(g + 1) * 512], wout_g[g], "wout", g) for g in range(2)]
        kwup = [cast_group(w_up[:, g * 512:(g + 1) * 512], wup_g[g], "wup", g) for g in range(11)]
        kwdown = [cast_group(w_down[:, c * 128:(c + 1) * 128], wdown_g[c], "wdown", c) for c in range(8)]

        def load_xT(src_rows, ntile):
            for t in range(ntile):
                xb = xin[t % 2]
                kx = K("xin", t % 2)
                self.dma("sp", xb[:], src_rows[t * 128:(t + 1) * 128, :], (), (kx, K("xinv", t % 2)), ("xin", t % 2))
                for half in range(2):
                    bk, bkey = self.bank()
                    for j in range(4):
                        kc = half * 4 + j
                        self.tr(bk[:, j * 128:(j + 1) * 128], xb[:, kc * 128:(kc + 1) * 128], ident_f[:],
                                (kx if half == 0 else K("xinv", t % 2), K("ident_f")), (bkey,))
                    self.cp("act", xT[:, half * 4:half * 4 + 4, t * 128:(t + 1) * 128],
                            bk[:].rearrange("p (j n) -> p j n", j=4), (bkey,), tuple(K("xT", half * 4 + j, t) for j in range(4)))

        def rms_bcast(srcT, n, src_keys_fn):
            bk, bkey = self.bank()
            for kc in range(8):
                if kc % 2 == 0:
                    self.act(sq[:, 0, 0:n], srcT[:, kc, 0:n], AF.Square, src_keys_fn(kc), (K("sq", 0),))
                else:
                    self.tt("dve", sq[:, 1, 0:n], srcT[:, kc, 0:n], srcT[:, kc, 0:n], ALU.mult, src_keys_fn(kc), (K("sq", 1),))
                self.mm(bk[:, 0:n], ones_b[:], sq[:, kc % 2, 0:n], kc == 0, kc == 7, (K("ones_b"), K("sq", kc % 2)), (bkey,))
            self.act(tmpf[:, 0:n], bk[:, 0:n], AF.Ln, (bkey, K("epsb")), (K("tmpf"),), bias=epsb[:, 0:1], scale=1.0 / D)
            self.act(rbc[:, 0:n], tmpf[:, 0:n], AF.Exp, (K("tmpf"),), (K("rbc"),), scale=-0.5)

        def modulate(n, ntile, Asc, Bsc, Afn, Bfn):
            for kc in range(8):
                xk = tuple(K("xT", kc, t) for t in range(ntile))
                self.tt("dve", tmpf2[:, 0:n], xT[:, kc, 0:n], rbc[:, 0:n], ALU.mult, xk + (K("rbc"),), (K("tmpf2"),))
                self.act(hT[:, kc, 0:n], tmpf2[:, 0:n], AF.Identity, (K("tmpf2"),) + modkeys[Asc] + modkeys[Bsc], (K("hT", kc),),
                         bias=Bfn(kc), scale=Afn(kc))

        def modulate_seq(At, Bt, Asc, Bsc):
            for kc in range(8):
                xk = (K("xT", kc, 0),)
                v3 = lambda ap: ap.rearrange("p (s w) -> p s w", s=4)
                self.tt("dve", tmpf2[:, 0:128], xT[:, kc, 0:128], rbc[:, 0:128], ALU.mult, xk + (K("rbc"),), (K("tmpf2"),))
                self.tt("dve", v3(tmpf2[:, 0:128]), v3(tmpf2[:, 0:128]), At[:, kc, 1:5].to_broadcast([128, 4, 32]), ALU.mult,
                        (K("tmpf2"),) + modkeys[Asc], (K("tmpf2"),))
                self.tt("dve", v3(hT[:, kc, 0:128]), v3(tmpf2[:, 0:128]), Bt[:, kc, 1:5].to_broadcast([128, 4, 32]), ALU.add,
                        (K("tmpf2"),) + modkeys[Bsc], (K("hT", kc),))

        xkeys = lambda ntile: (lambda kc: tuple(K("xT", kc, t) for t in range(ntile)))

        def heads_fm(wg, wk, col0, nheads, n, evac):
            for h in range(nheads):
                bk, bkey = self.bank()
                for kc in range(8):
                    self.mm(bk[0:64, 0:n], wg(kc, col0 + h * 64, col0 + (h + 1) * 64), hT[:, kc, 0:n], kc == 0, kc == 7, (wk, K("hT", kc)), (bkey,))
                evac(h, bk, bkey)

        def pairs_fm(wg, wk, n, evac):
            for c in range(4):
                bk, bkey = self.bank()
                for kc in range(8):
                    self.mm(bk[:, 0:n], wg(kc, c * 128, (c + 1) * 128), hT[:, kc, 0:n], kc == 0, kc == 7, (wk, K("hT", kc)), (bkey,))
                evac(c, bk, bkey)

        def units_tm(wg, wk, col0, ncols, units, evac):
            for ui, (c0, C) in enumerate(units):
                bk, bkey = self.bank()
                for kc in range(8):
                    self.mm(bk[0:C, 0:ncols], hT[:, kc, c0:c0 + C], wg(kc, col0, col0 + ncols), kc == 0, kc == 7, (wk, K("hT", kc)), (bkey,))
                evac(ui, bk, bkey)

        def gla_block(C, col0, S_t, S_b, Skey, out_cb, prefix_only):
            bk, bkey = self.bank()
            self.mm(bk[0:C, 0:256], gkT[0:17, col0:col0 + C], wgk_b[0:17, :], True, True, (K("gkT"), K("gkT_ones"), K("wgk_b")), (bkey,))
            self.act(e_sb[0:C, :], bk[0:C, 0:256], AF.Exp, (bkey,), (K("e_sb"),), scale=-1.0)
            self.act(Lsp[0:C, :], e_sb[0:C, :], AF.Ln, (K("e_sb"),), (K("Lsp"),), bias=1.0)
            bk2, bkey2 = self.bank()
            self.mm(bk2[0:C, 0:256], trisl_f[0:C, 0:C], Lsp[0:C, :], True, True, (K("trisl_f"), K("Lsp")), (bkey2,))
            self.act(e_sb[0:C, :], bk2[0:C, 0:256], AF.Exp, (bkey2,), (K("e_sb"),), scale=-1.0 / 16)
            self.tt("dve", kend[0:C, :], kb_tok[0:C, :], e_sb[0:C, :], ALU.mult, (K("kb_tok"), K("e_sb")), (K("kend"),))
            bk3, bkey3 = self.bank()
            for h in range(4):
                self.mm(bk3[0:64, h * 128:h * 128 + C], Lsp[0:C, h * 64:(h + 1) * 64], triu_f[0:C, 0:C], True, True,
                        (K("Lsp"), K("triu_f")), (bkey3,))
            b3 = bk3[0:64, :].rearrange("p (c t) -> p c t", c=4)
            self.act(dec[:, :], b3[:, :, C - 1], AF.Exp, (bkey3,), (K("dec"),), scale=-1.0 / 16)
            if not prefix_only:
                self.act(e1[:, :, 0:C], b3[:, :, 0:C], AF.Exp, (bkey3,), (K("e1"),), scale=-1.0 / 16)
                self.act(e2[:, :, 0:C], b3[:, :, 0:C], AF.Exp, (bkey3,), (K("e2"),), scale=1.0 / 16)
                self.stt("dve", qtT[:, :, 0:C], qbT[:, :, col0:col0 + C], 0.125, e1[:, :, 0:C], ALU.mult, ALU.mult,
                         tuple(K("qbT", j) for j in range(4)) + (K("e1"),), (K("qtT"),))
                self.tt("dve", ktT[:, :, 0:C], kbT[:, :, col0:col0 + C], e2[:, :, 0:C], ALU.mult, tuple(K("kbT", j) for j in range(4)) + (K("e2"),), (K("ktT"),))
                bk4, bkey4 = self.bank()
                for h in range(4):
                    self.mm(bk4[0:C, h * 128:h * 128 + C], ktT[:, h, 0:C], qtT[:, h, 0:C], True, True,
                            (K("ktT"), K("qtT")), (bkey4,))
                for h in range(4):
                    self.tt("dve", attT[0:C, h, 0:C], bk4[0:C, h * 128:h * 128 + C], triu_f[0:C, 0:C], ALU.mult,
                            (bkey4, K("triu_f")), (K("attT", h),))
                bk5, bkey5 = self.bank()
                for h in range(4):
                    self.mm(bk5[0:C, h * 128:(h + 1) * 128], attT[0:C, h, 0:C], vb_tok[0:C, h, :], True, False, (K("attT", h), K("vb_tok")), (bkey5,))
                    self.mm(bk5[0:C, h * 128:(h + 1) * 128], qtT[:, h, 0:C], S_b[:, h, :], False, True,
                            (K("qtT"), Skey + ("b",)), (bkey5,))
                out_cb(bk5, bkey5)
            bk6, bkey6 = self.bank()
            for h in range(4):
                self.mm(bk6[0:64, h * 128:(h + 1) * 128], kend[0:C, h * 64:(h + 1) * 64], vb_tok[0:C, h, :], True, True, (K("kend"), K("vb_tok")), (bkey6,))
            for h in range(4):
                self.stt("dve", S_t[:, h, :], S_t[:, h, :], dec[:, h:h + 1], bk6[0:64, h * 128:(h + 1) * 128],
                         ALU.mult, ALU.add, (Skey + (h,), K("dec"), bkey6), (Skey + (h,),))
            self.cp("act", S_b[:], S_t[:], tuple(Skey + (h,) for h in range(4)), (Skey + ("b",),))

        def gla_out(C, obk, obkey):
            self.memset("dve", ssq[0:C, :], 0.0, tuple(K("ssq", h) for h in range(4)))
            for h in range(4):
                self.act(attT[0:C, h, :], obk[0:C, h * 128:(h + 1) * 128], AF.Square, (obkey,), (K("attT", h), K("ssq", h)), accum_out=ssq[0:C, h:h + 1])
            self.act(rgl[0:C, :], ssq[0:C, :], AF.Ln, tuple(K("ssq", h) for h in range(4)) + (K("epsb"),), (K("rgl"),), bias=epsb[0:C, 0:1], scale=1.0 / 128)
            self.act(rgl[0:C, :], rgl[0:C, :], AF.Exp, (K("rgl"),), (K("rgl"),), scale=-0.5)
            self.act(gtanh[0:C, :], gb2[0:C, :], AF.Tanh, (K("gb2"),), (K("gtanh"),), scale=0.5)
            self.stt("dve", gtanh[0:C, :], gtanh[0:C, :], 1.0, gb2[0:C, :], ALU.add, ALU.mult, (K("gtanh"), K("gb2")), (K("gtanh"),))
            self.tt("dve", gtanh[0:C, :], gtanh[0:C, :], ggla_s[0:C, :], ALU.mult, (K("gtanh"), K("ggla")), (K("gtanh"),))
            for h in range(4):
                self.stt("dve", yb_tok[0:C, h * 128:(h + 1) * 128], obk[0:C, h * 128:(h + 1) * 128], rgl[0:C, h:h + 1],
                         gtanh[0:C, h * 128:(h + 1) * 128], ALU.mult, ALU.mult, (obkey, K("rgl"), K("gtanh")), (K("yb_tok", h),))

        def gla_units(units, pre, S_of, with_fn=None):
            wgk_, wkk = self.wload(win_g[G_QKB], kwin[G_QKB], 8, 512)
            wgv_, wkv = self.wload(win_g[G_VB], kwin[G_VB], 8, 512)
            wgg_, wkg = (None, None) if pre else self.wload(win_g[G_GB], kwin[G_GB], 8, 512)
            body = lambda: gla_body(units, pre, S_of, wgk_, wkk, wgv_, wkv, wgg_, wkg)
            if with_fn is None:
                body()
            else:
                self.interleave([(body, [0, 1, 2]), (with_fn, [3, 4])])

        def gla_body(units, pre, S_of, wgk_, wkk, wgv_, wkv, wgg_, wkg):
            for ui, (col0, C) in enumerate(units):
                bk, bkey = self.bank()
                for kc in range(8):
                    self.mm(bk[0:C, 0:256], hT[:, kc, col0:col0 + C], wgk_(kc, 256, 512), kc == 0, kc == 7, (wkk, K("hT", kc)), (bkey,))
                self.cp("act", kb_tok[0:C, :], bk[0:C, 0:256], (bkey,), (K("kb_tok"),))
                bk, bkey = self.bank()
                for kc in range(8):
                    self.mm(bk[0:C, :], hT[:, kc, col0:col0 + C], wgv_(kc), kc == 0, kc == 7, (wkv, K("hT", kc)), (bkey,))
                self.cp("act", vb_tok[0:C, :, :].rearrange("p h e -> p (h e)"), bk[0:C, :], (bkey,), (K("vb_tok"),))
                if not pre:
                    bk, bkey = self.bank()
                    for kc in range(8):
                        self.mm(bk[0:C, :], hT[:, kc, col0:col0 + C], wgg_(kc), kc == 0, kc == 7, (wkg, K("hT", kc)), (bkey,))
                    self.cp("act", gb2[0:C, :], bk[0:C, :], (bkey,), (K("gb2"),))
                S_t, S_b, Skey, before_fn, after_fn = S_of(ui)
                if before_fn is not None:
                    before_fn()

                def out_cb(obk, obkey, col0=col0, C=C, ui=ui):
                    gla_out(C, obk, obkey)
                    for c in range(4):
                        self.tr(pbf[:, 512 + c * 128:512 + c * 128 + C], yb_tok[0:C, c * 128:(c + 1) * 128], ident_b[0:C, 0:C],
                                (K("yb_tok", c), K("ident_b")), (K("pbf"),))
                    for c in range(4):
                        self.cp("act", ybT(c)[:, col0:col0 + C], pbf[:, 512 + c * 128:512 + c * 128 + C], (K("pbf"),), (YBK[c],))
                gla_block(C, col0, S_t, S_b, Skey, out_cb, pre)
                if after_fn is not None:
                    after_fn()

        def attn_tile(tglob, tl):
            pTv = lambda kb, par: pT[:, kb, :, :].rearrange("p (c two) q -> p two c q", two=2)[:, par]
            for kb in range(5):
                slot = (tglob - 4 + kb) % 8
                banks = [self.bank(), self.bank()]
                for c in range(4):
                    for par in range(2):
                        pb = par * 64
                        bk, bkey = banks[par]
                        self.mm(bk[:, c * 128:(c + 1) * 128], kaT[slot][pb:pb + 64, c, :], qaT[pb:pb + 64, c, tl * 128:(tl + 1) * 128],
                                True, True, (K("kaT", slot, c), K("qaT", c)), (bkey,))
                for par in range(2):
                    bk, bkey = banks[par]
                    self.act(pTv(kb, par), bk[:].rearrange("p (c q) -> p c q", c=4), AF.Exp, (bkey,), (K("pT", kb, par),))
                tab = {0: (tab_mask[:, :].to_broadcast([128, 8, 128]) if False else None), 3: tab_prev, 4: tab_own}.get(kb)
                pk = (K("pT", kb, 0), K("pT", kb, 1))
                if kb == 0:
                    for h0 in range(0, 8, 4):
                        pass
                    self.tt("dve", pT[:, 0, :, :], pT[:, 0, :, :], bass.AP(tab_mask, 0, [[128, 128], [0, 8], [1, 128]]), ALU.mult,
                            pk + (K("tab_mask"),), pk)
                elif tab is not None:
                    self.tt("dve", pT[:, kb, :, :], pT[:, kb, :, :], tab[:, :, :], ALU.mult, pk + (K("etab", kb - 3),), pk)
            for hg in range(2):
                ob, obkey = obank[hg], K("obank", hg)
                for hh in range(4):
                    h = hg * 4 + hh
                    for kb in range(5):
                        slot = (tglob - 4 + kb) % 8
                        self.mm(ob[:, hh * 65:(hh + 1) * 65], pT[:, kb, h, :], vaug[slot][:, h, :], kb == 0, kb == 4,
                                (K("pT", kb, h % 2), K("vaug", slot), K("vaug1", slot)), (obkey,))
            for hg in range(2):
                ob, obkey = obank[hg], K("obank", hg)
                ov = ob[:, 0:260].rearrange("p (h e) -> p h e", h=4)
                self.ts("dve", rden[:, hg * 4:(hg + 1) * 4], ov[:, :, 64], 1e-30, None, ALU.add, None, (obkey,), (K("rden", hg),))
                S.add("dve", lambda e, hg=hg: e.reciprocal(rden[:, hg * 4:(hg + 1) * 4], rden[:, hg * 4:(hg + 1) * 4]), (K("rden", hg),), (K("rden", hg),))
                for hh in range(4):
                    h = hg * 4 + hh
                    self.ts("dve", ya_tok[:, h * 64:(h + 1) * 64], ov[:, hh, 0:64], rden[:, h:h + 1], None, ALU.mult, None,
                            (obkey, K("rden", hg)), (K("ya_tok", h),))
            yk = tuple(K("ya_tok", h) for h in range(8))
            for c in range(4):
                self.tr(pbf[:, c * 128:(c + 1) * 128], ya_tok[:, c * 128:(c + 1) * 128], ident_b[:], yk + (K("ident_b"),), (K("pbf"),))
            for c in range(4):
                self.cp("act", yaT(c)[:, tl * 128:(tl + 1) * 128], pbf[:, c * 128:(c + 1) * 128], (K("pbf"),), (YAK[c],))

        def attn_sample(s, slot):
            for rb in range(4):
                xb = xin[rb % 2]
                kx = K("xin", rb % 2)
                self.dma("sp", xb[:, 0:512], ck[s, rb * 128:(rb + 1) * 128, :], (), (kx,), ("xin", rb % 2))
                bk, bkey = self.bank()
                for c in range(4):
                    self.tr(bk[:, c * 128:(c + 1) * 128], xb[:, c * 128:(c + 1) * 128], ident_f[:], (kx, K("ident_f")), (bkey,))
                for c in range(4):
                    self.cp("act", kcT(c)[:, rb * 128:(rb + 1) * 128], bk[:, c * 128:(c + 1) * 128], (bkey,), (K("m2", c),))
                self.dma("sp", xb[:, 512:1024], cv[s, rb * 128:(rb + 1) * 128, :], (), (K("xinv", rb % 2),), ("xinv", rb % 2))
                self.cp("dve", vcaug[:, rb, :, 0:64], xb[:, 512:1024].rearrange("p (h e) -> p h e", h=8), (K("xinv", rb % 2),), (K("vcaug", rb),))
                self.cp("dve", vcaug[:, rb, :, 64], ones_f[:, 0:8], (K("ones_f"),), (K("vcaug1", rb),))
            q0 = s * 32
            for kb in range(5):
                kn = 128 if kb < 4 else 32
                banks = [self.bank(), self.bank()]
                for c in range(4):
                    for par in range(2):
                        pb = par * 64
                        bk, bkey = banks[par]
                        if kb < 4:
                            lhs = kcT(c)[pb:pb + 64, kb * 128:(kb + 1) * 128]
                            rk = (K("m2", c),)
                        else:
                            lhs = kaT[slot][pb:pb + 64, c, q0:q0 + 32]
                            rk = (K("kaT", slot, c),)
                        self.mm(bk[0:kn, c * 32:(c + 1) * 32], lhs, qaT[pb:pb + 64, c, q0:q0 + 32], True, True, rk + (K("qaT", c),), (bkey,))
                for par in range(2):
                    bk, bkey = banks[par]
                    dstv = pT[0:kn, kb, :, 0:32].rearrange("p (c two) q -> p two c q", two=2)[:, par]
                    self.act(dstv, bk[0:kn, 0:128].rearrange("p (c q) -> p c q", c=4), AF.Exp, (bkey,), (K("pT", kb, par),))
                pk = (K("pT", kb, 0), K("pT", kb, 1))
                if kb == 3:
                    self.tt("dve", pT[:, 3, :, 0:32], pT[:, 3, :, 0:32], tab_prev[:, :, 0:32], ALU.mult, pk + (K("etab", 0),), pk)
                elif kb == 4:
                    self.tt("dve", pT[0:32, 4, :, 0:32], pT[0:32, 4, :, 0:32], tab_own[0:32, :, 0:32], ALU.mult, pk + (K("etab", 1),), pk)
            for h in range(8):
                hg, hh = h // 4, h % 4
                for kb in range(5):
                    kn = 128 if kb < 4 else 32
                    rhs = vcaug[:, kb, h, :] if kb < 4 else vown[0:32, s, h, :]
                    rk = (K("vcaug", kb), K("vcaug1", kb)) if kb < 4 else (K("vown", s), K("vown1", s))
                    self.mm(obank[hg][0:32, hh * 65:(hh + 1) * 65], pT[0:kn, kb, h, 0:32], rhs, kb == 0, kb == 4,
                            (K("pT", kb, h % 2),) + rk, (K("obank", hg),))
            for hg in range(2):
                ob, obkey = obank[hg], K("obank", hg)
                ov = ob[0:32, 0:260].rearrange("p (h e) -> p h e", h=4)
                S.add("dve", lambda e, ov=ov, hg=hg: e.reciprocal(rden[0:32, hg * 4:(hg + 1) * 4], ov[:, :, 64]), (obkey,), (K("rden", hg),))
                for hh in range(4):
                    h = hg * 4 + hh
                    self.ts("dve", ya_tok[0:32, h * 64:(h + 1) * 64], ov[:, hh, 0:64], rden[0:32, h:h + 1], None, ALU.mult, None,
                            (obkey, K("rden", hg)), (K("ya_tok", h),))
            yk = tuple(K("ya_tok", h) for h in range(8))
            for c in range(4):
                self.tr(pbf[:, c * 128:c * 128 + 32], ya_tok[0:32, c * 128:(c + 1) * 128], ident_b[0:32, 0:32], yk + (K("ident_b"),), (K("pbf"),))
            for c in range(4):
                self.cp("act", yaT(c)[:, q0:q0 + 32], pbf[:, c * 128:c * 128 + 32], (K("pbf"),), (YAK[c],))

        def resid(n, ntile, Gfn, gname):
            for kc in range(8):
                tb, tk = (tmpf2, K("tmpf2")) if kc % 2 == 0 else (tmpf, K("tmpf"))
                self.tt("pool", tb[:, 0:n], m2[:, kc, 0:n], rbc[:, 0:n], ALU.mult, (K("m2", kc), K("rbc")), (tk,))
                xk = tuple(K("xT", kc, t) for t in range(ntile))
                self.stt("dve", xT[:, kc, 0:n], tb[:, 0:n], Gfn(kc), xT[:, kc, 0:n], ALU.mult, ALU.add,
                         (tk,) + modkeys[gname] + xk, xk)

        def resid_seq(Gt, gname):
            v3 = lambda ap: ap.rearrange("p (s w) -> p s w", s=4)
            for kc in range(8):
                xk = (K("xT", kc, 0),)
                self.tt("dve", tmpf2[:, 0:128], m2[:, kc, 0:128], rbc[:, 0:128], ALU.mult, (K("m2", kc), K("rbc")), (K("tmpf2"),))
                self.tt("dve", v3(tmpf2[:, 0:128]), v3(tmpf2[:, 0:128]), Gt[:, kc, 1:5].to_broadcast([128, 4, 32]), ALU.mult,
                        (K("tmpf2"),) + modkeys[gname], (K("tmpf2"),))
                self.tt("dve", xT[:, kc, 0:128], xT[:, kc, 0:128], tmpf2[:, 0:128], ALU.add, (K("tmpf2"),) + xk, xk)

        def merge_and_out(n):
            def ev_gate(dst, dkey):
                def f(bk, bkey):
                    self.act(dst[:, 0:n], bk[:, 0:n], AF.Tanh, (bkey,), (dkey,), scale=0.5)
                    self.ts("dve", dst[:, 0:n], dst[:, 0:n], 0.5, 0.5, ALU.mult, ALU.add, (dkey,), (dkey,))
                return f
            for G in range(2):
                wga, wka = self.wload(win_g[G_GA + G], kwin[G_GA + G], 8, 512)
                wba, wkba = self.wload(wbra_g[G], kwbra[G], 4, 512)
                for j in range(4):
                    c = G * 4 + j
                    bk, bkey = self.bank()
                    for kc in range(8):
                        self.mm(bk[:, 0:n], wga(kc, j * 128, (j + 1) * 128), hT[:, kc, 0:n], kc == 0, kc == 7, (wka, K("hT", kc)), (bkey,))
                    ev_gate(sgA, K("sgA"))(bk, bkey)
                    bk, bkey = self.bank()
                    for kc in range(4):
                        self.mm(bk[:, 0:n], wba(kc, j * 128, (j + 1) * 128), yaT(kc)[:, 0:n], kc == 0, kc == 3, (wkba, YAK[kc]), (bkey,))
                    self.tt("dve", m2[:, c, 0:n], sgA[:, 0:n], bk[:, 0:n], ALU.mult, (K("sgA"), bkey), (K("m2", c),))
                wgb, wkb = self.wload(win_g[G_GBR + G], kwin[G_GBR + G], 8, 512)
                wbb, wkbb = self.wload(wbrb_g[G], kwbrb[G], 4, 512)
                for j in range(4):
                    c = G * 4 + j
                    bk, bkey = self.bank()
                    for kc in range(8):
                        self.mm(bk[:, 0:n], wgb(kc, j * 128, (j + 1) * 128), hT[:, kc, 0:n], kc == 0, kc == 7, (wkb, K("hT", kc)), (bkey,))
                    ev_gate(sgB, K("sgB"))(bk, bkey)
                    bk, bkey = self.bank()
                    for kc in range(4):
                        self.mm(bk[:, 0:n], wbb(kc, j * 128, (j + 1) * 128), ybT(kc)[:, 0:n], kc == 0, kc == 3, (wkbb, YBK[kc]), (bkey,))
                    self.tt("dve", sgB[:, 0:n], sgB[:, 0:n], bk[:, 0:n], ALU.mult, (K("sgB"), bkey), (K("sgB"),))
                    self.tt("dve", mergedT(c)[:, 0:n], sgB[:, 0:n], m2[:, c, 0:n], ALU.add, (K("sgB"), K("m2", c)), (MK(c),))
            for G in range(2):
                wg, wk = self.wload(wout_g[G], kwout[G], 8, 512)
                for j in range(4):
                    c = G * 4 + j
                    bk, bkey = self.bank()
                    for kc in range(8):
                        self.mm(bk[:, 0:n], wg(kc, j * 128, (j + 1) * 128), mergedT(kc)[:, 0:n], kc == 0, kc == 7, (wk, MK(kc)), (bkey,))
                    self.cp("act", m2[:, c, 0:n], bk[:, 0:n], (bkey,), (K("m2", c),))
            rms_bcast(m2, n, lambda kc: (K("m2", kc),))

        def ffn(n, nseg, hview, hkeyf, extra=()):
            w = n // nseg
            v3 = lambda ap: ap.rearrange("p (s w) -> p s w", s=nseg)
            for g in range(11):
                wg, wk = self.wload(wup_g[g], kwup[g], 8, 512)
                for pair in range(2):
                    ua_, y0_ = ua_sets[pair], y0_sets[pair]
                    for half in range(2):
                        jc = half * 2 + pair
                        jj = half * NFF + 2 * g + pair
                        bk, bkey = self.bank()
                        for kc in range(8):
                            self.mm(bk[:, 0:n], wg(kc, jc * 128, (jc + 1) * 128), hT[:, kc, 0:n], kc == 0, kc == 7, (wk, K("hT", kc)), (bkey,))
                        u3 = ua_[half][:, 0:nseg * (w + 2)].rearrange("p (s w) -> p s w", s=nseg)
                        uk, uh, yk = K("ua", pair, half), K("uah", pair, half), K("y0", pair, half)
                        yv = v3(y0_[half][:, 0:n])
                        self.cp("pool", u3[:, :, 0:2], hview(jj), (hkeyf(jj),) + extra, (uh,))
                        self.cp("act", u3[:, :, 2:2 + w], v3(bk[:, 0:n]), (bkey,), (uk,))
                        self.act(y0_[half][:, 0:n], bk[:, 0:n], AF.Identity, (bkey, K("wdwT2"), K("bdwT")), (yk,),
                                 bias=bdwT[:, jj:jj + 1], scale=wdwT[:, 2, jj:jj + 1])
                        self.stt("dve", yv, u3[:, :, 1:1 + w], wdwT[:, 1, jj:jj + 1], yv, ALU.mult, ALU.add, (uk, uh, yk, K("wdwT1")), (yk,))
                        self.stt("dve", yv, u3[:, :, 0:w], wdwT[:, 0, jj:jj + 1], yv, ALU.mult, ALU.add, (uk, uh, yk, K("wdwT0")), (yk,))
                        self.cp("pool", hview(jj), u3[:, :, w:w + 2], (uk,), (hkeyf(jj),))
                    j = 2 * g + pair
                    self.act(y0_[0][:, 0:n], y0_[0][:, 0:n], AF.Gelu_apprx_tanh, (K("y0", pair, 0),), (K("y0", pair, 0),))
                    self.tt("dve", actT[:, j, 0:n], y0_[0][:, 0:n], y0_[1][:, 0:n], ALU.mult, (K("y0", pair, 0), K("y0", pair, 1)), (K("actT", j),))
            for c in range(8):
                wg, wk = self.wload(wdown_g[c], kwdown[c], NFF, 128)
                bk, bkey = self.bank()
                for j in range(NFF):
                    self.mm(bk[:, 0:n], wg(j), actT[:, j, 0:n], j == 0, j == NFF - 1, (wk, K("actT", j)), (bkey,))
                self.cp("act", m2[:, c, 0:n], bk[:, 0:n], (bkey,), (K("m2", c),))
            rms_bcast(m2, n, lambda kc: (K("m2", kc),))

        def store_y(dst_rows, ntile):
            for t in range(ntile):
                yb_ = xin[t % 2]
                ky = K("xin", t % 2)
                for half in range(2):
                    bk, bkey = self.bank()
                    for j in range(4):
                        kc = half * 4 + j
                        self.tr(bk[:, j * 128:(j + 1) * 128], xT[:, kc, t * 128:(t + 1) * 128], ident_f[:], (K("xT", kc, t), K("ident_f")), (bkey,))
                    self.cp("act", yb_[:, half * 512:(half + 1) * 512], bk[:], (bkey,), (K("xinv", t % 2),) if half else (ky,))
                self.dma("sp", dst_rows[t * 128:(t + 1) * 128, :], yb_[:], (ky, K("xinv", t % 2)), (), ("xin", t % 2))

        def load_dma(src_rows, t):
            i = (0, 2)[t % 2]
            self.dma("sp", xin[i][:], src_rows[t * 128:(t + 1) * 128, :], (), (K("xin", i), K("xinv", i)), ("xin", i))

        def prefetch_x(nxt):
            if nxt is None or self.prefetched:
                return
            for t in range(min(2, nxt[1])):
                load_dma(nxt[0], t)
            self.prefetched = True

        def load_tile(src_rows, t, xb, i, nnt):
            kx, kv = K("xin", i), K("xinv", i)
            if not (self.prefetched and t < 2):
                pass
            for half in range(2):
                bk, bkey = self.bank()
                for j in range(4):
                    kc = half * 4 + j
                    self.tr(bk[:, j * 128:(j + 1) * 128], xb[:, kc * 128:(kc + 1) * 128], ident_f[:], (kx if half == 0 else kv, K("ident_f")), (bkey,))
                self.cp("act", xT[:, half * 4:half * 4 + 4, t * 128:(t + 1) * 128],
                        bk[:].rearrange("p (j n) -> p j n", j=4), (bkey,), tuple(K("xT", half * 4 + j, t) for j in range(4)))
            if t + 2 < nnt:
                load_dma(src_rows, t + 2)

        def store_tile(dst_rows, t, xb, i):
            kx, kv = K("xin", i), K("xinv", i)
            for half in range(2):
                bk, bkey = self.bank()
                for j in range(4):
                    kc = half * 4 + j
                    self.tr(bk[:, j * 128:(j + 1) * 128], xT[:, kc, t * 128:(t + 1) * 128], ident_f[:], (K("xT", kc, t), K("ident_f")), (bkey,))
                self.cp("act", xb[:, half * 512:(half + 1) * 512], bk[:], (bkey,), (kv,) if half else (kx,))
            self.dma("sp", dst_rows[t * 128:(t + 1) * 128, :], xb[:], (kx, kv), (), ("xin", i))

        def swap_tiles(dst_rows, ntile, nxt):
            nsrc, nnt = nxt if nxt is not None else (None, 0)
            prefetch_x(nxt)
            self.prefetched = False
            for t in range(max(ntile, nnt)):
                if t < ntile:
                    i = (1, 3)[t % 2]
                    store_tile(dst_rows, t, xin[i], i)
                if t < nnt:
                    i = (0, 2)[t % 2]
                    load_tile(nsrc, t, xin[i], i, nnt)

        def store_hist(hsrc, hkeys, dst):
            self.cp("dve", h88[:], hsrc.rearrange("p j t -> p t j"), hkeys, (K("h88"),))
            bk, bkey = self.bank()
            self.tr(bk[0:88, 0:128], h88[:].rearrange("p t j -> p (t j)"), ident_f[:], (K("h88"), K("ident_f")), (bkey,))
            self.cp("act", stage[0:88, :], bk[0:88, 0:128], (bkey,), (K("stage"),))
            self.dma("sp", dst, stage[0:88, :], (K("stage"),), (), ("stage_o",))

        def kv_stage(bk, bkey, which, dst):
            st = xin[1][:, which * 512:(which + 1) * 512]
            sk = K("xin", 1) if which == 0 else K("xinv", 1)
            self.cp("act", st, bk[:], (bkey,), (sk,))
            self.dma("sp", dst, st, (sk,), (), ("xin", 1) if which == 0 else ("xinv", 1))

        LAST_KV0 = 16 + NMAIN - 4

        def common_proj(n, units, tglob0, kv_from, pre, flagged, kv_out):
            if not pre:
                wg, wk = self.wload(win_g[G_QA], kwin[G_QA], 8, 512)
                def ev_q(j, bk, bkey):
                    self.act(qaT[:, j, 0:n], bk[:, 0:n], AF.Identity, (bkey,), (K("qaT", j),), scale=0.125)
                pairs_fm(wg, wk, n, ev_q)
            if kv_from < len(units):
                wg, wk = self.wload(win_g[G_KA], kwin[G_KA], 8, 512)
                def ev_k(j, bk, bkey):
                    for ui in range(kv_from, len(units)):
                        c0, C = units[ui]
                        if C == 128:
                            slot = (tglob0 + ui) % 8
                            self.cp("act", kaT[slot][:, j, :], bk[:, c0:c0 + 128], (bkey,), (K("kaT", slot, j),))
                    if units[0][1] != 128:
                        self.cp("act", kaT[tglob0 % 8][:, j, :], bk[:, 0:128], (bkey,), (K("kaT", tglob0 % 8, j),))
                pairs_fm(wg, wk, n, ev_k)
                kunits = [(ui, units[ui]) for ui in range(kv_from, len(units)) if kv_out(ui, 0) is not None] if units[0][1] == 128 else []
                if kunits:
                    units_tm(wg, wk, 0, 512, [u for _, u in kunits], lambda i, bk, bkey: kv_stage(bk, bkey, 0, kv_out(kunits[i][0], 0)))
                if units[0][1] != 128:
                    units_tm(wg, wk, 0, 512, [(0, 128)], lambda i, bk, bkey: kv_stage(bk, bkey, 0, kv_out(0, 0)))
                wg, wk = self.wload(win_g[G_VA], kwin[G_VA], 8, 512)
                if units[0][1] == 128:
                    def ev_v(i, bk, bkey):
                        ui = kv_from + i
                        slot = (tglob0 + ui) % 8
                        self.cp("act", vaug[slot][:, :, 0:64], bk[:].rearrange("p (h e) -> p h e", h=8), (bkey,), (K("vaug", slot),))
                        if flagged:
                            self.ts("dve", vaug[slot][:, :, 64], ones_f[:, 0:8], flag[:, 0:1], None, ALU.mult, None,
                                    (K("ones_f"), K("flag")), (K("vaug1", slot),))
                        else:
                            self.cp("dve", vaug[slot][:, :, 64], ones_f[:, 0:8], (K("ones_f"),), (K("vaug1", slot),))
                        if kv_out(ui, 1) is not None:
                            kv_stage(bk, bkey, 1, kv_out(ui, 1))
                    units_tm(wg, wk, 0, 512, units[kv_from:], ev_v)
                else:
                    units_tm(wg, wk, 0, 512, [(0, 128)], lambda i, bk, bkey: kv_stage(bk, bkey, 1, kv_out(0, 1)))
            if not pre:
                wg, wk = self.wload(win_g[G_QKB], kwin[G_QKB], 8, 512)
                def ev_qb(j, bk, bkey):
                    self.cp("act", qbT[:, j, 0:n], bk[0:64, 0:n], (bkey,), (K("qbT", j),))
                def ev_kb(j, bk, bkey):
                    self.cp("act", kbT[:, j, 0:n], bk[0:64, 0:n], (bkey,), (K("kbT", j),))
                heads_fm(wg, wk, 0, 4, n, ev_qb)
                heads_fm(wg, wk, 256, 4, n, ev_kb)
            wg, wk = self.wload(wgk_g, kwgk, 8, 16)
            bk, bkey = self.bank()
            for kc in range(8):
                self.mm(bk[0:16, 0:n], wg(kc), hT[:, kc, 0:n], kc == 0, kc == 7, (wk, K("hT", kc)), (bkey,))
            self.cp("act", gkT[0:16, 0:n], bk[0:16, 0:n], (bkey,), (K("gkT"),))

        SK = K("Sst")
        def prompt_block(src_rows, ntile, tglob0, mode, dst_rows, kv_from, preloaded=False, nxt=None):
            n = ntile * 128
            pre = mode == "prefix"
            flagged = mode != "full"
            units = [(t * 128, 128) for t in range(ntile)]
            mark = lambda nm: self.marks.append((mode, tglob0, nm, len(S.ops)))
            mark("start")
            if not preloaded:
                load_xT(src_rows, ntile)
            rms_bcast(xT, n, xkeys(ntile))
            if flagged:
                modulate(n, ntile, "AmP", "BmP", lambda kc: AmP[:, kc:kc + 1], lambda kc: BmP[:, kc:kc + 1])
            else:
                modulate(n, ntile, "Am", "Bm", lambda kc: Am[:, kc, 0:1], lambda kc: Bm[:, kc, 0:1])
            if pre and nxt is not None:
                swap_tiles(None, 0, nxt)
            mark("norm1")

            def kv_out(ui, which):
                tg = tglob0 + ui
                if mode != "full" or tg < LAST_KV0:
                    return None
                r0 = (tg - LAST_KV0) * 128
                return (kp_o if which == 0 else vp_o)[r0:r0 + 128, :]
            common_proj(n, units, tglob0, kv_from, pre, flagged, kv_out)
            mark("proj")
            if pre:
                gla_units(units, pre, lambda ui: (Sst, Sbf, SK, None, None))
                mark("gla")
                return
            def attn_all():
                for t in range(ntile):
                    attn_tile(tglob0 + t, t)
            gla_units(units, pre, lambda ui: (Sst, Sbf, SK, None, None), attn_all)
            mark("attn")
            merge_and_out(n)
            resid(n, ntile, lambda kc: Gm[:, kc, 0:1], "Gm")
            mark("merge")
            rms_bcast(xT, n, xkeys(ntile))
            if flagged:
                modulate(n, ntile, "AfP", "BfP", lambda kc: AfP[:, kc:kc + 1], lambda kc: BfP[:, kc:kc + 1])
            else:
                modulate(n, ntile, "Af", "Bf", lambda kc: Af[:, kc, 0:1], lambda kc: Bf[:, kc, 0:1])
            prefetch_x(nxt)
            ffn(n, 1, lambda jj: hist[:, jj:jj + 1, :], lambda jj: K("hist", jj))
            mark("ffn")
            resid(n, ntile, lambda kc: Gf[:, kc, 0:1], "Gf")
            swap_tiles(dst_rows, ntile, nxt)
            mark("end")

        def sample_block():
            n = 128
            slot = 0
            units = [(s * 32, 32) for s in range(4)]
            self.marks.append(("sample", 0, "start", len(S.ops)))
            rms_bcast(xT, n, xkeys(1))
            modulate_seq(Am, Bm, "Am", "Bm")
            smark = lambda nm: self.marks.append(("sample", 0, nm, len(S.ops)))
            smark("norm1")
            common_proj(n, units, slot, 0, False, False, lambda ui, which: (ks_o if which == 0 else vs_o))
            smark("proj")
            SSK = K("Sst")
            def S_of(ui):
                def before():
                    self.dma("sp", Ss[:], sgla[ui].rearrange("h k v -> k h v"), (), tuple(SSK + (h,) for h in range(4)), ("Ss",))
                    self.cp("act", Ssb[:], Ss[:], tuple(SSK + (h,) for h in range(4)), (SSK + ("b",),))
                def after():
                    self.dma("sp", glas_o[ui].rearrange("h k v -> k h v"), Ss[:], tuple(SSK + (h,) for h in range(4)), (), ("Ss",))
                return (Ss, Ssb, SSK, before, after)
            wgv, wkv = self.wload(win_g[G_VA], kwin[G_VA], 8, 512)
            for s in range(4):
                bk, bkey = self.bank()
                for kc in range(8):
                    self.mm(bk[0:32, :], hT[:, kc, s * 32:(s + 1) * 32], wgv(kc), kc == 0, kc == 7, (wkv, K("hT", kc)), (bkey,))
                self.cp("act", vown[:, s, :, 0:64], bk[0:32, :].rearrange("p (h e) -> p h e", h=8), (bkey,), (K("vown", s),))
                self.cp("dve", vown[:, s, :, 64], ones_f[0:32, 0:8], (K("ones_f"),), (K("vown1", s),))
            def attn_all():
                for s in range(4):
                    attn_sample(s, slot)
            gla_units(units, False, S_of, attn_all)
            smark("attn")
            merge_and_out(n)
            resid_seq(Gm, "Gm")
            smark("merge")
            rms_bcast(xT, n, xkeys(1))
            modulate_seq(Af, Bf, "Af", "Bf")
            for s in range(4):
                load_T(sconv[s], 88, hist_s[:, s, :, :].rearrange("p j t -> p t j"), "hist_s_in%d" % s,
                       view=lambda a: a.rearrange("p (t j) -> p t j", t=2))
            hs_in = tuple(K("hist_s_in%d" % s) for s in range(4))
            ffn(n, 4, lambda jj: hist_s[:, :, jj, :], lambda jj: K("hist_s", jj), hs_in)
            smark("ffn")
            resid_seq(Gf, "Gf")
            store_y(ysam, 1)
            for s in range(4):
                store_hist(hist_s[:, s, :, :], tuple(K("hist_s", jj) for jj in range(44)), convs_o[s])

        STG = int(os.environ.get("KSTAGE", "99"))
        if STG <= 0:
            return S.finalize()
        sched = []
        t0 = 0
        while t0 < 11:
            nt = min(NB, 11 - t0)
            sched.append(("prefix", xpre[t0 * 128:(t0 + nt) * 128, :], nt, t0, None, nt))
            t0 += nt
        while t0 < 15:
            nt = min(NB, 15 - t0)
            sched.append(("prefix", xpre[t0 * 128:(t0 + nt) * 128, :], nt, t0, None, 0))
            t0 += nt
        sched.append(("overlap", xov, 1, 15, yov, 0))
        for b in range(NMAIN // NB):
            sched.append(("full", xmain[b * NBT:(b + 1) * NBT, :], NB, 16 + b * NB, ymain[b * NBT:(b + 1) * NBT, :], 0))
        sched.append(("sample", xsam, 1, 0, None, 0))
        for bi, (mode, src, nt, tg, dst, kvf) in enumerate(sched):
            nxt = (sched[bi + 1][1], sched[bi + 1][2]) if bi + 1 < len(sched) else None
            if mode == "sample":
                break
            if mode == "overlap":
                ada_part2()
            prompt_block(src, nt, tg, mode, dst, kvf, preloaded=bi > 0, nxt=nxt)
            if mode == "overlap":
                self.ts("dve", hist[:].rearrange("p j t -> p (j t)"), hist[:].rearrange("p j t -> p (j t)"), flag[:, 0:1], None, ALU.mult, None,
                        tuple(K("hist", jj) for jj in range(44)) + (K("flag"),), tuple(K("hist", jj) for jj in range(44)))
        self.dma("sp", glap_o.rearrange("h k v -> k h v"), Sst[:], tuple(SK + (h,) for h in range(4)), (), ("glap",))
        store_hist(hist[:], tuple(K("hist", jj) for jj in range(44)), convp_o)
        sample_block()
        return S.finalize()


_CACHE = {}


def _program():
    if "nc" not in _CACHE:
        b = Builder()
        b.build()
        _CACHE["nc"] = b.nc
    return _CACHE["nc"]


def kernel(x_prompt, x_sample, cache_k_a, cache_v_a, state_gla, state_conv, c_prompt, c_sample,
           w_ada, b_ada, g_pre_mix, g_post_mix, g_pre_ffn, g_post_ffn, w_in, w_gk2, b_gk,
           rel_bias, g_gla, w_br_a, w_br_b, w_out, w_up, w_dw, b_dw, w_down):
    f = lambda a: np.ascontiguousarray(np.asarray(a, dtype=np.float32))
    x_prompt, x_sample = f(x_prompt), f(x_sample)
    rb = f(rel_bias)[0]
    kk = np.arange(128)[:, None]
    qq = np.arange(128)[None, :]
    idx_prev = np.clip(qq + 128 - kk, -128, 128) + 128
    idx_own = np.clip(qq - kk, -128, 128) + 128
    btab = np.stack([rb[:, idx_prev].transpose(1, 0, 2), rb[:, idx_own].transpose(1, 0, 2)])
    cvec = np.broadcast_to(rb[:, 256][None, :], (128, 8))
    wi = f(w_in)[0]
    wi = np.concatenate([wi[:, :3072], wi[:, 3088:], wi[:, 3072:3088]], axis=1)
    wu = f(w_up)[0]
    cols = []
    for g in range(11):
        cols += [wu[:, 2 * g * 128:(2 * g + 2) * 128], wu[:, DFF + 2 * g * 128:DFF + (2 * g + 2) * 128]]
    wu = np.concatenate(cols, axis=1)
    shared = {
        "w_ada": f(w_ada)[0], "b_ada": f(b_ada)[0].reshape(48, 128),
        "gvec": np.concatenate([f(g_pre_mix)[0], f(g_post_mix)[0], f(g_pre_ffn)[0], f(g_post_ffn)[0]]).reshape(32, 128),
        "w_in": f(wi), "w_gk2": f(w_gk2)[0], "b_gk": f(b_gk)[0].reshape(1, 256),
        "btab": f(btab), "cvec": f(cvec), "ggla": f(np.broadcast_to(np.tile(f(g_gla)[0], 4)[None, :], (128, 512))),
        "w_br_a": f(w_br_a)[0], "w_br_b": f(w_br_b)[0], "w_out": f(w_out)[0], "w_up": f(wu),
        "w_dw": f(w_dw)[0].reshape(3 * 44, 128), "b_dw": f(b_dw)[0].reshape(44, 128), "w_down": f(w_down)[0],
    }
    in_maps = []
    for c in range(8):
        b, hf = c // 2, c % 2
        m = dict(shared)
        if hf == 1:
            m["xpre"] = x_prompt[b, 0:NPRE * 128]
            m["xov"] = x_prompt[b, NPRE * 128:2048]
        else:
            m["xpre"] = np.zeros((NPRE * 128, D), np.float32)
            m["xov"] = np.zeros((128, D), np.float32)
        m["xmain"] = x_prompt[b, hf * 2048:(hf + 1) * 2048]
        m["xsam"] = x_sample[4 * c:4 * c + 4].reshape(128, D)
        m["crow"] = f(np.concatenate([f(c_prompt)[b:b + 1], f(c_sample)[4 * c:4 * c + 4]], 0).reshape(40, 128))
        m["flag"] = np.full((128, 1), float(hf), np.float32)
        m["ck"] = f(cache_k_a)[0, 4 * c:4 * c + 4].reshape(4, 512, 512)
        m["cv"] = f(cache_v_a)[0, 4 * c:4 * c + 4].reshape(4, 512, 512)
        m["sgla"] = f(state_gla)[0, 4 * c:4 * c + 4]
        m["sconv"] = f(state_conv)[0, 4 * c:4 * c + 4].reshape(4, 88, 128)
        in_maps.append({k: np.ascontiguousarray(v) for k, v in m.items()})
    nc = _program()
    cores = [int(t) for t in os.environ.get("KCORES", "0,1,2,3,4,5,6,7").split(",")]
    if os.environ.get("KTRACE"):
        res = run_bass_kernel_spmd(nc, [in_maps[c] for c in cores], core_ids=list(range(len(cores))), trace=True)
        print("EXEC_TIME_NS", res.exec_time_ns)
    else:
        res = run_bass_kernel_spmd(nc, [in_maps[c] for c in cores], core_ids=list(range(len(cores))))
    R = {c: res.results[i] for i, c in enumerate(cores)}
    y_prompt = np.zeros((4, 4096, D), np.float32)
    y_sample = np.zeros((32, 32, D), np.float32)
    k_p = np.zeros((1, 4, 512, 8, 64), np.float32); v_p = np.zeros_like(k_p)
    gla_p = np.zeros((1, 4, 4, 64, 128), np.float32)
    conv_p = np.zeros((1, 4, 2, 2 * DFF), np.float32)
    k_s = np.zeros((1, 32, 32, 8, 64), np.float32); v_s = np.zeros_like(k_s)
    gla_s = np.zeros((1, 32, 4, 64, 128), np.float32)
    conv_s = np.zeros((1, 32, 2, 2 * DFF), np.float32)
    for c in cores:
        b, hf = c // 2, c % 2
        r = R[c]
        y_prompt[b, hf * 2048:(hf + 1) * 2048] = r["ymain"]
        y_sample[4 * c:4 * c + 4] = r["ysam"].reshape(4, 32, D)
        if hf == 1:
            k_p[0, b] = r["kp"].reshape(512, 8, 64)
            v_p[0, b] = r["vp"].reshape(512, 8, 64)
            gla_p[0, b] = r["glap"]
            conv_p[0, b] = r["convp"].reshape(2, 2 * DFF)
        k_s[0, 4 * c:4 * c + 4] = r["ks"].reshape(4, 32, 8, 64)
        v_s[0, 4 * c:4 * c + 4] = r["vs"].reshape(4, 32, 8, 64)
        gla_s[0, 4 * c:4 * c + 4] = r["glas"]
        conv_s[0, 4 * c:4 * c + 4] = r["convs"].reshape(4, 2, 2 * DFF)
    return (y_prompt, y_sample, k_p, v_p, gla_p, conv_p, k_s, v_s, gla_s, conv_s)
```

```python
from contextlib import ExitStack
import os

import numpy as np
import concourse.bass as bass
import concourse.mybir as mybir
from concourse.bass_utils import run_bass_kernel_spmd

F32 = mybir.dt.float32
BF16 = mybir.dt.bfloat16
AF = mybir.ActivationFunctionType
ALU = mybir.AluOpType

D = 1024
KC = 8
DFF = 2816
NFF = 22
DIN = 5136
G_QA, G_KA, G_VA, G_QKB, G_VB, G_GB, G_GA, G_GBR = 0, 1, 2, 3, 4, 5, 6, 8
EPS = 1e-6
NEG = -30000.0
NPRE = 15
NMAIN = 16
NB = 4


class Op:
    __slots__ = ("eng", "fn", "reads", "writes", "dsem", "signal", "sigval", "deps")

    def __init__(self, eng, fn, reads, writes, dsem):
        self.eng, self.fn, self.reads, self.writes, self.dsem = eng, fn, reads, writes, dsem
        self.signal = False
        self.sigval = 0
        self.deps = ()


class Sched:
    DMA = ("sp", "pool_dma")

    def __init__(self, nc, stack):
        self.nc = nc
        self.stack = stack
        self.ops = []
        self.eng_obj = {"pe": nc.tensor, "act": nc.scalar, "dve": nc.vector, "pool": nc.gpsimd,
                        "sp": nc.sync, "pool_dma": nc.gpsimd}
        self.wuses = []
        self.wdepth = 2

    def add(self, eng, fn, reads=(), writes=(), dsem=None):
        op = Op(eng, fn, tuple(reads), tuple(writes), dsem)
        self.ops.append(op)
        return op

    def queue_of(self, op):
        return "pool" if op.eng == "pool_dma" else op.eng

    def finalize(self):
        nc = self.nc
        inserts = {}
        lastrd = {}
        for idx, op in enumerate(self.ops):
            for k in op.reads:
                if k and k[0] == "wuse":
                    lastrd[k[1]] = idx
        prev = 0
        for i, (pos, op) in enumerate(self.wuses):
            tgt = self.wuses[max(0, i - self.wdepth)][0]
            if i >= 3 and (i - 3) in lastrd:
                tgt = max(tgt, lastrd[i - 3] + 1)
            tgt = max(tgt, prev)
            prev = tgt
            assert tgt <= pos, (i, tgt, pos)
            inserts.setdefault(tgt, []).append(op)
        ops = []
        for i, op in enumerate(self.ops):
            if i in inserts:
                ops.extend(inserts[i])
            ops.append(op)
        self.ops = ops
        last_w = {}
        readers = {}
        for i, op in enumerate(ops):
            deps = set()
            q = self.queue_of(op)
            isdma = op.dsem is not None
            for k in op.reads:
                j = last_w.get(k)
                if j is not None:
                    deps.add(j)
            for k in op.writes:
                j = last_w.get(k)
                if j is not None:
                    oj = ops[j]
                    if isdma or oj.dsem is not None or self.queue_of(oj) != q or q != "pe":
                        deps.add(j)
                for j in readers.get(k, ()):
                    oj = ops[j]
                    if isdma or oj.dsem is not None or self.queue_of(oj) != q or (q != "pe" and os.environ.get("KWAR", "1") == "1"):
                        deps.add(j)
            deps.discard(i)
            op.deps = tuple(sorted(deps))
            for j in op.deps:
                ops[j].signal = True
            for k in op.reads:
                lst = readers.setdefault(k, [])
                if op.dsem is None:
                    lst[:] = [j for j in lst if ops[j].dsem is not None or self.queue_of(ops[j]) != q]
                lst.append(i)
            for k in op.writes:
                last_w[k] = i
                readers[k] = []
        esem = {}
        for e in ("pe", "act", "dve", "pool"):
            esem[e] = self.stack.enter_context(nc.semaphore("sem_" + e))
        dsems = {}
        cnt = {}
        for op in ops:
            if op.dsem is not None:
                if op.dsem not in dsems:
                    dsems[op.dsem] = self.stack.enter_context(nc.semaphore("dsem_%d" % len(dsems)))
                    cnt[op.dsem] = 0
                cnt[op.dsem] += 1
                op.sigval = 16 * cnt[op.dsem]
            elif op.signal:
                q = self.queue_of(op)
                cnt[q] = cnt.get(q, 0) + 1
                op.sigval = cnt[q]
        known = {q: {} for q in ("pe", "act", "dve", "pool", "sp")}
        for op in ops:
            q = self.queue_of(op)
            eng = self.eng_obj[op.eng]
            need = {}
            for j in op.deps:
                oj = ops[j]
                s = dsems[oj.dsem] if oj.dsem is not None else esem[self.queue_of(oj)]
                key = id(s)
                if key not in need or need[key][1] < oj.sigval:
                    need[key] = (s, oj.sigval)
            for key, (s, v) in need.items():
                if known[q].get(key, 0) >= v:
                    continue
                eng.wait_ge(s, v)
                known[q][key] = v
            ins = op.fn(eng)
            if op.dsem is not None:
                ins.then_inc(dsems[op.dsem], 16)
            elif op.signal:
                ins.then_inc(esem[q], 1)
        self.counts = dict((str(k), v) for k, v in cnt.items())
        for k, s in dsems.items():
            nc.sync.wait_ge(s, 16 * cnt[k])
        return len(ops)


class Builder:
    def __init__(self):
        self.stack = ExitStack()
        self.nc = bass.Bass("TRN2", target_bir_lowering=False)
        self.S = Sched(self.nc, self.stack)
        self.nbank = 0
        self.wuse_n = 0
        self.prefetched = False
        self.bank_set = [0, 1, 2, 3, 4]
        self.marks = []
        self.uid = 0

    def din(self, name, shape, dt=F32):
        return self.nc.dram_tensor(name, list(shape), dt, kind="ExternalInput").ap()

    def dout(self, name, shape):
        return self.nc.dram_tensor(name, list(shape), F32, kind="ExternalOutput").ap()

    def dscr(self, name, shape, dt=BF16):
        return self.nc.dram_tensor(name, list(shape), dt, kind="Internal").ap()

    def sb(self, name, shape, dt=F32):
        return self.stack.enter_context(self.nc.sbuf_tensor(name, list(shape), dt))

    def ps(self, name, shape, dt=F32):
        return self.stack.enter_context(self.nc.psum_tensor(name, list(shape), dt))

    def bank(self):
        bs = self.bank_set
        i = bs[self.nbank % len(bs)]
        self.nbank += 1
        return self.banks[i], ("ps", i)

    def key(self, name):
        self.uid += 1
        return (name, self.uid)

    def mm(self, out, lhsT, rhs, start, stop, reads, writes):
        self.S.add("pe", lambda e: e.matmul(out, lhsT, rhs, start=start, stop=stop), reads, writes)

    def tr(self, out, in_, ident, reads, writes):
        self.S.add("pe", lambda e: e.transpose(out, in_, ident), reads, writes)

    def act(self, out, in_, func, reads, writes, bias=None, scale=None, accum_out=None):
        kw = {}
        if bias is not None:
            kw["bias"] = bias
        if scale is not None:
            kw["scale"] = scale
        if accum_out is not None:
            kw["accum_out"] = accum_out
        self.S.add("act", lambda e: e.activation(out, in_, func, **kw), reads, writes)

    def tt(self, eng, out, in0, in1, op, reads, writes):
        self.S.add(eng, lambda e: e.tensor_tensor(out, in0, in1, op), reads, writes)

    def ts(self, eng, out, in0, s1, s2, op0, op1, reads, writes):
        if s2 is None:
            self.S.add(eng, lambda e: e.tensor_scalar(out, in0, s1, None, op0), reads, writes)
        else:
            self.S.add(eng, lambda e: e.tensor_scalar(out, in0, s1, s2, op0, op1), reads, writes)

    def stt(self, eng, out, in0, scalar, in1, op0, op1, reads, writes):
        self.S.add(eng, lambda e: e.scalar_tensor_tensor(out, in0, scalar, in1, op0=op0, op1=op1), reads, writes)

    def cp(self, eng, out, in_, reads, writes):
        if eng == "act":
            self.S.add("act", lambda e: e.copy(out, in_), reads, writes)
        else:
            self.S.add(eng, lambda e: e.tensor_copy(out, in_), reads, writes)

    def memset(self, eng, ap, val, writes):
        self.S.add(eng, lambda e: e.memset(ap, val), (), writes)

    def dma(self, q, out, in_, reads, writes, dsem, slow=False):
        if slow:
            self.S.add(q, lambda e: e.dma_start(out=out, in_=in_, allow_slow_non_contiguous=True), reads, writes, dsem)
        else:
            self.S.add(q, lambda e: e.dma_start(out=out, in_=in_), reads, writes, dsem)

    def interleave(self, fns_banks):
        main = self.S.ops
        streams = []
        for fn, banks in fns_banks:
            self.S.ops = []
            self.bank_set = banks
            fn()
            streams.append(self.S.ops)
        self.S.ops = main
        self.bank_set = [0, 1, 2, 3, 4]
        idx = [0] * len(streams)
        total = sum(len(st) for st in streams)
        for _ in range(total):
            best, bf = None, None
            for si, st in enumerate(streams):
                if idx[si] < len(st):
                    frac = idx[si] / len(st)
                    if bf is None or frac < bf:
                        best, bf = si, frac
            main.append(streams[best][idx[best]])
            idx[best] += 1

    def wload(self, scr_g, wkeys, kcn, width):
        i = self.wuse_n
        self.wuse_n += 1
        slot = i % 3
        buf = self.wbufs[slot]
        key = ("wuse", i)
        dst = buf[:, 0:kcn * width].rearrange("p (kc n) -> p kc n", kc=kcn)
        op = Op("sp", lambda e: e.dma_start(out=dst, in_=scr_g), tuple(wkeys),
                (key, ("wbuf", slot)) + ((("wuse", i - 3),) if i >= 3 else ()), ("wstream", slot))
        self.S.wuses.append((len(self.S.ops), op))
        return (lambda kc, a=0, b=width: buf[:, kc * width + a: kc * width + b]), key

    def build(self):
        nc = self.nc
        S = self.S
        sb, ps = self.sb, self.ps
        NBT = NB * 128
        K = lambda *a: tuple(a)
        xpre = self.din("xpre", [NPRE * 128, D])
        xov = self.din("xov", [128, D])
        xmain = self.din("xmain", [NMAIN * 128, D])
        xsam = self.din("xsam", [128, D])
        crow = self.din("crow", [40, 128])
        flag_d = self.din("flag", [128, 1])
        ck = self.din("ck", [4, 512, 512])
        cv = self.din("cv", [4, 512, 512])
        sgla = self.din("sgla", [4, 4, 64, 128])
        sconv = self.din("sconv", [4, 88, 128])
        w_ada = self.din("w_ada", [D, 6 * D])
        b_ada = self.din("b_ada", [48, 128])
        gvec = self.din("gvec", [32, 128])
        w_in = self.din("w_in", [D, DIN])
        w_gk2 = self.din("w_gk2", [16, 256])
        b_gk = self.din("b_gk", [1, 256])
        btab = self.din("btab", [2, 128, 8, 128])
        cvec = self.din("cvec", [128, 8])
        ggla = self.din("ggla", [128, 512])
        w_br_a = self.din("w_br_a", [512, D])
        w_br_b = self.din("w_br_b", [512, D])
        w_out = self.din("w_out", [D, D])
        w_up = self.din("w_up", [D, 2 * DFF])
        w_dw = self.din("w_dw", [3 * 44, 128])
        b_dw = self.din("b_dw", [44, 128])
        w_down = self.din("w_down", [DFF, D])

        ymain = self.dout("ymain", [NMAIN * 128, D])
        ysam = self.dout("ysam", [128, D])
        yov = self.dout("yov", [128, D])
        kp_o = self.dout("kp", [512, 512])
        vp_o = self.dout("vp", [512, 512])
        glap_o = self.dout("glap", [4, 64, 128])
        convp_o = self.dout("convp", [88, 128])
        ks_o = self.dout("ks", [128, 512])
        vs_o = self.dout("vs", [128, 512])
        glas_o = self.dout("glas", [4, 4, 64, 128])
        convs_o = self.dout("convs", [4, 88, 128])

        win_g = self.dscr("win_g", [10, 128, 8, 512])
        wgk_g = self.dscr("wgk_g", [128, 8, 16])
        wbra_g = self.dscr("wbra_g", [2, 128, 4, 512])
        wbrb_g = self.dscr("wbrb_g", [2, 128, 4, 512])
        wout_g = self.dscr("wout_g", [2, 128, 8, 512])
        wup_g = self.dscr("wup_g", [11, 128, 8, 512])
        wdown_g = self.dscr("wdown_g", [8, 128, NFF, 128])

        self.banks = [ps("bank%d" % i, [128, 512]) for i in range(5)]
        obank = [ps("obank%d" % i, [128, 512]) for i in range(2)]
        pbf = ps("pbf", [128, 1024], BF16)

        self.wbufs = [sb("wbuf%d" % i, [128, 4096], BF16) for i in range(3)]
        ident_f = sb("ident_f", [128, 128])
        ident_b = sb("ident_b", [128, 128], BF16)
        ones_b = sb("ones_b", [128, 128], BF16)
        triu_f = sb("triu_f", [128, 128])
        trisl_f = sb("trisl_f", [128, 128])
        ones_f = sb("ones_f", [128, 8])
        epsb = sb("epsb", [128, 1])
        mod = sb("mod", [128, 6, KC, 5])
        Am = sb("Am", [128, KC, 5]); Bm = sb("Bm", [128, KC, 5]); Gm = sb("Gm", [128, KC, 5])
        Af = sb("Af", [128, KC, 5]); Bf = sb("Bf", [128, KC, 5]); Gf = sb("Gf", [128, KC, 5])
        AmP = sb("AmP", [128, KC]); BmP = sb("BmP", [128, KC])
        AfP = sb("AfP", [128, KC]); BfP = sb("BfP", [128, KC])
        gT = sb("gT", [128, 32])
        badaT = sb("badaT", [128, 48])
        cT = sb("cT", [128, 40])
        siluT = sb("siluT", [128, 40], BF16)
        flag = sb("flag_s", [128, 1])
        wdwT = sb("wdwT", [128, 3, 44])
        bdwT = sb("bdwT", [128, 44])
        wgk_f = sb("wgk_f", [17, 256])
        wgk_b = sb("wgk_b", [17, 256], BF16)
        tab_prev = sb("tab_prev", [128, 8, 128], BF16)
        tab_own = sb("tab_own", [128, 8, 128], BF16)
        tab_mask = sb("tab_mask", [128, 128], BF16)
        cvec_s = sb("cvec_s", [128, 8])
        ggla_s = sb("ggla_s", [128, 512])
        stage = sb("stage", [128, 128])
        hist = sb("hist", [128, 44, 2])
        hist_s = sb("hist_s", [128, 4, 44, 2])
        h88 = sb("h88", [128, 2, 44])
        xin = [sb("xin%d" % i, [128, D]) for i in range(3)]
        xT = sb("xT", [128, KC, NBT])
        hT = sb("hT", [128, KC, NBT], BF16)
        sq = sb("sq", [128, 2, NBT], BF16)
        rbc = sb("rbc", [128, NBT])
        tmpf = sb("tmpf", [128, NBT])
        tmpf2 = sb("tmpf2", [128, NBT])
        qaT = sb("qaT", [128, 4, NBT], BF16)
        kaT = [sb("kaT%d" % i, [128, 4, 128], BF16) for i in range(8)]
        vaug = [sb("vaug%d" % i, [128, 8, 65], BF16) for i in range(8)]
        qbT = sb("qbT", [64, 4, NBT], BF16)
        kbT = sb("kbT", [64, 4, NBT], BF16)
        gkT = sb("gkT", [32, NBT], BF16)
        pT_raw = sb("pT_raw", [128, 2560])
        pT = pT_raw.bitcast(BF16)[:, :].rearrange("p (k h q) -> p k h q", k=5, h=8)
        ya_tok = sb("ya_tok", [128, 512], BF16)
        rden = sb("rden", [128, 8])
        kb_tok = sb("kb_tok", [128, 256])
        vb_tok2 = [sb("vb_tok%d" % i, [128, 4, 128], BF16) for i in range(2)]
        gb2_2 = [sb("gb2_%d" % i, [128, 512]) for i in range(2)]
        gtanh = sb("gtanh", [128, 512])
        Lsp = sb("Lsp", [128, 256])
        e_sb = sb("e_sb", [128, 256])
        e1 = sb("e1", [64, 4, 128]); e2 = sb("e2", [64, 4, 128])
        qtT2 = [sb("qtT%d" % i, [64, 4, 128], BF16) for i in range(2)]; ktT = sb("ktT", [64, 4, 128], BF16)
        kend2 = [sb("kend%d" % i, [128, 256], BF16) for i in range(2)]
        dec2 = [sb("dec%d" % i, [64, 4]) for i in range(2)]
        attT2 = [sb("attT%d" % i, [128, 4, 128], BF16) for i in range(2)]
        Sst = sb("Sst", [64, 4, 128])
        Sbf = sb("Sbf", [64, 4, 128], BF16)
        ssq = sb("ssq", [128, 4]); rgl = sb("rgl", [128, 4])
        yb_tok = sb("yb_tok", [128, 512], BF16)
        sgA = sb("sgA", [128, NBT]); sgB = sb("sgB", [128, NBT])
        m2 = sb("m2", [128, KC, NBT])
        ua = [sb("ua%d" % i, [128, NBT + 8]) for i in range(2)]
        y0 = [sb("y0_%d" % i, [128, NBT]) for i in range(2)]
        ua_sets = [ua, [pT_raw[:, 0:NBT + 8], pT_raw[:, NBT + 8:2 * NBT + 16]]]
        y0_sets = [y0, [pT_raw[:, 2 * NBT + 16:3 * NBT + 16], pT_raw[:, 3 * NBT + 16:4 * NBT + 16]]]
        actT = sb("actT", [128, NFF, NBT], BF16)
        mergedT = lambda c: actT[:, c, :]
        MK = lambda c: K("actT", c)
        yaT = lambda c: actT[:, 8 + c, :]
        YAK = tuple(K("actT", 8 + c) for c in range(4))
        ybT = lambda c: actT[:, 12 + c, :]
        YBK = tuple(K("actT", 12 + c) for c in range(4))
        m2b = m2.bitcast(BF16)
        kcT = lambda c: m2b[:, c, 0:512]
        vcaug = sb("vcaug", [128, 4, 8, 65], BF16)
        vown = sb("vown", [32, 4, 8, 65], BF16)
        Ss, Ssb = Sst, Sbf

        def const_mask(t, keyname, pattern, cm, base, cmp_op):
            self.memset("pool", t[:], 1.0, (K(keyname),))
            S.add("pool", lambda e: e.affine_select(t[:], t[:], pattern=pattern, compare_op=cmp_op, fill=0.0,
                                                    base=base, channel_multiplier=cm), (K(keyname),), (K(keyname),))
        self.memset("pool", ident_f[:], 0.0, (K("ident_f"),))
        S.add("pool", lambda e: e.affine_select(ident_f[:], ident_f[:], pattern=[[-1, 128]], compare_op=ALU.not_equal,
                                                fill=1.0, base=0, channel_multiplier=1), (K("ident_f"),), (K("ident_f"),))
        const_mask(triu_f, "triu_f", [[1, 128]], -1, 0, ALU.is_ge)
        const_mask(trisl_f, "trisl_f", [[-1, 128]], 1, -1, ALU.is_ge)
        self.cp("dve", ident_b[:], ident_f[:], (K("ident_f"),), (K("ident_b"),))
        self.memset("dve", ones_b[:], 1.0, (K("ones_b"),))
        self.memset("dve", ones_f[:], 1.0, (K("ones_f"),))
        self.memset("dve", epsb[:], EPS, (K("epsb"),))
        self.memset("dve", gkT[:], 1.0, (K("gkT_ones"),))
        self.memset("dve", hist[:], 0.0, tuple(K("hist", jj) for jj in range(44)))
        self.memset("dve", Sst[:], 0.0, tuple(K("Sst", h) for h in range(4)))
        self.memset("dve", Sbf[:], 0.0, (K("Sst", "b"),))

        def load_T(src2d, rows, dst, kname, view=None):
            self.dma("pool_dma", stage[0:rows, :], src2d, (), (K("stage"),), ("stage",))
            bk, bkey = self.bank()
            self.tr(bk[:, 0:rows], stage[0:rows, :], ident_f[0:rows, 0:rows], (K("stage"), K("ident_f")), (bkey,))
            src = bk[:, 0:rows] if view is None else view(bk[:, 0:rows])
            self.cp("dve", dst, src, (bkey,), (K(kname),))

        load_T(gvec, 32, gT[:], "gT")
        load_T(b_ada, 48, badaT[:], "badaT")
        load_T(crow, 40, cT[:], "cT")
        for j in range(3):
            load_T(w_dw[j * 44:(j + 1) * 44, :], 44, wdwT[:, j, :], "wdwT%d" % j)
        load_T(b_dw, 44, bdwT[:], "bdwT")
        self.dma("pool_dma", flag[:], flag_d, (), (K("flag"),), ("misc", 0))
        self.dma("pool_dma", wgk_f[0:16, :], w_gk2, (), (K("wgk_f0"),), ("misc", 1))
        self.dma("pool_dma", wgk_f[16:17, :], b_gk, (), (K("wgk_f1"),), ("misc", 2))
        self.cp("dve", wgk_b[:], wgk_f[:], (K("wgk_f0"), K("wgk_f1")), (K("wgk_b"),))
        self.dma("pool_dma", cvec_s[:], cvec, (), (K("cvec"),), ("misc", 3))
        self.dma("pool_dma", ggla_s[:], ggla, (), (K("ggla"),), ("misc", 4))
        self.ts("dve", ggla_s[:], ggla_s[:], 0.5, None, ALU.mult, None, (K("ggla"),), (K("ggla"),))
        tabf = xin[0][:, :].rearrange("p (h q) -> p h q", h=8)
        for ti, tdst in ((0, tab_prev), (1, tab_own)):
            self.dma("pool_dma", tabf, btab[ti], (), (K("xin", 0), K("xinv", 0)), ("misc", 5))
            for h in range(8):
                self.ts("dve", tdst[:, h, :], tabf[:, h, :], cvec_s[:, h:h + 1], None, ALU.subtract, None,
                        (K("xin", 0), K("xinv", 0), K("cvec")), (K("tab", ti, h),))
        self.memset("dve", tab_own[64:128, :, 0:64], NEG, tuple(K("tab", 1, h) for h in range(8)))
        for ti, tdst in ((0, tab_prev), (1, tab_own)):
            self.act(tdst[:], tdst[:], AF.Exp, tuple(K("tab", ti, h) for h in range(8)), (K("etab", ti),))
        self.memset("dve", tab_mask[:], 1.0, (K("tab_mask"),))
        self.memset("dve", tab_mask[0:64, 64:128], 0.0, (K("tab_mask"),))

        def cast_group(src_cols, dst_g, name, g):
            self.dma("pool_dma", dst_g, src_cols.rearrange("(kc p) n -> p kc n", p=128), (), (K("scr", name, g),), ("cast", name, g))
            return (K("scr", name, g),)

        self.act(tmpf[:, 0:40], cT[:], AF.Tanh, (K("cT"),), (K("tmpf"),), scale=0.5)
        self.ts("dve", tmpf[:, 0:40], tmpf[:, 0:40], 0.5, 0.5, ALU.mult, ALU.add, (K("tmpf"),), (K("tmpf"),))
        self.tt("dve", siluT[:], tmpf[:, 0:40], cT[:], ALU.mult, (K("tmpf"), K("cT")), (K("siluT"),))
        siluv = siluT[:].rearrange("p (s k) -> p k s", k=8)
        modkeys = {}
        k8 = lambda n: tuple(K(n, kc) for kc in range(8))

        def ada_groups(glist):
            for g in glist:
                sl = g % 2
                buf = actT[:, sl * 8:(sl + 1) * 8, :].rearrange("p c n -> p (c n)")
                bkeys = tuple(K("actT", sl * 8 + c) for c in range(8))
                self.dma("pool_dma", buf.rearrange("p (kc n) -> p kc n", kc=8),
                         w_ada.rearrange("(kc p) n -> p kc n", p=128)[:, :, g * 512:(g + 1) * 512], (), bkeys, ("ada", sl))
                bk, bkey = self.bank()
                for oc in range(4):
                    for kc in range(8):
                        self.mm(bk[:, oc * 8:oc * 8 + 5], buf[:, kc * 512 + oc * 128: kc * 512 + (oc + 1) * 128], siluv[:, kc, :],
                                kc == 0, kc == 7, bkeys + (K("siluT"),), (bkey,))
                for oc in range(4):
                    ch = g * 4 + oc
                    self.ts("dve", mod[:, ch // 8, ch % 8, :], bk[:, oc * 8:oc * 8 + 5], badaT[:, ch:ch + 1], None, ALU.add, None,
                            (bkey, K("badaT")), (K("mod", ch),))

        ada_groups(range(0, 4))
        mk1 = tuple(K("mod", ch) for ch in range(16))
        for kc in range(8):
            self.ts("dve", Am[:, kc, :], mod[:, 1, kc, :], 1.0, gT[:, kc:kc + 1], ALU.add, ALU.mult, mk1 + (K("gT"),), (K("Am", kc),))
        self.cp("dve", Bm[:], mod[:, 0, :, :], mk1, (K("Bm"),))
        self.ts("dve", AmP[:], Am[:, :, 0], flag[:, 0:1], None, ALU.mult, None, k8("Am") + (K("flag"),), (K("AmP"),))
        self.ts("dve", BmP[:], Bm[:, :, 0], flag[:, 0:1], None, ALU.mult, None, (K("Bm"), K("flag")), (K("BmP"),))
        modkeys.update({"Am": k8("Am"), "Bm": (K("Bm"),), "AmP": (K("AmP"),), "BmP": (K("BmP"),)})

        def ada_part2():
            ada_groups(range(4, 12))
            mk2 = tuple(K("mod", ch) for ch in range(16, 48))
            for kc in range(8):
                rw = mk2 + (K("gT"),)
                self.ts("dve", Af[:, kc, :], mod[:, 4, kc, :], 1.0, gT[:, 16 + kc:17 + kc], ALU.add, ALU.mult, rw, (K("Af", kc),))
                self.ts("dve", Gm[:, kc, :], mod[:, 2, kc, :], gT[:, 8 + kc:9 + kc], None, ALU.mult, None, rw, (K("Gm", kc),))
                self.ts("dve", Gf[:, kc, :], mod[:, 5, kc, :], gT[:, 24 + kc:25 + kc], None, ALU.mult, None, rw, (K("Gf", kc),))
            self.cp("dve", Bf[:], mod[:, 3, :, :], mk2, (K("Bf"),))
            self.ts("dve", AfP[:], Af[:, :, 0], flag[:, 0:1], None, ALU.mult, None, k8("Af") + (K("flag"),), (K("AfP"),))
            self.ts("dve", BfP[:], Bf[:, :, 0], flag[:, 0:1], None, ALU.mult, None, (K("Bf"), K("flag")), (K("BfP"),))
        modkeys.update({"Af": k8("Af"), "Gm": k8("Gm"), "Gf": k8("Gf"), "Bf": (K("Bf"),), "AfP": (K("AfP"),), "BfP": (K("BfP"),)})

        kwin = {}
        for g in (3, 4, 1, 2):
            kwin[g] = cast_group(w_in[:, g * 512:(g + 1) * 512], win_g[g], "win", g)
        self.dma("pool_dma", wgk_g, w_in.rearrange("(kc p) n -> p kc n", p=128)[:, :, 5120:5136], (), (K("scr", "wgk"),), ("cast", "wgk"))
        kwgk = (K("scr", "wgk"),)
        for g in (0, 5, 6, 7, 8, 9):
            kwin[g] = cast_group(w_in[:, g * 512:(g + 1) * 512], win_g[g], "win", g)
        kwbra = [cast_group(w_br_a[:, g * 512:(g + 1) * 512], wbra_g[g], "wbra", g) for g in range(2)]
        kwbrb = [cast_group(w_br_b[:, g * 512:(g + 1) * 512], wbrb_g[g], "wbrb", g) for g in range(2)]
        kwout = [cast_group(w_out[:, g * 512:(g + 1) * 512], wout_g[g], "wout", g) for g in range(2)]
        kwup = [cast_group(w_up[:, g * 512:(g + 1) * 512], wup_g[g], "wup", g) for g in range(11)]
        kwdown = [cast_group(w_down[:, c * 128:(c + 1) * 128], wdown_g[c], "wdown", c) for c in range(8)]

        def load_xT(src_rows, ntile):
            for t in range(ntile):
                xb = xin[t % 2]
                kx = K("xin", t % 2)
                self.dma("sp", xb[:], src_rows[t * 128:(t + 1) * 128, :], (), (kx, K("xinv", t % 2)), ("xin", t % 2))
                for half in range(2):
                    bk, bkey = self.bank()
                    for j in range(4):
                        kc = half * 4 + j
                        self.tr(bk[:, j * 128:(j + 1) * 128], xb[:, kc * 128:(kc + 1) * 128], ident_f[:],
                                (kx if half == 0 else K("xinv", t % 2), K("ident_f")), (bkey,))
                    self.cp("act", xT[:, half * 4:half * 4 + 4, t * 128:(t + 1) * 128],
                            bk[:].rearrange("p (j n) -> p j n", j=4), (bkey,), tuple(K("xT", half * 4 + j, t) for j in range(4)))

        def rms_bcast(srcT, n, src_keys_fn):
            bk, bkey = self.bank()
            for kc in range(8):
                if kc % 2 == 0:
                    self.act(sq[:, 0, 0:n], srcT[:, kc, 0:n], AF.Square, src_keys_fn(kc), (K("sq", 0),))
                else:
                    self.tt("dve", sq[:, 1, 0:n], srcT[:, kc, 0:n], srcT[:, kc, 0:n], ALU.mult, src_keys_fn(kc), (K("sq", 1),))
                self.mm(bk[:, 0:n], ones_b[:], sq[:, kc % 2, 0:n], kc == 0, kc == 7, (K("ones_b"), K("sq", kc % 2)), (bkey,))
            self.act(tmpf[:, 0:n], bk[:, 0:n], AF.Ln, (bkey, K("epsb")), (K("tmpf"),), bias=epsb[:, 0:1], scale=1.0 / D)
            self.act(rbc[:, 0:n], tmpf[:, 0:n], AF.Exp, (K("tmpf"),), (K("rbc"),), scale=-0.5)

        def modulate(n, ntile, Asc, Bsc, Afn, Bfn):
            for kc in range(8):
                xk = tuple(K("xT", kc, t) for t in range(ntile))
                self.tt("dve", tmpf2[:, 0:n], xT[:, kc, 0:n], rbc[:, 0:n], ALU.mult, xk + (K("rbc"),), (K("tmpf2"),))
                self.act(hT[:, kc, 0:n], tmpf2[:, 0:n], AF.Identity, (K("tmpf2"),) + modkeys[Asc] + modkeys[Bsc], (K("hT", kc),),
                         bias=Bfn(kc), scale=Afn(kc))

        def modulate_seq(At, Bt, Asc, Bsc):
            for kc in range(8):
                xk = (K("xT", kc, 0),)
                v3 = lambda ap: ap.rearrange("p (s w) -> p s w", s=4)
                self.tt("dve", tmpf2[:, 0:128], xT[:, kc, 0:128], rbc[:, 0:128], ALU.mult, xk + (K("rbc"),), (K("tmpf2"),))
                self.tt("dve", v3(tmpf2[:, 0:128]), v3(tmpf2[:, 0:128]), At[:, kc, 1:5].to_broadcast([128, 4, 32]), ALU.mult,
                        (K("tmpf2"),) + modkeys[Asc], (K("tmpf2"),))
                self.tt("dve", v3(hT[:, kc, 0:128]), v3(tmpf2[:, 0:128]), Bt[:, kc, 1:5].to_broadcast([128, 4, 32]), ALU.add,
                        (K("tmpf2"),) + modkeys[Bsc], (K("hT", kc),))

        xkeys = lambda ntile: (lambda kc: tuple(K("xT", kc, t) for t in range(ntile)))

        def heads_fm(wg, wk, col0, nheads, n, evac):
            for h in range(nheads):
                bk, bkey = self.bank()
                for kc in range(8):
                    self.mm(bk[0:64, 0:n], wg(kc, col0 + h * 64, col0 + (h + 1) * 64), hT[:, kc, 0:n], kc == 0, kc == 7, (wk, K("hT", kc)), (bkey,))
                evac(h, bk, bkey)

        def pairs_fm(wg, wk, n, evac):
            for c in range(4):
                bk, bkey = self.bank()
                for kc in range(8):
                    self.mm(bk[:, 0:n], wg(kc, c * 128, (c + 1) * 128), hT[:, kc, 0:n], kc == 0, kc == 7, (wk, K("hT", kc)), (bkey,))
                evac(c, bk, bkey)

        def units_tm(wg, wk, col0, ncols, units, evac):
            for ui, (c0, C) in enumerate(units):
                bk, bkey = self.bank()
                for kc in range(8):
                    self.mm(bk[0:C, 0:ncols], hT[:, kc, c0:c0 + C], wg(kc, col0, col0 + ncols), kc == 0, kc == 7, (wk, K("hT", kc)), (bkey,))
                evac(ui, bk, bkey)

        def gla_stageA(ui, col0, C, pre, W):
            p = ui % 2
            wgk_, wkk, wgv_, wkv, wgg_, wkg = W
            vb_tok, gb2, qtT, kend, dec, attT = vb_tok2[p], gb2_2[p], qtT2[p], kend2[p], dec2[p], attT2[p]
            bk, bkey = self.bank()
            for kc in range(8):
                self.mm(bk[0:C, 0:256], hT[:, kc, col0:col0 + C], wgk_(kc, 256, 512), kc == 0, kc == 7, (wkk, K("hT", kc)), (bkey,))
            self.cp("act", kb_tok[0:C, :], bk[0:C, 0:256], (bkey,), (K("kb_tok"),))
            bk, bkey = self.bank()
            for kc in range(8):
                self.mm(bk[0:C, :], hT[:, kc, col0:col0 + C], wgv_(kc), kc == 0, kc == 7, (wkv, K("hT", kc)), (bkey,))
            self.cp("act", vb_tok[0:C, :, :].rearrange("p h e -> p (h e)"), bk[0:C, :], (bkey,), (K("vb_tok", p),))
            if not pre:
                bk, bkey = self.bank()
                for kc in range(8):
                    self.mm(bk[0:C, :], hT[:, kc, col0:col0 + C], wgg_(kc), kc == 0, kc == 7, (wkg, K("hT", kc)), (bkey,))
                self.cp("act", gb2[0:C, :], bk[0:C, :], (bkey,), (K("gb2", p),))
            bk, bkey = self.bank()
            self.mm(bk[0:C, 0:256], gkT[0:17, col0:col0 + C], wgk_b[0:17, :], True, True, (K("gkT"), K("gkT_ones"), K("wgk_b")), (bkey,))
            self.act(e_sb[0:C, :], bk[0:C, 0:256], AF.Exp, (bkey,), (K("e_sb"),), scale=-1.0)
            self.act(Lsp[0:C, :], e_sb[0:C, :], AF.Ln, (K("e_sb"),), (K("Lsp"),), bias=1.0)
            bk2, bkey2 = self.bank()
            self.mm(bk2[0:C, 0:256], trisl_f[0:C, 0:C], Lsp[0:C, :], True, True, (K("trisl_f"), K("Lsp")), (bkey2,))
            self.act(e_sb[0:C, :], bk2[0:C, 0:256], AF.Exp, (bkey2,), (K("e_sb"),), scale=-1.0 / 16)
            self.tt("dve", kend[0:C, :], kb_tok[0:C, :], e_sb[0:C, :], ALU.mult, (K("kb_tok"), K("e_sb")), (K("kend", p),))
            bk3, bkey3 = self.bank()
            for h in range(4):
                self.mm(bk3[0:64, h * 128:h * 128 + C], Lsp[0:C, h * 64:(h + 1) * 64], triu_f[0:C, 0:C], True, True,
                        (K("Lsp"), K("triu_f")), (bkey3,))
            b3 = bk3[0:64, :].rearrange("p (c t) -> p c t", c=4)
            self.act(dec[:, :], b3[:, :, C - 1], AF.Exp, (bkey3,), (K("dec", p),), scale=-1.0 / 16)
            if pre:
                return
            self.act(e1[:, :, 0:C], b3[:, :, 0:C], AF.Exp, (bkey3,), (K("e1"),), scale=-1.0 / 16)
            self.act(e2[:, :, 0:C], b3[:, :, 0:C], AF.Exp, (bkey3,), (K("e2"),), scale=1.0 / 16)
            self.stt("dve", qtT[:, :, 0:C], qbT[:, :, col0:col0 + C], 0.125, e1[:, :, 0:C], ALU.mult, ALU.mult,
                     tuple(K("qbT", j) for j in range(4)) + (K("e1"),), (K("qtT", p),))
            self.tt("dve", ktT[:, :, 0:C], kbT[:, :, col0:col0 + C], e2[:, :, 0:C], ALU.mult, tuple(K("kbT", j) for j in range(4)) + (K("e2"),), (K("ktT"),))
            bk4, bkey4 = self.bank()
            for h in range(4):
                self.mm(bk4[0:C, h * 128:h * 128 + C], ktT[:, h, 0:C], qtT[:, h, 0:C], True, True, (K("ktT"), K("qtT", p)), (bkey4,))
            for h in range(4):
                self.tt("dve", attT[0:C, h, 0:C], bk4[0:C, h * 128:h * 128 + C], triu_f[0:C, 0:C], ALU.mult,
                        (bkey4, K("triu_f")), (K("attT", p, h),))

        def gla_stageB(ui, col0, C, pre, S_t, S_b, Skey):
            p = ui % 2
            vb_tok, gb2, qtT, kend, dec, attT = vb_tok2[p], gb2_2[p], qtT2[p], kend2[p], dec2[p], attT2[p]
            if not pre:
                obk, obkey = self.bank()
                for h in range(4):
                    self.mm(obk[0:C, h * 128:(h + 1) * 128], attT[0:C, h, 0:C], vb_tok[0:C, h, :], True, False, (K("attT", p, h), K("vb_tok", p)), (obkey,))
                    self.mm(obk[0:C, h * 128:(h + 1) * 128], qtT[:, h, 0:C], S_b[:, h, :], False, True,
                            (K("qtT", p), Skey + ("b",)), (obkey,))
                self.memset("dve", ssq[0:C, :], 0.0, tuple(K("ssq", h) for h in range(4)))
                for h in range(4):
                    self.act(attT[0:C, h, :], obk[0:C, h * 128:(h + 1) * 128], AF.Square, (obkey,), (K("attT", p, h), K("ssq", h)), accum_out=ssq[0:C, h:h + 1])
                self.act(rgl[0:C, :], ssq[0:C, :], AF.Ln, tuple(K("ssq", h) for h in range(4)) + (K("epsb"),), (K("rgl"),), bias=epsb[0:C, 0:1], scale=1.0 / 128)
                self.act(rgl[0:C, :], rgl[0:C, :], AF.Exp, (K("rgl"),), (K("rgl"),), scale=-0.5)
                self.act(gtanh[0:C, :], gb2[0:C, :], AF.Tanh, (K("gb2", p),), (K("gtanh"),), scale=0.5)
                self.stt("dve", gtanh[0:C, :], gtanh[0:C, :], 1.0, gb2[0:C, :], ALU.add, ALU.mult, (K("gtanh"), K("gb2", p)), (K("gtanh"),))
                self.tt("dve", gtanh[0:C, :], gtanh[0:C, :], ggla_s[0:C, :], ALU.mult, (K("gtanh"), K("ggla")), (K("gtanh"),))
                for h in range(4):
                    self.stt("dve", yb_tok[0:C, h * 128:(h + 1) * 128], obk[0:C, h * 128:(h + 1) * 128], rgl[0:C, h:h + 1],
                             gtanh[0:C, h * 128:(h + 1) * 128], ALU.mult, ALU.mult, (obkey, K("rgl"), K("gtanh")), (K("yb_tok", h),))
                for c in range(4):
                    self.tr(pbf[:, 512 + c * 128:512 + c * 128 + C], yb_tok[0:C, c * 128:(c + 1) * 128], ident_b[0:C, 0:C],
                            (K("yb_tok", c), K("ident_b")), (K("pbf"),))
                for c in range(4):
                    self.cp("act", ybT(c)[:, col0:col0 + C], pbf[:, 512 + c * 128:512 + c * 128 + C], (K("pbf"),), (YBK[c],))
            bk6, bkey6 = self.bank()
            for h in range(4):
                self.mm(bk6[0:64, h * 128:(h + 1) * 128], kend[0:C, h * 64:(h + 1) * 64], vb_tok[0:C, h, :], True, True, (K("kend", p), K("vb_tok", p)), (bkey6,))
            for h in range(4):
                self.stt("dve", S_t[:, h, :], S_t[:, h, :], dec[:, h:h + 1], bk6[0:64, h * 128:(h + 1) * 128],
                         ALU.mult, ALU.add, (Skey + (h,), K("dec", p), bkey6), (Skey + (h,),))
            self.cp("act", S_b[:], S_t[:], tuple(Skey + (h,) for h in range(4)), (Skey + ("b",),))

        def gla_units(units, pre, S_of, with_fn=None):
            wgk_, wkk = self.wload(win_g[G_QKB], kwin[G_QKB], 8, 512)
            wgv_, wkv = self.wload(win_g[G_VB], kwin[G_VB], 8, 512)
            wgg_, wkg = (None, None) if pre else self.wload(win_g[G_GB], kwin[G_GB], 8, 512)
            W = (wgk_, wkk, wgv_, wkv, wgg_, wkg)

            def body():
                gla_stageA(0, units[0][0], units[0][1], pre, W)
                for ui, (col0, C) in enumerate(units):
                    if ui + 1 < len(units):
                        gla_stageA(ui + 1, units[ui + 1][0], units[ui + 1][1], pre, W)
                    S_t, S_b, Skey, before_fn, after_fn = S_of(ui)
                    if before_fn is not None:
                        before_fn()
                    gla_stageB(ui, col0, C, pre, S_t, S_b, Skey)
                    if after_fn is not None:
                        after_fn()
            if with_fn is None:
                body()
            else:
                self.interleave([(body, [0, 1, 2]), (with_fn, [3, 4])])

        def attn_tile(tglob, tl):
            pTv = lambda kb, par: pT[:, kb, :, :].rearrange("p (c two) q -> p two c q", two=2)[:, par]
            for kb in range(5):
                slot = (tglob - 4 + kb) % 8
                banks = [self.bank(), self.bank()]
                for c in range(4):
                    for par in range(2):
                        pb = par * 64
                        bk, bkey = banks[par]
                        self.mm(bk[:, c * 128:(c + 1) * 128], kaT[slot][pb:pb + 64, c, :], qaT[pb:pb + 64, c, tl * 128:(tl + 1) * 128],
                                True, True, (K("kaT", slot, c), K("qaT", c)), (bkey,))
                for par in range(2):
                    bk, bkey = banks[par]
                    self.act(pTv(kb, par), bk[:].rearrange("p (c q) -> p c q", c=4), AF.Exp, (bkey,), (K("pT", kb, par),))
                tab = {0: (tab_mask[:, :].to_broadcast([128, 8, 128]) if False else None), 3: tab_prev, 4: tab_own}.get(kb)
                pk = (K("pT", kb, 0), K("pT", kb, 1))
                if kb == 0:
                    for h0 in range(0, 8, 4):
                        pass
                    self.tt("dve", pT[:, 0, :, :], pT[:, 0, :, :], bass.AP(tab_mask, 0, [[128, 128], [0, 8], [1, 128]]), ALU.mult,
                            pk + (K("tab_mask"),), pk)
                elif tab is not None:
                    self.tt("dve", pT[:, kb, :, :], pT[:, kb, :, :], tab[:, :, :], ALU.mult, pk + (K("etab", kb - 3),), pk)
            for hg in range(2):
                ob, obkey = obank[hg], K("obank", hg)
                for hh in range(4):
                    h = hg * 4 + hh
                    for kb in range(5):
                        slot = (tglob - 4 + kb) % 8
                        self.mm(ob[:, hh * 65:(hh + 1) * 65], pT[:, kb, h, :], vaug[slot][:, h, :], kb == 0, kb == 4,
                                (K("pT", kb, h % 2), K("vaug", slot), K("vaug1", slot)), (obkey,))
            for hg in range(2):
                ob, obkey = obank[hg], K("obank", hg)
                ov = ob[:, 0:260].rearrange("p (h e) -> p h e", h=4)
                self.ts("dve", rden[:, hg * 4:(hg + 1) * 4], ov[:, :, 64], 1e-30, None, ALU.add, None, (obkey,), (K("rden", hg),))
                S.add("dve", lambda e, hg=hg: e.reciprocal(rden[:, hg * 4:(hg + 1) * 4], rden[:, hg * 4:(hg + 1) * 4]), (K("rden", hg),), (K("rden", hg),))
                for hh in range(4):
                    h = hg * 4 + hh
                    self.ts("dve", ya_tok[:, h * 64:(h + 1) * 64], ov[:, hh, 0:64], rden[:, h:h + 1], None, ALU.mult, None,
                            (obkey, K("rden", hg)), (K("ya_tok", h),))
            yk = tuple(K("ya_tok", h) for h in range(8))
            for c in range(4):
                self.tr(pbf[:, c * 128:(c + 1) * 128], ya_tok[:, c * 128:(c + 1) * 128], ident_b[:], yk + (K("ident_b"),), (K("pbf"),))
            for c in range(4):
                self.cp("act", yaT(c)[:, tl * 128:(tl + 1) * 128], pbf[:, c * 128:(c + 1) * 128], (K("pbf"),), (YAK[c],))

        def attn_sample(s, slot):
            for rb in range(4):
                xb = xin[rb % 2]
                kx = K("xin", rb % 2)
                self.dma("sp", xb[:, 0:512], ck[s, rb * 128:(rb + 1) * 128, :], (), (kx,), ("xin", rb % 2))
                bk, bkey = self.bank()
                for c in range(4):
                    self.tr(bk[:, c * 128:(c + 1) * 128], xb[:, c * 128:(c + 1) * 128], ident_f[:], (kx, K("ident_f")), (bkey,))
                for c in range(4):
                    self.cp("act", kcT(c)[:, rb * 128:(rb + 1) * 128], bk[:, c * 128:(c + 1) * 128], (bkey,), (K("m2", c),))
                self.dma("sp", xb[:, 512:1024], cv[s, rb * 128:(rb + 1) * 128, :], (), (K("xinv", rb % 2),), ("xinv", rb % 2))
                self.cp("dve", vcaug[:, rb, :, 0:64], xb[:, 512:1024].rearrange("p (h e) -> p h e", h=8), (K("xinv", rb % 2),), (K("vcaug", rb),))
                self.cp("dve", vcaug[:, rb, :, 64], ones_f[:, 0:8], (K("ones_f"),), (K("vcaug1", rb),))
            q0 = s * 32
            for kb in range(5):
                kn = 128 if kb < 4 else 32
                banks = [self.bank(), self.bank()]
                for c in range(4):
                    for par in range(2):
                        pb = par * 64
                        bk, bkey = banks[par]
                        if kb < 4:
                            lhs = kcT(c)[pb:pb + 64, kb * 128:(kb + 1) * 128]
                            rk = (K("m2", c),)
                        else:
                            lhs = kaT[slot][pb:pb + 64, c, q0:q0 + 32]
                            rk = (K("kaT", slot, c),)
                        self.mm(bk[0:kn, c * 32:(c + 1) * 32], lhs, qaT[pb:pb + 64, c, q0:q0 + 32], True, True, rk + (K("qaT", c),), (bkey,))
                for par in range(2):
                    bk, bkey = banks[par]
                    dstv = pT[0:kn, kb, :, 0:32].rearrange("p (c two) q -> p two c q", two=2)[:, par]
                    self.act(dstv, bk[0:kn, 0:128].rearrange("p (c q) -> p c q", c=4), AF.Exp, (bkey,), (K("pT", kb, par),))
                pk = (K("pT", kb, 0), K("pT", kb, 1))
                if kb == 3:
                    self.tt("dve", pT[:, 3, :, 0:32], pT[:, 3, :, 0:32], tab_prev[:, :, 0:32], ALU.mult, pk + (K("etab", 0),), pk)
                elif kb == 4:
                    self.tt("dve", pT[0:32, 4, :, 0:32], pT[0:32, 4, :, 0:32], tab_own[0:32, :, 0:32], ALU.mult, pk + (K("etab", 1),), pk)
            for h in range(8):
                hg, hh = h // 4, h % 4
                for kb in range(5):
                    kn = 128 if kb < 4 else 32
                    rhs = vcaug[:, kb, h, :] if kb < 4 else vown[0:32, s, h, :]
                    rk = (K("vcaug", kb), K("vcaug1", kb)) if kb < 4 else (K("vown", s), K("vown1", s))
                    self.mm(obank[hg][0:32, hh * 65:(hh + 1) * 65], pT[0:kn, kb, h, 0:32], rhs, kb == 0, kb == 4,
                            (K("pT", kb, h % 2),) + rk, (K("obank", hg),))
            for hg in range(2):
                ob, obkey = obank[hg], K("obank", hg)
                ov = ob[0:32, 0:260].rearrange("p (h e) -> p h e", h=4)
                S.add("dve", lambda e, ov=ov, hg=hg: e.reciprocal(rden[0:32, hg * 4:(hg + 1) * 4], ov[:, :, 64]), (obkey,), (K("rden", hg),))
                for hh in range(4):
                    h = hg * 4 + hh
                    self.ts("dve", ya_tok[0:32, h * 64:(h + 1) * 64], ov[:, hh, 0:64], rden[0:32, h:h + 1], None, ALU.mult, None,
                            (obkey, K("rden", hg)), (K("ya_tok", h),))
            yk = tuple(K("ya_tok", h) for h in range(8))
            for c in range(4):
                self.tr(pbf[:, c * 128:c * 128 + 32], ya_tok[0:32, c * 128:(c + 1) * 128], ident_b[0:32, 0:32], yk + (K("ident_b"),), (K("pbf"),))
            for c in range(4):
                self.cp("act", yaT(c)[:, q0:q0 + 32], pbf[:, c * 128:c * 128 + 32], (K("pbf"),), (YAK[c],))

        def resid(n, ntile, Gfn, gname):
            for kc in range(8):
                tb, tk = (tmpf2, K("tmpf2")) if kc % 2 == 0 else (tmpf, K("tmpf"))
                self.tt("pool", tb[:, 0:n], m2[:, kc, 0:n], rbc[:, 0:n], ALU.mult, (K("m2", kc), K("rbc")), (tk,))
                xk = tuple(K("xT", kc, t) for t in range(ntile))
                self.stt("dve", xT[:, kc, 0:n], tb[:, 0:n], Gfn(kc), xT[:, kc, 0:n], ALU.mult, ALU.add,
                         (tk,) + modkeys[gname] + xk, xk)

        def resid_seq(Gt, gname):
            v3 = lambda ap: ap.rearrange("p (s w) -> p s w", s=4)
            for kc in range(8):
                xk = (K("xT", kc, 0),)
                self.tt("dve", tmpf2[:, 0:128], m2[:, kc, 0:128], rbc[:, 0:128], ALU.mult, (K("m2", kc), K("rbc")), (K("tmpf2"),))
                self.tt("dve", v3(tmpf2[:, 0:128]), v3(tmpf2[:, 0:128]), Gt[:, kc, 1:5].to_broadcast([128, 4, 32]), ALU.mult,
                        (K("tmpf2"),) + modkeys[gname], (K("tmpf2"),))
                self.tt("dve", xT[:, kc, 0:128], xT[:, kc, 0:128], tmpf2[:, 0:128], ALU.add, (K("tmpf2"),) + xk, xk)

        def merge_and_out(n):
            def ev_gate(dst, dkey):
                def f(bk, bkey):
                    self.act(dst[:, 0:n], bk[:, 0:n], AF.Tanh, (bkey,), (dkey,), scale=0.5)
                    self.ts("dve", dst[:, 0:n], dst[:, 0:n], 0.5, 0.5, ALU.mult, ALU.add, (dkey,), (dkey,))
                return f
            for G in range(2):
                wga, wka = self.wload(win_g[G_GA + G], kwin[G_GA + G], 8, 512)
                wba, wkba = self.wload(wbra_g[G], kwbra[G], 4, 512)
                for j in range(4):
                    c = G * 4 + j
                    bk, bkey = self.bank()
                    for kc in range(8):
                        self.mm(bk[:, 0:n], wga(kc, j * 128, (j + 1) * 128), hT[:, kc, 0:n], kc == 0, kc == 7, (wka, K("hT", kc)), (bkey,))
                    ev_gate(sgA, K("sgA"))(bk, bkey)
                    bk, bkey = self.bank()
                    for kc in range(4):
                        self.mm(bk[:, 0:n], wba(kc, j * 128, (j + 1) * 128), yaT(kc)[:, 0:n], kc == 0, kc == 3, (wkba, YAK[kc]), (bkey,))
                    self.tt("dve", m2[:, c, 0:n], sgA[:, 0:n], bk[:, 0:n], ALU.mult, (K("sgA"), bkey), (K("m2", c),))
                wgb, wkb = self.wload(win_g[G_GBR + G], kwin[G_GBR + G], 8, 512)
                wbb, wkbb = self.wload(wbrb_g[G], kwbrb[G], 4, 512)
                for j in range(4):
                    c = G * 4 + j
                    bk, bkey = self.bank()
                    for kc in range(8):
                        self.mm(bk[:, 0:n], wgb(kc, j * 128, (j + 1) * 128), hT[:, kc, 0:n], kc == 0, kc == 7, (wkb, K("hT", kc)), (bkey,))
                    ev_gate(sgB, K("sgB"))(bk, bkey)
                    bk, bkey = self.bank()
                    for kc in range(4):
                        self.mm(bk[:, 0:n], wbb(kc, j * 128, (j + 1) * 128), ybT(kc)[:, 0:n], kc == 0, kc == 3, (wkbb, YBK[kc]), (bkey,))
                    self.tt("dve", sgB[:, 0:n], sgB[:, 0:n], bk[:, 0:n], ALU.mult, (K("sgB"), bkey), (K("sgB"),))
                    self.tt("dve", mergedT(c)[:, 0:n], sgB[:, 0:n], m2[:, c, 0:n], ALU.add, (K("sgB"), K("m2", c)), (MK(c),))
            for G in range(2):
                wg, wk = self.wload(wout_g[G], kwout[G], 8, 512)
                for j in range(4):
                    c = G * 4 + j
                    bk, bkey = self.bank()
                    for kc in range(8):
                        self.mm(bk[:, 0:n], wg(kc, j * 128, (j + 1) * 128), mergedT(kc)[:, 0:n], kc == 0, kc == 7, (wk, MK(kc)), (bkey,))
                    self.cp("act", m2[:, c, 0:n], bk[:, 0:n], (bkey,), (K("m2", c),))
            rms_bcast(m2, n, lambda kc: (K("m2", kc),))

        ALIAS_KEYS = tuple(K("pT", kb, par) for kb in range(5) for par in range(2)) + \
            tuple(K(nm, 1, h) for nm in ("ua", "uah", "y0") for h in range(2))

        def alias_fence():
            self.memset("dve", pT_raw[:, 2559:2560], 0.0, ALIAS_KEYS)

        def ffn(n, nseg, hview, hkeyf, extra=()):
            alias_fence()
            w = n // nseg
            v3 = lambda ap: ap.rearrange("p (s w) -> p s w", s=nseg)
            for g in range(11):
                wg, wk = self.wload(wup_g[g], kwup[g], 8, 512)
                for pair in range(2):
                    ua_, y0_ = ua_sets[pair], y0_sets[pair]
                    for half in range(2):
                        jc = half * 2 + pair
                        jj = half * NFF + 2 * g + pair
                        bk, bkey = self.bank()
                        for kc in range(8):
                            self.mm(bk[:, 0:n], wg(kc, jc * 128, (jc + 1) * 128), hT[:, kc, 0:n], kc == 0, kc == 7, (wk, K("hT", kc)), (bkey,))
                        u3 = ua_[half][:, 0:nseg * (w + 2)].rearrange("p (s w) -> p s w", s=nseg)
                        uk, uh, yk = K("ua", pair, half), K("uah", pair, half), K("y0", pair, half)
                        yv = v3(y0_[half][:, 0:n])
                        self.cp("pool", u3[:, :, 0:2], hview(jj), (hkeyf(jj),) + extra, (uh,))
                        self.cp("act", u3[:, :, 2:2 + w], v3(bk[:, 0:n]), (bkey,), (uk,))
                        self.act(y0_[half][:, 0:n], bk[:, 0:n], AF.Identity, (bkey, K("wdwT2"), K("bdwT")), (yk,),
                                 bias=bdwT[:, jj:jj + 1], scale=wdwT[:, 2, jj:jj + 1])
                        self.stt("dve", yv, u3[:, :, 1:1 + w], wdwT[:, 1, jj:jj + 1], yv, ALU.mult, ALU.add, (uk, uh, yk, K("wdwT1")), (yk,))
                        self.stt("dve", yv, u3[:, :, 0:w], wdwT[:, 0, jj:jj + 1], yv, ALU.mult, ALU.add, (uk, uh, yk, K("wdwT0")), (yk,))
                        self.cp("pool", hview(jj), u3[:, :, w:w + 2], (uk,), (hkeyf(jj),))
                    j = 2 * g + pair
                    self.act(y0_[0][:, 0:n], y0_[0][:, 0:n], AF.Gelu_apprx_tanh, (K("y0", pair, 0),), (K("y0", pair, 0),))
                    self.tt("dve", actT[:, j, 0:n], y0_[0][:, 0:n], y0_[1][:, 0:n], ALU.mult, (K("y0", pair, 0), K("y0", pair, 1)), (K("actT", j),))
            for c in range(8):
                wg, wk = self.wload(wdown_g[c], kwdown[c], NFF, 128)
                bk, bkey = self.bank()
                for j in range(NFF):
                    self.mm(bk[:, 0:n], wg(j), actT[:, j, 0:n], j == 0, j == NFF - 1, (wk, K("actT", j)), (bkey,))
                self.cp("act", m2[:, c, 0:n], bk[:, 0:n], (bkey,), (K("m2", c),))
            rms_bcast(m2, n, lambda kc: (K("m2", kc),))
            alias_fence()

        def store_y(dst_rows, ntile):
            for t in range(ntile):
                yb_ = xin[t % 2]
                ky = K("xin", t % 2)
                for half in range(2):
                    bk, bkey = self.bank()
                    for j in range(4):
                        kc = half * 4 + j
                        self.tr(bk[:, j * 128:(j + 1) * 128], xT[:, kc, t * 128:(t + 1) * 128], ident_f[:], (K("xT", kc, t), K("ident_f")), (bkey,))
                    self.cp("act", yb_[:, half * 512:(half + 1) * 512], bk[:], (bkey,), (K("xinv", t % 2),) if half else (ky,))
                self.dma("sp", dst_rows[t * 128:(t + 1) * 128, :], yb_[:], (ky, K("xinv", t % 2)), (), ("xin", t % 2))

        def load_dma(src_rows, t):
            i = (0, 2)[t % 2]
            self.dma("sp", xin[i][:], src_rows[t * 128:(t + 1) * 128, :], (), (K("xin", i), K("xinv", i)), ("xin", i))

        def prefetch_x(nxt):
            if nxt is None or self.prefetched:
                return
            for t in range(min(2, nxt[1])):
                load_dma(nxt[0], t)
            self.prefetched = True

        def load_tile(src_rows, t, xb, i, nnt):
            kx, kv = K("xin", i), K("xinv", i)
            if not (self.prefetched and t < 2):
                pass
            for half in range(2):
                bk, bkey = self.bank()
                for j in range(4):
                    kc = half * 4 + j
                    self.tr(bk[:, j * 128:(j + 1) * 128], xb[:, kc * 128:(kc + 1) * 128], ident_f[:], (kx if half == 0 else kv, K("ident_f")), (bkey,))
                self.cp("act", xT[:, half * 4:half * 4 + 4, t * 128:(t + 1) * 128],
                        bk[:].rearrange("p (j n) -> p j n", j=4), (bkey,), tuple(K("xT", half * 4 + j, t) for j in range(4)))
            if t + 2 < nnt:
                load_dma(src_rows, t + 2)

        def store_tile(dst_rows, t, xb, i):
            kx, kv = K("xin", i), K("xinv", i)
            for half in range(2):
                bk, bkey = self.bank()
                for j in range(4):
                    kc = half * 4 + j
                    self.tr(bk[:, j * 128:(j + 1) * 128], xT[:, kc, t * 128:(t + 1) * 128], ident_f[:], (K("xT", kc, t), K("ident_f")), (bkey,))
                self.cp("act", xb[:, half * 512:(half + 1) * 512], bk[:], (bkey,), (kv,) if half else (kx,))
            self.dma("sp", dst_rows[t * 128:(t + 1) * 128, :], xb[:], (kx, kv), (), ("xin", i))

        def swap_tiles(dst_rows, ntile, nxt):
            nsrc, nnt = nxt if nxt is not None else (None, 0)
            prefetch_x(nxt)
            self.prefetched = False
            for t in range(max(ntile, nnt)):
                if t < ntile:
                    store_tile(dst_rows, t, xin[1], 1)
                if t < nnt:
                    i = (0, 2)[t % 2]
                    load_tile(nsrc, t, xin[i], i, nnt)

        def store_hist(hsrc, hkeys, dst):
            self.cp("dve", h88[:], hsrc.rearrange("p j t -> p t j"), hkeys, (K("h88"),))
            bk, bkey = self.bank()
            self.tr(bk[0:88, 0:128], h88[:].rearrange("p t j -> p (t j)"), ident_f[:], (K("h88"), K("ident_f")), (bkey,))
            self.cp("act", stage[0:88, :], bk[0:88, 0:128], (bkey,), (K("stage"),))
            self.dma("sp", dst, stage[0:88, :], (K("stage"),), (), ("stage_o",))

        def kv_stage(bk, bkey, which, dst):
            st = xin[1][:, which * 512:(which + 1) * 512]
            sk = K("xin", 1) if which == 0 else K("xinv", 1)
            self.cp("act", st, bk[:], (bkey,), (sk,))
            self.dma("sp", dst, st, (sk,), (), ("xin", 1) if which == 0 else ("xinv", 1))

        LAST_KV0 = 16 + NMAIN - 4

        def common_proj(n, units, tglob0, kv_from, pre, flagged, kv_out):
            if not pre:
                wg, wk = self.wload(win_g[G_QA], kwin[G_QA], 8, 512)
                def ev_q(j, bk, bkey):
                    self.act(qaT[:, j, 0:n], bk[:, 0:n], AF.Identity, (bkey,), (K("qaT", j),), scale=0.125)
                pairs_fm(wg, wk, n, ev_q)
            if kv_from < len(units):
                wg, wk = self.wload(win_g[G_KA], kwin[G_KA], 8, 512)
                def ev_k(j, bk, bkey):
                    for ui in range(kv_from, len(units)):
                        c0, C = units[ui]
                        if C == 128:
                            slot = (tglob0 + ui) % 8
                            self.cp("act", kaT[slot][:, j, :], bk[:, c0:c0 + 128], (bkey,), (K("kaT", slot, j),))
                    if units[0][1] != 128:
                        self.cp("act", kaT[tglob0 % 8][:, j, :], bk[:, 0:128], (bkey,), (K("kaT", tglob0 % 8, j),))
                pairs_fm(wg, wk, n, ev_k)
                kunits = [(ui, units[ui]) for ui in range(kv_from, len(units)) if kv_out(ui, 0) is not None] if units[0][1] == 128 else []
                if kunits:
                    units_tm(wg, wk, 0, 512, [u for _, u in kunits], lambda i, bk, bkey: kv_stage(bk, bkey, 0, kv_out(kunits[i][0], 0)))
                if units[0][1] != 128:
                    units_tm(wg, wk, 0, 512, [(0, 128)], lambda i, bk, bkey: kv_stage(bk, bkey, 0, kv_out(0, 0)))
                wg, wk = self.wload(win_g[G_VA], kwin[G_VA], 8, 512)
                if units[0][1] == 128:
                    def ev_v(i, bk, bkey):
                        ui = kv_from + i
                        slot = (tglob0 + ui) % 8
                        self.cp("act", vaug[slot][:, :, 0:64], bk[:].rearrange("p (h e) -> p h e", h=8), (bkey,), (K("vaug", slot),))
                        if flagged:
                            self.ts("dve", vaug[slot][:, :, 64], ones_f[:, 0:8], flag[:, 0:1], None, ALU.mult, None,
                                    (K("ones_f"), K("flag")), (K("vaug1", slot),))
                        else:
                            self.cp("dve", vaug[slot][:, :, 64], ones_f[:, 0:8], (K("ones_f"),), (K("vaug1", slot),))
                        if kv_out(ui, 1) is not None:
                            kv_stage(bk, bkey, 1, kv_out(ui, 1))
                    units_tm(wg, wk, 0, 512, units[kv_from:], ev_v)
                else:
                    units_tm(wg, wk, 0, 512, [(0, 128)], lambda i, bk, bkey: kv_stage(bk, bkey, 1, kv_out(0, 1)))
            if not pre:
                wg, wk = self.wload(win_g[G_QKB], kwin[G_QKB], 8, 512)
                def ev_qb(j, bk, bkey):
                    self.cp("act", qbT[:, j, 0:n], bk[0:64, 0:n], (bkey,), (K("qbT", j),))
                def ev_kb(j, bk, bkey):
                    self.cp("act", kbT[:, j, 0:n], bk[0:64, 0:n], (bkey,), (K("kbT", j),))
                heads_fm(wg, wk, 0, 4, n, ev_qb)
                heads_fm(wg, wk, 256, 4, n, ev_kb)
            wg, wk = self.wload(wgk_g, kwgk, 8, 16)
            bk, bkey = self.bank()
            for kc in range(8):
                self.mm(bk[0:16, 0:n], wg(kc), hT[:, kc, 0:n], kc == 0, kc == 7, (wk, K("hT", kc)), (bkey,))
            self.cp("act", gkT[0:16, 0:n], bk[0:16, 0:n], (bkey,), (K("gkT"),))

        SK = K("Sst")
        def prompt_block(src_rows, ntile, tglob0, mode, dst_rows, kv_from, preloaded=False, nxt=None):
            n = ntile * 128
            pre = mode == "prefix"
            flagged = mode != "full"
            units = [(t * 128, 128) for t in range(ntile)]
            mark = lambda nm: self.marks.append((mode, tglob0, nm, len(S.ops)))
            mark("start")
            if not preloaded:
                load_xT(src_rows, ntile)
            rms_bcast(xT, n, xkeys(ntile))
            if flagged:
                modulate(n, ntile, "AmP", "BmP", lambda kc: AmP[:, kc:kc + 1], lambda kc: BmP[:, kc:kc + 1])
            else:
                modulate(n, ntile, "Am", "Bm", lambda kc: Am[:, kc, 0:1], lambda kc: Bm[:, kc, 0:1])
            if pre and nxt is not None:
                swap_tiles(None, 0, nxt)
            mark("norm1")

            def kv_out(ui, which):
                tg = tglob0 + ui
                if mode != "full" or tg < LAST_KV0:
                    return None
                r0 = (tg - LAST_KV0) * 128
                return (kp_o if which == 0 else vp_o)[r0:r0 + 128, :]
            common_proj(n, units, tglob0, kv_from, pre, flagged, kv_out)
            mark("proj")
            if pre:
                gla_units(units, pre, lambda ui: (Sst, Sbf, SK, None, None))
                mark("gla")
                return
            def attn_all():
                for t in range(ntile):
                    attn_tile(tglob0 + t, t)
            gla_units(units, pre, lambda ui: (Sst, Sbf, SK, None, None), attn_all)
            mark("attn")
            merge_and_out(n)
            resid(n, ntile, lambda kc: Gm[:, kc, 0:1], "Gm")
            mark("merge")
            rms_bcast(xT, n, xkeys(ntile))
            if flagged:
                modulate(n, ntile, "AfP", "BfP", lambda kc: AfP[:, kc:kc + 1], lambda kc: BfP[:, kc:kc + 1])
            else:
                modulate(n, ntile, "Af", "Bf", lambda kc: Af[:, kc, 0:1], lambda kc: Bf[:, kc, 0:1])
            prefetch_x(nxt)
            ffn(n, 1, lambda jj: hist[:, jj:jj + 1, :], lambda jj: K("hist", jj))
            mark("ffn")
            resid(n, ntile, lambda kc: Gf[:, kc, 0:1], "Gf")
            swap_tiles(dst_rows, ntile, nxt)
            mark("end")

        def sample_block():
            n = 128
            slot = 0
            units = [(s * 32, 32) for s in range(4)]
            self.marks.append(("sample", 0, "start", len(S.ops)))
            rms_bcast(xT, n, xkeys(1))
            modulate_seq(Am, Bm, "Am", "Bm")
            smark = lambda nm: self.marks.append(("sample", 0, nm, len(S.ops)))
            smark("norm1")
            common_proj(n, units, slot, 0, False, False, lambda ui, which: (ks_o if which == 0 else vs_o))
            smark("proj")
            SSK = K("Sst")
            def S_of(ui):
                def before():
                    self.dma("sp", Ss[:], sgla[ui].rearrange("h k v -> k h v"), (), tuple(SSK + (h,) for h in range(4)), ("Ss",))
                    self.cp("act", Ssb[:], Ss[:], tuple(SSK + (h,) for h in range(4)), (SSK + ("b",),))
                def after():
                    self.dma("sp", glas_o[ui].rearrange("h k v -> k h v"), Ss[:], tuple(SSK + (h,) for h in range(4)), (), ("Ss",))
                return (Ss, Ssb, SSK, before, after)
            wgv, wkv = self.wload(win_g[G_VA], kwin[G_VA], 8, 512)
            for s in range(4):
                bk, bkey = self.bank()
                for kc in range(8):
                    self.mm(bk[0:32, :], hT[:, kc, s * 32:(s + 1) * 32], wgv(kc), kc == 0, kc == 7, (wkv, K("hT", kc)), (bkey,))
                self.cp("act", vown[:, s, :, 0:64], bk[0:32, :].rearrange("p (h e) -> p h e", h=8), (bkey,), (K("vown", s),))
                self.cp("dve", vown[:, s, :, 64], ones_f[0:32, 0:8], (K("ones_f"),), (K("vown1", s),))
            def attn_all():
                for s in range(4):
                    attn_sample(s, slot)
            gla_units(units, False, S_of, attn_all)
            smark("attn")
            merge_and_out(n)
            resid_seq(Gm, "Gm")
            smark("merge")
            rms_bcast(xT, n, xkeys(1))
            modulate_seq(Af, Bf, "Af", "Bf")
            for s in range(4):
                load_T(sconv[s], 88, hist_s[:, s, :, :].rearrange("p j t -> p t j"), "hist_s_in%d" % s,
                       view=lambda a: a.rearrange("p (t j) -> p t j", t=2))
            hs_in = tuple(K("hist_s_in%d" % s) for s in range(4))
            ffn(n, 4, lambda jj: hist_s[:, :, jj, :], lambda jj: K("hist_s", jj), hs_in)
            smark("ffn")
            resid_seq(Gf, "Gf")
            store_y(ysam, 1)
            for s in range(4):
                store_hist(hist_s[:, s, :, :], tuple(K("hist_s", jj) for jj in range(44)), convs_o[s])

        STG = int(os.environ.get("KSTAGE", "99"))
        if STG <= 0:
            return S.finalize()
        sched = []
        t0 = 0
        while t0 < 11:
            nt = min(NB, 11 - t0)
            sched.append(("prefix", xpre[t0 * 128:(t0 + nt) * 128, :], nt, t0, None, nt))
            t0 += nt
        while t0 < 15:
            nt = min(NB, 15 - t0)
            sched.append(("prefix", xpre[t0 * 128:(t0 + nt) * 128, :], nt, t0, None, 0))
            t0 += nt
        sched.append(("overlap", xov, 1, 15, yov, 0))
        for b in range(NMAIN // NB):
            sched.append(("full", xmain[b * NBT:(b + 1) * NBT, :], NB, 16 + b * NB, ymain[b * NBT:(b + 1) * NBT, :], 0))
        sched.append(("sample", xsam, 1, 0, None, 0))
        for bi, (mode, src, nt, tg, dst, kvf) in enumerate(sched):
            nxt = (sched[bi + 1][1], sched[bi + 1][2]) if bi + 1 < len(sched) else None
            if mode == "sample":
                break
            if mode == "overlap":
                ada_part2()
            prompt_block(src, nt, tg, mode, dst, kvf, preloaded=bi > 0, nxt=nxt)
            if mode == "overlap":
                self.ts("dve", hist[:].rearrange("p j t -> p (j t)"), hist[:].rearrange("p j t -> p (j t)"), flag[:, 0:1], None, ALU.mult, None,
                        tuple(K("hist", jj) for jj in range(44)) + (K("flag"),), tuple(K("hist", jj) for jj in range(44)))
        self.dma("sp", glap_o.rearrange("h k v -> k h v"), Sst[:], tuple(SK + (h,) for h in range(4)), (), ("glap",))
        store_hist(hist[:], tuple(K("hist", jj) for jj in range(44)), convp_o)
        sample_block()
        return S.finalize()


_CACHE = {}


def _program():
    if "nc" not in _CACHE:
        b = Builder()
        b.build()
        _CACHE["nc"] = b.nc
    return _CACHE["nc"]


def kernel(x_prompt, x_sample, cache_k_a, cache_v_a, state_gla, state_conv, c_prompt, c_sample,
           w_ada, b_ada, g_pre_mix, g_post_mix, g_pre_ffn, g_post_ffn, w_in, w_gk2, b_gk,
           rel_bias, g_gla, w_br_a, w_br_b, w_out, w_up, w_dw, b_dw, w_down):
    f = lambda a: np.ascontiguousarray(np.asarray(a, dtype=np.float32))
    x_prompt, x_sample = f(x_prompt), f(x_sample)
    rb = f(rel_bias)[0]
    kk = np.arange(128)[:, None]
    qq = np.arange(128)[None, :]
    idx_prev = np.clip(qq + 128 - kk, -128, 128) + 128
    idx_own = np.clip(qq - kk, -128, 128) + 128
    btab = np.stack([rb[:, idx_prev].transpose(1, 0, 2), rb[:, idx_own].transpose(1, 0, 2)])
    cvec = np.broadcast_to(rb[:, 256][None, :], (128, 8))
    wi = f(w_in)[0]
    wi = np.concatenate([wi[:, :3072], wi[:, 3088:], wi[:, 3072:3088]], axis=1)
    wu = f(w_up)[0]
    cols = []
    for g in range(11):
        cols += [wu[:, 2 * g * 128:(2 * g + 2) * 128], wu[:, DFF + 2 * g * 128:DFF + (2 * g + 2) * 128]]
    wu = np.concatenate(cols, axis=1)
    shared = {
        "w_ada": f(w_ada)[0], "b_ada": f(b_ada)[0].reshape(48, 128),
        "gvec": np.concatenate([f(g_pre_mix)[0], f(g_post_mix)[0], f(g_pre_ffn)[0], f(g_post_ffn)[0]]).reshape(32, 128),
        "w_in": f(wi), "w_gk2": f(w_gk2)[0], "b_gk": f(b_gk)[0].reshape(1, 256),
        "btab": f(btab), "cvec": f(cvec), "ggla": f(np.broadcast_to(np.tile(f(g_gla)[0], 4)[None, :], (128, 512))),
        "w_br_a": f(w_br_a)[0], "w_br_b": f(w_br_b)[0], "w_out": f(w_out)[0], "w_up": f(wu),
        "w_dw": f(w_dw)[0].reshape(3 * 44, 128), "b_dw": f(b_dw)[0].reshape(44, 128), "w_down": f(w_down)[0],
    }
    in_maps = []
    for c in range(8):
        b, hf = c // 2, c % 2
        m = dict(shared)
        if hf == 1:
            m["xpre"] = x_prompt[b, 0:NPRE * 128]
            m["xov"] = x_prompt[b, NPRE * 128:2048]
        else:
            m["xpre"] = np.zeros((NPRE * 128, D), np.float32)
            m["xov"] = np.zeros((128, D), np.float32)
        m["xmain"] = x_prompt[b, hf * 2048:(hf + 1) * 2048]
        m["xsam"] = x_sample[4 * c:4 * c + 4].reshape(128, D)
        m["crow"] = f(np.concatenate([f(c_prompt)[b:b + 1], f(c_sample)[4 * c:4 * c + 4]], 0).reshape(40, 128))
        m["flag"] = np.full((128, 1), float(hf), np.float32)
        m["ck"] = f(cache_k_a)[0, 4 * c:4 * c + 4].reshape(4, 512, 512)
        m["cv"] = f(cache_v_a)[0, 4 * c:4 * c + 4].reshape(4, 512, 512)
        m["sgla"] = f(state_gla)[0, 4 * c:4 * c + 4]
        m["sconv"] = f(state_conv)[0, 4 * c:4 * c + 4].reshape(4, 88, 128)
        in_maps.append({k: np.ascontiguousarray(v) for k, v in m.items()})
    nc = _program()
    cores = [int(t) for t in os.environ.get("KCORES", "0,1,2,3,4,5,6,7").split(",")]
    if os.environ.get("KTRACE"):
        res = run_bass_kernel_spmd(nc, [in_maps[c] for c in cores], core_ids=list(range(len(cores))), trace=True)
        print("EXEC_TIME_NS", res.exec_time_ns)
    else:
        res = run_bass_kernel_spmd(nc, [in_maps[c] for c in cores], core_ids=list(range(len(cores))))
    R = {c: res.results[i] for i, c in enumerate(cores)}
    y_prompt = np.zeros((4, 4096, D), np.float32)
    y_sample = np.zeros((32, 32, D), np.float32)
    k_p = np.zeros((1, 4, 512, 8, 64), np.float32); v_p = np.zeros_like(k_p)
    gla_p = np.zeros((1, 4, 4, 64, 128), np.float32)
    conv_p = np.zeros((1, 4, 2, 2 * DFF), np.float32)
    k_s = np.zeros((1, 32, 32, 8, 64), np.float32); v_s = np.zeros_like(k_s)
    gla_s = np.zeros((1, 32, 4, 64, 128), np.float32)
    conv_s = np.zeros((1, 32, 2, 2 * DFF), np.float32)
    for c in cores:
        b, hf = c // 2, c % 2
        r = R[c]
        y_prompt[b, hf * 2048:(hf + 1) * 2048] = r["ymain"]
        y_sample[4 * c:4 * c + 4] = r["ysam"].reshape(4, 32, D)
        if hf == 1:
            k_p[0, b] = r["kp"].reshape(512, 8, 64)
            v_p[0, b] = r["vp"].reshape(512, 8, 64)
            gla_p[0, b] = r["glap"]
            conv_p[0, b] = r["convp"].reshape(2, 2 * DFF)
        k_s[0, 4 * c:4 * c + 4] = r["ks"].reshape(4, 32, 8, 64)
        v_s[0, 4 * c:4 * c + 4] = r["vs"].reshape(4, 32, 8, 64)
        gla_s[0, 4 * c:4 * c + 4] = r["glas"]
        conv_s[0, 4 * c:4 * c + 4] = r["convs"].reshape(4, 2, 2 * DFF)
    return (y_prompt, y_sample, k_p, v_p, gla_p, conv_p, k_s, v_s, gla_s, conv_s)
```

```python
from contextlib import ExitStack
import os

import numpy as np
import concourse.bass as bass
import concourse.mybir as mybir
from concourse.bass_utils import run_bass_kernel_spmd

F32 = mybir.dt.float32
BF16 = mybir.dt.bfloat16
AF = mybir.ActivationFunctionType
ALU = mybir.AluOpType

D = 1024
KC = 8
DFF = 2816
NFF = 22
DIN = 5136
G_QA, G_KA, G_VA, G_QKB, G_VB, G_GB, G_GA, G_GBR = 0, 1, 2, 3, 4, 5, 6, 8
EPS = 1e-6
NEG = -30000.0
NPRE = 15
NMAIN = 16
NB = 4


class Op:
    __slots__ = ("eng", "fn", "reads", "writes", "dsem", "signal", "sigval", "deps")

    def __init__(self, eng, fn, reads, writes, dsem):
        self.eng, self.fn, self.reads, self.writes, self.dsem = eng, fn, reads, writes, dsem
        self.signal = False
        self.sigval = 0
        self.deps = ()


class Sched:
    DMA = ("sp", "pool_dma")

    def __init__(self, nc, stack):
        self.nc = nc
        self.stack = stack
        self.ops = []
        self.eng_obj = {"pe": nc.tensor, "act": nc.scalar, "dve": nc.vector, "pool": nc.gpsimd,
                        "sp": nc.sync, "pool_dma": nc.gpsimd}
        self.wuses = []
        self.wdepth = 2

    def add(self, eng, fn, reads=(), writes=(), dsem=None):
        op = Op(eng, fn, tuple(reads), tuple(writes), dsem)
        self.ops.append(op)
        return op

    def queue_of(self, op):
        return "pool" if op.eng == "pool_dma" else op.eng

    def finalize(self):
        nc = self.nc
        inserts = {}
        lastrd = {}
        for idx, op in enumerate(self.ops):
            for k in op.reads:
                if k and k[0] == "wuse":
                    lastrd[k[1]] = idx
        prev = 0
        for i, (pos, op) in enumerate(self.wuses):
            tgt = self.wuses[max(0, i - self.wdepth)][0]
            if i >= 3 and (i - 3) in lastrd:
                tgt = max(tgt, lastrd[i - 3] + 1)
            tgt = max(tgt, prev)
            prev = tgt
            assert tgt <= pos, (i, tgt, pos)
            inserts.setdefault(tgt, []).append(op)
        ops = []
        for i, op in enumerate(self.ops):
            if i in inserts:
                ops.extend(inserts[i])
            ops.append(op)
        self.ops = ops
        last_w = {}
        readers = {}
        for i, op in enumerate(ops):
            deps = set()
            q = self.queue_of(op)
            isdma = op.dsem is not None
            for k in op.reads:
                j = last_w.get(k)
                if j is not None:
                    deps.add(j)
            for k in op.writes:
                j = last_w.get(k)
                if j is not None:
                    oj = ops[j]
                    if isdma or oj.dsem is not None or self.queue_of(oj) != q or q != "pe":
                        deps.add(j)
                for j in readers.get(k, ()):
                    oj = ops[j]
                    if isdma or oj.dsem is not None or self.queue_of(oj) != q or (q != "pe" and os.environ.get("KWAR", "1") == "1"):
                        deps.add(j)
            deps.discard(i)
            op.deps = tuple(sorted(deps))
            for j in op.deps:
                ops[j].signal = True
            for k in op.reads:
                lst = readers.setdefault(k, [])
                if op.dsem is None:
                    lst[:] = [j for j in lst if ops[j].dsem is not None or self.queue_of(ops[j]) != q]
                lst.append(i)
            for k in op.writes:
                last_w[k] = i
                readers[k] = []
        esem = {}
        for e in ("pe", "act", "dve", "pool"):
            esem[e] = self.stack.enter_context(nc.semaphore("sem_" + e))
        dsems = {}
        cnt = {}
        for op in ops:
            if op.dsem is not None:
                if op.dsem not in dsems:
                    dsems[op.dsem] = self.stack.enter_context(nc.semaphore("dsem_%d" % len(dsems)))
                    cnt[op.dsem] = 0
                cnt[op.dsem] += 1
                op.sigval = 16 * cnt[op.dsem]
            elif op.signal:
                q = self.queue_of(op)
                cnt[q] = cnt.get(q, 0) + 1
                op.sigval = cnt[q]
        known = {q: {} for q in ("pe", "act", "dve", "pool", "sp")}
        for op in ops:
            q = self.queue_of(op)
            eng = self.eng_obj[op.eng]
            need = {}
            for j in op.deps:
                oj = ops[j]
                s = dsems[oj.dsem] if oj.dsem is not None else esem[self.queue_of(oj)]
                key = id(s)
                if key not in need or need[key][1] < oj.sigval:
                    need[key] = (s, oj.sigval)
            for key, (s, v) in need.items():
                if known[q].get(key, 0) >= v:
                    continue
                eng.wait_ge(s, v)
                known[q][key] = v
            ins = op.fn(eng)
            if op.dsem is not None:
                ins.then_inc(dsems[op.dsem], 16)
            elif op.signal:
                ins.then_inc(esem[q], 1)
        self.counts = dict((str(k), v) for k, v in cnt.items())
        for k, s in dsems.items():
            nc.sync.wait_ge(s, 16 * cnt[k])
        return len(ops)


class Builder:
    def __init__(self):
        self.stack = ExitStack()
        self.nc = bass.Bass("TRN2", target_bir_lowering=False)
        self.S = Sched(self.nc, self.stack)
        self.nbank = 0
        self.wuse_n = 0
        self.prefetched = False
        self.bank_set = [0, 1, 2, 3, 4]
        self.marks = []
        self.uid = 0

    def din(self, name, shape, dt=F32):
        return self.nc.dram_tensor(name, list(shape), dt, kind="ExternalInput").ap()

    def dout(self, name, shape):
        return self.nc.dram_tensor(name, list(shape), F32, kind="ExternalOutput").ap()

    def dscr(self, name, shape, dt=BF16):
        return self.nc.dram_tensor(name, list(shape), dt, kind="Internal").ap()

    def sb(self, name, shape, dt=F32):
        return self.stack.enter_context(self.nc.sbuf_tensor(name, list(shape), dt))

    def ps(self, name, shape, dt=F32):
        return self.stack.enter_context(self.nc.psum_tensor(name, list(shape), dt))

    def bank(self):
        bs = self.bank_set
        i = bs[self.nbank % len(bs)]
        self.nbank += 1
        return self.banks[i], ("ps", i)

    def key(self, name):
        self.uid += 1
        return (name, self.uid)

    def mm(self, out, lhsT, rhs, start, stop, reads, writes):
        self.S.add("pe", lambda e: e.matmul(out, lhsT, rhs, start=start, stop=stop), reads, writes)

    def tr(self, out, in_, ident, reads, writes):
        self.S.add("pe", lambda e: e.transpose(out, in_, ident), reads, writes)

    def act(self, out, in_, func, reads, writes, bias=None, scale=None, accum_out=None):
        kw = {}
        if bias is not None:
            kw["bias"] = bias
        if scale is not None:
            kw["scale"] = scale
        if accum_out is not None:
            kw["accum_out"] = accum_out
        self.S.add("act", lambda e: e.activation(out, in_, func, **kw), reads, writes)

    def tt(self, eng, out, in0, in1, op, reads, writes):
        self.S.add(eng, lambda e: e.tensor_tensor(out, in0, in1, op), reads, writes)

    def ts(self, eng, out, in0, s1, s2, op0, op1, reads, writes):
        if s2 is None:
            self.S.add(eng, lambda e: e.tensor_scalar(out, in0, s1, None, op0), reads, writes)
        else:
            self.S.add(eng, lambda e: e.tensor_scalar(out, in0, s1, s2, op0, op1), reads, writes)

    def stt(self, eng, out, in0, scalar, in1, op0, op1, reads, writes):
        self.S.add(eng, lambda e: e.scalar_tensor_tensor(out, in0, scalar, in1, op0=op0, op1=op1), reads, writes)

    def cp(self, eng, out, in_, reads, writes):
        if eng == "act":
            self.S.add("act", lambda e: e.copy(out, in_), reads, writes)
        else:
            self.S.add(eng, lambda e: e.tensor_copy(out, in_), reads, writes)

    def memset(self, eng, ap, val, writes):
        self.S.add(eng, lambda e: e.memset(ap, val), (), writes)

    def dma(self, q, out, in_, reads, writes, dsem, slow=False):
        if slow:
            self.S.add(q, lambda e: e.dma_start(out=out, in_=in_, allow_slow_non_contiguous=True), reads, writes, dsem)
        else:
            self.S.add(q, lambda e: e.dma_start(out=out, in_=in_), reads, writes, dsem)

    def interleave(self, fns_banks):
        main = self.S.ops
        streams = []
        for fn, banks in fns_banks:
            self.S.ops = []
            self.bank_set = banks
            fn()
            streams.append(self.S.ops)
        self.S.ops = main
        self.bank_set = [0, 1, 2, 3, 4]
        idx = [0] * len(streams)
        total = sum(len(st) for st in streams)
        for _ in range(total):
            best, bf = None, None
            for si, st in enumerate(streams):
                if idx[si] < len(st):
                    frac = idx[si] / len(st)
                    if bf is None or frac < bf:
                        best, bf = si, frac
            main.append(streams[best][idx[best]])
            idx[best] += 1

    def wload(self, scr_g, wkeys, kcn, width):
        i = self.wuse_n
        self.wuse_n += 1
        slot = i % 3
        buf = self.wbufs[slot]
        key = ("wuse", i)
        dst = buf[:, 0:kcn * width].rearrange("p (kc n) -> p kc n", kc=kcn)
        op = Op("sp", lambda e: e.dma_start(out=dst, in_=scr_g), tuple(wkeys),
                (key, ("wbuf", slot)) + ((("wuse", i - 3),) if i >= 3 else ()), ("wstream", slot))
        self.S.wuses.append((len(self.S.ops), op))
        return (lambda kc, a=0, b=width: buf[:, kc * width + a: kc * width + b]), key

    def build(self):
        nc = self.nc
        S = self.S
        sb, ps = self.sb, self.ps
        NBT = NB * 128
        K = lambda *a: tuple(a)
        xpre = self.din("xpre", [NPRE * 128, D])
        xov = self.din("xov", [128, D])
        xmain = self.din("xmain", [NMAIN * 128, D])
        xsam = self.din("xsam", [128, D])
        crow = self.din("crow", [40, 128])
        flag_d = self.din("flag", [128, 1])
        ck = self.din("ck", [4, 512, 512])
        cv = self.din("cv", [4, 512, 512])
        sgla = self.din("sgla", [4, 4, 64, 128])
        sconv = self.din("sconv", [4, 88, 128])
        w_ada = self.din("w_ada", [D, 6 * D])
        b_ada = self.din("b_ada", [48, 128])
        gvec = self.din("gvec", [32, 128])
        w_in = self.din("w_in", [D, DIN])
        w_gk2 = self.din("w_gk2", [16, 256])
        b_gk = self.din("b_gk", [1, 256])
        btab = self.din("btab", [2, 128, 8, 128])
        cvec = self.din("cvec", [128, 8])
        ggla = self.din("ggla", [128, 512])
        w_br_a = self.din("w_br_a", [512, D])
        w_br_b = self.din("w_br_b", [512, D])
        w_out = self.din("w_out", [D, D])
        w_up = self.din("w_up", [D, 2 * DFF])
        w_dw = self.din("w_dw", [3 * 44, 128])
        b_dw = self.din("b_dw", [44, 128])
        w_down = self.din("w_down", [DFF, D])

        ymain = self.dout("ymain", [NMAIN * 128, D])
        ysam = self.dout("ysam", [128, D])
        yov = self.dout("yov", [128, D])
        kp_o = self.dout("kp", [512, 512])
        vp_o = self.dout("vp", [512, 512])
        glap_o = self.dout("glap", [4, 64, 128])
        convp_o = self.dout("convp", [88, 128])
        ks_o = self.dout("ks", [128, 512])
        vs_o = self.dout("vs", [128, 512])
        glas_o = self.dout("glas", [4, 4, 64, 128])
        convs_o = self.dout("convs", [4, 88, 128])

        win_g = self.dscr("win_g", [10, 128, 8, 512])
        wgk_g = self.dscr("wgk_g", [128, 8, 16])
        wbra_g = self.dscr("wbra_g", [2, 128, 4, 512])
        wbrb_g = self.dscr("wbrb_g", [2, 128, 4, 512])
        wout_g = self.dscr("wout_g", [2, 128, 8, 512])
        wup_g = self.dscr("wup_g", [11, 128, 8, 512])
        wdown_g = self.dscr("wdown_g", [8, 128, NFF, 128])

        self.banks = [ps("bank%d" % i, [128, 512]) for i in range(5)]
        obank = [ps("obank%d" % i, [128, 512]) for i in range(2)]
        pbf = ps("pbf", [128, 1024], BF16)

        self.wbufs = [sb("wbuf%d" % i, [128, 4096], BF16) for i in range(3)]
        ident_f = sb("ident_f", [128, 128])
        ident_b = sb("ident_b", [128, 128], BF16)
        ones_b = sb("ones_b", [128, 128], BF16)
        triu_f = sb("triu_f", [128, 128])
        trisl_f = sb("trisl_f", [128, 128])
        ones_f = sb("ones_f", [128, 8])
        epsb = sb("epsb", [128, 1])
        mod = sb("mod", [128, 6, KC, 5])
        Am = sb("Am", [128, KC, 5]); Bm = sb("Bm", [128, KC, 5]); Gm = sb("Gm", [128, KC, 5])
        Af = sb("Af", [128, KC, 5]); Bf = sb("Bf", [128, KC, 5]); Gf = sb("Gf", [128, KC, 5])
        AmP = sb("AmP", [128, KC]); BmP = sb("BmP", [128, KC])
        AfP = sb("AfP", [128, KC]); BfP = sb("BfP", [128, KC])
        gT = sb("gT", [128, 32])
        badaT = sb("badaT", [128, 48])
        cT = sb("cT", [128, 40])
        siluT = sb("siluT", [128, 40], BF16)
        flag = sb("flag_s", [128, 1])
        wdwT = sb("wdwT", [128, 3, 44])
        bdwT = sb("bdwT", [128, 44])
        wgk_f = sb("wgk_f", [17, 256])
        wgk_b = sb("wgk_b", [17, 256], BF16)
        tab_prev = sb("tab_prev", [128, 8, 128], BF16)
        tab_own = sb("tab_own", [128, 8, 128], BF16)
        tab_mask = sb("tab_mask", [128, 128], BF16)
        cvec_s = sb("cvec_s", [128, 8])
        ggla_s = sb("ggla_s", [128, 512])
        stage = sb("stage", [128, 128])
        hist = sb("hist", [128, 44, 2])
        hist_s = sb("hist_s", [128, 8, 44, 2])
        h88 = sb("h88", [128, 2, 44])
        xin = [sb("xin%d" % i, [128, D]) for i in range(2)]
        xT = sb("xT", [128, KC, NBT])
        hT = sb("hT", [128, KC, NBT], BF16)
        sq = sb("sq", [128, 2, NBT], BF16)
        rbc = sb("rbc", [128, NBT])
        tmpf = sb("tmpf", [128, NBT])
        tmpf2 = sb("tmpf2", [128, NBT])
        qaT = sb("qaT", [128, 4, NBT], BF16)
        kaT = [sb("kaT%d" % i, [128, 4, 128], BF16) for i in range(8)]
        vaug = [sb("vaug%d" % i, [128, 8, 65], BF16) for i in range(8)]
        qbT = sb("qbT", [64, 4, NBT], BF16)
        kbT = sb("kbT", [64, 4, NBT], BF16)
        gkT = sb("gkT", [32, NBT], BF16)
        pT_raw = sb("pT_raw", [128, 2560])
        pT = pT_raw.bitcast(BF16)[:, :].rearrange("p (k h q) -> p k h q", k=5, h=8)
        ya_tok = sb("ya_tok", [128, 512], BF16)
        rden = sb("rden", [128, 8])
        kb_tok = sb("kb_tok", [128, 256])
        vb_tok2 = [sb("vb_tok%d" % i, [128, 4, 128], BF16) for i in range(2)]
        gb2_2 = [sb("gb2_%d" % i, [128, 512]) for i in range(2)]
        gtanh = sb("gtanh", [128, 512])
        Lsp = sb("Lsp", [128, 256])
        e_sb = sb("e_sb", [128, 256])
        e1 = sb("e1", [64, 4, 128]); e2 = sb("e2", [64, 4, 128])
        qtT2 = [sb("qtT%d" % i, [64, 4, 128], BF16) for i in range(2)]; ktT = sb("ktT", [64, 4, 128], BF16)
        kend2 = [sb("kend%d" % i, [128, 256], BF16) for i in range(2)]
        dec2 = [sb("dec%d" % i, [64, 4]) for i in range(2)]
        attT2 = [sb("attT%d" % i, [128, 4, 128], BF16) for i in range(2)]
        Sst = sb("Sst", [64, 4, 128])
        Sbf = sb("Sbf", [64, 4, 128], BF16)
        ssq = sb("ssq", [128, 4]); rgl = sb("rgl", [128, 4])
        yb_tok = sb("yb_tok", [128, 512], BF16)
        sgA = sb("sgA", [128, NBT]); sgB = sb("sgB", [128, NBT])
        m2 = sb("m2", [128, KC, NBT])
        ua = [sb("ua%d" % i, [128, NBT + 8]) for i in range(2)]
        y0 = [sb("y0_%d" % i, [128, NBT]) for i in range(2)]
        ua_sets = [ua, [pT_raw[:, 0:NBT + 8], pT_raw[:, NBT + 8:2 * NBT + 16]]]
        y0_sets = [y0, [pT_raw[:, 2 * NBT + 16:3 * NBT + 16], pT_raw[:, 3 * NBT + 16:4 * NBT + 16]]]
        actT = sb("actT", [128, NFF, NBT], BF16)
        mergedT = lambda c: actT[:, c, :]
        MK = lambda c: K("actT", c)
        yaT = lambda c: actT[:, 8 + c, :]
        YAK = tuple(K("actT", 8 + c) for c in range(4))
        ybT = lambda c: actT[:, 12 + c, :]
        YBK = tuple(K("actT", 12 + c) for c in range(4))
        m2b = m2.bitcast(BF16)
        kcT = lambda c: m2b[:, c, 0:512]
        vcaug = sb("vcaug", [128, 4, 8, 65], BF16)
        vown = sb("vown", [32, 4, 8, 65], BF16)
        Ss = sb("Ss", [64, 4, 128]); Ssb = sb("Ssb", [64, 4, 128], BF16)

        def const_mask(t, keyname, pattern, cm, base, cmp_op):
            self.memset("pool", t[:], 1.0, (K(keyname),))
            S.add("pool", lambda e: e.affine_select(t[:], t[:], pattern=pattern, compare_op=cmp_op, fill=0.0,
                                                    base=base, channel_multiplier=cm), (K(keyname),), (K(keyname),))
        self.memset("pool", ident_f[:], 0.0, (K("ident_f"),))
        S.add("pool", lambda e: e.affine_select(ident_f[:], ident_f[:], pattern=[[-1, 128]], compare_op=ALU.not_equal,
                                                fill=1.0, base=0, channel_multiplier=1), (K("ident_f"),), (K("ident_f"),))
        const_mask(triu_f, "triu_f", [[1, 128]], -1, 0, ALU.is_ge)
        const_mask(trisl_f, "trisl_f", [[-1, 128]], 1, -1, ALU.is_ge)
        self.cp("dve", ident_b[:], ident_f[:], (K("ident_f"),), (K("ident_b"),))
        self.memset("dve", ones_b[:], 1.0, (K("ones_b"),))
        self.memset("dve", ones_f[:], 1.0, (K("ones_f"),))
        self.memset("dve", epsb[:], EPS, (K("epsb"),))
        self.memset("dve", gkT[:], 1.0, (K("gkT_ones"),))
        self.memset("dve", hist[:], 0.0, tuple(K("hist", jj) for jj in range(44)))
        self.memset("dve", hist_s[:, 0:4, :, :], 0.0, (K("hist_s_ov"),))
        self.memset("dve", Sst[:], 0.0, tuple(K("Sst", h) for h in range(4)))
        self.memset("dve", Sbf[:], 0.0, (K("Sst", "b"),))

        def load_T(src2d, rows, dst, kname, view=None):
            self.dma("pool_dma", stage[0:rows, :], src2d, (), (K("stage"),), ("stage",))
            bk, bkey = self.bank()
            self.tr(bk[:, 0:rows], stage[0:rows, :], ident_f[0:rows, 0:rows], (K("stage"), K("ident_f")), (bkey,))
            src = bk[:, 0:rows] if view is None else view(bk[:, 0:rows])
            self.cp("dve", dst, src, (bkey,), (K(kname),))

        load_T(gvec, 32, gT[:], "gT")
        load_T(b_ada, 48, badaT[:], "badaT")
        load_T(crow, 40, cT[:], "cT")
        for j in range(3):
            load_T(w_dw[j * 44:(j + 1) * 44, :], 44, wdwT[:, j, :], "wdwT%d" % j)
        load_T(b_dw, 44, bdwT[:], "bdwT")
        self.dma("pool_dma", flag[:], flag_d, (), (K("flag"),), ("misc", 0))
        self.dma("pool_dma", wgk_f[0:16, :], w_gk2, (), (K("wgk_f0"),), ("misc", 1))
        self.dma("pool_dma", wgk_f[16:17, :], b_gk, (), (K("wgk_f1"),), ("misc", 2))
        self.cp("dve", wgk_b[:], wgk_f[:], (K("wgk_f0"), K("wgk_f1")), (K("wgk_b"),))
        self.dma("pool_dma", cvec_s[:], cvec, (), (K("cvec"),), ("misc", 3))
        self.dma("pool_dma", ggla_s[:], ggla, (), (K("ggla"),), ("misc", 4))
        self.ts("dve", ggla_s[:], ggla_s[:], 0.5, None, ALU.mult, None, (K("ggla"),), (K("ggla"),))
        tabf = xin[0][:, :].rearrange("p (h q) -> p h q", h=8)
        for ti, tdst in ((0, tab_prev), (1, tab_own)):
            self.dma("pool_dma", tabf, btab[ti], (), (K("xin", 0), K("xinv", 0)), ("misc", 5))
            for h in range(8):
                self.ts("dve", tdst[:, h, :], tabf[:, h, :], cvec_s[:, h:h + 1], None, ALU.subtract, None,
                        (K("xin", 0), K("xinv", 0), K("cvec")), (K("tab", ti, h),))
        self.memset("dve", tab_own[64:128, :, 0:64], NEG, tuple(K("tab", 1, h) for h in range(8)))
        for ti, tdst in ((0, tab_prev), (1, tab_own)):
            self.act(tdst[:], tdst[:], AF.Exp, tuple(K("tab", ti, h) for h in range(8)), (K("etab", ti),))
        self.memset("dve", tab_mask[:], 1.0, (K("tab_mask"),))
        self.memset("dve", tab_mask[0:64, 64:128], 0.0, (K("tab_mask"),))

        def cast_group(src_cols, dst_g, name, g):
            self.dma("pool_dma", dst_g, src_cols.rearrange("(kc p) n -> p kc n", p=128), (), (K("scr", name, g),), ("cast", name, g))
            return (K("scr", name, g),)

        self.act(tmpf[:, 0:40], cT[:], AF.Tanh, (K("cT"),), (K("tmpf"),), scale=0.5)
        self.ts("dve", tmpf[:, 0:40], tmpf[:, 0:40], 0.5, 0.5, ALU.mult, ALU.add, (K("tmpf"),), (K("tmpf"),))
        self.tt("dve", siluT[:], tmpf[:, 0:40], cT[:], ALU.mult, (K("tmpf"), K("cT")), (K("siluT"),))
        siluv = siluT[:].rearrange("p (s k) -> p k s", k=8)
        modkeys = {}
        k8 = lambda n: tuple(K(n, kc) for kc in range(8))

        def ada_groups(glist):
            for g in glist:
                sl = g % 2
                buf = actT[:, sl * 8:(sl + 1) * 8, :].rearrange("p c n -> p (c n)")
                bkeys = tuple(K("actT", sl * 8 + c) for c in range(8))
                self.dma("pool_dma", buf.rearrange("p (kc n) -> p kc n", kc=8),
                         w_ada.rearrange("(kc p) n -> p kc n", p=128)[:, :, g * 512:(g + 1) * 512], (), bkeys, ("ada", sl))
                bk, bkey = self.bank()
                for oc in range(4):
                    for kc in range(8):
                        self.mm(bk[:, oc * 8:oc * 8 + 5], buf[:, kc * 512 + oc * 128: kc * 512 + (oc + 1) * 128], siluv[:, kc, :],
                                kc == 0, kc == 7, bkeys + (K("siluT"),), (bkey,))
                for oc in range(4):
                    ch = g * 4 + oc
                    self.ts("dve", mod[:, ch // 8, ch % 8, :], bk[:, oc * 8:oc * 8 + 5], badaT[:, ch:ch + 1], None, ALU.add, None,
                            (bkey, K("badaT")), (K("mod", ch),))

        ada_groups(range(0, 4))
        mk1 = tuple(K("mod", ch) for ch in range(16))
        for kc in range(8):
            self.ts("dve", Am[:, kc, :], mod[:, 1, kc, :], 1.0, gT[:, kc:kc + 1], ALU.add, ALU.mult, mk1 + (K("gT"),), (K("Am", kc),))
        self.cp("dve", Bm[:], mod[:, 0, :, :], mk1, (K("Bm"),))
        self.ts("dve", AmP[:], Am[:, :, 0], flag[:, 0:1], None, ALU.mult, None, k8("Am") + (K("flag"),), (K("AmP"),))
        self.ts("dve", BmP[:], Bm[:, :, 0], flag[:, 0:1], None, ALU.mult, None, (K("Bm"), K("flag")), (K("BmP"),))
        modkeys.update({"Am": k8("Am"), "Bm": (K("Bm"),), "AmP": (K("AmP"),), "BmP": (K("BmP"),)})

        def ada_part2():
            ada_groups(range(4, 12))
            mk2 = tuple(K("mod", ch) for ch in range(16, 48))
            for kc in range(8):
                rw = mk2 + (K("gT"),)
                self.ts("dve", Af[:, kc, :], mod[:, 4, kc, :], 1.0, gT[:, 16 + kc:17 + kc], ALU.add, ALU.mult, rw, (K("Af", kc),))
                self.ts("dve", Gm[:, kc, :], mod[:, 2, kc, :], gT[:, 8 + kc:9 + kc], None, ALU.mult, None, rw, (K("Gm", kc),))
                self.ts("dve", Gf[:, kc, :], mod[:, 5, kc, :], gT[:, 24 + kc:25 + kc], None, ALU.mult, None, rw, (K("Gf", kc),))
            self.cp("dve", Bf[:], mod[:, 3, :, :], mk2, (K("Bf"),))
            self.ts("dve", AfP[:], Af[:, :, 0], flag[:, 0:1], None, ALU.mult, None, k8("Af") + (K("flag"),), (K("AfP"),))
            self.ts("dve", BfP[:], Bf[:, :, 0], flag[:, 0:1], None, ALU.mult, None, (K("Bf"), K("flag")), (K("BfP"),))
        modkeys.update({"Af": k8("Af"), "Gm": k8("Gm"), "Gf": k8("Gf"), "Bf": (K("Bf"),), "AfP": (K("AfP"),), "BfP": (K("BfP"),)})

        kwin = {}
        for g in (3, 4, 1, 2):
            kwin[g] = cast_group(w_in[:, g * 512:(g + 1) * 512], win_g[g], "win", g)
        self.dma("pool_dma", wgk_g, w_in.rearrange("(kc p) n -> p kc n", p=128)[:, :, 5120:5136], (), (K("scr", "wgk"),), ("cast", "wgk"))
        kwgk = (K("scr", "wgk"),)
        for g in (0, 5, 6, 7, 8, 9):
            kwin[g] = cast_group(w_in[:, g * 512:(g + 1) * 512], win_g[g], "win", g)
        kwbra = [cast_group(w_br_a[:, g * 512:(g + 1) * 512], wbra_g[g], "wbra", g) for g in range(2)]
        kwbrb = [cast_group(w_br_b[:, g * 512:(g + 1) * 512], wbrb_g[g], "wbrb", g) for g in range(2)]
        kwout = [cast_group(w_out[:, g * 512:(g + 1) * 512], wout_g[g], "wout", g) for g in range(2)]
        kwup = [cast_group(w_up[:, g * 512:(g + 1) * 512], wup_g[g], "wup", g) for g in range(11)]
        kwdown = [cast_group(w_down[:, c * 128:(c + 1) * 128], wdown_g[c], "wdown", c) for c in range(8)]

        def load_xT(src_rows, ntile):
            for t in range(ntile):
                xb = xin[t % 2]
                kx = K("xin", t % 2)
                self.dma("sp", xb[:], src_rows[t * 128:(t + 1) * 128, :], (), (kx, K("xinv", t % 2)), ("xin", t % 2))
                for half in range(2):
                    bk, bkey = self.bank()
                    for j in range(4):
                        kc = half * 4 + j
                        self.tr(bk[:, j * 128:(j + 1) * 128], xb[:, kc * 128:(kc + 1) * 128], ident_f[:],
                                (kx if half == 0 else K("xinv", t % 2), K("ident_f")), (bkey,))
                    self.cp("act", xT[:, half * 4:half * 4 + 4, t * 128:(t + 1) * 128],
                            bk[:].rearrange("p (j n) -> p j n", j=4), (bkey,), tuple(K("xT", half * 4 + j, t) for j in range(4)))

        def rms_bcast(srcT, n, src_keys_fn):
            bk, bkey = self.bank()
            for kc in range(8):
                if kc % 2 == 0:
                    self.act(sq[:, 0, 0:n], srcT[:, kc, 0:n], AF.Square, src_keys_fn(kc), (K("sq", 0),))
                else:
                    self.tt("dve", sq[:, 1, 0:n], srcT[:, kc, 0:n], srcT[:, kc, 0:n], ALU.mult, src_keys_fn(kc), (K("sq", 1),))
                self.mm(bk[:, 0:n], ones_b[:], sq[:, kc % 2, 0:n], kc == 0, kc == 7, (K("ones_b"), K("sq", kc % 2)), (bkey,))
            self.act(tmpf[:, 0:n], bk[:, 0:n], AF.Ln, (bkey, K("epsb")), (K("tmpf"),), bias=epsb[:, 0:1], scale=1.0 / D)
            self.act(rbc[:, 0:n], tmpf[:, 0:n], AF.Exp, (K("tmpf"),), (K("rbc"),), scale=-0.5)

        def modulate(n, ntile, Asc, Bsc, Afn, Bfn):
            for kc in range(8):
                xk = tuple(K("xT", kc, t) for t in range(ntile))
                self.tt("dve", tmpf2[:, 0:n], xT[:, kc, 0:n], rbc[:, 0:n], ALU.mult, xk + (K("rbc"),), (K("tmpf2"),))
                self.act(hT[:, kc, 0:n], tmpf2[:, 0:n], AF.Identity, (K("tmpf2"),) + modkeys[Asc] + modkeys[Bsc], (K("hT", kc),),
                         bias=Bfn(kc), scale=Afn(kc))

        def modulate_seq(At, Bt, Asc, Bsc):
            for kc in range(8):
                xk = (K("xT", kc, 0),)
                v3 = lambda ap: ap.rearrange("p (s w) -> p s w", s=4)
                self.tt("dve", tmpf2[:, 0:128], xT[:, kc, 0:128], rbc[:, 0:128], ALU.mult, xk + (K("rbc"),), (K("tmpf2"),))
                self.tt("dve", v3(tmpf2[:, 0:128]), v3(tmpf2[:, 0:128]), At[:, kc, 1:5].to_broadcast([128, 4, 32]), ALU.mult,
                        (K("tmpf2"),) + modkeys[Asc], (K("tmpf2"),))
                self.tt("dve", v3(hT[:, kc, 0:128]), v3(tmpf2[:, 0:128]), Bt[:, kc, 1:5].to_broadcast([128, 4, 32]), ALU.add,
                        (K("tmpf2"),) + modkeys[Bsc], (K("hT", kc),))

        xkeys = lambda ntile: (lambda kc: tuple(K("xT", kc, t) for t in range(ntile)))

        def heads_fm(wg, wk, col0, nheads, n, evac):
            for h in range(nheads):
                bk, bkey = self.bank()
                for kc in range(8):
                    self.mm(bk[0:64, 0:n], wg(kc, col0 + h * 64, col0 + (h + 1) * 64), hT[:, kc, 0:n], kc == 0, kc == 7, (wk, K("hT", kc)), (bkey,))
                evac(h, bk, bkey)

        def pairs_fm(wg, wk, n, evac):
            for c in range(4):
                bk, bkey = self.bank()
                for kc in range(8):
                    self.mm(bk[:, 0:n], wg(kc, c * 128, (c + 1) * 128), hT[:, kc, 0:n], kc == 0, kc == 7, (wk, K("hT", kc)), (bkey,))
                evac(c, bk, bkey)

        def units_tm(wg, wk, col0, ncols, units, evac):
            for ui, (c0, C) in enumerate(units):
                bk, bkey = self.bank()
                for kc in range(8):
                    self.mm(bk[0:C, 0:ncols], hT[:, kc, c0:c0 + C], wg(kc, col0, col0 + ncols), kc == 0, kc == 7, (wk, K("hT", kc)), (bkey,))
                evac(ui, bk, bkey)

        def gla_stageA(ui, col0, C, pre, W):
            p = ui % 2
            wgk_, wkk, wgv_, wkv, wgg_, wkg = W
            vb_tok, gb2, qtT, kend, dec, attT = vb_tok2[p], gb2_2[p], qtT2[p], kend2[p], dec2[p], attT2[p]
            bk, bkey = self.bank()
            for kc in range(8):
                self.mm(bk[0:C, 0:256], hT[:, kc, col0:col0 + C], wgk_(kc, 256, 512), kc == 0, kc == 7, (wkk, K("hT", kc)), (bkey,))
            self.cp("act", kb_tok[0:C, :], bk[0:C, 0:256], (bkey,), (K("kb_tok"),))
            bk, bkey = self.bank()
            for kc in range(8):
                self.mm(bk[0:C, :], hT[:, kc, col0:col0 + C], wgv_(kc), kc == 0, kc == 7, (wkv, K("hT", kc)), (bkey,))
            self.cp("act", vb_tok[0:C, :, :].rearrange("p h e -> p (h e)"), bk[0:C, :], (bkey,), (K("vb_tok", p),))
            if not pre:
                bk, bkey = self.bank()
                for kc in range(8):
                    self.mm(bk[0:C, :], hT[:, kc, col0:col0 + C], wgg_(kc), kc == 0, kc == 7, (wkg, K("hT", kc)), (bkey,))
                self.cp("act", gb2[0:C, :], bk[0:C, :], (bkey,), (K("gb2", p),))
            bk, bkey = self.bank()
            self.mm(bk[0:C, 0:256], gkT[0:17, col0:col0 + C], wgk_b[0:17, :], True, True, (K("gkT"), K("gkT_ones"), K("wgk_b")), (bkey,))
            self.act(e_sb[0:C, :], bk[0:C, 0:256], AF.Exp, (bkey,), (K("e_sb"),), scale=-1.0)
            self.act(Lsp[0:C, :], e_sb[0:C, :], AF.Ln, (K("e_sb"),), (K("Lsp"),), bias=1.0)
            bk2, bkey2 = self.bank()
            self.mm(bk2[0:C, 0:256], trisl_f[0:C, 0:C], Lsp[0:C, :], True, True, (K("trisl_f"), K("Lsp")), (bkey2,))
            self.act(e_sb[0:C, :], bk2[0:C, 0:256], AF.Exp, (bkey2,), (K("e_sb"),), scale=-1.0 / 16)
            self.tt("dve", kend[0:C, :], kb_tok[0:C, :], e_sb[0:C, :], ALU.mult, (K("kb_tok"), K("e_sb")), (K("kend", p),))
            bk3, bkey3 = self.bank()
            for h in range(4):
                self.mm(bk3[0:64, h * 128:h * 128 + C], Lsp[0:C, h * 64:(h + 1) * 64], triu_f[0:C, 0:C], True, True,
                        (K("Lsp"), K("triu_f")), (bkey3,))
            b3 = bk3[0:64, :].rearrange("p (c t) -> p c t", c=4)
            self.act(dec[:, :], b3[:, :, C - 1], AF.Exp, (bkey3,), (K("dec", p),), scale=-1.0 / 16)
            if pre:
                return
            self.act(e1[:, :, 0:C], b3[:, :, 0:C], AF.Exp, (bkey3,), (K("e1"),), scale=-1.0 / 16)
            self.act(e2[:, :, 0:C], b3[:, :, 0:C], AF.Exp, (bkey3,), (K("e2"),), scale=1.0 / 16)
            self.stt("dve", qtT[:, :, 0:C], qbT[:, :, col0:col0 + C], 0.125, e1[:, :, 0:C], ALU.mult, ALU.mult,
                     tuple(K("qbT", j) for j in range(4)) + (K("e1"),), (K("qtT", p),))
            self.tt("dve", ktT[:, :, 0:C], kbT[:, :, col0:col0 + C], e2[:, :, 0:C], ALU.mult, tuple(K("kbT", j) for j in range(4)) + (K("e2"),), (K("ktT"),))
            bk4, bkey4 = self.bank()
            for h in range(4):
                self.mm(bk4[0:C, h * 128:h * 128 + C], ktT[:, h, 0:C], qtT[:, h, 0:C], True, True, (K("ktT"), K("qtT", p)), (bkey4,))
            for h in range(4):
                self.tt("dve", attT[0:C, h, 0:C], bk4[0:C, h * 128:h * 128 + C], triu_f[0:C, 0:C], ALU.mult,
                        (bkey4, K("triu_f")), (K("attT", p, h),))

        def gla_stageB(ui, col0, C, pre, S_t, S_b, Skey):
            p = ui % 2
            vb_tok, gb2, qtT, kend, dec, attT = vb_tok2[p], gb2_2[p], qtT2[p], kend2[p], dec2[p], attT2[p]
            if not pre:
                obk, obkey = self.bank()
                for h in range(4):
                    self.mm(obk[0:C, h * 128:(h + 1) * 128], attT[0:C, h, 0:C], vb_tok[0:C, h, :], True, False, (K("attT", p, h), K("vb_tok", p)), (obkey,))
                    self.mm(obk[0:C, h * 128:(h + 1) * 128], qtT[:, h, 0:C], S_b[:, h, :], False, True,
                            (K("qtT", p), Skey + ("b",)), (obkey,))
                self.memset("dve", ssq[0:C, :], 0.0, tuple(K("ssq", h) for h in range(4)))
                for h in range(4):
                    self.act(attT[0:C, h, :], obk[0:C, h * 128:(h + 1) * 128], AF.Square, (obkey,), (K("attT", p, h), K("ssq", h)), accum_out=ssq[0:C, h:h + 1])
                self.act(rgl[0:C, :], ssq[0:C, :], AF.Ln, tuple(K("ssq", h) for h in range(4)) + (K("epsb"),), (K("rgl"),), bias=epsb[0:C, 0:1], scale=1.0 / 128)
                self.act(rgl[0:C, :], rgl[0:C, :], AF.Exp, (K("rgl"),), (K("rgl"),), scale=-0.5)
                self.act(gtanh[0:C, :], gb2[0:C, :], AF.Tanh, (K("gb2", p),), (K("gtanh"),), scale=0.5)
                self.stt("dve", gtanh[0:C, :], gtanh[0:C, :], 1.0, gb2[0:C, :], ALU.add, ALU.mult, (K("gtanh"), K("gb2", p)), (K("gtanh"),))
                self.tt("dve", gtanh[0:C, :], gtanh[0:C, :], ggla_s[0:C, :], ALU.mult, (K("gtanh"), K("ggla")), (K("gtanh"),))
                for h in range(4):
                    self.stt("dve", yb_tok[0:C, h * 128:(h + 1) * 128], obk[0:C, h * 128:(h + 1) * 128], rgl[0:C, h:h + 1],
                             gtanh[0:C, h * 128:(h + 1) * 128], ALU.mult, ALU.mult, (obkey, K("rgl"), K("gtanh")), (K("yb_tok", h),))
                for c in range(4):
                    self.tr(pbf[:, 512 + c * 128:512 + c * 128 + C], yb_tok[0:C, c * 128:(c + 1) * 128], ident_b[0:C, 0:C],
                            (K("yb_tok", c), K("ident_b")), (K("pbf"),))
                for c in range(4):
                    self.cp("act", ybT(c)[:, col0:col0 + C], pbf[:, 512 + c * 128:512 + c * 128 + C], (K("pbf"),), (YBK[c],))
            bk6, bkey6 = self.bank()
            for h in range(4):
                self.mm(bk6[0:64, h * 128:(h + 1) * 128], kend[0:C, h * 64:(h + 1) * 64], vb_tok[0:C, h, :], True, True, (K("kend", p), K("vb_tok", p)), (bkey6,))
            for h in range(4):
                self.stt("dve", S_t[:, h, :], S_t[:, h, :], dec[:, h:h + 1], bk6[0:64, h * 128:(h + 1) * 128],
                         ALU.mult, ALU.add, (Skey + (h,), K("dec", p), bkey6), (Skey + (h,),))
            self.cp("act", S_b[:], S_t[:], tuple(Skey + (h,) for h in range(4)), (Skey + ("b",),))

        def gla_units(units, pre, S_of, with_fn=None):
            wgk_, wkk = self.wload(win_g[G_QKB], kwin[G_QKB], 8, 512)
            wgv_, wkv = self.wload(win_g[G_VB], kwin[G_VB], 8, 512)
            wgg_, wkg = (None, None) if pre else self.wload(win_g[G_GB], kwin[G_GB], 8, 512)
            W = (wgk_, wkk, wgv_, wkv, wgg_, wkg)

            def body():
                gla_stageA(0, units[0][0], units[0][1], pre, W)
                for ui, (col0, C) in enumerate(units):
                    if ui + 1 < len(units):
                        gla_stageA(ui + 1, units[ui + 1][0], units[ui + 1][1], pre, W)
                    S_t, S_b, Skey, before_fn, after_fn = S_of(ui)
                    if before_fn is not None:
                        before_fn()
                    gla_stageB(ui, col0, C, pre, S_t, S_b, Skey)
                    if after_fn is not None:
                        after_fn()
            if with_fn is None:
                body()
            else:
                self.interleave([(body, [0, 1, 2]), (with_fn, [3, 4])])

        def attn_tile(tglob, tl):
            pTv = lambda kb, par: pT[:, kb, :, :].rearrange("p (c two) q -> p two c q", two=2)[:, par]
            for kb in range(5):
                slot = (tglob - 4 + kb) % 8
                banks = [self.bank(), self.bank()]
                for c in range(4):
                    for par in range(2):
                        pb = par * 64
                        bk, bkey = banks[par]
                        self.mm(bk[:, c * 128:(c + 1) * 128], kaT[slot][pb:pb + 64, c, :], qaT[pb:pb + 64, c, tl * 128:(tl + 1) * 128],
                                True, True, (K("kaT", slot, c), K("qaT", c)), (bkey,))
                for par in range(2):
                    bk, bkey = banks[par]
                    self.act(pTv(kb, par), bk[:].rearrange("p (c q) -> p c q", c=4), AF.Exp, (bkey,), (K("pT", kb, par),))
                tab = {0: (tab_mask[:, :].to_broadcast([128, 8, 128]) if False else None), 3: tab_prev, 4: tab_own}.get(kb)
                pk = (K("pT", kb, 0), K("pT", kb, 1))
                if kb == 0:
                    for h0 in range(0, 8, 4):
                        pass
                    self.tt("dve", pT[:, 0, :, :], pT[:, 0, :, :], bass.AP(tab_mask, 0, [[128, 128], [0, 8], [1, 128]]), ALU.mult,
                            pk + (K("tab_mask"),), pk)
                elif tab is not None:
                    self.tt("dve", pT[:, kb, :, :], pT[:, kb, :, :], tab[:, :, :], ALU.mult, pk + (K("etab", kb - 3),), pk)
            for hg in range(2):
                ob, obkey = obank[hg], K("obank", hg)
                for hh in range(4):
                    h = hg * 4 + hh
                    for kb in range(5):
                        slot = (tglob - 4 + kb) % 8
                        self.mm(ob[:, hh * 65:(hh + 1) * 65], pT[:, kb, h, :], vaug[slot][:, h, :], kb == 0, kb == 4,
                                (K("pT", kb, h % 2), K("vaug", slot), K("vaug1", slot)), (obkey,))
            for hg in range(2):
                ob, obkey = obank[hg], K("obank", hg)
                ov = ob[:, 0:260].rearrange("p (h e) -> p h e", h=4)
                self.ts("dve", rden[:, hg * 4:(hg + 1) * 4], ov[:, :, 64], 1e-30, None, ALU.add, None, (obkey,), (K("rden", hg),))
                S.add("dve", lambda e, hg=hg: e.reciprocal(rden[:, hg * 4:(hg + 1) * 4], rden[:, hg * 4:(hg + 1) * 4]), (K("rden", hg),), (K("rden", hg),))
                for hh in range(4):
                    h = hg * 4 + hh
                    self.ts("dve", ya_tok[:, h * 64:(h + 1) * 64], ov[:, hh, 0:64], rden[:, h:h + 1], None, ALU.mult, None,
                            (obkey, K("rden", hg)), (K("ya_tok", h),))
            yk = tuple(K("ya_tok", h) for h in range(8))
            for c in range(4):
                self.tr(pbf[:, c * 128:(c + 1) * 128], ya_tok[:, c * 128:(c + 1) * 128], ident_b[:], yk + (K("ident_b"),), (K("pbf"),))
            for c in range(4):
                self.cp("act", yaT(c)[:, tl * 128:(tl + 1) * 128], pbf[:, c * 128:(c + 1) * 128], (K("pbf"),), (YAK[c],))

        def attn_sample(s, slot, qbase=0):
            for rb in range(4):
                xb = xin[rb % 2]
                kx = K("xin", rb % 2)
                self.dma("sp", xb[:, 0:512], ck[s, rb * 128:(rb + 1) * 128, :], (), (kx,), ("xin", rb % 2))
                bk, bkey = self.bank()
                for c in range(4):
                    self.tr(bk[:, c * 128:(c + 1) * 128], xb[:, c * 128:(c + 1) * 128], ident_f[:], (kx, K("ident_f")), (bkey,))
                for c in range(4):
                    self.cp("act", kcT(c)[:, rb * 128:(rb + 1) * 128], bk[:, c * 128:(c + 1) * 128], (bkey,), (K("m2", c),))
                self.dma("sp", xb[:, 512:1024], cv[s, rb * 128:(rb + 1) * 128, :], (), (K("xinv", rb % 2),), ("xinv", rb % 2))
                self.cp("dve", vcaug[:, rb, :, 0:64], xb[:, 512:1024].rearrange("p (h e) -> p h e", h=8), (K("xinv", rb % 2),), (K("vcaug", rb),))
                self.cp("dve", vcaug[:, rb, :, 64], ones_f[:, 0:8], (K("ones_f"),), (K("vcaug1", rb),))
            q0 = qbase + s * 32
            l0 = s * 32
            for kb in range(5):
                kn = 128 if kb < 4 else 32
                banks = [self.bank(), self.bank()]
                for c in range(4):
                    for par in range(2):
                        pb = par * 64
                        bk, bkey = banks[par]
                        if kb < 4:
                            lhs = kcT(c)[pb:pb + 64, kb * 128:(kb + 1) * 128]
                            rk = (K("m2", c),)
                        else:
                            lhs = kaT[slot][pb:pb + 64, c, l0:l0 + 32]
                            rk = (K("kaT", slot, c),)
                        self.mm(bk[0:kn, c * 32:(c + 1) * 32], lhs, qaT[pb:pb + 64, c, q0:q0 + 32], True, True, rk + (K("qaT", c),), (bkey,))
                for par in range(2):
                    bk, bkey = banks[par]
                    dstv = pT[0:kn, kb, :, 0:32].rearrange("p (c two) q -> p two c q", two=2)[:, par]
                    self.act(dstv, bk[0:kn, 0:128].rearrange("p (c q) -> p c q", c=4), AF.Exp, (bkey,), (K("pT", kb, par),))
                pk = (K("pT", kb, 0), K("pT", kb, 1))
                if kb == 3:
                    self.tt("dve", pT[:, 3, :, 0:32], pT[:, 3, :, 0:32], tab_prev[:, :, 0:32], ALU.mult, pk + (K("etab", 0),), pk)
                elif kb == 4:
                    self.tt("dve", pT[0:32, 4, :, 0:32], pT[0:32, 4, :, 0:32], tab_own[0:32, :, 0:32], ALU.mult, pk + (K("etab", 1),), pk)
            for h in range(8):
                hg, hh = h // 4, h % 4
                for kb in range(5):
                    kn = 128 if kb < 4 else 32
                    rhs = vcaug[:, kb, h, :] if kb < 4 else vown[0:32, s, h, :]
                    rk = (K("vcaug", kb), K("vcaug1", kb)) if kb < 4 else (K("vown", s), K("vown1", s))
                    self.mm(obank[hg][0:32, hh * 65:(hh + 1) * 65], pT[0:kn, kb, h, 0:32], rhs, kb == 0, kb == 4,
                            (K("pT", kb, h % 2),) + rk, (K("obank", hg),))
            for hg in range(2):
                ob, obkey = obank[hg], K("obank", hg)
                ov = ob[0:32, 0:260].rearrange("p (h e) -> p h e", h=4)
                S.add("dve", lambda e, ov=ov, hg=hg: e.reciprocal(rden[0:32, hg * 4:(hg + 1) * 4], ov[:, :, 64]), (obkey,), (K("rden", hg),))
                for hh in range(4):
                    h = hg * 4 + hh
                    self.ts("dve", ya_tok[0:32, h * 64:(h + 1) * 64], ov[:, hh, 0:64], rden[0:32, h:h + 1], None, ALU.mult, None,
                            (obkey, K("rden", hg)), (K("ya_tok", h),))
            yk = tuple(K("ya_tok", h) for h in range(8))
            for c in range(4):
                self.tr(pbf[:, c * 128:c * 128 + 32], ya_tok[0:32, c * 128:(c + 1) * 128], ident_b[0:32, 0:32], yk + (K("ident_b"),), (K("pbf"),))
            for c in range(4):
                self.cp("act", yaT(c)[:, q0:q0 + 32], pbf[:, c * 128:c * 128 + 32], (K("pbf"),), (YAK[c],))

        def resid(n, ntile, Gfn, gname):
            for kc in range(8):
                tb, tk = (tmpf2, K("tmpf2")) if kc % 2 == 0 else (tmpf, K("tmpf"))
                self.tt("pool", tb[:, 0:n], m2[:, kc, 0:n], rbc[:, 0:n], ALU.mult, (K("m2", kc), K("rbc")), (tk,))
                xk = tuple(K("xT", kc, t) for t in range(ntile))
                self.stt("dve", xT[:, kc, 0:n], tb[:, 0:n], Gfn(kc), xT[:, kc, 0:n], ALU.mult, ALU.add,
                         (tk,) + modkeys[gname] + xk, xk)

        def resid_seq(Gt, gname):
            v3 = lambda ap: ap.rearrange("p (s w) -> p s w", s=4)
            for kc in range(8):
                xk = (K("xT", kc, 0),)
                self.tt("dve", tmpf2[:, 0:128], m2[:, kc, 0:128], rbc[:, 0:128], ALU.mult, (K("m2", kc), K("rbc")), (K("tmpf2"),))
                self.tt("dve", v3(tmpf2[:, 0:128]), v3(tmpf2[:, 0:128]), Gt[:, kc, 1:5].to_broadcast([128, 4, 32]), ALU.mult,
                        (K("tmpf2"),) + modkeys[gname], (K("tmpf2"),))
                self.tt("dve", xT[:, kc, 0:128], xT[:, kc, 0:128], tmpf2[:, 0:128], ALU.add, (K("tmpf2"),) + xk, xk)

        def merge_and_out(n):
            def ev_gate(dst, dkey):
                def f(bk, bkey):
                    self.act(dst[:, 0:n], bk[:, 0:n], AF.Tanh, (bkey,), (dkey,), scale=0.5)
                    self.ts("dve", dst[:, 0:n], dst[:, 0:n], 0.5, 0.5, ALU.mult, ALU.add, (dkey,), (dkey,))
                return f
            for G in range(2):
                wga, wka = self.wload(win_g[G_GA + G], kwin[G_GA + G], 8, 512)
                wba, wkba = self.wload(wbra_g[G], kwbra[G], 4, 512)
                for j in range(4):
                    c = G * 4 + j
                    bk, bkey = self.bank()
                    for kc in range(8):
                        self.mm(bk[:, 0:n], wga(kc, j * 128, (j + 1) * 128), hT[:, kc, 0:n], kc == 0, kc == 7, (wka, K("hT", kc)), (bkey,))
                    ev_gate(sgA, K("sgA"))(bk, bkey)
                    bk, bkey = self.bank()
                    for kc in range(4):
                        self.mm(bk[:, 0:n], wba(kc, j * 128, (j + 1) * 128), yaT(kc)[:, 0:n], kc == 0, kc == 3, (wkba, YAK[kc]), (bkey,))
                    self.tt("dve", m2[:, c, 0:n], sgA[:, 0:n], bk[:, 0:n], ALU.mult, (K("sgA"), bkey), (K("m2", c),))
                wgb, wkb = self.wload(win_g[G_GBR + G], kwin[G_GBR + G], 8, 512)
                wbb, wkbb = self.wload(wbrb_g[G], kwbrb[G], 4, 512)
                for j in range(4):
                    c = G * 4 + j
                    bk, bkey = self.bank()
                    for kc in range(8):
                        self.mm(bk[:, 0:n], wgb(kc, j * 128, (j + 1) * 128), hT[:, kc, 0:n], kc == 0, kc == 7, (wkb, K("hT", kc)), (bkey,))
                    ev_gate(sgB, K("sgB"))(bk, bkey)
                    bk, bkey = self.bank()
                    for kc in range(4):
                        self.mm(bk[:, 0:n], wbb(kc, j * 128, (j + 1) * 128), ybT(kc)[:, 0:n], kc == 0, kc == 3, (wkbb, YBK[kc]), (bkey,))
                    self.tt("dve", sgB[:, 0:n], sgB[:, 0:n], bk[:, 0:n], ALU.mult, (K("sgB"), bkey), (K("sgB"),))
                    self.tt("dve", mergedT(c)[:, 0:n], sgB[:, 0:n], m2[:, c, 0:n], ALU.add, (K("sgB"), K("m2", c)), (MK(c),))
            for G in range(2):
                wg, wk = self.wload(wout_g[G], kwout[G], 8, 512)
                for j in range(4):
                    c = G * 4 + j
                    bk, bkey = self.bank()
                    for kc in range(8):
                        self.mm(bk[:, 0:n], wg(kc, j * 128, (j + 1) * 128), mergedT(kc)[:, 0:n], kc == 0, kc == 7, (wk, MK(kc)), (bkey,))
                    self.cp("act", m2[:, c, 0:n], bk[:, 0:n], (bkey,), (K("m2", c),))
            rms_bcast(m2, n, lambda kc: (K("m2", kc),))

        ALIAS_KEYS = tuple(K("pT", kb, par) for kb in range(5) for par in range(2)) + \
            tuple(K(nm, 1, h) for nm in ("ua", "uah", "y0") for h in range(2))

        def alias_fence():
            self.memset("dve", pT_raw[:, 2559:2560], 0.0, ALIAS_KEYS)

        def ffn(n, groups):
            alias_fence()
            for g in range(11):
                wg, wk = self.wload(wup_g[g], kwup[g], 8, 512)
                for pair in range(2):
                    ua_, y0_ = ua_sets[pair], y0_sets[pair]
                    for half in range(2):
                        jc = half * 2 + pair
                        jj = half * NFF + 2 * g + pair
                        bk, bkey = self.bank()
                        for kc in range(8):
                            self.mm(bk[:, 0:n], wg(kc, jc * 128, (jc + 1) * 128), hT[:, kc, 0:n], kc == 0, kc == 7, (wk, K("hT", kc)), (bkey,))
                        uk, uh, yk = K("ua", pair, half), K("uah", pair, half), K("y0", pair, half)
                        self.act(y0_[half][:, 0:n], bk[:, 0:n], AF.Identity, (bkey, K("wdwT2"), K("bdwT")), (yk,),
                                 bias=bdwT[:, jj:jj + 1], scale=wdwT[:, 2, jj:jj + 1])
                        off = 0
                        for (c0, ncols, nseg, hview, hkeyf, extra, scale_ap) in groups:
                            w = ncols // nseg
                            v3 = lambda ap, nseg=nseg: ap.rearrange("p (s w) -> p s w", s=nseg)
                            u3 = v3(ua_[half][:, off:off + nseg * (w + 2)])
                            off += nseg * (w + 2)
                            yv = v3(y0_[half][:, c0:c0 + ncols])
                            self.cp("pool", u3[:, :, 0:2], hview(jj), (hkeyf(jj),) + extra, (uh,))
                            if scale_ap is None:
                                self.cp("act", u3[:, :, 2:2 + w], v3(bk[:, c0:c0 + ncols]), (bkey,), (uk,))
                            else:
                                self.act(u3[:, :, 2:2 + w], v3(bk[:, c0:c0 + ncols]), AF.Identity, (bkey, K("flag")), (uk,), scale=scale_ap)
                            self.stt("dve", yv, u3[:, :, 1:1 + w], wdwT[:, 1, jj:jj + 1], yv, ALU.mult, ALU.add, (uk, uh, yk, K("wdwT1")), (yk,))
                            self.stt("dve", yv, u3[:, :, 0:w], wdwT[:, 0, jj:jj + 1], yv, ALU.mult, ALU.add, (uk, uh, yk, K("wdwT0")), (yk,))
                            self.cp("pool", hview(jj), u3[:, :, w:w + 2], (uk,), (hkeyf(jj),))
                    j = 2 * g + pair
                    self.act(y0_[0][:, 0:n], y0_[0][:, 0:n], AF.Gelu_apprx_tanh, (K("y0", pair, 0),), (K("y0", pair, 0),))
                    self.tt("dve", actT[:, j, 0:n], y0_[0][:, 0:n], y0_[1][:, 0:n], ALU.mult, (K("y0", pair, 0), K("y0", pair, 1)), (K("actT", j),))
            for c in range(8):
                wg, wk = self.wload(wdown_g[c], kwdown[c], NFF, 128)
                bk, bkey = self.bank()
                for j in range(NFF):
                    self.mm(bk[:, 0:n], wg(j), actT[:, j, 0:n], j == 0, j == NFF - 1, (wk, K("actT", j)), (bkey,))
                self.cp("act", m2[:, c, 0:n], bk[:, 0:n], (bkey,), (K("m2", c),))
            rms_bcast(m2, n, lambda kc: (K("m2", kc),))
            alias_fence()

        def store_y(dst_rows, ntile):
            for t in range(ntile):
                yb_ = xin[t % 2]
                ky = K("xin", t % 2)
                for half in range(2):
                    bk, bkey = self.bank()
                    for j in range(4):
                        kc = half * 4 + j
                        self.tr(bk[:, j * 128:(j + 1) * 128], xT[:, kc, t * 128:(t + 1) * 128], ident_f[:], (K("xT", kc, t), K("ident_f")), (bkey,))
                    self.cp("act", yb_[:, half * 512:(half + 1) * 512], bk[:], (bkey,), (K("xinv", t % 2),) if half else (ky,))
                self.dma("sp", dst_rows[t * 128:(t + 1) * 128, :], yb_[:], (ky, K("xinv", t % 2)), (), ("xin", t % 2))

        def load_dma(tile_ap):
            self.dma("sp", xin[0][:], tile_ap, (), (K("xin", 0), K("xinv", 0)), ("xin", 0))

        def prefetch_x(nxt_tiles):
            if not nxt_tiles or self.prefetched:
                return
            load_dma(nxt_tiles[0])
            self.prefetched = True

        def load_tile(nxt_tiles, t):
            xb = xin[0]
            kx, kv = K("xin", 0), K("xinv", 0)
            for half in range(2):
                bk, bkey = self.bank()
                for j in range(4):
                    kc = half * 4 + j
                    self.tr(bk[:, j * 128:(j + 1) * 128], xb[:, kc * 128:(kc + 1) * 128], ident_f[:], (kx if half == 0 else kv, K("ident_f")), (bkey,))
                self.cp("act", xT[:, half * 4:half * 4 + 4, t * 128:(t + 1) * 128],
                        bk[:].rearrange("p (j n) -> p j n", j=4), (bkey,), tuple(K("xT", half * 4 + j, t) for j in range(4)))
            if t + 1 < len(nxt_tiles):
                load_dma(nxt_tiles[t + 1])

        def store_tile(dst_ap, t):
            xb = xin[1]
            kx, kv = K("xin", 1), K("xinv", 1)
            for half in range(2):
                bk, bkey = self.bank()
                for j in range(4):
                    kc = half * 4 + j
                    self.tr(bk[:, j * 128:(j + 1) * 128], xT[:, kc, t * 128:(t + 1) * 128], ident_f[:], (K("xT", kc, t), K("ident_f")), (bkey,))
                self.cp("act", xb[:, half * 512:(half + 1) * 512], bk[:], (bkey,), (kv,) if half else (kx,))
            self.dma("sp", dst_ap, xb[:], (kx, kv), (), ("xin", 1))

        def swap_tiles(dst_tiles, nxt_tiles):
            dst_tiles = dst_tiles or []
            nxt_tiles = nxt_tiles or []
            prefetch_x(nxt_tiles)
            self.prefetched = False
            for t in range(max(len(dst_tiles), len(nxt_tiles))):
                if t < len(dst_tiles):
                    store_tile(dst_tiles[t], t)
                if t < len(nxt_tiles):
                    load_tile(nxt_tiles, t)

        def store_hist(hsrc, hkeys, dst):
            self.cp("dve", h88[:], hsrc.rearrange("p j t -> p t j"), hkeys, (K("h88"),))
            bk, bkey = self.bank()
            self.tr(bk[0:88, 0:128], h88[:].rearrange("p t j -> p (t j)"), ident_f[:], (K("h88"), K("ident_f")), (bkey,))
            self.cp("act", stage[0:88, :], bk[0:88, 0:128], (bkey,), (K("stage"),))
            self.dma("sp", dst, stage[0:88, :], (K("stage"),), (), ("stage_o",))

        def kv_stage(bk, bkey, which, dst):
            st = xin[1][:, which * 512:(which + 1) * 512]
            sk = K("xin", 1) if which == 0 else K("xinv", 1)
            self.cp("act", st, bk[:], (bkey,), (sk,))
            self.dma("sp", dst, st, (sk,), (), ("xin", 1) if which == 0 else ("xinv", 1))

        LAST_KV0 = 16 + NMAIN - 4

        def common_proj(n, units, tglob0, kv_from, pre, flagged, kv_out):
            if not pre:
                wg, wk = self.wload(win_g[G_QA], kwin[G_QA], 8, 512)
                def ev_q(j, bk, bkey):
                    self.act(qaT[:, j, 0:n], bk[:, 0:n], AF.Identity, (bkey,), (K("qaT", j),), scale=0.125)
                pairs_fm(wg, wk, n, ev_q)
            if kv_from < len(units):
                wg, wk = self.wload(win_g[G_KA], kwin[G_KA], 8, 512)
                def ev_k(j, bk, bkey):
                    for ui in range(kv_from, len(units)):
                        c0, C = units[ui]
                        if C == 128:
                            slot = (tglob0 + ui) % 8
                            self.cp("act", kaT[slot][:, j, :], bk[:, c0:c0 + 128], (bkey,), (K("kaT", slot, j),))
                    if units[0][1] != 128:
                        self.cp("act", kaT[tglob0 % 8][:, j, :], bk[:, 0:128], (bkey,), (K("kaT", tglob0 % 8, j),))
                pairs_fm(wg, wk, n, ev_k)
                kunits = [(ui, units[ui]) for ui in range(kv_from, len(units)) if kv_out(ui, 0) is not None] if units[0][1] == 128 else []
                if kunits:
                    units_tm(wg, wk, 0, 512, [u for _, u in kunits], lambda i, bk, bkey: kv_stage(bk, bkey, 0, kv_out(kunits[i][0], 0)))
                if units[0][1] != 128:
                    units_tm(wg, wk, 0, 512, [(0, 128)], lambda i, bk, bkey: kv_stage(bk, bkey, 0, kv_out(0, 0)))
                wg, wk = self.wload(win_g[G_VA], kwin[G_VA], 8, 512)
                if units[0][1] == 128:
                    def ev_v(i, bk, bkey):
                        ui = kv_from + i
                        slot = (tglob0 + ui) % 8
                        self.cp("act", vaug[slot][:, :, 0:64], bk[:].rearrange("p (h e) -> p h e", h=8), (bkey,), (K("vaug", slot),))
                        if flagged:
                            self.ts("dve", vaug[slot][:, :, 64], ones_f[:, 0:8], flag[:, 0:1], None, ALU.mult, None,
                                    (K("ones_f"), K("flag")), (K("vaug1", slot),))
                        else:
                            self.cp("dve", vaug[slot][:, :, 64], ones_f[:, 0:8], (K("ones_f"),), (K("vaug1", slot),))
                        if kv_out(ui, 1) is not None:
                            kv_stage(bk, bkey, 1, kv_out(ui, 1))
                    units_tm(wg, wk, 0, 512, units[kv_from:], ev_v)
                else:
                    units_tm(wg, wk, 0, 512, [(0, 128)], lambda i, bk, bkey: kv_stage(bk, bkey, 1, kv_out(0, 1)))
            if not pre:
                wg, wk = self.wload(win_g[G_QKB], kwin[G_QKB], 8, 512)
                def ev_qb(j, bk, bkey):
                    self.cp("act", qbT[:, j, 0:n], bk[0:64, 0:n], (bkey,), (K("qbT", j),))
                def ev_kb(j, bk, bkey):
                    self.cp("act", kbT[:, j, 0:n], bk[0:64, 0:n], (bkey,), (K("kbT", j),))
                heads_fm(wg, wk, 0, 4, n, ev_qb)
                heads_fm(wg, wk, 256, 4, n, ev_kb)
            wg, wk = self.wload(wgk_g, kwgk, 8, 16)
            bk, bkey = self.bank()
            for kc in range(8):
                self.mm(bk[0:16, 0:n], wg(kc), hT[:, kc, 0:n], kc == 0, kc == 7, (wk, K("hT", kc)), (bkey,))
            self.cp("act", gkT[0:16, 0:n], bk[0:16, 0:n], (bkey,), (K("gkT"),))

        SK = K("Sst")
        def prompt_block(src_rows, ntile, tglob0, mode, dst_rows, kv_from, preloaded=False, nxt=None):
            n = ntile * 128
            pre = mode == "prefix"
            flagged = mode != "full"
            units = [(t * 128, 128) for t in range(ntile)]
            mark = lambda nm: self.marks.append((mode, tglob0, nm, len(S.ops)))
            mark("start")
            if not preloaded:
                swap_tiles(None, [src_rows[t * 128:(t + 1) * 128, :] for t in range(ntile)])
            rms_bcast(xT, n, xkeys(ntile))
            if flagged:
                modulate(n, ntile, "AmP", "BmP", lambda kc: AmP[:, kc:kc + 1], lambda kc: BmP[:, kc:kc + 1])
            else:
                modulate(n, ntile, "Am", "Bm", lambda kc: Am[:, kc, 0:1], lambda kc: Bm[:, kc, 0:1])
            if pre and nxt is not None:
                swap_tiles(None, nxt)
            mark("norm1")

            def kv_out(ui, which):
                tg = tglob0 + ui
                if mode != "full" or tg < LAST_KV0:
                    return None
                r0 = (tg - LAST_KV0) * 128
                return (kp_o if which == 0 else vp_o)[r0:r0 + 128, :]
            common_proj(n, units, tglob0, kv_from, pre, flagged, kv_out)
            mark("proj")
            if pre:
                gla_units(units, pre, lambda ui: (Sst, Sbf, SK, None, None))
                mark("gla")
                return
            def attn_all():
                for t in range(ntile):
                    attn_tile(tglob0 + t, t)
            gla_units(units, pre, lambda ui: (Sst, Sbf, SK, None, None), attn_all)
            mark("attn")
            merge_and_out(n)
            resid(n, ntile, lambda kc: Gm[:, kc, 0:1], "Gm")
            mark("merge")
            rms_bcast(xT, n, xkeys(ntile))
            if flagged:
                modulate(n, ntile, "AfP", "BfP", lambda kc: AfP[:, kc:kc + 1], lambda kc: BfP[:, kc:kc + 1])
            else:
                modulate(n, ntile, "Af", "Bf", lambda kc: Af[:, kc, 0:1], lambda kc: Bf[:, kc, 0:1])
            prefetch_x(nxt)
            ffn(n, [(0, n, 1, lambda jj: hist[:, jj:jj + 1, :], lambda jj: K("hist", jj), (), None)])
            mark("ffn")
            resid(n, ntile, lambda kc: Gf[:, kc, 0:1], "Gf")
            swap_tiles([dst_rows[t * 128:(t + 1) * 128, :] for t in range(ntile)], nxt)
            mark("end")

        def sample_block():
            n = 128
            slot = 0
            units = [(s * 32, 32) for s in range(4)]
            self.marks.append(("sample", 0, "start", len(S.ops)))
            rms_bcast(xT, n, xkeys(1))
            modulate_seq(Am, Bm, "Am", "Bm")
            smark = lambda nm: self.marks.append(("sample", 0, nm, len(S.ops)))
            smark("norm1")
            common_proj(n, units, slot, 0, False, False, lambda ui, which: (ks_o if which == 0 else vs_o))
            smark("proj")
            SSK = K("Sst")
            def S_of(ui):
                def before():
                    self.dma("sp", Ss[:], sgla[ui].rearrange("h k v -> k h v"), (), tuple(SSK + (h,) for h in range(4)), ("Ss",))
                    self.cp("act", Ssb[:], Ss[:], tuple(SSK + (h,) for h in range(4)), (SSK + ("b",),))
                def after():
                    self.dma("sp", glas_o[ui].rearrange("h k v -> k h v"), Ss[:], tuple(SSK + (h,) for h in range(4)), (), ("Ss",))
                return (Ss, Ssb, SSK, before, after)
            wgv, wkv = self.wload(win_g[G_VA], kwin[G_VA], 8, 512)
            for s in range(4):
                bk, bkey = self.bank()
                for kc in range(8):
                    self.mm(bk[0:32, :], hT[:, kc, s * 32:(s + 1) * 32], wgv(kc), kc == 0, kc == 7, (wkv, K("hT", kc)), (bkey,))
                self.cp("act", vown[:, s, :, 0:64], bk[0:32, :].rearrange("p (h e) -> p h e", h=8), (bkey,), (K("vown", s),))
                self.cp("dve", vown[:, s, :, 64], ones_f[0:32, 0:8], (K("ones_f"),), (K("vown1", s),))
            def attn_all():
                for s in range(4):
                    attn_sample(s, slot)
            gla_units(units, False, S_of, attn_all)
            smark("attn")
            merge_and_out(n)
            resid_seq(Gm, "Gm")
            smark("merge")
            rms_bcast(xT, n, xkeys(1))
            modulate_seq(Af, Bf, "Af", "Bf")
            for s in range(4):
                load_T(sconv[s], 88, hist_s[:, s, :, :].rearrange("p j t -> p t j"), "hist_s_in%d" % s,
                       view=lambda a: a.rearrange("p (t j) -> p t j", t=2))
            hs_in = tuple(K("hist_s_in%d" % s) for s in range(4))
            ffn(n, [(0, n, 4, lambda jj: hist_s[:, :, jj, :], lambda jj: K("hist_s", jj), hs_in, None)])
            smark("ffn")
            resid_seq(Gf, "Gf")
            store_y(ysam, 1)
            for s in range(4):
                store_hist(hist_s[:, s, :, :], tuple(K("hist_s", jj) for jj in range(44)), convs_o[s])

        def mod_mixed(At, Bt, AP, BP, Asc, Bsc, APk, BPk):
            for kc in range(8):
                xk = (K("xT", kc, 0), K("xT", kc, 1))
                v3 = lambda ap: ap.rearrange("p (s w) -> p s w", s=4)
                self.tt("dve", tmpf2[:, 0:256], xT[:, kc, 0:256], rbc[:, 0:256], ALU.mult, xk + (K("rbc"),), (K("tmpf2"),))
                self.act(hT[:, kc, 0:128], tmpf2[:, 0:128], AF.Identity, (K("tmpf2"),) + modkeys[APk] + modkeys[BPk], (K("hT", kc),),
                         bias=BP[:, kc:kc + 1], scale=AP[:, kc:kc + 1])
                self.tt("dve", v3(tmpf2[:, 128:256]), v3(tmpf2[:, 128:256]), At[:, kc, 1:5].to_broadcast([128, 4, 32]), ALU.mult,
                        (K("tmpf2"),) + modkeys[Asc], (K("tmpf2"),))
                self.tt("dve", v3(hT[:, kc, 128:256]), v3(tmpf2[:, 128:256]), Bt[:, kc, 1:5].to_broadcast([128, 4, 32]), ALU.add,
                        (K("tmpf2"), K("hT", kc)) + modkeys[Bsc], (K("hT", kc),))

        def resid_mixed(Gt, gname):
            v3 = lambda ap: ap.rearrange("p (s w) -> p s w", s=4)
            for kc in range(8):
                tb, tk = (tmpf2, K("tmpf2")) if kc % 2 == 0 else (tmpf, K("tmpf"))
                self.tt("pool", tb[:, 0:256], m2[:, kc, 0:256], rbc[:, 0:256], ALU.mult, (K("m2", kc), K("rbc")), (tk,))
                self.stt("dve", xT[:, kc, 0:128], tb[:, 0:128], Gt[:, kc, 0:1], xT[:, kc, 0:128], ALU.mult, ALU.add,
                         (tk, K("xT", kc, 0)) + modkeys[gname], (K("xT", kc, 0),))
                self.tt("dve", v3(tb[:, 128:256]), v3(tb[:, 128:256]), Gt[:, kc, 1:5].to_broadcast([128, 4, 32]), ALU.mult,
                        (tk,) + modkeys[gname], (tk,))
                self.tt("dve", xT[:, kc, 128:256], xT[:, kc, 128:256], tb[:, 128:256], ALU.add, (tk, K("xT", kc, 1)), (K("xT", kc, 1),))

        def mixed_block(nxt):
            n = 256
            SL, OVS = 0, 7
            mk = lambda nm: self.marks.append(("mixed", 15, nm, len(S.ops)))
            mk("start")
            rms_bcast(xT, n, xkeys(2))
            mod_mixed(Am, Bm, AmP, BmP, "Am", "Bm", "AmP", "BmP")
            mk("norm1")
            wg, wk = self.wload(win_g[G_QA], kwin[G_QA], 8, 512)
            def ev_q(j, bk, bkey):
                self.act(qaT[:, j, 0:n], bk[:, 0:n], AF.Identity, (bkey,), (K("qaT", j),), scale=0.125)
            pairs_fm(wg, wk, n, ev_q)
            wg, wk = self.wload(win_g[G_KA], kwin[G_KA], 8, 512)
            def ev_k(j, bk, bkey):
                self.cp("act", kaT[OVS][:, j, :], bk[:, 0:128], (bkey,), (K("kaT", OVS, j),))
                self.cp("act", kaT[SL][:, j, :], bk[:, 128:256], (bkey,), (K("kaT", SL, j),))
            pairs_fm(wg, wk, n, ev_k)
            units_tm(wg, wk, 0, 512, [(128, 128)], lambda i, bk, bkey: kv_stage(bk, bkey, 0, ks_o))
            wg, wk = self.wload(win_g[G_VA], kwin[G_VA], 8, 512)
            def ev_v(i, bk, bkey):
                self.cp("act", vaug[OVS][:, :, 0:64], bk[:].rearrange("p (h e) -> p h e", h=8), (bkey,), (K("vaug", OVS),))
                self.ts("dve", vaug[OVS][:, :, 64], ones_f[:, 0:8], flag[:, 0:1], None, ALU.mult, None,
                        (K("ones_f"), K("flag")), (K("vaug1", OVS),))
            units_tm(wg, wk, 0, 512, [(0, 128)], ev_v)
            units_tm(wg, wk, 0, 512, [(128, 128)], lambda i, bk, bkey: kv_stage(bk, bkey, 1, vs_o))
            for sq_ in range(4):
                bk, bkey = self.bank()
                c0 = 128 + sq_ * 32
                for kc in range(8):
                    self.mm(bk[0:32, :], hT[:, kc, c0:c0 + 32], wg(kc), kc == 0, kc == 7, (wk, K("hT", kc)), (bkey,))
                self.cp("act", vown[:, sq_, :, 0:64], bk[0:32, :].rearrange("p (h e) -> p h e", h=8), (bkey,), (K("vown", sq_),))
                self.cp("dve", vown[:, sq_, :, 64], ones_f[0:32, 0:8], (K("ones_f"),), (K("vown1", sq_),))
            wg, wk = self.wload(win_g[G_QKB], kwin[G_QKB], 8, 512)
            def ev_qb(j, bk, bkey):
                self.cp("act", qbT[:, j, 0:n], bk[0:64, 0:n], (bkey,), (K("qbT", j),))
            def ev_kb(j, bk, bkey):
                self.cp("act", kbT[:, j, 0:n], bk[0:64, 0:n], (bkey,), (K("kbT", j),))
            heads_fm(wg, wk, 0, 4, n, ev_qb)
            heads_fm(wg, wk, 256, 4, n, ev_kb)
            wg, wk = self.wload(wgk_g, kwgk, 8, 16)
            bk, bkey = self.bank()
            for kc in range(8):
                self.mm(bk[0:16, 0:n], wg(kc), hT[:, kc, 0:n], kc == 0, kc == 7, (wk, K("hT", kc)), (bkey,))
            self.cp("act", gkT[0:16, 0:n], bk[0:16, 0:n], (bkey,), (K("gkT"),))
            mk("proj")
            units = [(0, 128)] + [(128 + 32 * q, 32) for q in range(4)]
            SSK = K("Ss")
            def S_of(ui):
                if ui == 0:
                    return (Sst, Sbf, SK, None, None)
                sq_ = ui - 1
                def before():
                    self.dma("sp", Ss[:], sgla[sq_].rearrange("h k v -> k h v"), (), tuple(SSK + (h,) for h in range(4)), ("Ss",))
                    self.cp("act", Ssb[:], Ss[:], tuple(SSK + (h,) for h in range(4)), (SSK + ("b",),))
                def after():
                    self.dma("sp", glas_o[sq_].rearrange("h k v -> k h v"), Ss[:], tuple(SSK + (h,) for h in range(4)), (), ("Ss",))
                return (Ss, Ssb, SSK, before, after)
            def attn_all():
                attn_tile(15, 0)
                for q in range(4):
                    attn_sample(q, SL, 128)
            gla_units(units, False, S_of, attn_all)
            mk("attn")
            merge_and_out(n)
            resid_mixed(Gm, "Gm")
            mk("merge")
            rms_bcast(xT, n, xkeys(2))
            mod_mixed(Af, Bf, AfP, BfP, "Af", "Bf", "AfP", "BfP")
            for q in range(4):
                load_T(sconv[q], 88, hist_s[:, 4 + q, :, :].rearrange("p j t -> p t j"), "hist_s_in%d" % q,
                       view=lambda a: a.rearrange("p (t j) -> p t j", t=2))
            hs_in = tuple(K("hist_s_in%d" % q) for q in range(4)) + (K("hist_s_ov"),)
            prefetch_x(nxt)
            ffn(n, [(0, n, 8, lambda jj: hist_s[:, :, jj, :], lambda jj: K("hist_s", jj), hs_in, None)])
            mk("ffn")
            resid_mixed(Gf, "Gf")
            swap_tiles([yov[:, :], ysam[:, :]], nxt)
            mk("end")
            self.ts("dve", hist[:], hist_s[:, 3, :, :], flag[:, 0:1], None, ALU.mult, None,
                    tuple(K("hist_s", jj) for jj in range(44)) + (K("flag"),), tuple(K("hist", jj) for jj in range(44)))
            for q in range(4):
                store_hist(hist_s[:, 4 + q, :, :], tuple(K("hist_s", jj) for jj in range(44)), convs_o[q])

        tiles_of = lambda rows, nt: [rows[t * 128:(t + 1) * 128, :] for t in range(nt)]
        sched = []
        t0 = 0
        while t0 < 11:
            nt = min(NB, 11 - t0)
            sched.append(("prefix", xpre[t0 * 128:(t0 + nt) * 128, :], nt, t0, None, nt))
            t0 += nt
        while t0 < 15:
            nt = min(NB, 15 - t0)
            sched.append(("prefix", xpre[t0 * 128:(t0 + nt) * 128, :], nt, t0, None, 0))
            t0 += nt
        sched.append(("mixed", None, 2, 15, None, 0))
        for b in range(NMAIN // NB):
            sched.append(("full", xmain[b * NBT:(b + 1) * NBT, :], NB, 16 + b * NB, ymain[b * NBT:(b + 1) * NBT, :], 0))
        def tiles_for(e):
            return [xov[:, :], xsam[:, :]] if e[0] == "mixed" else tiles_of(e[1], e[2])
        for bi, (mode, src, nt, tg, dst, kvf) in enumerate(sched):
            nxt = tiles_for(sched[bi + 1]) if bi + 1 < len(sched) else None
            if mode == "mixed":
                ada_part2()
                mixed_block(nxt)
            else:
                prompt_block(src, nt, tg, mode, dst, kvf, preloaded=bi > 0, nxt=nxt)
        self.dma("sp", glap_o.rearrange("h k v -> k h v"), Sst[:], tuple(SK + (h,) for h in range(4)), (), ("glap",))
        store_hist(hist[:], tuple(K("hist", jj) for jj in range(44)), convp_o)
        return S.finalize()


_CACHE = {}


def _program():
    if "nc" not in _CACHE:
        b = Builder()
        b.build()
        _CACHE["nc"] = b.nc
    return _CACHE["nc"]


def kernel(x_prompt, x_sample, cache_k_a, cache_v_a, state_gla, state_conv, c_prompt, c_sample,
           w_ada, b_ada, g_pre_mix, g_post_mix, g_pre_ffn, g_post_ffn, w_in, w_gk2, b_gk,
           rel_bias, g_gla, w_br_a, w_br_b, w_out, w_up, w_dw, b_dw, w_down):
    f = lambda a: np.ascontiguousarray(np.asarray(a, dtype=np.float32))
    x_prompt, x_sample = f(x_prompt), f(x_sample)
    rb = f(rel_bias)[0]
    kk = np.arange(128)[:, None]
    qq = np.arange(128)[None, :]
    idx_prev = np.clip(qq + 128 - kk, -128, 128) + 128
    idx_own = np.clip(qq - kk, -128, 128) + 128
    btab = np.stack([rb[:, idx_prev].transpose(1, 0, 2), rb[:, idx_own].transpose(1, 0, 2)])
    cvec = np.broadcast_to(rb[:, 256][None, :], (128, 8))
    wi = f(w_in)[0]
    wi = np.concatenate([wi[:, :3072], wi[:, 3088:], wi[:, 3072:3088]], axis=1)
    wu = f(w_up)[0]
    cols = []
    for g in range(11):
        cols += [wu[:, 2 * g * 128:(2 * g + 2) * 128], wu[:, DFF + 2 * g * 128:DFF + (2 * g + 2) * 128]]
    wu = np.concatenate(cols, axis=1)
    shared = {
        "w_ada": f(w_ada)[0], "b_ada": f(b_ada)[0].reshape(48, 128),
        "gvec": np.concatenate([f(g_pre_mix)[0], f(g_post_mix)[0], f(g_pre_ffn)[0], f(g_post_ffn)[0]]).reshape(32, 128),
        "w_in": f(wi), "w_gk2": f(w_gk2)[0], "b_gk": f(b_gk)[0].reshape(1, 256),
        "btab": f(btab), "cvec": f(cvec), "ggla": f(np.broadcast_to(np.tile(f(g_gla)[0], 4)[None, :], (128, 512))),
        "w_br_a": f(w_br_a)[0], "w_br_b": f(w_br_b)[0], "w_out": f(w_out)[0], "w_up": f(wu),
        "w_dw": f(w_dw)[0].reshape(3 * 44, 128), "b_dw": f(b_dw)[0].reshape(44, 128), "w_down": f(w_down)[0],
    }
    in_maps = []
    for c in range(8):
        b, hf = c // 2, c % 2
        m = dict(shared)
        if hf == 1:
            m["xpre"] = x_prompt[b, 0:NPRE * 128]
            m["xov"] = x_prompt[b, NPRE * 128:2048]
        else:
            m["xpre"] = np.zeros((NPRE * 128, D), np.float32)
            m["xov"] = np.zeros((128, D), np.float32)
        m["xmain"] = x_prompt[b, hf * 2048:(hf + 1) * 2048]
        m["xsam"] = x_sample[4 * c:4 * c + 4].reshape(128, D)
        m["crow"] = f(np.concatenate([f(c_prompt)[b:b + 1], f(c_sample)[4 * c:4 * c + 4]], 0).reshape(40, 128))
        m["flag"] = np.full((128, 1), float(hf), np.float32)
        m["ck"] = f(cache_k_a)[0, 4 * c:4 * c + 4].reshape(4, 512, 512)
        m["cv"] = f(cache_v_a)[0, 4 * c:4 * c + 4].reshape(4, 512, 512)
        m["sgla"] = f(state_gla)[0, 4 * c:4 * c + 4]
        m["sconv"] = f(state_conv)[0, 4 * c:4 * c + 4].reshape(4, 88, 128)
        in_maps.append({k: np.ascontiguousarray(v) for k, v in m.items()})
    nc = _program()
    cores = [int(t) for t in os.environ.get("KCORES", "0,1,2,3,4,5,6,7").split(",")]
    if os.environ.get("KTRACE"):
        res = run_bass_kernel_spmd(nc, [in_maps[c] for c in cores], core_ids=list(range(len(cores))), trace=True)
        print("EXEC_TIME_NS", res.exec_time_ns)
    else:
        res = run_bass_kernel_spmd(nc, [in_maps[c] for c in cores], core_ids=list(range(len(cores))))
    R = {c: res.results[i] for i, c in enumerate(cores)}
    y_prompt = np.zeros((4, 4096, D), np.float32)
    y_sample = np.zeros((32, 32, D), np.float32)
    k_p = np.zeros((1, 4, 512, 8, 64), np.float32); v_p = np.zeros_like(k_p)
    gla_p = np.zeros((1, 4, 4, 64, 128), np.float32)
    conv_p = np.zeros((1, 4, 2, 2 * DFF), np.float32)
    k_s = np.zeros((1, 32, 32, 8, 64), np.float32); v_s = np.zeros_like(k_s)
    gla_s = np.zeros((1, 32, 4, 64, 128), np.float32)
    conv_s = np.zeros((1, 32, 2, 2 * DFF), np.float32)
    for c in cores:
        b, hf = c // 2, c % 2
        r = R[c]
        y_prompt[b, hf * 2048:(hf + 1) * 2048] = r["ymain"]
        y_sample[4 * c:4 * c + 4] = r["ysam"].reshape(4, 32, D)
        if hf == 1:
            k_p[0, b] = r["kp"].reshape(512, 8, 64)
            v_p[0, b] = r["vp"].reshape(512, 8, 64)
            gla_p[0, b] = r["glap"]
            conv_p[0, b] = r["convp"].reshape(2, 2 * DFF)
        k_s[0, 4 * c:4 * c + 4] = r["ks"].reshape(4, 32, 8, 64)
        v_s[0, 4 * c:4 * c + 4] = r["vs"].reshape(4, 32, 8, 64)
        gla_s[0, 4 * c:4 * c + 4] = r["glas"]
        conv_s[0, 4 * c:4 * c + 4] = r["convs"].reshape(4, 2, 2 * DFF)
    return (y_prompt, y_sample, k_p, v_p, gla_p, conv_p, k_s, v_s, gla_s, conv_s)
```

```python
from contextlib import ExitStack
import os

import numpy as np
import concourse.bass as bass
import concourse.mybir as mybir
from concourse.bass_utils import run_bass_kernel_spmd

F32 = mybir.dt.float32
BF16 = mybir.dt.bfloat16
AF = mybir.ActivationFunctionType
ALU = mybir.AluOpType

D = 1024
KC = 8
DFF = 2816
NFF = 22
DIN = 5136
G_QA, G_KA, G_VA, G_QKB, G_VB, G_GB, G_GA, G_GBR = 0, 1, 2, 3, 4, 5, 6, 8
EPS = 1e-6
NEG = -30000.0
NPRE = 15
NMAIN = 16
NB = 4


class Op:
    __slots__ = ("eng", "fn", "reads", "writes", "dsem", "signal", "sigval", "deps", "rdeps")

    def __init__(self, eng, fn, reads, writes, dsem):
        self.eng, self.fn, self.reads, self.writes, self.dsem = eng, fn, reads, writes, dsem
        self.signal = False
        self.sigval = 0
        self.deps = ()
        self.rdeps = frozenset()


class Sched:
    DMA = ("sp", "pool_dma")

    def __init__(self, nc, stack):
        self.nc = nc
        self.stack = stack
        self.ops = []
        self.eng_obj = {"pe": nc.tensor, "act": nc.scalar, "dve": nc.vector, "pool": nc.gpsimd,
                        "sp": nc.sync, "pool_dma": nc.gpsimd}
        self.wuses = []
        self.wdepth = 2

    def add(self, eng, fn, reads=(), writes=(), dsem=None):
        op = Op(eng, fn, tuple(reads), tuple(writes), dsem)
        self.ops.append(op)
        return op

    def queue_of(self, op):
        return "pool" if op.eng == "pool_dma" else op.eng

    def finalize(self):
        nc = self.nc
        inserts = {}
        lastrd = {}
        for idx, op in enumerate(self.ops):
            for k in op.reads:
                if k and k[0] == "wuse":
                    lastrd[k[1]] = idx
        prev = 0
        for i, (pos, op) in enumerate(self.wuses):
            tgt = self.wuses[max(0, i - self.wdepth)][0]
            if i >= 3 and (i - 3) in lastrd:
                tgt = max(tgt, lastrd[i - 3] + 1)
            tgt = max(tgt, prev)
            prev = tgt
            assert tgt <= pos, (i, tgt, pos)
            inserts.setdefault(tgt, []).append(op)
        ops = []
        for i, op in enumerate(self.ops):
            if i in inserts:
                ops.extend(inserts[i])
            ops.append(op)
        self.ops = ops
        last_w = {}
        readers = {}
        for i, op in enumerate(ops):
            deps = set()
            q = self.queue_of(op)
            isdma = op.dsem is not None
            for k in op.reads:
                j = last_w.get(k)
                if j is not None:
                    deps.add(j)
            rdeps = set(deps)
            for k in op.writes:
                j = last_w.get(k)
                if j is not None:
                    oj = ops[j]
                    if isdma or oj.dsem is not None or self.queue_of(oj) != q or q != "pe":
                        deps.add(j)
                for j in readers.get(k, ()):
                    oj = ops[j]
                    if isdma or oj.dsem is not None or self.queue_of(oj) != q or (q != "pe" and os.environ.get("KWAR", "1") == "1"):
                        deps.add(j)
            deps.discard(i)
            op.deps = tuple(sorted(deps))
            op.rdeps = frozenset(rdeps)
            for j in op.deps:
                ops[j].signal = True
            for k in op.reads:
                lst = readers.setdefault(k, [])
                if op.dsem is None:
                    lst[:] = [j for j in lst if ops[j].dsem is not None or self.queue_of(ops[j]) != q]
                lst.append(i)
            for k in op.writes:
                last_w[k] = i
                readers[k] = []
        esem = {}
        for e in ("pe", "act", "dve", "pool"):
            esem[e] = self.stack.enter_context(nc.semaphore("sem_" + e))
        dsems = {}
        cnt = {}
        for op in ops:
            if op.dsem is not None:
                if op.dsem not in dsems:
                    dsems[op.dsem] = self.stack.enter_context(nc.semaphore("dsem_%d" % len(dsems)))
                    cnt[op.dsem] = 0
                cnt[op.dsem] += 1
                op.sigval = 16 * cnt[op.dsem]
            elif op.signal:
                q = self.queue_of(op)
                cnt[q] = cnt.get(q, 0) + 1
                op.sigval = cnt[q]
        known = {q: {} for q in ("pe", "act", "dve", "pool", "sp")}
        for op in ops:
            q = self.queue_of(op)
            eng = self.eng_obj[op.eng]
            need = {}
            need_r = {}
            for j in op.deps:
                oj = ops[j]
                s = dsems[oj.dsem] if oj.dsem is not None else esem[self.queue_of(oj)]
                key = id(s)
                if key not in need or need[key][1] < oj.sigval:
                    need[key] = (s, oj.sigval)
                if j in op.rdeps and need_r.get(key, 0) < oj.sigval:
                    need_r[key] = oj.sigval
            pending = [(key, s_, v) for key, (s_, v) in need.items() if known[q].get(key, 0) < v]
            fuse = None
            if pending and op.dsem is None and q in ("act", "dve", "pool") and os.environ.get("KFUSE", "1") == "1":
                fuse = pending.pop()
            elif pending and q == "pe" and os.environ.get("KFUSEPE", "1") == "1":
                for pi, (key, s_, v) in enumerate(pending):
                    if need_r.get(key, 0) <= known[q].get(key, 0):
                        fuse = pending.pop(pi)
                        break
            for key, s_, v in pending:
                eng.wait_ge(s_, v)
                known[q][key] = v
            ins = op.fn(eng)
            if fuse is not None:
                ins._wait_ge(fuse[1], fuse[2])
                known[q][fuse[0]] = fuse[2]
            if op.dsem is not None:
                ins.then_inc(dsems[op.dsem], 16)
            elif op.signal:
                ins.then_inc(esem[q], 1)
        self.counts = dict((str(k), v) for k, v in cnt.items())
        for k, s in dsems.items():
            nc.sync.wait_ge(s, 16 * cnt[k])
        return len(ops)


class Builder:
    def __init__(self):
        self.stack = ExitStack()
        self.nc = bass.Bass("TRN2", target_bir_lowering=False)
        self.S = Sched(self.nc, self.stack)
        self.nbank = 0
        self.wuse_n = 0
        self.prefetched = False
        self.bank_set = [0, 1, 2, 3, 4]
        self.marks = []
        self.uid = 0

    def din(self, name, shape, dt=F32):
        return self.nc.dram_tensor(name, list(shape), dt, kind="ExternalInput").ap()

    def dout(self, name, shape):
        return self.nc.dram_tensor(name, list(shape), F32, kind="ExternalOutput").ap()

    def dscr(self, name, shape, dt=BF16):
        return self.nc.dram_tensor(name, list(shape), dt, kind="Internal").ap()

    def sb(self, name, shape, dt=F32):
        return self.stack.enter_context(self.nc.sbuf_tensor(name, list(shape), dt))

    def ps(self, name, shape, dt=F32):
        return self.stack.enter_context(self.nc.psum_tensor(name, list(shape), dt))

    def bank(self):
        bs = self.bank_set
        i = bs[self.nbank % len(bs)]
        self.nbank += 1
        return self.banks[i], ("ps", i)

    def key(self, name):
        self.uid += 1
        return (name, self.uid)

    def mm(self, out, lhsT, rhs, start, stop, reads, writes):
        self.S.add("pe", lambda e: e.matmul(out, lhsT, rhs, start=start, stop=stop), reads, writes)

    def tr(self, out, in_, ident, reads, writes):
        self.S.add("pe", lambda e: e.transpose(out, in_, ident), reads, writes)

    def act(self, out, in_, func, reads, writes, bias=None, scale=None, accum_out=None):
        kw = {}
        if bias is not None:
            kw["bias"] = bias
        if scale is not None:
            kw["scale"] = scale
        if accum_out is not None:
            kw["accum_out"] = accum_out
        self.S.add("act", lambda e: e.activation(out, in_, func, **kw), reads, writes)

    def tt(self, eng, out, in0, in1, op, reads, writes):
        self.S.add(eng, lambda e: e.tensor_tensor(out, in0, in1, op), reads, writes)

    def ts(self, eng, out, in0, s1, s2, op0, op1, reads, writes):
        if s2 is None:
            self.S.add(eng, lambda e: e.tensor_scalar(out, in0, s1, None, op0), reads, writes)
        else:
            self.S.add(eng, lambda e: e.tensor_scalar(out, in0, s1, s2, op0, op1), reads, writes)

    def stt(self, eng, out, in0, scalar, in1, op0, op1, reads, writes):
        self.S.add(eng, lambda e: e.scalar_tensor_tensor(out, in0, scalar, in1, op0=op0, op1=op1), reads, writes)

    def cp(self, eng, out, in_, reads, writes):
        if eng == "act":
            self.S.add("act", lambda e: e.copy(out, in_), reads, writes)
        else:
            self.S.add(eng, lambda e: e.tensor_copy(out, in_), reads, writes)

    def memset(self, eng, ap, val, writes):
        self.S.add(eng, lambda e: e.memset(ap, val), (), writes)

    def dma(self, q, out, in_, reads, writes, dsem, slow=False):
        if slow:
            self.S.add(q, lambda e: e.dma_start(out=out, in_=in_, allow_slow_non_contiguous=True), reads, writes, dsem)
        else:
            self.S.add(q, lambda e: e.dma_start(out=out, in_=in_), reads, writes, dsem)

    def interleave(self, fns_banks):
        main = self.S.ops
        streams = []
        for fn, banks in fns_banks:
            self.S.ops = []
            self.bank_set = banks
            fn()
            streams.append(self.S.ops)
        self.S.ops = main
        self.bank_set = [0, 1, 2, 3, 4]
        idx = [0] * len(streams)
        total = sum(len(st) for st in streams)
        for _ in range(total):
            best, bf = None, None
            for si, st in enumerate(streams):
                if idx[si] < len(st):
                    frac = idx[si] / len(st)
                    if bf is None or frac < bf:
                        best, bf = si, frac
            main.append(streams[best][idx[best]])
            idx[best] += 1

    def wload(self, scr_g, wkeys, kcn, width):
        i = self.wuse_n
        self.wuse_n += 1
        slot = i % 3
        buf = self.wbufs[slot]
        key = ("wuse", i)
        dst = buf[:, 0:kcn * width].rearrange("p (kc n) -> p kc n", kc=kcn)
        op = Op("sp", lambda e: e.dma_start(out=dst, in_=scr_g), tuple(wkeys),
                (key, ("wbuf", slot)) + ((("wuse", i - 3),) if i >= 3 else ()), ("wstream", slot))
        self.S.wuses.append((len(self.S.ops), op))
        return (lambda kc, a=0, b=width: buf[:, kc * width + a: kc * width + b]), key

    def build(self):
        nc = self.nc
        S = self.S
        sb, ps = self.sb, self.ps
        NBT = NB * 128
        K = lambda *a: tuple(a)
        xpre = self.din("xpre", [NPRE * 128, D])
        xov = self.din("xov", [128, D])
        xmain = self.din("xmain", [NMAIN * 128, D])
        xsam = self.din("xsam", [128, D])
        crow = self.din("crow", [40, 128])
        flag_d = self.din("flag", [128, 1])
        ck = self.din("ck", [4, 512, 512])
        cv = self.din("cv", [4, 512, 512])
        sgla = self.din("sgla", [4, 4, 64, 128])
        sconv = self.din("sconv", [4, 88, 128])
        w_ada = self.din("w_ada", [D, 6 * D])
        b_ada = self.din("b_ada", [48, 128])
        gvec = self.din("gvec", [32, 128])
        w_in = self.din("w_in", [D, DIN])
        w_gk2 = self.din("w_gk2", [16, 256])
        b_gk = self.din("b_gk", [1, 256])
        btab = self.din("btab", [2, 128, 8, 128])
        cvec = self.din("cvec", [128, 8])
        ggla = self.din("ggla", [128, 512])
        w_br_a = self.din("w_br_a", [512, D])
        w_br_b = self.din("w_br_b", [512, D])
        w_out = self.din("w_out", [D, D])
        w_up = self.din("w_up", [D, 2 * DFF])
        w_dw = self.din("w_dw", [3 * 44, 128])
        b_dw = self.din("b_dw", [44, 128])
        w_down = self.din("w_down", [DFF, D])

        ymain = self.dout("ymain", [NMAIN * 128, D])
        ysam = self.dout("ysam", [128, D])
        yov = self.dout("yov", [128, D])
        kp_o = self.dout("kp", [512, 512])
        vp_o = self.dout("vp", [512, 512])
        glap_o = self.dout("glap", [4, 64, 128])
        convp_o = self.dout("convp", [88, 128])
        ks_o = self.dout("ks", [128, 512])
        vs_o = self.dout("vs", [128, 512])
        glas_o = self.dout("glas", [4, 4, 64, 128])
        convs_o = self.dout("convs", [4, 88, 128])

        win_g = self.dscr("win_g", [10, 128, 8, 512])
        wgk_g = self.dscr("wgk_g", [128, 8, 16])
        wbra_g = self.dscr("wbra_g", [2, 128, 4, 512])
        wbrb_g = self.dscr("wbrb_g", [2, 128, 4, 512])
        wout_g = self.dscr("wout_g", [2, 128, 8, 512])
        wup_g = self.dscr("wup_g", [11, 128, 8, 512])
        wdown_g = self.dscr("wdown_g", [8, 128, NFF, 128])

        self.banks = [ps("bank%d" % i, [128, 512]) for i in range(5)]
        obank = [ps("obank%d" % i, [128, 512]) for i in range(2)]
        pbf = ps("pbf", [128, 1024], BF16)

        self.wbufs = [sb("wbuf%d" % i, [128, 4096], BF16) for i in range(3)]
        ident_f = sb("ident_f", [128, 128])
        ident_b = sb("ident_b", [128, 128], BF16)
        ones_b = sb("ones_b", [128, 128], BF16)
        triu_f = sb("triu_f", [128, 128])
        trisl_f = sb("trisl_f", [128, 128])
        ones_f = sb("ones_f", [128, 8])
        epsb = sb("epsb", [128, 1])
        mod = sb("mod", [128, 6, KC, 5])
        Am = sb("Am", [128, KC, 5]); Bm = sb("Bm", [128, KC, 5]); Gm = sb("Gm", [128, KC, 5])
        Af = sb("Af", [128, KC, 5]); Bf = sb("Bf", [128, KC, 5]); Gf = sb("Gf", [128, KC, 5])
        AmP = sb("AmP", [128, KC]); BmP = sb("BmP", [128, KC])
        AfP = sb("AfP", [128, KC]); BfP = sb("BfP", [128, KC])
        gT = sb("gT", [128, 32])
        badaT = sb("badaT", [128, 48])
        cT = sb("cT", [128, 40])
        siluT = sb("siluT", [128, 40], BF16)
        flag = sb("flag_s", [128, 1])
        wdwT = sb("wdwT", [128, 3, 44])
        bdwT = sb("bdwT", [128, 44])
        wgk_f = sb("wgk_f", [17, 256])
        wgk_b = sb("wgk_b", [17, 256], BF16)
        tab_prev = sb("tab_prev", [128, 8, 128], BF16)
        tab_own = sb("tab_own", [128, 8, 128], BF16)
        tab_mask = sb("tab_mask", [128, 128], BF16)
        cvec_s = sb("cvec_s", [128, 8])
        ggla_s = sb("ggla_s", [128, 512])
        stage = sb("stage", [128, 128])
        hist = sb("hist", [128, 44, 2])
        hist_s = sb("hist_s", [128, 8, 44, 2])
        h88 = sb("h88", [128, 2, 44])
        xin = [sb("xin%d" % i, [128, D]) for i in range(2)]
        xT = sb("xT", [128, KC, NBT])
        hT = sb("hT", [128, KC, NBT], BF16)
        sq = sb("sq", [128, 2, NBT], BF16)
        rbc = sb("rbc", [128, NBT])
        tmpf = sb("tmpf", [128, NBT])
        tmpf2 = sb("tmpf2", [128, NBT])
        qaT = sb("qaT", [128, 4, NBT], BF16)
        kaT = [sb("kaT%d" % i, [128, 4, 128], BF16) for i in range(8)]
        vaug = [sb("vaug%d" % i, [128, 8, 65], BF16) for i in range(8)]
        qbT = sb("qbT", [64, 4, NBT], BF16)
        kbT = sb("kbT", [64, 4, NBT], BF16)
        gkT = sb("gkT", [32, NBT], BF16)
        pT_raw = sb("pT_raw", [128, 2560])
        pT = pT_raw.bitcast(BF16)[:, :].rearrange("p (k h q) -> p k h q", k=5, h=8)
        ya_tok = sb("ya_tok", [128, 512], BF16)
        rden = sb("rden", [128, 8])
        kb_tok = sb("kb_tok", [128, 256])
        vb_tok2 = [sb("vb_tok%d" % i, [128, 4, 128], BF16) for i in range(2)]
        gb2_2 = [sb("gb2_%d" % i, [128, 512]) for i in range(2)]
        gtanh = sb("gtanh", [128, 512])
        Lsp = sb("Lsp", [128, 256])
        e_sb = sb("e_sb", [128, 256])
        e1 = sb("e1", [64, 4, 128]); e2 = sb("e2", [64, 4, 128])
        qtT2 = [sb("qtT%d" % i, [64, 4, 128], BF16) for i in range(2)]; ktT = sb("ktT", [64, 4, 128], BF16)
        kend2 = [sb("kend%d" % i, [128, 256], BF16) for i in range(2)]
        dec2 = [sb("dec%d" % i, [64, 4]) for i in range(2)]
        attT2 = [sb("attT%d" % i, [128, 4, 128], BF16) for i in range(2)]
        Sst = sb("Sst", [64, 4, 128])
        Sbf = sb("Sbf", [64, 4, 128], BF16)
        ssq = sb("ssq", [128, 4]); rgl = sb("rgl", [128, 4])
        yb_tok = sb("yb_tok", [128, 512], BF16)
        sgA = sb("sgA", [128, NBT]); sgB = sb("sgB", [128, NBT])
        m2 = sb("m2", [128, KC, NBT])
        ua = [sb("ua%d" % i, [128, NBT + 8]) for i in range(2)]
        y0 = [sb("y0_%d" % i, [128, NBT]) for i in range(2)]
        ua_sets = [ua, [pT_raw[:, 0:NBT + 8], pT_raw[:, NBT + 8:2 * NBT + 16]]]
        y0_sets = [y0, [pT_raw[:, 2 * NBT + 16:3 * NBT + 16], pT_raw[:, 3 * NBT + 16:4 * NBT + 16]]]
        actT = sb("actT", [128, NFF, NBT], BF16)
        mergedT = lambda c: actT[:, c, :]
        MK = lambda c: K("actT", c)
        yaT = lambda c: actT[:, 8 + c, :]
        YAK = tuple(K("actT", 8 + c) for c in range(4))
        ybT = lambda c: actT[:, 12 + c, :]
        YBK = tuple(K("actT", 12 + c) for c in range(4))
        m2b = m2.bitcast(BF16)
        kcT = lambda c: m2b[:, c, 0:512]
        vcaug = sb("vcaug", [128, 4, 8, 65], BF16)
        vown = sb("vown", [32, 4, 8, 65], BF16)
        Ss = sb("Ss", [64, 4, 128]); Ssb = sb("Ssb", [64, 4, 128], BF16)

        def const_mask(t, keyname, pattern, cm, base, cmp_op):
            self.memset("pool", t[:], 1.0, (K(keyname),))
            S.add("pool", lambda e: e.affine_select(t[:], t[:], pattern=pattern, compare_op=cmp_op, fill=0.0,
                                                    base=base, channel_multiplier=cm), (K(keyname),), (K(keyname),))
        self.memset("pool", ident_f[:], 0.0, (K("ident_f"),))
        S.add("pool", lambda e: e.affine_select(ident_f[:], ident_f[:], pattern=[[-1, 128]], compare_op=ALU.not_equal,
                                                fill=1.0, base=0, channel_multiplier=1), (K("ident_f"),), (K("ident_f"),))
        const_mask(triu_f, "triu_f", [[1, 128]], -1, 0, ALU.is_ge)
        const_mask(trisl_f, "trisl_f", [[-1, 128]], 1, -1, ALU.is_ge)
        self.cp("dve", ident_b[:], ident_f[:], (K("ident_f"),), (K("ident_b"),))
        self.memset("dve", ones_b[:], 1.0, (K("ones_b"),))
        self.memset("dve", ones_f[:], 1.0, (K("ones_f"),))
        self.memset("dve", epsb[:], EPS, (K("epsb"),))
        self.memset("dve", gkT[:], 1.0, (K("gkT_ones"),))
        self.memset("dve", hist[:], 0.0, tuple(K("hist", jj) for jj in range(44)))
        self.memset("dve", hist_s[:, 0:4, :, :], 0.0, (K("hist_s_ov"),))
        self.memset("dve", Sst[:], 0.0, tuple(K("Sst", h) for h in range(4)))
        self.memset("dve", Sbf[:], 0.0, (K("Sst", "b"),))

        def load_T(src2d, rows, dst, kname, view=None):
            self.dma("pool_dma", stage[0:rows, :], src2d, (), (K("stage"),), ("stage",))
            bk, bkey = self.bank()
            self.tr(bk[:, 0:rows], stage[0:rows, :], ident_f[0:rows, 0:rows], (K("stage"), K("ident_f")), (bkey,))
            src = bk[:, 0:rows] if view is None else view(bk[:, 0:rows])
            self.cp("dve", dst, src, (bkey,), (K(kname),))

        load_T(gvec, 32, gT[:], "gT")
        load_T(b_ada, 48, badaT[:], "badaT")
        load_T(crow, 40, cT[:], "cT")
        for j in range(3):
            load_T(w_dw[j * 44:(j + 1) * 44, :], 44, wdwT[:, j, :], "wdwT%d" % j)
        load_T(b_dw, 44, bdwT[:], "bdwT")
        self.dma("pool_dma", flag[:], flag_d, (), (K("flag"),), ("misc", 0))
        self.dma("pool_dma", wgk_f[0:16, :], w_gk2, (), (K("wgk_f0"),), ("misc", 1))
        self.dma("pool_dma", wgk_f[16:17, :], b_gk, (), (K("wgk_f1"),), ("misc", 2))
        self.cp("dve", wgk_b[:], wgk_f[:], (K("wgk_f0"), K("wgk_f1")), (K("wgk_b"),))
        self.dma("pool_dma", cvec_s[:], cvec, (), (K("cvec"),), ("misc", 3))
        self.dma("pool_dma", ggla_s[:], ggla, (), (K("ggla"),), ("misc", 4))
        self.ts("dve", ggla_s[:], ggla_s[:], 0.5, None, ALU.mult, None, (K("ggla"),), (K("ggla"),))
        tabf = xin[0][:, :].rearrange("p (h q) -> p h q", h=8)
        for ti, tdst in ((0, tab_prev), (1, tab_own)):
            self.dma("pool_dma", tabf, btab[ti], (), (K("xin", 0), K("xinv", 0)), ("misc", 5))
            for h in range(8):
                self.ts("dve", tdst[:, h, :], tabf[:, h, :], cvec_s[:, h:h + 1], None, ALU.subtract, None,
                        (K("xin", 0), K("xinv", 0), K("cvec")), (K("tab", ti, h),))
        self.memset("dve", tab_own[64:128, :, 0:64], NEG, tuple(K("tab", 1, h) for h in range(8)))
        for ti, tdst in ((0, tab_prev), (1, tab_own)):
            self.act(tdst[:], tdst[:], AF.Exp, tuple(K("tab", ti, h) for h in range(8)), (K("etab", ti),))
        self.memset("dve", tab_mask[:], 1.0, (K("tab_mask"),))
        self.memset("dve", tab_mask[0:64, 64:128], 0.0, (K("tab_mask"),))

        def cast_group(src_cols, dst_g, name, g):
            self.dma("pool_dma", dst_g, src_cols.rearrange("(kc p) n -> p kc n", p=128), (), (K("scr", name, g),), ("cast", name, g))
            return (K("scr", name, g),)

        self.act(tmpf[:, 0:40], cT[:], AF.Tanh, (K("cT"),), (K("tmpf"),), scale=0.5)
        self.ts("dve", tmpf[:, 0:40], tmpf[:, 0:40], 0.5, 0.5, ALU.mult, ALU.add, (K("tmpf"),), (K("tmpf"),))
        self.tt("dve", siluT[:], tmpf[:, 0:40], cT[:], ALU.mult, (K("tmpf"), K("cT")), (K("siluT"),))
        siluv = siluT[:].rearrange("p (s k) -> p k s", k=8)
        modkeys = {}
        k8 = lambda n: tuple(K(n, kc) for kc in range(8))

        def ada_groups(glist):
            for g in glist:
                sl = g % 2
                buf = actT[:, sl * 8:(sl + 1) * 8, :].rearrange("p c n -> p (c n)")
                bkeys = tuple(K("actT", sl * 8 + c) for c in range(8))
                self.dma("pool_dma", buf.rearrange("p (kc n) -> p kc n", kc=8),
                         w_ada.rearrange("(kc p) n -> p kc n", p=128)[:, :, g * 512:(g + 1) * 512], (), bkeys, ("ada", sl))
                bk, bkey = self.bank()
                for oc in range(4):
                    for kc in range(8):
                        self.mm(bk[:, oc * 8:oc * 8 + 5], buf[:, kc * 512 + oc * 128: kc * 512 + (oc + 1) * 128], siluv[:, kc, :],
                                kc == 0, kc == 7, bkeys + (K("siluT"),), (bkey,))
                for oc in range(4):
                    ch = g * 4 + oc
                    self.ts("dve", mod[:, ch // 8, ch % 8, :], bk[:, oc * 8:oc * 8 + 5], badaT[:, ch:ch + 1], None, ALU.add, None,
                            (bkey, K("badaT")), (K("mod", ch),))

        ada_groups(range(0, 4))
        mk1 = tuple(K("mod", ch) for ch in range(16))
        for kc in range(8):
            self.ts("dve", Am[:, kc, :], mod[:, 1, kc, :], 1.0, gT[:, kc:kc + 1], ALU.add, ALU.mult, mk1 + (K("gT"),), (K("Am", kc),))
        self.cp("dve", Bm[:], mod[:, 0, :, :], mk1, (K("Bm"),))
        self.ts("dve", AmP[:], Am[:, :, 0], flag[:, 0:1], None, ALU.mult, None, k8("Am") + (K("flag"),), (K("AmP"),))
        self.ts("dve", BmP[:], Bm[:, :, 0], flag[:, 0:1], None, ALU.mult, None, (K("Bm"), K("flag")), (K("BmP"),))
        modkeys.update({"Am": k8("Am"), "Bm": (K("Bm"),), "AmP": (K("AmP"),), "BmP": (K("BmP"),)})

        def ada_part2():
            ada_groups(range(4, 12))
            mk2 = tuple(K("mod", ch) for ch in range(16, 48))
            for kc in range(8):
                rw = mk2 + (K("gT"),)
                self.ts("dve", Af[:, kc, :], mod[:, 4, kc, :], 1.0, gT[:, 16 + kc:17 + kc], ALU.add, ALU.mult, rw, (K("Af", kc),))
                self.ts("dve", Gm[:, kc, :], mod[:, 2, kc, :], gT[:, 8 + kc:9 + kc], None, ALU.mult, None, rw, (K("Gm", kc),))
                self.ts("dve", Gf[:, kc, :], mod[:, 5, kc, :], gT[:, 24 + kc:25 + kc], None, ALU.mult, None, rw, (K("Gf", kc),))
            self.cp("dve", Bf[:], mod[:, 3, :, :], mk2, (K("Bf"),))
            self.ts("dve", AfP[:], Af[:, :, 0], flag[:, 0:1], None, ALU.mult, None, k8("Af") + (K("flag"),), (K("AfP"),))
            self.ts("dve", BfP[:], Bf[:, :, 0], flag[:, 0:1], None, ALU.mult, None, (K("Bf"), K("flag")), (K("BfP"),))
        modkeys.update({"Af": k8("Af"), "Gm": k8("Gm"), "Gf": k8("Gf"), "Bf": (K("Bf"),), "AfP": (K("AfP"),), "BfP": (K("BfP"),)})

        kwin = {}
        for g in (3, 4, 1, 2):
            kwin[g] = cast_group(w_in[:, g * 512:(g + 1) * 512], win_g[g], "win", g)
        self.dma("pool_dma", wgk_g, w_in.rearrange("(kc p) n -> p kc n", p=128)[:, :, 5120:5136], (), (K("scr", "wgk"),), ("cast", "wgk"))
        kwgk = (K("scr", "wgk"),)
        for g in (0, 5, 6, 7, 8, 9):
            kwin[g] = cast_group(w_in[:, g * 512:(g + 1) * 512], win_g[g], "win", g)
        kwbra = [cast_group(w_br_a[:, g * 512:(g + 1) * 512], wbra_g[g], "wbra", g) for g in range(2)]
        kwbrb = [cast_group(w_br_b[:, g * 512:(g + 1) * 512], wbrb_g[g], "wbrb", g) for g in range(2)]
        kwout = [cast_group(w_out[:, g * 512:(g + 1) * 512], wout_g[g], "wout", g) for g in range(2)]
        kwup = [cast_group(w_up[:, g * 512:(g + 1) * 512], wup_g[g], "wup", g) for g in range(11)]
        kwdown = [cast_group(w_down[:, c * 128:(c + 1) * 128], wdown_g[c], "wdown", c) for c in range(8)]

        def load_xT(src_rows, ntile):
            for t in range(ntile):
                xb = xin[t % 2]
                kx = K("xin", t % 2)
                self.dma("sp", xb[:], src_rows[t * 128:(t + 1) * 128, :], (), (kx, K("xinv", t % 2)), ("xin", t % 2))
                for half in range(2):
                    bk, bkey = self.bank()
                    for j in range(4):
                        kc = half * 4 + j
                        self.tr(bk[:, j * 128:(j + 1) * 128], xb[:, kc * 128:(kc + 1) * 128], ident_f[:],
                                (kx if half == 0 else K("xinv", t % 2), K("ident_f")), (bkey,))
                    self.cp("act", xT[:, half * 4:half * 4 + 4, t * 128:(t + 1) * 128],
                            bk[:].rearrange("p (j n) -> p j n", j=4), (bkey,), tuple(K("xT", half * 4 + j, t) for j in range(4)))

        def rms_bcast(srcT, n, src_keys_fn):
            bk, bkey = self.bank()
            for kc in range(8):
                if kc % 2 == 0:
                    self.act(sq[:, 0, 0:n], srcT[:, kc, 0:n], AF.Square, src_keys_fn(kc), (K("sq", 0),))
                else:
                    self.tt("dve", sq[:, 1, 0:n], srcT[:, kc, 0:n], srcT[:, kc, 0:n], ALU.mult, src_keys_fn(kc), (K("sq", 1),))
                self.mm(bk[:, 0:n], ones_b[:], sq[:, kc % 2, 0:n], kc == 0, kc == 7, (K("ones_b"), K("sq", kc % 2)), (bkey,))
            self.act(tmpf[:, 0:n], bk[:, 0:n], AF.Ln, (bkey, K("epsb")), (K("tmpf"),), bias=epsb[:, 0:1], scale=1.0 / D)
            self.act(rbc[:, 0:n], tmpf[:, 0:n], AF.Exp, (K("tmpf"),), (K("rbc"),), scale=-0.5)

        def modulate(n, ntile, Asc, Bsc, Afn, Bfn):
            for kc in range(8):
                xk = tuple(K("xT", kc, t) for t in range(ntile))
                self.tt("dve", tmpf2[:, 0:n], xT[:, kc, 0:n], rbc[:, 0:n], ALU.mult, xk + (K("rbc"),), (K("tmpf2"),))
                self.act(hT[:, kc, 0:n], tmpf2[:, 0:n], AF.Identity, (K("tmpf2"),) + modkeys[Asc] + modkeys[Bsc], (K("hT", kc),),
                         bias=Bfn(kc), scale=Afn(kc))

        def modulate_seq(At, Bt, Asc, Bsc):
            for kc in range(8):
                xk = (K("xT", kc, 0),)
                v3 = lambda ap: ap.rearrange("p (s w) -> p s w", s=4)
                self.tt("dve", tmpf2[:, 0:128], xT[:, kc, 0:128], rbc[:, 0:128], ALU.mult, xk + (K("rbc"),), (K("tmpf2"),))
                self.tt("dve", v3(tmpf2[:, 0:128]), v3(tmpf2[:, 0:128]), At[:, kc, 1:5].to_broadcast([128, 4, 32]), ALU.mult,
                        (K("tmpf2"),) + modkeys[Asc], (K("tmpf2"),))
                self.tt("dve", v3(hT[:, kc, 0:128]), v3(tmpf2[:, 0:128]), Bt[:, kc, 1:5].to_broadcast([128, 4, 32]), ALU.add,
                        (K("tmpf2"),) + modkeys[Bsc], (K("hT", kc),))

        xkeys = lambda ntile: (lambda kc: tuple(K("xT", kc, t) for t in range(ntile)))

        def heads_fm(wg, wk, col0, nheads, n, evac):
            for h in range(nheads):
                bk, bkey = self.bank()
                for kc in range(8):
                    self.mm(bk[0:64, 0:n], wg(kc, col0 + h * 64, col0 + (h + 1) * 64), hT[:, kc, 0:n], kc == 0, kc == 7, (wk, K("hT", kc)), (bkey,))
                evac(h, bk, bkey)

        def pairs_fm(wg, wk, n, evac):
            for c in range(4):
                bk, bkey = self.bank()
                for kc in range(8):
                    self.mm(bk[:, 0:n], wg(kc, c * 128, (c + 1) * 128), hT[:, kc, 0:n], kc == 0, kc == 7, (wk, K("hT", kc)), (bkey,))
                evac(c, bk, bkey)

        def units_tm(wg, wk, col0, ncols, units, evac):
            for ui, (c0, C) in enumerate(units):
                bk, bkey = self.bank()
                for kc in range(8):
                    self.mm(bk[0:C, 0:ncols], hT[:, kc, c0:c0 + C], wg(kc, col0, col0 + ncols), kc == 0, kc == 7, (wk, K("hT", kc)), (bkey,))
                evac(ui, bk, bkey)

        def gla_stageA(ui, col0, C, pre, W):
            p = ui % 2
            wgk_, wkk, wgv_, wkv, wgg_, wkg = W
            vb_tok, gb2, qtT, kend, dec, attT = vb_tok2[p], gb2_2[p], qtT2[p], kend2[p], dec2[p], attT2[p]
            bk, bkey = self.bank()
            for kc in range(8):
                self.mm(bk[0:C, 0:256], hT[:, kc, col0:col0 + C], wgk_(kc, 256, 512), kc == 0, kc == 7, (wkk, K("hT", kc)), (bkey,))
            self.cp("act", kb_tok[0:C, :], bk[0:C, 0:256], (bkey,), (K("kb_tok"),))
            bk, bkey = self.bank()
            for kc in range(8):
                self.mm(bk[0:C, :], hT[:, kc, col0:col0 + C], wgv_(kc), kc == 0, kc == 7, (wkv, K("hT", kc)), (bkey,))
            self.cp("act", vb_tok[0:C, :, :].rearrange("p h e -> p (h e)"), bk[0:C, :], (bkey,), (K("vb_tok", p),))
            if not pre:
                bk, bkey = self.bank()
                for kc in range(8):
                    self.mm(bk[0:C, :], hT[:, kc, col0:col0 + C], wgg_(kc), kc == 0, kc == 7, (wkg, K("hT", kc)), (bkey,))
                self.cp("act", gb2[0:C, :], bk[0:C, :], (bkey,), (K("gb2", p),))
            bk, bkey = self.bank()
            self.mm(bk[0:C, 0:256], gkT[0:17, col0:col0 + C], wgk_b[0:17, :], True, True, (K("gkT"), K("gkT_ones"), K("wgk_b")), (bkey,))
            self.act(e_sb[0:C, :], bk[0:C, 0:256], AF.Exp, (bkey,), (K("e_sb"),), scale=-1.0)
            self.act(Lsp[0:C, :], e_sb[0:C, :], AF.Ln, (K("e_sb"),), (K("Lsp"),), bias=1.0)
            bk2, bkey2 = self.bank()
            self.mm(bk2[0:C, 0:256], trisl_f[0:C, 0:C], Lsp[0:C, :], True, True, (K("trisl_f"), K("Lsp")), (bkey2,))
            self.act(e_sb[0:C, :], bk2[0:C, 0:256], AF.Exp, (bkey2,), (K("e_sb"),), scale=-1.0 / 16)
            self.tt("dve", kend[0:C, :], kb_tok[0:C, :], e_sb[0:C, :], ALU.mult, (K("kb_tok"), K("e_sb")), (K("kend", p),))
            bk3, bkey3 = self.bank()
            for h in range(4):
                self.mm(bk3[0:64, h * 128:h * 128 + C], Lsp[0:C, h * 64:(h + 1) * 64], triu_f[0:C, 0:C], True, True,
                        (K("Lsp"), K("triu_f")), (bkey3,))
            b3 = bk3[0:64, :].rearrange("p (c t) -> p c t", c=4)
            self.act(dec[:, :], b3[:, :, C - 1], AF.Exp, (bkey3,), (K("dec", p),), scale=-1.0 / 16)
            if pre:
                return
            self.act(e1[:, :, 0:C], b3[:, :, 0:C], AF.Exp, (bkey3,), (K("e1"),), scale=-1.0 / 16)
            self.act(e2[:, :, 0:C], b3[:, :, 0:C], AF.Exp, (bkey3,), (K("e2"),), scale=1.0 / 16)
            self.stt("dve", qtT[:, :, 0:C], qbT[:, :, col0:col0 + C], 0.125, e1[:, :, 0:C], ALU.mult, ALU.mult,
                     tuple(K("qbT", j) for j in range(4)) + (K("e1"),), (K("qtT", p),))
            self.tt("dve", ktT[:, :, 0:C], kbT[:, :, col0:col0 + C], e2[:, :, 0:C], ALU.mult, tuple(K("kbT", j) for j in range(4)) + (K("e2"),), (K("ktT"),))
            bk4, bkey4 = self.bank()
            for h in range(4):
                self.mm(bk4[0:C, h * 128:h * 128 + C], ktT[:, h, 0:C], qtT[:, h, 0:C], True, True, (K("ktT"), K("qtT", p)), (bkey4,))
            for h in range(4):
                self.tt("dve", attT[0:C, h, 0:C], bk4[0:C, h * 128:h * 128 + C], triu_f[0:C, 0:C], ALU.mult,
                        (bkey4, K("triu_f")), (K("attT", p, h),))

        def gla_stageB(ui, col0, C, pre, S_t, S_b, Skey):
            p = ui % 2
            vb_tok, gb2, qtT, kend, dec, attT = vb_tok2[p], gb2_2[p], qtT2[p], kend2[p], dec2[p], attT2[p]
            if not pre:
                obk, obkey = self.bank()
                for h in range(4):
                    self.mm(obk[0:C, h * 128:(h + 1) * 128], attT[0:C, h, 0:C], vb_tok[0:C, h, :], True, False, (K("attT", p, h), K("vb_tok", p)), (obkey,))
                    self.mm(obk[0:C, h * 128:(h + 1) * 128], qtT[:, h, 0:C], S_b[:, h, :], False, True,
                            (K("qtT", p), Skey + ("b",)), (obkey,))
                self.memset("dve", ssq[0:C, :], 0.0, tuple(K("ssq", h) for h in range(4)))
                for h in range(4):
                    self.act(attT[0:C, h, :], obk[0:C, h * 128:(h + 1) * 128], AF.Square, (obkey,), (K("attT", p, h), K("ssq", h)), accum_out=ssq[0:C, h:h + 1])
                self.act(rgl[0:C, :], ssq[0:C, :], AF.Ln, tuple(K("ssq", h) for h in range(4)) + (K("epsb"),), (K("rgl"),), bias=epsb[0:C, 0:1], scale=1.0 / 128)
                self.act(rgl[0:C, :], rgl[0:C, :], AF.Exp, (K("rgl"),), (K("rgl"),), scale=-0.5)
                self.act(gtanh[0:C, :], gb2[0:C, :], AF.Tanh, (K("gb2", p),), (K("gtanh"),), scale=0.5)
                self.stt("dve", gtanh[0:C, :], gtanh[0:C, :], 1.0, gb2[0:C, :], ALU.add, ALU.mult, (K("gtanh"), K("gb2", p)), (K("gtanh"),))
                self.tt("dve", gtanh[0:C, :], gtanh[0:C, :], ggla_s[0:C, :], ALU.mult, (K("gtanh"), K("ggla")), (K("gtanh"),))
                for h in range(4):
                    self.stt("dve", yb_tok[0:C, h * 128:(h + 1) * 128], obk[0:C, h * 128:(h + 1) * 128], rgl[0:C, h:h + 1],
                             gtanh[0:C, h * 128:(h + 1) * 128], ALU.mult, ALU.mult, (obkey, K("rgl"), K("gtanh")), (K("yb_tok", h),))
                for c in range(4):
                    self.tr(pbf[:, 512 + c * 128:512 + c * 128 + C], yb_tok[0:C, c * 128:(c + 1) * 128], ident_b[0:C, 0:C],
                            (K("yb_tok", c), K("ident_b")), (K("pbf"),))
                for c in range(4):
                    self.cp("act", ybT(c)[:, col0:col0 + C], pbf[:, 512 + c * 128:512 + c * 128 + C], (K("pbf"),), (YBK[c],))
            bk6, bkey6 = self.bank()
            for h in range(4):
                self.mm(bk6[0:64, h * 128:(h + 1) * 128], kend[0:C, h * 64:(h + 1) * 64], vb_tok[0:C, h, :], True, True, (K("kend", p), K("vb_tok", p)), (bkey6,))
            for h in range(4):
                self.stt("dve", S_t[:, h, :], S_t[:, h, :], dec[:, h:h + 1], bk6[0:64, h * 128:(h + 1) * 128],
                         ALU.mult, ALU.add, (Skey + (h,), K("dec", p), bkey6), (Skey + (h,),))
            self.cp("act", S_b[:], S_t[:], tuple(Skey + (h,) for h in range(4)), (Skey + ("b",),))

        def gla_units(units, pre, S_of, with_fn=None):
            wgk_, wkk = self.wload(win_g[G_QKB], kwin[G_QKB], 8, 512)
            wgv_, wkv = self.wload(win_g[G_VB], kwin[G_VB], 8, 512)
            wgg_, wkg = (None, None) if pre else self.wload(win_g[G_GB], kwin[G_GB], 8, 512)
            W = (wgk_, wkk, wgv_, wkv, wgg_, wkg)

            def body():
                gla_stageA(0, units[0][0], units[0][1], pre, W)
                for ui, (col0, C) in enumerate(units):
                    if ui + 1 < len(units):
                        gla_stageA(ui + 1, units[ui + 1][0], units[ui + 1][1], pre, W)
                    S_t, S_b, Skey, before_fn, after_fn = S_of(ui)
                    if before_fn is not None:
                        before_fn()
                    gla_stageB(ui, col0, C, pre, S_t, S_b, Skey)
                    if after_fn is not None:
                        after_fn()
            if with_fn is None:
                body()
            else:
                self.interleave([(body, [0, 1, 2]), (with_fn, [3, 4])])

        def attn_tile(tglob, tl):
            pTv = lambda kb, par: pT[:, kb, :, :].rearrange("p (c two) q -> p two c q", two=2)[:, par]
            for kb in range(5):
                slot = (tglob - 4 + kb) % 8
                banks = [self.bank(), self.bank()]
                for c in range(4):
                    for par in range(2):
                        pb = par * 64
                        bk, bkey = banks[par]
                        self.mm(bk[:, c * 128:(c + 1) * 128], kaT[slot][pb:pb + 64, c, :], qaT[pb:pb + 64, c, tl * 128:(tl + 1) * 128],
                                True, True, (K("kaT", slot, c), K("qaT", c)), (bkey,))
                for par in range(2):
                    bk, bkey = banks[par]
                    self.act(pTv(kb, par), bk[:].rearrange("p (c q) -> p c q", c=4), AF.Exp, (bkey,), (K("pT", kb, par),))
                tab = {0: (tab_mask[:, :].to_broadcast([128, 8, 128]) if False else None), 3: tab_prev, 4: tab_own}.get(kb)
                pk = (K("pT", kb, 0), K("pT", kb, 1))
                if kb == 0:
                    for h0 in range(0, 8, 4):
                        pass
                    self.tt("dve", pT[:, 0, :, :], pT[:, 0, :, :], bass.AP(tab_mask, 0, [[128, 128], [0, 8], [1, 128]]), ALU.mult,
                            pk + (K("tab_mask"),), pk)
                elif tab is not None:
                    self.tt("dve", pT[:, kb, :, :], pT[:, kb, :, :], tab[:, :, :], ALU.mult, pk + (K("etab", kb - 3),), pk)
            for hg in range(2):
                ob, obkey = obank[hg], K("obank", hg)
                for hh in range(4):
                    h = hg * 4 + hh
                    for kb in range(5):
                        slot = (tglob - 4 + kb) % 8
                        self.mm(ob[:, hh * 65:(hh + 1) * 65], pT[:, kb, h, :], vaug[slot][:, h, :], kb == 0, kb == 4,
                                (K("pT", kb, h % 2), K("vaug", slot), K("vaug1", slot)), (obkey,))
            for hg in range(2):
                ob, obkey = obank[hg], K("obank", hg)
                ov = ob[:, 0:260].rearrange("p (h e) -> p h e", h=4)
                self.ts("dve", rden[:, hg * 4:(hg + 1) * 4], ov[:, :, 64], 1e-30, None, ALU.add, None, (obkey,), (K("rden", hg),))
                S.add("dve", lambda e, hg=hg: e.reciprocal(rden[:, hg * 4:(hg + 1) * 4], rden[:, hg * 4:(hg + 1) * 4]), (K("rden", hg),), (K("rden", hg),))
                for hh in range(4):
                    h = hg * 4 + hh
                    self.ts("dve", ya_tok[:, h * 64:(h + 1) * 64], ov[:, hh, 0:64], rden[:, h:h + 1], None, ALU.mult, None,
                            (obkey, K("rden", hg)), (K("ya_tok", h),))
            yk = tuple(K("ya_tok", h) for h in range(8))
            for c in range(4):
                self.tr(pbf[:, c * 128:(c + 1) * 128], ya_tok[:, c * 128:(c + 1) * 128], ident_b[:], yk + (K("ident_b"),), (K("pbf"),))
            for c in range(4):
                self.cp("act", yaT(c)[:, tl * 128:(tl + 1) * 128], pbf[:, c * 128:(c + 1) * 128], (K("pbf"),), (YAK[c],))

        def attn_sample(s, slot, qbase=0):
            for rb in range(4):
                xb = xin[rb % 2]
                kx = K("xin", rb % 2)
                self.dma("sp", xb[:, 0:512], ck[s, rb * 128:(rb + 1) * 128, :], (), (kx,), ("xin", rb % 2))
                bk, bkey = self.bank()
                for c in range(4):
                    self.tr(bk[:, c * 128:(c + 1) * 128], xb[:, c * 128:(c + 1) * 128], ident_f[:], (kx, K("ident_f")), (bkey,))
                for c in range(4):
                    self.cp("act", kcT(c)[:, rb * 128:(rb + 1) * 128], bk[:, c * 128:(c + 1) * 128], (bkey,), (K("m2", c),))
                self.dma("sp", xb[:, 512:1024], cv[s, rb * 128:(rb + 1) * 128, :], (), (K("xinv", rb % 2),), ("xinv", rb % 2))
                self.cp("dve", vcaug[:, rb, :, 0:64], xb[:, 512:1024].rearrange("p (h e) -> p h e", h=8), (K("xinv", rb % 2),), (K("vcaug", rb),))
                self.cp("dve", vcaug[:, rb, :, 64], ones_f[:, 0:8], (K("ones_f"),), (K("vcaug1", rb),))
            q0 = qbase + s * 32
            l0 = s * 32
            for kb in range(5):
                kn = 128 if kb < 4 else 32
                banks = [self.bank(), self.bank()]
                for c in range(4):
                    for par in range(2):
                        pb = par * 64
                        bk, bkey = banks[par]
                        if kb < 4:
                            lhs = kcT(c)[pb:pb + 64, kb * 128:(kb + 1) * 128]
                            rk = (K("m2", c),)
                        else:
                            lhs = kaT[slot][pb:pb + 64, c, l0:l0 + 32]
                            rk = (K("kaT", slot, c),)
                        self.mm(bk[0:kn, c * 32:(c + 1) * 32], lhs, qaT[pb:pb + 64, c, q0:q0 + 32], True, True, rk + (K("qaT", c),), (bkey,))
                for par in range(2):
                    bk, bkey = banks[par]
                    dstv = pT[0:kn, kb, :, 0:32].rearrange("p (c two) q -> p two c q", two=2)[:, par]
                    self.act(dstv, bk[0:kn, 0:128].rearrange("p (c q) -> p c q", c=4), AF.Exp, (bkey,), (K("pT", kb, par),))
                pk = (K("pT", kb, 0), K("pT", kb, 1))
                if kb == 3:
                    self.tt("dve", pT[:, 3, :, 0:32], pT[:, 3, :, 0:32], tab_prev[:, :, 0:32], ALU.mult, pk + (K("etab", 0),), pk)
                elif kb == 4:
                    self.tt("dve", pT[0:32, 4, :, 0:32], pT[0:32, 4, :, 0:32], tab_own[0:32, :, 0:32], ALU.mult, pk + (K("etab", 1),), pk)
            for h in range(8):
                hg, hh = h // 4, h % 4
                for kb in range(5):
                    kn = 128 if kb < 4 else 32
                    rhs = vcaug[:, kb, h, :] if kb < 4 else vown[0:32, s, h, :]
                    rk = (K("vcaug", kb), K("vcaug1", kb)) if kb < 4 else (K("vown", s), K("vown1", s))
                    self.mm(obank[hg][0:32, hh * 65:(hh + 1) * 65], pT[0:kn, kb, h, 0:32], rhs, kb == 0, kb == 4,
                            (K("pT", kb, h % 2),) + rk, (K("obank", hg),))
            for hg in range(2):
                ob, obkey = obank[hg], K("obank", hg)
                ov = ob[0:32, 0:260].rearrange("p (h e) -> p h e", h=4)
                S.add("dve", lambda e, ov=ov, hg=hg: e.reciprocal(rden[0:32, hg * 4:(hg + 1) * 4], ov[:, :, 64]), (obkey,), (K("rden", hg),))
                for hh in range(4):
                    h = hg * 4 + hh
                    self.ts("dve", ya_tok[0:32, h * 64:(h + 1) * 64], ov[:, hh, 0:64], rden[0:32, h:h + 1], None, ALU.mult, None,
                            (obkey, K("rden", hg)), (K("ya_tok", h),))
            yk = tuple(K("ya_tok", h) for h in range(8))
            for c in range(4):
                self.tr(pbf[:, c * 128:c * 128 + 32], ya_tok[0:32, c * 128:(c + 1) * 128], ident_b[0:32, 0:32], yk + (K("ident_b"),), (K("pbf"),))
            for c in range(4):
                self.cp("act", yaT(c)[:, q0:q0 + 32], pbf[:, c * 128:c * 128 + 32], (K("pbf"),), (YAK[c],))

        def resid(n, ntile, Gfn, gname):
            for kc in range(8):
                tb, tk = (tmpf2, K("tmpf2")) if kc % 2 == 0 else (tmpf, K("tmpf"))
                self.tt("pool", tb[:, 0:n], m2[:, kc, 0:n], rbc[:, 0:n], ALU.mult, (K("m2", kc), K("rbc")), (tk,))
                xk = tuple(K("xT", kc, t) for t in range(ntile))
                self.stt("dve", xT[:, kc, 0:n], tb[:, 0:n], Gfn(kc), xT[:, kc, 0:n], ALU.mult, ALU.add,
                         (tk,) + modkeys[gname] + xk, xk)

        def resid_seq(Gt, gname):
            v3 = lambda ap: ap.rearrange("p (s w) -> p s w", s=4)
            for kc in range(8):
                xk = (K("xT", kc, 0),)
                self.tt("dve", tmpf2[:, 0:128], m2[:, kc, 0:128], rbc[:, 0:128], ALU.mult, (K("m2", kc), K("rbc")), (K("tmpf2"),))
                self.tt("dve", v3(tmpf2[:, 0:128]), v3(tmpf2[:, 0:128]), Gt[:, kc, 1:5].to_broadcast([128, 4, 32]), ALU.mult,
                        (K("tmpf2"),) + modkeys[gname], (K("tmpf2"),))
                self.tt("dve", xT[:, kc, 0:128], xT[:, kc, 0:128], tmpf2[:, 0:128], ALU.add, (K("tmpf2"),) + xk, xk)

        def merge_and_out(n):
            def ev_gate(dst, dkey):
                def f(bk, bkey):
                    self.act(dst[:, 0:n], bk[:, 0:n], AF.Tanh, (bkey,), (dkey,), scale=0.5)
                    self.ts("dve", dst[:, 0:n], dst[:, 0:n], 0.5, 0.5, ALU.mult, ALU.add, (dkey,), (dkey,))
                return f
            for G in range(2):
                wga, wka = self.wload(win_g[G_GA + G], kwin[G_GA + G], 8, 512)
                wba, wkba = self.wload(wbra_g[G], kwbra[G], 4, 512)
                for j in range(4):
                    c = G * 4 + j
                    bk, bkey = self.bank()
                    for kc in range(8):
                        self.mm(bk[:, 0:n], wga(kc, j * 128, (j + 1) * 128), hT[:, kc, 0:n], kc == 0, kc == 7, (wka, K("hT", kc)), (bkey,))
                    ev_gate(sgA, K("sgA"))(bk, bkey)
                    bk, bkey = self.bank()
                    for kc in range(4):
                        self.mm(bk[:, 0:n], wba(kc, j * 128, (j + 1) * 128), yaT(kc)[:, 0:n], kc == 0, kc == 3, (wkba, YAK[kc]), (bkey,))
                    self.tt("dve", m2[:, c, 0:n], sgA[:, 0:n], bk[:, 0:n], ALU.mult, (K("sgA"), bkey), (K("m2", c),))
                wgb, wkb = self.wload(win_g[G_GBR + G], kwin[G_GBR + G], 8, 512)
                wbb, wkbb = self.wload(wbrb_g[G], kwbrb[G], 4, 512)
                for j in range(4):
                    c = G * 4 + j
                    bk, bkey = self.bank()
                    for kc in range(8):
                        self.mm(bk[:, 0:n], wgb(kc, j * 128, (j + 1) * 128), hT[:, kc, 0:n], kc == 0, kc == 7, (wkb, K("hT", kc)), (bkey,))
                    ev_gate(sgB, K("sgB"))(bk, bkey)
                    bk, bkey = self.bank()
                    for kc in range(4):
                        self.mm(bk[:, 0:n], wbb(kc, j * 128, (j + 1) * 128), ybT(kc)[:, 0:n], kc == 0, kc == 3, (wkbb, YBK[kc]), (bkey,))
                    self.tt("dve", sgB[:, 0:n], sgB[:, 0:n], bk[:, 0:n], ALU.mult, (K("sgB"), bkey), (K("sgB"),))
                    self.tt("dve", mergedT(c)[:, 0:n], sgB[:, 0:n], m2[:, c, 0:n], ALU.add, (K("sgB"), K("m2", c)), (MK(c),))
            for G in range(2):
                wg, wk = self.wload(wout_g[G], kwout[G], 8, 512)
                for j in range(4):
                    c = G * 4 + j
                    bk, bkey = self.bank()
                    for kc in range(8):
                        self.mm(bk[:, 0:n], wg(kc, j * 128, (j + 1) * 128), mergedT(kc)[:, 0:n], kc == 0, kc == 7, (wk, MK(kc)), (bkey,))
                    self.cp("act", m2[:, c, 0:n], bk[:, 0:n], (bkey,), (K("m2", c),))
            rms_bcast(m2, n, lambda kc: (K("m2", kc),))

        ALIAS_KEYS = tuple(K("pT", kb, par) for kb in range(5) for par in range(2)) + \
            tuple(K(nm, 1, h) for nm in ("ua", "uah", "y0") for h in range(2))

        def alias_fence():
            self.memset("dve", pT_raw[:, 2559:2560], 0.0, ALIAS_KEYS)

        def ffn(n, groups):
            alias_fence()
            for g in range(11):
                wg, wk = self.wload(wup_g[g], kwup[g], 8, 512)
                for pair in range(2):
                    ua_, y0_ = ua_sets[pair], y0_sets[pair]
                    for half in range(2):
                        jc = half * 2 + pair
                        jj = half * NFF + 2 * g + pair
                        bk, bkey = self.bank()
                        for kc in range(8):
                            self.mm(bk[:, 0:n], wg(kc, jc * 128, (jc + 1) * 128), hT[:, kc, 0:n], kc == 0, kc == 7, (wk, K("hT", kc)), (bkey,))
                        uk, uh, yk = K("ua", pair, half), K("uah", pair, half), K("y0", pair, half)
                        self.act(y0_[half][:, 0:n], bk[:, 0:n], AF.Identity, (bkey, K("wdwT2"), K("bdwT")), (yk,),
                                 bias=bdwT[:, jj:jj + 1], scale=wdwT[:, 2, jj:jj + 1])
                        off = 0
                        for (c0, ncols, nseg, hview, hkeyf, extra, scale_ap) in groups:
                            w = ncols // nseg
                            v3 = lambda ap, nseg=nseg: ap.rearrange("p (s w) -> p s w", s=nseg)
                            u3 = v3(ua_[half][:, off:off + nseg * (w + 2)])
                            off += nseg * (w + 2)
                            yv = v3(y0_[half][:, c0:c0 + ncols])
                            self.cp("pool", u3[:, :, 0:2], hview(jj), (hkeyf(jj),) + extra, (uh,))
                            if scale_ap is None:
                                self.cp("act", u3[:, :, 2:2 + w], v3(bk[:, c0:c0 + ncols]), (bkey,), (uk,))
                            else:
                                self.act(u3[:, :, 2:2 + w], v3(bk[:, c0:c0 + ncols]), AF.Identity, (bkey, K("flag")), (uk,), scale=scale_ap)
                            self.stt("dve", yv, u3[:, :, 1:1 + w], wdwT[:, 1, jj:jj + 1], yv, ALU.mult, ALU.add, (uk, uh, yk, K("wdwT1")), (yk,))
                            self.stt("dve", yv, u3[:, :, 0:w], wdwT[:, 0, jj:jj + 1], yv, ALU.mult, ALU.add, (uk, uh, yk, K("wdwT0")), (yk,))
                            self.cp("pool", hview(jj), u3[:, :, w:w + 2], (uk,), (hkeyf(jj),))
                    j = 2 * g + pair
                    self.act(y0_[0][:, 0:n], y0_[0][:, 0:n], AF.Gelu_apprx_tanh, (K("y0", pair, 0),), (K("y0", pair, 0),))
                    self.tt("dve", actT[:, j, 0:n], y0_[0][:, 0:n], y0_[1][:, 0:n], ALU.mult, (K("y0", pair, 0), K("y0", pair, 1)), (K("actT", j),))
            for c in range(8):
                wg, wk = self.wload(wdown_g[c], kwdown[c], NFF, 128)
                bk, bkey = self.bank()
                for j in range(NFF):
                    self.mm(bk[:, 0:n], wg(j), actT[:, j, 0:n], j == 0, j == NFF - 1, (wk, K("actT", j)), (bkey,))
                self.cp("act", m2[:, c, 0:n], bk[:, 0:n], (bkey,), (K("m2", c),))
            rms_bcast(m2, n, lambda kc: (K("m2", kc),))
            alias_fence()

        def store_y(dst_rows, ntile):
            for t in range(ntile):
                yb_ = xin[t % 2]
                ky = K("xin", t % 2)
                for half in range(2):
                    bk, bkey = self.bank()
                    for j in range(4):
                        kc = half * 4 + j
                        self.tr(bk[:, j * 128:(j + 1) * 128], xT[:, kc, t * 128:(t + 1) * 128], ident_f[:], (K("xT", kc, t), K("ident_f")), (bkey,))
                    self.cp("act", yb_[:, half * 512:(half + 1) * 512], bk[:], (bkey,), (K("xinv", t % 2),) if half else (ky,))
                self.dma("sp", dst_rows[t * 128:(t + 1) * 128, :], yb_[:], (ky, K("xinv", t % 2)), (), ("xin", t % 2))

        def load_dma(tile_ap):
            self.dma("sp", xin[0][:], tile_ap, (), (K("xin", 0), K("xinv", 0)), ("xin", 0))

        def prefetch_x(nxt_tiles):
            if not nxt_tiles or self.prefetched:
                return
            load_dma(nxt_tiles[0])
            self.prefetched = True

        def load_tile(nxt_tiles, t):
            xb = xin[0]
            kx, kv = K("xin", 0), K("xinv", 0)
            for half in range(2):
                bk, bkey = self.bank()
                for j in range(4):
                    kc = half * 4 + j
                    self.tr(bk[:, j * 128:(j + 1) * 128], xb[:, kc * 128:(kc + 1) * 128], ident_f[:], (kx if half == 0 else kv, K("ident_f")), (bkey,))
                self.cp("act", xT[:, half * 4:half * 4 + 4, t * 128:(t + 1) * 128],
                        bk[:].rearrange("p (j n) -> p j n", j=4), (bkey,), tuple(K("xT", half * 4 + j, t) for j in range(4)))
            if t + 1 < len(nxt_tiles):
                load_dma(nxt_tiles[t + 1])

        def store_tile(dst_ap, t):
            xb = xin[1]
            kx, kv = K("xin", 1), K("xinv", 1)
            for half in range(2):
                bk, bkey = self.bank()
                for j in range(4):
                    kc = half * 4 + j
                    self.tr(bk[:, j * 128:(j + 1) * 128], xT[:, kc, t * 128:(t + 1) * 128], ident_f[:], (K("xT", kc, t), K("ident_f")), (bkey,))
                self.cp("act", xb[:, half * 512:(half + 1) * 512], bk[:], (bkey,), (kv,) if half else (kx,))
            self.dma("sp", dst_ap, xb[:], (kx, kv), (), ("xin", 1))

        def swap_tiles(dst_tiles, nxt_tiles):
            dst_tiles = dst_tiles or []
            nxt_tiles = nxt_tiles or []
            prefetch_x(nxt_tiles)
            self.prefetched = False
            for t in range(max(len(dst_tiles), len(nxt_tiles))):
                if t < len(dst_tiles):
                    store_tile(dst_tiles[t], t)
                if t < len(nxt_tiles):
                    load_tile(nxt_tiles, t)

        def store_hist(hsrc, hkeys, dst):
            self.cp("dve", h88[:], hsrc.rearrange("p j t -> p t j"), hkeys, (K("h88"),))
            bk, bkey = self.bank()
            self.tr(bk[0:88, 0:128], h88[:].rearrange("p t j -> p (t j)"), ident_f[:], (K("h88"), K("ident_f")), (bkey,))
            self.cp("act", stage[0:88, :], bk[0:88, 0:128], (bkey,), (K("stage"),))
            self.dma("sp", dst, stage[0:88, :], (K("stage"),), (), ("stage_o",))

        def kv_stage(bk, bkey, which, dst):
            st = xin[1][:, which * 512:(which + 1) * 512]
            sk = K("xin", 1) if which == 0 else K("xinv", 1)
            self.cp("act", st, bk[:], (bkey,), (sk,))
            self.dma("sp", dst, st, (sk,), (), ("xin", 1) if which == 0 else ("xinv", 1))

        LAST_KV0 = 16 + NMAIN - 4

        def common_proj(n, units, tglob0, kv_from, pre, flagged, kv_out):
            if not pre:
                wg, wk = self.wload(win_g[G_QA], kwin[G_QA], 8, 512)
                def ev_q(j, bk, bkey):
                    self.act(qaT[:, j, 0:n], bk[:, 0:n], AF.Identity, (bkey,), (K("qaT", j),), scale=0.125)
                pairs_fm(wg, wk, n, ev_q)
            if kv_from < len(units):
                wg, wk = self.wload(win_g[G_KA], kwin[G_KA], 8, 512)
                def ev_k(j, bk, bkey):
                    for ui in range(kv_from, len(units)):
                        c0, C = units[ui]
                        if C == 128:
                            slot = (tglob0 + ui) % 8
                            self.cp("act", kaT[slot][:, j, :], bk[:, c0:c0 + 128], (bkey,), (K("kaT", slot, j),))
                    if units[0][1] != 128:
                        self.cp("act", kaT[tglob0 % 8][:, j, :], bk[:, 0:128], (bkey,), (K("kaT", tglob0 % 8, j),))
                pairs_fm(wg, wk, n, ev_k)
                kunits = [(ui, units[ui]) for ui in range(kv_from, len(units)) if kv_out(ui, 0) is not None] if units[0][1] == 128 else []
                if kunits:
                    units_tm(wg, wk, 0, 512, [u for _, u in kunits], lambda i, bk, bkey: kv_stage(bk, bkey, 0, kv_out(kunits[i][0], 0)))
                if units[0][1] != 128:
                    units_tm(wg, wk, 0, 512, [(0, 128)], lambda i, bk, bkey: kv_stage(bk, bkey, 0, kv_out(0, 0)))
                wg, wk = self.wload(win_g[G_VA], kwin[G_VA], 8, 512)
                if units[0][1] == 128:
                    def ev_v(i, bk, bkey):
                        ui = kv_from + i
                        slot = (tglob0 + ui) % 8
                        self.cp("act", vaug[slot][:, :, 0:64], bk[:].rearrange("p (h e) -> p h e", h=8), (bkey,), (K("vaug", slot),))
                        if flagged:
                            self.ts("dve", vaug[slot][:, :, 64], ones_f[:, 0:8], flag[:, 0:1], None, ALU.mult, None,
                                    (K("ones_f"), K("flag")), (K("vaug1", slot),))
                        else:
                            self.cp("dve", vaug[slot][:, :, 64], ones_f[:, 0:8], (K("ones_f"),), (K("vaug1", slot),))
                        if kv_out(ui, 1) is not None:
                            kv_stage(bk, bkey, 1, kv_out(ui, 1))
                    units_tm(wg, wk, 0, 512, units[kv_from:], ev_v)
                else:
                    units_tm(wg, wk, 0, 512, [(0, 128)], lambda i, bk, bkey: kv_stage(bk, bkey, 1, kv_out(0, 1)))
            if not pre:
                wg, wk = self.wload(win_g[G_QKB], kwin[G_QKB], 8, 512)
                def ev_qb(j, bk, bkey):
                    self.cp("act", qbT[:, j, 0:n], bk[0:64, 0:n], (bkey,), (K("qbT", j),))
                def ev_kb(j, bk, bkey):
                    self.cp("act", kbT[:, j, 0:n], bk[0:64, 0:n], (bkey,), (K("kbT", j),))
                heads_fm(wg, wk, 0, 4, n, ev_qb)
                heads_fm(wg, wk, 256, 4, n, ev_kb)
            wg, wk = self.wload(wgk_g, kwgk, 8, 16)
            bk, bkey = self.bank()
            for kc in range(8):
                self.mm(bk[0:16, 0:n], wg(kc), hT[:, kc, 0:n], kc == 0, kc == 7, (wk, K("hT", kc)), (bkey,))
            self.cp("act", gkT[0:16, 0:n], bk[0:16, 0:n], (bkey,), (K("gkT"),))

        SK = K("Sst")
        def prompt_block(src_rows, ntile, tglob0, mode, dst_rows, kv_from, preloaded=False, nxt=None):
            n = ntile * 128
            pre = mode == "prefix"
            flagged = mode != "full"
            units = [(t * 128, 128) for t in range(ntile)]
            mark = lambda nm: self.marks.append((mode, tglob0, nm, len(S.ops)))
            mark("start")
            if not preloaded:
                swap_tiles(None, [src_rows[t * 128:(t + 1) * 128, :] for t in range(ntile)])
            rms_bcast(xT, n, xkeys(ntile))
            if flagged:
                modulate(n, ntile, "AmP", "BmP", lambda kc: AmP[:, kc:kc + 1], lambda kc: BmP[:, kc:kc + 1])
            else:
                modulate(n, ntile, "Am", "Bm", lambda kc: Am[:, kc, 0:1], lambda kc: Bm[:, kc, 0:1])
            if pre and nxt is not None:
                swap_tiles(None, nxt)
            mark("norm1")

            def kv_out(ui, which):
                tg = tglob0 + ui
                if mode != "full" or tg < LAST_KV0:
                    return None
                r0 = (tg - LAST_KV0) * 128
                return (kp_o if which == 0 else vp_o)[r0:r0 + 128, :]
            common_proj(n, units, tglob0, kv_from, pre, flagged, kv_out)
            mark("proj")
            if pre:
                gla_units(units, pre, lambda ui: (Sst, Sbf, SK, None, None))
                mark("gla")
                return
            def attn_all():
                for t in range(ntile):
                    attn_tile(tglob0 + t, t)
            gla_units(units, pre, lambda ui: (Sst, Sbf, SK, None, None), attn_all)
            mark("attn")
            merge_and_out(n)
            resid(n, ntile, lambda kc: Gm[:, kc, 0:1], "Gm")
            mark("merge")
            rms_bcast(xT, n, xkeys(ntile))
            if flagged:
                modulate(n, ntile, "AfP", "BfP", lambda kc: AfP[:, kc:kc + 1], lambda kc: BfP[:, kc:kc + 1])
            else:
                modulate(n, ntile, "Af", "Bf", lambda kc: Af[:, kc, 0:1], lambda kc: Bf[:, kc, 0:1])
            prefetch_x(nxt)
            ffn(n, [(0, n, 1, lambda jj: hist[:, jj:jj + 1, :], lambda jj: K("hist", jj), (), None)])
            mark("ffn")
            resid(n, ntile, lambda kc: Gf[:, kc, 0:1], "Gf")
            swap_tiles([dst_rows[t * 128:(t + 1) * 128, :] for t in range(ntile)], nxt)
            mark("end")

        def sample_block():
            n = 128
            slot = 0
            units = [(s * 32, 32) for s in range(4)]
            self.marks.append(("sample", 0, "start", len(S.ops)))
            rms_bcast(xT, n, xkeys(1))
            modulate_seq(Am, Bm, "Am", "Bm")
            smark = lambda nm: self.marks.append(("sample", 0, nm, len(S.ops)))
            smark("norm1")
            common_proj(n, units, slot, 0, False, False, lambda ui, which: (ks_o if which == 0 else vs_o))
            smark("proj")
            SSK = K("Sst")
            def S_of(ui):
                def before():
                    self.dma("sp", Ss[:], sgla[ui].rearrange("h k v -> k h v"), (), tuple(SSK + (h,) for h in range(4)), ("Ss",))
                    self.cp("act", Ssb[:], Ss[:], tuple(SSK + (h,) for h in range(4)), (SSK + ("b",),))
                def after():
                    self.dma("sp", glas_o[ui].rearrange("h k v -> k h v"), Ss[:], tuple(SSK + (h,) for h in range(4)), (), ("Ss",))
                return (Ss, Ssb, SSK, before, after)
            wgv, wkv = self.wload(win_g[G_VA], kwin[G_VA], 8, 512)
            for s in range(4):
                bk, bkey = self.bank()
                for kc in range(8):
                    self.mm(bk[0:32, :], hT[:, kc, s * 32:(s + 1) * 32], wgv(kc), kc == 0, kc == 7, (wkv, K("hT", kc)), (bkey,))
                self.cp("act", vown[:, s, :, 0:64], bk[0:32, :].rearrange("p (h e) -> p h e", h=8), (bkey,), (K("vown", s),))
                self.cp("dve", vown[:, s, :, 64], ones_f[0:32, 0:8], (K("ones_f"),), (K("vown1", s),))
            def attn_all():
                for s in range(4):
                    attn_sample(s, slot)
            gla_units(units, False, S_of, attn_all)
            smark("attn")
            merge_and_out(n)
            resid_seq(Gm, "Gm")
            smark("merge")
            rms_bcast(xT, n, xkeys(1))
            modulate_seq(Af, Bf, "Af", "Bf")
            for s in range(4):
                load_T(sconv[s], 88, hist_s[:, s, :, :].rearrange("p j t -> p t j"), "hist_s_in%d" % s,
                       view=lambda a: a.rearrange("p (t j) -> p t j", t=2))
            hs_in = tuple(K("hist_s_in%d" % s) for s in range(4))
            ffn(n, [(0, n, 4, lambda jj: hist_s[:, :, jj, :], lambda jj: K("hist_s", jj), hs_in, None)])
            smark("ffn")
            resid_seq(Gf, "Gf")
            store_y(ysam, 1)
            for s in range(4):
                store_hist(hist_s[:, s, :, :], tuple(K("hist_s", jj) for jj in range(44)), convs_o[s])

        def mod_mixed(At, Bt, AP, BP, Asc, Bsc, APk, BPk):
            for kc in range(8):
                xk = (K("xT", kc, 0), K("xT", kc, 1))
                v3 = lambda ap: ap.rearrange("p (s w) -> p s w", s=4)
                self.tt("dve", tmpf2[:, 0:256], xT[:, kc, 0:256], rbc[:, 0:256], ALU.mult, xk + (K("rbc"),), (K("tmpf2"),))
                self.act(hT[:, kc, 0:128], tmpf2[:, 0:128], AF.Identity, (K("tmpf2"),) + modkeys[APk] + modkeys[BPk], (K("hT", kc),),
                         bias=BP[:, kc:kc + 1], scale=AP[:, kc:kc + 1])
                self.tt("dve", v3(tmpf2[:, 128:256]), v3(tmpf2[:, 128:256]), At[:, kc, 1:5].to_broadcast([128, 4, 32]), ALU.mult,
                        (K("tmpf2"),) + modkeys[Asc], (K("tmpf2"),))
                self.tt("dve", v3(hT[:, kc, 128:256]), v3(tmpf2[:, 128:256]), Bt[:, kc, 1:5].to_broadcast([128, 4, 32]), ALU.add,
                        (K("tmpf2"), K("hT", kc)) + modkeys[Bsc], (K("hT", kc),))

        def resid_mixed(Gt, gname):
            v3 = lambda ap: ap.rearrange("p (s w) -> p s w", s=4)
            for kc in range(8):
                tb, tk = (tmpf2, K("tmpf2")) if kc % 2 == 0 else (tmpf, K("tmpf"))
                self.tt("pool", tb[:, 0:256], m2[:, kc, 0:256], rbc[:, 0:256], ALU.mult, (K("m2", kc), K("rbc")), (tk,))
                self.stt("dve", xT[:, kc, 0:128], tb[:, 0:128], Gt[:, kc, 0:1], xT[:, kc, 0:128], ALU.mult, ALU.add,
                         (tk, K("xT", kc, 0)) + modkeys[gname], (K("xT", kc, 0),))
                self.tt("dve", v3(tb[:, 128:256]), v3(tb[:, 128:256]), Gt[:, kc, 1:5].to_broadcast([128, 4, 32]), ALU.mult,
                        (tk,) + modkeys[gname], (tk,))
                self.tt("dve", xT[:, kc, 128:256], xT[:, kc, 128:256], tb[:, 128:256], ALU.add, (tk, K("xT", kc, 1)), (K("xT", kc, 1),))

        def mixed_block(nxt):
            n = 256
            SL, OVS = 0, 7
            mk = lambda nm: self.marks.append(("mixed", 15, nm, len(S.ops)))
            mk("start")
            rms_bcast(xT, n, xkeys(2))
            mod_mixed(Am, Bm, AmP, BmP, "Am", "Bm", "AmP", "BmP")
            mk("norm1")
            wg, wk = self.wload(win_g[G_QA], kwin[G_QA], 8, 512)
            def ev_q(j, bk, bkey):
                self.act(qaT[:, j, 0:n], bk[:, 0:n], AF.Identity, (bkey,), (K("qaT", j),), scale=0.125)
            pairs_fm(wg, wk, n, ev_q)
            wg, wk = self.wload(win_g[G_KA], kwin[G_KA], 8, 512)
            def ev_k(j, bk, bkey):
                self.cp("act", kaT[OVS][:, j, :], bk[:, 0:128], (bkey,), (K("kaT", OVS, j),))
                self.cp("act", kaT[SL][:, j, :], bk[:, 128:256], (bkey,), (K("kaT", SL, j),))
            pairs_fm(wg, wk, n, ev_k)
            units_tm(wg, wk, 0, 512, [(128, 128)], lambda i, bk, bkey: kv_stage(bk, bkey, 0, ks_o))
            wg, wk = self.wload(win_g[G_VA], kwin[G_VA], 8, 512)
            def ev_v(i, bk, bkey):
                self.cp("act", vaug[OVS][:, :, 0:64], bk[:].rearrange("p (h e) -> p h e", h=8), (bkey,), (K("vaug", OVS),))
                self.ts("dve", vaug[OVS][:, :, 64], ones_f[:, 0:8], flag[:, 0:1], None, ALU.mult, None,
                        (K("ones_f"), K("flag")), (K("vaug1", OVS),))
            units_tm(wg, wk, 0, 512, [(0, 128)], ev_v)
            units_tm(wg, wk, 0, 512, [(128, 128)], lambda i, bk, bkey: kv_stage(bk, bkey, 1, vs_o))
            for sq_ in range(4):
                bk, bkey = self.bank()
                c0 = 128 + sq_ * 32
                for kc in range(8):
                    self.mm(bk[0:32, :], hT[:, kc, c0:c0 + 32], wg(kc), kc == 0, kc == 7, (wk, K("hT", kc)), (bkey,))
                self.cp("act", vown[:, sq_, :, 0:64], bk[0:32, :].rearrange("p (h e) -> p h e", h=8), (bkey,), (K("vown", sq_),))
                self.cp("dve", vown[:, sq_, :, 64], ones_f[0:32, 0:8], (K("ones_f"),), (K("vown1", sq_),))
            wg, wk = self.wload(win_g[G_QKB], kwin[G_QKB], 8, 512)
            def ev_qb(j, bk, bkey):
                self.cp("act", qbT[:, j, 0:n], bk[0:64, 0:n], (bkey,), (K("qbT", j),))
            def ev_kb(j, bk, bkey):
                self.cp("act", kbT[:, j, 0:n], bk[0:64, 0:n], (bkey,), (K("kbT", j),))
            heads_fm(wg, wk, 0, 4, n, ev_qb)
            heads_fm(wg, wk, 256, 4, n, ev_kb)
            wg, wk = self.wload(wgk_g, kwgk, 8, 16)
            bk, bkey = self.bank()
            for kc in range(8):
                self.mm(bk[0:16, 0:n], wg(kc), hT[:, kc, 0:n], kc == 0, kc == 7, (wk, K("hT", kc)), (bkey,))
            self.cp("act", gkT[0:16, 0:n], bk[0:16, 0:n], (bkey,), (K("gkT"),))
            mk("proj")
            units = [(0, 128)] + [(128 + 32 * q, 32) for q in range(4)]
            SSK = K("Ss")
            def S_of(ui):
                if ui == 0:
                    return (Sst, Sbf, SK, None, None)
                sq_ = ui - 1
                def before():
                    self.dma("sp", Ss[:], sgla[sq_].rearrange("h k v -> k h v"), (), tuple(SSK + (h,) for h in range(4)), ("Ss",))
                    self.cp("act", Ssb[:], Ss[:], tuple(SSK + (h,) for h in range(4)), (SSK + ("b",),))
                def after():
                    self.dma("sp", glas_o[sq_].rearrange("h k v -> k h v"), Ss[:], tuple(SSK + (h,) for h in range(4)), (), ("Ss",))
                return (Ss, Ssb, SSK, before, after)
            def attn_all():
                attn_tile(15, 0)
                for q in range(4):
                    attn_sample(q, SL, 128)
            gla_units(units, False, S_of, attn_all)
            mk("attn")
            merge_and_out(n)
            resid_mixed(Gm, "Gm")
            mk("merge")
            rms_bcast(xT, n, xkeys(2))
            mod_mixed(Af, Bf, AfP, BfP, "Af", "Bf", "AfP", "BfP")
            for q in range(4):
                load_T(sconv[q], 88, hist_s[:, 4 + q, :, :].rearrange("p j t -> p t j"), "hist_s_in%d" % q,
                       view=lambda a: a.rearrange("p (t j) -> p t j", t=2))
            hs_in = tuple(K("hist_s_in%d" % q) for q in range(4)) + (K("hist_s_ov"),)
            prefetch_x(nxt)
            ffn(n, [(0, n, 8, lambda jj: hist_s[:, :, jj, :], lambda jj: K("hist_s", jj), hs_in, None)])
            mk("ffn")
            resid_mixed(Gf, "Gf")
            swap_tiles([yov[:, :], ysam[:, :]], nxt)
            mk("end")
            self.ts("dve", hist[:], hist_s[:, 3, :, :], flag[:, 0:1], None, ALU.mult, None,
                    tuple(K("hist_s", jj) for jj in range(44)) + (K("flag"),), tuple(K("hist", jj) for jj in range(44)))
            for q in range(4):
                store_hist(hist_s[:, 4 + q, :, :], tuple(K("hist_s", jj) for jj in range(44)), convs_o[q])

        tiles_of = lambda rows, nt: [rows[t * 128:(t + 1) * 128, :] for t in range(nt)]
        sched = []
        t0 = 0
        while t0 < 11:
            nt = min(NB, 11 - t0)
            sched.append(("prefix", xpre[t0 * 128:(t0 + nt) * 128, :], nt, t0, None, nt))
            t0 += nt
        while t0 < 15:
            nt = min(NB, 15 - t0)
            sched.append(("prefix", xpre[t0 * 128:(t0 + nt) * 128, :], nt, t0, None, 0))
            t0 += nt
        sched.append(("mixed", None, 2, 15, None, 0))
        for b in range(NMAIN // NB):
            sched.append(("full", xmain[b * NBT:(b + 1) * NBT, :], NB, 16 + b * NB, ymain[b * NBT:(b + 1) * NBT, :], 0))
        def tiles_for(e):
            return [xov[:, :], xsam[:, :]] if e[0] == "mixed" else tiles_of(e[1], e[2])
        for bi, (mode, src, nt, tg, dst, kvf) in enumerate(sched):
            nxt = tiles_for(sched[bi + 1]) if bi + 1 < len(sched) else None
            if mode == "mixed":
                ada_part2()
                mixed_block(nxt)
            else:
                prompt_block(src, nt, tg, mode, dst, kvf, preloaded=bi > 0, nxt=nxt)
        self.dma("sp", glap_o.rearrange("h k v -> k h v"), Sst[:], tuple(SK + (h,) for h in range(4)), (), ("glap",))
        store_hist(hist[:], tuple(K("hist", jj) for jj in range(44)), convp_o)
        return S.finalize()


_CACHE = {}


def _program():
    if "nc" not in _CACHE:
        b = Builder()
        b.build()
        _CACHE["nc"] = b.nc
    return _CACHE["nc"]


def kernel(x_prompt, x_sample, cache_k_a, cache_v_a, state_gla, state_conv, c_prompt, c_sample,
           w_ada, b_ada, g_pre_mix, g_post_mix, g_pre_ffn, g_post_ffn, w_in, w_gk2, b_gk,
           rel_bias, g_gla, w_br_a, w_br_b, w_out, w_up, w_dw, b_dw, w_down):
    f = lambda a: np.ascontiguousarray(np.asarray(a, dtype=np.float32))
    x_prompt, x_sample = f(x_prompt), f(x_sample)
    rb = f(rel_bias)[0]
    kk = np.arange(128)[:, None]
    qq = np.arange(128)[None, :]
    idx_prev = np.clip(qq + 128 - kk, -128, 128) + 128
    idx_own = np.clip(qq - kk, -128, 128) + 128
    btab = np.stack([rb[:, idx_prev].transpose(1, 0, 2), rb[:, idx_own].transpose(1, 0, 2)])
    cvec = np.broadcast_to(rb[:, 256][None, :], (128, 8))
    wi = f(w_in)[0]
    wi = np.concatenate([wi[:, :3072], wi[:, 3088:], wi[:, 3072:3088]], axis=1)
    wu = f(w_up)[0]
    cols = []
    for g in range(11):
        cols += [wu[:, 2 * g * 128:(2 * g + 2) * 128], wu[:, DFF + 2 * g * 128:DFF + (2 * g + 2) * 128]]
    wu = np.concatenate(cols, axis=1)
    shared = {
        "w_ada": f(w_ada)[0], "b_ada": f(b_ada)[0].reshape(48, 128),
        "gvec": np.concatenate([f(g_pre_mix)[0], f(g_post_mix)[0], f(g_pre_ffn)[0], f(g_post_ffn)[0]]).reshape(32, 128),
        "w_in": f(wi), "w_gk2": f(w_gk2)[0], "b_gk": f(b_gk)[0].reshape(1, 256),
        "btab": f(btab), "cvec": f(cvec), "ggla": f(np.broadcast_to(np.tile(f(g_gla)[0], 4)[None, :], (128, 512))),
        "w_br_a": f(w_br_a)[0], "w_br_b": f(w_br_b)[0], "w_out": f(w_out)[0], "w_up": f(wu),
        "w_dw": f(w_dw)[0].reshape(3 * 44, 128), "b_dw": f(b_dw)[0].reshape(44, 128), "w_down": f(w_down)[0],
    }
    in_maps = []
    for c in range(8):
        b, hf = c // 2, c % 2
        m = dict(shared)
        if hf == 1:
            m["xpre"] = x_prompt[b, 0:NPRE * 128]
            m["xov"] = x_prompt[b, NPRE * 128:2048]
        else:
            m["xpre"] = np.zeros((NPRE * 128, D), np.float32)
            m["xov"] = np.zeros((128, D), np.float32)
        m["xmain"] = x_prompt[b, hf * 2048:(hf + 1) * 2048]
        m["xsam"] = x_sample[4 * c:4 * c + 4].reshape(128, D)
        m["crow"] = f(np.concatenate([f(c_prompt)[b:b + 1], f(c_sample)[4 * c:4 * c + 4]], 0).reshape(40, 128))
        m["flag"] = np.full((128, 1), float(hf), np.float32)
        m["ck"] = f(cache_k_a)[0, 4 * c:4 * c + 4].reshape(4, 512, 512)
        m["cv"] = f(cache_v_a)[0, 4 * c:4 * c + 4].reshape(4, 512, 512)
        m["sgla"] = f(state_gla)[0, 4 * c:4 * c + 4]
        m["sconv"] = f(state_conv)[0, 4 * c:4 * c + 4].reshape(4, 88, 128)
        in_maps.append({k: np.ascontiguousarray(v) for k, v in m.items()})
    nc = _program()
    cores = [int(t) for t in os.environ.get("KCORES", "0,1,2,3,4,5,6,7").split(",")]
    if os.environ.get("KTRACE"):
        res = run_bass_kernel_spmd(nc, [in_maps[c] for c in cores], core_ids=list(range(len(cores))), trace=True)
        print("EXEC_TIME_NS", res.exec_time_ns)
    else:
        res = run_bass_kernel_spmd(nc, [in_maps[c] for c in cores], core_ids=list(range(len(cores))))
    R = {c: res.results[i] for i, c in enumerate(cores)}
    y_prompt = np.zeros((4, 4096, D), np.float32)
    y_sample = np.zeros((32, 32, D), np.float32)
    k_p = np.zeros((1, 4, 512, 8, 64), np.float32); v_p = np.zeros_like(k_p)
    gla_p = np.zeros((1, 4, 4, 64, 128), np.float32)
    conv_p = np.zeros((1, 4, 2, 2 * DFF), np.float32)
    k_s = np.zeros((1, 32, 32, 8, 64), np.float32); v_s = np.zeros_like(k_s)
    gla_s = np.zeros((1, 32, 4, 64, 128), np.float32)
    conv_s = np.zeros((1, 32, 2, 2 * DFF), np.float32)
    for c in cores:
        b, hf = c // 2, c % 2
        r = R[c]
        y_prompt[b, hf * 2048:(hf + 1) * 2048] = r["ymain"]
        y_sample[4 * c:4 * c + 4] = r["ysam"].reshape(4, 32, D)
        if hf == 1:
            k_p[0, b] = r["kp"].reshape(512, 8, 64)
            v_p[0, b] = r["vp"].reshape(512, 8, 64)
            gla_p[0, b] = r["glap"]
            conv_p[0, b] = r["convp"].reshape(2, 2 * DFF)
        k_s[0, 4 * c:4 * c + 4] = r["ks"].reshape(4, 32, 8, 64)
        v_s[0, 4 * c:4 * c + 4] = r["vs"].reshape(4, 32, 8, 64)
        gla_s[0, 4 * c:4 * c + 4] = r["glas"]
        conv_s[0, 4 * c:4 * c + 4] = r["convs"].reshape(4, 2, 2 * DFF)
    return (y_prompt, y_sample, k_p, v_p, gla_p, conv_p, k_s, v_s, gla_s, conv_s)
```

```python
from contextlib import ExitStack
import os

import numpy as np
import concourse.bass as bass
import concourse.mybir as mybir
from concourse.bass_utils import run_bass_kernel_spmd

F32 = mybir.dt.float32
BF16 = mybir.dt.bfloat16
AF = mybir.ActivationFunctionType
ALU = mybir.AluOpType

D = 1024
KC = 8
DFF = 2816
NFF = 22
DIN = 5136
G_QA, G_KA, G_VA, G_QKB, G_VB, G_GB, G_GA, G_GBR = 0, 1, 2, 3, 4, 5, 6, 8
EPS = 1e-6
NEG = -30000.0
NPRE = 15
NMAIN = 16
NB = 4


class Op:
    __slots__ = ("eng", "fn", "reads", "writes", "dsem", "signal", "sigval", "deps", "rdeps")

    def __init__(self, eng, fn, reads, writes, dsem):
        self.eng, self.fn, self.reads, self.writes, self.dsem = eng, fn, reads, writes, dsem
        self.signal = False
        self.sigval = 0
        self.deps = ()
        self.rdeps = frozenset()


class Sched:
    DMA = ("sp", "pool_dma")

    def __init__(self, nc, stack):
        self.nc = nc
        self.stack = stack
        self.ops = []
        self.eng_obj = {"pe": nc.tensor, "act": nc.scalar, "dve": nc.vector, "pool": nc.gpsimd,
                        "sp": nc.sync, "pool_dma": nc.gpsimd}
        self.wuses = []
        self.wdepth = 2

    def add(self, eng, fn, reads=(), writes=(), dsem=None):
        op = Op(eng, fn, tuple(reads), tuple(writes), dsem)
        self.ops.append(op)
        return op

    def queue_of(self, op):
        return "pool" if op.eng == "pool_dma" else op.eng

    def finalize(self):
        nc = self.nc
        inserts = {}
        lastrd = {}
        for idx, op in enumerate(self.ops):
            for k in op.reads:
                if k and k[0] == "wuse":
                    lastrd[k[1]] = idx
        prev = 0
        for i, (pos, op) in enumerate(self.wuses):
            tgt = self.wuses[max(0, i - self.wdepth)][0]
            if i >= 3 and (i - 3) in lastrd:
                tgt = max(tgt, lastrd[i - 3] + 1)
            tgt = max(tgt, prev)
            prev = tgt
            assert tgt <= pos, (i, tgt, pos)
            inserts.setdefault(tgt, []).append(op)
        ops = []
        for i, op in enumerate(self.ops):
            if i in inserts:
                ops.extend(inserts[i])
            ops.append(op)
        self.ops = ops
        last_w = {}
        readers = {}
        for i, op in enumerate(ops):
            deps = set()
            q = self.queue_of(op)
            isdma = op.dsem is not None
            for k in op.reads:
                j = last_w.get(k)
                if j is not None:
                    deps.add(j)
            rdeps = set(deps)
            for k in op.writes:
                j = last_w.get(k)
                if j is not None:
                    oj = ops[j]
                    if isdma or oj.dsem is not None or self.queue_of(oj) != q or q != "pe":
                        deps.add(j)
                for j in readers.get(k, ()):
                    oj = ops[j]
                    if isdma or oj.dsem is not None or self.queue_of(oj) != q or (q != "pe" and os.environ.get("KWAR", "1") == "1"):
                        deps.add(j)
            deps.discard(i)
            op.deps = tuple(sorted(deps))
            op.rdeps = frozenset(rdeps)
            for j in op.deps:
                ops[j].signal = True
            for k in op.reads:
                lst = readers.setdefault(k, [])
                if op.dsem is None:
                    lst[:] = [j for j in lst if ops[j].dsem is not None or self.queue_of(ops[j]) != q]
                lst.append(i)
            for k in op.writes:
                last_w[k] = i
                readers[k] = []
        esem = {}
        for e in ("pe", "act", "dve", "pool"):
            esem[e] = self.stack.enter_context(nc.semaphore("sem_" + e))
        dsems = {}
        cnt = {}
        for op in ops:
            if op.dsem is not None:
                if op.dsem not in dsems:
                    dsems[op.dsem] = self.stack.enter_context(nc.semaphore("dsem_%d" % len(dsems)))
                    cnt[op.dsem] = 0
                cnt[op.dsem] += 1
                op.sigval = 16 * cnt[op.dsem]
            elif op.signal:
                q = self.queue_of(op)
                cnt[q] = cnt.get(q, 0) + 1
                op.sigval = cnt[q]
        known = {q: {} for q in ("pe", "act", "dve", "pool", "sp")}
        for op in ops:
            q = self.queue_of(op)
            eng = self.eng_obj[op.eng]
            need = {}
            need_r = {}
            for j in op.deps:
                oj = ops[j]
                s = dsems[oj.dsem] if oj.dsem is not None else esem[self.queue_of(oj)]
                key = id(s)
                if key not in need or need[key][1] < oj.sigval:
                    need[key] = (s, oj.sigval)
                if j in op.rdeps and need_r.get(key, 0) < oj.sigval:
                    need_r[key] = oj.sigval
            pending = [(key, s_, v) for key, (s_, v) in need.items() if known[q].get(key, 0) < v]
            fuse = None
            if pending and op.dsem is None and q in ("act", "dve", "pool") and os.environ.get("KFUSE", "1") == "1":
                fuse = pending.pop()
            elif pending and q == "pe" and os.environ.get("KFUSEPE", "1") == "1":
                for pi, (key, s_, v) in enumerate(pending):
                    if need_r.get(key, 0) <= known[q].get(key, 0):
                        fuse = pending.pop(pi)
                        break
            for key, s_, v in pending:
                eng.wait_ge(s_, v)
                known[q][key] = v
            ins = op.fn(eng)
            if fuse is not None:
                ins._wait_ge(fuse[1], fuse[2])
                known[q][fuse[0]] = fuse[2]
            if op.dsem is not None:
                ins.then_inc(dsems[op.dsem], 16)
            elif op.signal:
                ins.then_inc(esem[q], 1)
        self.counts = dict((str(k), v) for k, v in cnt.items())
        for k, s in dsems.items():
            nc.sync.wait_ge(s, 16 * cnt[k])
        return len(ops)


class Builder:
    def __init__(self):
        self.stack = ExitStack()
        self.nc = bass.Bass("TRN2", target_bir_lowering=False)
        self.S = Sched(self.nc, self.stack)
        self.nbank = 0
        self.wuse_n = 0
        self.prefetched = False
        self.bank_set = [0, 1, 2, 3, 4]
        self.marks = []
        self.uid = 0

    def din(self, name, shape, dt=F32):
        return self.nc.dram_tensor(name, list(shape), dt, kind="ExternalInput").ap()

    def dout(self, name, shape):
        return self.nc.dram_tensor(name, list(shape), F32, kind="ExternalOutput").ap()

    def dscr(self, name, shape, dt=BF16):
        return self.nc.dram_tensor(name, list(shape), dt, kind="Internal").ap()

    def sb(self, name, shape, dt=F32):
        return self.stack.enter_context(self.nc.sbuf_tensor(name, list(shape), dt))

    def ps(self, name, shape, dt=F32):
        return self.stack.enter_context(self.nc.psum_tensor(name, list(shape), dt))

    def bank(self):
        bs = self.bank_set
        i = bs[self.nbank % len(bs)]
        self.nbank += 1
        return self.banks[i], ("ps", i)

    def key(self, name):
        self.uid += 1
        return (name, self.uid)

    def mm(self, out, lhsT, rhs, start, stop, reads, writes):
        self.S.add("pe", lambda e: e.matmul(out, lhsT, rhs, start=start, stop=stop), reads, writes)

    def tr(self, out, in_, ident, reads, writes):
        self.S.add("pe", lambda e: e.transpose(out, in_, ident), reads, writes)

    def act(self, out, in_, func, reads, writes, bias=None, scale=None, accum_out=None):
        kw = {}
        if bias is not None:
            kw["bias"] = bias
        if scale is not None:
            kw["scale"] = scale
        if accum_out is not None:
            kw["accum_out"] = accum_out
        self.S.add("act", lambda e: e.activation(out, in_, func, **kw), reads, writes)

    def tt(self, eng, out, in0, in1, op, reads, writes):
        self.S.add(eng, lambda e: e.tensor_tensor(out, in0, in1, op), reads, writes)

    def ts(self, eng, out, in0, s1, s2, op0, op1, reads, writes):
        if s2 is None:
            self.S.add(eng, lambda e: e.tensor_scalar(out, in0, s1, None, op0), reads, writes)
        else:
            self.S.add(eng, lambda e: e.tensor_scalar(out, in0, s1, s2, op0, op1), reads, writes)

    def stt(self, eng, out, in0, scalar, in1, op0, op1, reads, writes):
        self.S.add(eng, lambda e: e.scalar_tensor_tensor(out, in0, scalar, in1, op0=op0, op1=op1), reads, writes)

    def cp(self, eng, out, in_, reads, writes):
        if eng == "act":
            self.S.add("act", lambda e: e.copy(out, in_), reads, writes)
        else:
            self.S.add(eng, lambda e: e.tensor_copy(out, in_), reads, writes)

    def memset(self, eng, ap, val, writes):
        self.S.add(eng, lambda e: e.memset(ap, val), (), writes)

    def dma(self, q, out, in_, reads, writes, dsem, slow=False):
        if slow:
            self.S.add(q, lambda e: e.dma_start(out=out, in_=in_, allow_slow_non_contiguous=True), reads, writes, dsem)
        else:
            self.S.add(q, lambda e: e.dma_start(out=out, in_=in_), reads, writes, dsem)

    def interleave(self, fns_banks):
        main = self.S.ops
        streams = []
        for fn, banks in fns_banks:
            self.S.ops = []
            self.bank_set = banks
            fn()
            streams.append(self.S.ops)
        self.S.ops = main
        self.bank_set = [0, 1, 2, 3, 4]
        idx = [0] * len(streams)
        total = sum(len(st) for st in streams)
        for _ in range(total):
            best, bf = None, None
            for si, st in enumerate(streams):
                if idx[si] < len(st):
                    frac = idx[si] / len(st)
                    if bf is None or frac < bf:
                        best, bf = si, frac
            main.append(streams[best][idx[best]])
            idx[best] += 1

    def wload(self, scr_g, wkeys, kcn, width):
        i = self.wuse_n
        self.wuse_n += 1
        slot = i % 3
        buf = self.wbufs[slot]
        key = ("wuse", i)
        dst = buf[:, 0:kcn * width].rearrange("p (kc n) -> p kc n", kc=kcn)
        op = Op("sp", lambda e: e.dma_start(out=dst, in_=scr_g), tuple(wkeys),
                (key, ("wbuf", slot)) + ((("wuse", i - 3),) if i >= 3 else ()), ("wstream", slot))
        self.S.wuses.append((len(self.S.ops), op))
        return (lambda kc, a=0, b=width: buf[:, kc * width + a: kc * width + b]), key

    def build(self):
        nc = self.nc
        S = self.S
        sb, ps = self.sb, self.ps
        NBT = NB * 128
        K = lambda *a: tuple(a)
        xpre = self.din("xpre", [NPRE * 128, D])
        xov = self.din("xov", [128, D])
        xmain = self.din("xmain", [NMAIN * 128, D])
        xsam = self.din("xsam", [128, D])
        crow = self.din("crow", [40, 128])
        flag_d = self.din("flag", [128, 1])
        ck = self.din("ck", [4, 512, 512])
        cv = self.din("cv", [4, 512, 512])
        sgla = self.din("sgla", [4, 4, 64, 128])
        sconv = self.din("sconv", [4, 88, 128])
        w_ada = self.din("w_ada", [D, 6 * D])
        b_ada = self.din("b_ada", [48, 128])
        gvec = self.din("gvec", [32, 128])
        w_in = self.din("w_in", [D, DIN])
        w_gk2 = self.din("w_gk2", [16, 256])
        b_gk = self.din("b_gk", [1, 256])
        btab = self.din("btab", [2, 128, 8, 128])
        cvec = self.din("cvec", [128, 8])
        ggla = self.din("ggla", [128, 512])
        w_br_a = self.din("w_br_a", [512, D])
        w_br_b = self.din("w_br_b", [512, D])
        w_out = self.din("w_out", [D, D])
        w_up = self.din("w_up", [D, 2 * DFF])
        w_dw = self.din("w_dw", [3 * 44, 128])
        b_dw = self.din("b_dw", [44, 128])
        w_down = self.din("w_down", [DFF, D])

        ymain = self.dout("ymain", [NMAIN * 128, D])
        ysam = self.dout("ysam", [128, D])
        yov = self.dout("yov", [128, D])
        kp_o = self.dout("kp", [512, 512])
        vp_o = self.dout("vp", [512, 512])
        glap_o = self.dout("glap", [4, 64, 128])
        convp_o = self.dout("convp", [88, 128])
        ks_o = self.dout("ks", [128, 512])
        vs_o = self.dout("vs", [128, 512])
        glas_o = self.dout("glas", [4, 4, 64, 128])
        convs_o = self.dout("convs", [4, 88, 128])

        win_g = self.dscr("win_g", [10, 128, 8, 512])
        wgk_g = self.dscr("wgk_g", [128, 8, 16])
        wbra_g = self.dscr("wbra_g", [2, 128, 4, 512])
        wbrb_g = self.dscr("wbrb_g", [2, 128, 4, 512])
        wout_g = self.dscr("wout_g", [2, 128, 8, 512])
        wup_g = self.dscr("wup_g", [11, 128, 8, 512])
        wdown_g = self.dscr("wdown_g", [8, 128, NFF, 128])

        self.banks = [ps("bank%d" % i, [128, 512]) for i in range(5)]
        obank = [ps("obank%d" % i, [128, 512]) for i in range(2)]
        pbf = ps("pbf", [128, 1024], BF16)

        self.wbufs = [sb("wbuf%d" % i, [128, 4096], BF16) for i in range(3)]
        ident_f = sb("ident_f", [128, 128])
        ident_b = sb("ident_b", [128, 128], BF16)
        ones_b = sb("ones_b", [128, 128], BF16)
        triu_f = sb("triu_f", [128, 128])
        trisl_f = sb("trisl_f", [128, 128])
        ones_f = sb("ones_f", [128, 8])
        epsb = sb("epsb", [128, 1])
        mod = sb("mod", [128, 6, KC, 5])
        Am = sb("Am", [128, KC, 5]); Bm = sb("Bm", [128, KC, 5]); Gm = sb("Gm", [128, KC, 5])
        Af = sb("Af", [128, KC, 5]); Bf = sb("Bf", [128, KC, 5]); Gf = sb("Gf", [128, KC, 5])
        AmP = sb("AmP", [128, KC]); BmP = sb("BmP", [128, KC])
        AfP = sb("AfP", [128, KC]); BfP = sb("BfP", [128, KC])
        gT = sb("gT", [128, 32])
        badaT = sb("badaT", [128, 48])
        cT = sb("cT", [128, 40])
        siluT = sb("siluT", [128, 40], BF16)
        flag = sb("flag_s", [128, 1])
        wdwT = sb("wdwT", [128, 3, 44])
        bdwT = sb("bdwT", [128, 44])
        wgk_f = sb("wgk_f", [17, 256])
        wgk_b = sb("wgk_b", [17, 256], BF16)
        tab_prev = sb("tab_prev", [128, 8, 128], BF16)
        tab_own = sb("tab_own", [128, 8, 128], BF16)
        tab_mask = sb("tab_mask", [128, 128], BF16)
        cvec_s = sb("cvec_s", [128, 8])
        ggla_s = sb("ggla_s", [128, 512])
        stage = sb("stage", [128, 128])
        hist = sb("hist", [128, 44, 2])
        hist_s = sb("hist_s", [128, 8, 44, 2])
        h88 = sb("h88", [128, 2, 44])
        xin = [sb("xin%d" % i, [128, D]) for i in range(2)]
        xT = sb("xT", [128, KC, NBT])
        hT = sb("hT", [128, KC, NBT], BF16)
        sq = sb("sq", [128, 2, NBT], BF16)
        rbc = sb("rbc", [128, NBT])
        tmpf = sb("tmpf", [128, NBT])
        tmpf2 = sb("tmpf2", [128, NBT])
        qaT = sb("qaT", [128, 4, NBT], BF16)
        kaT = [sb("kaT%d" % i, [128, 4, 128], BF16) for i in range(8)]
        vaug = [sb("vaug%d" % i, [128, 8, 65], BF16) for i in range(8)]
        qbT = sb("qbT", [64, 4, NBT], BF16)
        kbT = sb("kbT", [64, 4, NBT], BF16)
        gkT = sb("gkT", [32, NBT], BF16)
        pT_raw = sb("pT_raw", [128, 2560])
        pT = pT_raw.bitcast(BF16)[:, :].rearrange("p (k h q) -> p k h q", k=5, h=8)
        ya_tok = sb("ya_tok", [128, 512], BF16)
        rden = sb("rden", [128, 8])
        kb_tok = sb("kb_tok", [128, 256])
        vb_tok2 = [sb("vb_tok%d" % i, [128, 4, 128], BF16) for i in range(2)]
        gb2_2 = [sb("gb2_%d" % i, [128, 512]) for i in range(2)]
        gtanh = sb("gtanh", [128, 512])
        Lsp = sb("Lsp", [128, 256])
        e_sb = sb("e_sb", [128, 256])
        e1 = sb("e1", [64, 4, 128]); e2 = sb("e2", [64, 4, 128])
        qtT2 = [sb("qtT%d" % i, [64, 4, 128], BF16) for i in range(2)]; ktT = sb("ktT", [64, 4, 128], BF16)
        kend2 = [sb("kend%d" % i, [128, 256], BF16) for i in range(2)]
        dec2 = [sb("dec%d" % i, [64, 4]) for i in range(2)]
        attT2 = [sb("attT%d" % i, [128, 4, 128], BF16) for i in range(2)]
        Sst = sb("Sst", [64, 4, 128])
        Sbf = sb("Sbf", [64, 4, 128], BF16)
        ssq = sb("ssq", [128, 4]); rgl = sb("rgl", [128, 4])
        yb_tok = sb("yb_tok", [128, 512], BF16)
        sgA = sb("sgA", [128, NBT]); sgB = sb("sgB", [128, NBT])
        m2 = sb("m2", [128, KC, NBT])
        ua = [sb("ua%d" % i, [128, NBT + 8]) for i in range(2)]
        y0 = [sb("y0_%d" % i, [128, NBT]) for i in range(2)]
        ua_sets = [ua, [pT_raw[:, 0:NBT + 8], pT_raw[:, NBT + 8:2 * NBT + 16]]]
        y0_sets = [y0, [pT_raw[:, 2 * NBT + 16:3 * NBT + 16], pT_raw[:, 3 * NBT + 16:4 * NBT + 16]]]
        actT = sb("actT", [128, NFF, NBT], BF16)
        mergedT = lambda c: actT[:, c, :]
        MK = lambda c: K("actT", c)
        yaT = lambda c: actT[:, 8 + c, :]
        YAK = tuple(K("actT", 8 + c) for c in range(4))
        ybT = lambda c: actT[:, 12 + c, :]
        YBK = tuple(K("actT", 12 + c) for c in range(4))
        m2b = m2.bitcast(BF16)
        kcT = lambda c: m2b[:, c, 0:512]
        vcaug = sb("vcaug", [128, 4, 8, 65], BF16)
        vown = sb("vown", [32, 4, 8, 65], BF16)
        Ss = sb("Ss", [64, 4, 128]); Ssb = sb("Ssb", [64, 4, 128], BF16)

        def const_mask(t, keyname, pattern, cm, base, cmp_op):
            self.memset("pool", t[:], 1.0, (K(keyname),))
            S.add("pool", lambda e: e.affine_select(t[:], t[:], pattern=pattern, compare_op=cmp_op, fill=0.0,
                                                    base=base, channel_multiplier=cm), (K(keyname),), (K(keyname),))
        self.memset("pool", ident_f[:], 0.0, (K("ident_f"),))
        S.add("pool", lambda e: e.affine_select(ident_f[:], ident_f[:], pattern=[[-1, 128]], compare_op=ALU.not_equal,
                                                fill=1.0, base=0, channel_multiplier=1), (K("ident_f"),), (K("ident_f"),))
        const_mask(triu_f, "triu_f", [[1, 128]], -1, 0, ALU.is_ge)
        const_mask(trisl_f, "trisl_f", [[-1, 128]], 1, -1, ALU.is_ge)
        self.cp("dve", ident_b[:], ident_f[:], (K("ident_f"),), (K("ident_b"),))
        self.memset("dve", ones_b[:], 1.0, (K("ones_b"),))
        self.memset("dve", ones_f[:], 1.0, (K("ones_f"),))
        self.memset("dve", epsb[:], EPS, (K("epsb"),))
        self.memset("dve", gkT[:], 1.0, (K("gkT_ones"),))
        self.memset("dve", hist[:], 0.0, tuple(K("hist", jj) for jj in range(44)))
        self.memset("dve", hist_s[:, 0:4, :, :], 0.0, (K("hist_s_ov"),))
        self.memset("dve", Sst[:], 0.0, tuple(K("Sst", h) for h in range(4)))
        self.memset("dve", Sbf[:], 0.0, (K("Sst", "b"),))

        def load_T(src2d, rows, dst, kname, view=None):
            self.dma("pool_dma", stage[0:rows, :], src2d, (), (K("stage"),), ("stage",))
            bk, bkey = self.bank()
            self.tr(bk[:, 0:rows], stage[0:rows, :], ident_f[0:rows, 0:rows], (K("stage"), K("ident_f")), (bkey,))
            src = bk[:, 0:rows] if view is None else view(bk[:, 0:rows])
            self.cp("dve", dst, src, (bkey,), (K(kname),))

        load_T(gvec, 32, gT[:], "gT")
        load_T(b_ada, 48, badaT[:], "badaT")
        load_T(crow, 40, cT[:], "cT")
        for j in range(3):
            load_T(w_dw[j * 44:(j + 1) * 44, :], 44, wdwT[:, j, :], "wdwT%d" % j)
        load_T(b_dw, 44, bdwT[:], "bdwT")
        self.dma("pool_dma", flag[:], flag_d, (), (K("flag"),), ("misc", 0))
        self.dma("pool_dma", wgk_f[0:16, :], w_gk2, (), (K("wgk_f0"),), ("misc", 1))
        self.dma("pool_dma", wgk_f[16:17, :], b_gk, (), (K("wgk_f1"),), ("misc", 2))
        self.cp("dve", wgk_b[:], wgk_f[:], (K("wgk_f0"), K("wgk_f1")), (K("wgk_b"),))
        self.dma("pool_dma", cvec_s[:], cvec, (), (K("cvec"),), ("misc", 3))
        self.dma("pool_dma", ggla_s[:], ggla, (), (K("ggla"),), ("misc", 4))
        self.ts("dve", ggla_s[:], ggla_s[:], 0.5, None, ALU.mult, None, (K("ggla"),), (K("ggla"),))
        tabf = xin[0][:, :].rearrange("p (h q) -> p h q", h=8)
        for ti, tdst in ((0, tab_prev), (1, tab_own)):
            self.dma("pool_dma", tabf, btab[ti], (), (K("xin", 0), K("xinv", 0)), ("misc", 5))
            for h in range(8):
                self.ts("dve", tdst[:, h, :], tabf[:, h, :], cvec_s[:, h:h + 1], None, ALU.subtract, None,
                        (K("xin", 0), K("xinv", 0), K("cvec")), (K("tab", ti, h),))
        self.memset("dve", tab_own[64:128, :, 0:64], NEG, tuple(K("tab", 1, h) for h in range(8)))
        for ti, tdst in ((0, tab_prev), (1, tab_own)):
            self.act(tdst[:], tdst[:], AF.Exp, tuple(K("tab", ti, h) for h in range(8)), (K("etab", ti),))
        self.memset("dve", tab_mask[:], 1.0, (K("tab_mask"),))
        self.memset("dve", tab_mask[0:64, 64:128], 0.0, (K("tab_mask"),))

        def cast_group(src_cols, dst_g, name, g):
            self.dma("pool_dma", dst_g, src_cols.rearrange("(kc p) n -> p kc n", p=128), (), (K("scr", name, g),), ("cast", name, g))
            return (K("scr", name, g),)

        self.act(tmpf[:, 0:40], cT[:], AF.Tanh, (K("cT"),), (K("tmpf"),), scale=0.5)
        self.ts("dve", tmpf[:, 0:40], tmpf[:, 0:40], 0.5, 0.5, ALU.mult, ALU.add, (K("tmpf"),), (K("tmpf"),))
        self.tt("dve", siluT[:], tmpf[:, 0:40], cT[:], ALU.mult, (K("tmpf"), K("cT")), (K("siluT"),))
        siluv = siluT[:].rearrange("p (s k) -> p k s", k=8)
        modkeys = {}
        k8 = lambda n: tuple(K(n, kc) for kc in range(8))

        def ada_groups(glist):
            for g in glist:
                sl = g % 2
                buf = actT[:, sl * 8:(sl + 1) * 8, :].rearrange("p c n -> p (c n)")
                bkeys = tuple(K("actT", sl * 8 + c) for c in range(8))
                self.dma("pool_dma", buf.rearrange("p (kc n) -> p kc n", kc=8),
                         w_ada.rearrange("(kc p) n -> p kc n", p=128)[:, :, g * 512:(g + 1) * 512], (), bkeys, ("ada", sl))
                bk, bkey = self.bank()
                for oc in range(4):
                    for kc in range(8):
                        self.mm(bk[:, oc * 8:oc * 8 + 5], buf[:, kc * 512 + oc * 128: kc * 512 + (oc + 1) * 128], siluv[:, kc, :],
                                kc == 0, kc == 7, bkeys + (K("siluT"),), (bkey,))
                for oc in range(4):
                    ch = g * 4 + oc
                    self.ts("dve", mod[:, ch // 8, ch % 8, :], bk[:, oc * 8:oc * 8 + 5], badaT[:, ch:ch + 1], None, ALU.add, None,
                            (bkey, K("badaT")), (K("mod", ch),))

        ada_groups(range(0, 4))
        mk1 = tuple(K("mod", ch) for ch in range(16))
        for kc in range(8):
            self.ts("dve", Am[:, kc, :], mod[:, 1, kc, :], 1.0, gT[:, kc:kc + 1], ALU.add, ALU.mult, mk1 + (K("gT"),), (K("Am", kc),))
        self.cp("dve", Bm[:], mod[:, 0, :, :], mk1, (K("Bm"),))
        self.ts("dve", AmP[:], Am[:, :, 0], flag[:, 0:1], None, ALU.mult, None, k8("Am") + (K("flag"),), (K("AmP"),))
        self.ts("dve", BmP[:], Bm[:, :, 0], flag[:, 0:1], None, ALU.mult, None, (K("Bm"), K("flag")), (K("BmP"),))
        modkeys.update({"Am": k8("Am"), "Bm": (K("Bm"),), "AmP": (K("AmP"),), "BmP": (K("BmP"),)})

        def ada_part2():
            ada_groups(range(4, 12))
            mk2 = tuple(K("mod", ch) for ch in range(16, 48))
            for kc in range(8):
                rw = mk2 + (K("gT"),)
                self.ts("dve", Af[:, kc, :], mod[:, 4, kc, :], 1.0, gT[:, 16 + kc:17 + kc], ALU.add, ALU.mult, rw, (K("Af", kc),))
                self.ts("dve", Gm[:, kc, :], mod[:, 2, kc, :], gT[:, 8 + kc:9 + kc], None, ALU.mult, None, rw, (K("Gm", kc),))
                self.ts("dve", Gf[:, kc, :], mod[:, 5, kc, :], gT[:, 24 + kc:25 + kc], None, ALU.mult, None, rw, (K("Gf", kc),))
            self.cp("dve", Bf[:], mod[:, 3, :, :], mk2, (K("Bf"),))
            self.ts("dve", AfP[:], Af[:, :, 0], flag[:, 0:1], None, ALU.mult, None, k8("Af") + (K("flag"),), (K("AfP"),))
            self.ts("dve", BfP[:], Bf[:, :, 0], flag[:, 0:1], None, ALU.mult, None, (K("Bf"), K("flag")), (K("BfP"),))
        modkeys.update({"Af": k8("Af"), "Gm": k8("Gm"), "Gf": k8("Gf"), "Bf": (K("Bf"),), "AfP": (K("AfP"),), "BfP": (K("BfP"),)})

        kwin = {}
        for g in (3, 4, 1, 2):
            kwin[g] = cast_group(w_in[:, g * 512:(g + 1) * 512], win_g[g], "win", g)
        self.dma("pool_dma", wgk_g, w_in.rearrange("(kc p) n -> p kc n", p=128)[:, :, 5120:5136], (), (K("scr", "wgk"),), ("cast", "wgk"))
        kwgk = (K("scr", "wgk"),)
        for g in (0, 5, 6, 7, 8, 9):
            kwin[g] = cast_group(w_in[:, g * 512:(g + 1) * 512], win_g[g], "win", g)
        kwbra = [cast_group(w_br_a[:, g * 512:(g + 1) * 512], wbra_g[g], "wbra", g) for g in range(2)]
        kwbrb = [cast_group(w_br_b[:, g * 512:(g + 1) * 512], wbrb_g[g], "wbrb", g) for g in range(2)]
        kwout = [cast_group(w_out[:, g * 512:(g + 1) * 512], wout_g[g], "wout", g) for g in range(2)]
        kwup = [cast_group(w_up[:, g * 512:(g + 1) * 512], wup_g[g], "wup", g) for g in range(11)]
        kwdown = [cast_group(w_down[:, c * 128:(c + 1) * 128], wdown_g[c], "wdown", c) for c in range(8)]

        def load_xT(src_rows, ntile):
            for t in range(ntile):
                xb = xin[t % 2]
                kx = K("xin", t % 2)
                self.dma("sp", xb[:], src_rows[t * 128:(t + 1) * 128, :], (), (kx, K("xinv", t % 2)), ("xin", t % 2))
                for half in range(2):
                    bk, bkey = self.bank()
                    for j in range(4):
                        kc = half * 4 + j
                        self.tr(bk[:, j * 128:(j + 1) * 128], xb[:, kc * 128:(kc + 1) * 128], ident_f[:],
                                (kx if half == 0 else K("xinv", t % 2), K("ident_f")), (bkey,))
                    self.cp("act", xT[:, half * 4:half * 4 + 4, t * 128:(t + 1) * 128],
                            bk[:].rearrange("p (j n) -> p j n", j=4), (bkey,), tuple(K("xT", half * 4 + j, t) for j in range(4)))

        def rms_bcast(srcT, n, src_keys_fn):
            bk, bkey = self.bank()
            for kc in range(8):
                if kc % 2 == 0:
                    self.act(sq[:, 0, 0:n], srcT[:, kc, 0:n], AF.Square, src_keys_fn(kc), (K("sq", 0),))
                else:
                    self.tt("dve", sq[:, 1, 0:n], srcT[:, kc, 0:n], srcT[:, kc, 0:n], ALU.mult, src_keys_fn(kc), (K("sq", 1),))
                self.mm(bk[:, 0:n], ones_b[:], sq[:, kc % 2, 0:n], kc == 0, kc == 7, (K("ones_b"), K("sq", kc % 2)), (bkey,))
            self.act(tmpf[:, 0:n], bk[:, 0:n], AF.Ln, (bkey, K("epsb")), (K("tmpf"),), bias=epsb[:, 0:1], scale=1.0 / D)
            self.act(rbc[:, 0:n], tmpf[:, 0:n], AF.Exp, (K("tmpf"),), (K("rbc"),), scale=-0.5)

        def modulate(n, ntile, Asc, Bsc, Afn, Bfn):
            for kc in range(8):
                xk = tuple(K("xT", kc, t) for t in range(ntile))
                self.tt("dve", tmpf2[:, 0:n], xT[:, kc, 0:n], rbc[:, 0:n], ALU.mult, xk + (K("rbc"),), (K("tmpf2"),))
                self.act(hT[:, kc, 0:n], tmpf2[:, 0:n], AF.Identity, (K("tmpf2"),) + modkeys[Asc] + modkeys[Bsc], (K("hT", kc),),
                         bias=Bfn(kc), scale=Afn(kc))

        def modulate_seq(At, Bt, Asc, Bsc):
            for kc in range(8):
                xk = (K("xT", kc, 0),)
                v3 = lambda ap: ap.rearrange("p (s w) -> p s w", s=4)
                self.tt("dve", tmpf2[:, 0:128], xT[:, kc, 0:128], rbc[:, 0:128], ALU.mult, xk + (K("rbc"),), (K("tmpf2"),))
                self.tt("dve", v3(tmpf2[:, 0:128]), v3(tmpf2[:, 0:128]), At[:, kc, 1:5].to_broadcast([128, 4, 32]), ALU.mult,
                        (K("tmpf2"),) + modkeys[Asc], (K("tmpf2"),))
                self.tt("dve", v3(hT[:, kc, 0:128]), v3(tmpf2[:, 0:128]), Bt[:, kc, 1:5].to_broadcast([128, 4, 32]), ALU.add,
                        (K("tmpf2"),) + modkeys[Bsc], (K("hT", kc),))

        xkeys = lambda ntile: (lambda kc: tuple(K("xT", kc, t) for t in range(ntile)))

        def heads_fm(wg, wk, col0, nheads, n, evac):
            for h in range(nheads):
                bk, bkey = self.bank()
                for kc in range(8):
                    self.mm(bk[0:64, 0:n], wg(kc, col0 + h * 64, col0 + (h + 1) * 64), hT[:, kc, 0:n], kc == 0, kc == 7, (wk, K("hT", kc)), (bkey,))
                evac(h, bk, bkey)

        def pairs_fm(wg, wk, n, evac, kc_outer=False):
            if kc_outer:
                bks = [self.bank() for c in range(4)]
                for kc in range(8):
                    for c in range(4):
                        self.mm(bks[c][0][:, 0:n], wg(kc, c * 128, (c + 1) * 128), hT[:, kc, 0:n], kc == 0, kc == 7, (wk, K("hT", kc)), (bks[c][1],))
                for c in range(4):
                    evac(c, bks[c][0], bks[c][1])
                return
            for c in range(4):
                bk, bkey = self.bank()
                for kc in range(8):
                    self.mm(bk[:, 0:n], wg(kc, c * 128, (c + 1) * 128), hT[:, kc, 0:n], kc == 0, kc == 7, (wk, K("hT", kc)), (bkey,))
                evac(c, bk, bkey)

        def units_tm(wg, wk, col0, ncols, units, evac):
            for ui, (c0, C) in enumerate(units):
                bk, bkey = self.bank()
                for kc in range(8):
                    self.mm(bk[0:C, 0:ncols], hT[:, kc, c0:c0 + C], wg(kc, col0, col0 + ncols), kc == 0, kc == 7, (wk, K("hT", kc)), (bkey,))
                evac(ui, bk, bkey)

        def gla_stageA(ui, col0, C, pre, W):
            p = ui % 2
            wgk_, wkk, wgv_, wkv, wgg_, wkg = W
            vb_tok, gb2, qtT, kend, dec, attT = vb_tok2[p], gb2_2[p], qtT2[p], kend2[p], dec2[p], attT2[p]
            bk, bkey = self.bank()
            for kc in range(8):
                self.mm(bk[0:C, 0:256], hT[:, kc, col0:col0 + C], wgk_(kc, 256, 512), kc == 0, kc == 7, (wkk, K("hT", kc)), (bkey,))
            self.cp("act", kb_tok[0:C, :], bk[0:C, 0:256], (bkey,), (K("kb_tok"),))
            bk, bkey = self.bank()
            for kc in range(8):
                self.mm(bk[0:C, :], hT[:, kc, col0:col0 + C], wgv_(kc), kc == 0, kc == 7, (wkv, K("hT", kc)), (bkey,))
            self.cp("act", vb_tok[0:C, :, :].rearrange("p h e -> p (h e)"), bk[0:C, :], (bkey,), (K("vb_tok", p),))
            if not pre:
                bk, bkey = self.bank()
                for kc in range(8):
                    self.mm(bk[0:C, :], hT[:, kc, col0:col0 + C], wgg_(kc), kc == 0, kc == 7, (wkg, K("hT", kc)), (bkey,))
                self.cp("act", gb2[0:C, :], bk[0:C, :], (bkey,), (K("gb2", p),))
            bk, bkey = self.bank()
            self.mm(bk[0:C, 0:256], gkT[0:17, col0:col0 + C], wgk_b[0:17, :], True, True, (K("gkT"), K("gkT_ones"), K("wgk_b")), (bkey,))
            self.act(e_sb[0:C, :], bk[0:C, 0:256], AF.Exp, (bkey,), (K("e_sb"),), scale=-1.0)
            self.act(Lsp[0:C, :], e_sb[0:C, :], AF.Ln, (K("e_sb"),), (K("Lsp"),), bias=1.0)
            bk2, bkey2 = self.bank()
            self.mm(bk2[0:C, 0:256], trisl_f[0:C, 0:C], Lsp[0:C, :], True, True, (K("trisl_f"), K("Lsp")), (bkey2,))
            self.act(e_sb[0:C, :], bk2[0:C, 0:256], AF.Exp, (bkey2,), (K("e_sb"),), scale=-1.0 / 16)
            self.tt("dve", kend[0:C, :], kb_tok[0:C, :], e_sb[0:C, :], ALU.mult, (K("kb_tok"), K("e_sb")), (K("kend", p),))
            bk3, bkey3 = self.bank()
            for h in range(4):
                self.mm(bk3[0:64, h * 128:h * 128 + C], Lsp[0:C, h * 64:(h + 1) * 64], triu_f[0:C, 0:C], True, True,
                        (K("Lsp"), K("triu_f")), (bkey3,))
            b3 = bk3[0:64, :].rearrange("p (c t) -> p c t", c=4)
            self.act(dec[:, :], b3[:, :, C - 1], AF.Exp, (bkey3,), (K("dec", p),), scale=-1.0 / 16)
            if pre:
                return
            self.act(e1[:, :, 0:C], b3[:, :, 0:C], AF.Exp, (bkey3,), (K("e1"),), scale=-1.0 / 16)
            self.act(e2[:, :, 0:C], b3[:, :, 0:C], AF.Exp, (bkey3,), (K("e2"),), scale=1.0 / 16)
            self.stt("dve", qtT[:, :, 0:C], qbT[:, :, col0:col0 + C], 0.125, e1[:, :, 0:C], ALU.mult, ALU.mult,
                     tuple(K("qbT", j) for j in range(4)) + (K("e1"),), (K("qtT", p),))
            self.tt("dve", ktT[:, :, 0:C], kbT[:, :, col0:col0 + C], e2[:, :, 0:C], ALU.mult, tuple(K("kbT", j) for j in range(4)) + (K("e2"),), (K("ktT"),))
            bk4, bkey4 = self.bank()
            for h in range(4):
                self.mm(bk4[0:C, h * 128:h * 128 + C], ktT[:, h, 0:C], qtT[:, h, 0:C], True, True, (K("ktT"), K("qtT", p)), (bkey4,))
            for h in range(4):
                self.tt("dve", attT[0:C, h, 0:C], bk4[0:C, h * 128:h * 128 + C], triu_f[0:C, 0:C], ALU.mult,
                        (bkey4, K("triu_f")), (K("attT", p, h),))

        def gla_stageB(ui, col0, C, pre, S_t, S_b, Skey):
            p = ui % 2
            vb_tok, gb2, qtT, kend, dec, attT = vb_tok2[p], gb2_2[p], qtT2[p], kend2[p], dec2[p], attT2[p]
            if not pre:
                obk, obkey = self.bank()
                for h in range(4):
                    self.mm(obk[0:C, h * 128:(h + 1) * 128], attT[0:C, h, 0:C], vb_tok[0:C, h, :], True, False, (K("attT", p, h), K("vb_tok", p)), (obkey,))
                    self.mm(obk[0:C, h * 128:(h + 1) * 128], qtT[:, h, 0:C], S_b[:, h, :], False, True,
                            (K("qtT", p), Skey + ("b",)), (obkey,))
                self.memset("dve", ssq[0:C, :], 0.0, tuple(K("ssq", h) for h in range(4)))
                for h in range(4):
                    self.act(attT[0:C, h, :], obk[0:C, h * 128:(h + 1) * 128], AF.Square, (obkey,), (K("attT", p, h), K("ssq", h)), accum_out=ssq[0:C, h:h + 1])
                self.act(rgl[0:C, :], ssq[0:C, :], AF.Ln, tuple(K("ssq", h) for h in range(4)) + (K("epsb"),), (K("rgl"),), bias=epsb[0:C, 0:1], scale=1.0 / 128)
                self.act(rgl[0:C, :], rgl[0:C, :], AF.Exp, (K("rgl"),), (K("rgl"),), scale=-0.5)
                self.act(gtanh[0:C, :], gb2[0:C, :], AF.Tanh, (K("gb2", p),), (K("gtanh"),), scale=0.5)
                self.stt("dve", gtanh[0:C, :], gtanh[0:C, :], 1.0, gb2[0:C, :], ALU.add, ALU.mult, (K("gtanh"), K("gb2", p)), (K("gtanh"),))
                self.tt("dve", gtanh[0:C, :], gtanh[0:C, :], ggla_s[0:C, :], ALU.mult, (K("gtanh"), K("ggla")), (K("gtanh"),))
                for h in range(4):
                    self.stt("dve", yb_tok[0:C, h * 128:(h + 1) * 128], obk[0:C, h * 128:(h + 1) * 128], rgl[0:C, h:h + 1],
                             gtanh[0:C, h * 128:(h + 1) * 128], ALU.mult, ALU.mult, (obkey, K("rgl"), K("gtanh")), (K("yb_tok", h),))
                for c in range(4):
                    self.tr(pbf[:, 512 + c * 128:512 + c * 128 + C], yb_tok[0:C, c * 128:(c + 1) * 128], ident_b[0:C, 0:C],
                            (K("yb_tok", c), K("ident_b")), (K("pbf"),))
                for c in range(4):
                    self.cp("act", ybT(c)[:, col0:col0 + C], pbf[:, 512 + c * 128:512 + c * 128 + C], (K("pbf"),), (YBK[c],))
            bk6, bkey6 = self.bank()
            for h in range(4):
                self.mm(bk6[0:64, h * 128:(h + 1) * 128], kend[0:C, h * 64:(h + 1) * 64], vb_tok[0:C, h, :], True, True, (K("kend", p), K("vb_tok", p)), (bkey6,))
            for h in range(4):
                self.stt("dve", S_t[:, h, :], S_t[:, h, :], dec[:, h:h + 1], bk6[0:64, h * 128:(h + 1) * 128],
                         ALU.mult, ALU.add, (Skey + (h,), K("dec", p), bkey6), (Skey + (h,),))
            self.cp("act", S_b[:], S_t[:], tuple(Skey + (h,) for h in range(4)), (Skey + ("b",),))

        def gla_units(units, pre, S_of, with_fn=None):
            wgk_, wkk = self.wload(win_g[G_QKB], kwin[G_QKB], 8, 512)
            wgv_, wkv = self.wload(win_g[G_VB], kwin[G_VB], 8, 512)
            wgg_, wkg = (None, None) if pre else self.wload(win_g[G_GB], kwin[G_GB], 8, 512)
            W = (wgk_, wkk, wgv_, wkv, wgg_, wkg)

            def body():
                gla_stageA(0, units[0][0], units[0][1], pre, W)
                for ui, (col0, C) in enumerate(units):
                    if ui + 1 < len(units):
                        gla_stageA(ui + 1, units[ui + 1][0], units[ui + 1][1], pre, W)
                    S_t, S_b, Skey, before_fn, after_fn = S_of(ui)
                    if before_fn is not None:
                        before_fn()
                    gla_stageB(ui, col0, C, pre, S_t, S_b, Skey)
                    if after_fn is not None:
                        after_fn()
            if with_fn is None:
                body()
            else:
                self.interleave([(body, [0, 1, 2]), (with_fn, [3, 4])])

        def attn_tile(tglob, tl):
            pTv = lambda kb, par: pT[:, kb, :, :].rearrange("p (c two) q -> p two c q", two=2)[:, par]
            for kb in range(5):
                slot = (tglob - 4 + kb) % 8
                banks = [self.bank(), self.bank()]
                for c in range(4):
                    for par in range(2):
                        pb = par * 64
                        bk, bkey = banks[par]
                        self.mm(bk[:, c * 128:(c + 1) * 128], kaT[slot][pb:pb + 64, c, :], qaT[pb:pb + 64, c, tl * 128:(tl + 1) * 128],
                                True, True, (K("kaT", slot, c), K("qaT", c)), (bkey,))
                for par in range(2):
                    bk, bkey = banks[par]
                    self.act(pTv(kb, par), bk[:].rearrange("p (c q) -> p c q", c=4), AF.Exp, (bkey,), (K("pT", kb, par),))
                tab = {0: (tab_mask[:, :].to_broadcast([128, 8, 128]) if False else None), 3: tab_prev, 4: tab_own}.get(kb)
                pk = (K("pT", kb, 0), K("pT", kb, 1))
                if kb == 0:
                    for h0 in range(0, 8, 4):
                        pass
                    self.tt("dve", pT[:, 0, :, :], pT[:, 0, :, :], bass.AP(tab_mask, 0, [[128, 128], [0, 8], [1, 128]]), ALU.mult,
                            pk + (K("tab_mask"),), pk)
                elif tab is not None:
                    self.tt("dve", pT[:, kb, :, :], pT[:, kb, :, :], tab[:, :, :], ALU.mult, pk + (K("etab", kb - 3),), pk)
            for hg in range(2):
                ob, obkey = obank[hg], K("obank", hg)
                for hh in range(4):
                    h = hg * 4 + hh
                    for kb in range(5):
                        slot = (tglob - 4 + kb) % 8
                        self.mm(ob[:, hh * 65:(hh + 1) * 65], pT[:, kb, h, :], vaug[slot][:, h, :], kb == 0, kb == 4,
                                (K("pT", kb, h % 2), K("vaug", slot), K("vaug1", slot)), (obkey,))
            for hg in range(2):
                ob, obkey = obank[hg], K("obank", hg)
                ov = ob[:, 0:260].rearrange("p (h e) -> p h e", h=4)
                self.ts("dve", rden[:, hg * 4:(hg + 1) * 4], ov[:, :, 64], 1e-30, None, ALU.add, None, (obkey,), (K("rden", hg),))
                S.add("dve", lambda e, hg=hg: e.reciprocal(rden[:, hg * 4:(hg + 1) * 4], rden[:, hg * 4:(hg + 1) * 4]), (K("rden", hg),), (K("rden", hg),))
                for hh in range(4):
                    h = hg * 4 + hh
                    self.ts("dve", ya_tok[:, h * 64:(h + 1) * 64], ov[:, hh, 0:64], rden[:, h:h + 1], None, ALU.mult, None,
                            (obkey, K("rden", hg)), (K("ya_tok", h),))
            yk = tuple(K("ya_tok", h) for h in range(8))
            for c in range(4):
                self.tr(pbf[:, c * 128:(c + 1) * 128], ya_tok[:, c * 128:(c + 1) * 128], ident_b[:], yk + (K("ident_b"),), (K("pbf"),))
            for c in range(4):
                self.cp("act", yaT(c)[:, tl * 128:(tl + 1) * 128], pbf[:, c * 128:(c + 1) * 128], (K("pbf"),), (YAK[c],))

        def attn_sample(s, slot, qbase=0):
            for rb in range(4):
                xb = xin[rb % 2]
                kx = K("xin", rb % 2)
                self.dma("sp", xb[:, 0:512], ck[s, rb * 128:(rb + 1) * 128, :], (), (kx,), ("xin", rb % 2))
                bk, bkey = self.bank()
                for c in range(4):
                    self.tr(bk[:, c * 128:(c + 1) * 128], xb[:, c * 128:(c + 1) * 128], ident_f[:], (kx, K("ident_f")), (bkey,))
                for c in range(4):
                    self.cp("act", kcT(c)[:, rb * 128:(rb + 1) * 128], bk[:, c * 128:(c + 1) * 128], (bkey,), (K("m2", c),))
                self.dma("sp", xb[:, 512:1024], cv[s, rb * 128:(rb + 1) * 128, :], (), (K("xinv", rb % 2),), ("xinv", rb % 2))
                self.cp("dve", vcaug[:, rb, :, 0:64], xb[:, 512:1024].rearrange("p (h e) -> p h e", h=8), (K("xinv", rb % 2),), (K("vcaug", rb),))
                self.cp("dve", vcaug[:, rb, :, 64], ones_f[:, 0:8], (K("ones_f"),), (K("vcaug1", rb),))
            q0 = qbase + s * 32
            l0 = s * 32
            for kb in range(5):
                kn = 128 if kb < 4 else 32
                banks = [self.bank(), self.bank()]
                for c in range(4):
                    for par in range(2):
                        pb = par * 64
                        bk, bkey = banks[par]
                        if kb < 4:
                            lhs = kcT(c)[pb:pb + 64, kb * 128:(kb + 1) * 128]
                            rk = (K("m2", c),)
                        else:
                            lhs = kaT[slot][pb:pb + 64, c, l0:l0 + 32]
                            rk = (K("kaT", slot, c),)
                        self.mm(bk[0:kn, c * 32:(c + 1) * 32], lhs, qaT[pb:pb + 64, c, q0:q0 + 32], True, True, rk + (K("qaT", c),), (bkey,))
                for par in range(2):
                    bk, bkey = banks[par]
                    dstv = pT[0:kn, kb, :, 0:32].rearrange("p (c two) q -> p two c q", two=2)[:, par]
                    self.act(dstv, bk[0:kn, 0:128].rearrange("p (c q) -> p c q", c=4), AF.Exp, (bkey,), (K("pT", kb, par),))
                pk = (K("pT", kb, 0), K("pT", kb, 1))
                if kb == 3:
                    self.tt("dve", pT[:, 3, :, 0:32], pT[:, 3, :, 0:32], tab_prev[:, :, 0:32], ALU.mult, pk + (K("etab", 0),), pk)
                elif kb == 4:
                    self.tt("dve", pT[0:32, 4, :, 0:32], pT[0:32, 4, :, 0:32], tab_own[0:32, :, 0:32], ALU.mult, pk + (K("etab", 1),), pk)
            for h in range(8):
                hg, hh = h // 4, h % 4
                for kb in range(5):
                    kn = 128 if kb < 4 else 32
                    rhs = vcaug[:, kb, h, :] if kb < 4 else vown[0:32, s, h, :]
                    rk = (K("vcaug", kb), K("vcaug1", kb)) if kb < 4 else (K("vown", s), K("vown1", s))
                    self.mm(obank[hg][0:32, hh * 65:(hh + 1) * 65], pT[0:kn, kb, h, 0:32], rhs, kb == 0, kb == 4,
                            (K("pT", kb, h % 2),) + rk, (K("obank", hg),))
            for hg in range(2):
                ob, obkey = obank[hg], K("obank", hg)
                ov = ob[0:32, 0:260].rearrange("p (h e) -> p h e", h=4)
                S.add("dve", lambda e, ov=ov, hg=hg: e.reciprocal(rden[0:32, hg * 4:(hg + 1) * 4], ov[:, :, 64]), (obkey,), (K("rden", hg),))
                for hh in range(4):
                    h = hg * 4 + hh
                    self.ts("dve", ya_tok[0:32, h * 64:(h + 1) * 64], ov[:, hh, 0:64], rden[0:32, h:h + 1], None, ALU.mult, None,
                            (obkey, K("rden", hg)), (K("ya_tok", h),))
            yk = tuple(K("ya_tok", h) for h in range(8))
            for c in range(4):
                self.tr(pbf[:, c * 128:c * 128 + 32], ya_tok[0:32, c * 128:(c + 1) * 128], ident_b[0:32, 0:32], yk + (K("ident_b"),), (K("pbf"),))
            for c in range(4):
                self.cp("act", yaT(c)[:, q0:q0 + 32], pbf[:, c * 128:c * 128 + 32], (K("pbf"),), (YAK[c],))

        def resid(n, ntile, Gfn, gname):
            for kc in range(8):
                tb, tk = (tmpf2, K("tmpf2")) if kc % 2 == 0 else (tmpf, K("tmpf"))
                self.tt("pool", tb[:, 0:n], m2[:, kc, 0:n], rbc[:, 0:n], ALU.mult, (K("m2", kc), K("rbc")), (tk,))
                xk = tuple(K("xT", kc, t) for t in range(ntile))
                self.stt("dve", xT[:, kc, 0:n], tb[:, 0:n], Gfn(kc), xT[:, kc, 0:n], ALU.mult, ALU.add,
                         (tk,) + modkeys[gname] + xk, xk)

        def resid_seq(Gt, gname):
            v3 = lambda ap: ap.rearrange("p (s w) -> p s w", s=4)
            for kc in range(8):
                xk = (K("xT", kc, 0),)
                self.tt("dve", tmpf2[:, 0:128], m2[:, kc, 0:128], rbc[:, 0:128], ALU.mult, (K("m2", kc), K("rbc")), (K("tmpf2"),))
                self.tt("dve", v3(tmpf2[:, 0:128]), v3(tmpf2[:, 0:128]), Gt[:, kc, 1:5].to_broadcast([128, 4, 32]), ALU.mult,
                        (K("tmpf2"),) + modkeys[gname], (K("tmpf2"),))
                self.tt("dve", xT[:, kc, 0:128], xT[:, kc, 0:128], tmpf2[:, 0:128], ALU.add, (K("tmpf2"),) + xk, xk)

        def merge_and_out(n):
            def ev_gate(dst, dkey):
                def f(bk, bkey):
                    self.act(dst[:, 0:n], bk[:, 0:n], AF.Tanh, (bkey,), (dkey,), scale=0.5)
                    self.ts("dve", dst[:, 0:n], dst[:, 0:n], 0.5, 0.5, ALU.mult, ALU.add, (dkey,), (dkey,))
                return f
            for G in range(2):
                wga, wka = self.wload(win_g[G_GA + G], kwin[G_GA + G], 8, 512)
                wba, wkba = self.wload(wbra_g[G], kwbra[G], 4, 512)
                for j in range(4):
                    c = G * 4 + j
                    bk, bkey = self.bank()
                    for kc in range(8):
                        self.mm(bk[:, 0:n], wga(kc, j * 128, (j + 1) * 128), hT[:, kc, 0:n], kc == 0, kc == 7, (wka, K("hT", kc)), (bkey,))
                    ev_gate(sgA, K("sgA"))(bk, bkey)
                    bk, bkey = self.bank()
                    for kc in range(4):
                        self.mm(bk[:, 0:n], wba(kc, j * 128, (j + 1) * 128), yaT(kc)[:, 0:n], kc == 0, kc == 3, (wkba, YAK[kc]), (bkey,))
                    self.tt("dve", m2[:, c, 0:n], sgA[:, 0:n], bk[:, 0:n], ALU.mult, (K("sgA"), bkey), (K("m2", c),))
                wgb, wkb = self.wload(win_g[G_GBR + G], kwin[G_GBR + G], 8, 512)
                wbb, wkbb = self.wload(wbrb_g[G], kwbrb[G], 4, 512)
                for j in range(4):
                    c = G * 4 + j
                    bk, bkey = self.bank()
                    for kc in range(8):
                        self.mm(bk[:, 0:n], wgb(kc, j * 128, (j + 1) * 128), hT[:, kc, 0:n], kc == 0, kc == 7, (wkb, K("hT", kc)), (bkey,))
                    ev_gate(sgB, K("sgB"))(bk, bkey)
                    bk, bkey = self.bank()
                    for kc in range(4):
                        self.mm(bk[:, 0:n], wbb(kc, j * 128, (j + 1) * 128), ybT(kc)[:, 0:n], kc == 0, kc == 3, (wkbb, YBK[kc]), (bkey,))
                    self.tt("dve", sgB[:, 0:n], sgB[:, 0:n], bk[:, 0:n], ALU.mult, (K("sgB"), bkey), (K("sgB"),))
                    self.tt("dve", mergedT(c)[:, 0:n], sgB[:, 0:n], m2[:, c, 0:n], ALU.add, (K("sgB"), K("m2", c)), (MK(c),))
            for G in range(2):
                wg, wk = self.wload(wout_g[G], kwout[G], 8, 512)
                for j in range(4):
                    c = G * 4 + j
                    bk, bkey = self.bank()
                    for kc in range(8):
                        self.mm(bk[:, 0:n], wg(kc, j * 128, (j + 1) * 128), mergedT(kc)[:, 0:n], kc == 0, kc == 7, (wk, MK(kc)), (bkey,))
                    self.cp("act", m2[:, c, 0:n], bk[:, 0:n], (bkey,), (K("m2", c),))
            rms_bcast(m2, n, lambda kc: (K("m2", kc),))

        ALIAS_KEYS = tuple(K("pT", kb, par) for kb in range(5) for par in range(2)) + \
            tuple(K(nm, 1, h) for nm in ("ua", "uah", "y0") for h in range(2))

        def alias_fence():
            self.memset("dve", pT_raw[:, 2559:2560], 0.0, ALIAS_KEYS)

        def ffn(n, groups):
            alias_fence()
            for g in range(11):
                wg, wk = self.wload(wup_g[g], kwup[g], 8, 512)
                pre_banks = None
                if g == 0:
                    pre_banks = [self.bank() for jc in range(4)]
                    for kc in range(8):
                        for jc in range(4):
                            self.mm(pre_banks[jc][0][:, 0:n], wg(kc, jc * 128, (jc + 1) * 128), hT[:, kc, 0:n], kc == 0, kc == 7,
                                    (wk, K("hT", kc)), (pre_banks[jc][1],))
                for pair in range(2):
                    ua_, y0_ = ua_sets[pair], y0_sets[pair]
                    for half in range(2):
                        jc = half * 2 + pair
                        jj = half * NFF + 2 * g + pair
                        if pre_banks is not None:
                            bk, bkey = pre_banks[jc]
                        else:
                            bk, bkey = self.bank()
                            for kc in range(8):
                                self.mm(bk[:, 0:n], wg(kc, jc * 128, (jc + 1) * 128), hT[:, kc, 0:n], kc == 0, kc == 7, (wk, K("hT", kc)), (bkey,))
                        uk, uh, yk = K("ua", pair, half), K("uah", pair, half), K("y0", pair, half)
                        self.act(y0_[half][:, 0:n], bk[:, 0:n], AF.Identity, (bkey, K("wdwT2"), K("bdwT")), (yk,),
                                 bias=bdwT[:, jj:jj + 1], scale=wdwT[:, 2, jj:jj + 1])
                        off = 0
                        for (c0, ncols, nseg, hview, hkeyf, extra, scale_ap) in groups:
                            w = ncols // nseg
                            v3 = lambda ap, nseg=nseg: ap.rearrange("p (s w) -> p s w", s=nseg)
                            u3 = v3(ua_[half][:, off:off + nseg * (w + 2)])
                            off += nseg * (w + 2)
                            yv = v3(y0_[half][:, c0:c0 + ncols])
                            self.cp("pool", u3[:, :, 0:2], hview(jj), (hkeyf(jj),) + extra, (uh,))
                            if scale_ap is None:
                                self.cp("act", u3[:, :, 2:2 + w], v3(bk[:, c0:c0 + ncols]), (bkey,), (uk,))
                            else:
                                self.act(u3[:, :, 2:2 + w], v3(bk[:, c0:c0 + ncols]), AF.Identity, (bkey, K("flag")), (uk,), scale=scale_ap)
                            self.stt("dve", yv, u3[:, :, 1:1 + w], wdwT[:, 1, jj:jj + 1], yv, ALU.mult, ALU.add, (uk, uh, yk, K("wdwT1")), (yk,))
                            self.stt("dve", yv, u3[:, :, 0:w], wdwT[:, 0, jj:jj + 1], yv, ALU.mult, ALU.add, (uk, uh, yk, K("wdwT0")), (yk,))
                            self.cp("pool", hview(jj), u3[:, :, w:w + 2], (uk,), (hkeyf(jj),))
                    j = 2 * g + pair
                    self.act(y0_[0][:, 0:n], y0_[0][:, 0:n], AF.Gelu_apprx_tanh, (K("y0", pair, 0),), (K("y0", pair, 0),))
                    self.tt("dve", actT[:, j, 0:n], y0_[0][:, 0:n], y0_[1][:, 0:n], ALU.mult, (K("y0", pair, 0), K("y0", pair, 1)), (K("actT", j),))
            for c in range(8):
                wg, wk = self.wload(wdown_g[c], kwdown[c], NFF, 128)
                bk, bkey = self.bank()
                for j in range(NFF):
                    self.mm(bk[:, 0:n], wg(j), actT[:, j, 0:n], j == 0, j == NFF - 1, (wk, K("actT", j)), (bkey,))
                self.cp("act", m2[:, c, 0:n], bk[:, 0:n], (bkey,), (K("m2", c),))
            rms_bcast(m2, n, lambda kc: (K("m2", kc),))
            alias_fence()

        def store_y(dst_rows, ntile):
            for t in range(ntile):
                yb_ = xin[t % 2]
                ky = K("xin", t % 2)
                for half in range(2):
                    bk, bkey = self.bank()
                    for j in range(4):
                        kc = half * 4 + j
                        self.tr(bk[:, j * 128:(j + 1) * 128], xT[:, kc, t * 128:(t + 1) * 128], ident_f[:], (K("xT", kc, t), K("ident_f")), (bkey,))
                    self.cp("act", yb_[:, half * 512:(half + 1) * 512], bk[:], (bkey,), (K("xinv", t % 2),) if half else (ky,))
                self.dma("sp", dst_rows[t * 128:(t + 1) * 128, :], yb_[:], (ky, K("xinv", t % 2)), (), ("xin", t % 2))

        def load_dma(tile_ap):
            self.dma("sp", xin[0][:], tile_ap, (), (K("xin", 0), K("xinv", 0)), ("xin", 0))

        def prefetch_x(nxt_tiles):
            if not nxt_tiles or self.prefetched:
                return
            load_dma(nxt_tiles[0])
            self.prefetched = True

        def load_tile(nxt_tiles, t):
            xb = xin[0]
            kx, kv = K("xin", 0), K("xinv", 0)
            for half in range(2):
                bk, bkey = self.bank()
                for j in range(4):
                    kc = half * 4 + j
                    self.tr(bk[:, j * 128:(j + 1) * 128], xb[:, kc * 128:(kc + 1) * 128], ident_f[:], (kx if half == 0 else kv, K("ident_f")), (bkey,))
                self.cp("act", xT[:, half * 4:half * 4 + 4, t * 128:(t + 1) * 128],
                        bk[:].rearrange("p (j n) -> p j n", j=4), (bkey,), tuple(K("xT", half * 4 + j, t) for j in range(4)))
            if t + 1 < len(nxt_tiles):
                load_dma(nxt_tiles[t + 1])

        def store_tile(dst_ap, t):
            xb = xin[1]
            kx, kv = K("xin", 1), K("xinv", 1)
            for half in range(2):
                bk, bkey = self.bank()
                for j in range(4):
                    kc = half * 4 + j
                    self.tr(bk[:, j * 128:(j + 1) * 128], xT[:, kc, t * 128:(t + 1) * 128], ident_f[:], (K("xT", kc, t), K("ident_f")), (bkey,))
                self.cp("act", xb[:, half * 512:(half + 1) * 512], bk[:], (bkey,), (kv,) if half else (kx,))
            self.dma("sp", dst_ap, xb[:], (kx, kv), (), ("xin", 1))

        def swap_tiles(dst_tiles, nxt_tiles):
            dst_tiles = dst_tiles or []
            nxt_tiles = nxt_tiles or []
            prefetch_x(nxt_tiles)
            self.prefetched = False
            for t in range(max(len(dst_tiles), len(nxt_tiles))):
                if t < len(dst_tiles):
                    store_tile(dst_tiles[t], t)
                if t < len(nxt_tiles):
                    load_tile(nxt_tiles, t)

        def store_hist(hsrc, hkeys, dst):
            self.cp("dve", h88[:], hsrc.rearrange("p j t -> p t j"), hkeys, (K("h88"),))
            bk, bkey = self.bank()
            self.tr(bk[0:88, 0:128], h88[:].rearrange("p t j -> p (t j)"), ident_f[:], (K("h88"), K("ident_f")), (bkey,))
            self.cp("act", stage[0:88, :], bk[0:88, 0:128], (bkey,), (K("stage"),))
            self.dma("sp", dst, stage[0:88, :], (K("stage"),), (), ("stage_o",))

        def kv_stage(bk, bkey, which, dst):
            st = xin[1][:, which * 512:(which + 1) * 512]
            sk = K("xin", 1) if which == 0 else K("xinv", 1)
            self.cp("act", st, bk[:], (bkey,), (sk,))
            self.dma("sp", dst, st, (sk,), (), ("xin", 1) if which == 0 else ("xinv", 1))

        LAST_KV0 = 16 + NMAIN - 4

        def common_proj(n, units, tglob0, kv_from, pre, flagged, kv_out):
            if not pre:
                wg, wk = self.wload(win_g[G_QA], kwin[G_QA], 8, 512)
                def ev_q(j, bk, bkey):
                    self.act(qaT[:, j, 0:n], bk[:, 0:n], AF.Identity, (bkey,), (K("qaT", j),), scale=0.125)
                pairs_fm(wg, wk, n, ev_q, kc_outer=True)
            if kv_from < len(units):
                wg, wk = self.wload(win_g[G_KA], kwin[G_KA], 8, 512)
                def ev_k(j, bk, bkey):
                    for ui in range(kv_from, len(units)):
                        c0, C = units[ui]
                        if C == 128:
                            slot = (tglob0 + ui) % 8
                            self.cp("act", kaT[slot][:, j, :], bk[:, c0:c0 + 128], (bkey,), (K("kaT", slot, j),))
                    if units[0][1] != 128:
                        self.cp("act", kaT[tglob0 % 8][:, j, :], bk[:, 0:128], (bkey,), (K("kaT", tglob0 % 8, j),))
                pairs_fm(wg, wk, n, ev_k)
                kunits = [(ui, units[ui]) for ui in range(kv_from, len(units)) if kv_out(ui, 0) is not None] if units[0][1] == 128 else []
                if kunits:
                    units_tm(wg, wk, 0, 512, [u for _, u in kunits], lambda i, bk, bkey: kv_stage(bk, bkey, 0, kv_out(kunits[i][0], 0)))
                if units[0][1] != 128:
                    units_tm(wg, wk, 0, 512, [(0, 128)], lambda i, bk, bkey: kv_stage(bk, bkey, 0, kv_out(0, 0)))
                wg, wk = self.wload(win_g[G_VA], kwin[G_VA], 8, 512)
                if units[0][1] == 128:
                    def ev_v(i, bk, bkey):
                        ui = kv_from + i
                        slot = (tglob0 + ui) % 8
                        self.cp("act", vaug[slot][:, :, 0:64], bk[:].rearrange("p (h e) -> p h e", h=8), (bkey,), (K("vaug", slot),))
                        if flagged:
                            self.ts("dve", vaug[slot][:, :, 64], ones_f[:, 0:8], flag[:, 0:1], None, ALU.mult, None,
                                    (K("ones_f"), K("flag")), (K("vaug1", slot),))
                        else:
                            self.cp("dve", vaug[slot][:, :, 64], ones_f[:, 0:8], (K("ones_f"),), (K("vaug1", slot),))
                        if kv_out(ui, 1) is not None:
                            kv_stage(bk, bkey, 1, kv_out(ui, 1))
                    units_tm(wg, wk, 0, 512, units[kv_from:], ev_v)
                else:
                    units_tm(wg, wk, 0, 512, [(0, 128)], lambda i, bk, bkey: kv_stage(bk, bkey, 1, kv_out(0, 1)))
            if not pre:
                wg, wk = self.wload(win_g[G_QKB], kwin[G_QKB], 8, 512)
                def ev_qb(j, bk, bkey):
                    self.cp("act", qbT[:, j, 0:n], bk[0:64, 0:n], (bkey,), (K("qbT", j),))
                def ev_kb(j, bk, bkey):
                    self.cp("act", kbT[:, j, 0:n], bk[0:64, 0:n], (bkey,), (K("kbT", j),))
                heads_fm(wg, wk, 0, 4, n, ev_qb)
                heads_fm(wg, wk, 256, 4, n, ev_kb)
            wg, wk = self.wload(wgk_g, kwgk, 8, 16)
            bk, bkey = self.bank()
            for kc in range(8):
                self.mm(bk[0:16, 0:n], wg(kc), hT[:, kc, 0:n], kc == 0, kc == 7, (wk, K("hT", kc)), (bkey,))
            self.cp("act", gkT[0:16, 0:n], bk[0:16, 0:n], (bkey,), (K("gkT"),))

        SK = K("Sst")
        def prompt_block(src_rows, ntile, tglob0, mode, dst_rows, kv_from, preloaded=False, nxt=None):
            n = ntile * 128
            pre = mode == "prefix"
            flagged = mode != "full"
            units = [(t * 128, 128) for t in range(ntile)]
            mark = lambda nm: self.marks.append((mode, tglob0, nm, len(S.ops)))
            mark("start")
            if not preloaded:
                swap_tiles(None, [src_rows[t * 128:(t + 1) * 128, :] for t in range(ntile)])
            rms_bcast(xT, n, xkeys(ntile))
            if flagged:
                modulate(n, ntile, "AmP", "BmP", lambda kc: AmP[:, kc:kc + 1], lambda kc: BmP[:, kc:kc + 1])
            else:
                modulate(n, ntile, "Am", "Bm", lambda kc: Am[:, kc, 0:1], lambda kc: Bm[:, kc, 0:1])
            if pre and nxt is not None:
                swap_tiles(None, nxt)
            mark("norm1")

            def kv_out(ui, which):
                tg = tglob0 + ui
                if mode != "full" or tg < LAST_KV0:
                    return None
                r0 = (tg - LAST_KV0) * 128
                return (kp_o if which == 0 else vp_o)[r0:r0 + 128, :]
            common_proj(n, units, tglob0, kv_from, pre, flagged, kv_out)
            mark("proj")
            if pre:
                gla_units(units, pre, lambda ui: (Sst, Sbf, SK, None, None))
                mark("gla")
                return
            def attn_all():
                for t in range(ntile):
                    attn_tile(tglob0 + t, t)
            gla_units(units, pre, lambda ui: (Sst, Sbf, SK, None, None), attn_all)
            mark("attn")
            merge_and_out(n)
            resid(n, ntile, lambda kc: Gm[:, kc, 0:1], "Gm")
            mark("merge")
            rms_bcast(xT, n, xkeys(ntile))
            if flagged:
                modulate(n, ntile, "AfP", "BfP", lambda kc: AfP[:, kc:kc + 1], lambda kc: BfP[:, kc:kc + 1])
            else:
                modulate(n, ntile, "Af", "Bf", lambda kc: Af[:, kc, 0:1], lambda kc: Bf[:, kc, 0:1])
            prefetch_x(nxt)
            ffn(n, [(0, n, 1, lambda jj: hist[:, jj:jj + 1, :], lambda jj: K("hist", jj), (), None)])
            mark("ffn")
            resid(n, ntile, lambda kc: Gf[:, kc, 0:1], "Gf")
            swap_tiles([dst_rows[t * 128:(t + 1) * 128, :] for t in range(ntile)], nxt)
            mark("end")

        def sample_block():
            n = 128
            slot = 0
            units = [(s * 32, 32) for s in range(4)]
            self.marks.append(("sample", 0, "start", len(S.ops)))
            rms_bcast(xT, n, xkeys(1))
            modulate_seq(Am, Bm, "Am", "Bm")
            smark = lambda nm: self.marks.append(("sample", 0, nm, len(S.ops)))
            smark("norm1")
            common_proj(n, units, slot, 0, False, False, lambda ui, which: (ks_o if which == 0 else vs_o))
            smark("proj")
            SSK = K("Sst")
            def S_of(ui):
                def before():
                    self.dma("sp", Ss[:], sgla[ui].rearrange("h k v -> k h v"), (), tuple(SSK + (h,) for h in range(4)), ("Ss",))
                    self.cp("act", Ssb[:], Ss[:], tuple(SSK + (h,) for h in range(4)), (SSK + ("b",),))
                def after():
                    self.dma("sp", glas_o[ui].rearrange("h k v -> k h v"), Ss[:], tuple(SSK + (h,) for h in range(4)), (), ("Ss",))
                return (Ss, Ssb, SSK, before, after)
            wgv, wkv = self.wload(win_g[G_VA], kwin[G_VA], 8, 512)
            for s in range(4):
                bk, bkey = self.bank()
                for kc in range(8):
                    self.mm(bk[0:32, :], hT[:, kc, s * 32:(s + 1) * 32], wgv(kc), kc == 0, kc == 7, (wkv, K("hT", kc)), (bkey,))
                self.cp("act", vown[:, s, :, 0:64], bk[0:32, :].rearrange("p (h e) -> p h e", h=8), (bkey,), (K("vown", s),))
                self.cp("dve", vown[:, s, :, 64], ones_f[0:32, 0:8], (K("ones_f"),), (K("vown1", s),))
            def attn_all():
                for s in range(4):
                    attn_sample(s, slot)
            gla_units(units, False, S_of, attn_all)
            smark("attn")
            merge_and_out(n)
            resid_seq(Gm, "Gm")
            smark("merge")
            rms_bcast(xT, n, xkeys(1))
            modulate_seq(Af, Bf, "Af", "Bf")
            for s in range(4):
                load_T(sconv[s], 88, hist_s[:, s, :, :].rearrange("p j t -> p t j"), "hist_s_in%d" % s,
                       view=lambda a: a.rearrange("p (t j) -> p t j", t=2))
            hs_in = tuple(K("hist_s_in%d" % s) for s in range(4))
            ffn(n, [(0, n, 4, lambda jj: hist_s[:, :, jj, :], lambda jj: K("hist_s", jj), hs_in, None)])
            smark("ffn")
            resid_seq(Gf, "Gf")
            store_y(ysam, 1)
            for s in range(4):
                store_hist(hist_s[:, s, :, :], tuple(K("hist_s", jj) for jj in range(44)), convs_o[s])

        def mod_mixed(At, Bt, AP, BP, Asc, Bsc, APk, BPk):
            for kc in range(8):
                xk = (K("xT", kc, 0), K("xT", kc, 1))
                v3 = lambda ap: ap.rearrange("p (s w) -> p s w", s=4)
                self.tt("dve", tmpf2[:, 0:256], xT[:, kc, 0:256], rbc[:, 0:256], ALU.mult, xk + (K("rbc"),), (K("tmpf2"),))
                self.act(hT[:, kc, 0:128], tmpf2[:, 0:128], AF.Identity, (K("tmpf2"),) + modkeys[APk] + modkeys[BPk], (K("hT", kc),),
                         bias=BP[:, kc:kc + 1], scale=AP[:, kc:kc + 1])
                self.tt("dve", v3(tmpf2[:, 128:256]), v3(tmpf2[:, 128:256]), At[:, kc, 1:5].to_broadcast([128, 4, 32]), ALU.mult,
                        (K("tmpf2"),) + modkeys[Asc], (K("tmpf2"),))
                self.tt("dve", v3(hT[:, kc, 128:256]), v3(tmpf2[:, 128:256]), Bt[:, kc, 1:5].to_broadcast([128, 4, 32]), ALU.add,
                        (K("tmpf2"), K("hT", kc)) + modkeys[Bsc], (K("hT", kc),))

        def resid_mixed(Gt, gname):
            v3 = lambda ap: ap.rearrange("p (s w) -> p s w", s=4)
            for kc in range(8):
                tb, tk = (tmpf2, K("tmpf2")) if kc % 2 == 0 else (tmpf, K("tmpf"))
                self.tt("pool", tb[:, 0:256], m2[:, kc, 0:256], rbc[:, 0:256], ALU.mult, (K("m2", kc), K("rbc")), (tk,))
                self.stt("dve", xT[:, kc, 0:128], tb[:, 0:128], Gt[:, kc, 0:1], xT[:, kc, 0:128], ALU.mult, ALU.add,
                         (tk, K("xT", kc, 0)) + modkeys[gname], (K("xT", kc, 0),))
                self.tt("dve", v3(tb[:, 128:256]), v3(tb[:, 128:256]), Gt[:, kc, 1:5].to_broadcast([128, 4, 32]), ALU.mult,
                        (tk,) + modkeys[gname], (tk,))
                self.tt("dve", xT[:, kc, 128:256], xT[:, kc, 128:256], tb[:, 128:256], ALU.add, (tk, K("xT", kc, 1)), (K("xT", kc, 1),))

        def mixed_block(nxt):
            n = 256
            SL, OVS = 0, 7
            mk = lambda nm: self.marks.append(("mixed", 15, nm, len(S.ops)))
            mk("start")
            rms_bcast(xT, n, xkeys(2))
            mod_mixed(Am, Bm, AmP, BmP, "Am", "Bm", "AmP", "BmP")
            mk("norm1")
            wg, wk = self.wload(win_g[G_QA], kwin[G_QA], 8, 512)
            def ev_q(j, bk, bkey):
                self.act(qaT[:, j, 0:n], bk[:, 0:n], AF.Identity, (bkey,), (K("qaT", j),), scale=0.125)
            pairs_fm(wg, wk, n, ev_q, kc_outer=True)
            wg, wk = self.wload(win_g[G_KA], kwin[G_KA], 8, 512)
            def ev_k(j, bk, bkey):
                self.cp("act", kaT[OVS][:, j, :], bk[:, 0:128], (bkey,), (K("kaT", OVS, j),))
                self.cp("act", kaT[SL][:, j, :], bk[:, 128:256], (bkey,), (K("kaT", SL, j),))
            pairs_fm(wg, wk, n, ev_k)
            units_tm(wg, wk, 0, 512, [(128, 128)], lambda i, bk, bkey: kv_stage(bk, bkey, 0, ks_o))
            wg, wk = self.wload(win_g[G_VA], kwin[G_VA], 8, 512)
            def ev_v(i, bk, bkey):
                self.cp("act", vaug[OVS][:, :, 0:64], bk[:].rearrange("p (h e) -> p h e", h=8), (bkey,), (K("vaug", OVS),))
                self.ts("dve", vaug[OVS][:, :, 64], ones_f[:, 0:8], flag[:, 0:1], None, ALU.mult, None,
                        (K("ones_f"), K("flag")), (K("vaug1", OVS),))
            units_tm(wg, wk, 0, 512, [(0, 128)], ev_v)
            units_tm(wg, wk, 0, 512, [(128, 128)], lambda i, bk, bkey: kv_stage(bk, bkey, 1, vs_o))
            for sq_ in range(4):
                bk, bkey = self.bank()
                c0 = 128 + sq_ * 32
                for kc in range(8):
                    self.mm(bk[0:32, :], hT[:, kc, c0:c0 + 32], wg(kc), kc == 0, kc == 7, (wk, K("hT", kc)), (bkey,))
                self.cp("act", vown[:, sq_, :, 0:64], bk[0:32, :].rearrange("p (h e) -> p h e", h=8), (bkey,), (K("vown", sq_),))
                self.cp("dve", vown[:, sq_, :, 64], ones_f[0:32, 0:8], (K("ones_f"),), (K("vown1", sq_),))
            wg, wk = self.wload(win_g[G_QKB], kwin[G_QKB], 8, 512)
            def ev_qb(j, bk, bkey):
                self.cp("act", qbT[:, j, 0:n], bk[0:64, 0:n], (bkey,), (K("qbT", j),))
            def ev_kb(j, bk, bkey):
                self.cp("act", kbT[:, j, 0:n], bk[0:64, 0:n], (bkey,), (K("kbT", j),))
            heads_fm(wg, wk, 0, 4, n, ev_qb)
            heads_fm(wg, wk, 256, 4, n, ev_kb)
            wg, wk = self.wload(wgk_g, kwgk, 8, 16)
            bk, bkey = self.bank()
            for kc in range(8):
                self.mm(bk[0:16, 0:n], wg(kc), hT[:, kc, 0:n], kc == 0, kc == 7, (wk, K("hT", kc)), (bkey,))
            self.cp("act", gkT[0:16, 0:n], bk[0:16, 0:n], (bkey,), (K("gkT"),))
            mk("proj")
            units = [(0, 128)] + [(128 + 32 * q, 32) for q in range(4)]
            SSK = K("Ss")
            def S_of(ui):
                if ui == 0:
                    return (Sst, Sbf, SK, None, None)
                sq_ = ui - 1
                def before():
                    self.dma("sp", Ss[:], sgla[sq_].rearrange("h k v -> k h v"), (), tuple(SSK + (h,) for h in range(4)), ("Ss",))
                    self.cp("act", Ssb[:], Ss[:], tuple(SSK + (h,) for h in range(4)), (SSK + ("b",),))
                def after():
                    self.dma("sp", glas_o[sq_].rearrange("h k v -> k h v"), Ss[:], tuple(SSK + (h,) for h in range(4)), (), ("Ss",))
                return (Ss, Ssb, SSK, before, after)
            def attn_all():
                attn_tile(15, 0)
                for q in range(4):
                    attn_sample(q, SL, 128)
            gla_units(units, False, S_of, attn_all)
            mk("attn")
            merge_and_out(n)
            resid_mixed(Gm, "Gm")
            mk("merge")
            rms_bcast(xT, n, xkeys(2))
            mod_mixed(Af, Bf, AfP, BfP, "Af", "Bf", "AfP", "BfP")
            for q in range(4):
                load_T(sconv[q], 88, hist_s[:, 4 + q, :, :].rearrange("p j t -> p t j"), "hist_s_in%d" % q,
                       view=lambda a: a.rearrange("p (t j) -> p t j", t=2))
            hs_in = tuple(K("hist_s_in%d" % q) for q in range(4)) + (K("hist_s_ov"),)
            prefetch_x(nxt)
            ffn(n, [(0, n, 8, lambda jj: hist_s[:, :, jj, :], lambda jj: K("hist_s", jj), hs_in, None)])
            mk("ffn")
            resid_mixed(Gf, "Gf")
            swap_tiles([yov[:, :], ysam[:, :]], nxt)
            mk("end")
            self.ts("dve", hist[:], hist_s[:, 3, :, :], flag[:, 0:1], None, ALU.mult, None,
                    tuple(K("hist_s", jj) for jj in range(44)) + (K("flag"),), tuple(K("hist", jj) for jj in range(44)))
            for q in range(4):
                store_hist(hist_s[:, 4 + q, :, :], tuple(K("hist_s", jj) for jj in range(44)), convs_o[q])

        tiles_of = lambda rows, nt: [rows[t * 128:(t + 1) * 128, :] for t in range(nt)]
        sched = []
        t0 = 0
        while t0 < 11:
            nt = min(NB, 11 - t0)
            sched.append(("prefix", xpre[t0 * 128:(t0 + nt) * 128, :], nt, t0, None, nt))
            t0 += nt
        while t0 < 15:
            nt = min(NB, 15 - t0)
            sched.append(("prefix", xpre[t0 * 128:(t0 + nt) * 128, :], nt, t0, None, 0))
            t0 += nt
        sched.append(("mixed", None, 2, 15, None, 0))
        for b in range(NMAIN // NB):
            sched.append(("full", xmain[b * NBT:(b + 1) * NBT, :], NB, 16 + b * NB, ymain[b * NBT:(b + 1) * NBT, :], 0))
        def tiles_for(e):
            return [xov[:, :], xsam[:, :]] if e[0] == "mixed" else tiles_of(e[1], e[2])
        for bi, (mode, src, nt, tg, dst, kvf) in enumerate(sched):
            nxt = tiles_for(sched[bi + 1]) if bi + 1 < len(sched) else None
            if mode == "mixed":
                ada_part2()
                mixed_block(nxt)
            else:
                prompt_block(src, nt, tg, mode, dst, kvf, preloaded=bi > 0, nxt=nxt)
        self.dma("sp", glap_o.rearrange("h k v -> k h v"), Sst[:], tuple(SK + (h,) for h in range(4)), (), ("glap",))
        store_hist(hist[:], tuple(K("hist", jj) for jj in range(44)), convp_o)
        return S.finalize()


_CACHE = {}


def _program():
    if "nc" not in _CACHE:
        b = Builder()
        b.build()
        _CACHE["nc"] = b.nc
    return _CACHE["nc"]


def kernel(x_prompt, x_sample, cache_k_a, cache_v_a, state_gla, state_conv, c_prompt, c_sample,
           w_ada, b_ada, g_pre_mix, g_post_mix, g_pre_ffn, g_post_ffn, w_in, w_gk2, b_gk,
           rel_bias, g_gla, w_br_a, w_br_b, w_out, w_up, w_dw, b_dw, w_down):
    f = lambda a: np.ascontiguousarray(np.asarray(a, dtype=np.float32))
    x_prompt, x_sample = f(x_prompt), f(x_sample)
    rb = f(rel_bias)[0]
    kk = np.arange(128)[:, None]
    qq = np.arange(128)[None, :]
    idx_prev = np.clip(qq + 128 - kk, -128, 128) + 128
    idx_own = np.clip(qq - kk, -128, 128) + 128
    btab = np.stack([rb[:, idx_prev].transpose(1, 0, 2), rb[:, idx_own].transpose(1, 0, 2)])
    cvec = np.broadcast_to(rb[:, 256][None, :], (128, 8))
    wi = f(w_in)[0]
    wi = np.concatenate([wi[:, :3072], wi[:, 3088:], wi[:, 3072:3088]], axis=1)
    wu = f(w_up)[0]
    cols = []
    for g in range(11):
        cols += [wu[:, 2 * g * 128:(2 * g + 2) * 128], wu[:, DFF + 2 * g * 128:DFF + (2 * g + 2) * 128]]
    wu = np.concatenate(cols, axis=1)
    shared = {
        "w_ada": f(w_ada)[0], "b_ada": f(b_ada)[0].reshape(48, 128),
        "gvec": np.concatenate([f(g_pre_mix)[0], f(g_post_mix)[0], f(g_pre_ffn)[0], f(g_post_ffn)[0]]).reshape(32, 128),
        "w_in": f(wi), "w_gk2": f(w_gk2)[0], "b_gk": f(b_gk)[0].reshape(1, 256),
        "btab": f(btab), "cvec": f(cvec), "ggla": f(np.broadcast_to(np.tile(f(g_gla)[0], 4)[None, :], (128, 512))),
        "w_br_a": f(w_br_a)[0], "w_br_b": f(w_br_b)[0], "w_out": f(w_out)[0], "w_up": f(wu),
        "w_dw": f(w_dw)[0].reshape(3 * 44, 128), "b_dw": f(b_dw)[0].reshape(44, 128), "w_down": f(w_down)[0],
    }
    in_maps = []
    for c in range(8):
        b, hf = c // 2, c % 2
        m = dict(shared)
        if hf == 1:
            m["xpre"] = x_prompt[b, 0:NPRE * 128]
            m["xov"] = x_prompt[b, NPRE * 128:2048]
        else:
            m["xpre"] = np.zeros((NPRE * 128, D), np.float32)
            m["xov"] = np.zeros((128, D), np.float32)
        m["xmain"] = x_prompt[b, hf * 2048:(hf + 1) * 2048]
        m["xsam"] = x_sample[4 * c:4 * c + 4].reshape(128, D)
        m["crow"] = f(np.concatenate([f(c_prompt)[b:b + 1], f(c_sample)[4 * c:4 * c + 4]], 0).reshape(40, 128))
        m["flag"] = np.full((128, 1), float(hf), np.float32)
        m["ck"] = f(cache_k_a)[0, 4 * c:4 * c + 4].reshape(4, 512, 512)
        m["cv"] = f(cache_v_a)[0, 4 * c:4 * c + 4].reshape(4, 512, 512)
        m["sgla"] = f(state_gla)[0, 4 * c:4 * c + 4]
        m["sconv"] = f(state_conv)[0, 4 * c:4 * c + 4].reshape(4, 88, 128)
        in_maps.append({k: np.ascontiguousarray(v) for k, v in m.items()})
    nc = _program()
    cores = [int(t) for t in os.environ.get("KCORES", "0,1,2,3,4,5,6,7").split(",")]
    if os.environ.get("KTRACE"):
        res = run_bass_kernel_spmd(nc, [in_maps[c] for c in cores], core_ids=list(range(len(cores))), trace=True)
        print("EXEC_TIME_NS", res.exec_time_ns)
    else:
        res = run_bass_kernel_spmd(nc, [in_maps[c] for c in cores], core_ids=list(range(len(cores))))
    R = {c: res.results[i] for i, c in enumerate(cores)}
    y_prompt = np.zeros((4, 4096, D), np.float32)
    y_sample = np.zeros((32, 32, D), np.float32)
    k_p = np.zeros((1, 4, 512, 8, 64), np.float32); v_p = np.zeros_like(k_p)
    gla_p = np.zeros((1, 4, 4, 64, 128), np.float32)
    conv_p = np.zeros((1, 4, 2, 2 * DFF), np.float32)
    k_s = np.zeros((1, 32, 32, 8, 64), np.float32); v_s = np.zeros_like(k_s)
    gla_s = np.zeros((1, 32, 4, 64, 128), np.float32)
    conv_s = np.zeros((1, 32, 2, 2 * DFF), np.float32)
    for c in cores:
        b, hf = c // 2, c % 2
        r = R[c]
        y_prompt[b, hf * 2048:(hf + 1) * 2048] = r["ymain"]
        y_sample[4 * c:4 * c + 4] = r["ysam"].reshape(4, 32, D)
        if hf == 1:
            k_p[0, b] = r["kp"].reshape(512, 8, 64)
            v_p[0, b] = r["vp"].reshape(512, 8, 64)
            gla_p[0, b] = r["glap"]
            conv_p[0, b] = r["convp"].reshape(2, 2 * DFF)
        k_s[0, 4 * c:4 * c + 4] = r["ks"].reshape(4, 32, 8, 64)
        v_s[0, 4 * c:4 * c + 4] = r["vs"].reshape(4, 32, 8, 64)
        gla_s[0, 4 * c:4 * c + 4] = r["glas"]
        conv_s[0, 4 * c:4 * c + 4] = r["convs"].reshape(4, 2, 2 * DFF)
    return (y_prompt, y_sample, k_p, v_p, gla_p, conv_p, k_s, v_s, gla_s, conv_s)
```

```python
from contextlib import ExitStack
import os

import numpy as np
import concourse.bass as bass
import concourse.mybir as mybir
from concourse.bass_utils import run_bass_kernel_spmd

F32 = mybir.dt.float32
BF16 = mybir.dt.bfloat16
AF = mybir.ActivationFunctionType
ALU = mybir.AluOpType

D = 1024
KC = 8
DFF = 2816
NFF = 22
DIN = 5136
G_QA, G_KA, G_VA, G_QKB, G_VB, G_GB, G_GA, G_GBR = 0, 1, 2, 3, 4, 5, 6, 8
EPS = 1e-6
NEG = -30000.0
NPRE = 15
NMAIN = 16
NB = 4


class Op:
    __slots__ = ("eng", "fn", "reads", "writes", "dsem", "signal", "sigval", "deps", "rdeps")

    def __init__(self, eng, fn, reads, writes, dsem):
        self.eng, self.fn, self.reads, self.writes, self.dsem = eng, fn, reads, writes, dsem
        self.signal = False
        self.sigval = 0
        self.deps = ()
        self.rdeps = frozenset()


class Sched:
    DMA = ("sp", "pool_dma")

    def __init__(self, nc, stack):
        self.nc = nc
        self.stack = stack
        self.ops = []
        self.eng_obj = {"pe": nc.tensor, "act": nc.scalar, "dve": nc.vector, "pool": nc.gpsimd,
                        "sp": nc.sync, "pool_dma": nc.gpsimd}
        self.wuses = []
        self.wdepth = 2

    def add(self, eng, fn, reads=(), writes=(), dsem=None):
        op = Op(eng, fn, tuple(reads), tuple(writes), dsem)
        self.ops.append(op)
        return op

    def queue_of(self, op):
        return "pool" if op.eng == "pool_dma" else op.eng

    def finalize(self):
        nc = self.nc
        inserts = {}
        lastrd = {}
        for idx, op in enumerate(self.ops):
            for k in op.reads:
                if k and k[0] == "wuse":
                    lastrd[k[1]] = idx
        prev = 0
        for i, (pos, op) in enumerate(self.wuses):
            tgt = self.wuses[max(0, i - self.wdepth)][0]
            if i >= 3 and (i - 3) in lastrd:
                tgt = max(tgt, lastrd[i - 3] + 1)
            tgt = max(tgt, prev)
            prev = tgt
            assert tgt <= pos, (i, tgt, pos)
            inserts.setdefault(tgt, []).append(op)
        ops = []
        for i, op in enumerate(self.ops):
            if i in inserts:
                ops.extend(inserts[i])
            ops.append(op)
        self.ops = ops
        last_w = {}
        readers = {}
        for i, op in enumerate(ops):
            deps = set()
            q = self.queue_of(op)
            isdma = op.dsem is not None
            for k in op.reads:
                j = last_w.get(k)
                if j is not None:
                    deps.add(j)
            rdeps = set(deps)
            for k in op.writes:
                j = last_w.get(k)
                if j is not None:
                    oj = ops[j]
                    if isdma or oj.dsem is not None or self.queue_of(oj) != q or q != "pe":
                        deps.add(j)
                for j in readers.get(k, ()):
                    oj = ops[j]
                    if isdma or oj.dsem is not None or self.queue_of(oj) != q or (q != "pe" and os.environ.get("KWAR", "1") == "1"):
                        deps.add(j)
            deps.discard(i)
            op.deps = tuple(sorted(deps))
            op.rdeps = frozenset(rdeps)
            for j in op.deps:
                ops[j].signal = True
            for k in op.reads:
                lst = readers.setdefault(k, [])
                if op.dsem is None:
                    lst[:] = [j for j in lst if ops[j].dsem is not None or self.queue_of(ops[j]) != q]
                lst.append(i)
            for k in op.writes:
                last_w[k] = i
                readers[k] = []
        esem = {}
        for e in ("pe", "act", "dve", "pool"):
            esem[e] = self.stack.enter_context(nc.semaphore("sem_" + e))
        dsems = {}
        cnt = {}
        for op in ops:
            if op.dsem is not None:
                if op.dsem not in dsems:
                    dsems[op.dsem] = self.stack.enter_context(nc.semaphore("dsem_%d" % len(dsems)))
                    cnt[op.dsem] = 0
                cnt[op.dsem] += 1
                op.sigval = 16 * cnt[op.dsem]
            elif op.signal:
                q = self.queue_of(op)
                cnt[q] = cnt.get(q, 0) + 1
                op.sigval = cnt[q]
        known = {q: {} for q in ("pe", "act", "dve", "pool", "sp")}
        for op in ops:
            q = self.queue_of(op)
            eng = self.eng_obj[op.eng]
            need = {}
            need_r = {}
            for j in op.deps:
                oj = ops[j]
                s = dsems[oj.dsem] if oj.dsem is not None else esem[self.queue_of(oj)]
                key = id(s)
                if key not in need or need[key][1] < oj.sigval:
                    need[key] = (s, oj.sigval)
                if j in op.rdeps and need_r.get(key, 0) < oj.sigval:
                    need_r[key] = oj.sigval
            pending = [(key, s_, v) for key, (s_, v) in need.items() if known[q].get(key, 0) < v]
            fuse = None
            if pending and op.dsem is None and q in ("act", "dve", "pool") and os.environ.get("KFUSE", "1") == "1":
                fuse = pending.pop()
            elif pending and q == "pe" and os.environ.get("KFUSEPE", "1") == "1":
                for pi, (key, s_, v) in enumerate(pending):
                    if need_r.get(key, 0) <= known[q].get(key, 0):
                        fuse = pending.pop(pi)
                        break
            for key, s_, v in pending:
                eng.wait_ge(s_, v)
                known[q][key] = v
            ins = op.fn(eng)
            if fuse is not None:
                ins._wait_ge(fuse[1], fuse[2])
                known[q][fuse[0]] = fuse[2]
            if op.dsem is not None:
                ins.then_inc(dsems[op.dsem], 16)
            elif op.signal:
                ins.then_inc(esem[q], 1)
        self.counts = dict((str(k), v) for k, v in cnt.items())
        for k, s in dsems.items():
            nc.sync.wait_ge(s, 16 * cnt[k])
        return len(ops)


class Builder:
    def __init__(self):
        self.stack = ExitStack()
        self.nc = bass.Bass("TRN2", target_bir_lowering=False)
        self.S = Sched(self.nc, self.stack)
        self.nbank = 0
        self.wuse_n = 0
        self.prefetched = False
        self.bank_set = [0, 1, 2, 3, 4]
        self.marks = []
        self.uid = 0

    def din(self, name, shape, dt=F32):
        return self.nc.dram_tensor(name, list(shape), dt, kind="ExternalInput").ap()

    def dout(self, name, shape):
        return self.nc.dram_tensor(name, list(shape), F32, kind="ExternalOutput").ap()

    def dscr(self, name, shape, dt=BF16):
        return self.nc.dram_tensor(name, list(shape), dt, kind="Internal").ap()

    def sb(self, name, shape, dt=F32):
        return self.stack.enter_context(self.nc.sbuf_tensor(name, list(shape), dt))

    def ps(self, name, shape, dt=F32):
        return self.stack.enter_context(self.nc.psum_tensor(name, list(shape), dt))

    def bank(self):
        bs = self.bank_set
        i = bs[self.nbank % len(bs)]
        self.nbank += 1
        return self.banks[i], ("ps", i)

    def key(self, name):
        self.uid += 1
        return (name, self.uid)

    def mm(self, out, lhsT, rhs, start, stop, reads, writes):
        self.S.add("pe", lambda e: e.matmul(out, lhsT, rhs, start=start, stop=stop), reads, writes)

    def tr(self, out, in_, ident, reads, writes):
        self.S.add("pe", lambda e: e.transpose(out, in_, ident), reads, writes)

    def act(self, out, in_, func, reads, writes, bias=None, scale=None, accum_out=None):
        kw = {}
        if bias is not None:
            kw["bias"] = bias
        if scale is not None:
            kw["scale"] = scale
        if accum_out is not None:
            kw["accum_out"] = accum_out
        self.S.add("act", lambda e: e.activation(out, in_, func, **kw), reads, writes)

    def tt(self, eng, out, in0, in1, op, reads, writes):
        self.S.add(eng, lambda e: e.tensor_tensor(out, in0, in1, op), reads, writes)

    def ts(self, eng, out, in0, s1, s2, op0, op1, reads, writes):
        if s2 is None:
            self.S.add(eng, lambda e: e.tensor_scalar(out, in0, s1, None, op0), reads, writes)
        else:
            self.S.add(eng, lambda e: e.tensor_scalar(out, in0, s1, s2, op0, op1), reads, writes)

    def stt(self, eng, out, in0, scalar, in1, op0, op1, reads, writes):
        self.S.add(eng, lambda e: e.scalar_tensor_tensor(out, in0, scalar, in1, op0=op0, op1=op1), reads, writes)

    def cp(self, eng, out, in_, reads, writes):
        if eng == "act":
            self.S.add("act", lambda e: e.copy(out, in_), reads, writes)
        else:
            self.S.add(eng, lambda e: e.tensor_copy(out, in_), reads, writes)

    def memset(self, eng, ap, val, writes):
        self.S.add(eng, lambda e: e.memset(ap, val), (), writes)

    def dma(self, q, out, in_, reads, writes, dsem, slow=False):
        if slow:
            self.S.add(q, lambda e: e.dma_start(out=out, in_=in_, allow_slow_non_contiguous=True), reads, writes, dsem)
        else:
            self.S.add(q, lambda e: e.dma_start(out=out, in_=in_), reads, writes, dsem)

    def interleave(self, fns_banks):
        main = self.S.ops
        streams = []
        for fn, banks in fns_banks:
            self.S.ops = []
            self.bank_set = banks
            fn()
            streams.append(self.S.ops)
        self.S.ops = main
        self.bank_set = [0, 1, 2, 3, 4, 5, 6]
        idx = [0] * len(streams)
        total = sum(len(st) for st in streams)
        for _ in range(total):
            best, bf = None, None
            for si, st in enumerate(streams):
                if idx[si] < len(st):
                    frac = idx[si] / len(st)
                    if bf is None or frac < bf:
                        best, bf = si, frac
            main.append(streams[best][idx[best]])
            idx[best] += 1

    def wload(self, scr_g, wkeys, kcn, width):
        i = self.wuse_n
        self.wuse_n += 1
        slot = i % 3
        buf = self.wbufs[slot]
        key = ("wuse", i)
        dst = buf[:, 0:kcn * width].rearrange("p (kc n) -> p kc n", kc=kcn)
        op = Op("sp", lambda e: e.dma_start(out=dst, in_=scr_g), tuple(wkeys),
                (key, ("wbuf", slot)) + ((("wuse", i - 3),) if i >= 3 else ()), ("wstream", slot))
        self.S.wuses.append((len(self.S.ops), op))
        return (lambda kc, a=0, b=width: buf[:, kc * width + a: kc * width + b]), key

    def build(self):
        nc = self.nc
        S = self.S
        sb, ps = self.sb, self.ps
        NBT = NB * 128
        K = lambda *a: tuple(a)
        xpre = self.din("xpre", [NPRE * 128, D])
        xov = self.din("xov", [128, D])
        xmain = self.din("xmain", [NMAIN * 128, D])
        xsam = self.din("xsam", [128, D])
        crow = self.din("crow", [40, 128])
        flag_d = self.din("flag", [128, 1])
        ck = self.din("ck", [4, 512, 512])
        cv = self.din("cv", [4, 512, 512])
        sgla = self.din("sgla", [4, 4, 64, 128])
        sconv = self.din("sconv", [4, 88, 128])
        w_ada = self.din("w_ada", [D, 6 * D])
        b_ada = self.din("b_ada", [48, 128])
        gvec = self.din("gvec", [32, 128])
        w_in = self.din("w_in", [D, DIN])
        w_gk2 = self.din("w_gk2", [16, 256])
        b_gk = self.din("b_gk", [1, 256])
        btab = self.din("btab", [2, 128, 8, 128])
        cvec = self.din("cvec", [128, 8])
        ggla = self.din("ggla", [128, 512])
        w_br_a = self.din("w_br_a", [512, D])
        w_br_b = self.din("w_br_b", [512, D])
        w_out = self.din("w_out", [D, D])
        w_up = self.din("w_up", [D, 2 * DFF])
        w_dw = self.din("w_dw", [3 * 44, 128])
        b_dw = self.din("b_dw", [44, 128])
        w_down = self.din("w_down", [DFF, D])

        ymain = self.dout("ymain", [NMAIN * 128, D])
        ysam = self.dout("ysam", [128, D])
        yov = self.dout("yov", [128, D])
        kp_o = self.dout("kp", [512, 512])
        vp_o = self.dout("vp", [512, 512])
        glap_o = self.dout("glap", [4, 64, 128])
        convp_o = self.dout("convp", [88, 128])
        ks_o = self.dout("ks", [128, 512])
        vs_o = self.dout("vs", [128, 512])
        glas_o = self.dout("glas", [4, 4, 64, 128])
        convs_o = self.dout("convs", [4, 88, 128])

        win_g = self.dscr("win_g", [10, 128, 8, 512])
        wgk_g = self.dscr("wgk_g", [128, 8, 16])
        wbra_g = self.dscr("wbra_g", [2, 128, 4, 512])
        wbrb_g = self.dscr("wbrb_g", [2, 128, 4, 512])
        wout_g = self.dscr("wout_g", [2, 128, 8, 512])
        wup_g = self.dscr("wup_g", [11, 128, 8, 512])
        wdown_g = self.dscr("wdown_g", [8, 128, NFF, 128])

        self.banks = [ps("bank%d" % i, [128, 512]) for i in range(5)]
        obank = [ps("obank%d" % i, [128, 512]) for i in range(2)]
        self.banks = self.banks + obank
        self.bank_set = [0, 1, 2, 3, 4, 5, 6]
        pbf = ps("pbf", [128, 1024], BF16)

        self.wbufs = [sb("wbuf%d" % i, [128, 4096], BF16) for i in range(3)]
        ident_f = sb("ident_f", [128, 128])
        ident_b = sb("ident_b", [128, 128], BF16)
        ones_b = sb("ones_b", [128, 128], BF16)
        triu_f = sb("triu_f", [128, 128])
        trisl_f = sb("trisl_f", [128, 128])
        ones_f = sb("ones_f", [128, 8])
        epsb = sb("epsb", [128, 1])
        mod = sb("mod", [128, 6, KC, 5])
        Am = sb("Am", [128, KC, 5]); Bm = sb("Bm", [128, KC, 5]); Gm = sb("Gm", [128, KC, 5])
        Af = sb("Af", [128, KC, 5]); Bf = sb("Bf", [128, KC, 5]); Gf = sb("Gf", [128, KC, 5])
        AmP = sb("AmP", [128, KC]); BmP = sb("BmP", [128, KC])
        AfP = sb("AfP", [128, KC]); BfP = sb("BfP", [128, KC])
        gT = sb("gT", [128, 32])
        badaT = sb("badaT", [128, 48])
        cT = sb("cT", [128, 40])
        siluT = sb("siluT", [128, 40], BF16)
        flag = sb("flag_s", [128, 1])
        wdwT = sb("wdwT", [128, 3, 44])
        bdwT = sb("bdwT", [128, 44])
        wgk_f = sb("wgk_f", [17, 256])
        wgk_b = sb("wgk_b", [17, 256], BF16)
        tab_prev = sb("tab_prev", [128, 8, 128], BF16)
        tab_own = sb("tab_own", [128, 8, 128], BF16)
        tab_mask = sb("tab_mask", [128, 128], BF16)
        cvec_s = sb("cvec_s", [128, 8])
        ggla_s = sb("ggla_s", [128, 512])
        stage = sb("stage", [128, 128])
        hist = sb("hist", [128, 44, 2])
        hist_s = sb("hist_s", [128, 8, 44, 2])
        h88 = sb("h88", [128, 2, 44])
        xin = [sb("xin%d" % i, [128, D]) for i in range(2)]
        xT = sb("xT", [128, KC, NBT])
        hT = sb("hT", [128, KC, NBT], BF16)
        sq = sb("sq", [128, 2, NBT], BF16)
        rbc = sb("rbc", [128, NBT])
        tmpf = sb("tmpf", [128, NBT])
        tmpf2 = sb("tmpf2", [128, NBT])
        qaT = sb("qaT", [128, 4, NBT], BF16)
        kaT = [sb("kaT%d" % i, [128, 4, 128], BF16) for i in range(8)]
        vaug = [sb("vaug%d" % i, [128, 8, 65], BF16) for i in range(8)]
        qbT = sb("qbT", [64, 4, NBT], BF16)
        kbT = sb("kbT", [64, 4, NBT], BF16)
        gkT = sb("gkT", [32, NBT], BF16)
        pT_raw = sb("pT_raw", [128, 2560])
        pT = pT_raw.bitcast(BF16)[:, :].rearrange("p (k h q) -> p k h q", k=5, h=8)
        ya_tok = sb("ya_tok", [128, 512], BF16)
        rden = sb("rden", [128, 8])
        kb_tok = sb("kb_tok", [128, 256])
        vb_tok2 = [sb("vb_tok%d" % i, [128, 4, 128], BF16) for i in range(2)]
        gb2_2 = [sb("gb2_%d" % i, [128, 512]) for i in range(2)]
        gtanh = sb("gtanh", [128, 512])
        Lsp = sb("Lsp", [128, 256])
        e_sb = sb("e_sb", [128, 256])
        e1 = sb("e1", [64, 4, 128]); e2 = sb("e2", [64, 4, 128])
        qtT2 = [sb("qtT%d" % i, [64, 4, 128], BF16) for i in range(2)]; ktT = sb("ktT", [64, 4, 128], BF16)
        kend2 = [sb("kend%d" % i, [128, 256], BF16) for i in range(2)]
        dec2 = [sb("dec%d" % i, [64, 4]) for i in range(2)]
        attT2 = [sb("attT%d" % i, [128, 4, 128], BF16) for i in range(2)]
        Sst = sb("Sst", [64, 4, 128])
        Sbf = sb("Sbf", [64, 4, 128], BF16)
        ssq = sb("ssq", [128, 4]); rgl = sb("rgl", [128, 4])
        yb_tok = sb("yb_tok", [128, 512], BF16)
        sgA = sb("sgA", [128, NBT]); sgB = sb("sgB", [128, NBT])
        m2 = sb("m2", [128, KC, NBT])
        ua = [sb("ua%d" % i, [128, NBT + 8]) for i in range(2)]
        y0 = [sb("y0_%d" % i, [128, NBT]) for i in range(2)]
        ua_sets = [ua, [pT_raw[:, 0:NBT + 8], pT_raw[:, NBT + 8:2 * NBT + 16]]]
        y0_sets = [y0, [pT_raw[:, 2 * NBT + 16:3 * NBT + 16], pT_raw[:, 3 * NBT + 16:4 * NBT + 16]]]
        actT = sb("actT", [128, NFF, NBT], BF16)
        mergedT = lambda c: actT[:, c, :]
        MK = lambda c: K("actT", c)
        yaT = lambda c: actT[:, 8 + c, :]
        YAK = tuple(K("actT", 8 + c) for c in range(4))
        ybT = lambda c: actT[:, 12 + c, :]
        YBK = tuple(K("actT", 12 + c) for c in range(4))
        m2b = m2.bitcast(BF16)
        kcT = lambda c: m2b[:, c, 0:512]
        vcaug = sb("vcaug", [128, 4, 8, 65], BF16)
        vown = sb("vown", [32, 4, 8, 65], BF16)
        Ss = sb("Ss", [64, 4, 128]); Ssb = sb("Ssb", [64, 4, 128], BF16)

        def const_mask(t, keyname, pattern, cm, base, cmp_op):
            self.memset("pool", t[:], 1.0, (K(keyname),))
            S.add("pool", lambda e: e.affine_select(t[:], t[:], pattern=pattern, compare_op=cmp_op, fill=0.0,
                                                    base=base, channel_multiplier=cm), (K(keyname),), (K(keyname),))
        self.memset("pool", ident_f[:], 0.0, (K("ident_f"),))
        S.add("pool", lambda e: e.affine_select(ident_f[:], ident_f[:], pattern=[[-1, 128]], compare_op=ALU.not_equal,
                                                fill=1.0, base=0, channel_multiplier=1), (K("ident_f"),), (K("ident_f"),))
        const_mask(triu_f, "triu_f", [[1, 128]], -1, 0, ALU.is_ge)
        const_mask(trisl_f, "trisl_f", [[-1, 128]], 1, -1, ALU.is_ge)
        self.cp("dve", ident_b[:], ident_f[:], (K("ident_f"),), (K("ident_b"),))
        self.memset("dve", ones_b[:], 1.0, (K("ones_b"),))
        self.memset("dve", ones_f[:], 1.0, (K("ones_f"),))
        self.memset("dve", epsb[:], EPS, (K("epsb"),))
        self.memset("dve", gkT[:], 1.0, (K("gkT_ones"),))
        self.memset("dve", hist[:], 0.0, tuple(K("hist", jj) for jj in range(44)))
        self.memset("dve", hist_s[:, 0:4, :, :], 0.0, (K("hist_s_ov"),))
        self.memset("dve", Sst[:], 0.0, tuple(K("Sst", h) for h in range(4)))
        self.memset("dve", Sbf[:], 0.0, (K("Sst", "b"),))

        def load_T(src2d, rows, dst, kname, view=None):
            self.dma("pool_dma", stage[0:rows, :], src2d, (), (K("stage"),), ("stage",))
            bk, bkey = self.bank()
            self.tr(bk[:, 0:rows], stage[0:rows, :], ident_f[0:rows, 0:rows], (K("stage"), K("ident_f")), (bkey,))
            src = bk[:, 0:rows] if view is None else view(bk[:, 0:rows])
            self.cp("dve", dst, src, (bkey,), (K(kname),))

        load_T(gvec, 32, gT[:], "gT")
        load_T(b_ada, 48, badaT[:], "badaT")
        load_T(crow, 40, cT[:], "cT")
        for j in range(3):
            load_T(w_dw[j * 44:(j + 1) * 44, :], 44, wdwT[:, j, :], "wdwT%d" % j)
        load_T(b_dw, 44, bdwT[:], "bdwT")
        self.dma("pool_dma", flag[:], flag_d, (), (K("flag"),), ("misc", 0))
        self.dma("pool_dma", wgk_f[0:16, :], w_gk2, (), (K("wgk_f0"),), ("misc", 1))
        self.dma("pool_dma", wgk_f[16:17, :], b_gk, (), (K("wgk_f1"),), ("misc", 2))
        self.cp("dve", wgk_b[:], wgk_f[:], (K("wgk_f0"), K("wgk_f1")), (K("wgk_b"),))
        self.dma("pool_dma", cvec_s[:], cvec, (), (K("cvec"),), ("misc", 3))
        self.dma("pool_dma", ggla_s[:], ggla, (), (K("ggla"),), ("misc", 4))
        self.ts("dve", ggla_s[:], ggla_s[:], 0.5, None, ALU.mult, None, (K("ggla"),), (K("ggla"),))
        tabf = xin[0][:, :].rearrange("p (h q) -> p h q", h=8)
        for ti, tdst in ((0, tab_prev), (1, tab_own)):
            self.dma("pool_dma", tabf, btab[ti], (), (K("xin", 0), K("xinv", 0)), ("misc", 5))
            for h in range(8):
                self.ts("dve", tdst[:, h, :], tabf[:, h, :], cvec_s[:, h:h + 1], None, ALU.subtract, None,
                        (K("xin", 0), K("xinv", 0), K("cvec")), (K("tab", ti, h),))
        self.memset("dve", tab_own[64:128, :, 0:64], NEG, tuple(K("tab", 1, h) for h in range(8)))
        for ti, tdst in ((0, tab_prev), (1, tab_own)):
            self.act(tdst[:], tdst[:], AF.Exp, tuple(K("tab", ti, h) for h in range(8)), (K("etab", ti),))
        self.memset("dve", tab_mask[:], 1.0, (K("tab_mask"),))
        self.memset("dve", tab_mask[0:64, 64:128], 0.0, (K("tab_mask"),))

        def cast_group(src_cols, dst_g, name, g):
            self.dma("pool_dma", dst_g, src_cols.rearrange("(kc p) n -> p kc n", p=128), (), (K("scr", name, g),), ("cast", name, g))
            return (K("scr", name, g),)

        self.act(tmpf[:, 0:40], cT[:], AF.Tanh, (K("cT"),), (K("tmpf"),), scale=0.5)
        self.ts("dve", tmpf[:, 0:40], tmpf[:, 0:40], 0.5, 0.5, ALU.mult, ALU.add, (K("tmpf"),), (K("tmpf"),))
        self.tt("dve", siluT[:], tmpf[:, 0:40], cT[:], ALU.mult, (K("tmpf"), K("cT")), (K("siluT"),))
        siluv = siluT[:].rearrange("p (s k) -> p k s", k=8)
        modkeys = {}
        k8 = lambda n: tuple(K(n, kc) for kc in range(8))

        def ada_groups(glist):
            for g in glist:
                sl = g % 2
                buf = actT[:, sl * 8:(sl + 1) * 8, :].rearrange("p c n -> p (c n)")
                bkeys = tuple(K("actT", sl * 8 + c) for c in range(8))
                self.dma("pool_dma", buf.rearrange("p (kc n) -> p kc n", kc=8),
                         w_ada.rearrange("(kc p) n -> p kc n", p=128)[:, :, g * 512:(g + 1) * 512], (), bkeys, ("ada", sl))
                bk, bkey = self.bank()
                for oc in range(4):
                    for kc in range(8):
                        self.mm(bk[:, oc * 8:oc * 8 + 5], buf[:, kc * 512 + oc * 128: kc * 512 + (oc + 1) * 128], siluv[:, kc, :],
                                kc == 0, kc == 7, bkeys + (K("siluT"),), (bkey,))
                for oc in range(4):
                    ch = g * 4 + oc
                    self.ts("dve", mod[:, ch // 8, ch % 8, :], bk[:, oc * 8:oc * 8 + 5], badaT[:, ch:ch + 1], None, ALU.add, None,
                            (bkey, K("badaT")), (K("mod", ch),))

        ada_groups(range(0, 4))
        mk1 = tuple(K("mod", ch) for ch in range(16))
        for kc in range(8):
            self.ts("dve", Am[:, kc, :], mod[:, 1, kc, :], 1.0, gT[:, kc:kc + 1], ALU.add, ALU.mult, mk1 + (K("gT"),), (K("Am", kc),))
        self.cp("dve", Bm[:], mod[:, 0, :, :], mk1, (K("Bm"),))
        self.ts("dve", AmP[:], Am[:, :, 0], flag[:, 0:1], None, ALU.mult, None, k8("Am") + (K("flag"),), (K("AmP"),))
        self.ts("dve", BmP[:], Bm[:, :, 0], flag[:, 0:1], None, ALU.mult, None, (K("Bm"), K("flag")), (K("BmP"),))
        modkeys.update({"Am": k8("Am"), "Bm": (K("Bm"),), "AmP": (K("AmP"),), "BmP": (K("BmP"),)})

        def ada_part2():
            ada_groups(range(4, 12))
            mk2 = tuple(K("mod", ch) for ch in range(16, 48))
            for kc in range(8):
                rw = mk2 + (K("gT"),)
                self.ts("dve", Af[:, kc, :], mod[:, 4, kc, :], 1.0, gT[:, 16 + kc:17 + kc], ALU.add, ALU.mult, rw, (K("Af", kc),))
                self.ts("dve", Gm[:, kc, :], mod[:, 2, kc, :], gT[:, 8 + kc:9 + kc], None, ALU.mult, None, rw, (K("Gm", kc),))
                self.ts("dve", Gf[:, kc, :], mod[:, 5, kc, :], gT[:, 24 + kc:25 + kc], None, ALU.mult, None, rw, (K("Gf", kc),))
            self.cp("dve", Bf[:], mod[:, 3, :, :], mk2, (K("Bf"),))
            self.ts("dve", AfP[:], Af[:, :, 0], flag[:, 0:1], None, ALU.mult, None, k8("Af") + (K("flag"),), (K("AfP"),))
            self.ts("dve", BfP[:], Bf[:, :, 0], flag[:, 0:1], None, ALU.mult, None, (K("Bf"), K("flag")), (K("BfP"),))
        modkeys.update({"Af": k8("Af"), "Gm": k8("Gm"), "Gf": k8("Gf"), "Bf": (K("Bf"),), "AfP": (K("AfP"),), "BfP": (K("BfP"),)})

        kwin = {}
        for g in (3, 4, 1, 2):
            kwin[g] = cast_group(w_in[:, g * 512:(g + 1) * 512], win_g[g], "win", g)
        self.dma("pool_dma", wgk_g, w_in.rearrange("(kc p) n -> p kc n", p=128)[:, :, 5120:5136], (), (K("scr", "wgk"),), ("cast", "wgk"))
        kwgk = (K("scr", "wgk"),)
        for g in (0, 5, 6, 7, 8, 9):
            kwin[g] = cast_group(w_in[:, g * 512:(g + 1) * 512], win_g[g], "win", g)
        kwbra = [cast_group(w_br_a[:, g * 512:(g + 1) * 512], wbra_g[g], "wbra", g) for g in range(2)]
        kwbrb = [cast_group(w_br_b[:, g * 512:(g + 1) * 512], wbrb_g[g], "wbrb", g) for g in range(2)]
        kwout = [cast_group(w_out[:, g * 512:(g + 1) * 512], wout_g[g], "wout", g) for g in range(2)]
        kwup = [cast_group(w_up[:, g * 512:(g + 1) * 512], wup_g[g], "wup", g) for g in range(11)]
        kwdown = [cast_group(w_down[:, c * 128:(c + 1) * 128], wdown_g[c], "wdown", c) for c in range(8)]

        def load_xT(src_rows, ntile):
            for t in range(ntile):
                xb = xin[t % 2]
                kx = K("xin", t % 2)
                self.dma("sp", xb[:], src_rows[t * 128:(t + 1) * 128, :], (), (kx, K("xinv", t % 2)), ("xin", t % 2))
                for half in range(2):
                    bk, bkey = self.bank()
                    for j in range(4):
                        kc = half * 4 + j
                        self.tr(bk[:, j * 128:(j + 1) * 128], xb[:, kc * 128:(kc + 1) * 128], ident_f[:],
                                (kx if half == 0 else K("xinv", t % 2), K("ident_f")), (bkey,))
                    self.cp("act", xT[:, half * 4:half * 4 + 4, t * 128:(t + 1) * 128],
                            bk[:].rearrange("p (j n) -> p j n", j=4), (bkey,), tuple(K("xT", half * 4 + j, t) for j in range(4)))

        def rms_bcast(srcT, n, src_keys_fn):
            bk, bkey = self.bank()
            for kc in range(8):
                if kc % 2 == 0:
                    self.act(sq[:, 0, 0:n], srcT[:, kc, 0:n], AF.Square, src_keys_fn(kc), (K("sq", 0),))
                else:
                    self.tt("dve", sq[:, 1, 0:n], srcT[:, kc, 0:n], srcT[:, kc, 0:n], ALU.mult, src_keys_fn(kc), (K("sq", 1),))
                self.mm(bk[:, 0:n], ones_b[:], sq[:, kc % 2, 0:n], kc == 0, kc == 7, (K("ones_b"), K("sq", kc % 2)), (bkey,))
            self.act(tmpf[:, 0:n], bk[:, 0:n], AF.Ln, (bkey, K("epsb")), (K("tmpf"),), bias=epsb[:, 0:1], scale=1.0 / D)
            self.act(rbc[:, 0:n], tmpf[:, 0:n], AF.Exp, (K("tmpf"),), (K("rbc"),), scale=-0.5)

        def modulate(n, ntile, Asc, Bsc, Afn, Bfn):
            for kc in range(8):
                xk = tuple(K("xT", kc, t) for t in range(ntile))
                self.tt("dve", tmpf2[:, 0:n], xT[:, kc, 0:n], rbc[:, 0:n], ALU.mult, xk + (K("rbc"),), (K("tmpf2"),))
                self.act(hT[:, kc, 0:n], tmpf2[:, 0:n], AF.Identity, (K("tmpf2"),) + modkeys[Asc] + modkeys[Bsc], (K("hT", kc),),
                         bias=Bfn(kc), scale=Afn(kc))

        def modulate_seq(At, Bt, Asc, Bsc):
            for kc in range(8):
                xk = (K("xT", kc, 0),)
                v3 = lambda ap: ap.rearrange("p (s w) -> p s w", s=4)
                self.tt("dve", tmpf2[:, 0:128], xT[:, kc, 0:128], rbc[:, 0:128], ALU.mult, xk + (K("rbc"),), (K("tmpf2"),))
                self.tt("dve", v3(tmpf2[:, 0:128]), v3(tmpf2[:, 0:128]), At[:, kc, 1:5].to_broadcast([128, 4, 32]), ALU.mult,
                        (K("tmpf2"),) + modkeys[Asc], (K("tmpf2"),))
                self.tt("dve", v3(hT[:, kc, 0:128]), v3(tmpf2[:, 0:128]), Bt[:, kc, 1:5].to_broadcast([128, 4, 32]), ALU.add,
                        (K("tmpf2"),) + modkeys[Bsc], (K("hT", kc),))

        xkeys = lambda ntile: (lambda kc: tuple(K("xT", kc, t) for t in range(ntile)))

        def heads_fm(wg, wk, col0, nheads, n, evac):
            for h in range(nheads):
                bk, bkey = self.bank()
                for kc in range(8):
                    self.mm(bk[0:64, 0:n], wg(kc, col0 + h * 64, col0 + (h + 1) * 64), hT[:, kc, 0:n], kc == 0, kc == 7, (wk, K("hT", kc)), (bkey,))
                evac(h, bk, bkey)

        def pairs_fm(wg, wk, n, evac, kc_outer=False):
            if kc_outer:
                bks = [self.bank() for c in range(4)]
                for kc in range(8):
                    for c in range(4):
                        self.mm(bks[c][0][:, 0:n], wg(kc, c * 128, (c + 1) * 128), hT[:, kc, 0:n], kc == 0, kc == 7, (wk, K("hT", kc)), (bks[c][1],))
                for c in range(4):
                    evac(c, bks[c][0], bks[c][1])
                return
            for c in range(4):
                bk, bkey = self.bank()
                for kc in range(8):
                    self.mm(bk[:, 0:n], wg(kc, c * 128, (c + 1) * 128), hT[:, kc, 0:n], kc == 0, kc == 7, (wk, K("hT", kc)), (bkey,))
                evac(c, bk, bkey)

        def units_tm(wg, wk, col0, ncols, units, evac):
            for ui, (c0, C) in enumerate(units):
                bk, bkey = self.bank()
                for kc in range(8):
                    self.mm(bk[0:C, 0:ncols], hT[:, kc, c0:c0 + C], wg(kc, col0, col0 + ncols), kc == 0, kc == 7, (wk, K("hT", kc)), (bkey,))
                evac(ui, bk, bkey)

        def gla_stageA(ui, col0, C, pre, W):
            p = ui % 2
            wgk_, wkk, wgv_, wkv, wgg_, wkg = W
            vb_tok, gb2, qtT, kend, dec, attT = vb_tok2[p], gb2_2[p], qtT2[p], kend2[p], dec2[p], attT2[p]
            bk, bkey = self.bank()
            for kc in range(8):
                self.mm(bk[0:C, 0:256], hT[:, kc, col0:col0 + C], wgk_(kc, 256, 512), kc == 0, kc == 7, (wkk, K("hT", kc)), (bkey,))
            self.cp("act", kb_tok[0:C, :], bk[0:C, 0:256], (bkey,), (K("kb_tok"),))
            bk, bkey = self.bank()
            for kc in range(8):
                self.mm(bk[0:C, :], hT[:, kc, col0:col0 + C], wgv_(kc), kc == 0, kc == 7, (wkv, K("hT", kc)), (bkey,))
            self.cp("act", vb_tok[0:C, :, :].rearrange("p h e -> p (h e)"), bk[0:C, :], (bkey,), (K("vb_tok", p),))
            if not pre:
                bk, bkey = self.bank()
                for kc in range(8):
                    self.mm(bk[0:C, :], hT[:, kc, col0:col0 + C], wgg_(kc), kc == 0, kc == 7, (wkg, K("hT", kc)), (bkey,))
                self.cp("act", gb2[0:C, :], bk[0:C, :], (bkey,), (K("gb2", p),))
            bk, bkey = self.bank()
            self.mm(bk[0:C, 0:256], gkT[0:17, col0:col0 + C], wgk_b[0:17, :], True, True, (K("gkT"), K("gkT_ones"), K("wgk_b")), (bkey,))
            self.act(e_sb[0:C, :], bk[0:C, 0:256], AF.Exp, (bkey,), (K("e_sb"),), scale=-1.0)
            self.act(Lsp[0:C, :], e_sb[0:C, :], AF.Ln, (K("e_sb"),), (K("Lsp"),), bias=1.0)
            bk2, bkey2 = self.bank()
            self.mm(bk2[0:C, 0:256], trisl_f[0:C, 0:C], Lsp[0:C, :], True, True, (K("trisl_f"), K("Lsp")), (bkey2,))
            self.act(e_sb[0:C, :], bk2[0:C, 0:256], AF.Exp, (bkey2,), (K("e_sb"),), scale=-1.0 / 16)
            self.tt("dve", kend[0:C, :], kb_tok[0:C, :], e_sb[0:C, :], ALU.mult, (K("kb_tok"), K("e_sb")), (K("kend", p),))
            bk3, bkey3 = self.bank()
            for h in range(4):
                self.mm(bk3[0:64, h * 128:h * 128 + C], Lsp[0:C, h * 64:(h + 1) * 64], triu_f[0:C, 0:C], True, True,
                        (K("Lsp"), K("triu_f")), (bkey3,))
            b3 = bk3[0:64, :].rearrange("p (c t) -> p c t", c=4)
            self.act(dec[:, :], b3[:, :, C - 1], AF.Exp, (bkey3,), (K("dec", p),), scale=-1.0 / 16)
            if pre:
                return
            self.act(e1[:, :, 0:C], b3[:, :, 0:C], AF.Exp, (bkey3,), (K("e1"),), scale=-1.0 / 16)
            self.act(e2[:, :, 0:C], b3[:, :, 0:C], AF.Exp, (bkey3,), (K("e2"),), scale=1.0 / 16)
            self.stt("dve", qtT[:, :, 0:C], qbT[:, :, col0:col0 + C], 0.125, e1[:, :, 0:C], ALU.mult, ALU.mult,
                     tuple(K("qbT", j) for j in range(4)) + (K("e1"),), (K("qtT", p),))
            self.tt("dve", ktT[:, :, 0:C], kbT[:, :, col0:col0 + C], e2[:, :, 0:C], ALU.mult, tuple(K("kbT", j) for j in range(4)) + (K("e2"),), (K("ktT"),))
            bk4, bkey4 = self.bank()
            for h in range(4):
                self.mm(bk4[0:C, h * 128:h * 128 + C], ktT[:, h, 0:C], qtT[:, h, 0:C], True, True, (K("ktT"), K("qtT", p)), (bkey4,))
            for h in range(4):
                self.tt("dve", attT[0:C, h, 0:C], bk4[0:C, h * 128:h * 128 + C], triu_f[0:C, 0:C], ALU.mult,
                        (bkey4, K("triu_f")), (K("attT", p, h),))

        def gla_stageB(ui, col0, C, pre, S_t, S_b, Skey):
            p = ui % 2
            vb_tok, gb2, qtT, kend, dec, attT = vb_tok2[p], gb2_2[p], qtT2[p], kend2[p], dec2[p], attT2[p]
            if not pre:
                obk, obkey = self.bank()
                for h in range(4):
                    self.mm(obk[0:C, h * 128:(h + 1) * 128], attT[0:C, h, 0:C], vb_tok[0:C, h, :], True, False, (K("attT", p, h), K("vb_tok", p)), (obkey,))
                    self.mm(obk[0:C, h * 128:(h + 1) * 128], qtT[:, h, 0:C], S_b[:, h, :], False, True,
                            (K("qtT", p), Skey + ("b",)), (obkey,))
                self.memset("dve", ssq[0:C, :], 0.0, tuple(K("ssq", h) for h in range(4)))
                for h in range(4):
                    self.act(attT[0:C, h, :], obk[0:C, h * 128:(h + 1) * 128], AF.Square, (obkey,), (K("attT", p, h), K("ssq", h)), accum_out=ssq[0:C, h:h + 1])
                self.act(rgl[0:C, :], ssq[0:C, :], AF.Ln, tuple(K("ssq", h) for h in range(4)) + (K("epsb"),), (K("rgl"),), bias=epsb[0:C, 0:1], scale=1.0 / 128)
                self.act(rgl[0:C, :], rgl[0:C, :], AF.Exp, (K("rgl"),), (K("rgl"),), scale=-0.5)
                self.act(gtanh[0:C, :], gb2[0:C, :], AF.Tanh, (K("gb2", p),), (K("gtanh"),), scale=0.5)
                self.stt("dve", gtanh[0:C, :], gtanh[0:C, :], 1.0, gb2[0:C, :], ALU.add, ALU.mult, (K("gtanh"), K("gb2", p)), (K("gtanh"),))
                self.tt("dve", gtanh[0:C, :], gtanh[0:C, :], ggla_s[0:C, :], ALU.mult, (K("gtanh"), K("ggla")), (K("gtanh"),))
                for h in range(4):
                    self.stt("dve", yb_tok[0:C, h * 128:(h + 1) * 128], obk[0:C, h * 128:(h + 1) * 128], rgl[0:C, h:h + 1],
                             gtanh[0:C, h * 128:(h + 1) * 128], ALU.mult, ALU.mult, (obkey, K("rgl"), K("gtanh")), (K("yb_tok", h),))
                for c in range(4):
                    self.tr(pbf[:, 512 + c * 128:512 + c * 128 + C], yb_tok[0:C, c * 128:(c + 1) * 128], ident_b[0:C, 0:C],
                            (K("yb_tok", c), K("ident_b")), (K("pbf"),))
                for c in range(4):
                    self.cp("act", ybT(c)[:, col0:col0 + C], pbf[:, 512 + c * 128:512 + c * 128 + C], (K("pbf"),), (YBK[c],))
            bk6, bkey6 = self.bank()
            for h in range(4):
                self.mm(bk6[0:64, h * 128:(h + 1) * 128], kend[0:C, h * 64:(h + 1) * 64], vb_tok[0:C, h, :], True, True, (K("kend", p), K("vb_tok", p)), (bkey6,))
            for h in range(4):
                self.stt("dve", S_t[:, h, :], S_t[:, h, :], dec[:, h:h + 1], bk6[0:64, h * 128:(h + 1) * 128],
                         ALU.mult, ALU.add, (Skey + (h,), K("dec", p), bkey6), (Skey + (h,),))
            self.cp("act", S_b[:], S_t[:], tuple(Skey + (h,) for h in range(4)), (Skey + ("b",),))

        def gla_units(units, pre, S_of, with_fn=None):
            wgk_, wkk = self.wload(win_g[G_QKB], kwin[G_QKB], 8, 512)
            wgv_, wkv = self.wload(win_g[G_VB], kwin[G_VB], 8, 512)
            wgg_, wkg = (None, None) if pre else self.wload(win_g[G_GB], kwin[G_GB], 8, 512)
            W = (wgk_, wkk, wgv_, wkv, wgg_, wkg)

            def body():
                gla_stageA(0, units[0][0], units[0][1], pre, W)
                for ui, (col0, C) in enumerate(units):
                    if ui + 1 < len(units):
                        gla_stageA(ui + 1, units[ui + 1][0], units[ui + 1][1], pre, W)
                    S_t, S_b, Skey, before_fn, after_fn = S_of(ui)
                    if before_fn is not None:
                        before_fn()
                    gla_stageB(ui, col0, C, pre, S_t, S_b, Skey)
                    if after_fn is not None:
                        after_fn()
            if with_fn is None:
                body()
            else:
                self.interleave([(body, [0, 1, 2]), (with_fn, [3, 4])])

        def attn_tile(tglob, tl):
            pTv = lambda kb, par: pT[:, kb, :, :].rearrange("p (c two) q -> p two c q", two=2)[:, par]
            for kb in range(5):
                slot = (tglob - 4 + kb) % 8
                banks = [self.bank(), self.bank()]
                for c in range(4):
                    for par in range(2):
                        pb = par * 64
                        bk, bkey = banks[par]
                        self.mm(bk[:, c * 128:(c + 1) * 128], kaT[slot][pb:pb + 64, c, :], qaT[pb:pb + 64, c, tl * 128:(tl + 1) * 128],
                                True, True, (K("kaT", slot, c), K("qaT", c)), (bkey,))
                for par in range(2):
                    bk, bkey = banks[par]
                    self.act(pTv(kb, par), bk[:].rearrange("p (c q) -> p c q", c=4), AF.Exp, (bkey,), (K("pT", kb, par),))
                tab = {0: (tab_mask[:, :].to_broadcast([128, 8, 128]) if False else None), 3: tab_prev, 4: tab_own}.get(kb)
                pk = (K("pT", kb, 0), K("pT", kb, 1))
                if kb == 0:
                    for h0 in range(0, 8, 4):
                        pass
                    self.tt("dve", pT[:, 0, :, :], pT[:, 0, :, :], bass.AP(tab_mask, 0, [[128, 128], [0, 8], [1, 128]]), ALU.mult,
                            pk + (K("tab_mask"),), pk)
                elif tab is not None:
                    self.tt("dve", pT[:, kb, :, :], pT[:, kb, :, :], tab[:, :, :], ALU.mult, pk + (K("etab", kb - 3),), pk)
            for hg in range(2):
                ob, obkey = obank[hg], ("ps", 5 + hg)
                for hh in range(4):
                    h = hg * 4 + hh
                    for kb in range(5):
                        slot = (tglob - 4 + kb) % 8
                        self.mm(ob[:, hh * 65:(hh + 1) * 65], pT[:, kb, h, :], vaug[slot][:, h, :], kb == 0, kb == 4,
                                (K("pT", kb, h % 2), K("vaug", slot), K("vaug1", slot)), (obkey,))
            for hg in range(2):
                ob, obkey = obank[hg], ("ps", 5 + hg)
                ov = ob[:, 0:260].rearrange("p (h e) -> p h e", h=4)
                self.ts("dve", rden[:, hg * 4:(hg + 1) * 4], ov[:, :, 64], 1e-30, None, ALU.add, None, (obkey,), (K("rden", hg),))
                S.add("dve", lambda e, hg=hg: e.reciprocal(rden[:, hg * 4:(hg + 1) * 4], rden[:, hg * 4:(hg + 1) * 4]), (K("rden", hg),), (K("rden", hg),))
                for hh in range(4):
                    h = hg * 4 + hh
                    self.ts("dve", ya_tok[:, h * 64:(h + 1) * 64], ov[:, hh, 0:64], rden[:, h:h + 1], None, ALU.mult, None,
                            (obkey, K("rden", hg)), (K("ya_tok", h),))
            yk = tuple(K("ya_tok", h) for h in range(8))
            for c in range(4):
                self.tr(pbf[:, c * 128:(c + 1) * 128], ya_tok[:, c * 128:(c + 1) * 128], ident_b[:], yk + (K("ident_b"),), (K("pbf"),))
            for c in range(4):
                self.cp("act", yaT(c)[:, tl * 128:(tl + 1) * 128], pbf[:, c * 128:(c + 1) * 128], (K("pbf"),), (YAK[c],))

        def attn_sample(s, slot, qbase=0):
            for rb in range(4):
                xb = xin[rb % 2]
                kx = K("xin", rb % 2)
                self.dma("sp", xb[:, 0:512], ck[s, rb * 128:(rb + 1) * 128, :], (), (kx,), ("xin", rb % 2))
                bk, bkey = self.bank()
                for c in range(4):
                    self.tr(bk[:, c * 128:(c + 1) * 128], xb[:, c * 128:(c + 1) * 128], ident_f[:], (kx, K("ident_f")), (bkey,))
                for c in range(4):
                    self.cp("act", kcT(c)[:, rb * 128:(rb + 1) * 128], bk[:, c * 128:(c + 1) * 128], (bkey,), (K("m2", c),))
                self.dma("sp", xb[:, 512:1024], cv[s, rb * 128:(rb + 1) * 128, :], (), (K("xinv", rb % 2),), ("xinv", rb % 2))
                self.cp("dve", vcaug[:, rb, :, 0:64], xb[:, 512:1024].rearrange("p (h e) -> p h e", h=8), (K("xinv", rb % 2),), (K("vcaug", rb),))
                self.cp("dve", vcaug[:, rb, :, 64], ones_f[:, 0:8], (K("ones_f"),), (K("vcaug1", rb),))
            q0 = qbase + s * 32
            l0 = s * 32
            for kb in range(5):
                kn = 128 if kb < 4 else 32
                banks = [self.bank(), self.bank()]
                for c in range(4):
                    for par in range(2):
                        pb = par * 64
                        bk, bkey = banks[par]
                        if kb < 4:
                            lhs = kcT(c)[pb:pb + 64, kb * 128:(kb + 1) * 128]
                            rk = (K("m2", c),)
                        else:
                            lhs = kaT[slot][pb:pb + 64, c, l0:l0 + 32]
                            rk = (K("kaT", slot, c),)
                        self.mm(bk[0:kn, c * 32:(c + 1) * 32], lhs, qaT[pb:pb + 64, c, q0:q0 + 32], True, True, rk + (K("qaT", c),), (bkey,))
                for par in range(2):
                    bk, bkey = banks[par]
                    dstv = pT[0:kn, kb, :, 0:32].rearrange("p (c two) q -> p two c q", two=2)[:, par]
                    self.act(dstv, bk[0:kn, 0:128].rearrange("p (c q) -> p c q", c=4), AF.Exp, (bkey,), (K("pT", kb, par),))
                pk = (K("pT", kb, 0), K("pT", kb, 1))
                if kb == 3:
                    self.tt("dve", pT[:, 3, :, 0:32], pT[:, 3, :, 0:32], tab_prev[:, :, 0:32], ALU.mult, pk + (K("etab", 0),), pk)
                elif kb == 4:
                    self.tt("dve", pT[0:32, 4, :, 0:32], pT[0:32, 4, :, 0:32], tab_own[0:32, :, 0:32], ALU.mult, pk + (K("etab", 1),), pk)
            for h in range(8):
                hg, hh = h // 4, h % 4
                for kb in range(5):
                    kn = 128 if kb < 4 else 32
                    rhs = vcaug[:, kb, h, :] if kb < 4 else vown[0:32, s, h, :]
                    rk = (K("vcaug", kb), K("vcaug1", kb)) if kb < 4 else (K("vown", s), K("vown1", s))
                    self.mm(obank[hg][0:32, hh * 65:(hh + 1) * 65], pT[0:kn, kb, h, 0:32], rhs, kb == 0, kb == 4,
                            (K("pT", kb, h % 2),) + rk, (("ps", 5 + hg),))
            for hg in range(2):
                ob, obkey = obank[hg], ("ps", 5 + hg)
                ov = ob[0:32, 0:260].rearrange("p (h e) -> p h e", h=4)
                S.add("dve", lambda e, ov=ov, hg=hg: e.reciprocal(rden[0:32, hg * 4:(hg + 1) * 4], ov[:, :, 64]), (obkey,), (K("rden", hg),))
                for hh in range(4):
                    h = hg * 4 + hh
                    self.ts("dve", ya_tok[0:32, h * 64:(h + 1) * 64], ov[:, hh, 0:64], rden[0:32, h:h + 1], None, ALU.mult, None,
                            (obkey, K("rden", hg)), (K("ya_tok", h),))
            yk = tuple(K("ya_tok", h) for h in range(8))
            for c in range(4):
                self.tr(pbf[:, c * 128:c * 128 + 32], ya_tok[0:32, c * 128:(c + 1) * 128], ident_b[0:32, 0:32], yk + (K("ident_b"),), (K("pbf"),))
            for c in range(4):
                self.cp("act", yaT(c)[:, q0:q0 + 32], pbf[:, c * 128:c * 128 + 32], (K("pbf"),), (YAK[c],))

        def resid(n, ntile, Gfn, gname):
            for kc in range(8):
                tb, tk = (tmpf2, K("tmpf2")) if kc % 2 == 0 else (tmpf, K("tmpf"))
                self.tt("pool", tb[:, 0:n], m2[:, kc, 0:n], rbc[:, 0:n], ALU.mult, (K("m2", kc), K("rbc")), (tk,))
                xk = tuple(K("xT", kc, t) for t in range(ntile))
                self.stt("dve", xT[:, kc, 0:n], tb[:, 0:n], Gfn(kc), xT[:, kc, 0:n], ALU.mult, ALU.add,
                         (tk,) + modkeys[gname] + xk, xk)

        def resid_seq(Gt, gname):
            v3 = lambda ap: ap.rearrange("p (s w) -> p s w", s=4)
            for kc in range(8):
                xk = (K("xT", kc, 0),)
                self.tt("dve", tmpf2[:, 0:128], m2[:, kc, 0:128], rbc[:, 0:128], ALU.mult, (K("m2", kc), K("rbc")), (K("tmpf2"),))
                self.tt("dve", v3(tmpf2[:, 0:128]), v3(tmpf2[:, 0:128]), Gt[:, kc, 1:5].to_broadcast([128, 4, 32]), ALU.mult,
                        (K("tmpf2"),) + modkeys[gname], (K("tmpf2"),))
                self.tt("dve", xT[:, kc, 0:128], xT[:, kc, 0:128], tmpf2[:, 0:128], ALU.add, (K("tmpf2"),) + xk, xk)

        def merge_and_out(n):
            def ev_gate(dst, dkey):
                def f(bk, bkey):
                    self.act(dst[:, 0:n], bk[:, 0:n], AF.Tanh, (bkey,), (dkey,), scale=0.5)
                    self.ts("dve", dst[:, 0:n], dst[:, 0:n], 0.5, 0.5, ALU.mult, ALU.add, (dkey,), (dkey,))
                return f
            for G in range(2):
                wga, wka = self.wload(win_g[G_GA + G], kwin[G_GA + G], 8, 512)
                wba, wkba = self.wload(wbra_g[G], kwbra[G], 4, 512)
                for j in range(4):
                    c = G * 4 + j
                    bk, bkey = self.bank()
                    for kc in range(8):
                        self.mm(bk[:, 0:n], wga(kc, j * 128, (j + 1) * 128), hT[:, kc, 0:n], kc == 0, kc == 7, (wka, K("hT", kc)), (bkey,))
                    ev_gate(sgA, K("sgA"))(bk, bkey)
                    bk, bkey = self.bank()
                    for kc in range(4):
                        self.mm(bk[:, 0:n], wba(kc, j * 128, (j + 1) * 128), yaT(kc)[:, 0:n], kc == 0, kc == 3, (wkba, YAK[kc]), (bkey,))
                    self.tt("dve", m2[:, c, 0:n], sgA[:, 0:n], bk[:, 0:n], ALU.mult, (K("sgA"), bkey), (K("m2", c),))
                wgb, wkb = self.wload(win_g[G_GBR + G], kwin[G_GBR + G], 8, 512)
                wbb, wkbb = self.wload(wbrb_g[G], kwbrb[G], 4, 512)
                for j in range(4):
                    c = G * 4 + j
                    bk, bkey = self.bank()
                    for kc in range(8):
                        self.mm(bk[:, 0:n], wgb(kc, j * 128, (j + 1) * 128), hT[:, kc, 0:n], kc == 0, kc == 7, (wkb, K("hT", kc)), (bkey,))
                    ev_gate(sgB, K("sgB"))(bk, bkey)
                    bk, bkey = self.bank()
                    for kc in range(4):
                        self.mm(bk[:, 0:n], wbb(kc, j * 128, (j + 1) * 128), ybT(kc)[:, 0:n], kc == 0, kc == 3, (wkbb, YBK[kc]), (bkey,))
                    self.tt("dve", sgB[:, 0:n], sgB[:, 0:n], bk[:, 0:n], ALU.mult, (K("sgB"), bkey), (K("sgB"),))
                    self.tt("dve", mergedT(c)[:, 0:n], sgB[:, 0:n], m2[:, c, 0:n], ALU.add, (K("sgB"), K("m2", c)), (MK(c),))
            for G in range(2):
                wg, wk = self.wload(wout_g[G], kwout[G], 8, 512)
                for j in range(4):
                    c = G * 4 + j
                    bk, bkey = self.bank()
                    for kc in range(8):
                        self.mm(bk[:, 0:n], wg(kc, j * 128, (j + 1) * 128), mergedT(kc)[:, 0:n], kc == 0, kc == 7, (wk, MK(kc)), (bkey,))
                    self.cp("act", m2[:, c, 0:n], bk[:, 0:n], (bkey,), (K("m2", c),))
            rms_bcast(m2, n, lambda kc: (K("m2", kc),))

        ALIAS_KEYS = tuple(K("pT", kb, par) for kb in range(5) for par in range(2)) + \
            tuple(K(nm, 1, h) for nm in ("ua", "uah", "y0") for h in range(2))

        def alias_fence():
            self.memset("dve", pT_raw[:, 2559:2560], 0.0, ALIAS_KEYS)

        def ffn(n, groups):
            alias_fence()
            for g in range(11):
                wg, wk = self.wload(wup_g[g], kwup[g], 8, 512)
                pre_banks = None
                if g == 0:
                    pre_banks = [self.bank() for jc in range(4)]
                    for kc in range(8):
                        for jc in range(4):
                            self.mm(pre_banks[jc][0][:, 0:n], wg(kc, jc * 128, (jc + 1) * 128), hT[:, kc, 0:n], kc == 0, kc == 7,
                                    (wk, K("hT", kc)), (pre_banks[jc][1],))
                for pair in range(2):
                    ua_, y0_ = ua_sets[pair], y0_sets[pair]
                    for half in range(2):
                        jc = half * 2 + pair
                        jj = half * NFF + 2 * g + pair
                        if pre_banks is not None:
                            bk, bkey = pre_banks[jc]
                        else:
                            bk, bkey = self.bank()
                            for kc in range(8):
                                self.mm(bk[:, 0:n], wg(kc, jc * 128, (jc + 1) * 128), hT[:, kc, 0:n], kc == 0, kc == 7, (wk, K("hT", kc)), (bkey,))
                        uk, uh, yk = K("ua", pair, half), K("uah", pair, half), K("y0", pair, half)
                        self.act(y0_[half][:, 0:n], bk[:, 0:n], AF.Identity, (bkey, K("wdwT2"), K("bdwT")), (yk,),
                                 bias=bdwT[:, jj:jj + 1], scale=wdwT[:, 2, jj:jj + 1])
                        off = 0
                        for (c0, ncols, nseg, hview, hkeyf, extra, scale_ap) in groups:
                            w = ncols // nseg
                            v3 = lambda ap, nseg=nseg: ap.rearrange("p (s w) -> p s w", s=nseg)
                            u3 = v3(ua_[half][:, off:off + nseg * (w + 2)])
                            off += nseg * (w + 2)
                            yv = v3(y0_[half][:, c0:c0 + ncols])
                            self.cp("pool", u3[:, :, 0:2], hview(jj), (hkeyf(jj),) + extra, (uh,))
                            if scale_ap is None:
                                self.cp("act", u3[:, :, 2:2 + w], v3(bk[:, c0:c0 + ncols]), (bkey,), (uk,))
                            else:
                                self.act(u3[:, :, 2:2 + w], v3(bk[:, c0:c0 + ncols]), AF.Identity, (bkey, K("flag")), (uk,), scale=scale_ap)
                            self.stt("dve", yv, u3[:, :, 1:1 + w], wdwT[:, 1, jj:jj + 1], yv, ALU.mult, ALU.add, (uk, uh, yk, K("wdwT1")), (yk,))
                            self.stt("dve", yv, u3[:, :, 0:w], wdwT[:, 0, jj:jj + 1], yv, ALU.mult, ALU.add, (uk, uh, yk, K("wdwT0")), (yk,))
                            self.cp("pool", hview(jj), u3[:, :, w:w + 2], (uk,), (hkeyf(jj),))
                    j = 2 * g + pair
                    self.act(y0_[0][:, 0:n], y0_[0][:, 0:n], AF.Gelu_apprx_tanh, (K("y0", pair, 0),), (K("y0", pair, 0),))
                    self.tt("dve", actT[:, j, 0:n], y0_[0][:, 0:n], y0_[1][:, 0:n], ALU.mult, (K("y0", pair, 0), K("y0", pair, 1)), (K("actT", j),))
            for c in range(8):
                wg, wk = self.wload(wdown_g[c], kwdown[c], NFF, 128)
                bk, bkey = self.bank()
                for j in range(NFF):
                    self.mm(bk[:, 0:n], wg(j), actT[:, j, 0:n], j == 0, j == NFF - 1, (wk, K("actT", j)), (bkey,))
                self.cp("act", m2[:, c, 0:n], bk[:, 0:n], (bkey,), (K("m2", c),))
            rms_bcast(m2, n, lambda kc: (K("m2", kc),))
            alias_fence()

        def store_y(dst_rows, ntile):
            for t in range(ntile):
                yb_ = xin[t % 2]
                ky = K("xin", t % 2)
                for half in range(2):
                    bk, bkey = self.bank()
                    for j in range(4):
                        kc = half * 4 + j
                        self.tr(bk[:, j * 128:(j + 1) * 128], xT[:, kc, t * 128:(t + 1) * 128], ident_f[:], (K("xT", kc, t), K("ident_f")), (bkey,))
                    self.cp("act", yb_[:, half * 512:(half + 1) * 512], bk[:], (bkey,), (K("xinv", t % 2),) if half else (ky,))
                self.dma("sp", dst_rows[t * 128:(t + 1) * 128, :], yb_[:], (ky, K("xinv", t % 2)), (), ("xin", t % 2))

        def load_dma(tile_ap):
            self.dma("sp", xin[0][:], tile_ap, (), (K("xin", 0), K("xinv", 0)), ("xin", 0))

        def prefetch_x(nxt_tiles):
            if not nxt_tiles or self.prefetched:
                return
            load_dma(nxt_tiles[0])
            self.prefetched = True

        def load_tile(nxt_tiles, t):
            xb = xin[0]
            kx, kv = K("xin", 0), K("xinv", 0)
            for half in range(2):
                bk, bkey = self.bank()
                for j in range(4):
                    kc = half * 4 + j
                    self.tr(bk[:, j * 128:(j + 1) * 128], xb[:, kc * 128:(kc + 1) * 128], ident_f[:], (kx if half == 0 else kv, K("ident_f")), (bkey,))
                self.cp("act", xT[:, half * 4:half * 4 + 4, t * 128:(t + 1) * 128],
                        bk[:].rearrange("p (j n) -> p j n", j=4), (bkey,), tuple(K("xT", half * 4 + j, t) for j in range(4)))
            if t + 1 < len(nxt_tiles):
                load_dma(nxt_tiles[t + 1])

        def store_tile(dst_ap, t):
            xb = xin[1]
            kx, kv = K("xin", 1), K("xinv", 1)
            for half in range(2):
                bk, bkey = self.bank()
                for j in range(4):
                    kc = half * 4 + j
                    self.tr(bk[:, j * 128:(j + 1) * 128], xT[:, kc, t * 128:(t + 1) * 128], ident_f[:], (K("xT", kc, t), K("ident_f")), (bkey,))
                self.cp("act", xb[:, half * 512:(half + 1) * 512], bk[:], (bkey,), (kv,) if half else (kx,))
            self.dma("sp", dst_ap, xb[:], (kx, kv), (), ("xin", 1))

        def swap_tiles(dst_tiles, nxt_tiles):
            dst_tiles = dst_tiles or []
            nxt_tiles = nxt_tiles or []
            prefetch_x(nxt_tiles)
            self.prefetched = False
            for t in range(max(len(dst_tiles), len(nxt_tiles))):
                if t < len(dst_tiles):
                    store_tile(dst_tiles[t], t)
                if t < len(nxt_tiles):
                    load_tile(nxt_tiles, t)

        def store_hist(hsrc, hkeys, dst):
            self.cp("dve", h88[:], hsrc.rearrange("p j t -> p t j"), hkeys, (K("h88"),))
            bk, bkey = self.bank()
            self.tr(bk[0:88, 0:128], h88[:].rearrange("p t j -> p (t j)"), ident_f[:], (K("h88"), K("ident_f")), (bkey,))
            self.cp("act", stage[0:88, :], bk[0:88, 0:128], (bkey,), (K("stage"),))
            self.dma("sp", dst, stage[0:88, :], (K("stage"),), (), ("stage_o",))

        def kv_stage(bk, bkey, which, dst):
            st = xin[1][:, which * 512:(which + 1) * 512]
            sk = K("xin", 1) if which == 0 else K("xinv", 1)
            self.cp("act", st, bk[:], (bkey,), (sk,))
            self.dma("sp", dst, st, (sk,), (), ("xin", 1) if which == 0 else ("xinv", 1))

        LAST_KV0 = 16 + NMAIN - 4

        def common_proj(n, units, tglob0, kv_from, pre, flagged, kv_out):
            if not pre:
                wg, wk = self.wload(win_g[G_QA], kwin[G_QA], 8, 512)
                def ev_q(j, bk, bkey):
                    self.act(qaT[:, j, 0:n], bk[:, 0:n], AF.Identity, (bkey,), (K("qaT", j),), scale=0.125)
                pairs_fm(wg, wk, n, ev_q, kc_outer=True)
            if kv_from < len(units):
                wg, wk = self.wload(win_g[G_KA], kwin[G_KA], 8, 512)
                def ev_k(j, bk, bkey):
                    for ui in range(kv_from, len(units)):
                        c0, C = units[ui]
                        if C == 128:
                            slot = (tglob0 + ui) % 8
                            self.cp("act", kaT[slot][:, j, :], bk[:, c0:c0 + 128], (bkey,), (K("kaT", slot, j),))
                    if units[0][1] != 128:
                        self.cp("act", kaT[tglob0 % 8][:, j, :], bk[:, 0:128], (bkey,), (K("kaT", tglob0 % 8, j),))
                pairs_fm(wg, wk, n, ev_k)
                kunits = [(ui, units[ui]) for ui in range(kv_from, len(units)) if kv_out(ui, 0) is not None] if units[0][1] == 128 else []
                if kunits:
                    units_tm(wg, wk, 0, 512, [u for _, u in kunits], lambda i, bk, bkey: kv_stage(bk, bkey, 0, kv_out(kunits[i][0], 0)))
                if units[0][1] != 128:
                    units_tm(wg, wk, 0, 512, [(0, 128)], lambda i, bk, bkey: kv_stage(bk, bkey, 0, kv_out(0, 0)))
                wg, wk = self.wload(win_g[G_VA], kwin[G_VA], 8, 512)
                if units[0][1] == 128:
                    def ev_v(i, bk, bkey):
                        ui = kv_from + i
                        slot = (tglob0 + ui) % 8
                        self.cp("act", vaug[slot][:, :, 0:64], bk[:].rearrange("p (h e) -> p h e", h=8), (bkey,), (K("vaug", slot),))
                        if flagged:
                            self.ts("dve", vaug[slot][:, :, 64], ones_f[:, 0:8], flag[:, 0:1], None, ALU.mult, None,
                                    (K("ones_f"), K("flag")), (K("vaug1", slot),))
                        else:
                            self.cp("dve", vaug[slot][:, :, 64], ones_f[:, 0:8], (K("ones_f"),), (K("vaug1", slot),))
                        if kv_out(ui, 1) is not None:
                            kv_stage(bk, bkey, 1, kv_out(ui, 1))
                    units_tm(wg, wk, 0, 512, units[kv_from:], ev_v)
                else:
                    units_tm(wg, wk, 0, 512, [(0, 128)], lambda i, bk, bkey: kv_stage(bk, bkey, 1, kv_out(0, 1)))
            if not pre:
                wg, wk = self.wload(win_g[G_QKB], kwin[G_QKB], 8, 512)
                def ev_qb(j, bk, bkey):
                    self.cp("act", qbT[:, j, 0:n], bk[0:64, 0:n], (bkey,), (K("qbT", j),))
                def ev_kb(j, bk, bkey):
                    self.cp("act", kbT[:, j, 0:n], bk[0:64, 0:n], (bkey,), (K("kbT", j),))
                heads_fm(wg, wk, 0, 4, n, ev_qb)
                heads_fm(wg, wk, 256, 4, n, ev_kb)
            wg, wk = self.wload(wgk_g, kwgk, 8, 16)
            bk, bkey = self.bank()
            for kc in range(8):
                self.mm(bk[0:16, 0:n], wg(kc), hT[:, kc, 0:n], kc == 0, kc == 7, (wk, K("hT", kc)), (bkey,))
            self.cp("act", gkT[0:16, 0:n], bk[0:16, 0:n], (bkey,), (K("gkT"),))

        SK = K("Sst")
        def prompt_block(src_rows, ntile, tglob0, mode, dst_rows, kv_from, preloaded=False, nxt=None):
            n = ntile * 128
            pre = mode == "prefix"
            flagged = mode != "full"
            units = [(t * 128, 128) for t in range(ntile)]
            mark = lambda nm: self.marks.append((mode, tglob0, nm, len(S.ops)))
            mark("start")
            if not preloaded:
                swap_tiles(None, [src_rows[t * 128:(t + 1) * 128, :] for t in range(ntile)])
            rms_bcast(xT, n, xkeys(ntile))
            if flagged:
                modulate(n, ntile, "AmP", "BmP", lambda kc: AmP[:, kc:kc + 1], lambda kc: BmP[:, kc:kc + 1])
            else:
                modulate(n, ntile, "Am", "Bm", lambda kc: Am[:, kc, 0:1], lambda kc: Bm[:, kc, 0:1])
            if pre and nxt is not None:
                swap_tiles(None, nxt)
            mark("norm1")

            def kv_out(ui, which):
                tg = tglob0 + ui
                if mode != "full" or tg < LAST_KV0:
                    return None
                r0 = (tg - LAST_KV0) * 128
                return (kp_o if which == 0 else vp_o)[r0:r0 + 128, :]
            common_proj(n, units, tglob0, kv_from, pre, flagged, kv_out)
            mark("proj")
            if pre:
                gla_units(units, pre, lambda ui: (Sst, Sbf, SK, None, None))
                mark("gla")
                return
            def attn_all():
                for t in range(ntile):
                    attn_tile(tglob0 + t, t)
            gla_units(units, pre, lambda ui: (Sst, Sbf, SK, None, None), attn_all)
            mark("attn")
            merge_and_out(n)
            resid(n, ntile, lambda kc: Gm[:, kc, 0:1], "Gm")
            mark("merge")
            rms_bcast(xT, n, xkeys(ntile))
            if flagged:
                modulate(n, ntile, "AfP", "BfP", lambda kc: AfP[:, kc:kc + 1], lambda kc: BfP[:, kc:kc + 1])
            else:
                modulate(n, ntile, "Af", "Bf", lambda kc: Af[:, kc, 0:1], lambda kc: Bf[:, kc, 0:1])
            prefetch_x(nxt)
            ffn(n, [(0, n, 1, lambda jj: hist[:, jj:jj + 1, :], lambda jj: K("hist", jj), (), None)])
            mark("ffn")
            resid(n, ntile, lambda kc: Gf[:, kc, 0:1], "Gf")
            swap_tiles([dst_rows[t * 128:(t + 1) * 128, :] for t in range(ntile)], nxt)
            mark("end")

        def sample_block():
            n = 128
            slot = 0
            units = [(s * 32, 32) for s in range(4)]
            self.marks.append(("sample", 0, "start", len(S.ops)))
            rms_bcast(xT, n, xkeys(1))
            modulate_seq(Am, Bm, "Am", "Bm")
            smark = lambda nm: self.marks.append(("sample", 0, nm, len(S.ops)))
            smark("norm1")
            common_proj(n, units, slot, 0, False, False, lambda ui, which: (ks_o if which == 0 else vs_o))
            smark("proj")
            SSK = K("Sst")
            def S_of(ui):
                def before():
                    self.dma("sp", Ss[:], sgla[ui].rearrange("h k v -> k h v"), (), tuple(SSK + (h,) for h in range(4)), ("Ss",))
                    self.cp("act", Ssb[:], Ss[:], tuple(SSK + (h,) for h in range(4)), (SSK + ("b",),))
                def after():
                    self.dma("sp", glas_o[ui].rearrange("h k v -> k h v"), Ss[:], tuple(SSK + (h,) for h in range(4)), (), ("Ss",))
                return (Ss, Ssb, SSK, before, after)
            wgv, wkv = self.wload(win_g[G_VA], kwin[G_VA], 8, 512)
            for s in range(4):
                bk, bkey = self.bank()
                for kc in range(8):
                    self.mm(bk[0:32, :], hT[:, kc, s * 32:(s + 1) * 32], wgv(kc), kc == 0, kc == 7, (wkv, K("hT", kc)), (bkey,))
                self.cp("act", vown[:, s, :, 0:64], bk[0:32, :].rearrange("p (h e) -> p h e", h=8), (bkey,), (K("vown", s),))
                self.cp("dve", vown[:, s, :, 64], ones_f[0:32, 0:8], (K("ones_f"),), (K("vown1", s),))
            def attn_all():
                for s in range(4):
                    attn_sample(s, slot)
            gla_units(units, False, S_of, attn_all)
            smark("attn")
            merge_and_out(n)
            resid_seq(Gm, "Gm")
            smark("merge")
            rms_bcast(xT, n, xkeys(1))
            modulate_seq(Af, Bf, "Af", "Bf")
            for s in range(4):
                load_T(sconv[s], 88, hist_s[:, s, :, :].rearrange("p j t -> p t j"), "hist_s_in%d" % s,
                       view=lambda a: a.rearrange("p (t j) -> p t j", t=2))
            hs_in = tuple(K("hist_s_in%d" % s) for s in range(4))
            ffn(n, [(0, n, 4, lambda jj: hist_s[:, :, jj, :], lambda jj: K("hist_s", jj), hs_in, None)])
            smark("ffn")
            resid_seq(Gf, "Gf")
            store_y(ysam, 1)
            for s in range(4):
                store_hist(hist_s[:, s, :, :], tuple(K("hist_s", jj) for jj in range(44)), convs_o[s])

        def mod_mixed(At, Bt, AP, BP, Asc, Bsc, APk, BPk):
            for kc in range(8):
                xk = (K("xT", kc, 0), K("xT", kc, 1))
                v3 = lambda ap: ap.rearrange("p (s w) -> p s w", s=4)
                self.tt("dve", tmpf2[:, 0:256], xT[:, kc, 0:256], rbc[:, 0:256], ALU.mult, xk + (K("rbc"),), (K("tmpf2"),))
                self.act(hT[:, kc, 0:128], tmpf2[:, 0:128], AF.Identity, (K("tmpf2"),) + modkeys[APk] + modkeys[BPk], (K("hT", kc),),
                         bias=BP[:, kc:kc + 1], scale=AP[:, kc:kc + 1])
                self.tt("dve", v3(tmpf2[:, 128:256]), v3(tmpf2[:, 128:256]), At[:, kc, 1:5].to_broadcast([128, 4, 32]), ALU.mult,
                        (K("tmpf2"),) + modkeys[Asc], (K("tmpf2"),))
                self.tt("dve", v3(hT[:, kc, 128:256]), v3(tmpf2[:, 128:256]), Bt[:, kc, 1:5].to_broadcast([128, 4, 32]), ALU.add,
                        (K("tmpf2"), K("hT", kc)) + modkeys[Bsc], (K("hT", kc),))

        def resid_mixed(Gt, gname):
            v3 = lambda ap: ap.rearrange("p (s w) -> p s w", s=4)
            for kc in range(8):
                tb, tk = (tmpf2, K("tmpf2")) if kc % 2 == 0 else (tmpf, K("tmpf"))
                self.tt("pool", tb[:, 0:256], m2[:, kc, 0:256], rbc[:, 0:256], ALU.mult, (K("m2", kc), K("rbc")), (tk,))
                self.stt("dve", xT[:, kc, 0:128], tb[:, 0:128], Gt[:, kc, 0:1], xT[:, kc, 0:128], ALU.mult, ALU.add,
                         (tk, K("xT", kc, 0)) + modkeys[gname], (K("xT", kc, 0),))
                self.tt("dve", v3(tb[:, 128:256]), v3(tb[:, 128:256]), Gt[:, kc, 1:5].to_broadcast([128, 4, 32]), ALU.mult,
                        (tk,) + modkeys[gname], (tk,))
                self.tt("dve", xT[:, kc, 128:256], xT[:, kc, 128:256], tb[:, 128:256], ALU.add, (tk, K("xT", kc, 1)), (K("xT", kc, 1),))

        def mixed_block(nxt):
            n = 256
            SL, OVS = 0, 7
            mk = lambda nm: self.marks.append(("mixed", 15, nm, len(S.ops)))
            mk("start")
            rms_bcast(xT, n, xkeys(2))
            mod_mixed(Am, Bm, AmP, BmP, "Am", "Bm", "AmP", "BmP")
            mk("norm1")
            wg, wk = self.wload(win_g[G_QA], kwin[G_QA], 8, 512)
            def ev_q(j, bk, bkey):
                self.act(qaT[:, j, 0:n], bk[:, 0:n], AF.Identity, (bkey,), (K("qaT", j),), scale=0.125)
            pairs_fm(wg, wk, n, ev_q, kc_outer=True)
            wg, wk = self.wload(win_g[G_KA], kwin[G_KA], 8, 512)
            def ev_k(j, bk, bkey):
                self.cp("act", kaT[OVS][:, j, :], bk[:, 0:128], (bkey,), (K("kaT", OVS, j),))
                self.cp("act", kaT[SL][:, j, :], bk[:, 128:256], (bkey,), (K("kaT", SL, j),))
            pairs_fm(wg, wk, n, ev_k)
            units_tm(wg, wk, 0, 512, [(128, 128)], lambda i, bk, bkey: kv_stage(bk, bkey, 0, ks_o))
            wg, wk = self.wload(win_g[G_VA], kwin[G_VA], 8, 512)
            def ev_v(i, bk, bkey):
                self.cp("act", vaug[OVS][:, :, 0:64], bk[:].rearrange("p (h e) -> p h e", h=8), (bkey,), (K("vaug", OVS),))
                self.ts("dve", vaug[OVS][:, :, 64], ones_f[:, 0:8], flag[:, 0:1], None, ALU.mult, None,
                        (K("ones_f"), K("flag")), (K("vaug1", OVS),))
            units_tm(wg, wk, 0, 512, [(0, 128)], ev_v)
            units_tm(wg, wk, 0, 512, [(128, 128)], lambda i, bk, bkey: kv_stage(bk, bkey, 1, vs_o))
            for sq_ in range(4):
                bk, bkey = self.bank()
                c0 = 128 + sq_ * 32
                for kc in range(8):
                    self.mm(bk[0:32, :], hT[:, kc, c0:c0 + 32], wg(kc), kc == 0, kc == 7, (wk, K("hT", kc)), (bkey,))
                self.cp("act", vown[:, sq_, :, 0:64], bk[0:32, :].rearrange("p (h e) -> p h e", h=8), (bkey,), (K("vown", sq_),))
                self.cp("dve", vown[:, sq_, :, 64], ones_f[0:32, 0:8], (K("ones_f"),), (K("vown1", sq_),))
            wg, wk = self.wload(win_g[G_QKB], kwin[G_QKB], 8, 512)
            def ev_qb(j, bk, bkey):
                self.cp("act", qbT[:, j, 0:n], bk[0:64, 0:n], (bkey,), (K("qbT", j),))
            def ev_kb(j, bk, bkey):
                self.cp("act", kbT[:, j, 0:n], bk[0:64, 0:n], (bkey,), (K("kbT", j),))
            heads_fm(wg, wk, 0, 4, n, ev_qb)
            heads_fm(wg, wk, 256, 4, n, ev_kb)
            wg, wk = self.wload(wgk_g, kwgk, 8, 16)
            bk, bkey = self.bank()
            for kc in range(8):
                self.mm(bk[0:16, 0:n], wg(kc), hT[:, kc, 0:n], kc == 0, kc == 7, (wk, K("hT", kc)), (bkey,))
            self.cp("act", gkT[0:16, 0:n], bk[0:16, 0:n], (bkey,), (K("gkT"),))
            mk("proj")
            units = [(0, 128)] + [(128 + 32 * q, 32) for q in range(4)]
            SSK = K("Ss")
            def S_of(ui):
                if ui == 0:
                    return (Sst, Sbf, SK, None, None)
                sq_ = ui - 1
                def before():
                    self.dma("sp", Ss[:], sgla[sq_].rearrange("h k v -> k h v"), (), tuple(SSK + (h,) for h in range(4)), ("Ss",))
                    self.cp("act", Ssb[:], Ss[:], tuple(SSK + (h,) for h in range(4)), (SSK + ("b",),))
                def after():
                    self.dma("sp", glas_o[sq_].rearrange("h k v -> k h v"), Ss[:], tuple(SSK + (h,) for h in range(4)), (), ("Ss",))
                return (Ss, Ssb, SSK, before, after)
            def attn_all():
                attn_tile(15, 0)
                for q in range(4):
                    attn_sample(q, SL, 128)
            gla_units(units, False, S_of, attn_all)
            mk("attn")
            merge_and_out(n)
            resid_mixed(Gm, "Gm")
            mk("merge")
            rms_bcast(xT, n, xkeys(2))
            mod_mixed(Af, Bf, AfP, BfP, "Af", "Bf", "AfP", "BfP")
            for q in range(4):
                load_T(sconv[q], 88, hist_s[:, 4 + q, :, :].rearrange("p j t -> p t j"), "hist_s_in%d" % q,
                       view=lambda a: a.rearrange("p (t j) -> p t j", t=2))
            hs_in = tuple(K("hist_s_in%d" % q) for q in range(4)) + (K("hist_s_ov"),)
            prefetch_x(nxt)
            ffn(n, [(0, n, 8, lambda jj: hist_s[:, :, jj, :], lambda jj: K("hist_s", jj), hs_in, None)])
            mk("ffn")
            resid_mixed(Gf, "Gf")
            swap_tiles([yov[:, :], ysam[:, :]], nxt)
            mk("end")
            self.ts("dve", hist[:], hist_s[:, 3, :, :], flag[:, 0:1], None, ALU.mult, None,
                    tuple(K("hist_s", jj) for jj in range(44)) + (K("flag"),), tuple(K("hist", jj) for jj in range(44)))
            for q in range(4):
                store_hist(hist_s[:, 4 + q, :, :], tuple(K("hist_s", jj) for jj in range(44)), convs_o[q])

        tiles_of = lambda rows, nt: [rows[t * 128:(t + 1) * 128, :] for t in range(nt)]
        sched = []
        t0 = 0
        while t0 < 11:
            nt = min(NB, 11 - t0)
            sched.append(("prefix", xpre[t0 * 128:(t0 + nt) * 128, :], nt, t0, None, nt))
            t0 += nt
        while t0 < 15:
            nt = min(NB, 15 - t0)
            sched.append(("prefix", xpre[t0 * 128:(t0 + nt) * 128, :], nt, t0, None, 0))
            t0 += nt
        sched.append(("mixed", None, 2, 15, None, 0))
        for b in range(NMAIN // NB):
            sched.append(("full", xmain[b * NBT:(b + 1) * NBT, :], NB, 16 + b * NB, ymain[b * NBT:(b + 1) * NBT, :], 0))
        def tiles_for(e):
            return [xov[:, :], xsam[:, :]] if e[0] == "mixed" else tiles_of(e[1], e[2])
        for bi, (mode, src, nt, tg, dst, kvf) in enumerate(sched):
            nxt = tiles_for(sched[bi + 1]) if bi + 1 < len(sched) else None
            if mode == "mixed":
                ada_part2()
                mixed_block(nxt)
            else:
                prompt_block(src, nt, tg, mode, dst, kvf, preloaded=bi > 0, nxt=nxt)
        self.dma("sp", glap_o.rearrange("h k v -> k h v"), Sst[:], tuple(SK + (h,) for h in range(4)), (), ("glap",))
        store_hist(hist[:], tuple(K("hist", jj) for jj in range(44)), convp_o)
        return S.finalize()


_CACHE = {}


def _program():
    if "nc" not in _CACHE:
        b = Builder()
        b.build()
        _CACHE["nc"] = b.nc
    return _CACHE["nc"]


def kernel(x_prompt, x_sample, cache_k_a, cache_v_a, state_gla, state_conv, c_prompt, c_sample,
           w_ada, b_ada, g_pre_mix, g_post_mix, g_pre_ffn, g_post_ffn, w_in, w_gk2, b_gk,
           rel_bias, g_gla, w_br_a, w_br_b, w_out, w_up, w_dw, b_dw, w_down):
    f = lambda a: np.ascontiguousarray(np.asarray(a, dtype=np.float32))
    x_prompt, x_sample = f(x_prompt), f(x_sample)
    rb = f(rel_bias)[0]
    kk = np.arange(128)[:, None]
    qq = np.arange(128)[None, :]
    idx_prev = np.clip(qq + 128 - kk, -128, 128) + 128
    idx_own = np.clip(qq - kk, -128, 128) + 128
    btab = np.stack([rb[:, idx_prev].transpose(1, 0, 2), rb[:, idx_own].transpose(1, 0, 2)])
    cvec = np.broadcast_to(rb[:, 256][None, :], (128, 8))
    wi = f(w_in)[0]
    wi = np.concatenate([wi[:, :3072], wi[:, 3088:], wi[:, 3072:3088]], axis=1)
    wu = f(w_up)[0]
    cols = []
    for g in range(11):
        cols += [wu[:, 2 * g * 128:(2 * g + 2) * 128], wu[:, DFF + 2 * g * 128:DFF + (2 * g + 2) * 128]]
    wu = np.concatenate(cols, axis=1)
    shared = {
        "w_ada": f(w_ada)[0], "b_ada": f(b_ada)[0].reshape(48, 128),
        "gvec": np.concatenate([f(g_pre_mix)[0], f(g_post_mix)[0], f(g_pre_ffn)[0], f(g_post_ffn)[0]]).reshape(32, 128),
        "w_in": f(wi), "w_gk2": f(w_gk2)[0], "b_gk": f(b_gk)[0].reshape(1, 256),
        "btab": f(btab), "cvec": f(cvec), "ggla": f(np.broadcast_to(np.tile(f(g_gla)[0], 4)[None, :], (128, 512))),
        "w_br_a": f(w_br_a)[0], "w_br_b": f(w_br_b)[0], "w_out": f(w_out)[0], "w_up": f(wu),
        "w_dw": f(w_dw)[0].reshape(3 * 44, 128), "b_dw": f(b_dw)[0].reshape(44, 128), "w_down": f(w_down)[0],
    }
    in_maps = []
    for c in range(8):
        b, hf = c // 2, c % 2
        m = dict(shared)
        if hf == 1:
            m["xpre"] = x_prompt[b, 0:NPRE * 128]
            m["xov"] = x_prompt[b, NPRE * 128:2048]
        else:
            m["xpre"] = np.zeros((NPRE * 128, D), np.float32)
            m["xov"] = np.zeros((128, D), np.float32)
        m["xmain"] = x_prompt[b, hf * 2048:(hf + 1) * 2048]
        m["xsam"] = x_sample[4 * c:4 * c + 4].reshape(128, D)
        m["crow"] = f(np.concatenate([f(c_prompt)[b:b + 1], f(c_sample)[4 * c:4 * c + 4]], 0).reshape(40, 128))
        m["flag"] = np.full((128, 1), float(hf), np.float32)
        m["ck"] = f(cache_k_a)[0, 4 * c:4 * c + 4].reshape(4, 512, 512)
        m["cv"] = f(cache_v_a)[0, 4 * c:4 * c + 4].reshape(4, 512, 512)
        m["sgla"] = f(state_gla)[0, 4 * c:4 * c + 4]
        m["sconv"] = f(state_conv)[0, 4 * c:4 * c + 4].reshape(4, 88, 128)
        in_maps.append({k: np.ascontiguousarray(v) for k, v in m.items()})
    nc = _program()
    cores = [int(t) for t in os.environ.get("KCORES", "0,1,2,3,4,5,6,7").split(",")]
    if os.environ.get("KTRACE"):
        res = run_bass_kernel_spmd(nc, [in_maps[c] for c in cores], core_ids=list(range(len(cores))), trace=True)
        print("EXEC_TIME_NS", res.exec_time_ns)
    else:
        res = run_bass_kernel_spmd(nc, [in_maps[c] for c in cores], core_ids=list(range(len(cores))))
    R = {c: res.results[i] for i, c in enumerate(cores)}
    y_prompt = np.zeros((4, 4096, D), np.float32)
    y_sample = np.zeros((32, 32, D), np.float32)
    k_p = np.zeros((1, 4, 512, 8, 64), np.float32); v_p = np.zeros_like(k_p)
    gla_p = np.zeros((1, 4, 4, 64, 128), np.float32)
    conv_p = np.zeros((1, 4, 2, 2 * DFF), np.float32)
    k_s = np.zeros((1, 32, 32, 8, 64), np.float32); v_s = np.zeros_like(k_s)
    gla_s = np.zeros((1, 32, 4, 64, 128), np.float32)
    conv_s = np.zeros((1, 32, 2, 2 * DFF), np.float32)
    for c in cores:
        b, hf = c // 2, c % 2
        r = R[c]
        y_prompt[b, hf * 2048:(hf + 1) * 2048] = r["ymain"]
        y_sample[4 * c:4 * c + 4] = r["ysam"].reshape(4, 32, D)
        if hf == 1:
            k_p[0, b] = r["kp"].reshape(512, 8, 64)
            v_p[0, b] = r["vp"].reshape(512, 8, 64)
            gla_p[0, b] = r["glap"]
            conv_p[0, b] = r["convp"].reshape(2, 2 * DFF)
        k_s[0, 4 * c:4 * c + 4] = r["ks"].reshape(4, 32, 8, 64)
        v_s[0, 4 * c:4 * c + 4] = r["vs"].reshape(4, 32, 8, 64)
        gla_s[0, 4 * c:4 * c + 4] = r["glas"]
        conv_s[0, 4 * c:4 * c + 4] = r["convs"].reshape(4, 2, 2 * DFF)
    return (y_prompt, y_sample, k_p, v_p, gla_p, conv_p, k_s, v_s, gla_s, conv_s)
```
